# Optimizing a Trainium2 kernel written in Bass

```python
import math
import jax, jax.numpy as jnp
from jax import lax
import numpy as np

D_MODEL = 1024
BATCH = 16
SEQ = 2048
DEPTH = 4

SSD_HEAD_DIM = 64
SSD_WIDTH = D_MODEL
SSD_HEADS = SSD_WIDTH // SSD_HEAD_DIM
SSD_GROUPS = 4
SSD_STATE = 128
SSD_CONV = 4
SSD_CHUNK = 128
SSD_CONV_CH = SSD_WIDTH + 2 * SSD_GROUPS * SSD_STATE

DA_QK_DIM = 64
DA_V_DIM = 128
DA_HEADS = D_MODEL // DA_V_DIM
DA_QK_WIDTH = DA_HEADS * 2 * DA_QK_DIM
DA_WIDTH = DA_HEADS * DA_V_DIM
Q_BLOCK = 128
REL_BUCKETS = 32
REL_MAX_DIST = 128

EVEN_SPLITS = (SSD_WIDTH, SSD_CONV_CH, SSD_HEADS, DA_QK_WIDTH, DA_QK_WIDTH, DA_WIDTH, DA_WIDTH)
EVEN_IN = sum(EVEN_SPLITS)
EVEN_MIX = SSD_WIDTH + DA_WIDTH

HG_HEAD_DIM = 128
HG_WIDTH = 2 * D_MODEL
HG_HEADS = HG_WIDTH // HG_HEAD_DIM
HG_CHUNK = 16
ODD_IN = 4 * HG_WIDTH

N_EVEN = (DEPTH + 1) // 2
N_ODD = DEPTH // 2
EPS = 1e-6

kernel_name = "ssd_diffattn_hgrn2_hybrid"


def rmsnorm(x, w):
    xf = x.astype(jnp.float32)
    y = xf * lax.rsqrt(jnp.mean(xf * xf, axis=-1, keepdims=True) + EPS)
    return (y * w.astype(jnp.float32)).astype(x.dtype)


def causal_dwconv(x, w, b):
    K, C = w.shape
    y = lax.conv_general_dilated(x, w[:, None, :].astype(x.dtype), window_strides=(1,),
                                 padding=[(K - 1, 0)], dimension_numbers=("NWC", "WIO", "NWC"),
                                 feature_group_count=C)
    return y + b.astype(x.dtype)


def segsum(a):
    T = a.shape[-1]
    cs = jnp.cumsum(a, axis=-1)
    diff = cs[..., :, None] - cs[..., None, :]
    return jnp.where(jnp.tril(jnp.ones((T, T), bool)), diff, -jnp.inf)


def ssd_scan(x, dt, A, Bm, Cm):
    b, L, H, P = x.shape
    G, N = Bm.shape[2], Bm.shape[3]
    R = H // G
    c, l = L // SSD_CHUNK, SSD_CHUNK
    X = (x.astype(jnp.float32) * dt[..., None]).reshape(b, c, l, G, R, P)
    a = (dt * A).reshape(b, c, l, G, R).transpose(0, 1, 3, 4, 2)
    Bc = Bm.reshape(b, c, l, G, N)
    Cc = Cm.reshape(b, c, l, G, N)
    a_cs = jnp.cumsum(a, axis=-1)
    CB = jnp.einsum("bclgn,bcsgn->bcgls", Cc, Bc)
    scores = CB[:, :, :, None] * jnp.exp(segsum(a))
    y_diag = jnp.einsum("bcgrls,bcsgrp->bclgrp", scores, X)
    decay_states = jnp.exp(a_cs[..., -1:] - a_cs)
    states = jnp.einsum("bclgn,bcgrl,bclgrp->bcgrpn", Bc, decay_states, X)
    chunk_a = jnp.pad(a_cs[..., -1].transpose(0, 2, 3, 1), ((0, 0), (0, 0), (0, 0), (1, 0)))
    decay_chunk = jnp.exp(segsum(chunk_a))
    states = jnp.concatenate([jnp.zeros_like(states[:, :1]), states], axis=1)
    new_states = jnp.einsum("bgrzc,bcgrpn->bzgrpn", decay_chunk, states)
    prev = new_states[:, :-1]
    y_off = jnp.einsum("bclgn,bcgrpn,bcgrl->bclgrp", Cc, prev, jnp.exp(a_cs))
    return (y_diag + y_off).reshape(b, L, H, P)


def t5_bucket(rel):
    n = jnp.maximum(rel, 0)
    max_exact = REL_BUCKETS // 2
    large = max_exact + (jnp.log(jnp.maximum(n, 1).astype(jnp.float32) / max_exact)
                         / math.log(REL_MAX_DIST / max_exact) * (REL_BUCKETS - max_exact)).astype(jnp.int32)
    large = jnp.minimum(large, REL_BUCKETS - 1)
    return jnp.where(n < max_exact, n, large)


def diff_attention(q, k, v, lam, rel_table):
    b, L, H, _, d = q.shape
    nb = L // Q_BLOCK
    scale = d ** -0.5
    kpos = jnp.arange(L)
    qb = q.reshape(b, nb, Q_BLOCK, H, 2, d).transpose(1, 0, 2, 3, 4, 5)

    def block(args):
        qi, i = args
        qpos = i * Q_BLOCK + jnp.arange(Q_BLOCK)
        rel = qpos[:, None] - kpos[None, :]
        bias = rel_table[t5_bucket(rel)].transpose(2, 0, 1).astype(jnp.float32)
        s = jnp.einsum("bqhcd,bkhcd->bhcqk", qi, k).astype(jnp.float32) * scale + bias[None, :, None]
        s = jnp.where(rel >= 0, s, -jnp.inf)
        p = jax.nn.softmax(s, axis=-1)
        w = p[:, :, 0] - lam * p[:, :, 1]
        return jnp.einsum("bhqk,bkhv->bqhv", w.astype(v.dtype), v)

    out = lax.map(block, (qb, jnp.arange(nb)))
    return out.transpose(1, 0, 2, 3, 4).reshape(b, L, H, v.shape[-1])


def ssd_diffattn_mixer(u, w_in, w_out, conv_w, conv_b, dt_bias, A_log, D_skip, ssd_norm_w,
                       lq1, lk1, lq2, lk2, subln_w, rel_table, layer_idx):
    b, L, _ = u.shape
    idx = [int(s) for s in np.cumsum(EVEN_SPLITS)[:-1]]
    z, xBC, dt, q, k, v, g = jnp.split(u @ w_in, idx, axis=-1)
    xBC = jax.nn.silu(causal_dwconv(xBC, conv_w, conv_b))
    xs, Bm, Cm = jnp.split(xBC, [SSD_WIDTH, SSD_WIDTH + SSD_GROUPS * SSD_STATE], axis=-1)
    xs = xs.reshape(b, L, SSD_HEADS, SSD_HEAD_DIM)
    Bm = Bm.reshape(b, L, SSD_GROUPS, SSD_STATE)
    Cm = Cm.reshape(b, L, SSD_GROUPS, SSD_STATE)
    dt = jax.nn.softplus(dt.astype(jnp.float32) + dt_bias.astype(jnp.float32))
    A = -jnp.exp(A_log.astype(jnp.float32))
    y = ssd_scan(xs, dt, A, Bm, Cm) + xs * D_skip[:, None]
    y = y.astype(u.dtype).reshape(b, L, SSD_WIDTH)
    y_a = rmsnorm(y * jax.nn.silu(z), ssd_norm_w)
    q = q.reshape(b, L, DA_HEADS, 2, DA_QK_DIM)
    k = k.reshape(b, L, DA_HEADS, 2, DA_QK_DIM)
    v = v.reshape(b, L, DA_HEADS, DA_V_DIM)
    lam_init = 0.8 - 0.6 * math.exp(-0.3 * layer_idx)
    lam = (jnp.exp(jnp.sum(lq1.astype(jnp.float32) * lk1.astype(jnp.float32)))
           - jnp.exp(jnp.sum(lq2.astype(jnp.float32) * lk2.astype(jnp.float32))) + lam_init)
    o = diff_attention(q, k, v, lam, rel_table)
    o = rmsnorm(o, subln_w) * (1.0 - lam_init)
    y_b = o.reshape(b, L, DA_WIDTH) * jax.nn.silu(g)
    return jnp.concatenate([y_a, y_b], axis=-1) @ w_out


def gated_linear_recurrence(q, k, v, log_f):
    b, L, H, dk = q.shape
    dv = v.shape[-1]
    C = HG_CHUNK
    nc = L // C

    def to_chunks(t):
        return t.reshape(b, nc, C, H, t.shape[-1]).transpose(1, 0, 3, 2, 4)

    causal = jnp.tril(jnp.ones((C, C), bool))

    def step(S, inp):
        qi, ki, vi, gi = inp
        bcum = jnp.cumsum(gi.astype(jnp.float32), axis=2)
        diff = bcum[:, :, :, None, :] - bcum[:, :, None, :, :]
        decay = jnp.exp(jnp.where(causal[:, :, None], diff, -jnp.inf))
        att = jnp.sum(qi[:, :, :, None, :] * ki[:, :, None, :, :] * decay, axis=-1)
        o = (jnp.einsum("bhts,bhsv->bhtv", att, vi)
             + jnp.einsum("bhtd,bhdv->bhtv", qi * jnp.exp(bcum), S))
        blast = bcum[:, :, -1:, :]
        S = (jnp.exp(blast[:, :, 0, :, None]) * S
             + jnp.einsum("bhsd,bhsv->bhdv", ki * jnp.exp(blast - bcum), vi))
        return S, o

    S0 = jnp.zeros((b, H, dk, dv), jnp.float32)
    _, o = lax.scan(step, S0, (to_chunks(q), to_chunks(k), to_chunks(v), to_chunks(log_f)))
    return o.transpose(1, 0, 3, 2, 4).reshape(b, L, H, dv).astype(v.dtype)


def hgrn2_mixer(u, w_in, w_out, lower_bounds, norm_w, layer_idx):
    b, L, _ = u.shape
    q, f, i, g = jnp.split(u @ w_in, 4, axis=-1)
    lb_all = jax.nn.softmax(lower_bounds.astype(jnp.float32), axis=0)
    lb_all = jnp.cumsum(lb_all, axis=0) - lb_all[0]
    lb = lb_all[layer_idx]
    f = lb + (1.0 - lb) * jax.nn.sigmoid(f.astype(jnp.float32))
    log_f = jnp.log(f)
    k = (1.0 - f).astype(u.dtype)
    q = jax.nn.silu(q)
    hs = lambda t: t.reshape(b, L, HG_HEADS, HG_HEAD_DIM)
    o = gated_linear_recurrence(hs(q), hs(k), hs(i), hs(log_f))
    o = rmsnorm(o, norm_w).reshape(b, L, HG_WIDTH) * jax.nn.silu(g)
    return o @ w_out


def setup_inputs(seed: int = 0) -> dict:
    key = jax.random.key(seed)
    ks = jax.random.split(key, 24)
    nrm = lambda k, shape, s: jax.random.normal(k, shape, jnp.float32) * s
    dt = jnp.exp(jax.random.uniform(ks[7], (N_EVEN, SSD_HEADS), jnp.float32)
                 * (math.log(0.1) - math.log(1e-3)) + math.log(1e-3))
    dt = jnp.maximum(dt, 1e-4)
    return {
        "x": jax.random.normal(ks[0], (BATCH, SEQ, D_MODEL), jnp.float32),
        "norm_w": 1.0 + nrm(ks[1], (DEPTH, D_MODEL), 0.02),
        "final_norm_w": 1.0 + nrm(ks[2], (D_MODEL,), 0.02),
        "rel_bias": nrm(ks[3], (REL_BUCKETS, DA_HEADS), 0.1),
        "even_w_in": nrm(ks[4], (N_EVEN, D_MODEL, EVEN_IN), D_MODEL ** -0.5),
        "even_w_out": nrm(ks[5], (N_EVEN, EVEN_MIX, D_MODEL), EVEN_MIX ** -0.5),
        "conv_w": nrm(ks[6], (N_EVEN, SSD_CONV, SSD_CONV_CH), SSD_CONV ** -0.5),
        "conv_b": nrm(ks[8], (N_EVEN, SSD_CONV_CH), 0.02),
        "dt_bias": dt + jnp.log(-jnp.expm1(-dt)),
        "A_log": jnp.log(jax.random.uniform(ks[9], (N_EVEN, SSD_HEADS), jnp.float32, 1.0, 16.0)),
        "D_skip": 1.0 + nrm(ks[10], (N_EVEN, SSD_HEADS), 0.02),
        "ssd_norm_w": 1.0 + nrm(ks[11], (N_EVEN, SSD_WIDTH), 0.02),
        "lambda_q1": nrm(ks[12], (N_EVEN, DA_QK_DIM), 0.1),
        "lambda_k1": nrm(ks[13], (N_EVEN, DA_QK_DIM), 0.1),
        "lambda_q2": nrm(ks[14], (N_EVEN, DA_QK_DIM), 0.1),
        "lambda_k2": nrm(ks[15], (N_EVEN, DA_QK_DIM), 0.1),
        "subln_w": 1.0 + nrm(ks[16], (N_EVEN, DA_V_DIM), 0.02),
        "odd_w_in": nrm(ks[17], (N_ODD, D_MODEL, ODD_IN), D_MODEL ** -0.5),
        "odd_w_out": nrm(ks[18], (N_ODD, HG_WIDTH, D_MODEL), HG_WIDTH ** -0.5),
        "hgrn_lower_bounds": nrm(ks[19], (DEPTH, HG_WIDTH), 0.1),
        "hgrn_norm_w": 1.0 + nrm(ks[20], (N_ODD, HG_HEAD_DIM), 0.02),
    }


def reference(x, norm_w, final_norm_w, rel_bias, even_w_in, even_w_out, conv_w, conv_b, dt_bias,
              A_log, D_skip, ssd_norm_w, lambda_q1, lambda_k1, lambda_q2, lambda_k2, subln_w,
              odd_w_in, odd_w_out, hgrn_lower_bounds, hgrn_norm_w):
    h = x
    for layer in range(DEPTH):
        u = rmsnorm(h, norm_w[layer])
        if layer % 2 == 0:
            e = layer // 2
            h = h + ssd_diffattn_mixer(u, even_w_in[e], even_w_out[e], conv_w[e], conv_b[e],
                                       dt_bias[e], A_log[e], D_skip[e], ssd_norm_w[e],
                                       lambda_q1[e], lambda_k1[e], lambda_q2[e], lambda_k2[e],
                                       subln_w[e], rel_bias, layer)
        else:
            o = layer // 2
            h = h + hgrn2_mixer(u, odd_w_in[o], odd_w_out[o], hgrn_lower_bounds,
                                hgrn_norm_w[o], layer)
    return rmsnorm(h, final_norm_w)
```

```python
import math, contextlib
import numpy as np
import concourse.bass as bass
import concourse.mybir as mybir
from concourse.bass_utils import run_bass_kernel_spmd
from concourse.alu_op_type import AluOpType as ALU

F32 = mybir.dt.float32
BF16 = mybir.dt.bfloat16
AF = mybir.ActivationFunctionType
AX = mybir.AxisListType

D = 1024
KC = 8
EPS = 1e-6
DEPTH = 4
HG_W = 2048
ARF_N = 7616
ARB_N = 35840


class Unit:
    __slots__ = ("name", "lw", "rd")

    def __init__(self, name):
        self.name = name
        self.lw = None
        self.rd = []


class _Rec:
    def __init__(self):
        self.call = None

    def __getattr__(self, name):
        def f(*args, **kw):
            assert self.call is None
            self.call = (name, args, kw)
            return None
        return f


class Prog:
    ENGS = ("pe", "act", "dve", "pool", "sp")

    def __init__(self, nc):
        self.nc = nc
        self.ops = []
        self.nunits = 0
        self.last_eng = {}
        self.last_key = {}

    def unit(self, name=None):
        self.nunits += 1
        return Unit(name or f"u{self.nunits}")

    def units(self, n, name="u"):
        return [self.unit(f"{name}{i}") for i in range(n)]

    def op(self, eng, fn, reads=(), writes=(), dma_key=None, extra_deps=()):
        idx = len(self.ops)
        deps = set(extra_deps)
        for u in reads:
            if u.lw is not None:
                deps.add(u.lw)
        for u in writes:
            if u.lw is not None:
                deps.add(u.lw)
            deps.update(u.rd)
        for u in reads:
            u.rd.append(idx)
        for u in writes:
            u.lw = idx
            u.rd = []
        deps.discard(idx)
        if fn is not None:
            rec = _Rec()
            fn(rec)
            assert rec.call is not None
            fn = rec.call
        self.ops.append(dict(eng=eng, fn=fn, deps=deps, dma_key=dma_key))
        if fn is not None:
            if dma_key is None:
                self.last_eng[eng] = idx
            else:
                self.last_key[dma_key] = idx
        return idx

    def pe(self, fn, reads=(), writes=()):
        return self.op("pe", fn, reads, writes)

    def act(self, fn, reads=(), writes=()):
        return self.op("act", fn, reads, writes)

    def dve(self, fn, reads=(), writes=()):
        return self.op("dve", fn, reads, writes)

    def pool(self, fn, reads=(), writes=()):
        return self.op("pool", fn, reads, writes)

    def dma(self, eng, fn, key, reads=(), writes=()):
        return self.op(eng, fn, reads, writes, dma_key=key)

    def barrier(self):
        deps = set(self.last_eng.values()) | set(self.last_key.values())
        for e in self.ENGS:
            self.op(e, None, extra_deps=deps)

    def emit(self, final_wait_ops=()):
        nc = self.nc
        ops = self.ops
        n = len(ops)

        def skip(od, o):
            return (od["eng"] == "pe" and o["eng"] == "pe" and od["dma_key"] is None
                    and o["dma_key"] is None and o["fn"] is not None)

        needed = [False] * n
        for i, o in enumerate(ops):
            for d in o["deps"]:
                if skip(ops[d], o):
                    continue
                needed[d] = True
        for d in final_wait_ops:
            needed[d] = True
        chan_count = {}
        ev = [None] * n
        for i, o in enumerate(ops):
            if o["fn"] is None:
                continue
            if o["dma_key"] is not None:
                ch = ("dma", o["dma_key"])
                chan_count[ch] = chan_count.get(ch, 0) + 16
                ev[i] = (ch, chan_count[ch])
            elif needed[i]:
                ch = ("eng", o["eng"])
                chan_count[ch] = chan_count.get(ch, 0) + 1
                ev[i] = (ch, chan_count[ch])
        chans = sorted(chan_count.keys(), key=str)
        self.n_sems = len(chans)
        sems = {}
        stack = contextlib.ExitStack()
        for ci, ch in enumerate(chans):
            sems[ch] = stack.enter_context(nc.semaphore(f"s{ci}"))
        known = {e: {} for e in self.ENGS}
        clock = [None] * n
        streams = {e: [] for e in self.ENGS}
        for i, o in enumerate(ops):
            e = o["eng"]
            kn = known[e]
            wd = {}
            for d in sorted(o["deps"]):
                od = ops[d]
                if skip(od, o):
                    continue
                ch, v = ev[d]
                if kn.get(ch, 0) >= v:
                    continue
                for c2, v2 in clock[d].items():
                    if kn.get(c2, 0) < v2:
                        kn[c2] = v2
                wd[ch] = max(wd.get(ch, 0), v)
            ck = dict(kn)
            if ev[i] is not None:
                ch, v = ev[i]
                ck[ch] = v
            clock[i] = ck
            streams[e].append((list(wd.items()), o["fn"], ev[i]))
        final = [ev[d] for d in final_wait_ops]
        for ch, tot in chan_count.items():
            if ch[0] == "dma":
                final.append((ch, tot))
        self.sems, self.streams, self.final, self._stack = sems, streams, final, stack

    def run_block(self):
        nc = self.nc
        sems, streams, final = self.sems, self.streams, self.final
        with nc.Block() as block:
            def mk(ename):
                def body(eng):
                    for waits, fn, e in streams[ename]:
                        for ch, v in waits:
                            eng.wait_ge(sems[ch], v)
                        if fn is None:
                            continue
                        ins = getattr(eng, fn[0])(*fn[1], **fn[2])
                        if e is not None:
                            ins.then_inc(sems[e[0]], 16 if e[0][0] == "dma" else 1)
                    if ename == "sp":
                        for ch, v in final:
                            eng.wait_ge(sems[ch], v)
                return body
            block.tensor(mk("pe"))
            block.scalar(mk("act"))
            block.vector(mk("dve"))
            block.gpsimd(mk("pool"))
            block.sync(mk("sp"))
        self._stack.close()


def _t5_bucket(rel):
    n = np.maximum(rel, 0)
    max_exact = 16
    large = max_exact + (np.log(np.maximum(n, 1).astype(np.float32) / max_exact)
                         / math.log(128 / max_exact) * (32 - max_exact)).astype(np.int32)
    large = np.minimum(large, 31)
    return np.where(n < max_exact, n, large)


def host_consts():
    s = np.arange(128)[:, None]
    t = np.arange(128)[None, :]
    c = {}
    c["ident"] = np.eye(128, dtype=np.float32)
    c["ones"] = np.ones((128, 128), np.float32)
    c["triC"] = ((s <= t).astype(np.float32) - (s <= 63).astype(np.float32))
    c["triU"] = (s > t).astype(np.float32)
    c["triI"] = (s <= t).astype(np.float32)
    sel = np.zeros((128, 2), np.float32)
    sel[:64, 0] = 1.0
    sel[:, 1] = 1.0
    c["sel"] = sel
    c["mg16"] = np.eye(16, dtype=np.float32)
    mq = np.zeros((128, 2), np.float32)
    mq[:64, 0] = 1.0
    mq[64:, 1] = 1.0
    c["maskq"] = mq
    return c


def host_layout(inp):
    f = lambda a: np.ascontiguousarray(np.asarray(a, dtype=np.float32))
    m = dict(host_consts())
    m["final_norm_w"] = f(inp["final_norm_w"])
    m["norm_w_cols"] = f(np.asarray(inp["norm_w"]).reshape(4, 8, 128).transpose(0, 2, 1))
    owin = np.asarray(inp["odd_w_in"])
    t = owin.reshape(2, 8, 128, 4, 16, 128).transpose(0, 4, 2, 1, 3, 5)
    m["odd_w_in_t"] = f(t).reshape(2, 16, 128, 8 * 512)
    m["odd_w_out"] = f(inp["odd_w_out"])
    m["hgrn_lower_bounds"] = f(inp["hgrn_lower_bounds"])
    m["hgrn_norm_w"] = f(inp["hgrn_norm_w"])
    ew = np.asarray(inp["even_w_in"]).reshape(2, 8, 128, 7184)
    z = ew[..., 0:1024]; xs = ew[..., 1024:2048]; Bm = ew[..., 2048:2560]; Cm = ew[..., 2560:3072]
    dt = ew[..., 3072:3088]
    q = ew[..., 3088:4112]; kk = ew[..., 4112:5136]; v = ew[..., 5136:6160]; gg = ew[..., 6160:7184]
    ssd = np.concatenate([z.reshape(2, 8, 128, 4, 256), xs.reshape(2, 8, 128, 4, 256),
                          Bm.reshape(2, 8, 128, 4, 128), Cm.reshape(2, 8, 128, 4, 128)], axis=-1)
    m["ev_w_ssd"] = f(ssd.transpose(0, 3, 2, 1, 4)).reshape(2, 4, 128, 8 * 768)
    m["ev_w_dt"] = f(dt.transpose(0, 2, 1, 3)).reshape(2, 128, 8 * 16)
    att = np.concatenate([q.reshape(2, 8, 128, 8, 128), kk.reshape(2, 8, 128, 8, 128),
                          v.reshape(2, 8, 128, 8, 128), gg.reshape(2, 8, 128, 8, 128)], axis=-1)
    m["ev_w_att"] = f(att.transpose(0, 3, 2, 1, 4)).reshape(2, 8, 128, 8 * 512)
    wo = np.asarray(inp["even_w_out"]).reshape(2, 16, 128, 8, 128)
    m["ev_w_out_t"] = f(wo.transpose(0, 3, 2, 1, 4)).reshape(2, 8, 128, 16 * 128)
    m["conv_w_cols"] = f(np.asarray(inp["conv_w"]).reshape(2, 4, 16, 128).transpose(0, 3, 2, 1)).reshape(2, 128, 64)
    m["conv_b_cols"] = f(np.asarray(inp["conv_b"]).reshape(2, 16, 128).transpose(0, 2, 1))
    for nm in ("dt_bias", "A_log", "D_skip", "lambda_q1", "lambda_k1", "lambda_q2", "lambda_k2", "subln_w"):
        m[nm] = f(inp[nm])
    m["ssd_norm_w_cols"] = f(np.asarray(inp["ssd_norm_w"]).reshape(2, 8, 128).transpose(0, 2, 1))
    rb = np.asarray(inp["rel_bias"], dtype=np.float32)
    kpos = np.arange(128)[:, None]
    qpos = np.arange(128)[None, :]
    bd = np.empty((128, 8, 2, 128), np.float32)
    for Dd in range(2):
        rel = qpos - kpos + 128 * Dd
        bidx = _t5_bucket(rel)
        g_ = rb[bidx]
        g_ = np.where((rel >= 0)[:, :, None], g_, np.float32(-30000.0))
        bd[:, :, Dd, :] = g_.transpose(0, 2, 1)
    m["rel_biasD"] = f(bd).reshape(128, 8 * 2 * 128)
    m["rel_b31"] = f(rb[31])
    return m


class NS:
    pass


class Carver:
    def __init__(self, g):
        self.g = g
        self.fo = 0
        self.bo = 0

    def f(self, n):
        ap = self.g.arf[:, self.fo:self.fo + n]
        self.fo += (n + 7) // 8 * 8
        assert self.fo <= ARF_N, ("ARF overflow", self.fo)
        return ap

    def b(self, n):
        ap = self.g.arb[:, self.bo:self.bo + n]
        self.bo += (n + 15) // 16 * 16
        assert self.bo <= ARB_N, ("ARB overflow", self.bo)
        return ap


def bank(g, i):
    return g.ps[:, i, :]


def build(L=2048, NSEQ=2, layers=(0, 1, 2, 3)):
    nc = bass.Bass("TRN2", target_bir_lowering=False)
    NT, NB = L // 512, L // 128
    g = NS()
    g.nc, g.L, g.NT, g.NB = nc, L, NT, NB
    dr = lambda name, shape, kind="ExternalInput": nc.dram_tensor(name, list(shape), F32, kind=kind).ap()
    g.x_d = dr("x", [NSEQ, L, D])
    g.out_d = dr("out", [NSEQ, L, D], "ExternalOutput")
    g.d = {}
    shapes = {
        "final_norm_w": [D], "norm_w_cols": [DEPTH, 128, KC],
        "ident": [128, 128], "ones": [128, 128], "triC": [128, 128], "triU": [128, 128], "triI": [128, 128],
        "sel": [128, 2], "mg16": [16, 16], "maskq": [128, 2],
        "odd_w_in_t": [2, 16, 128, KC * 512], "odd_w_out": [2, HG_W, D], "hgrn_lower_bounds": [DEPTH, HG_W],
        "hgrn_norm_w": [2, 128],
        "ev_w_ssd": [2, 4, 128, 8 * 768], "ev_w_dt": [2, 128, 8 * 16], "ev_w_att": [2, 8, 128, 8 * 512],
        "ev_w_out_t": [2, 8, 128, 16 * 128], "conv_w_cols": [2, 128, 64], "conv_b_cols": [2, 128, 16],
        "dt_bias": [2, 16], "A_log": [2, 16], "D_skip": [2, 16], "lambda_q1": [2, 64], "lambda_k1": [2, 64],
        "lambda_q2": [2, 64], "lambda_k2": [2, 64], "subln_w": [2, 128], "ssd_norm_w_cols": [2, 128, 8],
        "rel_biasD": [128, 8 * 2 * 128], "rel_b31": [8],
    }
    for nm, shp in shapes.items():
        g.d[nm] = dr(nm, shp)
    g.in_names = ["x"] + list(shapes.keys())

    es = contextlib.ExitStack()
    sb = lambda name, shape, dt=F32: es.enter_context(nc.sbuf_tensor(name, list(shape), dt))
    P = Prog(nc)
    g.P = P
    g.hT = sb("hT", [128, KC, L]); g.hU = [[P.unit(f"h{k}_{n}") for n in range(NT)] for k in range(KC)]
    g.uT = sb("uT", [128, KC, L], BF16); g.uU = [P.unit(f"u{n}") for n in range(NT)]
    g.cst = {}
    g.cU = P.unit("consts")
    for nm in ("ident", "ones", "triI"):
        g.cst[nm] = sb("c_" + nm, [128, 128])
    g.cst["sel"] = sb("c_sel", [128, 2])
    g.cst["maskq"] = sb("c_maskq", [128, 2])
    g.cst["mg16"] = sb("c_mg16", [16, 16])
    g.nwc = sb("nwc", [128, DEPTH, KC])
    g.stat = sb("stat", [128, 16]); g.statU = P.unit("stat")
    g.sq = [sb(f"sq{i}", [128, 512]) for i in range(2)]; g.sqU = P.units(2, "sq")
    g.rstd_t = sb("rstd_t", [128, 512]); g.rstdU = P.unit("rstd_t")
    g.arf = sb("arf", [128, ARF_N])
    g.arb = sb("arb", [128, ARB_N], BF16)
    g.ps = es.enter_context(nc.psum_tensor("ps", [128, 8, 512], F32))
    g.bU = P.units(8, "bank")

    for nm in ("ident", "ones", "triI", "sel", "maskq", "mg16"):
        P.dma("sp", lambda e, nm=nm: e.dma_start(out=g.cst[nm][:], in_=g.d[nm]), "c_" + nm, writes=[g.cU])
    P.dma("sp", lambda e: e.dma_start(out=g.nwc[:], in_=g.d["norm_w_cols"].rearrange("l p k -> p l k")), "c_nwc", writes=[g.cU])

    out_ops = []
    for s in range(NSEQ):
        P.barrier()
        load_x(g, s)
        for li in layers:
            rms_to_uT(g, li)
            P.barrier()
            if li % 2 == 1:
                odd_layer(g, li)
            else:
                even_ssd(g, li)
                P.barrier()
                even_attn(g, li)
            P.barrier()
        out_ops += final_norm_store(g, s)
    P.emit(final_wait_ops=out_ops[-4:])
    P.run_block()
    es.close()
    return nc, P


def load_x(g, s):
    P, NT = g.P, g.NT
    A = Carver(g)
    xst = A.f(4096).rearrange("p (b d) -> p b d", b=4)
    xU = P.unit("xst")
    ident = g.cst["ident"]
    for n in range(NT):
        src = g.x_d[s, n * 512:(n + 1) * 512, :].rearrange("(b p) d -> p b d", p=128)
        P.dma("sp", lambda e, src=src: e.dma_start(out=xst, in_=src), "xst", writes=[xU])
        for k in range(KC):
            bk = k % 8
            for b in range(4):
                P.pe(lambda e, bk=bk, b=b, k=k: e.transpose(
                    bank(g, bk)[:, b * 128:(b + 1) * 128], xst[:, b, k * 128:(k + 1) * 128], ident[:]),
                    reads=[xU, g.cU], writes=[g.bU[bk]])
            if k % 2 == 0:
                P.dve(lambda e, bk=bk, k=k, n=n: e.tensor_copy(g.hT[:, k, n * 512:(n + 1) * 512], bank(g, bk)),
                      reads=[g.bU[bk]], writes=[g.hU[k][n]])
            else:
                P.act(lambda e, bk=bk, k=k, n=n: e.copy(g.hT[:, k, n * 512:(n + 1) * 512], bank(g, bk)),
                      reads=[g.bU[bk]], writes=[g.hU[k][n]])


def final_norm_store(g, s):
    P, NB = g.P, g.NB
    A = Carver(g)
    fnw = A.f(D); fnwU = P.unit("fnw")
    ost = [A.f(D) for _ in range(2)]; ostU = P.units(2, "ost")
    junk = A.f(1024); junkU = P.unit("junk")
    ident = g.cst["ident"]
    P.dma("sp", lambda e: e.dma_start(out=fnw, in_=g.d["final_norm_w"].partition_broadcast(128)), "fnw", writes=[fnwU])
    outs = []
    for b in range(NB):
        n = b // 4
        oi = b % 2
        for k in range(KC):
            bk = k // 4
            P.pe(lambda e, bk=bk, k=k, b=b: e.transpose(
                bank(g, bk)[:, (k % 4) * 128:(k % 4 + 1) * 128], g.hT[:, k, b * 128:(b + 1) * 128], ident[:]),
                reads=[g.hU[k][n], g.cU], writes=[g.bU[bk]])
        for half in range(2):
            P.act(lambda e, half=half: e.activation(
                out=junk[:, half * 512:(half + 1) * 512], in_=bank(g, half), func=AF.Square,
                accum_out=g.stat[:, half:half + 1]),
                reads=[g.bU[half]], writes=[junkU, g.statU])
        P.dve(lambda e: e.tensor_tensor(out=g.stat[:, 2:3], in0=g.stat[:, 0:1], in1=g.stat[:, 1:2], op=ALU.add),
              reads=[g.statU], writes=[g.statU])
        P.act(lambda e: e.activation(out=g.stat[:, 3:4], in_=g.stat[:, 2:3], func=AF.Ln, scale=1.0 / D, bias=EPS),
              reads=[g.statU], writes=[g.statU])
        P.act(lambda e: e.activation(out=g.stat[:, 4:5], in_=g.stat[:, 3:4], func=AF.Exp, scale=-0.5),
              reads=[g.statU], writes=[g.statU])
        for half in range(2):
            P.dve(lambda e, half=half, oi=oi: e.scalar_tensor_tensor(
                out=ost[oi][:, half * 512:(half + 1) * 512], in0=bank(g, half), scalar=g.stat[:, 4:5],
                in1=fnw[:, half * 512:(half + 1) * 512], op0=ALU.mult, op1=ALU.mult),
                reads=[g.bU[half], g.statU, fnwU], writes=[ostU[oi]])
        o = P.dma("sp", lambda e, oi=oi, s=s, b=b: e.dma_start(out=g.out_d[s, b * 128:(b + 1) * 128, :], in_=ost[oi]),
                  f"ost{oi}", reads=[ostU[oi]])
        outs.append(o)
    return outs


def rms_rstd_tile(g, src_fn, reads_fn, nchunks, dim):
    P = g.P
    ones = g.cst["ones"]
    for k in range(nchunks):
        i = k % 2
        P.act(lambda e, i=i, k=k: e.activation(out=g.sq[i][:], in_=src_fn(k), func=AF.Square),
              reads=reads_fn(k), writes=[g.sqU[i]])
        P.pe(lambda e, i=i, k=k: e.matmul(bank(g, 7), lhsT=ones[:], rhs=g.sq[i][:], start=(k == 0), stop=(k == nchunks - 1)),
             reads=[g.sqU[i], g.cU], writes=[g.bU[7]])
    P.act(lambda e: e.activation(out=g.rstd_t[:], in_=bank(g, 7), func=AF.Ln, scale=1.0 / dim, bias=EPS),
          reads=[g.bU[7]], writes=[g.rstdU])
    P.act(lambda e: e.activation(out=g.rstd_t[:], in_=g.rstd_t[:], func=AF.Exp, scale=-0.5),
          reads=[g.rstdU], writes=[g.rstdU])


def rms_to_uT(g, li):
    P, NT = g.P, g.NT
    for n in range(NT):
        sl = slice(n * 512, (n + 1) * 512)
        rms_rstd_tile(g, lambda k, sl=sl: g.hT[:, k, sl], lambda k, n=n: [g.hU[k][n]], KC, D)
        for k in range(KC):
            P.dve(lambda e, k=k, sl=sl: e.scalar_tensor_tensor(
                out=g.uT[:, k, sl], in0=g.hT[:, k, sl], scalar=g.nwc[:, li, k:k + 1], in1=g.rstd_t[:],
                op0=ALU.mult, op1=ALU.mult),
                reads=[g.hU[k][n], g.rstdU, g.cU], writes=[g.uU[n]])


def odd_layer(g, li):
    P, L, NB, NT = g.P, g.L, g.NB, g.NT
    oi = li // 2
    A = Carver(g)
    o = NS()
    c = g.cst
    f3 = lambda: A.f(512).rearrange("p (b d) -> p b d", b=4)
    o.logf = f3(); o.logfU = P.unit()
    o.tA = f3(); o.tAU = P.unit()
    o.kk = f3(); o.kkU = P.unit()
    o.qs = f3(); o.qsU = P.unit()
    o.e13 = A.f(1024).rearrange("p (t b d) -> p t b d", t=2, b=4); o.e13U = P.unit()
    o.e2 = f3(); o.e2U = P.unit()
    o.qt = f3(); o.qtU = P.unit()
    o.kt = f3(); o.ktU = P.unit()
    o.lbr = f3(); o.lbrU = P.unit()
    o.lbh = A.f(128); o.omlh = A.f(128); o.den = A.f(128); o.lbU = P.unit()
    o.eb = [A.f(NB * 2).rearrange("p (b t) -> p b t", t=2) for _ in range(2)]
    o.S = A.f(128); o.SU = P.unit()
    o.yn = A.f(128); o.ynU = P.unit()
    o.yb = A.f(128); o.ybU = P.unit()
    o.osb = A.f(128); o.osbU = P.unit()
    o.junk2 = A.f(128); o.junk2U = P.unit()
    o.hnw = A.f(128); o.hnwU = P.unit()
    o.triC = A.f(128); o.triU = A.f(128); o.triUU = P.unit()
    o.bst = A.f(8); o.bstU = P.unit()
    o.w = [A.b(KC * 512).rearrange("p (k c) -> p k c", k=KC) for _ in range(2)]; o.wU = P.units(2)
    o.wout = A.b(2 * D).rearrange("p (j m) -> p j m", j=2); o.woutU = P.units(2)
    o.qT = [A.b(L) for _ in range(2)]
    o.kT = [A.b(L) for _ in range(2)]
    hb3 = lambda: A.b(NB * 128).rearrange("p (b d) -> p b d", d=128)
    o.kh = [hb3() for _ in range(2)]
    o.v = [hb3() for _ in range(2)]
    o.gs = [hb3() for _ in range(2)]
    o.hbU = [[[P.unit() for _ in range(NT)] for _ in range(6)] for _ in range(2)]
    o.attm = [A.b(128) for _ in range(2)]; o.attmU = P.units(2)
    o.Sb = A.b(128); o.SbU = P.unit()
    o.yT = A.b(2 * L).rearrange("p (j t) -> p j t", j=2); o.yTU = P.units(2)
    QT, KT, KH, VV, GS, EB = range(6)

    P.dma("sp", lambda e: e.dma_start(out=o.hnw, in_=g.d["hgrn_norm_w"][oi].partition_broadcast(128)), "o_hnw", writes=[o.hnwU])
    P.dma("sp", lambda e: e.dma_start(out=o.triC, in_=g.d["triC"]), "o_tri", writes=[o.triUU])
    P.dma("sp", lambda e: e.dma_start(out=o.triU, in_=g.d["triU"]), "o_tri", writes=[o.triUU])
    for i in range(2):
        P.dve(lambda e, i=i: e.memset(o.attm[i], 0.0), writes=[o.attmU[i]])

    def load_w(h):
        i = h % 2
        P.dma("pool", lambda e: e.dma_start(out=o.w[i].rearrange("p k c -> p (k c)"), in_=g.d["odd_w_in_t"][oi, h]),
              f"o_w{i}", writes=[o.wU[i]])

    def head_lb(h):
        hs = slice(h * 128, (h + 1) * 128)
        P.dma("sp", lambda e: e.dma_start(out=o.lbr, in_=g.d["hgrn_lower_bounds"][:, hs].partition_broadcast(128)),
              "o_lbr", writes=[o.lbrU])
        P.act(lambda e: e.activation(out=o.lbr, in_=o.lbr, func=AF.Exp), reads=[o.lbrU], writes=[o.lbrU])
        P.dve(lambda e: e.tensor_tensor(out=o.den, in0=o.lbr[:, 0, :], in1=o.lbr[:, 1, :], op=ALU.add), reads=[o.lbrU], writes=[o.lbU])
        P.dve(lambda e: e.tensor_tensor(out=o.den, in0=o.den, in1=o.lbr[:, 2, :], op=ALU.add), reads=[o.lbrU, o.lbU], writes=[o.lbU])
        P.dve(lambda e: e.tensor_tensor(out=o.den, in0=o.den, in1=o.lbr[:, 3, :], op=ALU.add), reads=[o.lbrU, o.lbU], writes=[o.lbU])
        P.dve(lambda e: e.reciprocal(o.den, o.den), reads=[o.lbU], writes=[o.lbU])
        if li == 1:
            P.dve(lambda e: e.tensor_tensor(out=o.lbh, in0=o.lbr[:, 1, :], in1=o.den, op=ALU.mult), reads=[o.lbrU, o.lbU], writes=[o.lbU])
        else:
            P.dve(lambda e: e.tensor_tensor(out=o.lbh, in0=o.lbr[:, 1, :], in1=o.lbr[:, 2, :], op=ALU.add), reads=[o.lbrU, o.lbU], writes=[o.lbU])
            for j in range(3, li + 1):
                P.dve(lambda e, j=j: e.tensor_tensor(out=o.lbh, in0=o.lbh, in1=o.lbr[:, j, :], op=ALU.add), reads=[o.lbrU, o.lbU], writes=[o.lbU])
            P.dve(lambda e: e.tensor_tensor(out=o.lbh, in0=o.lbh, in1=o.den, op=ALU.mult), reads=[o.lbU], writes=[o.lbU])
        P.dve(lambda e: e.tensor_scalar(out=o.omlh, in0=o.lbh, scalar1=-1.0, scalar2=1.0, op0=ALU.mult, op1=ALU.add),
              reads=[o.lbU], writes=[o.lbU])

    def stageA(h, n):
        hb = h % 2
        wi = h % 2
        U = o.hbU[hb]
        for b in range(4):
            tb = n * 4 + b
            for k in range(KC):
                P.pe(lambda e, b=b, tb=tb, k=k: e.matmul(bank(g, b), lhsT=g.uT[:, k, tb * 128:(tb + 1) * 128], rhs=o.w[wi][:, k, :],
                                                         start=(k == 0), stop=(k == KC - 1)),
                     reads=[g.uU[n], o.wU[wi]], writes=[g.bU[b]])
        pj = g.ps[:, 0:4, :]
        pb = [g.bU[0], g.bU[1], g.bU[2], g.bU[3]]
        bc4 = lambda t: t.unsqueeze(1).to_broadcast([128, 4, 128])
        P.act(lambda e: e.activation(out=o.tA, in_=pj[:, :, 128:256], func=AF.Sigmoid), reads=pb, writes=[o.tAU])
        P.dve(lambda e: e.tensor_tensor(out=o.tA, in0=o.tA, in1=bc4(o.omlh), op=ALU.mult), reads=[o.tAU, o.lbU], writes=[o.tAU])
        P.dve(lambda e: e.tensor_tensor(out=o.tA, in0=o.tA, in1=bc4(o.lbh), op=ALU.add), reads=[o.tAU, o.lbU], writes=[o.tAU])
        P.act(lambda e: e.activation(out=o.logf, in_=o.tA, func=AF.Ln), reads=[o.tAU], writes=[o.logfU])
        P.pool(lambda e: e.tensor_scalar(out=o.kk, in0=o.tA, scalar1=-1.0, scalar2=1.0, op0=ALU.mult, op1=ALU.add),
               reads=[o.tAU], writes=[o.kkU])
        P.act(lambda e: e.activation(out=o.qs, in_=pj[:, :, 0:128], func=AF.Silu), reads=pb, writes=[o.qsU])
        P.act(lambda e: e.activation(out=o.gs[hb][:, n * 4:(n + 1) * 4, :], in_=pj[:, :, 384:512], func=AF.Silu),
              reads=pb, writes=[U[GS][n]])
        P.act(lambda e: e.copy(o.v[hb][:, n * 4:(n + 1) * 4, :], pj[:, :, 256:384]), reads=pb, writes=[U[VV][n]])
        for b in range(4):
            P.pe(lambda e, b=b: e.matmul(bank(g, 4)[:, b * 128:(b + 1) * 128], lhsT=o.triC, rhs=o.logf[:, b, :], start=True, stop=True),
                 reads=[o.logfU, o.triUU], writes=[g.bU[4]])
        for b in range(4):
            P.pe(lambda e, b=b: e.matmul(bank(g, 5)[:, b * 128:(b + 1) * 128], lhsT=o.triU, rhs=o.logf[:, b, :], start=True, stop=True),
                 reads=[o.logfU, o.triUU], writes=[g.bU[5]])
        for b in range(4):
            P.pe(lambda e, b=b: e.matmul(bank(g, 6)[:, b * 2:b * 2 + 2], lhsT=o.logf[:, b, :], rhs=c["sel"][:], start=True, stop=True),
                 reads=[o.logfU, g.cU], writes=[g.bU[6]])
        P.act(lambda e: e.activation(out=o.eb[hb][:, n * 4:(n + 1) * 4, :], in_=bank(g, 6)[:, 0:8].rearrange("p (b t) -> p b t", t=2), func=AF.Exp),
              reads=[g.bU[6]], writes=[U[EB][n]])
        P.act(lambda e: e.activation(out=o.e13, in_=g.ps[:, 4:6, :].rearrange("p t (b d) -> p t b d", b=4), func=AF.Exp),
              reads=[g.bU[4], g.bU[5]], writes=[o.e13U])
        P.act(lambda e: e.activation(out=o.e2, in_=bank(g, 4).rearrange("p (b d) -> p b d", b=4), func=AF.Exp, scale=-1.0),
              reads=[g.bU[4]], writes=[o.e2U])
        P.dve(lambda e: e.tensor_tensor(out=o.qt, in0=o.qs, in1=o.e13[:, 0], op=ALU.mult), reads=[o.qsU, o.e13U], writes=[o.qtU])
        P.pool(lambda e: e.tensor_tensor(out=o.kt, in0=o.kk, in1=o.e2, op=ALU.mult), reads=[o.kkU, o.e2U], writes=[o.ktU])
        P.dve(lambda e: e.tensor_tensor(out=o.kh[hb][:, n * 4:(n + 1) * 4, :], in0=o.kk, in1=o.e13[:, 1], op=ALU.mult),
              reads=[o.kkU, o.e13U], writes=[U[KH][n]])
        idf = c["ident"]
        for b in range(4):
            P.pe(lambda e, b=b: e.transpose(bank(g, 7)[:, b * 128:(b + 1) * 128], o.qt[:, b, :], idf[:]),
                 reads=[o.qtU, g.cU], writes=[g.bU[7]])
        for b in range(4):
            P.pe(lambda e, b=b: e.transpose(bank(g, 6)[:, b * 128:(b + 1) * 128], o.kt[:, b, :], idf[:]),
                 reads=[o.ktU, g.cU], writes=[g.bU[6]])
        P.dve(lambda e: e.tensor_copy(o.qT[hb][:, n * 512:(n + 1) * 512], bank(g, 7)), reads=[g.bU[7]], writes=[U[QT][n]])
        P.act(lambda e: e.copy(o.kT[hb][:, n * 512:(n + 1) * 512], bank(g, 6)), reads=[g.bU[6]], writes=[U[KT][n]])

    def stageB(h, n):
        hb = h % 2
        U = o.hbU[hb]
        hj = h % 2
        for b in range(4):
            tb = n * 4 + b
            ts = slice(tb * 128, (tb + 1) * 128)
            ai = tb % 2
            P.pe(lambda e, ts=ts, tb=tb: e.matmul(bank(g, 5)[:, 64:128], lhsT=o.kT[hb][:, ts],
                                                   rhs=o.qT[hb][:, tb * 128 + 64:(tb + 1) * 128], start=True, stop=True),
                 reads=[U[QT][n], U[KT][n]], writes=[g.bU[5]])
            P.pe(lambda e, tb=tb: e.matmul(bank(g, 5)[0:64, 0:64], lhsT=o.kT[hb][:, tb * 128:tb * 128 + 64],
                                           rhs=o.qT[hb][:, tb * 128:tb * 128 + 64], start=True, stop=True),
                 reads=[U[QT][n], U[KT][n]], writes=[g.bU[5]])
            P.dve(lambda e, ai=ai: e.tensor_tensor(out=o.attm[ai][:, 64:128], in0=bank(g, 5)[:, 64:128], in1=c["triI"][:, 64:128], op=ALU.mult),
                  reads=[g.bU[5], g.cU], writes=[o.attmU[ai]])
            P.dve(lambda e, ai=ai: e.tensor_tensor(out=o.attm[ai][0:64, 0:64], in0=bank(g, 5)[0:64, 0:64], in1=c["triI"][0:64, 0:64], op=ALU.mult),
                  reads=[g.bU[5], g.cU], writes=[o.attmU[ai]])
            if tb > 0:
                P.pool(lambda e, tb=tb: e.tensor_scalar(out=o.Sb, in0=o.S, scalar1=o.eb[hb][:, tb, 0:1], scalar2=None, op0=ALU.mult),
                       reads=[o.SU, U[EB][n]], writes=[o.SbU])
            P.pe(lambda e, ai=ai, tb=tb: e.matmul(bank(g, 4)[:, 0:128], lhsT=o.attm[ai], rhs=o.v[hb][:, tb, :], start=True, stop=(tb == 0)),
                 reads=[o.attmU[ai], U[VV][n]], writes=[g.bU[4]])
            if tb > 0:
                P.pe(lambda e, ts=ts: e.matmul(bank(g, 4)[:, 0:128], lhsT=o.qT[hb][:, ts], rhs=o.Sb, start=False, stop=True),
                     reads=[o.SbU, U[QT][n]], writes=[g.bU[4]])
            P.pe(lambda e, tb=tb: e.matmul(bank(g, 6)[:, 128:256], lhsT=o.kh[hb][:, tb, :], rhs=o.v[hb][:, tb, :], start=True, stop=True),
                 reads=[U[KH][n], U[VV][n]], writes=[g.bU[6]])
            if tb == 0:
                P.dve(lambda e: e.tensor_copy(o.S, bank(g, 6)[:, 128:256]), reads=[g.bU[6]], writes=[o.SU])
            else:
                P.dve(lambda e, tb=tb: e.scalar_tensor_tensor(out=o.S, in0=o.S, scalar=o.eb[hb][:, tb, 1:2], in1=bank(g, 6)[:, 128:256],
                                                               op0=ALU.mult, op1=ALU.add),
                      reads=[o.SU, g.bU[6], U[EB][n]], writes=[o.SU])
            P.act(lambda e: e.copy(o.osb, bank(g, 4)[:, 0:128]), reads=[g.bU[4]], writes=[o.osbU])
            P.dve(lambda e: e.tensor_tensor(out=o.junk2, in0=o.osb, in1=o.osb, op=ALU.mult), reads=[o.osbU], writes=[o.junk2U])
            P.dve(lambda e: e.tensor_reduce(out=o.bst[:, 0:1], in_=o.junk2, axis=AX.X, op=ALU.add), reads=[o.junk2U], writes=[o.bstU])
            P.act(lambda e: e.activation(out=o.bst[:, 1:2], in_=o.bst[:, 0:1], func=AF.Ln, scale=1.0 / 128, bias=EPS),
                  reads=[o.bstU], writes=[o.bstU])
            P.act(lambda e: e.activation(out=o.bst[:, 2:3], in_=o.bst[:, 1:2], func=AF.Exp, scale=-0.5),
                  reads=[o.bstU], writes=[o.bstU])
            P.dve(lambda e: e.scalar_tensor_tensor(out=o.yn, in0=o.osb, scalar=o.bst[:, 2:3], in1=o.hnw,
                                                   op0=ALU.mult, op1=ALU.mult),
                  reads=[o.osbU, o.bstU, o.hnwU], writes=[o.ynU])
            P.dve(lambda e, tb=tb: e.tensor_tensor(out=o.yb, in0=o.yn, in1=o.gs[hb][:, tb, :], op=ALU.mult),
                  reads=[o.ynU, U[GS][n]], writes=[o.ybU])
            P.pe(lambda e: e.transpose(bank(g, 7)[:, 0:128], o.yb, c["ident"][:]), reads=[o.ybU, g.cU], writes=[g.bU[7]])
            P.act(lambda e, ts=ts: e.copy(o.yT[:, hj, ts], bank(g, 7)[:, 0:128]), reads=[g.bU[7]], writes=[o.yTU[hj]])

    def outproj(hp):
        for j in range(2):
            src = g.d["odd_w_out"][oi, (hp * 2 + j) * 128:(hp * 2 + j + 1) * 128, :]
            P.dma("pool", lambda e, j=j, src=src: e.dma_start(out=o.wout[:, j, :], in_=src), f"o_wout{j}", writes=[o.woutU[j]])
        cnt = 0
        for m in range(KC):
            for n in range(NT):
                bk = cnt % 4
                cnt += 1
                for j in range(2):
                    P.pe(lambda e, bk=bk, m=m, n=n, j=j: e.matmul(bank(g, bk), lhsT=o.wout[:, j, m * 128:(m + 1) * 128],
                                                                     rhs=o.yT[:, j, n * 512:(n + 1) * 512], start=(j == 0), stop=(j == 1)),
                         reads=[o.woutU[j], o.yTU[j]], writes=[g.bU[bk]])
                P.dve(lambda e, bk=bk, m=m, n=n: e.tensor_tensor(out=g.hT[:, m, n * 512:(n + 1) * 512], in0=g.hT[:, m, n * 512:(n + 1) * 512],
                                                                   in1=bank(g, bk), op=ALU.add),
                      reads=[g.bU[bk], g.hU[m][n]], writes=[g.hU[m][n]])

    load_w(0)
    for h in range(16):
        if h + 1 < 16:
            load_w(h + 1)
        head_lb(h)
        for n in range(NT):
            stageA(h, n)
        for n in range(NT):
            stageB(h, n)
        if h % 2 == 1:
            outproj(h // 2)


def even_ssd(g, li):
    P, L, NB, NT = g.P, g.L, g.NB, g.NT
    ei = li // 2
    c = g.cst
    A = Carver(g)
    s = NS()
    HB = NB * 16
    s.xpre = A.f(515); s.xpreU = P.unit()
    s.cacc = A.f(512); s.caccU = P.unit()
    v3 = lambda ap: ap.rearrange("p (b h) -> p b h", h=16)
    s.dt = A.f(HB); s.atok = A.f(HB); s.acs = A.f(HB); s.eacs = A.f(HB); s.dtd = A.f(HB); s.edl = A.f(HB)
    s.dtU = P.unit()
    s.acsTb = A.f(128); s.acsTbU = P.unit()
    s.Rbd = A.f(512); s.RbdU = P.unit()
    s.CBm = A.f(128); s.CBmU = P.unit()
    s.Dm = A.f(512); s.DmU = P.unit()
    s.E = A.f(512); s.EU = P.unit()
    s.t1 = A.f(256); s.t1U = P.unit()
    s.t2 = A.f(256); s.t2U = P.unit()
    s.S = A.f(256); s.SU = P.unit()
    s.ytmp = A.f(256); s.ytmpU = P.unit()
    s.rstd_all = A.f(L); s.rstdallU = P.unit()
    s.cw = A.f(64); s.cb = A.f(16); s.dtb = A.f(16); s.Abc = A.f(16); s.Dsk = A.f(16); s.snw = A.f(8)
    s.smallU = P.unit()
    s.w = A.b(KC * 768).rearrange("p (k c) -> p k c", k=KC); s.wU = P.unit()
    s.wdt = A.b(KC * 16).rearrange("p (k c) -> p k c", k=KC); s.wdtU = P.unit()
    s.BT = A.b(L); s.CT = A.b(L); s.BCU = [P.unit() for _ in range(NT)]
    s.xtok = A.b(NB * 256).rearrange("p (b c) -> p b c", c=256); s.xtokU = [P.unit() for _ in range(NT)]
    s.Btok = A.b(NB * 128).rearrange("p (b c) -> p b c", c=128); s.BtokU = [P.unit() for _ in range(NT)]
    s.zs = A.b(256); s.zsU = P.unit()
    s.sc = A.b(512).rearrange("p (h l) -> p h l", h=4); s.scU = P.unit()
    s.Xdt = A.b(256); s.XdtU = P.unit()
    s.XB = A.b(256); s.XBU = P.unit()
    s.Sbf = A.b(256); s.SbfU = P.unit()
    s.yTa = A.b(8 * L).rearrange("p (c t) -> p c t", c=8); s.yTaU = [[P.unit() for _ in range(NT)] for _ in range(8)]
    s.wo = A.b(1024).rearrange("p (c j) -> p c j", c=8); s.woU = P.unit()
    s.wos = s.wo; s.wosU = s.woU
    ident = c["ident"]

    sm = [s.smallU]
    P.dma("sp", lambda e: e.dma_start(out=s.cw, in_=g.d["conv_w_cols"][ei]), "s_small", writes=sm)
    P.dma("sp", lambda e: e.dma_start(out=s.cb, in_=g.d["conv_b_cols"][ei]), "s_small", writes=sm)
    P.dma("sp", lambda e: e.dma_start(out=s.dtb, in_=g.d["dt_bias"][ei].partition_broadcast(128)), "s_small", writes=sm)
    P.dma("sp", lambda e: e.dma_start(out=s.Abc, in_=g.d["A_log"][ei].partition_broadcast(128)), "s_small", writes=sm)
    P.dma("sp", lambda e: e.dma_start(out=s.Dsk, in_=g.d["D_skip"][ei].partition_broadcast(128)), "s_small", writes=sm)
    P.dma("sp", lambda e: e.dma_start(out=s.snw, in_=g.d["ssd_norm_w_cols"][ei]), "s_small", writes=sm)
    P.act(lambda e: e.activation(out=s.Abc, in_=s.Abc, func=AF.Exp), reads=sm, writes=sm)
    P.dve(lambda e: e.tensor_scalar(out=s.Abc, in0=s.Abc, scalar1=-1.0, scalar2=None, op0=ALU.mult), reads=sm, writes=sm)
    P.dma("pool", lambda e: e.dma_start(out=s.wdt.rearrange("p k c -> p (k c)"), in_=g.d["ev_w_dt"][ei]), "s_wdt", writes=[s.wdtU])

    for b in range(NB):
        for k in range(KC):
            P.pe(lambda e, b=b, k=k: e.matmul(bank(g, 0)[:, b * 16:(b + 1) * 16], lhsT=g.uT[:, k, b * 128:(b + 1) * 128],
                                              rhs=s.wdt[:, k, :], start=(k == 0), stop=(k == KC - 1)),
                 reads=[g.uU[b // 4], s.wdtU], writes=[g.bU[0]])
    bc_h = lambda t: t.unsqueeze(1).to_broadcast([128, NB, 16])
    du = [s.dtU]
    P.dve(lambda e: e.tensor_tensor(out=v3(s.dt), in0=v3(bank(g, 0)[:, 0:HB]), in1=bc_h(s.dtb), op=ALU.add),
          reads=[g.bU[0]] + sm, writes=du)
    P.act(lambda e: e.activation(out=s.dt, in_=s.dt, func=AF.Exp), reads=du, writes=du)
    P.act(lambda e: e.activation(out=s.dt, in_=s.dt, func=AF.Ln, bias=1.0), reads=du, writes=du)
    P.dve(lambda e: e.tensor_tensor(out=v3(s.atok), in0=v3(s.dt), in1=bc_h(s.Abc), op=ALU.mult), reads=du + sm, writes=du)
    for b in range(NB):
        P.pe(lambda e, b=b: e.matmul(bank(g, 1)[:, b * 16:(b + 1) * 16], lhsT=c["triI"][:], rhs=s.atok[:, b * 16:(b + 1) * 16],
                                     start=True, stop=True), reads=du + [g.cU], writes=[g.bU[1]])
    for b in range(NB):
        P.pe(lambda e, b=b: e.matmul(bank(g, 2)[:, b * 16:(b + 1) * 16], lhsT=c["ones"][:], rhs=s.atok[:, b * 16:(b + 1) * 16],
                                     start=True, stop=True), reads=du + [g.cU], writes=[g.bU[2]])
    P.dve(lambda e: e.tensor_copy(s.acs, bank(g, 1)[:, 0:HB]), reads=[g.bU[1]], writes=du)
    P.act(lambda e: e.activation(out=s.eacs, in_=s.acs, func=AF.Exp), reads=du, writes=du)
    P.dve(lambda e: e.tensor_copy(s.edl, bank(g, 2)[:, 0:HB]), reads=[g.bU[2]], writes=du)
    P.dve(lambda e: e.tensor_tensor(out=s.dtd, in0=s.edl, in1=s.acs, op=ALU.subtract), reads=du, writes=du)
    P.act(lambda e: e.activation(out=s.dtd, in_=s.dtd, func=AF.Exp), reads=du, writes=du)
    P.dve(lambda e: e.tensor_tensor(out=s.dtd, in0=s.dtd, in1=s.dt, op=ALU.mult), reads=du, writes=du)
    P.act(lambda e: e.activation(out=s.edl, in_=s.edl, func=AF.Exp), reads=du, writes=du)

    pcnt = [0]
    for grp in range(4):
        P.dma("pool", lambda e, grp=grp: e.dma_start(out=s.w.rearrange("p k c -> p (k c)"), in_=g.d["ev_w_ssd"][ei, grp]),
              "s_w", writes=[s.wU])
        chunks = [(256, 2 * grp, "x0"), (384, 2 * grp + 1, "x1"), (512, 8 + grp, "B"), (640, 12 + grp, "C")]
        for wc0, cch, kind in chunks:
            for n in range(NT):
                sl = slice(n * 512, (n + 1) * 512)
                bk = 3 + (pcnt[0] % 2)
                pcnt[0] += 1
                for k in range(KC):
                    P.pe(lambda e, bk=bk, k=k, wc0=wc0, sl=sl: e.matmul(bank(g, bk), lhsT=s.w[:, k, wc0:wc0 + 128], rhs=g.uT[:, k, sl],
                                                                        start=(k == 0), stop=(k == KC - 1)),
                         reads=[g.uU[n], s.wU], writes=[g.bU[bk]])
                if n == 0:
                    P.dve(lambda e: e.memset(s.xpre[:, 0:3], 0.0), writes=[s.xpreU])
                else:
                    P.dve(lambda e: e.tensor_copy(s.xpre[:, 0:3], s.xpre[:, 512:515]), reads=[s.xpreU], writes=[s.xpreU])
                P.act(lambda e, bk=bk: e.copy(s.xpre[:, 3:515], bank(g, bk)), reads=[g.bU[bk]], writes=[s.xpreU])
                P.dve(lambda e, cch=cch: e.tensor_scalar(out=s.cacc, in0=s.xpre[:, 3:515], scalar1=s.cw[:, cch * 4 + 3:cch * 4 + 4],
                                                          scalar2=s.cb[:, cch:cch + 1], op0=ALU.mult, op1=ALU.add),
                      reads=[s.xpreU] + sm, writes=[s.caccU])
                for tap in (2, 1, 0):
                    P.dve(lambda e, cch=cch, tap=tap: e.scalar_tensor_tensor(
                        out=s.cacc, in0=s.xpre[:, tap:tap + 512], scalar=s.cw[:, cch * 4 + tap:cch * 4 + tap + 1], in1=s.cacc,
                        op0=ALU.mult, op1=ALU.add), reads=[s.xpreU, s.caccU] + sm, writes=[s.caccU])
                P.act(lambda e: e.activation(out=s.cacc, in_=s.cacc, func=AF.Silu), reads=[s.caccU], writes=[s.caccU])
                if kind in ("B", "C"):
                    dst = s.BT if kind == "B" else s.CT
                    P.dve(lambda e, dst=dst, sl=sl: e.tensor_copy(dst[:, sl], s.cacc), reads=[s.caccU], writes=[s.BCU[n]])
                if kind != "C":
                    for j in range(4):
                        P.pe(lambda e, j=j: e.transpose(bank(g, 5)[:, j * 128:(j + 1) * 128], s.cacc[:, j * 128:(j + 1) * 128], ident[:]),
                             reads=[s.caccU, g.cU], writes=[g.bU[5]])
                    src = bank(g, 5).rearrange("p (b c) -> p b c", b=4)
                    if kind == "B":
                        P.act(lambda e, n=n, src=src: e.copy(s.Btok[:, n * 4:(n + 1) * 4, :], src), reads=[g.bU[5]], writes=[s.BtokU[n]])
                    else:
                        co = 0 if kind == "x0" else 128
                        P.act(lambda e, n=n, src=src, co=co: e.copy(s.xtok[:, n * 4:(n + 1) * 4, co:co + 128], src),
                              reads=[g.bU[5]], writes=[s.xtokU[n]])
        hs4 = slice(4 * grp, 4 * grp + 4)
        for b in range(NB):
            n = b // 4
            blk = slice(b * 128, (b + 1) * 128)
            hcol = lambda t, b=b: v3(t)[:, b, hs4]
            bch = lambda t, w, b=b: hcol(t, b).unsqueeze(2).to_broadcast([128, 4, w])
            x4 = s.xtok[:, b, :].rearrange("p (h q) -> p h q", h=4)
            for k in range(KC):
                P.pe(lambda e, k=k, blk=blk: e.matmul(bank(g, 6)[:, 0:256], lhsT=g.uT[:, k, blk], rhs=s.w[:, k, 0:256],
                                                      start=(k == 0), stop=(k == KC - 1)),
                     reads=[g.uU[n], s.wU], writes=[g.bU[6]])
            P.act(lambda e: e.activation(out=s.zs, in_=bank(g, 6)[:, 0:256], func=AF.Silu), reads=[g.bU[6]], writes=[s.zsU])
            P.pe(lambda e, blk=blk: e.matmul(bank(g, 7)[:, 0:128], lhsT=s.BT[:, blk], rhs=s.CT[:, blk], start=True, stop=True),
                 reads=[s.BCU[n]], writes=[g.bU[7]])
            P.dve(lambda e: e.tensor_tensor(out=s.CBm, in0=bank(g, 7)[:, 0:128], in1=c["triI"][:], op=ALU.mult),
                  reads=[g.bU[7], g.cU], writes=[s.CBmU])
            P.pe(lambda e, b=b: e.transpose(bank(g, 0)[0:16, 0:128], s.acs[:, b * 16:(b + 1) * 16], ident[:]),
                 reads=du + [g.cU], writes=[g.bU[0]])
            P.act(lambda e: e.copy(s.acsTb[0:16, :], bank(g, 0)[0:16, 0:128]), reads=[g.bU[0]], writes=[s.acsTbU])
            P.dve(lambda e: e.tensor_tensor(out=s.Rbd[0:16, :].rearrange("p (h l) -> p h l", h=4),
                                            in0=s.acsTb[0:16, :].unsqueeze(1).to_broadcast([16, 4, 128]),
                                            in1=c["mg16"][:, hs4].unsqueeze(2).to_broadcast([16, 4, 128]), op=ALU.mult),
                  reads=[s.acsTbU, g.cU], writes=[s.RbdU])
            P.pe(lambda e: e.matmul(bank(g, 1), lhsT=c["ones"][0:16, :], rhs=s.Rbd[0:16, :], start=True, stop=True),
                 reads=[s.RbdU, g.cU], writes=[g.bU[1]])
            P.dve(lambda e, b=b: e.tensor_tensor(out=s.Dm.rearrange("p (h l) -> p h l", h=4),
                                                 in0=bank(g, 1).rearrange("p (h l) -> p h l", h=4),
                                                 in1=bch(s.acs, 128, b), op=ALU.subtract),
                  reads=[g.bU[1]] + du, writes=[s.DmU])
            P.dve(lambda e: e.tensor_scalar(out=s.Dm, in0=s.Dm, scalar1=0.0, scalar2=None, op0=ALU.min), reads=[s.DmU], writes=[s.DmU])
            P.act(lambda e: e.activation(out=s.E, in_=s.Dm, func=AF.Exp), reads=[s.DmU], writes=[s.EU])
            P.dve(lambda e: e.tensor_tensor(out=s.sc, in0=s.E.rearrange("p (h l) -> p h l", h=4),
                                            in1=s.CBm.unsqueeze(1).to_broadcast([128, 4, 128]), op=ALU.mult),
                  reads=[s.EU, s.CBmU], writes=[s.scU])
            P.dve(lambda e, b=b: e.tensor_tensor(out=s.Xdt.rearrange("p (h q) -> p h q", h=4), in0=x4, in1=bch(s.dt, 64, b), op=ALU.mult),
                  reads=[s.xtokU[n]] + du, writes=[s.XdtU])
            P.dve(lambda e, b=b: e.tensor_tensor(out=s.XB.rearrange("p (h q) -> p h q", h=4), in0=x4, in1=bch(s.dtd, 64, b), op=ALU.mult),
                  reads=[s.xtokU[n]] + du, writes=[s.XBU])
            for h4 in range(4):
                P.pe(lambda e, h4=h4: e.matmul(bank(g, 2)[:, h4 * 64:(h4 + 1) * 64], lhsT=s.sc[:, h4, :], rhs=s.Xdt[:, h4 * 64:(h4 + 1) * 64],
                                               start=True, stop=True), reads=[s.scU, s.XdtU], writes=[g.bU[2]])
            if b > 0:
                P.pe(lambda e, blk=blk: e.matmul(bank(g, 3)[:, 0:256], lhsT=s.CT[:, blk], rhs=s.Sbf, start=True, stop=True),
                     reads=[s.BCU[n], s.SbfU], writes=[g.bU[3]])
            P.pe(lambda e, b=b: e.matmul(bank(g, 4)[:, 0:256], lhsT=s.Btok[:, b, :], rhs=s.XB, start=True, stop=True),
                 reads=[s.BtokU[n], s.XBU], writes=[g.bU[4]])
            if b > 0:
                P.dve(lambda e, b=b: e.tensor_tensor(out=s.t1.rearrange("p (h q) -> p h q", h=4),
                                                     in0=bank(g, 3)[:, 0:256].rearrange("p (h q) -> p h q", h=4),
                                                     in1=bch(s.eacs, 64, b), op=ALU.mult), reads=[g.bU[3]] + du, writes=[s.t1U])
                P.dve(lambda e: e.tensor_tensor(out=s.t2, in0=bank(g, 2)[:, 0:256], in1=s.t1, op=ALU.add), reads=[g.bU[2], s.t1U], writes=[s.t2U])
            else:
                P.dve(lambda e: e.tensor_copy(s.t2, bank(g, 2)[:, 0:256]), reads=[g.bU[2]], writes=[s.t2U])
            P.dve(lambda e: e.tensor_tensor(out=s.t1.rearrange("p (h q) -> p h q", h=4), in0=x4,
                                            in1=s.Dsk[:, hs4].unsqueeze(2).to_broadcast([128, 4, 64]), op=ALU.mult),
                  reads=[s.xtokU[n]] + sm, writes=[s.t1U])
            P.dve(lambda e: e.tensor_tensor(out=s.t2, in0=s.t2, in1=s.t1, op=ALU.add), reads=[s.t1U, s.t2U], writes=[s.t2U])
            P.dve(lambda e: e.tensor_tensor(out=s.ytmp, in0=s.t2, in1=s.zs, op=ALU.mult), reads=[s.t2U, s.zsU], writes=[s.ytmpU])
            for j in range(2):
                P.pe(lambda e, j=j: e.transpose(bank(g, 5)[:, j * 128:(j + 1) * 128], s.ytmp[:, j * 128:(j + 1) * 128], ident[:]),
                     reads=[s.ytmpU, g.cU], writes=[g.bU[5]])
            P.act(lambda e, blk=blk: e.copy(s.yTa[:, 2 * grp:2 * grp + 2, blk], bank(g, 5)[:, 0:256].rearrange("p (j t) -> p j t", j=2)),
                  reads=[g.bU[5]], writes=[s.yTaU[2 * grp][n], s.yTaU[2 * grp + 1][n]])
            if b == 0:
                P.dve(lambda e: e.tensor_copy(s.S, bank(g, 4)[:, 0:256]), reads=[g.bU[4]], writes=[s.SU])
            else:
                P.dve(lambda e, b=b: e.tensor_tensor(out=s.S.rearrange("p (h q) -> p h q", h=4), in0=s.S.rearrange("p (h q) -> p h q", h=4),
                                                     in1=bch(s.edl, 64, b), op=ALU.mult), reads=[s.SU] + du, writes=[s.SU])
                P.dve(lambda e: e.tensor_tensor(out=s.S, in0=s.S, in1=bank(g, 4)[:, 0:256], op=ALU.add), reads=[s.SU, g.bU[4]], writes=[s.SU])
            if b + 1 < NB:
                P.act(lambda e: e.copy(s.Sbf, s.S), reads=[s.SU], writes=[s.SbfU])

    for n in range(NT):
        sl = slice(n * 512, (n + 1) * 512)
        rms_rstd_tile(g, lambda k, sl=sl: s.yTa[:, k, sl], lambda k, n=n: [s.yTaU[k][n]], 8, 1024)
        P.dve(lambda e, sl=sl: e.tensor_copy(s.rstd_all[:, sl], g.rstd_t[:]), reads=[g.rstdU], writes=[s.rstdallU])
    cnt = 0
    for m in range(KC):
        P.dma("pool", lambda e, m=m: e.dma_start(out=s.wo.rearrange("p c j -> p (c j)"), in_=g.d["ev_w_out_t"][ei, m, :, 0:1024]),
              "s_wo", writes=[s.woU])
        P.dve(lambda e: e.tensor_tensor(out=s.wos, in0=s.wo, in1=s.snw[:, 0:8].unsqueeze(2).to_broadcast([128, 8, 128]), op=ALU.mult),
              reads=[s.woU] + sm, writes=[s.wosU])
        for n in range(NT):
            sl = slice(n * 512, (n + 1) * 512)
            bk = cnt % 2
            cnt += 1
            for k in range(8):
                P.pe(lambda e, bk=bk, k=k, sl=sl: e.matmul(bank(g, bk), lhsT=s.wos[:, k, :], rhs=s.yTa[:, k, sl], start=(k == 0), stop=(k == 7)),
                     reads=[s.wosU, s.yTaU[k][n]], writes=[g.bU[bk]])
            P.dve(lambda e, bk=bk, sl=sl: e.tensor_tensor(out=s.cacc, in0=bank(g, bk), in1=s.rstd_all[:, sl], op=ALU.mult),
                  reads=[g.bU[bk], s.rstdallU], writes=[s.caccU])
            P.pool(lambda e, m=m, sl=sl: e.tensor_tensor(out=g.hT[:, m, sl], in0=g.hT[:, m, sl], in1=s.cacc, op=ALU.add),
                   reads=[s.caccU, g.hU[m][n]], writes=[g.hU[m][n]])


def even_attn(g, li):
    P, L, NB, NT = g.P, g.L, g.NB, g.NT
    ei = li // 2
    lam_init = 0.8 - 0.6 * math.exp(-0.3 * li)
    c = g.cst
    A = Carver(g)
    a = NS()
    a.corr = A.f(2048).rearrange("p (h d q) -> p h d q", h=8, d=2); a.corrU = P.unit()
    a.b31 = A.f(8); a.nb31 = A.f(8); a.bU_ = P.unit()
    a.lq = [A.f(64) for _ in range(4)]; a.lamU = P.unit()
    a.lam = A.f(8)
    a.slnw = A.f(128); a.slnwU = P.unit()
    a.r = A.f(8); a.rU = P.unit()
    a.t = A.f(128); a.tU = P.unit()
    a.o = A.f(128); a.oU = P.unit()
    a.sqo = A.f(128); a.sqoU = P.unit()
    a.y = A.f(128); a.yU = P.unit()
    a.w = [A.b(KC * 512).rearrange("p (k c) -> p k c", k=KC) for _ in range(2)]; a.wU = P.units(2)
    a.qT = [A.b(L) for _ in range(2)]; a.qTU = [P.unit() for _ in range(NT)]
    a.kT = A.b(L); a.kTU = [P.unit() for _ in range(NT)]
    a.v = A.b(NB * 132).rearrange("p (b c) -> p b c", c=132); a.vU = P.unit()
    a.gs = A.b(NB * 128).rearrange("p (b c) -> p b c", c=128); a.gsU = P.unit()
    a.PT = [[A.b(512) for _ in range(2)] for _ in range(2)]; a.PTU = [P.units(2), P.units(2)]
    a.yT = A.b(4 * L).rearrange("p (j t) -> p j t", j=4); a.yTU = P.units(4)
    a.wo = [A.b(512).rearrange("p (j c) -> p j c", j=4) for _ in range(2)]; a.woU = P.units(2)
    ident = c["ident"]

    P.dma("sp", lambda e: e.dma_start(out=a.corr.rearrange("p h d q -> p (h d q)"), in_=g.d["rel_biasD"]), "a_corr", writes=[a.corrU])
    P.dma("sp", lambda e: e.dma_start(out=a.b31, in_=g.d["rel_b31"].partition_broadcast(128)), "a_b31", writes=[a.bU_])
    P.dve(lambda e: e.tensor_scalar(out=a.nb31, in0=a.b31, scalar1=-1.0, scalar2=None, op0=ALU.mult), reads=[a.bU_], writes=[a.bU_])
    for h in range(8):
        P.act(lambda e, h=h: e.activation(out=a.corr[:, h], in_=a.corr[:, h], func=AF.Exp, bias=a.nb31[:, h:h + 1]),
              reads=[a.corrU, a.bU_], writes=[a.corrU])
    for i, nm in enumerate(("lambda_q1", "lambda_k1", "lambda_q2", "lambda_k2")):
        P.dma("sp", lambda e, i=i, nm=nm: e.dma_start(out=a.lq[i], in_=g.d[nm][ei].partition_broadcast(128)), "a_lam", writes=[a.lamU])
    lu = [a.lamU]
    P.dve(lambda e: e.tensor_tensor(out=a.lq[0], in0=a.lq[0], in1=a.lq[1], op=ALU.mult), reads=lu, writes=lu)
    P.dve(lambda e: e.tensor_tensor(out=a.lq[2], in0=a.lq[2], in1=a.lq[3], op=ALU.mult), reads=lu, writes=lu)
    P.dve(lambda e: e.tensor_reduce(out=a.lam[:, 0:1], in_=a.lq[0], axis=AX.X, op=ALU.add), reads=lu, writes=lu)
    P.dve(lambda e: e.tensor_reduce(out=a.lam[:, 1:2], in_=a.lq[2], axis=AX.X, op=ALU.add), reads=lu, writes=lu)
    P.act(lambda e: e.activation(out=a.lam[:, 2:4], in_=a.lam[:, 0:2], func=AF.Exp), reads=lu, writes=lu)
    P.dve(lambda e: e.tensor_tensor(out=a.lam[:, 4:5], in0=a.lam[:, 3:4], in1=a.lam[:, 2:3], op=ALU.subtract), reads=lu, writes=lu)
    P.dve(lambda e: e.tensor_scalar(out=a.lam[:, 5:6], in0=a.lam[:, 4:5], scalar1=-lam_init, scalar2=None, op0=ALU.add), reads=lu, writes=lu)
    P.dma("sp", lambda e: e.dma_start(out=a.slnw, in_=g.d["subln_w"][ei].partition_broadcast(128)), "a_slnw", writes=[a.slnwU])
    P.dve(lambda e: e.tensor_scalar(out=a.slnw, in0=a.slnw, scalar1=1.0 - lam_init, scalar2=None, op0=ALU.mult),
          reads=[a.slnwU], writes=[a.slnwU])
    P.dve(lambda e: e.memset(a.v, 1.0), writes=[a.vU])

    def load_w(h):
        i = h % 2
        P.dma("pool", lambda e: e.dma_start(out=a.w[i].rearrange("p k c -> p (k c)"), in_=g.d["ev_w_att"][ei, h]),
              f"a_w{i}", writes=[a.wU[i]])

    pc = [0]

    def project(h):
        wi = h % 2
        w = a.w[wi]
        for n in range(NT):
            sl = slice(n * 512, (n + 1) * 512)
            for which in range(2):
                bk = pc[0] % 2
                pc[0] += 1
                for k in range(KC):
                    P.pe(lambda e, bk=bk, k=k, sl=sl, which=which: e.matmul(bank(g, bk), lhsT=w[:, k, which * 128:(which + 1) * 128],
                                                                            rhs=g.uT[:, k, sl], start=(k == 0), stop=(k == KC - 1)),
                         reads=[g.uU[n], a.wU[wi]], writes=[g.bU[bk]])
                if which == 0:
                    for cc in range(2):
                        P.dve(lambda e, bk=bk, sl=sl, cc=cc: e.tensor_scalar(out=a.qT[cc][:, sl], in0=bank(g, bk), scalar1=c["maskq"][:, cc:cc + 1],
                                                                             scalar2=None, op0=ALU.mult),
                              reads=[g.bU[bk], g.cU], writes=[a.qTU[n]])
                else:
                    P.act(lambda e, bk=bk, sl=sl: e.copy(a.kT[:, sl], bank(g, bk)), reads=[g.bU[bk]], writes=[a.kTU[n]])
        for b in range(NB):
            bk = 2 + (b % 2)
            for k in range(KC):
                P.pe(lambda e, bk=bk, k=k, b=b: e.matmul(bank(g, bk)[:, 0:256], lhsT=g.uT[:, k, b * 128:(b + 1) * 128], rhs=w[:, k, 256:512],
                                                         start=(k == 0), stop=(k == KC - 1)),
                     reads=[g.uU[b // 4], a.wU[wi]], writes=[g.bU[bk]])
            P.act(lambda e, bk=bk, b=b: e.copy(a.v[:, b, 0:128], bank(g, bk)[:, 0:128]), reads=[g.bU[bk]], writes=[a.vU])
            P.act(lambda e, bk=bk, b=b: e.activation(out=a.gs[:, b, :], in_=bank(g, bk)[:, 128:256], func=AF.Silu),
                  reads=[g.bU[bk]], writes=[a.gsU])

    gc = [0]

    def attend(h):
        hj = h % 4
        for qb in range(NB):
            qn = qb // 4
            qs = slice(qb * 128, (qb + 1) * 128)
            ngrp = qb // 4 + 1
            for gi in range(ngrp):
                kbs = [kb for kb in range(gi * 4, gi * 4 + 4) if kb <= qb]
                nv = len(kbs)
                buf = gc[0] % 2
                gc[0] += 1
                for cc in range(2):
                    bk = 4 + 2 * cc + buf
                    for j, kb in enumerate(kbs):
                        P.pe(lambda e, bk=bk, j=j, kb=kb, cc=cc: e.matmul(bank(g, bk)[:, j * 128:(j + 1) * 128],
                                                                          lhsT=a.kT[:, kb * 128:(kb + 1) * 128], rhs=a.qT[cc][:, qs],
                                                                          start=True, stop=True),
                             reads=[a.kTU[kb // 4], a.qTU[qn]], writes=[g.bU[bk]])
                for cc in range(2):
                    bk = 4 + 2 * cc + buf
                    pt = a.PT[cc][buf]
                    ptu = a.PTU[cc][buf]
                    P.act(lambda e, bk=bk, pt=pt, nv=nv: e.activation(out=pt[:, 0:nv * 128], in_=bank(g, bk)[:, 0:nv * 128], func=AF.Exp,
                                                                      scale=0.125, bias=a.b31[:, h:h + 1]),
                          reads=[g.bU[bk], a.bU_], writes=[ptu])
                    for j, kb in enumerate(kbs):
                        Dd = qb - kb
                        if Dd <= 1:
                            P.dve(lambda e, pt=pt, j=j, Dd=Dd: e.tensor_tensor(out=pt[:, j * 128:(j + 1) * 128], in0=pt[:, j * 128:(j + 1) * 128],
                                                                               in1=a.corr[:, h, Dd, :], op=ALU.mult),
                                  reads=[ptu, a.corrU], writes=[ptu])
                for cc in range(2):
                    pt = a.PT[cc][buf]
                    ptu = a.PTU[cc][buf]
                    for j, kb in enumerate(kbs):
                        P.pe(lambda e, cc=cc, pt=pt, j=j, kb=kb: e.matmul(bank(g, 2 + cc)[:, 0:129], lhsT=pt[:, j * 128:(j + 1) * 128],
                                                                          rhs=a.v[:, kb, 0:129], start=(kb == 0), stop=(kb == qb)),
                             reads=[ptu, a.vU], writes=[g.bU[2 + cc]])
            ru = [a.rU]
            P.dve(lambda e: e.reciprocal(a.r[:, 0:1], bank(g, 2)[:, 128:129]), reads=[g.bU[2]], writes=ru)
            P.dve(lambda e: e.reciprocal(a.r[:, 1:2], bank(g, 3)[:, 128:129]), reads=[g.bU[3]], writes=ru)
            P.dve(lambda e: e.tensor_tensor(out=a.r[:, 2:3], in0=a.r[:, 1:2], in1=a.lam[:, 5:6], op=ALU.mult), reads=ru + lu, writes=ru)
            P.dve(lambda e: e.tensor_scalar(out=a.t, in0=bank(g, 2)[:, 0:128], scalar1=a.r[:, 0:1], scalar2=None, op0=ALU.mult),
                  reads=[g.bU[2]] + ru, writes=[a.tU])
            P.dve(lambda e: e.scalar_tensor_tensor(out=a.o, in0=bank(g, 3)[:, 0:128], scalar=a.r[:, 2:3], in1=a.t, op0=ALU.mult, op1=ALU.add),
                  reads=[g.bU[3], a.tU] + ru, writes=[a.oU])
            P.dve(lambda e: e.tensor_tensor(out=a.sqo, in0=a.o, in1=a.o, op=ALU.mult), reads=[a.oU], writes=[a.sqoU])
            P.dve(lambda e: e.tensor_reduce(out=a.r[:, 3:4], in_=a.sqo, axis=AX.X, op=ALU.add), reads=[a.sqoU], writes=ru)
            P.act(lambda e: e.activation(out=a.r[:, 4:5], in_=a.r[:, 3:4], func=AF.Ln, scale=1.0 / 128, bias=EPS), reads=ru, writes=ru)
            P.act(lambda e: e.activation(out=a.r[:, 5:6], in_=a.r[:, 4:5], func=AF.Exp, scale=-0.5), reads=ru, writes=ru)
            P.dve(lambda e: e.scalar_tensor_tensor(out=a.y, in0=a.o, scalar=a.r[:, 5:6], in1=a.slnw, op0=ALU.mult, op1=ALU.mult),
                  reads=[a.oU, a.slnwU] + ru, writes=[a.yU])
            P.dve(lambda e, qb=qb: e.tensor_tensor(out=a.y, in0=a.y, in1=a.gs[:, qb, :], op=ALU.mult), reads=[a.yU, a.gsU], writes=[a.yU])
            P.pe(lambda e: e.transpose(bank(g, 0)[:, 0:128], a.y, ident[:]), reads=[a.yU, g.cU], writes=[g.bU[0]])
            P.act(lambda e, qs=qs: e.copy(a.yT[:, hj, qs], bank(g, 0)[:, 0:128]), reads=[g.bU[0]], writes=[a.yTU[hj]])

    oc = [0]

    def outproj(hg):
        for m in range(KC):
            wi = oc[0] % 2
            oc[0] += 1
            c0 = (8 + hg * 4) * 128
            P.dma("pool", lambda e, m=m, wi=wi, c0=c0: e.dma_start(out=a.wo[wi].rearrange("p j c -> p (j c)"),
                                                                  in_=g.d["ev_w_out_t"][ei, m, :, c0:c0 + 512]),
                  f"a_wo{wi}", writes=[a.woU[wi]])
            for n in range(NT):
                sl = slice(n * 512, (n + 1) * 512)
                bk = n % 2
                for j in range(4):
                    P.pe(lambda e, bk=bk, j=j, sl=sl, wi=wi: e.matmul(bank(g, bk), lhsT=a.wo[wi][:, j, :], rhs=a.yT[:, j, sl],
                                                                      start=(j == 0), stop=(j == 3)),
                         reads=[a.woU[wi], a.yTU[j]], writes=[g.bU[bk]])
                P.dve(lambda e, bk=bk, m=m, sl=sl: e.tensor_tensor(out=g.hT[:, m, sl], in0=g.hT[:, m, sl], in1=bank(g, bk), op=ALU.add),
                      reads=[g.bU[bk], g.hU[m][n]], writes=[g.hU[m][n]])

    load_w(0)
    for h in range(8):
        if h + 1 < 8:
            load_w(h + 1)
        project(h)
        attend(h)
        if h % 4 == 3:
            outproj(h // 4)


_CACHE = {}


def kernel(**inputs):
    x = np.ascontiguousarray(np.asarray(inputs["x"], dtype=np.float32))
    Bsz, L, _ = x.shape
    n_cores = 8
    nseq = Bsz // n_cores
    key = (L, nseq)
    if key not in _CACHE:
        _CACHE[key] = build(L, nseq, (0, 1, 2, 3))
    nc, _ = _CACHE[key]
    common = host_layout(inputs)
    in_maps = []
    for cidx in range(n_cores):
        m = dict(common)
        m["x"] = x[cidx * nseq:(cidx + 1) * nseq]
        in_maps.append(m)
    res = run_bass_kernel_spmd(nc, in_maps, core_ids=list(range(n_cores)))
    out = np.concatenate([np.asarray(r["out"]) for r in res.results], axis=0)
    return out.astype(np.float32)
```

```python
import math, contextlib
import numpy as np
import concourse.bass as bass
import concourse.mybir as mybir
from concourse.bass_utils import run_bass_kernel_spmd
from concourse.alu_op_type import AluOpType as ALU

F32 = mybir.dt.float32
BF16 = mybir.dt.bfloat16
AF = mybir.ActivationFunctionType
AX = mybir.AxisListType

D = 1024
KC = 8
EPS = 1e-6
DEPTH = 4
HG_W = 2048
ARF_N = 7616
ARB_N = 35840


class Unit:
    __slots__ = ("name", "lw", "rd")

    def __init__(self, name):
        self.name = name
        self.lw = None
        self.rd = []


class _Rec:
    def __init__(self):
        self.call = None

    def __getattr__(self, name):
        def f(*args, **kw):
            assert self.call is None
            self.call = (name, args, kw)
            return None
        return f


class Prog:
    ENGS = ("pe", "act", "dve", "pool", "sp")

    def __init__(self, nc):
        self.nc = nc
        self.ops = []
        self.nunits = 0
        self.last_eng = {}
        self.last_key = {}

    def unit(self, name=None):
        self.nunits += 1
        return Unit(name or f"u{self.nunits}")

    def units(self, n, name="u"):
        return [self.unit(f"{name}{i}") for i in range(n)]

    def op(self, eng, fn, reads=(), writes=(), dma_key=None, extra_deps=()):
        idx = len(self.ops)
        deps = set(extra_deps)
        for u in reads:
            if u.lw is not None:
                deps.add(u.lw)
        for u in writes:
            if u.lw is not None:
                deps.add(u.lw)
            deps.update(u.rd)
        for u in reads:
            u.rd.append(idx)
        for u in writes:
            u.lw = idx
            u.rd = []
        deps.discard(idx)
        if fn is not None:
            rec = _Rec()
            fn(rec)
            assert rec.call is not None
            fn = rec.call
        self.ops.append(dict(eng=eng, fn=fn, deps=deps, dma_key=dma_key))
        if fn is not None:
            if dma_key is None:
                self.last_eng[eng] = idx
            else:
                self.last_key[dma_key] = idx
        return idx

    def pe(self, fn, reads=(), writes=()):
        return self.op("pe", fn, reads, writes)

    def act(self, fn, reads=(), writes=()):
        return self.op("act", fn, reads, writes)

    def dve(self, fn, reads=(), writes=()):
        return self.op("dve", fn, reads, writes)

    def pool(self, fn, reads=(), writes=()):
        return self.op("pool", fn, reads, writes)

    def dma(self, eng, fn, key, reads=(), writes=()):
        return self.op(eng, fn, reads, writes, dma_key=key)

    def barrier(self):
        deps = set(self.last_eng.values()) | set(self.last_key.values())
        for e in self.ENGS:
            self.op(e, None, extra_deps=deps)

    def emit(self, final_wait_ops=()):
        nc = self.nc
        ops = self.ops
        n = len(ops)

        def skip(od, o):
            return (od["eng"] == "pe" and o["eng"] == "pe" and od["dma_key"] is None
                    and o["dma_key"] is None and o["fn"] is not None)

        needed = [False] * n
        for i, o in enumerate(ops):
            for d in o["deps"]:
                if skip(ops[d], o):
                    continue
                needed[d] = True
        for d in final_wait_ops:
            needed[d] = True
        chan_count = {}
        ev = [None] * n
        for i, o in enumerate(ops):
            if o["fn"] is None:
                continue
            if o["dma_key"] is not None:
                ch = ("dma", o["dma_key"])
                chan_count[ch] = chan_count.get(ch, 0) + 16
                ev[i] = (ch, chan_count[ch])
            elif needed[i]:
                ch = ("eng", o["eng"])
                chan_count[ch] = chan_count.get(ch, 0) + 1
                ev[i] = (ch, chan_count[ch])
        chans = sorted(chan_count.keys(), key=str)
        self.n_sems = len(chans)
        sems = {}
        stack = contextlib.ExitStack()
        for ci, ch in enumerate(chans):
            sems[ch] = stack.enter_context(nc.semaphore(f"s{ci}"))
        known = {e: {} for e in self.ENGS}
        clock = [None] * n
        streams = {e: [] for e in self.ENGS}
        for i, o in enumerate(ops):
            e = o["eng"]
            kn = known[e]
            wd = {}
            for d in sorted(o["deps"]):
                od = ops[d]
                if skip(od, o):
                    continue
                ch, v = ev[d]
                if kn.get(ch, 0) >= v:
                    continue
                for c2, v2 in clock[d].items():
                    if kn.get(c2, 0) < v2:
                        kn[c2] = v2
                wd[ch] = max(wd.get(ch, 0), v)
            ck = dict(kn)
            if ev[i] is not None:
                ch, v = ev[i]
                ck[ch] = v
            clock[i] = ck
            streams[e].append((list(wd.items()), o["fn"], ev[i]))
        final = [ev[d] for d in final_wait_ops]
        for ch, tot in chan_count.items():
            if ch[0] == "dma":
                final.append((ch, tot))
        self.sems, self.streams, self.final, self._stack = sems, streams, final, stack

    def run_block(self):
        nc = self.nc
        sems, streams, final = self.sems, self.streams, self.final
        with nc.Block() as block:
            def mk(ename):
                def body(eng):
                    for waits, fn, e in streams[ename]:
                        for ch, v in waits:
                            eng.wait_ge(sems[ch], v)
                        if fn is None:
                            continue
                        ins = getattr(eng, fn[0])(*fn[1], **fn[2])
                        if e is not None:
                            ins.then_inc(sems[e[0]], 16 if e[0][0] == "dma" else 1)
                    if ename == "sp":
                        for ch, v in final:
                            eng.wait_ge(sems[ch], v)
                return body
            block.tensor(mk("pe"))
            block.scalar(mk("act"))
            block.vector(mk("dve"))
            block.gpsimd(mk("pool"))
            block.sync(mk("sp"))
        self._stack.close()


def _t5_bucket(rel):
    n = np.maximum(rel, 0)
    max_exact = 16
    large = max_exact + (np.log(np.maximum(n, 1).astype(np.float32) / max_exact)
                         / math.log(128 / max_exact) * (32 - max_exact)).astype(np.int32)
    large = np.minimum(large, 31)
    return np.where(n < max_exact, n, large)


def host_consts():
    s = np.arange(128)[:, None]
    t = np.arange(128)[None, :]
    c = {}
    c["ident"] = np.eye(128, dtype=np.float32)
    c["ones"] = np.ones((128, 128), np.float32)
    c["triC"] = ((s <= t).astype(np.float32) - (s <= 63).astype(np.float32))
    c["triU"] = (s > t).astype(np.float32)
    c["triI"] = (s <= t).astype(np.float32)
    sel = np.zeros((128, 2), np.float32)
    sel[:64, 0] = 1.0
    sel[:, 1] = 1.0
    c["sel"] = sel
    c["mg16"] = np.eye(16, dtype=np.float32)
    mq = np.zeros((128, 2), np.float32)
    mq[:64, 0] = 1.0
    mq[64:, 1] = 1.0
    c["maskq"] = mq
    return c


def host_layout(inp):
    f = lambda a: np.ascontiguousarray(np.asarray(a, dtype=np.float32))
    m = dict(host_consts())
    m["final_norm_w"] = f(inp["final_norm_w"])
    m["norm_w_cols"] = f(np.asarray(inp["norm_w"]).reshape(4, 8, 128).transpose(0, 2, 1))
    owin = np.asarray(inp["odd_w_in"])
    t = owin.reshape(2, 8, 128, 4, 16, 128).transpose(0, 4, 2, 1, 3, 5)
    m["odd_w_in_t"] = f(t).reshape(2, 16, 128, 8 * 512)
    m["odd_w_out"] = f(inp["odd_w_out"])
    m["hgrn_lower_bounds"] = f(inp["hgrn_lower_bounds"])
    m["hgrn_norm_w"] = f(inp["hgrn_norm_w"])
    ew = np.asarray(inp["even_w_in"]).reshape(2, 8, 128, 7184)
    z = ew[..., 0:1024]; xs = ew[..., 1024:2048]; Bm = ew[..., 2048:2560]; Cm = ew[..., 2560:3072]
    dt = ew[..., 3072:3088]
    q = ew[..., 3088:4112]; kk = ew[..., 4112:5136]; v = ew[..., 5136:6160]; gg = ew[..., 6160:7184]
    ssd = np.concatenate([z.reshape(2, 8, 128, 4, 256), xs.reshape(2, 8, 128, 4, 256),
                          Bm.reshape(2, 8, 128, 4, 128), Cm.reshape(2, 8, 128, 4, 128)], axis=-1)
    m["ev_w_ssd"] = f(ssd.transpose(0, 3, 2, 1, 4)).reshape(2, 4, 128, 8 * 768)
    m["ev_w_dt"] = f(dt.transpose(0, 2, 1, 3)).reshape(2, 128, 8 * 16)
    att = np.concatenate([q.reshape(2, 8, 128, 8, 128), kk.reshape(2, 8, 128, 8, 128),
                          v.reshape(2, 8, 128, 8, 128), gg.reshape(2, 8, 128, 8, 128)], axis=-1)
    m["ev_w_att"] = f(att.transpose(0, 3, 2, 1, 4)).reshape(2, 8, 128, 8 * 512)
    wo = np.asarray(inp["even_w_out"]).reshape(2, 16, 128, 8, 128)
    m["ev_w_out_t"] = f(wo.transpose(0, 3, 2, 1, 4)).reshape(2, 8, 128, 16 * 128)
    m["conv_w_cols"] = f(np.asarray(inp["conv_w"]).reshape(2, 4, 16, 128).transpose(0, 3, 2, 1)).reshape(2, 128, 64)
    m["conv_b_cols"] = f(np.asarray(inp["conv_b"]).reshape(2, 16, 128).transpose(0, 2, 1))
    for nm in ("dt_bias", "A_log", "D_skip", "lambda_q1", "lambda_k1", "lambda_q2", "lambda_k2", "subln_w"):
        m[nm] = f(inp[nm])
    m["ssd_norm_w_cols"] = f(np.asarray(inp["ssd_norm_w"]).reshape(2, 8, 128).transpose(0, 2, 1))
    rb = np.asarray(inp["rel_bias"], dtype=np.float32)
    kpos = np.arange(128)[:, None]
    qpos = np.arange(128)[None, :]
    bd = np.empty((128, 8, 2, 128), np.float32)
    for Dd in range(2):
        rel = qpos - kpos + 128 * Dd
        bidx = _t5_bucket(rel)
        g_ = rb[bidx]
        g_ = np.where((rel >= 0)[:, :, None], g_, np.float32(-30000.0))
        bd[:, :, Dd, :] = g_.transpose(0, 2, 1)
    m["rel_biasD"] = f(bd).reshape(128, 8 * 2 * 128)
    m["rel_b31"] = f(rb[31])
    return m


class NS:
    pass


class Carver:
    def __init__(self, g):
        self.g = g
        self.fo = 0
        self.bo = 0

    def f(self, n):
        ap = self.g.arf[:, self.fo:self.fo + n]
        self.fo += (n + 7) // 8 * 8
        assert self.fo <= ARF_N, ("ARF overflow", self.fo)
        return ap

    def b(self, n):
        ap = self.g.arb[:, self.bo:self.bo + n]
        self.bo += (n + 15) // 16 * 16
        assert self.bo <= ARB_N, ("ARB overflow", self.bo)
        return ap


def bank(g, i):
    return g.ps[:, i, :]


def build(L=2048, NSEQ=2, layers=(0, 1, 2, 3)):
    nc = bass.Bass("TRN2", target_bir_lowering=False)
    NT, NB = L // 512, L // 128
    g = NS()
    g.nc, g.L, g.NT, g.NB = nc, L, NT, NB
    dr = lambda name, shape, kind="ExternalInput": nc.dram_tensor(name, list(shape), F32, kind=kind).ap()
    g.x_d = dr("x", [NSEQ, L, D])
    g.out_d = dr("out", [NSEQ, L, D], "ExternalOutput")
    g.d = {}
    shapes = {
        "final_norm_w": [D], "norm_w_cols": [DEPTH, 128, KC],
        "ident": [128, 128], "ones": [128, 128], "triC": [128, 128], "triU": [128, 128], "triI": [128, 128],
        "sel": [128, 2], "mg16": [16, 16], "maskq": [128, 2],
        "odd_w_in_t": [2, 16, 128, KC * 512], "odd_w_out": [2, HG_W, D], "hgrn_lower_bounds": [DEPTH, HG_W],
        "hgrn_norm_w": [2, 128],
        "ev_w_ssd": [2, 4, 128, 8 * 768], "ev_w_dt": [2, 128, 8 * 16], "ev_w_att": [2, 8, 128, 8 * 512],
        "ev_w_out_t": [2, 8, 128, 16 * 128], "conv_w_cols": [2, 128, 64], "conv_b_cols": [2, 128, 16],
        "dt_bias": [2, 16], "A_log": [2, 16], "D_skip": [2, 16], "lambda_q1": [2, 64], "lambda_k1": [2, 64],
        "lambda_q2": [2, 64], "lambda_k2": [2, 64], "subln_w": [2, 128], "ssd_norm_w_cols": [2, 128, 8],
        "rel_biasD": [128, 8 * 2 * 128], "rel_b31": [8],
    }
    for nm, shp in shapes.items():
        g.d[nm] = dr(nm, shp)
    g.in_names = ["x"] + list(shapes.keys())

    es = contextlib.ExitStack()
    sb = lambda name, shape, dt=F32: es.enter_context(nc.sbuf_tensor(name, list(shape), dt))
    P = Prog(nc)
    g.P = P
    g.hT = sb("hT", [128, KC, L]); g.hU = [[P.unit(f"h{k}_{n}") for n in range(NT)] for k in range(KC)]
    g.uT = sb("uT", [128, KC, L], BF16); g.uU = [P.unit(f"u{n}") for n in range(NT)]
    g.cst = {}
    g.cU = P.unit("consts")
    for nm in ("ident", "ones", "triI"):
        g.cst[nm] = sb("c_" + nm, [128, 128])
    g.cst["sel"] = sb("c_sel", [128, 2])
    g.cst["maskq"] = sb("c_maskq", [128, 2])
    g.cst["mg16"] = sb("c_mg16", [16, 16])
    g.nwc = sb("nwc", [128, DEPTH, KC])
    g.stat = sb("stat", [128, 16]); g.statU = P.unit("stat")
    g.sq = [sb(f"sq{i}", [128, 512]) for i in range(2)]; g.sqU = P.units(2, "sq")
    g.rstd_t = sb("rstd_t", [128, 512]); g.rstdU = P.unit("rstd_t")
    g.arf = sb("arf", [128, ARF_N])
    g.arb = sb("arb", [128, ARB_N], BF16)
    g.ps = es.enter_context(nc.psum_tensor("ps", [128, 8, 512], F32))
    g.bU = P.units(8, "bank")

    for nm in ("ident", "ones", "triI", "sel", "maskq", "mg16"):
        P.dma("sp", lambda e, nm=nm: e.dma_start(out=g.cst[nm][:], in_=g.d[nm]), "c_" + nm, writes=[g.cU])
    P.dma("sp", lambda e: e.dma_start(out=g.nwc[:], in_=g.d["norm_w_cols"].rearrange("l p k -> p l k")), "c_nwc", writes=[g.cU])

    out_ops = []
    for s in range(NSEQ):
        P.barrier()
        load_x(g, s)
        for li in layers:
            rms_to_uT(g, li)
            P.barrier()
            if li % 2 == 1:
                odd_layer(g, li)
            else:
                even_ssd(g, li)
                P.barrier()
                even_attn(g, li)
            P.barrier()
        out_ops += final_norm_store(g, s)
    P.emit(final_wait_ops=out_ops[-4:])
    P.run_block()
    es.close()
    return nc, P


def load_x(g, s):
    P, NT = g.P, g.NT
    A = Carver(g)
    xst = A.f(4096).rearrange("p (b d) -> p b d", b=4)
    xU = P.unit("xst")
    ident = g.cst["ident"]
    for n in range(NT):
        src = g.x_d[s, n * 512:(n + 1) * 512, :].rearrange("(b p) d -> p b d", p=128)
        P.dma("sp", lambda e, src=src: e.dma_start(out=xst, in_=src), "xst", writes=[xU])
        for k in range(KC):
            bk = k % 8
            for b in range(4):
                P.pe(lambda e, bk=bk, b=b, k=k: e.transpose(
                    bank(g, bk)[:, b * 128:(b + 1) * 128], xst[:, b, k * 128:(k + 1) * 128], ident[:]),
                    reads=[xU, g.cU], writes=[g.bU[bk]])
            if k % 2 == 0:
                P.dve(lambda e, bk=bk, k=k, n=n: e.tensor_copy(g.hT[:, k, n * 512:(n + 1) * 512], bank(g, bk)),
                      reads=[g.bU[bk]], writes=[g.hU[k][n]])
            else:
                P.act(lambda e, bk=bk, k=k, n=n: e.copy(g.hT[:, k, n * 512:(n + 1) * 512], bank(g, bk)),
                      reads=[g.bU[bk]], writes=[g.hU[k][n]])


def final_norm_store(g, s):
    P, NB = g.P, g.NB
    A = Carver(g)
    fnw = A.f(D); fnwU = P.unit("fnw")
    ost = [A.f(D) for _ in range(2)]; ostU = P.units(2, "ost")
    junk = A.f(1024); junkU = P.unit("junk")
    ident = g.cst["ident"]
    P.dma("sp", lambda e: e.dma_start(out=fnw, in_=g.d["final_norm_w"].partition_broadcast(128)), "fnw", writes=[fnwU])
    outs = []
    for b in range(NB):
        n = b // 4
        oi = b % 2
        for k in range(KC):
            bk = k // 4
            P.pe(lambda e, bk=bk, k=k, b=b: e.transpose(
                bank(g, bk)[:, (k % 4) * 128:(k % 4 + 1) * 128], g.hT[:, k, b * 128:(b + 1) * 128], ident[:]),
                reads=[g.hU[k][n], g.cU], writes=[g.bU[bk]])
        for half in range(2):
            P.act(lambda e, half=half: e.activation(
                out=junk[:, half * 512:(half + 1) * 512], in_=bank(g, half), func=AF.Square,
                accum_out=g.stat[:, half:half + 1]),
                reads=[g.bU[half]], writes=[junkU, g.statU])
        P.dve(lambda e: e.tensor_tensor(out=g.stat[:, 2:3], in0=g.stat[:, 0:1], in1=g.stat[:, 1:2], op=ALU.add),
              reads=[g.statU], writes=[g.statU])
        P.act(lambda e: e.activation(out=g.stat[:, 3:4], in_=g.stat[:, 2:3], func=AF.Ln, scale=1.0 / D, bias=EPS),
              reads=[g.statU], writes=[g.statU])
        P.act(lambda e: e.activation(out=g.stat[:, 4:5], in_=g.stat[:, 3:4], func=AF.Exp, scale=-0.5),
              reads=[g.statU], writes=[g.statU])
        for half in range(2):
            P.dve(lambda e, half=half, oi=oi: e.scalar_tensor_tensor(
                out=ost[oi][:, half * 512:(half + 1) * 512], in0=bank(g, half), scalar=g.stat[:, 4:5],
                in1=fnw[:, half * 512:(half + 1) * 512], op0=ALU.mult, op1=ALU.mult),
                reads=[g.bU[half], g.statU, fnwU], writes=[ostU[oi]])
        o = P.dma("sp", lambda e, oi=oi, s=s, b=b: e.dma_start(out=g.out_d[s, b * 128:(b + 1) * 128, :], in_=ost[oi]),
                  f"ost{oi}", reads=[ostU[oi]])
        outs.append(o)
    return outs


def rms_rstd_tile(g, src_fn, reads_fn, nchunks, dim):
    P = g.P
    ones = g.cst["ones"]
    for k in range(nchunks):
        i = k % 2
        P.act(lambda e, i=i, k=k: e.activation(out=g.sq[i][:], in_=src_fn(k), func=AF.Square),
              reads=reads_fn(k), writes=[g.sqU[i]])
        P.pe(lambda e, i=i, k=k: e.matmul(bank(g, 7), lhsT=ones[:], rhs=g.sq[i][:], start=(k == 0), stop=(k == nchunks - 1)),
             reads=[g.sqU[i], g.cU], writes=[g.bU[7]])
    P.act(lambda e: e.activation(out=g.rstd_t[:], in_=bank(g, 7), func=AF.Ln, scale=1.0 / dim, bias=EPS),
          reads=[g.bU[7]], writes=[g.rstdU])
    P.act(lambda e: e.activation(out=g.rstd_t[:], in_=g.rstd_t[:], func=AF.Exp, scale=-0.5),
          reads=[g.rstdU], writes=[g.rstdU])


def rms_to_uT(g, li):
    P, NT = g.P, g.NT
    for n in range(NT):
        sl = slice(n * 512, (n + 1) * 512)
        rms_rstd_tile(g, lambda k, sl=sl: g.hT[:, k, sl], lambda k, n=n: [g.hU[k][n]], KC, D)
        for k in range(KC):
            P.dve(lambda e, k=k, sl=sl: e.scalar_tensor_tensor(
                out=g.uT[:, k, sl], in0=g.hT[:, k, sl], scalar=g.nwc[:, li, k:k + 1], in1=g.rstd_t[:],
                op0=ALU.mult, op1=ALU.mult),
                reads=[g.hU[k][n], g.rstdU, g.cU], writes=[g.uU[n]])


def odd_layer(g, li):
    P, L, NB, NT = g.P, g.L, g.NB, g.NT
    oi = li // 2
    A = Carver(g)
    o = NS()
    c = g.cst
    f3 = lambda: A.f(512).rearrange("p (b d) -> p b d", b=4)
    o.logf = f3(); o.logfU = P.unit()
    o.tA = f3(); o.tAU = P.unit()
    o.kk = f3(); o.kkU = P.unit()
    o.qs = f3(); o.qsU = P.unit()
    o.e13 = A.f(1024).rearrange("p (t b d) -> p t b d", t=2, b=4); o.e13U = P.unit()
    o.e2 = f3(); o.e2U = P.unit()
    o.qt = o.qs; o.qtU = o.qsU
    o.kt = o.e2; o.ktU = o.e2U
    o.lbr = f3(); o.lbrU = P.unit()
    o.lbh = A.f(128); o.omlh = A.f(128); o.den = A.f(128); o.lbU = P.unit()
    o.eb = [A.f(NB * 2).rearrange("p (b t) -> p b t", t=2) for _ in range(2)]
    o.S = A.f(128); o.SU = P.unit()
    o.junk2 = A.f(128); o.junk2U = P.unit()
    o.oall = A.f(NB * 128).rearrange("p (b d) -> p b d", d=128); o.oallU = P.unit()
    o.ssall = A.f(NB); o.rsall = A.f(NB); o.ssU = P.unit()
    o.hnw = A.f(128); o.hnwU = P.unit()
    o.triC = A.f(128); o.triU = A.f(128); o.triUU = P.unit()
    o.bst = A.f(8); o.bstU = P.unit()
    o.w = [A.b(KC * 512).rearrange("p (k c) -> p k c", k=KC) for _ in range(2)]; o.wU = P.units(2)
    o.wout = A.b(2 * D).rearrange("p (j m) -> p j m", j=2); o.woutU = P.units(2)
    o.qT = [A.b(L) for _ in range(2)]
    o.kT = [A.b(L) for _ in range(2)]
    hb3 = lambda: A.b(NB * 128).rearrange("p (b d) -> p b d", d=128)
    o.kh = [hb3() for _ in range(2)]
    o.v = [hb3() for _ in range(2)]
    o.gs = [hb3() for _ in range(2)]
    o.hbU = [[[P.unit() for _ in range(NT)] for _ in range(6)] for _ in range(2)]
    o.attm = [A.b(128) for _ in range(2)]; o.attmU = P.units(2)
    o.Sb = A.b(128); o.SbU = P.unit()
    o.yT = A.b(2 * L).rearrange("p (j t) -> p j t", j=2); o.yTU = P.units(2)
    QT, KT, KH, VV, GS, EB = range(6)

    P.dma("sp", lambda e: e.dma_start(out=o.hnw, in_=g.d["hgrn_norm_w"][oi].partition_broadcast(128)), "o_hnw", writes=[o.hnwU])
    P.dma("sp", lambda e: e.dma_start(out=o.triC, in_=g.d["triC"]), "o_tri", writes=[o.triUU])
    P.dma("sp", lambda e: e.dma_start(out=o.triU, in_=g.d["triU"]), "o_tri", writes=[o.triUU])
    for i in range(2):
        P.dve(lambda e, i=i: e.memset(o.attm[i], 0.0), writes=[o.attmU[i]])

    def load_w(h):
        i = h % 2
        P.dma("pool", lambda e: e.dma_start(out=o.w[i].rearrange("p k c -> p (k c)"), in_=g.d["odd_w_in_t"][oi, h]),
              f"o_w{i}", writes=[o.wU[i]])

    def head_lb(h):
        hs = slice(h * 128, (h + 1) * 128)
        P.dma("sp", lambda e: e.dma_start(out=o.lbr, in_=g.d["hgrn_lower_bounds"][:, hs].partition_broadcast(128)),
              "o_lbr", writes=[o.lbrU])
        P.act(lambda e: e.activation(out=o.lbr, in_=o.lbr, func=AF.Exp), reads=[o.lbrU], writes=[o.lbrU])
        P.dve(lambda e: e.tensor_tensor(out=o.den, in0=o.lbr[:, 0, :], in1=o.lbr[:, 1, :], op=ALU.add), reads=[o.lbrU], writes=[o.lbU])
        P.dve(lambda e: e.tensor_tensor(out=o.den, in0=o.den, in1=o.lbr[:, 2, :], op=ALU.add), reads=[o.lbrU, o.lbU], writes=[o.lbU])
        P.dve(lambda e: e.tensor_tensor(out=o.den, in0=o.den, in1=o.lbr[:, 3, :], op=ALU.add), reads=[o.lbrU, o.lbU], writes=[o.lbU])
        P.dve(lambda e: e.reciprocal(o.den, o.den), reads=[o.lbU], writes=[o.lbU])
        if li == 1:
            P.dve(lambda e: e.tensor_tensor(out=o.lbh, in0=o.lbr[:, 1, :], in1=o.den, op=ALU.mult), reads=[o.lbrU, o.lbU], writes=[o.lbU])
        else:
            P.dve(lambda e: e.tensor_tensor(out=o.lbh, in0=o.lbr[:, 1, :], in1=o.lbr[:, 2, :], op=ALU.add), reads=[o.lbrU, o.lbU], writes=[o.lbU])
            for j in range(3, li + 1):
                P.dve(lambda e, j=j: e.tensor_tensor(out=o.lbh, in0=o.lbh, in1=o.lbr[:, j, :], op=ALU.add), reads=[o.lbrU, o.lbU], writes=[o.lbU])
            P.dve(lambda e: e.tensor_tensor(out=o.lbh, in0=o.lbh, in1=o.den, op=ALU.mult), reads=[o.lbU], writes=[o.lbU])
        P.dve(lambda e: e.tensor_scalar(out=o.omlh, in0=o.lbh, scalar1=-1.0, scalar2=1.0, op0=ALU.mult, op1=ALU.add),
              reads=[o.lbU], writes=[o.lbU])

    def stageA(h, n):
        hb = h % 2
        wi = h % 2
        U = o.hbU[hb]
        for b in range(4):
            tb = n * 4 + b
            for k in range(KC):
                P.pe(lambda e, b=b, tb=tb, k=k: e.matmul(bank(g, b), lhsT=g.uT[:, k, tb * 128:(tb + 1) * 128], rhs=o.w[wi][:, k, :],
                                                         start=(k == 0), stop=(k == KC - 1)),
                     reads=[g.uU[n], o.wU[wi]], writes=[g.bU[b]])
        pj = g.ps[:, 0:4, :]
        pb = [g.bU[0], g.bU[1], g.bU[2], g.bU[3]]
        bc4 = lambda t: t.unsqueeze(1).to_broadcast([128, 4, 128])
        P.act(lambda e: e.activation(out=o.tA, in_=pj[:, :, 128:256], func=AF.Sigmoid), reads=pb, writes=[o.tAU])
        P.dve(lambda e: e.tensor_tensor(out=o.tA, in0=o.tA, in1=bc4(o.omlh), op=ALU.mult), reads=[o.tAU, o.lbU], writes=[o.tAU])
        P.dve(lambda e: e.tensor_tensor(out=o.tA, in0=o.tA, in1=bc4(o.lbh), op=ALU.add), reads=[o.tAU, o.lbU], writes=[o.tAU])
        P.act(lambda e: e.activation(out=o.logf, in_=o.tA, func=AF.Ln), reads=[o.tAU], writes=[o.logfU])
        P.pool(lambda e: e.tensor_scalar(out=o.kk, in0=o.tA, scalar1=-1.0, scalar2=1.0, op0=ALU.mult, op1=ALU.add),
               reads=[o.tAU], writes=[o.kkU])
        P.act(lambda e: e.activation(out=o.qs, in_=pj[:, :, 0:128], func=AF.Silu), reads=pb, writes=[o.qsU])
        P.act(lambda e: e.activation(out=o.gs[hb][:, n * 4:(n + 1) * 4, :], in_=pj[:, :, 384:512], func=AF.Silu),
              reads=pb, writes=[U[GS][n]])
        P.act(lambda e: e.copy(o.v[hb][:, n * 4:(n + 1) * 4, :], pj[:, :, 256:384]), reads=pb, writes=[U[VV][n]])
        for b in range(4):
            P.pe(lambda e, b=b: e.matmul(bank(g, 4)[:, b * 128:(b + 1) * 128], lhsT=o.triC, rhs=o.logf[:, b, :], start=True, stop=True),
                 reads=[o.logfU, o.triUU], writes=[g.bU[4]])
        for b in range(4):
            P.pe(lambda e, b=b: e.matmul(bank(g, 5)[:, b * 128:(b + 1) * 128], lhsT=o.triU, rhs=o.logf[:, b, :], start=True, stop=True),
                 reads=[o.logfU, o.triUU], writes=[g.bU[5]])
        for b in range(4):
            P.pe(lambda e, b=b: e.matmul(bank(g, 6)[:, b * 2:b * 2 + 2], lhsT=o.logf[:, b, :], rhs=c["sel"][:], start=True, stop=True),
                 reads=[o.logfU, g.cU], writes=[g.bU[6]])
        P.act(lambda e: e.activation(out=o.eb[hb][:, n * 4:(n + 1) * 4, :], in_=bank(g, 6)[:, 0:8].rearrange("p (b t) -> p b t", t=2), func=AF.Exp),
              reads=[g.bU[6]], writes=[U[EB][n]])
        P.act(lambda e: e.activation(out=o.e13, in_=g.ps[:, 4:6, :].rearrange("p t (b d) -> p t b d", b=4), func=AF.Exp),
              reads=[g.bU[4], g.bU[5]], writes=[o.e13U])
        P.act(lambda e: e.activation(out=o.e2, in_=bank(g, 4).rearrange("p (b d) -> p b d", b=4), func=AF.Exp, scale=-1.0),
              reads=[g.bU[4]], writes=[o.e2U])
        P.dve(lambda e: e.tensor_tensor(out=o.qs, in0=o.qs, in1=o.e13[:, 0], op=ALU.mult), reads=[o.qsU, o.e13U], writes=[o.qsU])
        P.pool(lambda e: e.tensor_tensor(out=o.e2, in0=o.kk, in1=o.e2, op=ALU.mult), reads=[o.kkU, o.e2U], writes=[o.e2U])
        P.dve(lambda e: e.tensor_tensor(out=o.kh[hb][:, n * 4:(n + 1) * 4, :], in0=o.kk, in1=o.e13[:, 1], op=ALU.mult),
              reads=[o.kkU, o.e13U], writes=[U[KH][n]])
        idf = c["ident"]
        for b in range(4):
            P.pe(lambda e, b=b: e.transpose(bank(g, 7)[:, b * 128:(b + 1) * 128], o.qt[:, b, :], idf[:]),
                 reads=[o.qtU, g.cU], writes=[g.bU[7]])
        for b in range(4):
            P.pe(lambda e, b=b: e.transpose(bank(g, 6)[:, b * 128:(b + 1) * 128], o.kt[:, b, :], idf[:]),
                 reads=[o.ktU, g.cU], writes=[g.bU[6]])
        P.dve(lambda e: e.tensor_copy(o.qT[hb][:, n * 512:(n + 1) * 512], bank(g, 7)), reads=[g.bU[7]], writes=[U[QT][n]])
        P.act(lambda e: e.copy(o.kT[hb][:, n * 512:(n + 1) * 512], bank(g, 6)), reads=[g.bU[6]], writes=[U[KT][n]])

    def stageB(h, n):
        hb = h % 2
        U = o.hbU[hb]
        hj = h % 2
        for b in range(4):
            tb = n * 4 + b
            ts = slice(tb * 128, (tb + 1) * 128)
            ai = tb % 2
            P.pe(lambda e, ts=ts, tb=tb: e.matmul(bank(g, 5)[:, 64:128], lhsT=o.kT[hb][:, ts],
                                                   rhs=o.qT[hb][:, tb * 128 + 64:(tb + 1) * 128], start=True, stop=True),
                 reads=[U[QT][n], U[KT][n]], writes=[g.bU[5]])
            P.pe(lambda e, tb=tb: e.matmul(bank(g, 5)[0:64, 0:64], lhsT=o.kT[hb][:, tb * 128:tb * 128 + 64],
                                           rhs=o.qT[hb][:, tb * 128:tb * 128 + 64], start=True, stop=True),
                 reads=[U[QT][n], U[KT][n]], writes=[g.bU[5]])
            P.dve(lambda e, ai=ai: e.tensor_tensor(out=o.attm[ai][:, 64:128], in0=bank(g, 5)[:, 64:128], in1=c["triI"][:, 64:128], op=ALU.mult),
                  reads=[g.bU[5], g.cU], writes=[o.attmU[ai]])
            P.dve(lambda e, ai=ai: e.tensor_tensor(out=o.attm[ai][0:64, 0:64], in0=bank(g, 5)[0:64, 0:64], in1=c["triI"][0:64, 0:64], op=ALU.mult),
                  reads=[g.bU[5], g.cU], writes=[o.attmU[ai]])
            if tb > 0:
                P.dve(lambda e, tb=tb: e.tensor_scalar(out=o.Sb, in0=o.S, scalar1=o.eb[hb][:, tb, 0:1], scalar2=None, op0=ALU.mult),
                      reads=[o.SU, U[EB][n]], writes=[o.SbU])
            P.pe(lambda e, ai=ai, tb=tb: e.matmul(bank(g, 4)[:, 0:128], lhsT=o.attm[ai], rhs=o.v[hb][:, tb, :], start=True, stop=(tb == 0)),
                 reads=[o.attmU[ai], U[VV][n]], writes=[g.bU[4]])
            if tb > 0:
                P.pe(lambda e, ts=ts: e.matmul(bank(g, 4)[:, 0:128], lhsT=o.qT[hb][:, ts], rhs=o.Sb, start=False, stop=True),
                     reads=[o.SbU, U[QT][n]], writes=[g.bU[4]])
            P.pe(lambda e, tb=tb: e.matmul(bank(g, 6)[:, 128:256], lhsT=o.kh[hb][:, tb, :], rhs=o.v[hb][:, tb, :], start=True, stop=True),
                 reads=[U[KH][n], U[VV][n]], writes=[g.bU[6]])
            if tb == 0:
                P.dve(lambda e: e.tensor_copy(o.S, bank(g, 6)[:, 128:256]), reads=[g.bU[6]], writes=[o.SU])
            else:
                P.dve(lambda e, tb=tb: e.scalar_tensor_tensor(out=o.S, in0=o.S, scalar=o.eb[hb][:, tb, 1:2], in1=bank(g, 6)[:, 128:256],
                                                               op0=ALU.mult, op1=ALU.add),
                      reads=[o.SU, g.bU[6], U[EB][n]], writes=[o.SU])
            P.act(lambda e, tb=tb: e.copy(o.oall[:, tb, :], bank(g, 4)[:, 0:128]), reads=[g.bU[4]], writes=[o.oallU])
            P.act(lambda e, tb=tb: e.activation(out=o.junk2, in_=bank(g, 4)[:, 0:128], func=AF.Square, accum_out=o.ssall[:, tb:tb + 1]),
                  reads=[g.bU[4]], writes=[o.junk2U, o.ssU])

    def stageC(h):
        hb = h % 2
        U = o.hbU[hb]
        hj = h % 2
        P.act(lambda e: e.activation(out=o.rsall, in_=o.ssall, func=AF.Ln, scale=1.0 / 128, bias=EPS), reads=[o.ssU], writes=[o.ssU])
        P.act(lambda e: e.activation(out=o.rsall, in_=o.rsall, func=AF.Exp, scale=-0.5), reads=[o.ssU], writes=[o.ssU])
        P.dve(lambda e: e.tensor_tensor(out=o.oall, in0=o.oall, in1=o.rsall.unsqueeze(2).to_broadcast([128, NB, 128]), op=ALU.mult),
              reads=[o.oallU, o.ssU], writes=[o.oallU])
        P.dve(lambda e: e.tensor_tensor(out=o.oall, in0=o.oall, in1=o.hnw.unsqueeze(1).to_broadcast([128, NB, 128]), op=ALU.mult),
              reads=[o.oallU, o.hnwU], writes=[o.oallU])
        P.dve(lambda e: e.tensor_tensor(out=o.oall, in0=o.oall, in1=o.gs[hb], op=ALU.mult),
              reads=[o.oallU] + [U[GS][n] for n in range(NT)], writes=[o.oallU])
        for n in range(NT):
            bk = 4 + (n % 4)
            for b in range(4):
                P.pe(lambda e, bk=bk, b=b, n=n: e.transpose(bank(g, bk)[:, b * 128:(b + 1) * 128], o.oall[:, n * 4 + b, :], c["ident"][:]),
                     reads=[o.oallU, g.cU], writes=[g.bU[bk]])
            P.act(lambda e, bk=bk, n=n: e.copy(o.yT[:, hj, n * 512:(n + 1) * 512], bank(g, bk)), reads=[g.bU[bk]], writes=[o.yTU[hj]])

    def outproj(hp):
        for j in range(2):
            src = g.d["odd_w_out"][oi, (hp * 2 + j) * 128:(hp * 2 + j + 1) * 128, :]
            P.dma("pool", lambda e, j=j, src=src: e.dma_start(out=o.wout[:, j, :], in_=src), f"o_wout{j}", writes=[o.woutU[j]])
        cnt = 0
        for m in range(KC):
            for n in range(NT):
                bk = cnt % 4
                cnt += 1
                for j in range(2):
                    P.pe(lambda e, bk=bk, m=m, n=n, j=j: e.matmul(bank(g, bk), lhsT=o.wout[:, j, m * 128:(m + 1) * 128],
                                                                     rhs=o.yT[:, j, n * 512:(n + 1) * 512], start=(j == 0), stop=(j == 1)),
                         reads=[o.woutU[j], o.yTU[j]], writes=[g.bU[bk]])
                P.dve(lambda e, bk=bk, m=m, n=n: e.tensor_tensor(out=g.hT[:, m, n * 512:(n + 1) * 512], in0=g.hT[:, m, n * 512:(n + 1) * 512],
                                                                   in1=bank(g, bk), op=ALU.add),
                      reads=[g.bU[bk], g.hU[m][n]], writes=[g.hU[m][n]])

    load_w(0)
    for h in range(16):
        if h + 1 < 16:
            load_w(h + 1)
        head_lb(h)
        for n in range(NT):
            stageA(h, n)
        for n in range(NT):
            stageB(h, n)
        stageC(h)
        if h % 2 == 1:
            outproj(h // 2)


def even_ssd(g, li):
    P, L, NB, NT = g.P, g.L, g.NB, g.NT
    ei = li // 2
    c = g.cst
    A = Carver(g)
    s = NS()
    HB = NB * 16
    s.xpre = A.f(515); s.xpreU = P.unit()
    s.cacc = A.f(512); s.caccU = P.unit()
    v3 = lambda ap: ap.rearrange("p (b h) -> p b h", h=16)
    s.dt = A.f(HB); s.atok = A.f(HB); s.acs = A.f(HB); s.eacs = A.f(HB); s.dtd = A.f(HB); s.edl = A.f(HB)
    s.dtU = P.unit()
    s.acsTb = A.f(128); s.acsTbU = P.unit()
    s.Rbd = A.f(512); s.RbdU = P.unit()
    s.CBm = A.f(128); s.CBmU = P.unit()
    s.Dm = A.f(512); s.DmU = P.unit()
    s.E = A.f(512); s.EU = P.unit()
    s.t1 = A.f(256); s.t1U = P.unit()
    s.t2 = A.f(256); s.t2U = P.unit()
    s.S = A.f(256); s.SU = P.unit()
    s.ytmp = A.f(256); s.ytmpU = P.unit()
    s.rstd_all = A.f(L); s.rstdallU = P.unit()
    s.cw = A.f(64); s.cb = A.f(16); s.dtb = A.f(16); s.Abc = A.f(16); s.Dsk = A.f(16); s.snw = A.f(8)
    s.smallU = P.unit()
    s.w = A.b(KC * 768).rearrange("p (k c) -> p k c", k=KC); s.wU = P.unit()
    s.wdt = A.b(KC * 16).rearrange("p (k c) -> p k c", k=KC); s.wdtU = P.unit()
    s.BT = A.b(L); s.CT = A.b(L); s.BCU = [P.unit() for _ in range(NT)]
    s.xtok = A.b(NB * 256).rearrange("p (b c) -> p b c", c=256); s.xtokU = [P.unit() for _ in range(NT)]
    s.Btok = A.b(NB * 128).rearrange("p (b c) -> p b c", c=128); s.BtokU = [P.unit() for _ in range(NT)]
    s.zs = A.b(256); s.zsU = P.unit()
    s.sc = A.b(512).rearrange("p (h l) -> p h l", h=4); s.scU = P.unit()
    s.Xdt = A.b(256); s.XdtU = P.unit()
    s.XB = A.b(256); s.XBU = P.unit()
    s.Sbf = A.b(256); s.SbfU = P.unit()
    s.yTa = A.b(8 * L).rearrange("p (c t) -> p c t", c=8); s.yTaU = [[P.unit() for _ in range(NT)] for _ in range(8)]
    s.wo = A.b(1024).rearrange("p (c j) -> p c j", c=8); s.woU = P.unit()
    s.wos = s.wo; s.wosU = s.woU
    ident = c["ident"]

    sm = [s.smallU]
    P.dma("sp", lambda e: e.dma_start(out=s.cw, in_=g.d["conv_w_cols"][ei]), "s_small", writes=sm)
    P.dma("sp", lambda e: e.dma_start(out=s.cb, in_=g.d["conv_b_cols"][ei]), "s_small", writes=sm)
    P.dma("sp", lambda e: e.dma_start(out=s.dtb, in_=g.d["dt_bias"][ei].partition_broadcast(128)), "s_small", writes=sm)
    P.dma("sp", lambda e: e.dma_start(out=s.Abc, in_=g.d["A_log"][ei].partition_broadcast(128)), "s_small", writes=sm)
    P.dma("sp", lambda e: e.dma_start(out=s.Dsk, in_=g.d["D_skip"][ei].partition_broadcast(128)), "s_small", writes=sm)
    P.dma("sp", lambda e: e.dma_start(out=s.snw, in_=g.d["ssd_norm_w_cols"][ei]), "s_small", writes=sm)
    P.act(lambda e: e.activation(out=s.Abc, in_=s.Abc, func=AF.Exp), reads=sm, writes=sm)
    P.dve(lambda e: e.tensor_scalar(out=s.Abc, in0=s.Abc, scalar1=-1.0, scalar2=None, op0=ALU.mult), reads=sm, writes=sm)
    P.dma("pool", lambda e: e.dma_start(out=s.wdt.rearrange("p k c -> p (k c)"), in_=g.d["ev_w_dt"][ei]), "s_wdt", writes=[s.wdtU])

    for b in range(NB):
        for k in range(KC):
            P.pe(lambda e, b=b, k=k: e.matmul(bank(g, 0)[:, b * 16:(b + 1) * 16], lhsT=g.uT[:, k, b * 128:(b + 1) * 128],
                                              rhs=s.wdt[:, k, :], start=(k == 0), stop=(k == KC - 1)),
                 reads=[g.uU[b // 4], s.wdtU], writes=[g.bU[0]])
    bc_h = lambda t: t.unsqueeze(1).to_broadcast([128, NB, 16])
    du = [s.dtU]
    P.dve(lambda e: e.tensor_tensor(out=v3(s.dt), in0=v3(bank(g, 0)[:, 0:HB]), in1=bc_h(s.dtb), op=ALU.add),
          reads=[g.bU[0]] + sm, writes=du)
    P.act(lambda e: e.activation(out=s.dt, in_=s.dt, func=AF.Exp), reads=du, writes=du)
    P.act(lambda e: e.activation(out=s.dt, in_=s.dt, func=AF.Ln, bias=1.0), reads=du, writes=du)
    P.dve(lambda e: e.tensor_tensor(out=v3(s.atok), in0=v3(s.dt), in1=bc_h(s.Abc), op=ALU.mult), reads=du + sm, writes=du)
    for b in range(NB):
        P.pe(lambda e, b=b: e.matmul(bank(g, 1)[:, b * 16:(b + 1) * 16], lhsT=c["triI"][:], rhs=s.atok[:, b * 16:(b + 1) * 16],
                                     start=True, stop=True), reads=du + [g.cU], writes=[g.bU[1]])
    for b in range(NB):
        P.pe(lambda e, b=b: e.matmul(bank(g, 2)[:, b * 16:(b + 1) * 16], lhsT=c["ones"][:], rhs=s.atok[:, b * 16:(b + 1) * 16],
                                     start=True, stop=True), reads=du + [g.cU], writes=[g.bU[2]])
    P.dve(lambda e: e.tensor_copy(s.acs, bank(g, 1)[:, 0:HB]), reads=[g.bU[1]], writes=du)
    P.act(lambda e: e.activation(out=s.eacs, in_=s.acs, func=AF.Exp), reads=du, writes=du)
    P.dve(lambda e: e.tensor_copy(s.edl, bank(g, 2)[:, 0:HB]), reads=[g.bU[2]], writes=du)
    P.dve(lambda e: e.tensor_tensor(out=s.dtd, in0=s.edl, in1=s.acs, op=ALU.subtract), reads=du, writes=du)
    P.act(lambda e: e.activation(out=s.dtd, in_=s.dtd, func=AF.Exp), reads=du, writes=du)
    P.dve(lambda e: e.tensor_tensor(out=s.dtd, in0=s.dtd, in1=s.dt, op=ALU.mult), reads=du, writes=du)
    P.act(lambda e: e.activation(out=s.edl, in_=s.edl, func=AF.Exp), reads=du, writes=du)

    pcnt = [0]
    for grp in range(4):
        P.dma("pool", lambda e, grp=grp: e.dma_start(out=s.w.rearrange("p k c -> p (k c)"), in_=g.d["ev_w_ssd"][ei, grp]),
              "s_w", writes=[s.wU])
        chunks = [(256, 2 * grp, "x0"), (384, 2 * grp + 1, "x1"), (512, 8 + grp, "B"), (640, 12 + grp, "C")]
        for wc0, cch, kind in chunks:
            for n in range(NT):
                sl = slice(n * 512, (n + 1) * 512)
                bk = 3 + (pcnt[0] % 2)
                pcnt[0] += 1
                for k in range(KC):
                    P.pe(lambda e, bk=bk, k=k, wc0=wc0, sl=sl: e.matmul(bank(g, bk), lhsT=s.w[:, k, wc0:wc0 + 128], rhs=g.uT[:, k, sl],
                                                                        start=(k == 0), stop=(k == KC - 1)),
                         reads=[g.uU[n], s.wU], writes=[g.bU[bk]])
                if n == 0:
                    P.dve(lambda e: e.memset(s.xpre[:, 0:3], 0.0), writes=[s.xpreU])
                else:
                    P.dve(lambda e: e.tensor_copy(s.xpre[:, 0:3], s.xpre[:, 512:515]), reads=[s.xpreU], writes=[s.xpreU])
                P.act(lambda e, bk=bk: e.copy(s.xpre[:, 3:515], bank(g, bk)), reads=[g.bU[bk]], writes=[s.xpreU])
                P.dve(lambda e, cch=cch: e.tensor_scalar(out=s.cacc, in0=s.xpre[:, 3:515], scalar1=s.cw[:, cch * 4 + 3:cch * 4 + 4],
                                                          scalar2=s.cb[:, cch:cch + 1], op0=ALU.mult, op1=ALU.add),
                      reads=[s.xpreU] + sm, writes=[s.caccU])
                for tap in (2, 1, 0):
                    P.dve(lambda e, cch=cch, tap=tap: e.scalar_tensor_tensor(
                        out=s.cacc, in0=s.xpre[:, tap:tap + 512], scalar=s.cw[:, cch * 4 + tap:cch * 4 + tap + 1], in1=s.cacc,
                        op0=ALU.mult, op1=ALU.add), reads=[s.xpreU, s.caccU] + sm, writes=[s.caccU])
                P.act(lambda e: e.activation(out=s.cacc, in_=s.cacc, func=AF.Silu), reads=[s.caccU], writes=[s.caccU])
                if kind in ("B", "C"):
                    dst = s.BT if kind == "B" else s.CT
                    P.dve(lambda e, dst=dst, sl=sl: e.tensor_copy(dst[:, sl], s.cacc), reads=[s.caccU], writes=[s.BCU[n]])
                if kind != "C":
                    for j in range(4):
                        P.pe(lambda e, j=j: e.transpose(bank(g, 5)[:, j * 128:(j + 1) * 128], s.cacc[:, j * 128:(j + 1) * 128], ident[:]),
                             reads=[s.caccU, g.cU], writes=[g.bU[5]])
                    src = bank(g, 5).rearrange("p (b c) -> p b c", b=4)
                    if kind == "B":
                        P.act(lambda e, n=n, src=src: e.copy(s.Btok[:, n * 4:(n + 1) * 4, :], src), reads=[g.bU[5]], writes=[s.BtokU[n]])
                    else:
                        co = 0 if kind == "x0" else 128
                        P.act(lambda e, n=n, src=src, co=co: e.copy(s.xtok[:, n * 4:(n + 1) * 4, co:co + 128], src),
                              reads=[g.bU[5]], writes=[s.xtokU[n]])
        hs4 = slice(4 * grp, 4 * grp + 4)
        for b in range(NB):
            n = b // 4
            blk = slice(b * 128, (b + 1) * 128)
            hcol = lambda t, b=b: v3(t)[:, b, hs4]
            bch = lambda t, w, b=b: hcol(t, b).unsqueeze(2).to_broadcast([128, 4, w])
            x4 = s.xtok[:, b, :].rearrange("p (h q) -> p h q", h=4)
            for k in range(KC):
                P.pe(lambda e, k=k, blk=blk: e.matmul(bank(g, 6)[:, 0:256], lhsT=g.uT[:, k, blk], rhs=s.w[:, k, 0:256],
                                                      start=(k == 0), stop=(k == KC - 1)),
                     reads=[g.uU[n], s.wU], writes=[g.bU[6]])
            P.act(lambda e: e.activation(out=s.zs, in_=bank(g, 6)[:, 0:256], func=AF.Silu), reads=[g.bU[6]], writes=[s.zsU])
            P.pe(lambda e, blk=blk: e.matmul(bank(g, 7)[:, 0:128], lhsT=s.BT[:, blk], rhs=s.CT[:, blk], start=True, stop=True),
                 reads=[s.BCU[n]], writes=[g.bU[7]])
            P.dve(lambda e: e.tensor_tensor(out=s.CBm, in0=bank(g, 7)[:, 0:128], in1=c["triI"][:], op=ALU.mult),
                  reads=[g.bU[7], g.cU], writes=[s.CBmU])
            P.pe(lambda e, b=b: e.transpose(bank(g, 0)[0:16, 0:128], s.acs[:, b * 16:(b + 1) * 16], ident[:]),
                 reads=du + [g.cU], writes=[g.bU[0]])
            P.act(lambda e: e.copy(s.acsTb[0:16, :], bank(g, 0)[0:16, 0:128]), reads=[g.bU[0]], writes=[s.acsTbU])
            P.dve(lambda e: e.tensor_tensor(out=s.Rbd[0:16, :].rearrange("p (h l) -> p h l", h=4),
                                            in0=s.acsTb[0:16, :].unsqueeze(1).to_broadcast([16, 4, 128]),
                                            in1=c["mg16"][:, hs4].unsqueeze(2).to_broadcast([16, 4, 128]), op=ALU.mult),
                  reads=[s.acsTbU, g.cU], writes=[s.RbdU])
            P.pe(lambda e: e.matmul(bank(g, 1), lhsT=c["ones"][0:16, :], rhs=s.Rbd[0:16, :], start=True, stop=True),
                 reads=[s.RbdU, g.cU], writes=[g.bU[1]])
            P.dve(lambda e, b=b: e.tensor_tensor(out=s.Dm.rearrange("p (h l) -> p h l", h=4),
                                                 in0=bank(g, 1).rearrange("p (h l) -> p h l", h=4),
                                                 in1=bch(s.acs, 128, b), op=ALU.subtract),
                  reads=[g.bU[1]] + du, writes=[s.DmU])
            P.dve(lambda e: e.tensor_scalar(out=s.Dm, in0=s.Dm, scalar1=0.0, scalar2=None, op0=ALU.min), reads=[s.DmU], writes=[s.DmU])
            P.act(lambda e: e.activation(out=s.E, in_=s.Dm, func=AF.Exp), reads=[s.DmU], writes=[s.EU])
            P.dve(lambda e: e.tensor_tensor(out=s.sc, in0=s.E.rearrange("p (h l) -> p h l", h=4),
                                            in1=s.CBm.unsqueeze(1).to_broadcast([128, 4, 128]), op=ALU.mult),
                  reads=[s.EU, s.CBmU], writes=[s.scU])
            P.dve(lambda e, b=b: e.tensor_tensor(out=s.Xdt.rearrange("p (h q) -> p h q", h=4), in0=x4, in1=bch(s.dt, 64, b), op=ALU.mult),
                  reads=[s.xtokU[n]] + du, writes=[s.XdtU])
            P.dve(lambda e, b=b: e.tensor_tensor(out=s.XB.rearrange("p (h q) -> p h q", h=4), in0=x4, in1=bch(s.dtd, 64, b), op=ALU.mult),
                  reads=[s.xtokU[n]] + du, writes=[s.XBU])
            for h4 in range(4):
                P.pe(lambda e, h4=h4: e.matmul(bank(g, 2)[:, h4 * 64:(h4 + 1) * 64], lhsT=s.sc[:, h4, :], rhs=s.Xdt[:, h4 * 64:(h4 + 1) * 64],
                                               start=True, stop=True), reads=[s.scU, s.XdtU], writes=[g.bU[2]])
            if b > 0:
                P.pe(lambda e, blk=blk: e.matmul(bank(g, 3)[:, 0:256], lhsT=s.CT[:, blk], rhs=s.Sbf, start=True, stop=True),
                     reads=[s.BCU[n], s.SbfU], writes=[g.bU[3]])
            P.pe(lambda e, b=b: e.matmul(bank(g, 4)[:, 0:256], lhsT=s.Btok[:, b, :], rhs=s.XB, start=True, stop=True),
                 reads=[s.BtokU[n], s.XBU], writes=[g.bU[4]])
            if b > 0:
                P.dve(lambda e, b=b: e.tensor_tensor(out=s.t1.rearrange("p (h q) -> p h q", h=4),
                                                     in0=bank(g, 3)[:, 0:256].rearrange("p (h q) -> p h q", h=4),
                                                     in1=bch(s.eacs, 64, b), op=ALU.mult), reads=[g.bU[3]] + du, writes=[s.t1U])
                P.dve(lambda e: e.tensor_tensor(out=s.t2, in0=bank(g, 2)[:, 0:256], in1=s.t1, op=ALU.add), reads=[g.bU[2], s.t1U], writes=[s.t2U])
            else:
                P.dve(lambda e: e.tensor_copy(s.t2, bank(g, 2)[:, 0:256]), reads=[g.bU[2]], writes=[s.t2U])
            P.dve(lambda e: e.tensor_tensor(out=s.t1.rearrange("p (h q) -> p h q", h=4), in0=x4,
                                            in1=s.Dsk[:, hs4].unsqueeze(2).to_broadcast([128, 4, 64]), op=ALU.mult),
                  reads=[s.xtokU[n]] + sm, writes=[s.t1U])
            P.dve(lambda e: e.tensor_tensor(out=s.t2, in0=s.t2, in1=s.t1, op=ALU.add), reads=[s.t1U, s.t2U], writes=[s.t2U])
            P.dve(lambda e: e.tensor_tensor(out=s.ytmp, in0=s.t2, in1=s.zs, op=ALU.mult), reads=[s.t2U, s.zsU], writes=[s.ytmpU])
            for j in range(2):
                P.pe(lambda e, j=j: e.transpose(bank(g, 5)[:, j * 128:(j + 1) * 128], s.ytmp[:, j * 128:(j + 1) * 128], ident[:]),
                     reads=[s.ytmpU, g.cU], writes=[g.bU[5]])
            P.act(lambda e, blk=blk: e.copy(s.yTa[:, 2 * grp:2 * grp + 2, blk], bank(g, 5)[:, 0:256].rearrange("p (j t) -> p j t", j=2)),
                  reads=[g.bU[5]], writes=[s.yTaU[2 * grp][n], s.yTaU[2 * grp + 1][n]])
            if b == 0:
                P.dve(lambda e: e.tensor_copy(s.S, bank(g, 4)[:, 0:256]), reads=[g.bU[4]], writes=[s.SU])
            else:
                P.dve(lambda e, b=b: e.tensor_tensor(out=s.S.rearrange("p (h q) -> p h q", h=4), in0=s.S.rearrange("p (h q) -> p h q", h=4),
                                                     in1=bch(s.edl, 64, b), op=ALU.mult), reads=[s.SU] + du, writes=[s.SU])
                P.dve(lambda e: e.tensor_tensor(out=s.S, in0=s.S, in1=bank(g, 4)[:, 0:256], op=ALU.add), reads=[s.SU, g.bU[4]], writes=[s.SU])
            if b + 1 < NB:
                P.act(lambda e: e.copy(s.Sbf, s.S), reads=[s.SU], writes=[s.SbfU])

    for n in range(NT):
        sl = slice(n * 512, (n + 1) * 512)
        rms_rstd_tile(g, lambda k, sl=sl: s.yTa[:, k, sl], lambda k, n=n: [s.yTaU[k][n]], 8, 1024)
        P.dve(lambda e, sl=sl: e.tensor_copy(s.rstd_all[:, sl], g.rstd_t[:]), reads=[g.rstdU], writes=[s.rstdallU])
    cnt = 0
    for m in range(KC):
        P.dma("pool", lambda e, m=m: e.dma_start(out=s.wo.rearrange("p c j -> p (c j)"), in_=g.d["ev_w_out_t"][ei, m, :, 0:1024]),
              "s_wo", writes=[s.woU])
        P.dve(lambda e: e.tensor_tensor(out=s.wos, in0=s.wo, in1=s.snw[:, 0:8].unsqueeze(2).to_broadcast([128, 8, 128]), op=ALU.mult),
              reads=[s.woU] + sm, writes=[s.wosU])
        for n in range(NT):
            sl = slice(n * 512, (n + 1) * 512)
            bk = cnt % 2
            cnt += 1
            for k in range(8):
                P.pe(lambda e, bk=bk, k=k, sl=sl: e.matmul(bank(g, bk), lhsT=s.wos[:, k, :], rhs=s.yTa[:, k, sl], start=(k == 0), stop=(k == 7)),
                     reads=[s.wosU, s.yTaU[k][n]], writes=[g.bU[bk]])
            P.dve(lambda e, bk=bk, sl=sl: e.tensor_tensor(out=s.cacc, in0=bank(g, bk), in1=s.rstd_all[:, sl], op=ALU.mult),
                  reads=[g.bU[bk], s.rstdallU], writes=[s.caccU])
            P.pool(lambda e, m=m, sl=sl: e.tensor_tensor(out=g.hT[:, m, sl], in0=g.hT[:, m, sl], in1=s.cacc, op=ALU.add),
                   reads=[s.caccU, g.hU[m][n]], writes=[g.hU[m][n]])


def even_attn(g, li):
    P, L, NB, NT = g.P, g.L, g.NB, g.NT
    ei = li // 2
    lam_init = 0.8 - 0.6 * math.exp(-0.3 * li)
    c = g.cst
    A = Carver(g)
    a = NS()
    a.corr = A.f(2048).rearrange("p (h d q) -> p h d q", h=8, d=2); a.corrU = P.unit()
    a.b31 = A.f(8); a.nb31 = A.f(8); a.bU_ = P.unit()
    a.lq = [A.f(64) for _ in range(4)]; a.lamU = P.unit()
    a.lam = A.f(8)
    a.slnw = A.f(128); a.slnwU = P.unit()
    a.r = A.f(8); a.rU = P.unit()
    a.t = A.f(128); a.tU = P.unit()
    a.o = A.f(128); a.oU = P.unit()
    a.sqo = A.f(128); a.sqoU = P.unit()
    a.y = A.f(128); a.yU = P.unit()
    a.w = [A.b(KC * 512).rearrange("p (k c) -> p k c", k=KC) for _ in range(2)]; a.wU = P.units(2)
    a.qT = [A.b(L) for _ in range(2)]; a.qTU = [P.unit() for _ in range(NT)]
    a.kT = A.b(L); a.kTU = [P.unit() for _ in range(NT)]
    a.v = A.b(NB * 132).rearrange("p (b c) -> p b c", c=132); a.vU = P.unit()
    a.gs = A.b(NB * 128).rearrange("p (b c) -> p b c", c=128); a.gsU = P.unit()
    a.PT = [[A.b(512) for _ in range(2)] for _ in range(2)]; a.PTU = [P.units(2), P.units(2)]
    a.yT = A.b(4 * L).rearrange("p (j t) -> p j t", j=4); a.yTU = P.units(4)
    a.wo = [A.b(512).rearrange("p (j c) -> p j c", j=4) for _ in range(2)]; a.woU = P.units(2)
    ident = c["ident"]

    P.dma("sp", lambda e: e.dma_start(out=a.corr.rearrange("p h d q -> p (h d q)"), in_=g.d["rel_biasD"]), "a_corr", writes=[a.corrU])
    P.dma("sp", lambda e: e.dma_start(out=a.b31, in_=g.d["rel_b31"].partition_broadcast(128)), "a_b31", writes=[a.bU_])
    P.dve(lambda e: e.tensor_scalar(out=a.nb31, in0=a.b31, scalar1=-1.0, scalar2=None, op0=ALU.mult), reads=[a.bU_], writes=[a.bU_])
    for h in range(8):
        P.act(lambda e, h=h: e.activation(out=a.corr[:, h], in_=a.corr[:, h], func=AF.Exp, bias=a.nb31[:, h:h + 1]),
              reads=[a.corrU, a.bU_], writes=[a.corrU])
    for i, nm in enumerate(("lambda_q1", "lambda_k1", "lambda_q2", "lambda_k2")):
        P.dma("sp", lambda e, i=i, nm=nm: e.dma_start(out=a.lq[i], in_=g.d[nm][ei].partition_broadcast(128)), "a_lam", writes=[a.lamU])
    lu = [a.lamU]
    P.dve(lambda e: e.tensor_tensor(out=a.lq[0], in0=a.lq[0], in1=a.lq[1], op=ALU.mult), reads=lu, writes=lu)
    P.dve(lambda e: e.tensor_tensor(out=a.lq[2], in0=a.lq[2], in1=a.lq[3], op=ALU.mult), reads=lu, writes=lu)
    P.dve(lambda e: e.tensor_reduce(out=a.lam[:, 0:1], in_=a.lq[0], axis=AX.X, op=ALU.add), reads=lu, writes=lu)
    P.dve(lambda e: e.tensor_reduce(out=a.lam[:, 1:2], in_=a.lq[2], axis=AX.X, op=ALU.add), reads=lu, writes=lu)
    P.act(lambda e: e.activation(out=a.lam[:, 2:4], in_=a.lam[:, 0:2], func=AF.Exp), reads=lu, writes=lu)
    P.dve(lambda e: e.tensor_tensor(out=a.lam[:, 4:5], in0=a.lam[:, 3:4], in1=a.lam[:, 2:3], op=ALU.subtract), reads=lu, writes=lu)
    P.dve(lambda e: e.tensor_scalar(out=a.lam[:, 5:6], in0=a.lam[:, 4:5], scalar1=-lam_init, scalar2=None, op0=ALU.add), reads=lu, writes=lu)
    P.dma("sp", lambda e: e.dma_start(out=a.slnw, in_=g.d["subln_w"][ei].partition_broadcast(128)), "a_slnw", writes=[a.slnwU])
    P.dve(lambda e: e.tensor_scalar(out=a.slnw, in0=a.slnw, scalar1=1.0 - lam_init, scalar2=None, op0=ALU.mult),
          reads=[a.slnwU], writes=[a.slnwU])
    P.dve(lambda e: e.memset(a.v, 1.0), writes=[a.vU])

    def load_w(h):
        i = h % 2
        P.dma("pool", lambda e: e.dma_start(out=a.w[i].rearrange("p k c -> p (k c)"), in_=g.d["ev_w_att"][ei, h]),
              f"a_w{i}", writes=[a.wU[i]])

    pc = [0]

    def project(h):
        wi = h % 2
        w = a.w[wi]
        for n in range(NT):
            sl = slice(n * 512, (n + 1) * 512)
            for which in range(2):
                bk = pc[0] % 2
                pc[0] += 1
                for k in range(KC):
                    P.pe(lambda e, bk=bk, k=k, sl=sl, which=which: e.matmul(bank(g, bk), lhsT=w[:, k, which * 128:(which + 1) * 128],
                                                                            rhs=g.uT[:, k, sl], start=(k == 0), stop=(k == KC - 1)),
                         reads=[g.uU[n], a.wU[wi]], writes=[g.bU[bk]])
                if which == 0:
                    for cc in range(2):
                        P.dve(lambda e, bk=bk, sl=sl, cc=cc: e.tensor_scalar(out=a.qT[cc][:, sl], in0=bank(g, bk), scalar1=c["maskq"][:, cc:cc + 1],
                                                                             scalar2=None, op0=ALU.mult),
                              reads=[g.bU[bk], g.cU], writes=[a.qTU[n]])
                else:
                    P.act(lambda e, bk=bk, sl=sl: e.copy(a.kT[:, sl], bank(g, bk)), reads=[g.bU[bk]], writes=[a.kTU[n]])
        for b in range(NB):
            bk = 2 + (b % 2)
            for k in range(KC):
                P.pe(lambda e, bk=bk, k=k, b=b: e.matmul(bank(g, bk)[:, 0:256], lhsT=g.uT[:, k, b * 128:(b + 1) * 128], rhs=w[:, k, 256:512],
                                                         start=(k == 0), stop=(k == KC - 1)),
                     reads=[g.uU[b // 4], a.wU[wi]], writes=[g.bU[bk]])
            P.act(lambda e, bk=bk, b=b: e.copy(a.v[:, b, 0:128], bank(g, bk)[:, 0:128]), reads=[g.bU[bk]], writes=[a.vU])
            P.act(lambda e, bk=bk, b=b: e.activation(out=a.gs[:, b, :], in_=bank(g, bk)[:, 128:256], func=AF.Silu),
                  reads=[g.bU[bk]], writes=[a.gsU])

    gc = [0]

    def attend(h):
        hj = h % 4
        for qb in range(NB):
            qn = qb // 4
            qs = slice(qb * 128, (qb + 1) * 128)
            ngrp = qb // 4 + 1
            for gi in range(ngrp):
                kbs = [kb for kb in range(gi * 4, gi * 4 + 4) if kb <= qb]
                nv = len(kbs)
                buf = gc[0] % 2
                gc[0] += 1
                for cc in range(2):
                    bk = 4 + 2 * cc + buf
                    for j, kb in enumerate(kbs):
                        P.pe(lambda e, bk=bk, j=j, kb=kb, cc=cc: e.matmul(bank(g, bk)[:, j * 128:(j + 1) * 128],
                                                                          lhsT=a.kT[:, kb * 128:(kb + 1) * 128], rhs=a.qT[cc][:, qs],
                                                                          start=True, stop=True),
                             reads=[a.kTU[kb // 4], a.qTU[qn]], writes=[g.bU[bk]])
                for cc in range(2):
                    bk = 4 + 2 * cc + buf
                    pt = a.PT[cc][buf]
                    ptu = a.PTU[cc][buf]
                    P.act(lambda e, bk=bk, pt=pt, nv=nv: e.activation(out=pt[:, 0:nv * 128], in_=bank(g, bk)[:, 0:nv * 128], func=AF.Exp,
                                                                      scale=0.125, bias=a.b31[:, h:h + 1]),
                          reads=[g.bU[bk], a.bU_], writes=[ptu])
                    for j, kb in enumerate(kbs):
                        Dd = qb - kb
                        if Dd <= 1:
                            P.dve(lambda e, pt=pt, j=j, Dd=Dd: e.tensor_tensor(out=pt[:, j * 128:(j + 1) * 128], in0=pt[:, j * 128:(j + 1) * 128],
                                                                               in1=a.corr[:, h, Dd, :], op=ALU.mult),
                                  reads=[ptu, a.corrU], writes=[ptu])
                for cc in range(2):
                    pt = a.PT[cc][buf]
                    ptu = a.PTU[cc][buf]
                    for j, kb in enumerate(kbs):
                        P.pe(lambda e, cc=cc, pt=pt, j=j, kb=kb: e.matmul(bank(g, 2 + cc)[:, 0:129], lhsT=pt[:, j * 128:(j + 1) * 128],
                                                                          rhs=a.v[:, kb, 0:129], start=(kb == 0), stop=(kb == qb)),
                             reads=[ptu, a.vU], writes=[g.bU[2 + cc]])
            ru = [a.rU]
            P.dve(lambda e: e.reciprocal(a.r[:, 0:1], bank(g, 2)[:, 128:129]), reads=[g.bU[2]], writes=ru)
            P.dve(lambda e: e.reciprocal(a.r[:, 1:2], bank(g, 3)[:, 128:129]), reads=[g.bU[3]], writes=ru)
            P.dve(lambda e: e.tensor_tensor(out=a.r[:, 2:3], in0=a.r[:, 1:2], in1=a.lam[:, 5:6], op=ALU.mult), reads=ru + lu, writes=ru)
            P.dve(lambda e: e.tensor_scalar(out=a.t, in0=bank(g, 2)[:, 0:128], scalar1=a.r[:, 0:1], scalar2=None, op0=ALU.mult),
                  reads=[g.bU[2]] + ru, writes=[a.tU])
            P.dve(lambda e: e.scalar_tensor_tensor(out=a.o, in0=bank(g, 3)[:, 0:128], scalar=a.r[:, 2:3], in1=a.t, op0=ALU.mult, op1=ALU.add),
                  reads=[g.bU[3], a.tU] + ru, writes=[a.oU])
            P.dve(lambda e: e.tensor_tensor(out=a.sqo, in0=a.o, in1=a.o, op=ALU.mult), reads=[a.oU], writes=[a.sqoU])
            P.dve(lambda e: e.tensor_reduce(out=a.r[:, 3:4], in_=a.sqo, axis=AX.X, op=ALU.add), reads=[a.sqoU], writes=ru)
            P.act(lambda e: e.activation(out=a.r[:, 4:5], in_=a.r[:, 3:4], func=AF.Ln, scale=1.0 / 128, bias=EPS), reads=ru, writes=ru)
            P.act(lambda e: e.activation(out=a.r[:, 5:6], in_=a.r[:, 4:5], func=AF.Exp, scale=-0.5), reads=ru, writes=ru)
            P.dve(lambda e: e.scalar_tensor_tensor(out=a.y, in0=a.o, scalar=a.r[:, 5:6], in1=a.slnw, op0=ALU.mult, op1=ALU.mult),
                  reads=[a.oU, a.slnwU] + ru, writes=[a.yU])
            P.dve(lambda e, qb=qb: e.tensor_tensor(out=a.y, in0=a.y, in1=a.gs[:, qb, :], op=ALU.mult), reads=[a.yU, a.gsU], writes=[a.yU])
            P.pe(lambda e: e.transpose(bank(g, 0)[:, 0:128], a.y, ident[:]), reads=[a.yU, g.cU], writes=[g.bU[0]])
            P.act(lambda e, qs=qs: e.copy(a.yT[:, hj, qs], bank(g, 0)[:, 0:128]), reads=[g.bU[0]], writes=[a.yTU[hj]])

    oc = [0]

    def outproj(hg):
        for m in range(KC):
            wi = oc[0] % 2
            oc[0] += 1
            c0 = (8 + hg * 4) * 128
            P.dma("pool", lambda e, m=m, wi=wi, c0=c0: e.dma_start(out=a.wo[wi].rearrange("p j c -> p (j c)"),
                                                                  in_=g.d["ev_w_out_t"][ei, m, :, c0:c0 + 512]),
                  f"a_wo{wi}", writes=[a.woU[wi]])
            for n in range(NT):
                sl = slice(n * 512, (n + 1) * 512)
                bk = n % 2
                for j in range(4):
                    P.pe(lambda e, bk=bk, j=j, sl=sl, wi=wi: e.matmul(bank(g, bk), lhsT=a.wo[wi][:, j, :], rhs=a.yT[:, j, sl],
                                                                      start=(j == 0), stop=(j == 3)),
                         reads=[a.woU[wi], a.yTU[j]], writes=[g.bU[bk]])
                P.dve(lambda e, bk=bk, m=m, sl=sl: e.tensor_tensor(out=g.hT[:, m, sl], in0=g.hT[:, m, sl], in1=bank(g, bk), op=ALU.add),
                      reads=[g.bU[bk], g.hU[m][n]], writes=[g.hU[m][n]])

    load_w(0)
    for h in range(8):
        if h + 1 < 8:
            load_w(h + 1)
        project(h)
        attend(h)
        if h % 4 == 3:
            outproj(h // 4)


_CACHE = {}


def kernel(**inputs):
    x = np.ascontiguousarray(np.asarray(inputs["x"], dtype=np.float32))
    Bsz, L, _ = x.shape
    n_cores = 8
    nseq = Bsz // n_cores
    key = (L, nseq)
    if key not in _CACHE:
        _CACHE[key] = build(L, nseq, (0, 1, 2, 3))
    nc, _ = _CACHE[key]
    common = host_layout(inputs)
    in_maps = []
    for cidx in range(n_cores):
        m = dict(common)
        m["x"] = x[cidx * nseq:(cidx + 1) * nseq]
        in_maps.append(m)
    res = run_bass_kernel_spmd(nc, in_maps, core_ids=list(range(n_cores)))
    out = np.concatenate([np.asarray(r["out"]) for r in res.results], axis=0)
    return out.astype(np.float32)
```

```python
import math, contextlib
import numpy as np
import concourse.bass as bass
import concourse.mybir as mybir
from concourse.bass_utils import run_bass_kernel_spmd
from concourse.alu_op_type import AluOpType as ALU

F32 = mybir.dt.float32
BF16 = mybir.dt.bfloat16
AF = mybir.ActivationFunctionType
AX = mybir.AxisListType

D = 1024
KC = 8
EPS = 1e-6
DEPTH = 4
HG_W = 2048
ARF_N = 7616
ARB_N = 35840


class Unit:
    __slots__ = ("name", "lw", "rd")

    def __init__(self, name):
        self.name = name
        self.lw = None
        self.rd = []


class _Rec:
    def __init__(self):
        self.call = None

    def __getattr__(self, name):
        def f(*args, **kw):
            assert self.call is None
            self.call = (name, args, kw)
            return None
        return f


class Prog:
    ENGS = ("pe", "act", "dve", "pool", "sp")

    def __init__(self, nc):
        self.nc = nc
        self.ops = []
        self.nunits = 0
        self.last_eng = {}
        self.last_key = {}

    def unit(self, name=None):
        self.nunits += 1
        return Unit(name or f"u{self.nunits}")

    def units(self, n, name="u"):
        return [self.unit(f"{name}{i}") for i in range(n)]

    def op(self, eng, fn, reads=(), writes=(), dma_key=None, extra_deps=()):
        idx = len(self.ops)
        deps = set(extra_deps)
        for u in reads:
            if u.lw is not None:
                deps.add(u.lw)
        for u in writes:
            if u.lw is not None:
                deps.add(u.lw)
            deps.update(u.rd)
        for u in reads:
            u.rd.append(idx)
        for u in writes:
            u.lw = idx
            u.rd = []
        deps.discard(idx)
        if fn is not None:
            rec = _Rec()
            fn(rec)
            assert rec.call is not None
            fn = rec.call
        self.ops.append(dict(eng=eng, fn=fn, deps=deps, dma_key=dma_key))
        if fn is not None:
            if dma_key is None:
                self.last_eng[eng] = idx
            else:
                self.last_key[dma_key] = idx
        return idx

    def pe(self, fn, reads=(), writes=()):
        return self.op("pe", fn, reads, writes)

    def act(self, fn, reads=(), writes=()):
        return self.op("act", fn, reads, writes)

    def dve(self, fn, reads=(), writes=()):
        return self.op("dve", fn, reads, writes)

    def pool(self, fn, reads=(), writes=()):
        return self.op("pool", fn, reads, writes)

    def dma(self, eng, fn, key, reads=(), writes=()):
        return self.op(eng, fn, reads, writes, dma_key=key)

    def barrier(self):
        deps = set(self.last_eng.values()) | set(self.last_key.values())
        for e in self.ENGS:
            self.op(e, None, extra_deps=deps)

    def emit(self, final_wait_ops=()):
        nc = self.nc
        ops = self.ops
        n = len(ops)

        def skip(od, o):
            return (od["eng"] == "pe" and o["eng"] == "pe" and od["dma_key"] is None
                    and o["dma_key"] is None and o["fn"] is not None)

        needed = [False] * n
        for i, o in enumerate(ops):
            for d in o["deps"]:
                if skip(ops[d], o):
                    continue
                needed[d] = True
        for d in final_wait_ops:
            needed[d] = True
        chan_count = {}
        ev = [None] * n
        for i, o in enumerate(ops):
            if o["fn"] is None:
                continue
            if o["dma_key"] is not None:
                ch = ("dma", o["dma_key"])
                chan_count[ch] = chan_count.get(ch, 0) + 16
                ev[i] = (ch, chan_count[ch])
            elif needed[i]:
                ch = ("eng", o["eng"])
                chan_count[ch] = chan_count.get(ch, 0) + 1
                ev[i] = (ch, chan_count[ch])
        chans = sorted(chan_count.keys(), key=str)
        self.n_sems = len(chans)
        sems = {}
        stack = contextlib.ExitStack()
        for ci, ch in enumerate(chans):
            sems[ch] = stack.enter_context(nc.semaphore(f"s{ci}"))
        known = {e: {} for e in self.ENGS}
        clock = [None] * n
        streams = {e: [] for e in self.ENGS}
        for i, o in enumerate(ops):
            e = o["eng"]
            kn = known[e]
            wd = {}
            for d in sorted(o["deps"]):
                od = ops[d]
                if skip(od, o):
                    continue
                ch, v = ev[d]
                if kn.get(ch, 0) >= v:
                    continue
                for c2, v2 in clock[d].items():
                    if kn.get(c2, 0) < v2:
                        kn[c2] = v2
                wd[ch] = max(wd.get(ch, 0), v)
            ck = dict(kn)
            if ev[i] is not None:
                ch, v = ev[i]
                ck[ch] = v
            clock[i] = ck
            streams[e].append((list(wd.items()), o["fn"], ev[i]))
        final = [ev[d] for d in final_wait_ops]
        for ch, tot in chan_count.items():
            if ch[0] == "dma":
                final.append((ch, tot))
        self.sems, self.streams, self.final, self._stack = sems, streams, final, stack

    def run_block(self):
        nc = self.nc
        sems, streams, final = self.sems, self.streams, self.final
        with nc.Block() as block:
            def mk(ename):
                def body(eng):
                    for waits, fn, e in streams[ename]:
                        for ch, v in waits:
                            eng.wait_ge(sems[ch], v)
                        if fn is None:
                            continue
                        ins = getattr(eng, fn[0])(*fn[1], **fn[2])
                        if e is not None:
                            ins.then_inc(sems[e[0]], 16 if e[0][0] == "dma" else 1)
                    if ename == "sp":
                        for ch, v in final:
                            eng.wait_ge(sems[ch], v)
                return body
            block.tensor(mk("pe"))
            block.scalar(mk("act"))
            block.vector(mk("dve"))
            block.gpsimd(mk("pool"))
            block.sync(mk("sp"))
        self._stack.close()


def _t5_bucket(rel):
    n = np.maximum(rel, 0)
    max_exact = 16
    large = max_exact + (np.log(np.maximum(n, 1).astype(np.float32) / max_exact)
                         / math.log(128 / max_exact) * (32 - max_exact)).astype(np.int32)
    large = np.minimum(large, 31)
    return np.where(n < max_exact, n, large)


def host_consts():
    s = np.arange(128)[:, None]
    t = np.arange(128)[None, :]
    c = {}
    c["ident"] = np.eye(128, dtype=np.float32)
    c["ones"] = np.ones((128, 128), np.float32)
    c["triC"] = ((s <= t).astype(np.float32) - (s <= 63).astype(np.float32))
    c["triU"] = (s > t).astype(np.float32)
    c["triI"] = (s <= t).astype(np.float32)
    sel = np.zeros((128, 2), np.float32)
    sel[:64, 0] = 1.0
    sel[:, 1] = 1.0
    c["sel"] = sel
    c["mg16"] = np.eye(16, dtype=np.float32)
    mq = np.zeros((128, 2), np.float32)
    mq[:64, 0] = 1.0
    mq[64:, 1] = 1.0
    c["maskq"] = mq
    return c


def host_layout(inp):
    f = lambda a: np.ascontiguousarray(np.asarray(a, dtype=np.float32))
    m = dict(host_consts())
    m["final_norm_w"] = f(inp["final_norm_w"])
    m["norm_w_cols"] = f(np.asarray(inp["norm_w"]).reshape(4, 8, 128).transpose(0, 2, 1))
    owin = np.asarray(inp["odd_w_in"])
    t = owin.reshape(2, 8, 128, 4, 16, 128).transpose(0, 4, 2, 1, 3, 5)
    m["odd_w_in_t"] = f(t).reshape(2, 16, 128, 8 * 512)
    m["odd_w_out"] = f(inp["odd_w_out"])
    m["hgrn_lower_bounds"] = f(inp["hgrn_lower_bounds"])
    m["hgrn_norm_w"] = f(inp["hgrn_norm_w"])
    ew = np.asarray(inp["even_w_in"]).reshape(2, 8, 128, 7184)
    z = ew[..., 0:1024]; xs = ew[..., 1024:2048]; Bm = ew[..., 2048:2560]; Cm = ew[..., 2560:3072]
    dt = ew[..., 3072:3088]
    q = ew[..., 3088:4112]; kk = ew[..., 4112:5136]; v = ew[..., 5136:6160]; gg = ew[..., 6160:7184]
    ssd = np.concatenate([z.reshape(2, 8, 128, 4, 256), xs.reshape(2, 8, 128, 4, 256),
                          Bm.reshape(2, 8, 128, 4, 128), Cm.reshape(2, 8, 128, 4, 128)], axis=-1)
    m["ev_w_ssd"] = f(ssd.transpose(0, 3, 2, 1, 4)).reshape(2, 4, 128, 8 * 768)
    m["ev_w_dt"] = f(dt.transpose(0, 2, 1, 3)).reshape(2, 128, 8 * 16)
    att = np.concatenate([q.reshape(2, 8, 128, 8, 128), kk.reshape(2, 8, 128, 8, 128),
                          v.reshape(2, 8, 128, 8, 128), gg.reshape(2, 8, 128, 8, 128)], axis=-1)
    m["ev_w_att"] = f(att.transpose(0, 3, 2, 1, 4)).reshape(2, 8, 128, 8 * 512)
    wo = np.asarray(inp["even_w_out"]).reshape(2, 16, 128, 8, 128)
    m["ev_w_out_t"] = f(wo.transpose(0, 3, 2, 1, 4)).reshape(2, 8, 128, 16 * 128)
    m["conv_w_cols"] = f(np.asarray(inp["conv_w"]).reshape(2, 4, 16, 128).transpose(0, 3, 2, 1)).reshape(2, 128, 64)
    m["conv_b_cols"] = f(np.asarray(inp["conv_b"]).reshape(2, 16, 128).transpose(0, 2, 1))
    for nm in ("dt_bias", "A_log", "D_skip", "lambda_q1", "lambda_k1", "lambda_q2", "lambda_k2", "subln_w"):
        m[nm] = f(inp[nm])
    m["ssd_norm_w_cols"] = f(np.asarray(inp["ssd_norm_w"]).reshape(2, 8, 128).transpose(0, 2, 1))
    rb = np.asarray(inp["rel_bias"], dtype=np.float32)
    kpos = np.arange(128)[:, None]
    qpos = np.arange(128)[None, :]
    bd = np.empty((128, 8, 2, 128), np.float32)
    for Dd in range(2):
        rel = qpos - kpos + 128 * Dd
        bidx = _t5_bucket(rel)
        g_ = rb[bidx]
        g_ = np.where((rel >= 0)[:, :, None], g_, np.float32(-30000.0))
        bd[:, :, Dd, :] = g_.transpose(0, 2, 1)
    m["rel_biasD"] = f(bd).reshape(128, 8 * 2 * 128)
    m["rel_b31"] = f(rb[31])
    return m


class NS:
    pass


class Carver:
    def __init__(self, g):
        self.g = g
        self.fo = 0
        self.bo = 0

    def f(self, n):
        ap = self.g.arf[:, self.fo:self.fo + n]
        self.fo += (n + 7) // 8 * 8
        assert self.fo <= ARF_N, ("ARF overflow", self.fo)
        return ap

    def b(self, n):
        ap = self.g.arb[:, self.bo:self.bo + n]
        self.bo += (n + 15) // 16 * 16
        assert self.bo <= ARB_N, ("ARB overflow", self.bo)
        return ap


def bank(g, i):
    return g.ps[:, i, :]


def build(L=2048, NSEQ=2, layers=(0, 1, 2, 3)):
    nc = bass.Bass("TRN2", target_bir_lowering=False)
    NT, NB = L // 512, L // 128
    g = NS()
    g.nc, g.L, g.NT, g.NB = nc, L, NT, NB
    dr = lambda name, shape, kind="ExternalInput": nc.dram_tensor(name, list(shape), F32, kind=kind).ap()
    g.x_d = dr("x", [NSEQ, L, D])
    g.out_d = dr("out", [NSEQ, L, D], "ExternalOutput")
    g.d = {}
    shapes = {
        "final_norm_w": [D], "norm_w_cols": [DEPTH, 128, KC],
        "ident": [128, 128], "ones": [128, 128], "triC": [128, 128], "triU": [128, 128], "triI": [128, 128],
        "sel": [128, 2], "mg16": [16, 16], "maskq": [128, 2],
        "odd_w_in_t": [2, 16, 128, KC * 512], "odd_w_out": [2, HG_W, D], "hgrn_lower_bounds": [DEPTH, HG_W],
        "hgrn_norm_w": [2, 128],
        "ev_w_ssd": [2, 4, 128, 8 * 768], "ev_w_dt": [2, 128, 8 * 16], "ev_w_att": [2, 8, 128, 8 * 512],
        "ev_w_out_t": [2, 8, 128, 16 * 128], "conv_w_cols": [2, 128, 64], "conv_b_cols": [2, 128, 16],
        "dt_bias": [2, 16], "A_log": [2, 16], "D_skip": [2, 16], "lambda_q1": [2, 64], "lambda_k1": [2, 64],
        "lambda_q2": [2, 64], "lambda_k2": [2, 64], "subln_w": [2, 128], "ssd_norm_w_cols": [2, 128, 8],
        "rel_biasD": [128, 8 * 2 * 128], "rel_b31": [8],
    }
    for nm, shp in shapes.items():
        g.d[nm] = dr(nm, shp)
    g.in_names = ["x"] + list(shapes.keys())

    es = contextlib.ExitStack()
    sb = lambda name, shape, dt=F32: es.enter_context(nc.sbuf_tensor(name, list(shape), dt))
    P = Prog(nc)
    g.P = P
    g.hT = sb("hT", [128, KC, L]); g.hU = [[P.unit(f"h{k}_{n}") for n in range(NT)] for k in range(KC)]
    g.uT = sb("uT", [128, KC, L], BF16); g.uU = [P.unit(f"u{n}") for n in range(NT)]
    g.cst = {}
    g.cU = P.unit("consts")
    for nm in ("ident", "ones", "triI"):
        g.cst[nm] = sb("c_" + nm, [128, 128])
    g.cst["sel"] = sb("c_sel", [128, 2])
    g.cst["maskq"] = sb("c_maskq", [128, 2])
    g.cst["mg16"] = sb("c_mg16", [16, 16])
    g.nwc = sb("nwc", [128, DEPTH, KC])
    g.stat = sb("stat", [128, 16]); g.statU = P.unit("stat")
    g.sq = [sb(f"sq{i}", [128, 512]) for i in range(2)]; g.sqU = P.units(2, "sq")
    g.rstd_t = sb("rstd_t", [128, 512]); g.rstdU = P.unit("rstd_t")
    g.arf = sb("arf", [128, ARF_N])
    g.arb = sb("arb", [128, ARB_N], BF16)
    g.ps = es.enter_context(nc.psum_tensor("ps", [128, 8, 512], F32))
    g.bU = P.units(8, "bank")

    for nm in ("ident", "ones", "triI", "sel", "maskq", "mg16"):
        P.dma("sp", lambda e, nm=nm: e.dma_start(out=g.cst[nm][:], in_=g.d[nm]), "c_" + nm, writes=[g.cU])
    P.dma("sp", lambda e: e.dma_start(out=g.nwc[:], in_=g.d["norm_w_cols"].rearrange("l p k -> p l k")), "c_nwc", writes=[g.cU])

    out_ops = []
    for s in range(NSEQ):
        P.barrier()
        load_x(g, s)
        for li in layers:
            rms_to_uT(g, li)
            P.barrier()
            if li % 2 == 1:
                odd_layer(g, li)
            else:
                even_ssd(g, li)
                P.barrier()
                even_attn(g, li)
            P.barrier()
        out_ops += final_norm_store(g, s)
    P.emit(final_wait_ops=out_ops[-4:])
    P.run_block()
    es.close()
    return nc, P


def load_x(g, s):
    P, NT = g.P, g.NT
    A = Carver(g)
    xst = A.f(4096).rearrange("p (b d) -> p b d", b=4)
    xU = P.unit("xst")
    ident = g.cst["ident"]
    for n in range(NT):
        src = g.x_d[s, n * 512:(n + 1) * 512, :].rearrange("(b p) d -> p b d", p=128)
        P.dma("sp", lambda e, src=src: e.dma_start(out=xst, in_=src), "xst", writes=[xU])
        for k in range(KC):
            bk = k % 8
            for b in range(4):
                P.pe(lambda e, bk=bk, b=b, k=k: e.transpose(
                    bank(g, bk)[:, b * 128:(b + 1) * 128], xst[:, b, k * 128:(k + 1) * 128], ident[:]),
                    reads=[xU, g.cU], writes=[g.bU[bk]])
            if k % 2 == 0:
                P.dve(lambda e, bk=bk, k=k, n=n: e.tensor_copy(g.hT[:, k, n * 512:(n + 1) * 512], bank(g, bk)),
                      reads=[g.bU[bk]], writes=[g.hU[k][n]])
            else:
                P.act(lambda e, bk=bk, k=k, n=n: e.copy(g.hT[:, k, n * 512:(n + 1) * 512], bank(g, bk)),
                      reads=[g.bU[bk]], writes=[g.hU[k][n]])


def final_norm_store(g, s):
    P, NB = g.P, g.NB
    A = Carver(g)
    fnw = A.f(D); fnwU = P.unit("fnw")
    ost = [A.f(D) for _ in range(2)]; ostU = P.units(2, "ost")
    junk = A.f(1024); junkU = P.unit("junk")
    ident = g.cst["ident"]
    P.dma("sp", lambda e: e.dma_start(out=fnw, in_=g.d["final_norm_w"].partition_broadcast(128)), "fnw", writes=[fnwU])
    outs = []
    for b in range(NB):
        n = b // 4
        oi = b % 2
        for k in range(KC):
            bk = k // 4
            P.pe(lambda e, bk=bk, k=k, b=b: e.transpose(
                bank(g, bk)[:, (k % 4) * 128:(k % 4 + 1) * 128], g.hT[:, k, b * 128:(b + 1) * 128], ident[:]),
                reads=[g.hU[k][n], g.cU], writes=[g.bU[bk]])
        for half in range(2):
            P.act(lambda e, half=half: e.activation(
                out=junk[:, half * 512:(half + 1) * 512], in_=bank(g, half), func=AF.Square,
                accum_out=g.stat[:, half:half + 1]),
                reads=[g.bU[half]], writes=[junkU, g.statU])
        P.dve(lambda e: e.tensor_tensor(out=g.stat[:, 2:3], in0=g.stat[:, 0:1], in1=g.stat[:, 1:2], op=ALU.add),
              reads=[g.statU], writes=[g.statU])
        P.act(lambda e: e.activation(out=g.stat[:, 3:4], in_=g.stat[:, 2:3], func=AF.Ln, scale=1.0 / D, bias=EPS),
              reads=[g.statU], writes=[g.statU])
        P.act(lambda e: e.activation(out=g.stat[:, 4:5], in_=g.stat[:, 3:4], func=AF.Exp, scale=-0.5),
              reads=[g.statU], writes=[g.statU])
        for half in range(2):
            P.dve(lambda e, half=half, oi=oi: e.scalar_tensor_tensor(
                out=ost[oi][:, half * 512:(half + 1) * 512], in0=bank(g, half), scalar=g.stat[:, 4:5],
                in1=fnw[:, half * 512:(half + 1) * 512], op0=ALU.mult, op1=ALU.mult),
                reads=[g.bU[half], g.statU, fnwU], writes=[ostU[oi]])
        o = P.dma("sp", lambda e, oi=oi, s=s, b=b: e.dma_start(out=g.out_d[s, b * 128:(b + 1) * 128, :], in_=ost[oi]),
                  f"ost{oi}", reads=[ostU[oi]])
        outs.append(o)
    return outs


def rms_rstd_tile(g, src_fn, reads_fn, nchunks, dim):
    P = g.P
    ones = g.cst["ones"]
    for k in range(nchunks):
        i = k % 2
        P.act(lambda e, i=i, k=k: e.activation(out=g.sq[i][:], in_=src_fn(k), func=AF.Square),
              reads=reads_fn(k), writes=[g.sqU[i]])
        P.pe(lambda e, i=i, k=k: e.matmul(bank(g, 7), lhsT=ones[:], rhs=g.sq[i][:], start=(k == 0), stop=(k == nchunks - 1)),
             reads=[g.sqU[i], g.cU], writes=[g.bU[7]])
    P.act(lambda e: e.activation(out=g.rstd_t[:], in_=bank(g, 7), func=AF.Ln, scale=1.0 / dim, bias=EPS),
          reads=[g.bU[7]], writes=[g.rstdU])
    P.act(lambda e: e.activation(out=g.rstd_t[:], in_=g.rstd_t[:], func=AF.Exp, scale=-0.5),
          reads=[g.rstdU], writes=[g.rstdU])


def rms_to_uT(g, li):
    P, NT = g.P, g.NT
    for n in range(NT):
        sl = slice(n * 512, (n + 1) * 512)
        rms_rstd_tile(g, lambda k, sl=sl: g.hT[:, k, sl], lambda k, n=n: [g.hU[k][n]], KC, D)
        for k in range(KC):
            P.dve(lambda e, k=k, sl=sl: e.scalar_tensor_tensor(
                out=g.uT[:, k, sl], in0=g.hT[:, k, sl], scalar=g.nwc[:, li, k:k + 1], in1=g.rstd_t[:],
                op0=ALU.mult, op1=ALU.mult),
                reads=[g.hU[k][n], g.rstdU, g.cU], writes=[g.uU[n]])


def odd_layer(g, li):
    P, L, NB, NT = g.P, g.L, g.NB, g.NT
    oi = li // 2
    A = Carver(g)
    o = NS()
    c = g.cst
    f3 = lambda: A.f(512).rearrange("p (b d) -> p b d", b=4)
    o.logf = f3(); o.logfU = P.unit()
    o.tA = f3(); o.tAU = P.unit()
    o.kk = f3(); o.kkU = P.unit()
    o.qs = f3(); o.qsU = P.unit()
    o.e13 = A.f(1024).rearrange("p (t b d) -> p t b d", t=2, b=4); o.e13U = P.unit()
    o.e2 = f3(); o.e2U = P.unit()
    o.qt = o.qs; o.qtU = o.qsU
    o.kt = o.e2; o.ktU = o.e2U
    o.lbr = f3(); o.lbrU = P.unit()
    o.lbh = A.f(128); o.omlh = A.f(128); o.den = A.f(128); o.lbU = P.unit()
    o.eb = [A.f(NB * 2).rearrange("p (b t) -> p b t", t=2) for _ in range(2)]
    o.S = A.f(128); o.SU = P.unit()
    o.junk2 = A.f(128); o.junk2U = P.unit()
    o.oall = A.f(NB * 128).rearrange("p (b d) -> p b d", d=128); o.oallU = P.unit()
    o.ssall = A.f(NB); o.rsall = A.f(NB); o.ssU = P.unit()
    o.hnw = A.f(128); o.hnwU = P.unit()
    o.triC = A.f(128); o.triU = A.f(128); o.triUU = P.unit()
    o.bst = A.f(8); o.bstU = P.unit()
    o.w = [A.b(KC * 512).rearrange("p (k c) -> p k c", k=KC) for _ in range(2)]; o.wU = P.units(2)
    o.wout = A.b(2 * D).rearrange("p (j m) -> p j m", j=2); o.woutU = P.units(2)
    o.qT = [A.b(L) for _ in range(2)]
    o.kT = [A.b(L) for _ in range(2)]
    hb3 = lambda: A.b(NB * 128).rearrange("p (b d) -> p b d", d=128)
    o.kh = [hb3() for _ in range(2)]
    o.v = [hb3() for _ in range(2)]
    o.gs = [hb3() for _ in range(2)]
    o.hbU = [[[P.unit() for _ in range(NT)] for _ in range(6)] for _ in range(2)]
    o.attm = [A.b(128) for _ in range(2)]; o.attmU = P.units(2)
    o.Sb = A.b(128); o.SbU = P.unit()
    o.yT = A.b(2 * L).rearrange("p (j t) -> p j t", j=2); o.yTU = P.units(2)
    QT, KT, KH, VV, GS, EB = range(6)

    P.dma("sp", lambda e: e.dma_start(out=o.hnw, in_=g.d["hgrn_norm_w"][oi].partition_broadcast(128)), "o_hnw", writes=[o.hnwU])
    P.dma("sp", lambda e: e.dma_start(out=o.triC, in_=g.d["triC"]), "o_tri", writes=[o.triUU])
    P.dma("sp", lambda e: e.dma_start(out=o.triU, in_=g.d["triU"]), "o_tri", writes=[o.triUU])
    for i in range(2):
        P.dve(lambda e, i=i: e.memset(o.attm[i], 0.0), writes=[o.attmU[i]])

    def load_w(h):
        i = h % 2
        P.dma("pool", lambda e: e.dma_start(out=o.w[i].rearrange("p k c -> p (k c)"), in_=g.d["odd_w_in_t"][oi, h]),
              f"o_w{i}", writes=[o.wU[i]])

    def head_lb(h):
        hs = slice(h * 128, (h + 1) * 128)
        P.dma("sp", lambda e: e.dma_start(out=o.lbr, in_=g.d["hgrn_lower_bounds"][:, hs].partition_broadcast(128)),
              "o_lbr", writes=[o.lbrU])
        P.act(lambda e: e.activation(out=o.lbr, in_=o.lbr, func=AF.Exp), reads=[o.lbrU], writes=[o.lbrU])
        P.dve(lambda e: e.tensor_tensor(out=o.den, in0=o.lbr[:, 0, :], in1=o.lbr[:, 1, :], op=ALU.add), reads=[o.lbrU], writes=[o.lbU])
        P.dve(lambda e: e.tensor_tensor(out=o.den, in0=o.den, in1=o.lbr[:, 2, :], op=ALU.add), reads=[o.lbrU, o.lbU], writes=[o.lbU])
        P.dve(lambda e: e.tensor_tensor(out=o.den, in0=o.den, in1=o.lbr[:, 3, :], op=ALU.add), reads=[o.lbrU, o.lbU], writes=[o.lbU])
        P.dve(lambda e: e.reciprocal(o.den, o.den), reads=[o.lbU], writes=[o.lbU])
        if li == 1:
            P.dve(lambda e: e.tensor_tensor(out=o.lbh, in0=o.lbr[:, 1, :], in1=o.den, op=ALU.mult), reads=[o.lbrU, o.lbU], writes=[o.lbU])
        else:
            P.dve(lambda e: e.tensor_tensor(out=o.lbh, in0=o.lbr[:, 1, :], in1=o.lbr[:, 2, :], op=ALU.add), reads=[o.lbrU, o.lbU], writes=[o.lbU])
            for j in range(3, li + 1):
                P.dve(lambda e, j=j: e.tensor_tensor(out=o.lbh, in0=o.lbh, in1=o.lbr[:, j, :], op=ALU.add), reads=[o.lbrU, o.lbU], writes=[o.lbU])
            P.dve(lambda e: e.tensor_tensor(out=o.lbh, in0=o.lbh, in1=o.den, op=ALU.mult), reads=[o.lbU], writes=[o.lbU])
        P.dve(lambda e: e.tensor_scalar(out=o.omlh, in0=o.lbh, scalar1=-1.0, scalar2=1.0, op0=ALU.mult, op1=ALU.add),
              reads=[o.lbU], writes=[o.lbU])

    def stageA(h, n):
        hb = h % 2
        wi = h % 2
        U = o.hbU[hb]
        for b in range(4):
            tb = n * 4 + b
            for k in range(KC):
                P.pe(lambda e, b=b, tb=tb, k=k: e.matmul(bank(g, b), lhsT=g.uT[:, k, tb * 128:(tb + 1) * 128], rhs=o.w[wi][:, k, :],
                                                         start=(k == 0), stop=(k == KC - 1)),
                     reads=[g.uU[n], o.wU[wi]], writes=[g.bU[b]])
        pj = g.ps[:, 0:4, :]
        pb = [g.bU[0], g.bU[1], g.bU[2], g.bU[3]]
        bc4 = lambda t: t.unsqueeze(1).to_broadcast([128, 4, 128])
        P.act(lambda e: e.activation(out=o.tA, in_=pj[:, :, 128:256], func=AF.Sigmoid), reads=pb, writes=[o.tAU])
        P.dve(lambda e: e.tensor_tensor(out=o.tA, in0=o.tA, in1=bc4(o.omlh), op=ALU.mult), reads=[o.tAU, o.lbU], writes=[o.tAU])
        P.dve(lambda e: e.tensor_tensor(out=o.tA, in0=o.tA, in1=bc4(o.lbh), op=ALU.add), reads=[o.tAU, o.lbU], writes=[o.tAU])
        P.act(lambda e: e.activation(out=o.logf, in_=o.tA, func=AF.Ln), reads=[o.tAU], writes=[o.logfU])
        P.pool(lambda e: e.tensor_scalar(out=o.kk, in0=o.tA, scalar1=-1.0, scalar2=1.0, op0=ALU.mult, op1=ALU.add),
               reads=[o.tAU], writes=[o.kkU])
        P.act(lambda e: e.activation(out=o.qs, in_=pj[:, :, 0:128], func=AF.Silu), reads=pb, writes=[o.qsU])
        P.act(lambda e: e.activation(out=o.gs[hb][:, n * 4:(n + 1) * 4, :], in_=pj[:, :, 384:512], func=AF.Silu),
              reads=pb, writes=[U[GS][n]])
        P.act(lambda e: e.copy(o.v[hb][:, n * 4:(n + 1) * 4, :], pj[:, :, 256:384]), reads=pb, writes=[U[VV][n]])
        for b in range(4):
            P.pe(lambda e, b=b: e.matmul(bank(g, 4)[:, b * 128:(b + 1) * 128], lhsT=o.triC, rhs=o.logf[:, b, :], start=True, stop=True),
                 reads=[o.logfU, o.triUU], writes=[g.bU[4]])
        for b in range(4):
            P.pe(lambda e, b=b: e.matmul(bank(g, 5)[:, b * 128:(b + 1) * 128], lhsT=o.triU, rhs=o.logf[:, b, :], start=True, stop=True),
                 reads=[o.logfU, o.triUU], writes=[g.bU[5]])
        for b in range(4):
            P.pe(lambda e, b=b: e.matmul(bank(g, 6)[:, b * 2:b * 2 + 2], lhsT=o.logf[:, b, :], rhs=c["sel"][:], start=True, stop=True),
                 reads=[o.logfU, g.cU], writes=[g.bU[6]])
        P.act(lambda e: e.activation(out=o.eb[hb][:, n * 4:(n + 1) * 4, :], in_=bank(g, 6)[:, 0:8].rearrange("p (b t) -> p b t", t=2), func=AF.Exp),
              reads=[g.bU[6]], writes=[U[EB][n]])
        P.act(lambda e: e.activation(out=o.e13, in_=g.ps[:, 4:6, :].rearrange("p t (b d) -> p t b d", b=4), func=AF.Exp),
              reads=[g.bU[4], g.bU[5]], writes=[o.e13U])
        P.act(lambda e: e.activation(out=o.e2, in_=bank(g, 4).rearrange("p (b d) -> p b d", b=4), func=AF.Exp, scale=-1.0),
              reads=[g.bU[4]], writes=[o.e2U])
        P.dve(lambda e: e.tensor_tensor(out=o.qs, in0=o.qs, in1=o.e13[:, 0], op=ALU.mult), reads=[o.qsU, o.e13U], writes=[o.qsU])
        P.pool(lambda e: e.tensor_tensor(out=o.e2, in0=o.kk, in1=o.e2, op=ALU.mult), reads=[o.kkU, o.e2U], writes=[o.e2U])
        P.dve(lambda e: e.tensor_tensor(out=o.kh[hb][:, n * 4:(n + 1) * 4, :], in0=o.kk, in1=o.e13[:, 1], op=ALU.mult),
              reads=[o.kkU, o.e13U], writes=[U[KH][n]])
        idf = c["ident"]
        for b in range(4):
            P.pe(lambda e, b=b: e.transpose(bank(g, 7)[:, b * 128:(b + 1) * 128], o.qt[:, b, :], idf[:]),
                 reads=[o.qtU, g.cU], writes=[g.bU[7]])
        for b in range(4):
            P.pe(lambda e, b=b: e.transpose(bank(g, 6)[:, b * 128:(b + 1) * 128], o.kt[:, b, :], idf[:]),
                 reads=[o.ktU, g.cU], writes=[g.bU[6]])
        P.dve(lambda e: e.tensor_copy(o.qT[hb][:, n * 512:(n + 1) * 512], bank(g, 7)), reads=[g.bU[7]], writes=[U[QT][n]])
        P.act(lambda e: e.copy(o.kT[hb][:, n * 512:(n + 1) * 512], bank(g, 6)), reads=[g.bU[6]], writes=[U[KT][n]])

    def stageB(h, n):
        hb = h % 2
        U = o.hbU[hb]
        hj = h % 2
        for b in range(4):
            tb = n * 4 + b
            ts = slice(tb * 128, (tb + 1) * 128)
            ai = tb % 2
            P.pe(lambda e, ts=ts, tb=tb: e.matmul(bank(g, 5)[:, 64:128], lhsT=o.kT[hb][:, ts],
                                                   rhs=o.qT[hb][:, tb * 128 + 64:(tb + 1) * 128], start=True, stop=True),
                 reads=[U[QT][n], U[KT][n]], writes=[g.bU[5]])
            P.pe(lambda e, tb=tb: e.matmul(bank(g, 5)[0:64, 0:64], lhsT=o.kT[hb][:, tb * 128:tb * 128 + 64],
                                           rhs=o.qT[hb][:, tb * 128:tb * 128 + 64], start=True, stop=True),
                 reads=[U[QT][n], U[KT][n]], writes=[g.bU[5]])
            P.dve(lambda e, ai=ai: e.tensor_tensor(out=o.attm[ai][:, 64:128], in0=bank(g, 5)[:, 64:128], in1=c["triI"][:, 64:128], op=ALU.mult),
                  reads=[g.bU[5], g.cU], writes=[o.attmU[ai]])
            P.dve(lambda e, ai=ai: e.tensor_tensor(out=o.attm[ai][0:64, 0:64], in0=bank(g, 5)[0:64, 0:64], in1=c["triI"][0:64, 0:64], op=ALU.mult),
                  reads=[g.bU[5], g.cU], writes=[o.attmU[ai]])
            if tb > 0:
                P.dve(lambda e, tb=tb: e.tensor_scalar(out=o.Sb, in0=o.S, scalar1=o.eb[hb][:, tb, 0:1], scalar2=None, op0=ALU.mult),
                      reads=[o.SU, U[EB][n]], writes=[o.SbU])
            P.pe(lambda e, ai=ai, tb=tb: e.matmul(bank(g, 4)[:, 0:128], lhsT=o.attm[ai], rhs=o.v[hb][:, tb, :], start=True, stop=(tb == 0)),
                 reads=[o.attmU[ai], U[VV][n]], writes=[g.bU[4]])
            if tb > 0:
                P.pe(lambda e, ts=ts: e.matmul(bank(g, 4)[:, 0:128], lhsT=o.qT[hb][:, ts], rhs=o.Sb, start=False, stop=True),
                     reads=[o.SbU, U[QT][n]], writes=[g.bU[4]])
            P.pe(lambda e, tb=tb: e.matmul(bank(g, 6)[:, 128:256], lhsT=o.kh[hb][:, tb, :], rhs=o.v[hb][:, tb, :], start=True, stop=True),
                 reads=[U[KH][n], U[VV][n]], writes=[g.bU[6]])
            if tb == 0:
                P.dve(lambda e: e.tensor_copy(o.S, bank(g, 6)[:, 128:256]), reads=[g.bU[6]], writes=[o.SU])
            else:
                P.dve(lambda e, tb=tb: e.scalar_tensor_tensor(out=o.S, in0=o.S, scalar=o.eb[hb][:, tb, 1:2], in1=bank(g, 6)[:, 128:256],
                                                               op0=ALU.mult, op1=ALU.add),
                      reads=[o.SU, g.bU[6], U[EB][n]], writes=[o.SU])
            P.act(lambda e, tb=tb: e.copy(o.oall[:, tb, :], bank(g, 4)[:, 0:128]), reads=[g.bU[4]], writes=[o.oallU])
            P.act(lambda e, tb=tb: e.activation(out=o.junk2, in_=bank(g, 4)[:, 0:128], func=AF.Square, accum_out=o.ssall[:, tb:tb + 1]),
                  reads=[g.bU[4]], writes=[o.junk2U, o.ssU])

    def stageC(h):
        hb = h % 2
        U = o.hbU[hb]
        hj = h % 2
        P.act(lambda e: e.activation(out=o.rsall, in_=o.ssall, func=AF.Ln, scale=1.0 / 128, bias=EPS), reads=[o.ssU], writes=[o.ssU])
        P.act(lambda e: e.activation(out=o.rsall, in_=o.rsall, func=AF.Exp, scale=-0.5), reads=[o.ssU], writes=[o.ssU])
        P.dve(lambda e: e.tensor_tensor(out=o.oall, in0=o.oall, in1=o.rsall.unsqueeze(2).to_broadcast([128, NB, 128]), op=ALU.mult),
              reads=[o.oallU, o.ssU], writes=[o.oallU])
        P.dve(lambda e: e.tensor_tensor(out=o.oall, in0=o.oall, in1=o.hnw.unsqueeze(1).to_broadcast([128, NB, 128]), op=ALU.mult),
              reads=[o.oallU, o.hnwU], writes=[o.oallU])
        P.dve(lambda e: e.tensor_tensor(out=o.oall, in0=o.oall, in1=o.gs[hb], op=ALU.mult),
              reads=[o.oallU] + [U[GS][n] for n in range(NT)], writes=[o.oallU])
        for n in range(NT):
            bk = 4 + (n % 4)
            for b in range(4):
                P.pe(lambda e, bk=bk, b=b, n=n: e.transpose(bank(g, bk)[:, b * 128:(b + 1) * 128], o.oall[:, n * 4 + b, :], c["ident"][:]),
                     reads=[o.oallU, g.cU], writes=[g.bU[bk]])
            P.act(lambda e, bk=bk, n=n: e.copy(o.yT[:, hj, n * 512:(n + 1) * 512], bank(g, bk)), reads=[g.bU[bk]], writes=[o.yTU[hj]])

    def outproj(hp):
        for j in range(2):
            src = g.d["odd_w_out"][oi, (hp * 2 + j) * 128:(hp * 2 + j + 1) * 128, :]
            P.dma("pool", lambda e, j=j, src=src: e.dma_start(out=o.wout[:, j, :], in_=src), f"o_wout{j}", writes=[o.woutU[j]])
        cnt = 0
        for m in range(KC):
            for n in range(NT):
                bk = cnt % 4
                cnt += 1
                for j in range(2):
                    P.pe(lambda e, bk=bk, m=m, n=n, j=j: e.matmul(bank(g, bk), lhsT=o.wout[:, j, m * 128:(m + 1) * 128],
                                                                     rhs=o.yT[:, j, n * 512:(n + 1) * 512], start=(j == 0), stop=(j == 1)),
                         reads=[o.woutU[j], o.yTU[j]], writes=[g.bU[bk]])
                P.dve(lambda e, bk=bk, m=m, n=n: e.tensor_tensor(out=g.hT[:, m, n * 512:(n + 1) * 512], in0=g.hT[:, m, n * 512:(n + 1) * 512],
                                                                   in1=bank(g, bk), op=ALU.add),
                      reads=[g.bU[bk], g.hU[m][n]], writes=[g.hU[m][n]])

    load_w(0)
    for h in range(16):
        if h + 1 < 16:
            load_w(h + 1)
        head_lb(h)
        for n in range(NT):
            stageA(h, n)
        for n in range(NT):
            stageB(h, n)
        stageC(h)
        if h % 2 == 1:
            outproj(h // 2)


def even_ssd(g, li):
    P, L, NB, NT = g.P, g.L, g.NB, g.NT
    ei = li // 2
    c = g.cst
    A = Carver(g)
    s = NS()
    HB = NB * 16
    s.xpre = A.f(515); s.xpreU = P.unit()
    s.cacc = A.f(512); s.caccU = P.unit()
    v3 = lambda ap: ap.rearrange("p (b h) -> p b h", h=16)
    s.dt = A.f(HB); s.atok = A.f(HB); s.acs = A.f(HB); s.eacs = A.f(HB); s.dtd = A.f(HB); s.edl = A.f(HB)
    s.dtU = P.unit()
    s.cw = A.f(64); s.cb = A.f(16); s.dtb = A.f(16); s.Abc = A.f(16); s.Dsk = A.f(16); s.snw = A.f(8)
    s.smallU = P.unit()
    s.S = A.f(256); s.SU = P.unit()
    scan_off = A.fo
    s.Rbd = A.f(512); s.RbdU = P.unit()
    D2 = lambda n: ([A.f(n) for _ in range(2)], P.units(2))
    s.acsTb2, s.acsTbU2 = D2(128)
    s.CBm2, s.CBmU2 = D2(128)
    s.Dm2, s.DmU2 = D2(512)
    s.E2, s.EU2 = D2(512)
    s.t12, s.t1U2 = D2(256)
    s.t22, s.t2U2 = D2(256)
    s.ytmp2, s.ytmpU2 = D2(256)
    s.rstd_all = g.arf[:, scan_off:scan_off + L]; s.rstdallU = P.unit()
    assert scan_off + L <= ARF_N
    s.w = A.b(KC * 768).rearrange("p (k c) -> p k c", k=KC); s.wU = P.unit()
    s.wdt = A.b(KC * 16).rearrange("p (k c) -> p k c", k=KC); s.wdtU = P.unit()
    s.BT = A.b(L); s.CT = A.b(L); s.BCU = [P.unit() for _ in range(NT)]
    s.xtok = A.b(NB * 256).rearrange("p (b c) -> p b c", c=256); s.xtokU = [P.unit() for _ in range(NT)]
    s.Btok = A.b(NB * 128).rearrange("p (b c) -> p b c", c=128); s.BtokU = [P.unit() for _ in range(NT)]
    scan_bo = A.bo
    s.zs2 = [A.b(256) for _ in range(2)]; s.zsU2 = P.units(2)
    s.sc2 = [A.b(512).rearrange("p (h l) -> p h l", h=4) for _ in range(2)]; s.scU2 = P.units(2)
    s.Xdt2 = [A.b(256) for _ in range(2)]; s.XdtU2 = P.units(2)
    s.XB2 = [A.b(256) for _ in range(2)]; s.XBU2 = P.units(2)
    s.Sbf = A.b(256); s.SbfU = P.unit()
    s.yTa = A.b(8 * L).rearrange("p (c t) -> p c t", c=8); s.yTaU = [[P.unit() for _ in range(NT)] for _ in range(8)]
    s.wo = g.arb[:, scan_bo:scan_bo + 1024].rearrange("p (c j) -> p c j", c=8); s.woU = P.unit()
    s.wos = s.wo; s.wosU = s.woU
    ident = c["ident"]

    sm = [s.smallU]
    P.dma("sp", lambda e: e.dma_start(out=s.cw, in_=g.d["conv_w_cols"][ei]), "s_small", writes=sm)
    P.dma("sp", lambda e: e.dma_start(out=s.cb, in_=g.d["conv_b_cols"][ei]), "s_small", writes=sm)
    P.dma("sp", lambda e: e.dma_start(out=s.dtb, in_=g.d["dt_bias"][ei].partition_broadcast(128)), "s_small", writes=sm)
    P.dma("sp", lambda e: e.dma_start(out=s.Abc, in_=g.d["A_log"][ei].partition_broadcast(128)), "s_small", writes=sm)
    P.dma("sp", lambda e: e.dma_start(out=s.Dsk, in_=g.d["D_skip"][ei].partition_broadcast(128)), "s_small", writes=sm)
    P.dma("sp", lambda e: e.dma_start(out=s.snw, in_=g.d["ssd_norm_w_cols"][ei]), "s_small", writes=sm)
    P.act(lambda e: e.activation(out=s.Abc, in_=s.Abc, func=AF.Exp), reads=sm, writes=sm)
    P.dve(lambda e: e.tensor_scalar(out=s.Abc, in0=s.Abc, scalar1=-1.0, scalar2=None, op0=ALU.mult), reads=sm, writes=sm)
    P.dma("pool", lambda e: e.dma_start(out=s.wdt.rearrange("p k c -> p (k c)"), in_=g.d["ev_w_dt"][ei]), "s_wdt", writes=[s.wdtU])

    for b in range(NB):
        for k in range(KC):
            P.pe(lambda e, b=b, k=k: e.matmul(bank(g, 0)[:, b * 16:(b + 1) * 16], lhsT=g.uT[:, k, b * 128:(b + 1) * 128],
                                              rhs=s.wdt[:, k, :], start=(k == 0), stop=(k == KC - 1)),
                 reads=[g.uU[b // 4], s.wdtU], writes=[g.bU[0]])
    bc_h = lambda t: t.unsqueeze(1).to_broadcast([128, NB, 16])
    du = [s.dtU]
    P.dve(lambda e: e.tensor_tensor(out=v3(s.dt), in0=v3(bank(g, 0)[:, 0:HB]), in1=bc_h(s.dtb), op=ALU.add),
          reads=[g.bU[0]] + sm, writes=du)
    P.act(lambda e: e.activation(out=s.dt, in_=s.dt, func=AF.Exp), reads=du, writes=du)
    P.act(lambda e: e.activation(out=s.dt, in_=s.dt, func=AF.Ln, bias=1.0), reads=du, writes=du)
    P.dve(lambda e: e.tensor_tensor(out=v3(s.atok), in0=v3(s.dt), in1=bc_h(s.Abc), op=ALU.mult), reads=du + sm, writes=du)
    for b in range(NB):
        P.pe(lambda e, b=b: e.matmul(bank(g, 1)[:, b * 16:(b + 1) * 16], lhsT=c["triI"][:], rhs=s.atok[:, b * 16:(b + 1) * 16],
                                     start=True, stop=True), reads=du + [g.cU], writes=[g.bU[1]])
    for b in range(NB):
        P.pe(lambda e, b=b: e.matmul(bank(g, 2)[:, b * 16:(b + 1) * 16], lhsT=c["ones"][:], rhs=s.atok[:, b * 16:(b + 1) * 16],
                                     start=True, stop=True), reads=du + [g.cU], writes=[g.bU[2]])
    P.dve(lambda e: e.tensor_copy(s.acs, bank(g, 1)[:, 0:HB]), reads=[g.bU[1]], writes=du)
    P.act(lambda e: e.activation(out=s.eacs, in_=s.acs, func=AF.Exp), reads=du, writes=du)
    P.dve(lambda e: e.tensor_copy(s.edl, bank(g, 2)[:, 0:HB]), reads=[g.bU[2]], writes=du)
    P.dve(lambda e: e.tensor_tensor(out=s.dtd, in0=s.edl, in1=s.acs, op=ALU.subtract), reads=du, writes=du)
    P.act(lambda e: e.activation(out=s.dtd, in_=s.dtd, func=AF.Exp), reads=du, writes=du)
    P.dve(lambda e: e.tensor_tensor(out=s.dtd, in0=s.dtd, in1=s.dt, op=ALU.mult), reads=du, writes=du)
    P.act(lambda e: e.activation(out=s.edl, in_=s.edl, func=AF.Exp), reads=du, writes=du)

    pcnt = [0]
    for grp in range(4):
        P.dma("pool", lambda e, grp=grp: e.dma_start(out=s.w.rearrange("p k c -> p (k c)"), in_=g.d["ev_w_ssd"][ei, grp]),
              "s_w", writes=[s.wU])
        chunks = [(256, 2 * grp, "x0"), (384, 2 * grp + 1, "x1"), (512, 8 + grp, "B"), (640, 12 + grp, "C")]
        for wc0, cch, kind in chunks:
            for n in range(NT):
                sl = slice(n * 512, (n + 1) * 512)
                bk = 3 + (pcnt[0] % 2)
                pcnt[0] += 1
                for k in range(KC):
                    P.pe(lambda e, bk=bk, k=k, wc0=wc0, sl=sl: e.matmul(bank(g, bk), lhsT=s.w[:, k, wc0:wc0 + 128], rhs=g.uT[:, k, sl],
                                                                        start=(k == 0), stop=(k == KC - 1)),
                         reads=[g.uU[n], s.wU], writes=[g.bU[bk]])
                if n == 0:
                    P.dve(lambda e: e.memset(s.xpre[:, 0:3], 0.0), writes=[s.xpreU])
                else:
                    P.dve(lambda e: e.tensor_copy(s.xpre[:, 0:3], s.xpre[:, 512:515]), reads=[s.xpreU], writes=[s.xpreU])
                P.act(lambda e, bk=bk: e.copy(s.xpre[:, 3:515], bank(g, bk)), reads=[g.bU[bk]], writes=[s.xpreU])
                P.dve(lambda e, cch=cch: e.tensor_scalar(out=s.cacc, in0=s.xpre[:, 3:515], scalar1=s.cw[:, cch * 4 + 3:cch * 4 + 4],
                                                          scalar2=s.cb[:, cch:cch + 1], op0=ALU.mult, op1=ALU.add),
                      reads=[s.xpreU] + sm, writes=[s.caccU])
                for tap in (2, 1, 0):
                    P.dve(lambda e, cch=cch, tap=tap: e.scalar_tensor_tensor(
                        out=s.cacc, in0=s.xpre[:, tap:tap + 512], scalar=s.cw[:, cch * 4 + tap:cch * 4 + tap + 1], in1=s.cacc,
                        op0=ALU.mult, op1=ALU.add), reads=[s.xpreU, s.caccU] + sm, writes=[s.caccU])
                P.act(lambda e: e.activation(out=s.cacc, in_=s.cacc, func=AF.Silu), reads=[s.caccU], writes=[s.caccU])
                if kind in ("B", "C"):
                    dst = s.BT if kind == "B" else s.CT
                    P.dve(lambda e, dst=dst, sl=sl: e.tensor_copy(dst[:, sl], s.cacc), reads=[s.caccU], writes=[s.BCU[n]])
                if kind != "C":
                    for j in range(4):
                        P.pe(lambda e, j=j: e.transpose(bank(g, 5)[:, j * 128:(j + 1) * 128], s.cacc[:, j * 128:(j + 1) * 128], ident[:]),
                             reads=[s.caccU, g.cU], writes=[g.bU[5]])
                    src = bank(g, 5).rearrange("p (b c) -> p b c", b=4)
                    if kind == "B":
                        P.act(lambda e, n=n, src=src: e.copy(s.Btok[:, n * 4:(n + 1) * 4, :], src), reads=[g.bU[5]], writes=[s.BtokU[n]])
                    else:
                        co = 0 if kind == "x0" else 128
                        P.act(lambda e, n=n, src=src, co=co: e.copy(s.xtok[:, n * 4:(n + 1) * 4, co:co + 128], src),
                              reads=[g.bU[5]], writes=[s.xtokU[n]])
        hs4 = slice(4 * grp, 4 * grp + 4)
        for b in range(NB):
            n = b // 4
            blk = slice(b * 128, (b + 1) * 128)
            hcol = lambda t, b=b: v3(t)[:, b, hs4]
            bch = lambda t, w, b=b: hcol(t, b).unsqueeze(2).to_broadcast([128, 4, w])
            x4 = s.xtok[:, b, :].rearrange("p (h q) -> p h q", h=4)
            pb_ = b % 2
            s.acsTb, s.acsTbU = s.acsTb2[pb_], s.acsTbU2[pb_]
            s.CBm, s.CBmU = s.CBm2[pb_], s.CBmU2[pb_]
            s.Dm, s.DmU = s.Dm2[pb_], s.DmU2[pb_]
            s.E, s.EU = s.E2[pb_], s.EU2[pb_]
            s.t1, s.t1U = s.t12[pb_], s.t1U2[pb_]
            s.t2, s.t2U = s.t22[pb_], s.t2U2[pb_]
            s.ytmp, s.ytmpU = s.ytmp2[pb_], s.ytmpU2[pb_]
            s.zs, s.zsU = s.zs2[pb_], s.zsU2[pb_]
            s.sc, s.scU = s.sc2[pb_], s.scU2[pb_]
            s.Xdt, s.XdtU = s.Xdt2[pb_], s.XdtU2[pb_]
            s.XB, s.XBU = s.XB2[pb_], s.XBU2[pb_]
            for k in range(KC):
                P.pe(lambda e, k=k, blk=blk: e.matmul(bank(g, 6)[:, 0:256], lhsT=g.uT[:, k, blk], rhs=s.w[:, k, 0:256],
                                                      start=(k == 0), stop=(k == KC - 1)),
                     reads=[g.uU[n], s.wU], writes=[g.bU[6]])
            P.act(lambda e: e.activation(out=s.zs, in_=bank(g, 6)[:, 0:256], func=AF.Silu), reads=[g.bU[6]], writes=[s.zsU])
            P.pe(lambda e, blk=blk: e.matmul(bank(g, 7)[:, 0:128], lhsT=s.BT[:, blk], rhs=s.CT[:, blk], start=True, stop=True),
                 reads=[s.BCU[n]], writes=[g.bU[7]])
            P.dve(lambda e: e.tensor_tensor(out=s.CBm, in0=bank(g, 7)[:, 0:128], in1=c["triI"][:], op=ALU.mult),
                  reads=[g.bU[7], g.cU], writes=[s.CBmU])
            P.pe(lambda e, b=b: e.transpose(bank(g, 0)[0:16, 0:128], s.acs[:, b * 16:(b + 1) * 16], ident[:]),
                 reads=du + [g.cU], writes=[g.bU[0]])
            P.act(lambda e: e.copy(s.acsTb[0:16, :], bank(g, 0)[0:16, 0:128]), reads=[g.bU[0]], writes=[s.acsTbU])
            P.dve(lambda e: e.tensor_tensor(out=s.Rbd[0:16, :].rearrange("p (h l) -> p h l", h=4),
                                            in0=s.acsTb[0:16, :].unsqueeze(1).to_broadcast([16, 4, 128]),
                                            in1=c["mg16"][:, hs4].unsqueeze(2).to_broadcast([16, 4, 128]), op=ALU.mult),
                  reads=[s.acsTbU, g.cU], writes=[s.RbdU])
            P.pe(lambda e: e.matmul(bank(g, 1), lhsT=c["ones"][0:16, :], rhs=s.Rbd[0:16, :], start=True, stop=True),
                 reads=[s.RbdU, g.cU], writes=[g.bU[1]])
            P.dve(lambda e, b=b: e.tensor_tensor(out=s.Dm.rearrange("p (h l) -> p h l", h=4),
                                                 in0=bank(g, 1).rearrange("p (h l) -> p h l", h=4),
                                                 in1=bch(s.acs, 128, b), op=ALU.subtract),
                  reads=[g.bU[1]] + du, writes=[s.DmU])
            P.dve(lambda e: e.tensor_scalar(out=s.Dm, in0=s.Dm, scalar1=0.0, scalar2=None, op0=ALU.min), reads=[s.DmU], writes=[s.DmU])
            P.act(lambda e: e.activation(out=s.E, in_=s.Dm, func=AF.Exp), reads=[s.DmU], writes=[s.EU])
            P.dve(lambda e: e.tensor_tensor(out=s.sc, in0=s.E.rearrange("p (h l) -> p h l", h=4),
                                            in1=s.CBm.unsqueeze(1).to_broadcast([128, 4, 128]), op=ALU.mult),
                  reads=[s.EU, s.CBmU], writes=[s.scU])
            P.dve(lambda e, b=b: e.tensor_tensor(out=s.Xdt.rearrange("p (h q) -> p h q", h=4), in0=x4, in1=bch(s.dt, 64, b), op=ALU.mult),
                  reads=[s.xtokU[n]] + du, writes=[s.XdtU])
            P.dve(lambda e, b=b: e.tensor_tensor(out=s.XB.rearrange("p (h q) -> p h q", h=4), in0=x4, in1=bch(s.dtd, 64, b), op=ALU.mult),
                  reads=[s.xtokU[n]] + du, writes=[s.XBU])
            for h4 in range(4):
                P.pe(lambda e, h4=h4: e.matmul(bank(g, 2)[:, h4 * 64:(h4 + 1) * 64], lhsT=s.sc[:, h4, :], rhs=s.Xdt[:, h4 * 64:(h4 + 1) * 64],
                                               start=True, stop=True), reads=[s.scU, s.XdtU], writes=[g.bU[2]])
            if b > 0:
                P.pe(lambda e, blk=blk: e.matmul(bank(g, 3)[:, 0:256], lhsT=s.CT[:, blk], rhs=s.Sbf, start=True, stop=True),
                     reads=[s.BCU[n], s.SbfU], writes=[g.bU[3]])
            P.pe(lambda e, b=b: e.matmul(bank(g, 4)[:, 0:256], lhsT=s.Btok[:, b, :], rhs=s.XB, start=True, stop=True),
                 reads=[s.BtokU[n], s.XBU], writes=[g.bU[4]])
            if b > 0:
                P.dve(lambda e, b=b: e.tensor_tensor(out=s.t1.rearrange("p (h q) -> p h q", h=4),
                                                     in0=bank(g, 3)[:, 0:256].rearrange("p (h q) -> p h q", h=4),
                                                     in1=bch(s.eacs, 64, b), op=ALU.mult), reads=[g.bU[3]] + du, writes=[s.t1U])
                P.dve(lambda e: e.tensor_tensor(out=s.t2, in0=bank(g, 2)[:, 0:256], in1=s.t1, op=ALU.add), reads=[g.bU[2], s.t1U], writes=[s.t2U])
            else:
                P.dve(lambda e: e.tensor_copy(s.t2, bank(g, 2)[:, 0:256]), reads=[g.bU[2]], writes=[s.t2U])
            P.dve(lambda e: e.tensor_tensor(out=s.t1.rearrange("p (h q) -> p h q", h=4), in0=x4,
                                            in1=s.Dsk[:, hs4].unsqueeze(2).to_broadcast([128, 4, 64]), op=ALU.mult),
                  reads=[s.xtokU[n]] + sm, writes=[s.t1U])
            P.dve(lambda e: e.tensor_tensor(out=s.t2, in0=s.t2, in1=s.t1, op=ALU.add), reads=[s.t1U, s.t2U], writes=[s.t2U])
            P.dve(lambda e: e.tensor_tensor(out=s.ytmp, in0=s.t2, in1=s.zs, op=ALU.mult), reads=[s.t2U, s.zsU], writes=[s.ytmpU])
            for j in range(2):
                P.pe(lambda e, j=j: e.transpose(bank(g, 5)[:, j * 128:(j + 1) * 128], s.ytmp[:, j * 128:(j + 1) * 128], ident[:]),
                     reads=[s.ytmpU, g.cU], writes=[g.bU[5]])
            P.act(lambda e, blk=blk: e.copy(s.yTa[:, 2 * grp:2 * grp + 2, blk], bank(g, 5)[:, 0:256].rearrange("p (j t) -> p j t", j=2)),
                  reads=[g.bU[5]], writes=[s.yTaU[2 * grp][n], s.yTaU[2 * grp + 1][n]])
            if b == 0:
                P.dve(lambda e: e.tensor_copy(s.S, bank(g, 4)[:, 0:256]), reads=[g.bU[4]], writes=[s.SU])
            else:
                P.dve(lambda e, b=b: e.tensor_tensor(out=s.S.rearrange("p (h q) -> p h q", h=4), in0=s.S.rearrange("p (h q) -> p h q", h=4),
                                                     in1=bch(s.edl, 64, b), op=ALU.mult), reads=[s.SU] + du, writes=[s.SU])
                P.dve(lambda e: e.tensor_tensor(out=s.S, in0=s.S, in1=bank(g, 4)[:, 0:256], op=ALU.add), reads=[s.SU, g.bU[4]], writes=[s.SU])
            if b + 1 < NB:
                P.act(lambda e: e.copy(s.Sbf, s.S), reads=[s.SU], writes=[s.SbfU])

    P.barrier()
    for n in range(NT):
        sl = slice(n * 512, (n + 1) * 512)
        rms_rstd_tile(g, lambda k, sl=sl: s.yTa[:, k, sl], lambda k, n=n: [s.yTaU[k][n]], 8, 1024)
        P.dve(lambda e, sl=sl: e.tensor_copy(s.rstd_all[:, sl], g.rstd_t[:]), reads=[g.rstdU], writes=[s.rstdallU])
    cnt = 0
    for m in range(KC):
        P.dma("pool", lambda e, m=m: e.dma_start(out=s.wo.rearrange("p c j -> p (c j)"), in_=g.d["ev_w_out_t"][ei, m, :, 0:1024]),
              "s_wo", writes=[s.woU])
        P.dve(lambda e: e.tensor_tensor(out=s.wos, in0=s.wo, in1=s.snw[:, 0:8].unsqueeze(2).to_broadcast([128, 8, 128]), op=ALU.mult),
              reads=[s.woU] + sm, writes=[s.wosU])
        for n in range(NT):
            sl = slice(n * 512, (n + 1) * 512)
            bk = cnt % 2
            cnt += 1
            for k in range(8):
                P.pe(lambda e, bk=bk, k=k, sl=sl: e.matmul(bank(g, bk), lhsT=s.wos[:, k, :], rhs=s.yTa[:, k, sl], start=(k == 0), stop=(k == 7)),
                     reads=[s.wosU, s.yTaU[k][n]], writes=[g.bU[bk]])
            P.dve(lambda e, bk=bk, sl=sl: e.tensor_tensor(out=s.cacc, in0=bank(g, bk), in1=s.rstd_all[:, sl], op=ALU.mult),
                  reads=[g.bU[bk], s.rstdallU], writes=[s.caccU])
            P.pool(lambda e, m=m, sl=sl: e.tensor_tensor(out=g.hT[:, m, sl], in0=g.hT[:, m, sl], in1=s.cacc, op=ALU.add),
                   reads=[s.caccU, g.hU[m][n]], writes=[g.hU[m][n]])


def even_attn(g, li):
    P, L, NB, NT = g.P, g.L, g.NB, g.NT
    ei = li // 2
    lam_init = 0.8 - 0.6 * math.exp(-0.3 * li)
    c = g.cst
    A = Carver(g)
    a = NS()
    a.corr = A.f(2048).rearrange("p (h d q) -> p h d q", h=8, d=2); a.corrU = P.unit()
    a.b31 = A.f(8); a.nb31 = A.f(8); a.bU_ = P.unit()
    a.lq = [A.f(64) for _ in range(4)]; a.lamU = P.unit()
    a.lam = A.f(8)
    a.slnw = A.f(128); a.slnwU = P.unit()
    a.w = [A.b(KC * 512).rearrange("p (k c) -> p k c", k=KC) for _ in range(2)]; a.wU = P.units(2)
    a.qT = [A.b(L) for _ in range(2)]; a.qTU = [P.unit() for _ in range(NT)]
    a.kT = A.b(L); a.kTU = [P.unit() for _ in range(NT)]
    a.v = A.b(NB * 132).rearrange("p (b c) -> p b c", c=132); a.vU = P.unit()
    a.gs = A.b(NB * 128).rearrange("p (b c) -> p b c", c=128); a.gsU = P.unit()
    a.PT = [[A.b(512) for _ in range(2)] for _ in range(2)]; a.PTU = [P.units(2), P.units(2)]
    a.yT = A.b(4 * L).rearrange("p (j t) -> p j t", j=4); a.yTU = P.units(4)
    a.wo = [A.b(512).rearrange("p (j c) -> p j c", j=4) for _ in range(2)]; a.woU = P.units(2)
    ident = c["ident"]

    P.dma("sp", lambda e: e.dma_start(out=a.corr.rearrange("p h d q -> p (h d q)"), in_=g.d["rel_biasD"]), "a_corr", writes=[a.corrU])
    P.dma("sp", lambda e: e.dma_start(out=a.b31, in_=g.d["rel_b31"].partition_broadcast(128)), "a_b31", writes=[a.bU_])
    P.dve(lambda e: e.tensor_scalar(out=a.nb31, in0=a.b31, scalar1=-1.0, scalar2=None, op0=ALU.mult), reads=[a.bU_], writes=[a.bU_])
    for h in range(8):
        P.act(lambda e, h=h: e.activation(out=a.corr[:, h], in_=a.corr[:, h], func=AF.Exp, bias=a.nb31[:, h:h + 1]),
              reads=[a.corrU, a.bU_], writes=[a.corrU])
    for i, nm in enumerate(("lambda_q1", "lambda_k1", "lambda_q2", "lambda_k2")):
        P.dma("sp", lambda e, i=i, nm=nm: e.dma_start(out=a.lq[i], in_=g.d[nm][ei].partition_broadcast(128)), "a_lam", writes=[a.lamU])
    lu = [a.lamU]
    P.dve(lambda e: e.tensor_tensor(out=a.lq[0], in0=a.lq[0], in1=a.lq[1], op=ALU.mult), reads=lu, writes=lu)
    P.dve(lambda e: e.tensor_tensor(out=a.lq[2], in0=a.lq[2], in1=a.lq[3], op=ALU.mult), reads=lu, writes=lu)
    P.dve(lambda e: e.tensor_reduce(out=a.lam[:, 0:1], in_=a.lq[0], axis=AX.X, op=ALU.add), reads=lu, writes=lu)
    P.dve(lambda e: e.tensor_reduce(out=a.lam[:, 1:2], in_=a.lq[2], axis=AX.X, op=ALU.add), reads=lu, writes=lu)
    P.act(lambda e: e.activation(out=a.lam[:, 2:4], in_=a.lam[:, 0:2], func=AF.Exp), reads=lu, writes=lu)
    P.dve(lambda e: e.tensor_tensor(out=a.lam[:, 4:5], in0=a.lam[:, 3:4], in1=a.lam[:, 2:3], op=ALU.subtract), reads=lu, writes=lu)
    P.dve(lambda e: e.tensor_scalar(out=a.lam[:, 5:6], in0=a.lam[:, 4:5], scalar1=-lam_init, scalar2=None, op0=ALU.add), reads=lu, writes=lu)
    P.dma("sp", lambda e: e.dma_start(out=a.slnw, in_=g.d["subln_w"][ei].partition_broadcast(128)), "a_slnw", writes=[a.slnwU])
    P.dve(lambda e: e.tensor_scalar(out=a.slnw, in0=a.slnw, scalar1=1.0 - lam_init, scalar2=None, op0=ALU.mult),
          reads=[a.slnwU], writes=[a.slnwU])
    P.dve(lambda e: e.memset(a.v, 1.0), writes=[a.vU])

    def load_w(h):
        i = h % 2
        P.dma("pool", lambda e: e.dma_start(out=a.w[i].rearrange("p k c -> p (k c)"), in_=g.d["ev_w_att"][ei, h]),
              f"a_w{i}", writes=[a.wU[i]])

    pc = [0]

    def project(h):
        wi = h % 2
        w = a.w[wi]
        for n in range(NT):
            sl = slice(n * 512, (n + 1) * 512)
            for which in range(2):
                bk = pc[0] % 2
                pc[0] += 1
                for k in range(KC):
                    P.pe(lambda e, bk=bk, k=k, sl=sl, which=which: e.matmul(bank(g, bk), lhsT=w[:, k, which * 128:(which + 1) * 128],
                                                                            rhs=g.uT[:, k, sl], start=(k == 0), stop=(k == KC - 1)),
                         reads=[g.uU[n], a.wU[wi]], writes=[g.bU[bk]])
                if which == 0:
                    for cc in range(2):
                        P.dve(lambda e, bk=bk, sl=sl, cc=cc: e.tensor_scalar(out=a.qT[cc][:, sl], in0=bank(g, bk), scalar1=c["maskq"][:, cc:cc + 1],
                                                                             scalar2=None, op0=ALU.mult),
                              reads=[g.bU[bk], g.cU], writes=[a.qTU[n]])
                else:
                    P.act(lambda e, bk=bk, sl=sl: e.copy(a.kT[:, sl], bank(g, bk)), reads=[g.bU[bk]], writes=[a.kTU[n]])
        for b in range(NB):
            bk = 2 + (b % 2)
            for k in range(KC):
                P.pe(lambda e, bk=bk, k=k, b=b: e.matmul(bank(g, bk)[:, 0:256], lhsT=g.uT[:, k, b * 128:(b + 1) * 128], rhs=w[:, k, 256:512],
                                                         start=(k == 0), stop=(k == KC - 1)),
                     reads=[g.uU[b // 4], a.wU[wi]], writes=[g.bU[bk]])
            P.act(lambda e, bk=bk, b=b: e.copy(a.v[:, b, 0:128], bank(g, bk)[:, 0:128]), reads=[g.bU[bk]], writes=[a.vU])
            P.act(lambda e, bk=bk, b=b: e.activation(out=a.gs[:, b, :], in_=bank(g, bk)[:, 128:256], func=AF.Silu),
                  reads=[g.bU[bk]], writes=[a.gsU])

    gc = [0]
    a.r2 = [A.f(8) for _ in range(2)]; a.rU2 = P.units(2)
    a.t2 = [A.f(128) for _ in range(2)]; a.tU2 = P.units(2)
    a.o2 = [A.f(128) for _ in range(2)]; a.oU2 = P.units(2)
    a.sqo2 = [A.f(128) for _ in range(2)]; a.sqoU2 = P.units(2)
    a.y2 = [A.f(128) for _ in range(2)]; a.yU2 = P.units(2)

    def attend(h):
        hj = h % 4
        groups = []
        for qb in range(NB):
            for gi in range(qb // 4 + 1):
                kbs = [kb for kb in range(gi * 4, gi * 4 + 4) if kb <= qb]
                groups.append((qb, gi, kbs, gc[0] % 2))
                gc[0] += 1

        def accb(qb, cc):
            return (2 + cc) if qb % 2 == 0 else cc

        def S_(grp):
            qb, gi, kbs, buf = grp
            qs = slice(qb * 128, (qb + 1) * 128)
            for cc in range(2):
                bk = 4 + 2 * cc + buf
                for j, kb in enumerate(kbs):
                    P.pe(lambda e, bk=bk, j=j, kb=kb, cc=cc: e.matmul(bank(g, bk)[:, j * 128:(j + 1) * 128],
                                                                      lhsT=a.kT[:, kb * 128:(kb + 1) * 128], rhs=a.qT[cc][:, qs],
                                                                      start=True, stop=True),
                         reads=[a.kTU[kb // 4], a.qTU[qb // 4]], writes=[g.bU[bk]])

        def E_(grp):
            qb, gi, kbs, buf = grp
            nv = len(kbs)
            for cc in range(2):
                bk = 4 + 2 * cc + buf
                pt = a.PT[cc][buf]
                ptu = a.PTU[cc][buf]
                P.act(lambda e, bk=bk, pt=pt, nv=nv: e.activation(out=pt[:, 0:nv * 128], in_=bank(g, bk)[:, 0:nv * 128], func=AF.Exp,
                                                                  scale=0.125, bias=a.b31[:, h:h + 1]),
                      reads=[g.bU[bk], a.bU_], writes=[ptu])
                for j, kb in enumerate(kbs):
                    Dd = qb - kb
                    if Dd <= 1:
                        P.dve(lambda e, pt=pt, j=j, Dd=Dd: e.tensor_tensor(out=pt[:, j * 128:(j + 1) * 128], in0=pt[:, j * 128:(j + 1) * 128],
                                                                           in1=a.corr[:, h, Dd, :], op=ALU.mult),
                              reads=[ptu, a.corrU], writes=[ptu])

        def PV_(grp):
            qb, gi, kbs, buf = grp
            for cc in range(2):
                pt = a.PT[cc][buf]
                ptu = a.PTU[cc][buf]
                ab = accb(qb, cc)
                for j, kb in enumerate(kbs):
                    P.pe(lambda e, ab=ab, pt=pt, j=j, kb=kb: e.matmul(bank(g, ab)[:, 0:129], lhsT=pt[:, j * 128:(j + 1) * 128],
                                                                      rhs=a.v[:, kb, 0:129], start=(kb == 0), stop=(kb == qb)),
                         reads=[ptu, a.vU], writes=[g.bU[ab]])

        def FIN_(qb):
            qs = slice(qb * 128, (qb + 1) * 128)
            pq = qb % 2
            b0, b1 = accb(qb, 0), accb(qb, 1)
            r, t_, o_, sqo, y_ = a.r2[pq], a.t2[pq], a.o2[pq], a.sqo2[pq], a.y2[pq]
            ru = [a.rU2[pq]]
            tU, oU, sqoU, yU = a.tU2[pq], a.oU2[pq], a.sqoU2[pq], a.yU2[pq]
            P.dve(lambda e: e.reciprocal(r[:, 0:1], bank(g, b0)[:, 128:129]), reads=[g.bU[b0]], writes=ru)
            P.dve(lambda e: e.reciprocal(r[:, 1:2], bank(g, b1)[:, 128:129]), reads=[g.bU[b1]], writes=ru)
            P.dve(lambda e: e.tensor_tensor(out=r[:, 2:3], in0=r[:, 1:2], in1=a.lam[:, 5:6], op=ALU.mult), reads=ru + lu, writes=ru)
            P.dve(lambda e: e.tensor_scalar(out=t_, in0=bank(g, b0)[:, 0:128], scalar1=r[:, 0:1], scalar2=None, op0=ALU.mult),
                  reads=[g.bU[b0]] + ru, writes=[tU])
            P.dve(lambda e: e.scalar_tensor_tensor(out=o_, in0=bank(g, b1)[:, 0:128], scalar=r[:, 2:3], in1=t_, op0=ALU.mult, op1=ALU.add),
                  reads=[g.bU[b1], tU] + ru, writes=[oU])
            P.dve(lambda e: e.tensor_tensor(out=sqo, in0=o_, in1=o_, op=ALU.mult), reads=[oU], writes=[sqoU])
            P.dve(lambda e: e.tensor_reduce(out=r[:, 3:4], in_=sqo, axis=AX.X, op=ALU.add), reads=[sqoU], writes=ru)
            P.act(lambda e: e.activation(out=r[:, 4:5], in_=r[:, 3:4], func=AF.Ln, scale=1.0 / 128, bias=EPS), reads=ru, writes=ru)
            P.act(lambda e: e.activation(out=r[:, 5:6], in_=r[:, 4:5], func=AF.Exp, scale=-0.5), reads=ru, writes=ru)
            P.dve(lambda e: e.scalar_tensor_tensor(out=y_, in0=o_, scalar=r[:, 5:6], in1=a.slnw, op0=ALU.mult, op1=ALU.mult),
                  reads=[oU, a.slnwU] + ru, writes=[yU])
            P.dve(lambda e: e.tensor_tensor(out=y_, in0=y_, in1=a.gs[:, qb, :], op=ALU.mult), reads=[yU, a.gsU], writes=[yU])
            P.pe(lambda e: e.transpose(bank(g, b0)[:, 256:384], y_, ident[:]), reads=[yU, g.cU], writes=[g.bU[b0]])
            P.act(lambda e: e.copy(a.yT[:, hj, qs], bank(g, b0)[:, 256:384]), reads=[g.bU[b0]], writes=[a.yTU[hj]])

        M = len(groups)
        S_(groups[0])
        pending_fin = None
        for i in range(M):
            if i + 1 < M:
                S_(groups[i + 1])
            E_(groups[i])
            PV_(groups[i])
            if pending_fin is not None:
                FIN_(pending_fin)
                pending_fin = None
            qb, gi, kbs, buf = groups[i]
            if kbs[-1] == qb:
                pending_fin = qb
        if pending_fin is not None:
            FIN_(pending_fin)

    oc = [0]

    def outproj(hg):
        for m in range(KC):
            wi = oc[0] % 2
            oc[0] += 1
            c0 = (8 + hg * 4) * 128
            P.dma("pool", lambda e, m=m, wi=wi, c0=c0: e.dma_start(out=a.wo[wi].rearrange("p j c -> p (j c)"),
                                                                  in_=g.d["ev_w_out_t"][ei, m, :, c0:c0 + 512]),
                  f"a_wo{wi}", writes=[a.woU[wi]])
            for n in range(NT):
                sl = slice(n * 512, (n + 1) * 512)
                bk = n % 2
                for j in range(4):
                    P.pe(lambda e, bk=bk, j=j, sl=sl, wi=wi: e.matmul(bank(g, bk), lhsT=a.wo[wi][:, j, :], rhs=a.yT[:, j, sl],
                                                                      start=(j == 0), stop=(j == 3)),
                         reads=[a.woU[wi], a.yTU[j]], writes=[g.bU[bk]])
                P.dve(lambda e, bk=bk, m=m, sl=sl: e.tensor_tensor(out=g.hT[:, m, sl], in0=g.hT[:, m, sl], in1=bank(g, bk), op=ALU.add),
                      reads=[g.bU[bk], g.hU[m][n]], writes=[g.hU[m][n]])

    load_w(0)
    for h in range(8):
        if h + 1 < 8:
            load_w(h + 1)
        project(h)
        attend(h)
        if h % 4 == 3:
            outproj(h // 4)


_CACHE = {}


def kernel(**inputs):
    x = np.ascontiguousarray(np.asarray(inputs["x"], dtype=np.float32))
    Bsz, L, _ = x.shape
    n_cores = 8
    nseq = Bsz // n_cores
    key = (L, nseq)
    if key not in _CACHE:
        _CACHE[key] = build(L, nseq, (0, 1, 2, 3))
    nc, _ = _CACHE[key]
    common = host_layout(inputs)
    in_maps = []
    for cidx in range(n_cores):
        m = dict(common)
        m["x"] = x[cidx * nseq:(cidx + 1) * nseq]
        in_maps.append(m)
    res = run_bass_kernel_spmd(nc, in_maps, core_ids=list(range(n_cores)))
    out = np.concatenate([np.asarray(r["out"]) for r in res.results], axis=0)
    return out.astype(np.float32)
```

```python
import math, contextlib
import numpy as np
import concourse.bass as bass
import concourse.mybir as mybir
from concourse.bass_utils import run_bass_kernel_spmd
from concourse.alu_op_type import AluOpType as ALU

F32 = mybir.dt.float32
BF16 = mybir.dt.bfloat16
AF = mybir.ActivationFunctionType
AX = mybir.AxisListType

D = 1024
KC = 8
EPS = 1e-6
DEPTH = 4
HG_W = 2048
ARF_N = 7616
ARB_N = 35840


class Unit:
    __slots__ = ("name", "lw", "rd")

    def __init__(self, name):
        self.name = name
        self.lw = None
        self.rd = []


class _Rec:
    def __init__(self):
        self.call = None

    def __getattr__(self, name):
        def f(*args, **kw):
            assert self.call is None
            self.call = (name, args, kw)
            return None
        return f


class Prog:
    ENGS = ("pe", "act", "dve", "pool", "sp")

    def __init__(self, nc):
        self.nc = nc
        self.ops = []
        self.nunits = 0
        self.last_eng = {}
        self.last_key = {}

    def unit(self, name=None):
        self.nunits += 1
        return Unit(name or f"u{self.nunits}")

    def units(self, n, name="u"):
        return [self.unit(f"{name}{i}") for i in range(n)]

    def capture(self):
        self._cap = []
        return self._cap

    def end_capture(self):
        c, self._cap = self._cap, None
        return c

    def replay_merged(self, A, B):
        na, nb = len(A), len(B)
        ia = ib = 0
        while ia < na or ib < nb:
            if ib >= nb or (ia < na and ia * nb <= ib * na):
                self.op(*A[ia]); ia += 1
            else:
                self.op(*B[ib]); ib += 1

    def op(self, eng, fn, reads=(), writes=(), dma_key=None, extra_deps=()):
        if fn is not None and not isinstance(fn, tuple):
            rec = _Rec()
            fn(rec)
            assert rec.call is not None
            fn = rec.call
        if getattr(self, "_cap", None) is not None:
            self._cap.append((eng, fn, tuple(reads), tuple(writes), dma_key, tuple(extra_deps)))
            return None
        idx = len(self.ops)
        deps = set(extra_deps)
        for u in reads:
            if u.lw is not None:
                deps.add(u.lw)
        for u in writes:
            if u.lw is not None:
                deps.add(u.lw)
            deps.update(u.rd)
        for u in reads:
            u.rd.append(idx)
        for u in writes:
            u.lw = idx
            u.rd = []
        deps.discard(idx)
        self.ops.append(dict(eng=eng, fn=fn, deps=deps, dma_key=dma_key))
        if fn is not None:
            if dma_key is None:
                self.last_eng[eng] = idx
            else:
                self.last_key[dma_key] = idx
        return idx

    def pe(self, fn, reads=(), writes=()):
        return self.op("pe", fn, reads, writes)

    def act(self, fn, reads=(), writes=()):
        return self.op("act", fn, reads, writes)

    def dve(self, fn, reads=(), writes=()):
        return self.op("dve", fn, reads, writes)

    def pool(self, fn, reads=(), writes=()):
        return self.op("pool", fn, reads, writes)

    def dma(self, eng, fn, key, reads=(), writes=()):
        return self.op(eng, fn, reads, writes, dma_key=key)

    def barrier(self):
        deps = set(self.last_eng.values()) | set(self.last_key.values())
        for e in self.ENGS:
            self.op(e, None, extra_deps=deps)

    def emit(self, final_wait_ops=()):
        nc = self.nc
        ops = self.ops
        n = len(ops)

        def skip(od, o):
            return (od["eng"] == "pe" and o["eng"] == "pe" and od["dma_key"] is None
                    and o["dma_key"] is None and o["fn"] is not None)

        needed = [False] * n
        for i, o in enumerate(ops):
            for d in o["deps"]:
                if skip(ops[d], o):
                    continue
                needed[d] = True
        for d in final_wait_ops:
            needed[d] = True
        chan_count = {}
        ev = [None] * n
        for i, o in enumerate(ops):
            if o["fn"] is None:
                continue
            if o["dma_key"] is not None:
                ch = ("dma", o["dma_key"])
                chan_count[ch] = chan_count.get(ch, 0) + 16
                ev[i] = (ch, chan_count[ch])
            elif needed[i]:
                ch = ("eng", o["eng"])
                chan_count[ch] = chan_count.get(ch, 0) + 1
                ev[i] = (ch, chan_count[ch])
        chans = sorted(chan_count.keys(), key=str)
        self.n_sems = len(chans)
        sems = {}
        stack = contextlib.ExitStack()
        for ci, ch in enumerate(chans):
            sems[ch] = stack.enter_context(nc.semaphore(f"s{ci}"))
        known = {e: {} for e in self.ENGS}
        clock = [None] * n
        streams = {e: [] for e in self.ENGS}
        for i, o in enumerate(ops):
            e = o["eng"]
            kn = known[e]
            wd = {}
            for d in sorted(o["deps"]):
                od = ops[d]
                if skip(od, o):
                    continue
                ch, v = ev[d]
                if kn.get(ch, 0) >= v:
                    continue
                for c2, v2 in clock[d].items():
                    if kn.get(c2, 0) < v2:
                        kn[c2] = v2
                wd[ch] = max(wd.get(ch, 0), v)
            ck = dict(kn)
            if ev[i] is not None:
                ch, v = ev[i]
                ck[ch] = v
            clock[i] = ck
            streams[e].append((list(wd.items()), o["fn"], ev[i]))
        final = [ev[d] for d in final_wait_ops]
        for ch, tot in chan_count.items():
            if ch[0] == "dma":
                final.append((ch, tot))
        self.sems, self.streams, self.final, self._stack = sems, streams, final, stack

    def run_block(self):
        nc = self.nc
        sems, streams, final = self.sems, self.streams, self.final
        with nc.Block() as block:
            def mk(ename):
                def body(eng):
                    for waits, fn, e in streams[ename]:
                        for ch, v in waits:
                            eng.wait_ge(sems[ch], v)
                        if fn is None:
                            continue
                        ins = getattr(eng, fn[0])(*fn[1], **fn[2])
                        if e is not None:
                            ins.then_inc(sems[e[0]], 16 if e[0][0] == "dma" else 1)
                    if ename == "sp":
                        for ch, v in final:
                            eng.wait_ge(sems[ch], v)
                return body
            block.tensor(mk("pe"))
            block.scalar(mk("act"))
            block.vector(mk("dve"))
            block.gpsimd(mk("pool"))
            block.sync(mk("sp"))
        self._stack.close()


def _t5_bucket(rel):
    n = np.maximum(rel, 0)
    max_exact = 16
    large = max_exact + (np.log(np.maximum(n, 1).astype(np.float32) / max_exact)
                         / math.log(128 / max_exact) * (32 - max_exact)).astype(np.int32)
    large = np.minimum(large, 31)
    return np.where(n < max_exact, n, large)


def host_consts():
    s = np.arange(128)[:, None]
    t = np.arange(128)[None, :]
    c = {}
    c["ident"] = np.eye(128, dtype=np.float32)
    c["ones"] = np.ones((128, 128), np.float32)
    c["triC"] = ((s <= t).astype(np.float32) - (s <= 63).astype(np.float32))
    c["triU"] = (s > t).astype(np.float32)
    c["triI"] = (s <= t).astype(np.float32)
    sel = np.zeros((128, 2), np.float32)
    sel[:64, 0] = 1.0
    sel[:, 1] = 1.0
    c["sel"] = sel
    c["mg16"] = np.eye(16, dtype=np.float32)
    mq = np.zeros((128, 2), np.float32)
    mq[:64, 0] = 1.0
    mq[64:, 1] = 1.0
    c["maskq"] = mq
    return c


def host_layout(inp):
    f = lambda a: np.ascontiguousarray(np.asarray(a, dtype=np.float32))
    m = dict(host_consts())
    m["final_norm_w"] = f(inp["final_norm_w"])
    m["norm_w_cols"] = f(np.asarray(inp["norm_w"]).reshape(4, 8, 128).transpose(0, 2, 1))
    owin = np.asarray(inp["odd_w_in"])
    t = owin.reshape(2, 8, 128, 4, 16, 128).transpose(0, 4, 2, 1, 3, 5)
    m["odd_w_in_t"] = f(t).reshape(2, 16, 128, 8 * 512)
    m["odd_w_out"] = f(inp["odd_w_out"])
    m["hgrn_lower_bounds"] = f(inp["hgrn_lower_bounds"])
    m["hgrn_norm_w"] = f(inp["hgrn_norm_w"])
    ew = np.asarray(inp["even_w_in"]).reshape(2, 8, 128, 7184)
    z = ew[..., 0:1024]; xs = ew[..., 1024:2048]; Bm = ew[..., 2048:2560]; Cm = ew[..., 2560:3072]
    dt = ew[..., 3072:3088]
    q = ew[..., 3088:4112]; kk = ew[..., 4112:5136]; v = ew[..., 5136:6160]; gg = ew[..., 6160:7184]
    ssd = np.concatenate([z.reshape(2, 8, 128, 4, 256), xs.reshape(2, 8, 128, 4, 256),
                          Bm.reshape(2, 8, 128, 4, 128), Cm.reshape(2, 8, 128, 4, 128)], axis=-1)
    m["ev_w_ssd"] = f(ssd.transpose(0, 3, 2, 1, 4)).reshape(2, 4, 128, 8 * 768)
    m["ev_w_dt"] = f(dt.transpose(0, 2, 1, 3)).reshape(2, 128, 8 * 16)
    att = np.concatenate([q.reshape(2, 8, 128, 8, 128), kk.reshape(2, 8, 128, 8, 128),
                          v.reshape(2, 8, 128, 8, 128), gg.reshape(2, 8, 128, 8, 128)], axis=-1)
    m["ev_w_att"] = f(att.transpose(0, 3, 2, 1, 4)).reshape(2, 8, 128, 8 * 512)
    wo = np.asarray(inp["even_w_out"]).reshape(2, 16, 128, 8, 128)
    m["ev_w_out_t"] = f(wo.transpose(0, 3, 2, 1, 4)).reshape(2, 8, 128, 16 * 128)
    m["conv_w_cols"] = f(np.asarray(inp["conv_w"]).reshape(2, 4, 16, 128).transpose(0, 3, 2, 1)).reshape(2, 128, 64)
    m["conv_b_cols"] = f(np.asarray(inp["conv_b"]).reshape(2, 16, 128).transpose(0, 2, 1))
    for nm in ("dt_bias", "A_log", "D_skip", "lambda_q1", "lambda_k1", "lambda_q2", "lambda_k2", "subln_w"):
        m[nm] = f(inp[nm])
    m["ssd_norm_w_cols"] = f(np.asarray(inp["ssd_norm_w"]).reshape(2, 8, 128).transpose(0, 2, 1))
    rb = np.asarray(inp["rel_bias"], dtype=np.float32)
    kpos = np.arange(128)[:, None]
    qpos = np.arange(128)[None, :]
    bd = np.empty((128, 8, 2, 128), np.float32)
    for Dd in range(2):
        rel = qpos - kpos + 128 * Dd
        bidx = _t5_bucket(rel)
        g_ = rb[bidx]
        g_ = np.where((rel >= 0)[:, :, None], g_, np.float32(-30000.0))
        bd[:, :, Dd, :] = g_.transpose(0, 2, 1)
    m["rel_biasD"] = f(bd).reshape(128, 8 * 2 * 128)
    m["rel_b31"] = f(rb[31])
    return m


class NS:
    pass


class Carver:
    def __init__(self, g):
        self.g = g
        self.fo = 0
        self.bo = 0

    def f(self, n):
        ap = self.g.arf[:, self.fo:self.fo + n]
        self.fo += (n + 7) // 8 * 8
        assert self.fo <= ARF_N, ("ARF overflow", self.fo)
        return ap

    def b(self, n):
        ap = self.g.arb[:, self.bo:self.bo + n]
        self.bo += (n + 15) // 16 * 16
        assert self.bo <= ARB_N, ("ARB overflow", self.bo)
        return ap


def bank(g, i):
    return g.ps[:, i, :]


def build(L=2048, NSEQ=2, layers=(0, 1, 2, 3)):
    nc = bass.Bass("TRN2", target_bir_lowering=False)
    NT, NB = L // 512, L // 128
    g = NS()
    g.nc, g.L, g.NT, g.NB = nc, L, NT, NB
    dr = lambda name, shape, kind="ExternalInput": nc.dram_tensor(name, list(shape), F32, kind=kind).ap()
    g.x_d = dr("x", [NSEQ, L, D])
    g.out_d = dr("out", [NSEQ, L, D], "ExternalOutput")
    g.d = {}
    shapes = {
        "final_norm_w": [D], "norm_w_cols": [DEPTH, 128, KC],
        "ident": [128, 128], "ones": [128, 128], "triC": [128, 128], "triU": [128, 128], "triI": [128, 128],
        "sel": [128, 2], "mg16": [16, 16], "maskq": [128, 2],
        "odd_w_in_t": [2, 16, 128, KC * 512], "odd_w_out": [2, HG_W, D], "hgrn_lower_bounds": [DEPTH, HG_W],
        "hgrn_norm_w": [2, 128],
        "ev_w_ssd": [2, 4, 128, 8 * 768], "ev_w_dt": [2, 128, 8 * 16], "ev_w_att": [2, 8, 128, 8 * 512],
        "ev_w_out_t": [2, 8, 128, 16 * 128], "conv_w_cols": [2, 128, 64], "conv_b_cols": [2, 128, 16],
        "dt_bias": [2, 16], "A_log": [2, 16], "D_skip": [2, 16], "lambda_q1": [2, 64], "lambda_k1": [2, 64],
        "lambda_q2": [2, 64], "lambda_k2": [2, 64], "subln_w": [2, 128], "ssd_norm_w_cols": [2, 128, 8],
        "rel_biasD": [128, 8 * 2 * 128], "rel_b31": [8],
    }
    for nm, shp in shapes.items():
        g.d[nm] = dr(nm, shp)
    g.in_names = ["x"] + list(shapes.keys())

    es = contextlib.ExitStack()
    sb = lambda name, shape, dt=F32: es.enter_context(nc.sbuf_tensor(name, list(shape), dt))
    P = Prog(nc)
    g.P = P
    g.hT = sb("hT", [128, KC, L]); g.hU = [[P.unit(f"h{k}_{n}") for n in range(NT)] for k in range(KC)]
    g.uT = sb("uT", [128, KC, L], BF16); g.uU = [P.unit(f"u{n}") for n in range(NT)]
    g.cst = {}
    g.cU = P.unit("consts")
    for nm in ("ident", "ones", "triI"):
        g.cst[nm] = sb("c_" + nm, [128, 128])
    g.cst["sel"] = sb("c_sel", [128, 2])
    g.cst["maskq"] = sb("c_maskq", [128, 2])
    g.cst["mg16"] = sb("c_mg16", [16, 16])
    g.nwc = sb("nwc", [128, DEPTH, KC])
    g.stat = sb("stat", [128, 16]); g.statU = P.unit("stat")
    g.sq = [sb(f"sq{i}", [128, 512]) for i in range(2)]; g.sqU = P.units(2, "sq")
    g.rstd_t = sb("rstd_t", [128, 512]); g.rstdU = P.unit("rstd_t")
    g.arf = sb("arf", [128, ARF_N])
    g.arb = sb("arb", [128, ARB_N], BF16)
    g.ps = es.enter_context(nc.psum_tensor("ps", [128, 8, 512], F32))
    g.bU = P.units(8, "bank")

    for nm in ("ident", "ones", "triI", "sel", "maskq", "mg16"):
        P.dma("sp", lambda e, nm=nm: e.dma_start(out=g.cst[nm][:], in_=g.d[nm]), "c_" + nm, writes=[g.cU])
    P.dma("sp", lambda e: e.dma_start(out=g.nwc[:], in_=g.d["norm_w_cols"].rearrange("l p k -> p l k")), "c_nwc", writes=[g.cU])

    out_ops = []
    for s in range(NSEQ):
        P.barrier()
        load_x(g, s)
        for li in layers:
            rms_to_uT(g, li)
            P.barrier()
            if li % 2 == 1:
                odd_layer(g, li)
            else:
                even_ssd(g, li)
                P.barrier()
                even_attn(g, li)
            P.barrier()
        out_ops += final_norm_store(g, s)
    P.emit(final_wait_ops=out_ops[-4:])
    P.run_block()
    es.close()
    return nc, P


def load_x(g, s):
    P, NT = g.P, g.NT
    A = Carver(g)
    xst = A.f(4096).rearrange("p (b d) -> p b d", b=4)
    xU = P.unit("xst")
    ident = g.cst["ident"]
    for n in range(NT):
        src = g.x_d[s, n * 512:(n + 1) * 512, :].rearrange("(b p) d -> p b d", p=128)
        P.dma("sp", lambda e, src=src: e.dma_start(out=xst, in_=src), "xst", writes=[xU])
        for k in range(KC):
            bk = k % 8
            for b in range(4):
                P.pe(lambda e, bk=bk, b=b, k=k: e.transpose(
                    bank(g, bk)[:, b * 128:(b + 1) * 128], xst[:, b, k * 128:(k + 1) * 128], ident[:]),
                    reads=[xU, g.cU], writes=[g.bU[bk]])
            if k % 2 == 0:
                P.dve(lambda e, bk=bk, k=k, n=n: e.tensor_copy(g.hT[:, k, n * 512:(n + 1) * 512], bank(g, bk)),
                      reads=[g.bU[bk]], writes=[g.hU[k][n]])
            else:
                P.act(lambda e, bk=bk, k=k, n=n: e.copy(g.hT[:, k, n * 512:(n + 1) * 512], bank(g, bk)),
                      reads=[g.bU[bk]], writes=[g.hU[k][n]])


def final_norm_store(g, s):
    P, NB = g.P, g.NB
    A = Carver(g)
    fnw = A.f(D); fnwU = P.unit("fnw")
    ost = [A.f(D) for _ in range(2)]; ostU = P.units(2, "ost")
    junk = A.f(1024); junkU = P.unit("junk")
    ident = g.cst["ident"]
    P.dma("sp", lambda e: e.dma_start(out=fnw, in_=g.d["final_norm_w"].partition_broadcast(128)), "fnw", writes=[fnwU])
    outs = []
    for b in range(NB):
        n = b // 4
        oi = b % 2
        for k in range(KC):
            bk = k // 4
            P.pe(lambda e, bk=bk, k=k, b=b: e.transpose(
                bank(g, bk)[:, (k % 4) * 128:(k % 4 + 1) * 128], g.hT[:, k, b * 128:(b + 1) * 128], ident[:]),
                reads=[g.hU[k][n], g.cU], writes=[g.bU[bk]])
        for half in range(2):
            P.act(lambda e, half=half: e.activation(
                out=junk[:, half * 512:(half + 1) * 512], in_=bank(g, half), func=AF.Square,
                accum_out=g.stat[:, half:half + 1]),
                reads=[g.bU[half]], writes=[junkU, g.statU])
        P.dve(lambda e: e.tensor_tensor(out=g.stat[:, 2:3], in0=g.stat[:, 0:1], in1=g.stat[:, 1:2], op=ALU.add),
              reads=[g.statU], writes=[g.statU])
        P.act(lambda e: e.activation(out=g.stat[:, 3:4], in_=g.stat[:, 2:3], func=AF.Ln, scale=1.0 / D, bias=EPS),
              reads=[g.statU], writes=[g.statU])
        P.act(lambda e: e.activation(out=g.stat[:, 4:5], in_=g.stat[:, 3:4], func=AF.Exp, scale=-0.5),
              reads=[g.statU], writes=[g.statU])
        for half in range(2):
            P.dve(lambda e, half=half, oi=oi: e.scalar_tensor_tensor(
                out=ost[oi][:, half * 512:(half + 1) * 512], in0=bank(g, half), scalar=g.stat[:, 4:5],
                in1=fnw[:, half * 512:(half + 1) * 512], op0=ALU.mult, op1=ALU.mult),
                reads=[g.bU[half], g.statU, fnwU], writes=[ostU[oi]])
        o = P.dma("sp", lambda e, oi=oi, s=s, b=b: e.dma_start(out=g.out_d[s, b * 128:(b + 1) * 128, :], in_=ost[oi]),
                  f"ost{oi}", reads=[ostU[oi]])
        outs.append(o)
    return outs


def rms_rstd_tile(g, src_fn, reads_fn, nchunks, dim):
    P = g.P
    ones = g.cst["ones"]
    for k in range(nchunks):
        i = k % 2
        P.act(lambda e, i=i, k=k: e.activation(out=g.sq[i][:], in_=src_fn(k), func=AF.Square),
              reads=reads_fn(k), writes=[g.sqU[i]])
        P.pe(lambda e, i=i, k=k: e.matmul(bank(g, 7), lhsT=ones[:], rhs=g.sq[i][:], start=(k == 0), stop=(k == nchunks - 1)),
             reads=[g.sqU[i], g.cU], writes=[g.bU[7]])
    P.act(lambda e: e.activation(out=g.rstd_t[:], in_=bank(g, 7), func=AF.Ln, scale=1.0 / dim, bias=EPS),
          reads=[g.bU[7]], writes=[g.rstdU])
    P.act(lambda e: e.activation(out=g.rstd_t[:], in_=g.rstd_t[:], func=AF.Exp, scale=-0.5),
          reads=[g.rstdU], writes=[g.rstdU])


def rms_to_uT(g, li):
    P, NT = g.P, g.NT
    for n in range(NT):
        sl = slice(n * 512, (n + 1) * 512)
        rms_rstd_tile(g, lambda k, sl=sl: g.hT[:, k, sl], lambda k, n=n: [g.hU[k][n]], KC, D)
        for k in range(KC):
            P.dve(lambda e, k=k, sl=sl: e.scalar_tensor_tensor(
                out=g.uT[:, k, sl], in0=g.hT[:, k, sl], scalar=g.nwc[:, li, k:k + 1], in1=g.rstd_t[:],
                op0=ALU.mult, op1=ALU.mult),
                reads=[g.hU[k][n], g.rstdU, g.cU], writes=[g.uU[n]])


def odd_layer(g, li):
    P, L, NB, NT = g.P, g.L, g.NB, g.NT
    oi = li // 2
    A = Carver(g)
    o = NS()
    c = g.cst
    f3 = lambda: A.f(512).rearrange("p (b d) -> p b d", b=4)
    o.logf = f3(); o.logfU = P.unit()
    o.tA = f3(); o.tAU = P.unit()
    o.kk = f3(); o.kkU = P.unit()
    o.qs = f3(); o.qsU = P.unit()
    o.e13 = A.f(1024).rearrange("p (t b d) -> p t b d", t=2, b=4); o.e13U = P.unit()
    o.e2 = f3(); o.e2U = P.unit()
    o.qt = o.qs; o.qtU = o.qsU
    o.kt = o.e2; o.ktU = o.e2U
    o.lbr = f3(); o.lbrU = P.unit()
    o.lbh = A.f(128); o.omlh = A.f(128); o.den = A.f(128); o.lbU = P.unit()
    o.eb = [A.f(NB * 2).rearrange("p (b t) -> p b t", t=2) for _ in range(2)]
    o.S = A.f(128); o.SU = P.unit()
    o.junk2 = A.f(128); o.junk2U = P.unit()
    o.oall = A.f(NB * 128).rearrange("p (b d) -> p b d", d=128); o.oallU = P.unit()
    o.ssall = A.f(NB); o.rsall = A.f(NB); o.ssU = P.unit()
    o.hnw = A.f(128); o.hnwU = P.unit()
    o.triC = A.f(128); o.triU = A.f(128); o.triUU = P.unit()
    o.bst = A.f(8); o.bstU = P.unit()
    o.w = [A.b(KC * 512).rearrange("p (k c) -> p k c", k=KC) for _ in range(2)]; o.wU = P.units(2)
    o.wout = A.b(2 * D).rearrange("p (j m) -> p j m", j=2); o.woutU = P.units(2)
    o.qT = [A.b(L) for _ in range(2)]
    o.kT = [A.b(L) for _ in range(2)]
    hb3 = lambda: A.b(NB * 128).rearrange("p (b d) -> p b d", d=128)
    o.kh = [hb3() for _ in range(2)]
    o.v = [hb3() for _ in range(2)]
    o.gs = [hb3() for _ in range(2)]
    o.hbU = [[[P.unit() for _ in range(NT)] for _ in range(6)] for _ in range(2)]
    o.attm = [A.b(128) for _ in range(2)]; o.attmU = P.units(2)
    o.Sb2 = [A.b(128) for _ in range(2)]; o.SbU2 = P.units(2)
    o.yT = A.b(2 * L).rearrange("p (j t) -> p j t", j=2); o.yTU = P.units(2)
    QT, KT, KH, VV, GS, EB = range(6)

    P.dma("sp", lambda e: e.dma_start(out=o.hnw, in_=g.d["hgrn_norm_w"][oi].partition_broadcast(128)), "o_hnw", writes=[o.hnwU])
    P.dma("sp", lambda e: e.dma_start(out=o.triC, in_=g.d["triC"]), "o_tri", writes=[o.triUU])
    P.dma("sp", lambda e: e.dma_start(out=o.triU, in_=g.d["triU"]), "o_tri", writes=[o.triUU])
    for i in range(2):
        P.dve(lambda e, i=i: e.memset(o.attm[i], 0.0), writes=[o.attmU[i]])

    def load_w(h):
        i = h % 2
        P.dma("pool", lambda e: e.dma_start(out=o.w[i].rearrange("p k c -> p (k c)"), in_=g.d["odd_w_in_t"][oi, h]),
              f"o_w{i}", writes=[o.wU[i]])

    def head_lb(h):
        hs = slice(h * 128, (h + 1) * 128)
        P.dma("sp", lambda e: e.dma_start(out=o.lbr, in_=g.d["hgrn_lower_bounds"][:, hs].partition_broadcast(128)),
              "o_lbr", writes=[o.lbrU])
        P.act(lambda e: e.activation(out=o.lbr, in_=o.lbr, func=AF.Exp), reads=[o.lbrU], writes=[o.lbrU])
        P.dve(lambda e: e.tensor_tensor(out=o.den, in0=o.lbr[:, 0, :], in1=o.lbr[:, 1, :], op=ALU.add), reads=[o.lbrU], writes=[o.lbU])
        P.dve(lambda e: e.tensor_tensor(out=o.den, in0=o.den, in1=o.lbr[:, 2, :], op=ALU.add), reads=[o.lbrU, o.lbU], writes=[o.lbU])
        P.dve(lambda e: e.tensor_tensor(out=o.den, in0=o.den, in1=o.lbr[:, 3, :], op=ALU.add), reads=[o.lbrU, o.lbU], writes=[o.lbU])
        P.dve(lambda e: e.reciprocal(o.den, o.den), reads=[o.lbU], writes=[o.lbU])
        if li == 1:
            P.dve(lambda e: e.tensor_tensor(out=o.lbh, in0=o.lbr[:, 1, :], in1=o.den, op=ALU.mult), reads=[o.lbrU, o.lbU], writes=[o.lbU])
        else:
            P.dve(lambda e: e.tensor_tensor(out=o.lbh, in0=o.lbr[:, 1, :], in1=o.lbr[:, 2, :], op=ALU.add), reads=[o.lbrU, o.lbU], writes=[o.lbU])
            for j in range(3, li + 1):
                P.dve(lambda e, j=j: e.tensor_tensor(out=o.lbh, in0=o.lbh, in1=o.lbr[:, j, :], op=ALU.add), reads=[o.lbrU, o.lbU], writes=[o.lbU])
            P.dve(lambda e: e.tensor_tensor(out=o.lbh, in0=o.lbh, in1=o.den, op=ALU.mult), reads=[o.lbU], writes=[o.lbU])
        P.dve(lambda e: e.tensor_scalar(out=o.omlh, in0=o.lbh, scalar1=-1.0, scalar2=1.0, op0=ALU.mult, op1=ALU.add),
              reads=[o.lbU], writes=[o.lbU])

    def stageA(h, n):
        hb = h % 2
        wi = h % 2
        U = o.hbU[hb]
        for b in range(4):
            tb = n * 4 + b
            for k in range(KC):
                P.pe(lambda e, b=b, tb=tb, k=k: e.matmul(bank(g, b), lhsT=g.uT[:, k, tb * 128:(tb + 1) * 128], rhs=o.w[wi][:, k, :],
                                                         start=(k == 0), stop=(k == KC - 1)),
                     reads=[g.uU[n], o.wU[wi]], writes=[g.bU[b]])
        pj = g.ps[:, 0:4, :]
        pb = [g.bU[0], g.bU[1], g.bU[2], g.bU[3]]
        bc4 = lambda t: t.unsqueeze(1).to_broadcast([128, 4, 128])
        P.act(lambda e: e.activation(out=o.tA, in_=pj[:, :, 128:256], func=AF.Sigmoid), reads=pb, writes=[o.tAU])
        P.act(lambda e: e.activation(out=o.qs, in_=pj[:, :, 0:128], func=AF.Silu), reads=pb, writes=[o.qsU])
        P.act(lambda e: e.activation(out=o.gs[hb][:, n * 4:(n + 1) * 4, :], in_=pj[:, :, 384:512], func=AF.Silu),
              reads=pb, writes=[U[GS][n]])
        P.act(lambda e: e.copy(o.v[hb][:, n * 4:(n + 1) * 4, :], pj[:, :, 256:384]), reads=pb, writes=[U[VV][n]])
        P.dve(lambda e: e.tensor_tensor(out=o.tA, in0=o.tA, in1=bc4(o.omlh), op=ALU.mult), reads=[o.tAU, o.lbU], writes=[o.tAU])
        P.dve(lambda e: e.tensor_tensor(out=o.tA, in0=o.tA, in1=bc4(o.lbh), op=ALU.add), reads=[o.tAU, o.lbU], writes=[o.tAU])
        P.act(lambda e: e.activation(out=o.logf, in_=o.tA, func=AF.Ln), reads=[o.tAU], writes=[o.logfU])
        P.pool(lambda e: e.tensor_scalar(out=o.kk, in0=o.tA, scalar1=-1.0, scalar2=1.0, op0=ALU.mult, op1=ALU.add),
               reads=[o.tAU], writes=[o.kkU])
        for b in range(4):
            P.pe(lambda e, b=b: e.matmul(bank(g, 0)[:, b * 128:(b + 1) * 128], lhsT=o.triC, rhs=o.logf[:, b, :], start=True, stop=True),
                 reads=[o.logfU, o.triUU], writes=[g.bU[0]])
        for b in range(4):
            P.pe(lambda e, b=b: e.matmul(bank(g, 1)[:, b * 128:(b + 1) * 128], lhsT=o.triU, rhs=o.logf[:, b, :], start=True, stop=True),
                 reads=[o.logfU, o.triUU], writes=[g.bU[1]])
        for b in range(4):
            P.pe(lambda e, b=b: e.matmul(bank(g, 2)[:, b * 2:b * 2 + 2], lhsT=o.logf[:, b, :], rhs=c["sel"][:], start=True, stop=True),
                 reads=[o.logfU, g.cU], writes=[g.bU[2]])
        P.act(lambda e: e.activation(out=o.eb[hb][:, n * 4:(n + 1) * 4, :], in_=bank(g, 2)[:, 0:8].rearrange("p (b t) -> p b t", t=2), func=AF.Exp),
              reads=[g.bU[2]], writes=[U[EB][n]])
        P.act(lambda e: e.activation(out=o.e13, in_=g.ps[:, 0:2, :].rearrange("p t (b d) -> p t b d", b=4), func=AF.Exp),
              reads=[g.bU[0], g.bU[1]], writes=[o.e13U])
        P.act(lambda e: e.activation(out=o.e2, in_=bank(g, 0).rearrange("p (b d) -> p b d", b=4), func=AF.Exp, scale=-1.0),
              reads=[g.bU[0]], writes=[o.e2U])
        P.dve(lambda e: e.tensor_tensor(out=o.qs, in0=o.qs, in1=o.e13[:, 0], op=ALU.mult), reads=[o.qsU, o.e13U], writes=[o.qsU])
        P.pool(lambda e: e.tensor_tensor(out=o.e2, in0=o.kk, in1=o.e2, op=ALU.mult), reads=[o.kkU, o.e2U], writes=[o.e2U])
        P.dve(lambda e: e.tensor_tensor(out=o.kh[hb][:, n * 4:(n + 1) * 4, :], in0=o.kk, in1=o.e13[:, 1], op=ALU.mult),
              reads=[o.kkU, o.e13U], writes=[U[KH][n]])
        idf = c["ident"]
        for b in range(4):
            P.pe(lambda e, b=b: e.transpose(bank(g, 3)[:, b * 128:(b + 1) * 128], o.qt[:, b, :], idf[:]),
                 reads=[o.qtU, g.cU], writes=[g.bU[3]])
        for b in range(4):
            P.pe(lambda e, b=b: e.transpose(bank(g, 2)[:, b * 128:(b + 1) * 128], o.kt[:, b, :], idf[:]),
                 reads=[o.ktU, g.cU], writes=[g.bU[2]])
        P.act(lambda e: e.copy(o.qT[hb][:, n * 512:(n + 1) * 512], bank(g, 3)), reads=[g.bU[3]], writes=[U[QT][n]])
        P.act(lambda e: e.copy(o.kT[hb][:, n * 512:(n + 1) * 512], bank(g, 2)), reads=[g.bU[2]], writes=[U[KT][n]])

    def stageB(h):
        hb = h % 2
        U = o.hbU[hb]

        def att(tb):
            n = tb // 4
            ts = slice(tb * 128, (tb + 1) * 128)
            bk = 5 + 2 * (tb % 2)
            P.pe(lambda e: e.matmul(bank(g, bk)[:, 64:128], lhsT=o.kT[hb][:, ts],
                                    rhs=o.qT[hb][:, tb * 128 + 64:(tb + 1) * 128], start=True, stop=True),
                 reads=[U[QT][n], U[KT][n]], writes=[g.bU[bk]])
            P.pe(lambda e: e.matmul(bank(g, bk)[0:64, 0:64], lhsT=o.kT[hb][:, tb * 128:tb * 128 + 64],
                                    rhs=o.qT[hb][:, tb * 128:tb * 128 + 64], start=True, stop=True),
                 reads=[U[QT][n], U[KT][n]], writes=[g.bU[bk]])

        def mask(tb):
            ai = tb % 2
            bk = 5 + 2 * (tb % 2)
            P.dve(lambda e: e.tensor_tensor(out=o.attm[ai][:, 64:128], in0=bank(g, bk)[:, 64:128], in1=c["triI"][:, 64:128], op=ALU.mult),
                  reads=[g.bU[bk], g.cU], writes=[o.attmU[ai]])
            P.dve(lambda e: e.tensor_tensor(out=o.attm[ai][0:64, 0:64], in0=bank(g, bk)[0:64, 0:64], in1=c["triI"][0:64, 0:64], op=ALU.mult),
                  reads=[g.bU[bk], g.cU], writes=[o.attmU[ai]])

        att(0)
        mask(0)
        for tb in range(NB):
            n = tb // 4
            ts = slice(tb * 128, (tb + 1) * 128)
            ai = tb % 2
            si = tb % 2
            P.pe(lambda e: e.matmul(bank(g, 6)[:, 128:256], lhsT=o.kh[hb][:, tb, :], rhs=o.v[hb][:, tb, :], start=True, stop=True),
                 reads=[U[KH][n], U[VV][n]], writes=[g.bU[6]])
            if tb + 1 < NB:
                att(tb + 1)
            P.pe(lambda e: e.matmul(bank(g, 4)[:, 0:128], lhsT=o.attm[ai], rhs=o.v[hb][:, tb, :], start=True, stop=(tb == 0)),
                 reads=[o.attmU[ai], U[VV][n]], writes=[g.bU[4]])
            if tb > 0:
                P.pe(lambda e: e.matmul(bank(g, 4)[:, 0:128], lhsT=o.qT[hb][:, ts], rhs=o.Sb2[si], start=False, stop=True),
                     reads=[o.SbU2[si], U[QT][n]], writes=[g.bU[4]])
            if tb == 0:
                P.dve(lambda e: e.tensor_copy(o.S, bank(g, 6)[:, 128:256]), reads=[g.bU[6]], writes=[o.SU])
            else:
                P.dve(lambda e: e.scalar_tensor_tensor(out=o.S, in0=o.S, scalar=o.eb[hb][:, tb, 1:2], in1=bank(g, 6)[:, 128:256],
                                                       op0=ALU.mult, op1=ALU.add),
                      reads=[o.SU, g.bU[6], U[EB][n]], writes=[o.SU])
            if tb + 1 < NB:
                nn = (tb + 1) // 4
                sn = (tb + 1) % 2
                P.dve(lambda e: e.tensor_scalar(out=o.Sb2[sn], in0=o.S, scalar1=o.eb[hb][:, tb + 1, 0:1], scalar2=None, op0=ALU.mult),
                      reads=[o.SU, U[EB][nn]], writes=[o.SbU2[sn]])
                mask(tb + 1)
            P.act(lambda e: e.copy(o.oall[:, tb, :], bank(g, 4)[:, 0:128]), reads=[g.bU[4]], writes=[o.oallU])
            P.act(lambda e: e.activation(out=o.junk2, in_=bank(g, 4)[:, 0:128], func=AF.Square, accum_out=o.ssall[:, tb:tb + 1]),
                  reads=[g.bU[4]], writes=[o.junk2U, o.ssU])

    def stageC(h):
        hb = h % 2
        U = o.hbU[hb]
        hj = h % 2
        P.act(lambda e: e.activation(out=o.rsall, in_=o.ssall, func=AF.Ln, scale=1.0 / 128, bias=EPS), reads=[o.ssU], writes=[o.ssU])
        P.act(lambda e: e.activation(out=o.rsall, in_=o.rsall, func=AF.Exp, scale=-0.5), reads=[o.ssU], writes=[o.ssU])
        P.dve(lambda e: e.tensor_tensor(out=o.oall, in0=o.oall, in1=o.rsall.unsqueeze(2).to_broadcast([128, NB, 128]), op=ALU.mult),
              reads=[o.oallU, o.ssU], writes=[o.oallU])
        P.dve(lambda e: e.tensor_tensor(out=o.oall, in0=o.oall, in1=o.hnw.unsqueeze(1).to_broadcast([128, NB, 128]), op=ALU.mult),
              reads=[o.oallU, o.hnwU], writes=[o.oallU])
        P.dve(lambda e: e.tensor_tensor(out=o.oall, in0=o.oall, in1=o.gs[hb], op=ALU.mult),
              reads=[o.oallU] + [U[GS][n] for n in range(NT)], writes=[o.oallU])
        for n in range(NT):
            bk = 4 + (n % 4)
            for b in range(4):
                P.pe(lambda e, bk=bk, b=b, n=n: e.transpose(bank(g, bk)[:, b * 128:(b + 1) * 128], o.oall[:, n * 4 + b, :], c["ident"][:]),
                     reads=[o.oallU, g.cU], writes=[g.bU[bk]])
            P.act(lambda e, bk=bk, n=n: e.copy(o.yT[:, hj, n * 512:(n + 1) * 512], bank(g, bk)), reads=[g.bU[bk]], writes=[o.yTU[hj]])

    def outproj(hp):
        for j in range(2):
            src = g.d["odd_w_out"][oi, (hp * 2 + j) * 128:(hp * 2 + j + 1) * 128, :]
            P.dma("pool", lambda e, j=j, src=src: e.dma_start(out=o.wout[:, j, :], in_=src), f"o_wout{j}", writes=[o.woutU[j]])
        cnt = 0
        for m in range(KC):
            for n in range(NT):
                bk = 4 + cnt % 4
                cnt += 1
                for j in range(2):
                    P.pe(lambda e, bk=bk, m=m, n=n, j=j: e.matmul(bank(g, bk), lhsT=o.wout[:, j, m * 128:(m + 1) * 128],
                                                                     rhs=o.yT[:, j, n * 512:(n + 1) * 512], start=(j == 0), stop=(j == 1)),
                         reads=[o.woutU[j], o.yTU[j]], writes=[g.bU[bk]])
                P.dve(lambda e, bk=bk, m=m, n=n: e.tensor_tensor(out=g.hT[:, m, n * 512:(n + 1) * 512], in0=g.hT[:, m, n * 512:(n + 1) * 512],
                                                                   in1=bank(g, bk), op=ALU.add),
                      reads=[g.bU[bk], g.hU[m][n]], writes=[g.hU[m][n]])

    load_w(0)
    load_w(1)
    head_lb(0)
    for n in range(NT):
        stageA(0, n)
    for h in range(16):
        P.capture()
        stageB(h)
        stageC(h)
        if h % 2 == 1:
            outproj(h // 2)
        LB = P.end_capture()
        P.capture()
        if h + 1 < 16:
            if h + 2 < 16:
                load_w(h + 2)
            head_lb(h + 1)
            for n in range(NT):
                stageA(h + 1, n)
        LA = P.end_capture()
        P.replay_merged(LA, LB)


def even_ssd(g, li):
    P, L, NB, NT = g.P, g.L, g.NB, g.NT
    ei = li // 2
    c = g.cst
    A = Carver(g)
    s = NS()
    HB = NB * 16
    s.xpre = A.f(515); s.xpreU = P.unit()
    s.cacc = A.f(512); s.caccU = P.unit()
    v3 = lambda ap: ap.rearrange("p (b h) -> p b h", h=16)
    s.dt = A.f(HB); s.atok = A.f(HB); s.acs = A.f(HB); s.eacs = A.f(HB); s.dtd = A.f(HB); s.edl = A.f(HB)
    s.dtU = P.unit()
    s.cw = A.f(64); s.cb = A.f(16); s.dtb = A.f(16); s.Abc = A.f(16); s.Dsk = A.f(16); s.snw = A.f(8)
    s.smallU = P.unit()
    s.S = A.f(256); s.SU = P.unit()
    scan_off = A.fo
    s.Rbd = A.f(512); s.RbdU = P.unit()
    D2 = lambda n: ([A.f(n) for _ in range(2)], P.units(2))
    s.acsTb2, s.acsTbU2 = D2(128)
    s.CBm2, s.CBmU2 = D2(128)
    s.Dm2, s.DmU2 = D2(512)
    s.E2, s.EU2 = D2(512)
    s.t12, s.t1U2 = D2(256)
    s.t22, s.t2U2 = D2(256)
    s.ytmp2, s.ytmpU2 = D2(256)
    s.rstd_all = g.arf[:, scan_off:scan_off + L]; s.rstdallU = P.unit()
    assert scan_off + L <= ARF_N
    s.w = A.b(KC * 768).rearrange("p (k c) -> p k c", k=KC); s.wU = P.unit()
    s.wdt = A.b(KC * 16).rearrange("p (k c) -> p k c", k=KC); s.wdtU = P.unit()
    s.BT = A.b(L); s.CT = A.b(L); s.BCU = [P.unit() for _ in range(NT)]
    s.xtok = A.b(NB * 256).rearrange("p (b c) -> p b c", c=256); s.xtokU = [P.unit() for _ in range(NT)]
    s.Btok = A.b(NB * 128).rearrange("p (b c) -> p b c", c=128); s.BtokU = [P.unit() for _ in range(NT)]
    scan_bo = A.bo
    s.zs2 = [A.b(256) for _ in range(2)]; s.zsU2 = P.units(2)
    s.sc2 = [A.b(512).rearrange("p (h l) -> p h l", h=4) for _ in range(2)]; s.scU2 = P.units(2)
    s.Xdt2 = [A.b(256) for _ in range(2)]; s.XdtU2 = P.units(2)
    s.XB2 = [A.b(256) for _ in range(2)]; s.XBU2 = P.units(2)
    s.Sbf = A.b(256); s.SbfU = P.unit()
    s.yTa = A.b(8 * L).rearrange("p (c t) -> p c t", c=8); s.yTaU = [[P.unit() for _ in range(NT)] for _ in range(8)]
    s.wo = g.arb[:, scan_bo:scan_bo + 1024].rearrange("p (c j) -> p c j", c=8); s.woU = P.unit()
    s.wos = s.wo; s.wosU = s.woU
    ident = c["ident"]

    sm = [s.smallU]
    P.dma("sp", lambda e: e.dma_start(out=s.cw, in_=g.d["conv_w_cols"][ei]), "s_small", writes=sm)
    P.dma("sp", lambda e: e.dma_start(out=s.cb, in_=g.d["conv_b_cols"][ei]), "s_small", writes=sm)
    P.dma("sp", lambda e: e.dma_start(out=s.dtb, in_=g.d["dt_bias"][ei].partition_broadcast(128)), "s_small", writes=sm)
    P.dma("sp", lambda e: e.dma_start(out=s.Abc, in_=g.d["A_log"][ei].partition_broadcast(128)), "s_small", writes=sm)
    P.dma("sp", lambda e: e.dma_start(out=s.Dsk, in_=g.d["D_skip"][ei].partition_broadcast(128)), "s_small", writes=sm)
    P.dma("sp", lambda e: e.dma_start(out=s.snw, in_=g.d["ssd_norm_w_cols"][ei]), "s_small", writes=sm)
    P.act(lambda e: e.activation(out=s.Abc, in_=s.Abc, func=AF.Exp), reads=sm, writes=sm)
    P.dve(lambda e: e.tensor_scalar(out=s.Abc, in0=s.Abc, scalar1=-1.0, scalar2=None, op0=ALU.mult), reads=sm, writes=sm)
    P.dma("pool", lambda e: e.dma_start(out=s.wdt.rearrange("p k c -> p (k c)"), in_=g.d["ev_w_dt"][ei]), "s_wdt", writes=[s.wdtU])

    for b in range(NB):
        for k in range(KC):
            P.pe(lambda e, b=b, k=k: e.matmul(bank(g, 0)[:, b * 16:(b + 1) * 16], lhsT=g.uT[:, k, b * 128:(b + 1) * 128],
                                              rhs=s.wdt[:, k, :], start=(k == 0), stop=(k == KC - 1)),
                 reads=[g.uU[b // 4], s.wdtU], writes=[g.bU[0]])
    bc_h = lambda t: t.unsqueeze(1).to_broadcast([128, NB, 16])
    du = [s.dtU]
    P.dve(lambda e: e.tensor_tensor(out=v3(s.dt), in0=v3(bank(g, 0)[:, 0:HB]), in1=bc_h(s.dtb), op=ALU.add),
          reads=[g.bU[0]] + sm, writes=du)
    P.act(lambda e: e.activation(out=s.dt, in_=s.dt, func=AF.Exp), reads=du, writes=du)
    P.act(lambda e: e.activation(out=s.dt, in_=s.dt, func=AF.Ln, bias=1.0), reads=du, writes=du)
    P.dve(lambda e: e.tensor_tensor(out=v3(s.atok), in0=v3(s.dt), in1=bc_h(s.Abc), op=ALU.mult), reads=du + sm, writes=du)
    for b in range(NB):
        P.pe(lambda e, b=b: e.matmul(bank(g, 1)[:, b * 16:(b + 1) * 16], lhsT=c["triI"][:], rhs=s.atok[:, b * 16:(b + 1) * 16],
                                     start=True, stop=True), reads=du + [g.cU], writes=[g.bU[1]])
    for b in range(NB):
        P.pe(lambda e, b=b: e.matmul(bank(g, 2)[:, b * 16:(b + 1) * 16], lhsT=c["ones"][:], rhs=s.atok[:, b * 16:(b + 1) * 16],
                                     start=True, stop=True), reads=du + [g.cU], writes=[g.bU[2]])
    P.dve(lambda e: e.tensor_copy(s.acs, bank(g, 1)[:, 0:HB]), reads=[g.bU[1]], writes=du)
    P.act(lambda e: e.activation(out=s.eacs, in_=s.acs, func=AF.Exp), reads=du, writes=du)
    P.dve(lambda e: e.tensor_copy(s.edl, bank(g, 2)[:, 0:HB]), reads=[g.bU[2]], writes=du)
    P.dve(lambda e: e.tensor_tensor(out=s.dtd, in0=s.edl, in1=s.acs, op=ALU.subtract), reads=du, writes=du)
    P.act(lambda e: e.activation(out=s.dtd, in_=s.dtd, func=AF.Exp), reads=du, writes=du)
    P.dve(lambda e: e.tensor_tensor(out=s.dtd, in0=s.dtd, in1=s.dt, op=ALU.mult), reads=du, writes=du)
    P.act(lambda e: e.activation(out=s.edl, in_=s.edl, func=AF.Exp), reads=du, writes=du)

    pcnt = [0]
    for grp in range(4):
        P.dma("pool", lambda e, grp=grp: e.dma_start(out=s.w.rearrange("p k c -> p (k c)"), in_=g.d["ev_w_ssd"][ei, grp]),
              "s_w", writes=[s.wU])
        chunks = [(256, 2 * grp, "x0"), (384, 2 * grp + 1, "x1"), (512, 8 + grp, "B"), (640, 12 + grp, "C")]
        for wc0, cch, kind in chunks:
            for n in range(NT):
                sl = slice(n * 512, (n + 1) * 512)
                bk = 3 + (pcnt[0] % 2)
                pcnt[0] += 1
                for k in range(KC):
                    P.pe(lambda e, bk=bk, k=k, wc0=wc0, sl=sl: e.matmul(bank(g, bk), lhsT=s.w[:, k, wc0:wc0 + 128], rhs=g.uT[:, k, sl],
                                                                        start=(k == 0), stop=(k == KC - 1)),
                         reads=[g.uU[n], s.wU], writes=[g.bU[bk]])
                if n == 0:
                    P.dve(lambda e: e.memset(s.xpre[:, 0:3], 0.0), writes=[s.xpreU])
                else:
                    P.dve(lambda e: e.tensor_copy(s.xpre[:, 0:3], s.xpre[:, 512:515]), reads=[s.xpreU], writes=[s.xpreU])
                P.act(lambda e, bk=bk: e.copy(s.xpre[:, 3:515], bank(g, bk)), reads=[g.bU[bk]], writes=[s.xpreU])
                P.dve(lambda e, cch=cch: e.tensor_scalar(out=s.cacc, in0=s.xpre[:, 3:515], scalar1=s.cw[:, cch * 4 + 3:cch * 4 + 4],
                                                          scalar2=s.cb[:, cch:cch + 1], op0=ALU.mult, op1=ALU.add),
                      reads=[s.xpreU] + sm, writes=[s.caccU])
                for tap in (2, 1, 0):
                    P.dve(lambda e, cch=cch, tap=tap: e.scalar_tensor_tensor(
                        out=s.cacc, in0=s.xpre[:, tap:tap + 512], scalar=s.cw[:, cch * 4 + tap:cch * 4 + tap + 1], in1=s.cacc,
                        op0=ALU.mult, op1=ALU.add), reads=[s.xpreU, s.caccU] + sm, writes=[s.caccU])
                P.act(lambda e: e.activation(out=s.cacc, in_=s.cacc, func=AF.Silu), reads=[s.caccU], writes=[s.caccU])
                if kind in ("B", "C"):
                    dst = s.BT if kind == "B" else s.CT
                    P.dve(lambda e, dst=dst, sl=sl: e.tensor_copy(dst[:, sl], s.cacc), reads=[s.caccU], writes=[s.BCU[n]])
                if kind != "C":
                    for j in range(4):
                        P.pe(lambda e, j=j: e.transpose(bank(g, 5)[:, j * 128:(j + 1) * 128], s.cacc[:, j * 128:(j + 1) * 128], ident[:]),
                             reads=[s.caccU, g.cU], writes=[g.bU[5]])
                    src = bank(g, 5).rearrange("p (b c) -> p b c", b=4)
                    if kind == "B":
                        P.act(lambda e, n=n, src=src: e.copy(s.Btok[:, n * 4:(n + 1) * 4, :], src), reads=[g.bU[5]], writes=[s.BtokU[n]])
                    else:
                        co = 0 if kind == "x0" else 128
                        P.act(lambda e, n=n, src=src, co=co: e.copy(s.xtok[:, n * 4:(n + 1) * 4, co:co + 128], src),
                              reads=[g.bU[5]], writes=[s.xtokU[n]])
        hs4 = slice(4 * grp, 4 * grp + 4)
        for b in range(NB):
            n = b // 4
            blk = slice(b * 128, (b + 1) * 128)
            hcol = lambda t, b=b: v3(t)[:, b, hs4]
            bch = lambda t, w, b=b: hcol(t, b).unsqueeze(2).to_broadcast([128, 4, w])
            x4 = s.xtok[:, b, :].rearrange("p (h q) -> p h q", h=4)
            pb_ = b % 2
            s.acsTb, s.acsTbU = s.acsTb2[pb_], s.acsTbU2[pb_]
            s.CBm, s.CBmU = s.CBm2[pb_], s.CBmU2[pb_]
            s.Dm, s.DmU = s.Dm2[pb_], s.DmU2[pb_]
            s.E, s.EU = s.E2[pb_], s.EU2[pb_]
            s.t1, s.t1U = s.t12[pb_], s.t1U2[pb_]
            s.t2, s.t2U = s.t22[pb_], s.t2U2[pb_]
            s.ytmp, s.ytmpU = s.ytmp2[pb_], s.ytmpU2[pb_]
            s.zs, s.zsU = s.zs2[pb_], s.zsU2[pb_]
            s.sc, s.scU = s.sc2[pb_], s.scU2[pb_]
            s.Xdt, s.XdtU = s.Xdt2[pb_], s.XdtU2[pb_]
            s.XB, s.XBU = s.XB2[pb_], s.XBU2[pb_]
            for k in range(KC):
                P.pe(lambda e, k=k, blk=blk: e.matmul(bank(g, 6)[:, 0:256], lhsT=g.uT[:, k, blk], rhs=s.w[:, k, 0:256],
                                                      start=(k == 0), stop=(k == KC - 1)),
                     reads=[g.uU[n], s.wU], writes=[g.bU[6]])
            P.act(lambda e: e.activation(out=s.zs, in_=bank(g, 6)[:, 0:256], func=AF.Silu), reads=[g.bU[6]], writes=[s.zsU])
            P.pe(lambda e, blk=blk: e.matmul(bank(g, 7)[:, 0:128], lhsT=s.BT[:, blk], rhs=s.CT[:, blk], start=True, stop=True),
                 reads=[s.BCU[n]], writes=[g.bU[7]])
            P.dve(lambda e: e.tensor_tensor(out=s.CBm, in0=bank(g, 7)[:, 0:128], in1=c["triI"][:], op=ALU.mult),
                  reads=[g.bU[7], g.cU], writes=[s.CBmU])
            P.pe(lambda e, b=b: e.transpose(bank(g, 0)[0:16, 0:128], s.acs[:, b * 16:(b + 1) * 16], ident[:]),
                 reads=du + [g.cU], writes=[g.bU[0]])
            P.act(lambda e: e.copy(s.acsTb[0:16, :], bank(g, 0)[0:16, 0:128]), reads=[g.bU[0]], writes=[s.acsTbU])
            P.dve(lambda e: e.tensor_tensor(out=s.Rbd[0:16, :].rearrange("p (h l) -> p h l", h=4),
                                            in0=s.acsTb[0:16, :].unsqueeze(1).to_broadcast([16, 4, 128]),
                                            in1=c["mg16"][:, hs4].unsqueeze(2).to_broadcast([16, 4, 128]), op=ALU.mult),
                  reads=[s.acsTbU, g.cU], writes=[s.RbdU])
            P.pe(lambda e: e.matmul(bank(g, 1), lhsT=c["ones"][0:16, :], rhs=s.Rbd[0:16, :], start=True, stop=True),
                 reads=[s.RbdU, g.cU], writes=[g.bU[1]])
            P.dve(lambda e, b=b: e.tensor_tensor(out=s.Dm.rearrange("p (h l) -> p h l", h=4),
                                                 in0=bank(g, 1).rearrange("p (h l) -> p h l", h=4),
                                                 in1=bch(s.acs, 128, b), op=ALU.subtract),
                  reads=[g.bU[1]] + du, writes=[s.DmU])
            P.dve(lambda e: e.tensor_scalar(out=s.Dm, in0=s.Dm, scalar1=0.0, scalar2=None, op0=ALU.min), reads=[s.DmU], writes=[s.DmU])
            P.act(lambda e: e.activation(out=s.E, in_=s.Dm, func=AF.Exp), reads=[s.DmU], writes=[s.EU])
            P.dve(lambda e: e.tensor_tensor(out=s.sc, in0=s.E.rearrange("p (h l) -> p h l", h=4),
                                            in1=s.CBm.unsqueeze(1).to_broadcast([128, 4, 128]), op=ALU.mult),
                  reads=[s.EU, s.CBmU], writes=[s.scU])
            P.dve(lambda e, b=b: e.tensor_tensor(out=s.Xdt.rearrange("p (h q) -> p h q", h=4), in0=x4, in1=bch(s.dt, 64, b), op=ALU.mult),
                  reads=[s.xtokU[n]] + du, writes=[s.XdtU])
            P.dve(lambda e, b=b: e.tensor_tensor(out=s.XB.rearrange("p (h q) -> p h q", h=4), in0=x4, in1=bch(s.dtd, 64, b), op=ALU.mult),
                  reads=[s.xtokU[n]] + du, writes=[s.XBU])
            for h4 in range(4):
                P.pe(lambda e, h4=h4: e.matmul(bank(g, 2)[:, h4 * 64:(h4 + 1) * 64], lhsT=s.sc[:, h4, :], rhs=s.Xdt[:, h4 * 64:(h4 + 1) * 64],
                                               start=True, stop=True), reads=[s.scU, s.XdtU], writes=[g.bU[2]])
            if b > 0:
                P.pe(lambda e, blk=blk: e.matmul(bank(g, 3)[:, 0:256], lhsT=s.CT[:, blk], rhs=s.Sbf, start=True, stop=True),
                     reads=[s.BCU[n], s.SbfU], writes=[g.bU[3]])
            P.pe(lambda e, b=b: e.matmul(bank(g, 4)[:, 0:256], lhsT=s.Btok[:, b, :], rhs=s.XB, start=True, stop=True),
                 reads=[s.BtokU[n], s.XBU], writes=[g.bU[4]])
            if b > 0:
                P.dve(lambda e, b=b: e.tensor_tensor(out=s.t1.rearrange("p (h q) -> p h q", h=4),
                                                     in0=bank(g, 3)[:, 0:256].rearrange("p (h q) -> p h q", h=4),
                                                     in1=bch(s.eacs, 64, b), op=ALU.mult), reads=[g.bU[3]] + du, writes=[s.t1U])
                P.dve(lambda e: e.tensor_tensor(out=s.t2, in0=bank(g, 2)[:, 0:256], in1=s.t1, op=ALU.add), reads=[g.bU[2], s.t1U], writes=[s.t2U])
            else:
                P.dve(lambda e: e.tensor_copy(s.t2, bank(g, 2)[:, 0:256]), reads=[g.bU[2]], writes=[s.t2U])
            P.dve(lambda e: e.tensor_tensor(out=s.t1.rearrange("p (h q) -> p h q", h=4), in0=x4,
                                            in1=s.Dsk[:, hs4].unsqueeze(2).to_broadcast([128, 4, 64]), op=ALU.mult),
                  reads=[s.xtokU[n]] + sm, writes=[s.t1U])
            P.dve(lambda e: e.tensor_tensor(out=s.t2, in0=s.t2, in1=s.t1, op=ALU.add), reads=[s.t1U, s.t2U], writes=[s.t2U])
            P.dve(lambda e: e.tensor_tensor(out=s.ytmp, in0=s.t2, in1=s.zs, op=ALU.mult), reads=[s.t2U, s.zsU], writes=[s.ytmpU])
            for j in range(2):
                P.pe(lambda e, j=j: e.transpose(bank(g, 5)[:, j * 128:(j + 1) * 128], s.ytmp[:, j * 128:(j + 1) * 128], ident[:]),
                     reads=[s.ytmpU, g.cU], writes=[g.bU[5]])
            P.act(lambda e, blk=blk: e.copy(s.yTa[:, 2 * grp:2 * grp + 2, blk], bank(g, 5)[:, 0:256].rearrange("p (j t) -> p j t", j=2)),
                  reads=[g.bU[5]], writes=[s.yTaU[2 * grp][n], s.yTaU[2 * grp + 1][n]])
            if b == 0:
                P.dve(lambda e: e.tensor_copy(s.S, bank(g, 4)[:, 0:256]), reads=[g.bU[4]], writes=[s.SU])
            else:
                P.dve(lambda e, b=b: e.tensor_tensor(out=s.S.rearrange("p (h q) -> p h q", h=4), in0=s.S.rearrange("p (h q) -> p h q", h=4),
                                                     in1=bch(s.edl, 64, b), op=ALU.mult), reads=[s.SU] + du, writes=[s.SU])
                P.dve(lambda e: e.tensor_tensor(out=s.S, in0=s.S, in1=bank(g, 4)[:, 0:256], op=ALU.add), reads=[s.SU, g.bU[4]], writes=[s.SU])
            if b + 1 < NB:
                P.act(lambda e: e.copy(s.Sbf, s.S), reads=[s.SU], writes=[s.SbfU])

    P.barrier()
    for n in range(NT):
        sl = slice(n * 512, (n + 1) * 512)
        rms_rstd_tile(g, lambda k, sl=sl: s.yTa[:, k, sl], lambda k, n=n: [s.yTaU[k][n]], 8, 1024)
        P.dve(lambda e, sl=sl: e.tensor_copy(s.rstd_all[:, sl], g.rstd_t[:]), reads=[g.rstdU], writes=[s.rstdallU])
    cnt = 0
    for m in range(KC):
        P.dma("pool", lambda e, m=m: e.dma_start(out=s.wo.rearrange("p c j -> p (c j)"), in_=g.d["ev_w_out_t"][ei, m, :, 0:1024]),
              "s_wo", writes=[s.woU])
        P.dve(lambda e: e.tensor_tensor(out=s.wos, in0=s.wo, in1=s.snw[:, 0:8].unsqueeze(2).to_broadcast([128, 8, 128]), op=ALU.mult),
              reads=[s.woU] + sm, writes=[s.wosU])
        for n in range(NT):
            sl = slice(n * 512, (n + 1) * 512)
            bk = cnt % 2
            cnt += 1
            for k in range(8):
                P.pe(lambda e, bk=bk, k=k, sl=sl: e.matmul(bank(g, bk), lhsT=s.wos[:, k, :], rhs=s.yTa[:, k, sl], start=(k == 0), stop=(k == 7)),
                     reads=[s.wosU, s.yTaU[k][n]], writes=[g.bU[bk]])
            P.dve(lambda e, bk=bk, sl=sl: e.tensor_tensor(out=s.cacc, in0=bank(g, bk), in1=s.rstd_all[:, sl], op=ALU.mult),
                  reads=[g.bU[bk], s.rstdallU], writes=[s.caccU])
            P.pool(lambda e, m=m, sl=sl: e.tensor_tensor(out=g.hT[:, m, sl], in0=g.hT[:, m, sl], in1=s.cacc, op=ALU.add),
                   reads=[s.caccU, g.hU[m][n]], writes=[g.hU[m][n]])


def even_attn(g, li):
    P, L, NB, NT = g.P, g.L, g.NB, g.NT
    ei = li // 2
    lam_init = 0.8 - 0.6 * math.exp(-0.3 * li)
    c = g.cst
    A = Carver(g)
    a = NS()
    a.corr = A.f(2048).rearrange("p (h d q) -> p h d q", h=8, d=2); a.corrU = P.unit()
    a.b31 = A.f(8); a.nb31 = A.f(8); a.bU_ = P.unit()
    a.lq = [A.f(64) for _ in range(4)]; a.lamU = P.unit()
    a.lam = A.f(8)
    a.slnw = A.f(128); a.slnwU = P.unit()
    a.w = [A.b(KC * 512).rearrange("p (k c) -> p k c", k=KC) for _ in range(2)]; a.wU = P.units(2)
    a.qT = [A.b(L) for _ in range(2)]; a.qTU = [P.unit() for _ in range(NT)]
    a.kT = A.b(L); a.kTU = [P.unit() for _ in range(NT)]
    a.v = A.b(NB * 132).rearrange("p (b c) -> p b c", c=132); a.vU = P.unit()
    a.gs = A.b(NB * 128).rearrange("p (b c) -> p b c", c=128); a.gsU = P.unit()
    a.PT = [[A.b(512) for _ in range(2)] for _ in range(2)]; a.PTU = [P.units(2), P.units(2)]
    a.yT = A.b(4 * L).rearrange("p (j t) -> p j t", j=4); a.yTU = P.units(4)
    a.wo = [A.b(512).rearrange("p (j c) -> p j c", j=4) for _ in range(2)]; a.woU = P.units(2)
    ident = c["ident"]

    P.dma("sp", lambda e: e.dma_start(out=a.corr.rearrange("p h d q -> p (h d q)"), in_=g.d["rel_biasD"]), "a_corr", writes=[a.corrU])
    P.dma("sp", lambda e: e.dma_start(out=a.b31, in_=g.d["rel_b31"].partition_broadcast(128)), "a_b31", writes=[a.bU_])
    P.dve(lambda e: e.tensor_scalar(out=a.nb31, in0=a.b31, scalar1=-1.0, scalar2=None, op0=ALU.mult), reads=[a.bU_], writes=[a.bU_])
    for h in range(8):
        P.act(lambda e, h=h: e.activation(out=a.corr[:, h], in_=a.corr[:, h], func=AF.Exp, bias=a.nb31[:, h:h + 1]),
              reads=[a.corrU, a.bU_], writes=[a.corrU])
    for i, nm in enumerate(("lambda_q1", "lambda_k1", "lambda_q2", "lambda_k2")):
        P.dma("sp", lambda e, i=i, nm=nm: e.dma_start(out=a.lq[i], in_=g.d[nm][ei].partition_broadcast(128)), "a_lam", writes=[a.lamU])
    lu = [a.lamU]
    P.dve(lambda e: e.tensor_tensor(out=a.lq[0], in0=a.lq[0], in1=a.lq[1], op=ALU.mult), reads=lu, writes=lu)
    P.dve(lambda e: e.tensor_tensor(out=a.lq[2], in0=a.lq[2], in1=a.lq[3], op=ALU.mult), reads=lu, writes=lu)
    P.dve(lambda e: e.tensor_reduce(out=a.lam[:, 0:1], in_=a.lq[0], axis=AX.X, op=ALU.add), reads=lu, writes=lu)
    P.dve(lambda e: e.tensor_reduce(out=a.lam[:, 1:2], in_=a.lq[2], axis=AX.X, op=ALU.add), reads=lu, writes=lu)
    P.act(lambda e: e.activation(out=a.lam[:, 2:4], in_=a.lam[:, 0:2], func=AF.Exp), reads=lu, writes=lu)
    P.dve(lambda e: e.tensor_tensor(out=a.lam[:, 4:5], in0=a.lam[:, 3:4], in1=a.lam[:, 2:3], op=ALU.subtract), reads=lu, writes=lu)
    P.dve(lambda e: e.tensor_scalar(out=a.lam[:, 5:6], in0=a.lam[:, 4:5], scalar1=-lam_init, scalar2=None, op0=ALU.add), reads=lu, writes=lu)
    P.dma("sp", lambda e: e.dma_start(out=a.slnw, in_=g.d["subln_w"][ei].partition_broadcast(128)), "a_slnw", writes=[a.slnwU])
    P.dve(lambda e: e.tensor_scalar(out=a.slnw, in0=a.slnw, scalar1=1.0 - lam_init, scalar2=None, op0=ALU.mult),
          reads=[a.slnwU], writes=[a.slnwU])
    P.dve(lambda e: e.memset(a.v, 1.0), writes=[a.vU])

    def load_w(h):
        i = h % 2
        P.dma("pool", lambda e: e.dma_start(out=a.w[i].rearrange("p k c -> p (k c)"), in_=g.d["ev_w_att"][ei, h]),
              f"a_w{i}", writes=[a.wU[i]])

    pc = [0]

    def project(h):
        wi = h % 2
        w = a.w[wi]
        for n in range(NT):
            sl = slice(n * 512, (n + 1) * 512)
            for which in range(2):
                bk = pc[0] % 2
                pc[0] += 1
                for k in range(KC):
                    P.pe(lambda e, bk=bk, k=k, sl=sl, which=which: e.matmul(bank(g, bk), lhsT=w[:, k, which * 128:(which + 1) * 128],
                                                                            rhs=g.uT[:, k, sl], start=(k == 0), stop=(k == KC - 1)),
                         reads=[g.uU[n], a.wU[wi]], writes=[g.bU[bk]])
                if which == 0:
                    for cc in range(2):
                        P.dve(lambda e, bk=bk, sl=sl, cc=cc: e.tensor_scalar(out=a.qT[cc][:, sl], in0=bank(g, bk), scalar1=c["maskq"][:, cc:cc + 1],
                                                                             scalar2=None, op0=ALU.mult),
                              reads=[g.bU[bk], g.cU], writes=[a.qTU[n]])
                else:
                    P.act(lambda e, bk=bk, sl=sl: e.copy(a.kT[:, sl], bank(g, bk)), reads=[g.bU[bk]], writes=[a.kTU[n]])
        for b in range(NB):
            bk = 2 + (b % 2)
            for k in range(KC):
                P.pe(lambda e, bk=bk, k=k, b=b: e.matmul(bank(g, bk)[:, 0:256], lhsT=g.uT[:, k, b * 128:(b + 1) * 128], rhs=w[:, k, 256:512],
                                                         start=(k == 0), stop=(k == KC - 1)),
                     reads=[g.uU[b // 4], a.wU[wi]], writes=[g.bU[bk]])
            P.act(lambda e, bk=bk, b=b: e.copy(a.v[:, b, 0:128], bank(g, bk)[:, 0:128]), reads=[g.bU[bk]], writes=[a.vU])
            P.act(lambda e, bk=bk, b=b: e.activation(out=a.gs[:, b, :], in_=bank(g, bk)[:, 128:256], func=AF.Silu),
                  reads=[g.bU[bk]], writes=[a.gsU])

    gc = [0]
    a.r2 = [A.f(8) for _ in range(2)]; a.rU2 = P.units(2)
    a.t2 = [A.f(128) for _ in range(2)]; a.tU2 = P.units(2)
    a.o2 = [A.f(128) for _ in range(2)]; a.oU2 = P.units(2)
    a.sqo2 = [A.f(128) for _ in range(2)]; a.sqoU2 = P.units(2)
    a.y2 = [A.f(128) for _ in range(2)]; a.yU2 = P.units(2)

    def attend(h):
        hj = h % 4
        groups = []
        for qb in range(NB):
            for gi in range(qb // 4 + 1):
                kbs = [kb for kb in range(gi * 4, gi * 4 + 4) if kb <= qb]
                groups.append((qb, gi, kbs, gc[0] % 2))
                gc[0] += 1

        def accb(qb, cc):
            return (2 + cc) if qb % 2 == 0 else cc

        def S_(grp):
            qb, gi, kbs, buf = grp
            qs = slice(qb * 128, (qb + 1) * 128)
            for cc in range(2):
                bk = 4 + 2 * cc + buf
                for j, kb in enumerate(kbs):
                    P.pe(lambda e, bk=bk, j=j, kb=kb, cc=cc: e.matmul(bank(g, bk)[:, j * 128:(j + 1) * 128],
                                                                      lhsT=a.kT[:, kb * 128:(kb + 1) * 128], rhs=a.qT[cc][:, qs],
                                                                      start=True, stop=True),
                         reads=[a.kTU[kb // 4], a.qTU[qb // 4]], writes=[g.bU[bk]])

        def E_(grp):
            qb, gi, kbs, buf = grp
            nv = len(kbs)
            for cc in range(2):
                bk = 4 + 2 * cc + buf
                pt = a.PT[cc][buf]
                ptu = a.PTU[cc][buf]
                P.act(lambda e, bk=bk, pt=pt, nv=nv: e.activation(out=pt[:, 0:nv * 128], in_=bank(g, bk)[:, 0:nv * 128], func=AF.Exp,
                                                                  scale=0.125, bias=a.b31[:, h:h + 1]),
                      reads=[g.bU[bk], a.bU_], writes=[ptu])
                for j, kb in enumerate(kbs):
                    Dd = qb - kb
                    if Dd <= 1:
                        P.dve(lambda e, pt=pt, j=j, Dd=Dd: e.tensor_tensor(out=pt[:, j * 128:(j + 1) * 128], in0=pt[:, j * 128:(j + 1) * 128],
                                                                           in1=a.corr[:, h, Dd, :], op=ALU.mult),
                              reads=[ptu, a.corrU], writes=[ptu])

        def PV_(grp):
            qb, gi, kbs, buf = grp
            for cc in range(2):
                pt = a.PT[cc][buf]
                ptu = a.PTU[cc][buf]
                ab = accb(qb, cc)
                for j, kb in enumerate(kbs):
                    P.pe(lambda e, ab=ab, pt=pt, j=j, kb=kb: e.matmul(bank(g, ab)[:, 0:129], lhsT=pt[:, j * 128:(j + 1) * 128],
                                                                      rhs=a.v[:, kb, 0:129], start=(kb == 0), stop=(kb == qb)),
                         reads=[ptu, a.vU], writes=[g.bU[ab]])

        def FIN_(qb):
            qs = slice(qb * 128, (qb + 1) * 128)
            pq = qb % 2
            b0, b1 = accb(qb, 0), accb(qb, 1)
            r, t_, o_, sqo, y_ = a.r2[pq], a.t2[pq], a.o2[pq], a.sqo2[pq], a.y2[pq]
            ru = [a.rU2[pq]]
            tU, oU, sqoU, yU = a.tU2[pq], a.oU2[pq], a.sqoU2[pq], a.yU2[pq]
            P.dve(lambda e: e.reciprocal(r[:, 0:1], bank(g, b0)[:, 128:129]), reads=[g.bU[b0]], writes=ru)
            P.dve(lambda e: e.reciprocal(r[:, 1:2], bank(g, b1)[:, 128:129]), reads=[g.bU[b1]], writes=ru)
            P.dve(lambda e: e.tensor_tensor(out=r[:, 2:3], in0=r[:, 1:2], in1=a.lam[:, 5:6], op=ALU.mult), reads=ru + lu, writes=ru)
            P.dve(lambda e: e.tensor_scalar(out=t_, in0=bank(g, b0)[:, 0:128], scalar1=r[:, 0:1], scalar2=None, op0=ALU.mult),
                  reads=[g.bU[b0]] + ru, writes=[tU])
            P.dve(lambda e: e.scalar_tensor_tensor(out=o_, in0=bank(g, b1)[:, 0:128], scalar=r[:, 2:3], in1=t_, op0=ALU.mult, op1=ALU.add),
                  reads=[g.bU[b1], tU] + ru, writes=[oU])
            P.dve(lambda e: e.tensor_tensor(out=sqo, in0=o_, in1=o_, op=ALU.mult), reads=[oU], writes=[sqoU])
            P.dve(lambda e: e.tensor_reduce(out=r[:, 3:4], in_=sqo, axis=AX.X, op=ALU.add), reads=[sqoU], writes=ru)
            P.act(lambda e: e.activation(out=r[:, 4:5], in_=r[:, 3:4], func=AF.Ln, scale=1.0 / 128, bias=EPS), reads=ru, writes=ru)
            P.act(lambda e: e.activation(out=r[:, 5:6], in_=r[:, 4:5], func=AF.Exp, scale=-0.5), reads=ru, writes=ru)
            P.dve(lambda e: e.scalar_tensor_tensor(out=y_, in0=o_, scalar=r[:, 5:6], in1=a.slnw, op0=ALU.mult, op1=ALU.mult),
                  reads=[oU, a.slnwU] + ru, writes=[yU])
            P.dve(lambda e: e.tensor_tensor(out=y_, in0=y_, in1=a.gs[:, qb, :], op=ALU.mult), reads=[yU, a.gsU], writes=[yU])
            P.pe(lambda e: e.transpose(bank(g, b0)[:, 256:384], y_, ident[:]), reads=[yU, g.cU], writes=[g.bU[b0]])
            P.act(lambda e: e.copy(a.yT[:, hj, qs], bank(g, b0)[:, 256:384]), reads=[g.bU[b0]], writes=[a.yTU[hj]])

        M = len(groups)
        S_(groups[0])
        pending_fin = None
        for i in range(M):
            if i + 1 < M:
                S_(groups[i + 1])
            E_(groups[i])
            PV_(groups[i])
            if pending_fin is not None:
                FIN_(pending_fin)
                pending_fin = None
            qb, gi, kbs, buf = groups[i]
            if kbs[-1] == qb:
                pending_fin = qb
        if pending_fin is not None:
            FIN_(pending_fin)

    oc = [0]

    def outproj(hg):
        for m in range(KC):
            wi = oc[0] % 2
            oc[0] += 1
            c0 = (8 + hg * 4) * 128
            P.dma("pool", lambda e, m=m, wi=wi, c0=c0: e.dma_start(out=a.wo[wi].rearrange("p j c -> p (j c)"),
                                                                  in_=g.d["ev_w_out_t"][ei, m, :, c0:c0 + 512]),
                  f"a_wo{wi}", writes=[a.woU[wi]])
            for n in range(NT):
                sl = slice(n * 512, (n + 1) * 512)
                bk = n % 2
                for j in range(4):
                    P.pe(lambda e, bk=bk, j=j, sl=sl, wi=wi: e.matmul(bank(g, bk), lhsT=a.wo[wi][:, j, :], rhs=a.yT[:, j, sl],
                                                                      start=(j == 0), stop=(j == 3)),
                         reads=[a.woU[wi], a.yTU[j]], writes=[g.bU[bk]])
                P.dve(lambda e, bk=bk, m=m, sl=sl: e.tensor_tensor(out=g.hT[:, m, sl], in0=g.hT[:, m, sl], in1=bank(g, bk), op=ALU.add),
                      reads=[g.bU[bk], g.hU[m][n]], writes=[g.hU[m][n]])

    load_w(0)
    for h in range(8):
        if h + 1 < 8:
            load_w(h + 1)
        project(h)
        attend(h)
        if h % 4 == 3:
            outproj(h // 4)


_CACHE = {}


def kernel(**inputs):
    x = np.ascontiguousarray(np.asarray(inputs["x"], dtype=np.float32))
    Bsz, L, _ = x.shape
    n_cores = 8
    nseq = Bsz // n_cores
    key = (L, nseq)
    if key not in _CACHE:
        _CACHE[key] = build(L, nseq, (0, 1, 2, 3))
    nc, _ = _CACHE[key]
    common = host_layout(inputs)
    in_maps = []
    for cidx in range(n_cores):
        m = dict(common)
        m["x"] = x[cidx * nseq:(cidx + 1) * nseq]
        in_maps.append(m)
    res = run_bass_kernel_spmd(nc, in_maps, core_ids=list(range(n_cores)))
    out = np.concatenate([np.asarray(r["out"]) for r in res.results], axis=0)
    return out.astype(np.float32)
```

```python
import math, contextlib
import numpy as np
import concourse.bass as bass
import concourse.mybir as mybir
from concourse.bass_utils import run_bass_kernel_spmd
from concourse.alu_op_type import AluOpType as ALU

F32 = mybir.dt.float32
BF16 = mybir.dt.bfloat16
AF = mybir.ActivationFunctionType
AX = mybir.AxisListType

D = 1024
KC = 8
EPS = 1e-6
DEPTH = 4
HG_W = 2048
ARF_N = 7616
ARB_N = 35840


class Unit:
    __slots__ = ("name", "lw", "rd")

    def __init__(self, name):
        self.name = name
        self.lw = None
        self.rd = []


class _Rec:
    def __init__(self):
        self.call = None

    def __getattr__(self, name):
        def f(*args, **kw):
            assert self.call is None
            self.call = (name, args, kw)
            return None
        return f


class Prog:
    ENGS = ("pe", "act", "dve", "pool", "sp")

    def __init__(self, nc):
        self.nc = nc
        self.ops = []
        self.nunits = 0
        self.last_eng = {}
        self.last_key = {}

    def unit(self, name=None):
        self.nunits += 1
        return Unit(name or f"u{self.nunits}")

    def units(self, n, name="u"):
        return [self.unit(f"{name}{i}") for i in range(n)]

    def capture(self):
        self._cap = []
        return self._cap

    def end_capture(self):
        c, self._cap = self._cap, None
        return c

    def replay_merged(self, A, B):
        na, nb = len(A), len(B)
        ia = ib = 0
        while ia < na or ib < nb:
            if ib >= nb or (ia < na and ia * nb <= ib * na):
                self.op(*A[ia]); ia += 1
            else:
                self.op(*B[ib]); ib += 1

    def op(self, eng, fn, reads=(), writes=(), dma_key=None, extra_deps=()):
        if fn is not None and not isinstance(fn, tuple):
            rec = _Rec()
            fn(rec)
            assert rec.call is not None
            fn = rec.call
        if getattr(self, "_cap", None) is not None:
            self._cap.append((eng, fn, tuple(reads), tuple(writes), dma_key, tuple(extra_deps)))
            return None
        idx = len(self.ops)
        deps = set(extra_deps)
        for u in reads:
            if u.lw is not None:
                deps.add(u.lw)
        for u in writes:
            if u.lw is not None:
                deps.add(u.lw)
            deps.update(u.rd)
        for u in reads:
            u.rd.append(idx)
        for u in writes:
            u.lw = idx
            u.rd = []
        deps.discard(idx)
        self.ops.append(dict(eng=eng, fn=fn, deps=deps, dma_key=dma_key))
        if fn is not None:
            if dma_key is None:
                self.last_eng[eng] = idx
            else:
                self.last_key[dma_key] = idx
        return idx

    def pe(self, fn, reads=(), writes=()):
        return self.op("pe", fn, reads, writes)

    def act(self, fn, reads=(), writes=()):
        return self.op("act", fn, reads, writes)

    def dve(self, fn, reads=(), writes=()):
        return self.op("dve", fn, reads, writes)

    def pool(self, fn, reads=(), writes=()):
        return self.op("pool", fn, reads, writes)

    def dma(self, eng, fn, key, reads=(), writes=()):
        return self.op(eng, fn, reads, writes, dma_key=key)

    def barrier(self):
        deps = set(self.last_eng.values()) | set(self.last_key.values())
        for e in self.ENGS:
            self.op(e, None, extra_deps=deps)

    def emit(self, final_wait_ops=()):
        nc = self.nc
        ops = self.ops
        n = len(ops)

        def skip(od, o):
            return (od["eng"] == "pe" and o["eng"] == "pe" and od["dma_key"] is None
                    and o["dma_key"] is None and o["fn"] is not None)

        needed = [False] * n
        for i, o in enumerate(ops):
            for d in o["deps"]:
                if skip(ops[d], o):
                    continue
                needed[d] = True
        for d in final_wait_ops:
            needed[d] = True
        chan_count = {}
        ev = [None] * n
        for i, o in enumerate(ops):
            if o["fn"] is None:
                continue
            if o["dma_key"] is not None:
                ch = ("dma", o["dma_key"])
                chan_count[ch] = chan_count.get(ch, 0) + 16
                ev[i] = (ch, chan_count[ch])
            elif needed[i]:
                ch = ("eng", o["eng"])
                chan_count[ch] = chan_count.get(ch, 0) + 1
                ev[i] = (ch, chan_count[ch])
        chans = sorted(chan_count.keys(), key=str)
        self.n_sems = len(chans)
        sems = {}
        stack = contextlib.ExitStack()
        for ci, ch in enumerate(chans):
            sems[ch] = stack.enter_context(nc.semaphore(f"s{ci}"))
        known = {e: {} for e in self.ENGS}
        clock = [None] * n
        streams = {e: [] for e in self.ENGS}
        for i, o in enumerate(ops):
            e = o["eng"]
            kn = known[e]
            wd = {}
            for d in sorted(o["deps"]):
                od = ops[d]
                if skip(od, o):
                    continue
                ch, v = ev[d]
                if kn.get(ch, 0) >= v:
                    continue
                for c2, v2 in clock[d].items():
                    if kn.get(c2, 0) < v2:
                        kn[c2] = v2
                wd[ch] = max(wd.get(ch, 0), v)
            ck = dict(kn)
            if ev[i] is not None:
                ch, v = ev[i]
                ck[ch] = v
            clock[i] = ck
            streams[e].append((list(wd.items()), o["fn"], ev[i]))
        final = [ev[d] for d in final_wait_ops]
        for ch, tot in chan_count.items():
            if ch[0] == "dma":
                final.append((ch, tot))
        self.sems, self.streams, self.final, self._stack = sems, streams, final, stack

    def run_block(self):
        nc = self.nc
        sems, streams, final = self.sems, self.streams, self.final
        with nc.Block() as block:
            def mk(ename):
                def body(eng):
                    for waits, fn, e in streams[ename]:
                        for ch, v in waits:
                            eng.wait_ge(sems[ch], v)
                        if fn is None:
                            continue
                        ins = getattr(eng, fn[0])(*fn[1], **fn[2])
                        if e is not None:
                            ins.then_inc(sems[e[0]], 16 if e[0][0] == "dma" else 1)
                    if ename == "sp":
                        for ch, v in final:
                            eng.wait_ge(sems[ch], v)
                return body
            block.tensor(mk("pe"))
            block.scalar(mk("act"))
            block.vector(mk("dve"))
            block.gpsimd(mk("pool"))
            block.sync(mk("sp"))
        self._stack.close()


def _t5_bucket(rel):
    n = np.maximum(rel, 0)
    max_exact = 16
    large = max_exact + (np.log(np.maximum(n, 1).astype(np.float32) / max_exact)
                         / math.log(128 / max_exact) * (32 - max_exact)).astype(np.int32)
    large = np.minimum(large, 31)
    return np.where(n < max_exact, n, large)


def host_consts():
    s = np.arange(128)[:, None]
    t = np.arange(128)[None, :]
    c = {}
    c["ident"] = np.eye(128, dtype=np.float32)
    c["ones"] = np.ones((128, 128), np.float32)
    c["triC"] = ((s <= t).astype(np.float32) - (s <= 63).astype(np.float32))
    c["triU"] = (s > t).astype(np.float32)
    c["triI"] = (s <= t).astype(np.float32)
    sel = np.zeros((128, 2), np.float32)
    sel[:64, 0] = 1.0
    sel[:, 1] = 1.0
    c["sel"] = sel
    c["mg16"] = np.eye(16, dtype=np.float32)
    mq = np.zeros((128, 2), np.float32)
    mq[:64, 0] = 1.0
    mq[64:, 1] = 1.0
    c["maskq"] = mq
    return c


def host_layout(inp):
    f = lambda a: np.ascontiguousarray(np.asarray(a, dtype=np.float32))
    m = dict(host_consts())
    m["final_norm_w"] = f(inp["final_norm_w"])
    m["norm_w_cols"] = f(np.asarray(inp["norm_w"]).reshape(4, 8, 128).transpose(0, 2, 1))
    owin = np.asarray(inp["odd_w_in"])
    t = owin.reshape(2, 8, 128, 4, 16, 128).transpose(0, 4, 2, 1, 3, 5)
    m["odd_w_in_t"] = f(t).reshape(2, 16, 128, 8 * 512)
    m["odd_w_out"] = f(inp["odd_w_out"])
    m["hgrn_lower_bounds"] = f(inp["hgrn_lower_bounds"])
    m["hgrn_norm_w"] = f(inp["hgrn_norm_w"])
    ew = np.asarray(inp["even_w_in"]).reshape(2, 8, 128, 7184)
    z = ew[..., 0:1024]; xs = ew[..., 1024:2048]; Bm = ew[..., 2048:2560]; Cm = ew[..., 2560:3072]
    dt = ew[..., 3072:3088]
    q = ew[..., 3088:4112]; kk = ew[..., 4112:5136]; v = ew[..., 5136:6160]; gg = ew[..., 6160:7184]
    ssd = np.concatenate([z.reshape(2, 8, 128, 4, 256), xs.reshape(2, 8, 128, 4, 256),
                          Bm.reshape(2, 8, 128, 4, 128), Cm.reshape(2, 8, 128, 4, 128)], axis=-1)
    m["ev_w_ssd"] = f(ssd.transpose(0, 3, 2, 1, 4)).reshape(2, 4, 128, 8 * 768)
    m["ev_w_dt"] = f(dt.transpose(0, 2, 1, 3)).reshape(2, 128, 8 * 16)
    att = np.concatenate([q.reshape(2, 8, 128, 8, 128), kk.reshape(2, 8, 128, 8, 128),
                          v.reshape(2, 8, 128, 8, 128), gg.reshape(2, 8, 128, 8, 128)], axis=-1)
    m["ev_w_att"] = f(att.transpose(0, 3, 2, 1, 4)).reshape(2, 8, 128, 8 * 512)
    wo = np.asarray(inp["even_w_out"]).reshape(2, 16, 128, 8, 128)
    m["ev_w_out_t"] = f(wo.transpose(0, 3, 2, 1, 4)).reshape(2, 8, 128, 16 * 128)
    m["conv_w_cols"] = f(np.asarray(inp["conv_w"]).reshape(2, 4, 16, 128).transpose(0, 3, 2, 1)).reshape(2, 128, 64)
    m["conv_b_cols"] = f(np.asarray(inp["conv_b"]).reshape(2, 16, 128).transpose(0, 2, 1))
    for nm in ("dt_bias", "A_log", "D_skip", "lambda_q1", "lambda_k1", "lambda_q2", "lambda_k2", "subln_w"):
        m[nm] = f(inp[nm])
    m["ssd_norm_w_cols"] = f(np.asarray(inp["ssd_norm_w"]).reshape(2, 8, 128).transpose(0, 2, 1))
    rb = np.asarray(inp["rel_bias"], dtype=np.float32)
    kpos = np.arange(128)[:, None]
    qpos = np.arange(128)[None, :]
    bd = np.empty((128, 8, 2, 128), np.float32)
    for Dd in range(2):
        rel = qpos - kpos + 128 * Dd
        bidx = _t5_bucket(rel)
        g_ = rb[bidx]
        g_ = np.where((rel >= 0)[:, :, None], g_, np.float32(-30000.0))
        bd[:, :, Dd, :] = g_.transpose(0, 2, 1)
    m["rel_biasD"] = f(bd).reshape(128, 8 * 2 * 128)
    m["rel_b31"] = f(rb[31])
    return m


class NS:
    pass


class Carver:
    def __init__(self, g):
        self.g = g
        self.fo = 0
        self.bo = 0

    def f(self, n):
        ap = self.g.arf[:, self.fo:self.fo + n]
        self.fo += (n + 7) // 8 * 8
        assert self.fo <= ARF_N, ("ARF overflow", self.fo)
        return ap

    def b(self, n):
        ap = self.g.arb[:, self.bo:self.bo + n]
        self.bo += (n + 15) // 16 * 16
        assert self.bo <= ARB_N, ("ARB overflow", self.bo)
        return ap


def bank(g, i):
    return g.ps[:, i, :]


def build(L=2048, NSEQ=2, layers=(0, 1, 2, 3)):
    nc = bass.Bass("TRN2", target_bir_lowering=False)
    NT, NB = L // 512, L // 128
    g = NS()
    g.nc, g.L, g.NT, g.NB = nc, L, NT, NB
    dr = lambda name, shape, kind="ExternalInput": nc.dram_tensor(name, list(shape), F32, kind=kind).ap()
    g.x_d = dr("x", [NSEQ, L, D])
    g.out_d = dr("out", [NSEQ, L, D], "ExternalOutput")
    g.d = {}
    shapes = {
        "final_norm_w": [D], "norm_w_cols": [DEPTH, 128, KC],
        "ident": [128, 128], "ones": [128, 128], "triC": [128, 128], "triU": [128, 128], "triI": [128, 128],
        "sel": [128, 2], "mg16": [16, 16], "maskq": [128, 2],
        "odd_w_in_t": [2, 16, 128, KC * 512], "odd_w_out": [2, HG_W, D], "hgrn_lower_bounds": [DEPTH, HG_W],
        "hgrn_norm_w": [2, 128],
        "ev_w_ssd": [2, 4, 128, 8 * 768], "ev_w_dt": [2, 128, 8 * 16], "ev_w_att": [2, 8, 128, 8 * 512],
        "ev_w_out_t": [2, 8, 128, 16 * 128], "conv_w_cols": [2, 128, 64], "conv_b_cols": [2, 128, 16],
        "dt_bias": [2, 16], "A_log": [2, 16], "D_skip": [2, 16], "lambda_q1": [2, 64], "lambda_k1": [2, 64],
        "lambda_q2": [2, 64], "lambda_k2": [2, 64], "subln_w": [2, 128], "ssd_norm_w_cols": [2, 128, 8],
        "rel_biasD": [128, 8 * 2 * 128], "rel_b31": [8],
    }
    for nm, shp in shapes.items():
        g.d[nm] = dr(nm, shp)
    g.in_names = ["x"] + list(shapes.keys())

    es = contextlib.ExitStack()
    sb = lambda name, shape, dt=F32: es.enter_context(nc.sbuf_tensor(name, list(shape), dt))
    P = Prog(nc)
    g.P = P
    g.hT = sb("hT", [128, KC, L]); g.hU = [[P.unit(f"h{k}_{n}") for n in range(NT)] for k in range(KC)]
    g.uT = sb("uT", [128, KC, L], BF16); g.uU = [P.unit(f"u{n}") for n in range(NT)]
    g.cst = {}
    g.cU = P.unit("consts")
    for nm in ("ident", "ones", "triI"):
        g.cst[nm] = sb("c_" + nm, [128, 128])
    g.cst["sel"] = sb("c_sel", [128, 2])
    g.cst["maskq"] = sb("c_maskq", [128, 2])
    g.cst["mg16"] = sb("c_mg16", [16, 16])
    g.nwc = sb("nwc", [128, DEPTH, KC])
    g.stat = sb("stat", [128, 16]); g.statU = P.unit("stat")
    g.sq = [sb(f"sq{i}", [128, 512]) for i in range(2)]; g.sqU = P.units(2, "sq")
    g.rstd_t = sb("rstd_t", [128, 512]); g.rstdU = P.unit("rstd_t")
    g.arf = sb("arf", [128, ARF_N])
    g.arb = sb("arb", [128, ARB_N], BF16)
    g.ps = es.enter_context(nc.psum_tensor("ps", [128, 8, 512], F32))
    g.bU = P.units(8, "bank")

    for nm in ("ident", "ones", "triI", "sel", "maskq", "mg16"):
        P.dma("sp", lambda e, nm=nm: e.dma_start(out=g.cst[nm][:], in_=g.d[nm]), "c_" + nm, writes=[g.cU])
    P.dma("sp", lambda e: e.dma_start(out=g.nwc[:], in_=g.d["norm_w_cols"].rearrange("l p k -> p l k")), "c_nwc", writes=[g.cU])

    out_ops = []
    for s in range(NSEQ):
        P.barrier()
        load_x(g, s)
        for li in layers:
            rms_to_uT(g, li)
            P.barrier()
            if li % 2 == 1:
                odd_layer(g, li)
            else:
                even_ssd(g, li)
                P.barrier()
                even_attn(g, li)
            P.barrier()
        out_ops += final_norm_store(g, s)
    P.emit(final_wait_ops=out_ops[-4:])
    P.run_block()
    es.close()
    return nc, P


def load_x(g, s):
    P, NT = g.P, g.NT
    A = Carver(g)
    xst = A.f(4096).rearrange("p (b d) -> p b d", b=4)
    xU = P.unit("xst")
    ident = g.cst["ident"]
    for n in range(NT):
        src = g.x_d[s, n * 512:(n + 1) * 512, :].rearrange("(b p) d -> p b d", p=128)
        P.dma("sp", lambda e, src=src: e.dma_start(out=xst, in_=src), "xst", writes=[xU])
        for k in range(KC):
            bk = k % 8
            for b in range(4):
                P.pe(lambda e, bk=bk, b=b, k=k: e.transpose(
                    bank(g, bk)[:, b * 128:(b + 1) * 128], xst[:, b, k * 128:(k + 1) * 128], ident[:]),
                    reads=[xU, g.cU], writes=[g.bU[bk]])
            if k % 2 == 0:
                P.dve(lambda e, bk=bk, k=k, n=n: e.tensor_copy(g.hT[:, k, n * 512:(n + 1) * 512], bank(g, bk)),
                      reads=[g.bU[bk]], writes=[g.hU[k][n]])
            else:
                P.act(lambda e, bk=bk, k=k, n=n: e.copy(g.hT[:, k, n * 512:(n + 1) * 512], bank(g, bk)),
                      reads=[g.bU[bk]], writes=[g.hU[k][n]])


def final_norm_store(g, s):
    P, NB = g.P, g.NB
    A = Carver(g)
    fnw = A.f(D); fnwU = P.unit("fnw")
    ost = [A.f(D) for _ in range(2)]; ostU = P.units(2, "ost")
    junk = A.f(1024); junkU = P.unit("junk")
    ident = g.cst["ident"]
    P.dma("sp", lambda e: e.dma_start(out=fnw, in_=g.d["final_norm_w"].partition_broadcast(128)), "fnw", writes=[fnwU])
    outs = []
    for b in range(NB):
        n = b // 4
        oi = b % 2
        for k in range(KC):
            bk = k // 4
            P.pe(lambda e, bk=bk, k=k, b=b: e.transpose(
                bank(g, bk)[:, (k % 4) * 128:(k % 4 + 1) * 128], g.hT[:, k, b * 128:(b + 1) * 128], ident[:]),
                reads=[g.hU[k][n], g.cU], writes=[g.bU[bk]])
        for half in range(2):
            P.act(lambda e, half=half: e.activation(
                out=junk[:, half * 512:(half + 1) * 512], in_=bank(g, half), func=AF.Square,
                accum_out=g.stat[:, half:half + 1]),
                reads=[g.bU[half]], writes=[junkU, g.statU])
        P.dve(lambda e: e.tensor_tensor(out=g.stat[:, 2:3], in0=g.stat[:, 0:1], in1=g.stat[:, 1:2], op=ALU.add),
              reads=[g.statU], writes=[g.statU])
        P.act(lambda e: e.activation(out=g.stat[:, 3:4], in_=g.stat[:, 2:3], func=AF.Ln, scale=1.0 / D, bias=EPS),
              reads=[g.statU], writes=[g.statU])
        P.act(lambda e: e.activation(out=g.stat[:, 4:5], in_=g.stat[:, 3:4], func=AF.Exp, scale=-0.5),
              reads=[g.statU], writes=[g.statU])
        for half in range(2):
            P.dve(lambda e, half=half, oi=oi: e.scalar_tensor_tensor(
                out=ost[oi][:, half * 512:(half + 1) * 512], in0=bank(g, half), scalar=g.stat[:, 4:5],
                in1=fnw[:, half * 512:(half + 1) * 512], op0=ALU.mult, op1=ALU.mult),
                reads=[g.bU[half], g.statU, fnwU], writes=[ostU[oi]])
        o = P.dma("sp", lambda e, oi=oi, s=s, b=b: e.dma_start(out=g.out_d[s, b * 128:(b + 1) * 128, :], in_=ost[oi]),
                  f"ost{oi}", reads=[ostU[oi]])
        outs.append(o)
    return outs


def rms_rstd_tile(g, src_fn, reads_fn, nchunks, dim):
    P = g.P
    ones = g.cst["ones"]
    for k in range(nchunks):
        i = k % 2
        P.act(lambda e, i=i, k=k: e.activation(out=g.sq[i][:], in_=src_fn(k), func=AF.Square),
              reads=reads_fn(k), writes=[g.sqU[i]])
        P.pe(lambda e, i=i, k=k: e.matmul(bank(g, 7), lhsT=ones[:], rhs=g.sq[i][:], start=(k == 0), stop=(k == nchunks - 1)),
             reads=[g.sqU[i], g.cU], writes=[g.bU[7]])
    P.act(lambda e: e.activation(out=g.rstd_t[:], in_=bank(g, 7), func=AF.Ln, scale=1.0 / dim, bias=EPS),
          reads=[g.bU[7]], writes=[g.rstdU])
    P.act(lambda e: e.activation(out=g.rstd_t[:], in_=g.rstd_t[:], func=AF.Exp, scale=-0.5),
          reads=[g.rstdU], writes=[g.rstdU])


def rms_to_uT(g, li):
    P, NT = g.P, g.NT
    for n in range(NT):
        sl = slice(n * 512, (n + 1) * 512)
        rms_rstd_tile(g, lambda k, sl=sl: g.hT[:, k, sl], lambda k, n=n: [g.hU[k][n]], KC, D)
        for k in range(KC):
            P.dve(lambda e, k=k, sl=sl: e.scalar_tensor_tensor(
                out=g.uT[:, k, sl], in0=g.hT[:, k, sl], scalar=g.nwc[:, li, k:k + 1], in1=g.rstd_t[:],
                op0=ALU.mult, op1=ALU.mult),
                reads=[g.hU[k][n], g.rstdU, g.cU], writes=[g.uU[n]])


def odd_layer(g, li):
    P, L, NB, NT = g.P, g.L, g.NB, g.NT
    oi = li // 2
    A = Carver(g)
    o = NS()
    c = g.cst
    f3 = lambda: A.f(512).rearrange("p (b d) -> p b d", b=4)
    o.logf = f3(); o.logfU = P.unit()
    o.tA = f3(); o.tAU = P.unit()
    o.kk = f3(); o.kkU = P.unit()
    o.qs = f3(); o.qsU = P.unit()
    o.e13 = A.f(1024).rearrange("p (t b d) -> p t b d", t=2, b=4); o.e13U = P.unit()
    o.e2 = f3(); o.e2U = P.unit()
    o.qt = o.qs; o.qtU = o.qsU
    o.kt = o.e2; o.ktU = o.e2U
    o.lbr = f3(); o.lbrU = P.unit()
    o.lbh = A.f(128); o.omlh = A.f(128); o.den = A.f(128); o.lbU = P.unit()
    o.eb = [A.f(NB * 2).rearrange("p (b t) -> p b t", t=2) for _ in range(2)]
    o.S = A.f(128); o.SU = P.unit()
    o.junk2 = A.f(128); o.junk2U = P.unit()
    o.oall = A.f(NB * 128).rearrange("p (b d) -> p b d", d=128); o.oallU = P.unit()
    o.ssall = A.f(NB); o.rsall = A.f(NB); o.ssU = P.unit()
    o.hnw = A.f(128); o.hnwU = P.unit()
    o.triC = A.f(128); o.triU = A.f(128); o.triUU = P.unit()
    o.bst = A.f(8); o.bstU = P.unit()
    o.w = [A.b(KC * 512).rearrange("p (k c) -> p k c", k=KC) for _ in range(2)]; o.wU = P.units(2)
    o.wout = A.b(2 * D).rearrange("p (j m) -> p j m", j=2); o.woutU = P.units(2)
    o.qT = [A.b(L) for _ in range(2)]
    o.kT = [A.b(L) for _ in range(2)]
    hb3 = lambda: A.b(NB * 128).rearrange("p (b d) -> p b d", d=128)
    o.kh = [hb3() for _ in range(2)]
    o.v = [hb3() for _ in range(2)]
    o.gs = [hb3() for _ in range(2)]
    o.hbU = [[[P.unit() for _ in range(NT)] for _ in range(6)] for _ in range(2)]
    o.attm = [A.b(128) for _ in range(2)]; o.attmU = P.units(2)
    o.Sb2 = [A.b(128) for _ in range(2)]; o.SbU2 = P.units(2)
    o.yT = A.b(2 * L).rearrange("p (j t) -> p j t", j=2); o.yTU = P.units(2)
    QT, KT, KH, VV, GS, EB = range(6)

    P.dma("sp", lambda e: e.dma_start(out=o.hnw, in_=g.d["hgrn_norm_w"][oi].partition_broadcast(128)), "o_hnw", writes=[o.hnwU])
    P.dma("sp", lambda e: e.dma_start(out=o.triC, in_=g.d["triC"]), "o_tri", writes=[o.triUU])
    P.dma("sp", lambda e: e.dma_start(out=o.triU, in_=g.d["triU"]), "o_tri", writes=[o.triUU])
    for i in range(2):
        P.dve(lambda e, i=i: e.memset(o.attm[i], 0.0), writes=[o.attmU[i]])

    def load_w(h):
        i = h % 2
        P.dma("pool", lambda e: e.dma_start(out=o.w[i].rearrange("p k c -> p (k c)"), in_=g.d["odd_w_in_t"][oi, h]),
              f"o_w{i}", writes=[o.wU[i]])

    def head_lb(h):
        hs = slice(h * 128, (h + 1) * 128)
        P.dma("sp", lambda e: e.dma_start(out=o.lbr, in_=g.d["hgrn_lower_bounds"][:, hs].partition_broadcast(128)),
              "o_lbr", writes=[o.lbrU])
        P.act(lambda e: e.activation(out=o.lbr, in_=o.lbr, func=AF.Exp), reads=[o.lbrU], writes=[o.lbrU])
        P.dve(lambda e: e.tensor_tensor(out=o.den, in0=o.lbr[:, 0, :], in1=o.lbr[:, 1, :], op=ALU.add), reads=[o.lbrU], writes=[o.lbU])
        P.dve(lambda e: e.tensor_tensor(out=o.den, in0=o.den, in1=o.lbr[:, 2, :], op=ALU.add), reads=[o.lbrU, o.lbU], writes=[o.lbU])
        P.dve(lambda e: e.tensor_tensor(out=o.den, in0=o.den, in1=o.lbr[:, 3, :], op=ALU.add), reads=[o.lbrU, o.lbU], writes=[o.lbU])
        P.dve(lambda e: e.reciprocal(o.den, o.den), reads=[o.lbU], writes=[o.lbU])
        if li == 1:
            P.dve(lambda e: e.tensor_tensor(out=o.lbh, in0=o.lbr[:, 1, :], in1=o.den, op=ALU.mult), reads=[o.lbrU, o.lbU], writes=[o.lbU])
        else:
            P.dve(lambda e: e.tensor_tensor(out=o.lbh, in0=o.lbr[:, 1, :], in1=o.lbr[:, 2, :], op=ALU.add), reads=[o.lbrU, o.lbU], writes=[o.lbU])
            for j in range(3, li + 1):
                P.dve(lambda e, j=j: e.tensor_tensor(out=o.lbh, in0=o.lbh, in1=o.lbr[:, j, :], op=ALU.add), reads=[o.lbrU, o.lbU], writes=[o.lbU])
            P.dve(lambda e: e.tensor_tensor(out=o.lbh, in0=o.lbh, in1=o.den, op=ALU.mult), reads=[o.lbU], writes=[o.lbU])
        P.dve(lambda e: e.tensor_scalar(out=o.omlh, in0=o.lbh, scalar1=-1.0, scalar2=1.0, op0=ALU.mult, op1=ALU.add),
              reads=[o.lbU], writes=[o.lbU])

    def stageA(h, n):
        hb = h % 2
        wi = h % 2
        U = o.hbU[hb]
        for b in range(4):
            tb = n * 4 + b
            for k in range(KC):
                P.pe(lambda e, b=b, tb=tb, k=k: e.matmul(bank(g, b), lhsT=g.uT[:, k, tb * 128:(tb + 1) * 128], rhs=o.w[wi][:, k, :],
                                                         start=(k == 0), stop=(k == KC - 1)),
                     reads=[g.uU[n], o.wU[wi]], writes=[g.bU[b]])
        pj = g.ps[:, 0:4, :]
        pb = [g.bU[0], g.bU[1], g.bU[2], g.bU[3]]
        bc4 = lambda t: t.unsqueeze(1).to_broadcast([128, 4, 128])
        P.act(lambda e: e.activation(out=o.tA, in_=pj[:, :, 128:256], func=AF.Sigmoid), reads=pb, writes=[o.tAU])
        P.act(lambda e: e.activation(out=o.qs, in_=pj[:, :, 0:128], func=AF.Silu), reads=pb, writes=[o.qsU])
        P.act(lambda e: e.activation(out=o.gs[hb][:, n * 4:(n + 1) * 4, :], in_=pj[:, :, 384:512], func=AF.Silu),
              reads=pb, writes=[U[GS][n]])
        P.act(lambda e: e.copy(o.v[hb][:, n * 4:(n + 1) * 4, :], pj[:, :, 256:384]), reads=pb, writes=[U[VV][n]])
        P.dve(lambda e: e.tensor_tensor(out=o.tA, in0=o.tA, in1=bc4(o.omlh), op=ALU.mult), reads=[o.tAU, o.lbU], writes=[o.tAU])
        P.dve(lambda e: e.tensor_tensor(out=o.tA, in0=o.tA, in1=bc4(o.lbh), op=ALU.add), reads=[o.tAU, o.lbU], writes=[o.tAU])
        P.act(lambda e: e.activation(out=o.logf, in_=o.tA, func=AF.Ln), reads=[o.tAU], writes=[o.logfU])
        P.pool(lambda e: e.tensor_scalar(out=o.kk, in0=o.tA, scalar1=-1.0, scalar2=1.0, op0=ALU.mult, op1=ALU.add),
               reads=[o.tAU], writes=[o.kkU])
        for b in range(4):
            P.pe(lambda e, b=b: e.matmul(bank(g, 0)[:, b * 128:(b + 1) * 128], lhsT=o.triC, rhs=o.logf[:, b, :], start=True, stop=True),
                 reads=[o.logfU, o.triUU], writes=[g.bU[0]])
        for b in range(4):
            P.pe(lambda e, b=b: e.matmul(bank(g, 1)[:, b * 128:(b + 1) * 128], lhsT=o.triU, rhs=o.logf[:, b, :], start=True, stop=True),
                 reads=[o.logfU, o.triUU], writes=[g.bU[1]])
        for b in range(4):
            P.pe(lambda e, b=b: e.matmul(bank(g, 2)[:, b * 2:b * 2 + 2], lhsT=o.logf[:, b, :], rhs=c["sel"][:], start=True, stop=True),
                 reads=[o.logfU, g.cU], writes=[g.bU[2]])
        P.act(lambda e: e.activation(out=o.eb[hb][:, n * 4:(n + 1) * 4, :], in_=bank(g, 2)[:, 0:8].rearrange("p (b t) -> p b t", t=2), func=AF.Exp),
              reads=[g.bU[2]], writes=[U[EB][n]])
        P.act(lambda e: e.activation(out=o.e13, in_=g.ps[:, 0:2, :].rearrange("p t (b d) -> p t b d", b=4), func=AF.Exp),
              reads=[g.bU[0], g.bU[1]], writes=[o.e13U])
        P.act(lambda e: e.activation(out=o.e2, in_=bank(g, 0).rearrange("p (b d) -> p b d", b=4), func=AF.Exp, scale=-1.0),
              reads=[g.bU[0]], writes=[o.e2U])
        P.dve(lambda e: e.tensor_tensor(out=o.qs, in0=o.qs, in1=o.e13[:, 0], op=ALU.mult), reads=[o.qsU, o.e13U], writes=[o.qsU])
        P.pool(lambda e: e.tensor_tensor(out=o.e2, in0=o.kk, in1=o.e2, op=ALU.mult), reads=[o.kkU, o.e2U], writes=[o.e2U])
        P.dve(lambda e: e.tensor_tensor(out=o.kh[hb][:, n * 4:(n + 1) * 4, :], in0=o.kk, in1=o.e13[:, 1], op=ALU.mult),
              reads=[o.kkU, o.e13U], writes=[U[KH][n]])
        idf = c["ident"]
        for b in range(4):
            P.pe(lambda e, b=b: e.transpose(bank(g, 3)[:, b * 128:(b + 1) * 128], o.qt[:, b, :], idf[:]),
                 reads=[o.qtU, g.cU], writes=[g.bU[3]])
        for b in range(4):
            P.pe(lambda e, b=b: e.transpose(bank(g, 2)[:, b * 128:(b + 1) * 128], o.kt[:, b, :], idf[:]),
                 reads=[o.ktU, g.cU], writes=[g.bU[2]])
        P.act(lambda e: e.copy(o.qT[hb][:, n * 512:(n + 1) * 512], bank(g, 3)), reads=[g.bU[3]], writes=[U[QT][n]])
        P.act(lambda e: e.copy(o.kT[hb][:, n * 512:(n + 1) * 512], bank(g, 2)), reads=[g.bU[2]], writes=[U[KT][n]])

    def stageB(h):
        hb = h % 2
        U = o.hbU[hb]

        def att(tb):
            n = tb // 4
            ts = slice(tb * 128, (tb + 1) * 128)
            bk = 5 + 2 * (tb % 2)
            P.pe(lambda e: e.matmul(bank(g, bk)[:, 64:128], lhsT=o.kT[hb][:, ts],
                                    rhs=o.qT[hb][:, tb * 128 + 64:(tb + 1) * 128], start=True, stop=True),
                 reads=[U[QT][n], U[KT][n]], writes=[g.bU[bk]])
            P.pe(lambda e: e.matmul(bank(g, bk)[0:64, 0:64], lhsT=o.kT[hb][:, tb * 128:tb * 128 + 64],
                                    rhs=o.qT[hb][:, tb * 128:tb * 128 + 64], start=True, stop=True),
                 reads=[U[QT][n], U[KT][n]], writes=[g.bU[bk]])

        def mask(tb):
            ai = tb % 2
            bk = 5 + 2 * (tb % 2)
            P.dve(lambda e: e.tensor_tensor(out=o.attm[ai][:, 64:128], in0=bank(g, bk)[:, 64:128], in1=c["triI"][:, 64:128], op=ALU.mult),
                  reads=[g.bU[bk], g.cU], writes=[o.attmU[ai]])
            P.dve(lambda e: e.tensor_tensor(out=o.attm[ai][0:64, 0:64], in0=bank(g, bk)[0:64, 0:64], in1=c["triI"][0:64, 0:64], op=ALU.mult),
                  reads=[g.bU[bk], g.cU], writes=[o.attmU[ai]])

        att(0)
        mask(0)
        for tb in range(NB):
            n = tb // 4
            ts = slice(tb * 128, (tb + 1) * 128)
            ai = tb % 2
            si = tb % 2
            P.pe(lambda e: e.matmul(bank(g, 6)[:, 128:256], lhsT=o.kh[hb][:, tb, :], rhs=o.v[hb][:, tb, :], start=True, stop=True),
                 reads=[U[KH][n], U[VV][n]], writes=[g.bU[6]])
            if tb + 1 < NB:
                att(tb + 1)
            P.pe(lambda e: e.matmul(bank(g, 4)[:, 0:128], lhsT=o.attm[ai], rhs=o.v[hb][:, tb, :], start=True, stop=(tb == 0)),
                 reads=[o.attmU[ai], U[VV][n]], writes=[g.bU[4]])
            if tb > 0:
                P.pe(lambda e: e.matmul(bank(g, 4)[:, 0:128], lhsT=o.qT[hb][:, ts], rhs=o.Sb2[si], start=False, stop=True),
                     reads=[o.SbU2[si], U[QT][n]], writes=[g.bU[4]])
            if tb == 0:
                P.dve(lambda e: e.tensor_copy(o.S, bank(g, 6)[:, 128:256]), reads=[g.bU[6]], writes=[o.SU])
            else:
                P.dve(lambda e: e.scalar_tensor_tensor(out=o.S, in0=o.S, scalar=o.eb[hb][:, tb, 1:2], in1=bank(g, 6)[:, 128:256],
                                                       op0=ALU.mult, op1=ALU.add),
                      reads=[o.SU, g.bU[6], U[EB][n]], writes=[o.SU])
            if tb + 1 < NB:
                nn = (tb + 1) // 4
                sn = (tb + 1) % 2
                P.dve(lambda e: e.tensor_scalar(out=o.Sb2[sn], in0=o.S, scalar1=o.eb[hb][:, tb + 1, 0:1], scalar2=None, op0=ALU.mult),
                      reads=[o.SU, U[EB][nn]], writes=[o.SbU2[sn]])
                mask(tb + 1)
            P.act(lambda e: e.copy(o.oall[:, tb, :], bank(g, 4)[:, 0:128]), reads=[g.bU[4]], writes=[o.oallU])
            P.act(lambda e: e.activation(out=o.junk2, in_=bank(g, 4)[:, 0:128], func=AF.Square, accum_out=o.ssall[:, tb:tb + 1]),
                  reads=[g.bU[4]], writes=[o.junk2U, o.ssU])

    def stageC(h):
        hb = h % 2
        U = o.hbU[hb]
        hj = h % 2
        P.act(lambda e: e.activation(out=o.rsall, in_=o.ssall, func=AF.Ln, scale=1.0 / 128, bias=EPS), reads=[o.ssU], writes=[o.ssU])
        P.act(lambda e: e.activation(out=o.rsall, in_=o.rsall, func=AF.Exp, scale=-0.5), reads=[o.ssU], writes=[o.ssU])
        P.dve(lambda e: e.tensor_tensor(out=o.oall, in0=o.oall, in1=o.rsall.unsqueeze(2).to_broadcast([128, NB, 128]), op=ALU.mult),
              reads=[o.oallU, o.ssU], writes=[o.oallU])
        P.dve(lambda e: e.tensor_tensor(out=o.oall, in0=o.oall, in1=o.hnw.unsqueeze(1).to_broadcast([128, NB, 128]), op=ALU.mult),
              reads=[o.oallU, o.hnwU], writes=[o.oallU])
        P.dve(lambda e: e.tensor_tensor(out=o.oall, in0=o.oall, in1=o.gs[hb], op=ALU.mult),
              reads=[o.oallU] + [U[GS][n] for n in range(NT)], writes=[o.oallU])
        for n in range(NT):
            bk = 4 + (n % 4)
            for b in range(4):
                P.pe(lambda e, bk=bk, b=b, n=n: e.transpose(bank(g, bk)[:, b * 128:(b + 1) * 128], o.oall[:, n * 4 + b, :], c["ident"][:]),
                     reads=[o.oallU, g.cU], writes=[g.bU[bk]])
            P.act(lambda e, bk=bk, n=n: e.copy(o.yT[:, hj, n * 512:(n + 1) * 512], bank(g, bk)), reads=[g.bU[bk]], writes=[o.yTU[hj]])

    def outproj(hp):
        for j in range(2):
            src = g.d["odd_w_out"][oi, (hp * 2 + j) * 128:(hp * 2 + j + 1) * 128, :]
            P.dma("pool", lambda e, j=j, src=src: e.dma_start(out=o.wout[:, j, :], in_=src), f"o_wout{j}", writes=[o.woutU[j]])
        cnt = 0
        for m in range(KC):
            for n in range(NT):
                bk = 4 + cnt % 4
                cnt += 1
                for j in range(2):
                    P.pe(lambda e, bk=bk, m=m, n=n, j=j: e.matmul(bank(g, bk), lhsT=o.wout[:, j, m * 128:(m + 1) * 128],
                                                                     rhs=o.yT[:, j, n * 512:(n + 1) * 512], start=(j == 0), stop=(j == 1)),
                         reads=[o.woutU[j], o.yTU[j]], writes=[g.bU[bk]])
                P.dve(lambda e, bk=bk, m=m, n=n: e.tensor_tensor(out=g.hT[:, m, n * 512:(n + 1) * 512], in0=g.hT[:, m, n * 512:(n + 1) * 512],
                                                                   in1=bank(g, bk), op=ALU.add),
                      reads=[g.bU[bk], g.hU[m][n]], writes=[g.hU[m][n]])

    load_w(0)
    load_w(1)
    head_lb(0)
    for n in range(NT):
        stageA(0, n)
    for h in range(16):
        P.capture()
        stageB(h)
        stageC(h)
        if h % 2 == 1:
            outproj(h // 2)
        LB = P.end_capture()
        P.capture()
        if h + 1 < 16:
            if h + 2 < 16:
                load_w(h + 2)
            head_lb(h + 1)
            for n in range(NT):
                stageA(h + 1, n)
        LA = P.end_capture()
        P.replay_merged(LA, LB)


def even_ssd(g, li):
    P, L, NB, NT = g.P, g.L, g.NB, g.NT
    ei = li // 2
    c = g.cst
    A = Carver(g)
    s = NS()
    HB = NB * 16
    s.xpre = A.f(515); s.xpreU = P.unit()
    s.cacc = A.f(512); s.caccU = P.unit()
    v3 = lambda ap: ap.rearrange("p (b h) -> p b h", h=16)
    s.dt = A.f(HB); s.atok = A.f(HB); s.acs = A.f(HB); s.eacs = A.f(HB); s.dtd = A.f(HB); s.edl = A.f(HB)
    s.dtU = P.unit()
    s.cw = A.f(64); s.cb = A.f(16); s.dtb = A.f(16); s.Abc = A.f(16); s.Dsk = A.f(16); s.snw = A.f(8)
    s.smallU = P.unit()
    s.S = A.f(256); s.SU = P.unit()
    scan_off = A.fo
    s.Rbd = A.f(512); s.RbdU = P.unit()
    D2 = lambda n: ([A.f(n) for _ in range(2)], P.units(2))
    s.acsTb2, s.acsTbU2 = D2(128)
    s.CBm2, s.CBmU2 = D2(128)
    s.Dm2, s.DmU2 = D2(512)
    s.E2, s.EU2 = D2(512)
    s.t12, s.t1U2 = D2(256)
    s.t22, s.t2U2 = D2(256)
    s.ytmp2, s.ytmpU2 = D2(256)
    s.rstd_all = g.arf[:, scan_off:scan_off + L]; s.rstdallU = P.unit()
    assert scan_off + L <= ARF_N
    s.w = A.b(KC * 768).rearrange("p (k c) -> p k c", k=KC); s.wU = P.unit()
    s.wdt = A.b(KC * 16).rearrange("p (k c) -> p k c", k=KC); s.wdtU = P.unit()
    s.BT = A.b(L); s.CT = A.b(L); s.BCU = [P.unit() for _ in range(NT)]
    s.xtok = A.b(NB * 256).rearrange("p (b c) -> p b c", c=256); s.xtokU = [P.unit() for _ in range(NT)]
    s.Btok = A.b(NB * 128).rearrange("p (b c) -> p b c", c=128); s.BtokU = [P.unit() for _ in range(NT)]
    scan_bo = A.bo
    s.zs2 = [A.b(256) for _ in range(2)]; s.zsU2 = P.units(2)
    s.sc2 = [A.b(512).rearrange("p (h l) -> p h l", h=4) for _ in range(2)]; s.scU2 = P.units(2)
    s.Xdt2 = [A.b(256) for _ in range(2)]; s.XdtU2 = P.units(2)
    s.XB2 = [A.b(256) for _ in range(2)]; s.XBU2 = P.units(2)
    s.Sbf = A.b(256); s.SbfU = P.unit()
    s.yTa = A.b(8 * L).rearrange("p (c t) -> p c t", c=8); s.yTaU = [[P.unit() for _ in range(NT)] for _ in range(8)]
    s.wo = g.arb[:, scan_bo:scan_bo + 1024].rearrange("p (c j) -> p c j", c=8); s.woU = P.unit()
    s.wos = s.wo; s.wosU = s.woU
    ident = c["ident"]

    sm = [s.smallU]
    P.dma("sp", lambda e: e.dma_start(out=s.cw, in_=g.d["conv_w_cols"][ei]), "s_small", writes=sm)
    P.dma("sp", lambda e: e.dma_start(out=s.cb, in_=g.d["conv_b_cols"][ei]), "s_small", writes=sm)
    P.dma("sp", lambda e: e.dma_start(out=s.dtb, in_=g.d["dt_bias"][ei].partition_broadcast(128)), "s_small", writes=sm)
    P.dma("sp", lambda e: e.dma_start(out=s.Abc, in_=g.d["A_log"][ei].partition_broadcast(128)), "s_small", writes=sm)
    P.dma("sp", lambda e: e.dma_start(out=s.Dsk, in_=g.d["D_skip"][ei].partition_broadcast(128)), "s_small", writes=sm)
    P.dma("sp", lambda e: e.dma_start(out=s.snw, in_=g.d["ssd_norm_w_cols"][ei]), "s_small", writes=sm)
    P.act(lambda e: e.activation(out=s.Abc, in_=s.Abc, func=AF.Exp), reads=sm, writes=sm)
    P.dve(lambda e: e.tensor_scalar(out=s.Abc, in0=s.Abc, scalar1=-1.0, scalar2=None, op0=ALU.mult), reads=sm, writes=sm)
    P.dma("pool", lambda e: e.dma_start(out=s.wdt.rearrange("p k c -> p (k c)"), in_=g.d["ev_w_dt"][ei]), "s_wdt", writes=[s.wdtU])

    for b in range(NB):
        for k in range(KC):
            P.pe(lambda e, b=b, k=k: e.matmul(bank(g, 0)[:, b * 16:(b + 1) * 16], lhsT=g.uT[:, k, b * 128:(b + 1) * 128],
                                              rhs=s.wdt[:, k, :], start=(k == 0), stop=(k == KC - 1)),
                 reads=[g.uU[b // 4], s.wdtU], writes=[g.bU[0]])
    bc_h = lambda t: t.unsqueeze(1).to_broadcast([128, NB, 16])
    du = [s.dtU]
    P.dve(lambda e: e.tensor_tensor(out=v3(s.dt), in0=v3(bank(g, 0)[:, 0:HB]), in1=bc_h(s.dtb), op=ALU.add),
          reads=[g.bU[0]] + sm, writes=du)
    P.act(lambda e: e.activation(out=s.dt, in_=s.dt, func=AF.Exp), reads=du, writes=du)
    P.act(lambda e: e.activation(out=s.dt, in_=s.dt, func=AF.Ln, bias=1.0), reads=du, writes=du)
    P.dve(lambda e: e.tensor_tensor(out=v3(s.atok), in0=v3(s.dt), in1=bc_h(s.Abc), op=ALU.mult), reads=du + sm, writes=du)
    for b in range(NB):
        P.pe(lambda e, b=b: e.matmul(bank(g, 1)[:, b * 16:(b + 1) * 16], lhsT=c["triI"][:], rhs=s.atok[:, b * 16:(b + 1) * 16],
                                     start=True, stop=True), reads=du + [g.cU], writes=[g.bU[1]])
    for b in range(NB):
        P.pe(lambda e, b=b: e.matmul(bank(g, 2)[:, b * 16:(b + 1) * 16], lhsT=c["ones"][:], rhs=s.atok[:, b * 16:(b + 1) * 16],
                                     start=True, stop=True), reads=du + [g.cU], writes=[g.bU[2]])
    P.dve(lambda e: e.tensor_copy(s.acs, bank(g, 1)[:, 0:HB]), reads=[g.bU[1]], writes=du)
    P.act(lambda e: e.activation(out=s.eacs, in_=s.acs, func=AF.Exp), reads=du, writes=du)
    P.dve(lambda e: e.tensor_copy(s.edl, bank(g, 2)[:, 0:HB]), reads=[g.bU[2]], writes=du)
    P.dve(lambda e: e.tensor_tensor(out=s.dtd, in0=s.edl, in1=s.acs, op=ALU.subtract), reads=du, writes=du)
    P.act(lambda e: e.activation(out=s.dtd, in_=s.dtd, func=AF.Exp), reads=du, writes=du)
    P.dve(lambda e: e.tensor_tensor(out=s.dtd, in0=s.dtd, in1=s.dt, op=ALU.mult), reads=du, writes=du)
    P.act(lambda e: e.activation(out=s.edl, in_=s.edl, func=AF.Exp), reads=du, writes=du)

    pcnt = [0]
    for grp in range(4):
        P.dma("pool", lambda e, grp=grp: e.dma_start(out=s.w.rearrange("p k c -> p (k c)"), in_=g.d["ev_w_ssd"][ei, grp]),
              "s_w", writes=[s.wU])
        chunks = [(256, 2 * grp, "x0"), (384, 2 * grp + 1, "x1"), (512, 8 + grp, "B"), (640, 12 + grp, "C")]
        cp1, cp2 = [], []
        for wc0, cch, kind in chunks:
            for n in range(NT):
                sl = slice(n * 512, (n + 1) * 512)
                bk = 3 + (pcnt[0] % 2)
                pcnt[0] += 1
                P.capture()
                for k in range(KC):
                    P.pe(lambda e, bk=bk, k=k, wc0=wc0, sl=sl: e.matmul(bank(g, bk), lhsT=s.w[:, k, wc0:wc0 + 128], rhs=g.uT[:, k, sl],
                                                                        start=(k == 0), stop=(k == KC - 1)),
                         reads=[g.uU[n], s.wU], writes=[g.bU[bk]])
                cp1.append(P.end_capture())
                P.capture()
                if n == 0:
                    P.dve(lambda e: e.memset(s.xpre[:, 0:3], 0.0), writes=[s.xpreU])
                else:
                    P.dve(lambda e: e.tensor_copy(s.xpre[:, 0:3], s.xpre[:, 512:515]), reads=[s.xpreU], writes=[s.xpreU])
                P.act(lambda e, bk=bk: e.copy(s.xpre[:, 3:515], bank(g, bk)), reads=[g.bU[bk]], writes=[s.xpreU])
                P.dve(lambda e, cch=cch: e.tensor_scalar(out=s.cacc, in0=s.xpre[:, 3:515], scalar1=s.cw[:, cch * 4 + 3:cch * 4 + 4],
                                                          scalar2=s.cb[:, cch:cch + 1], op0=ALU.mult, op1=ALU.add),
                      reads=[s.xpreU] + sm, writes=[s.caccU])
                for tap in (2, 1, 0):
                    P.dve(lambda e, cch=cch, tap=tap: e.scalar_tensor_tensor(
                        out=s.cacc, in0=s.xpre[:, tap:tap + 512], scalar=s.cw[:, cch * 4 + tap:cch * 4 + tap + 1], in1=s.cacc,
                        op0=ALU.mult, op1=ALU.add), reads=[s.xpreU, s.caccU] + sm, writes=[s.caccU])
                P.act(lambda e: e.activation(out=s.cacc, in_=s.cacc, func=AF.Silu), reads=[s.caccU], writes=[s.caccU])
                if kind in ("B", "C"):
                    dst = s.BT if kind == "B" else s.CT
                    P.dve(lambda e, dst=dst, sl=sl: e.tensor_copy(dst[:, sl], s.cacc), reads=[s.caccU], writes=[s.BCU[n]])
                if kind != "C":
                    for j in range(4):
                        P.pe(lambda e, j=j: e.transpose(bank(g, 5)[:, j * 128:(j + 1) * 128], s.cacc[:, j * 128:(j + 1) * 128], ident[:]),
                             reads=[s.caccU, g.cU], writes=[g.bU[5]])
                    src = bank(g, 5).rearrange("p (b c) -> p b c", b=4)
                    if kind == "B":
                        P.act(lambda e, n=n, src=src: e.copy(s.Btok[:, n * 4:(n + 1) * 4, :], src), reads=[g.bU[5]], writes=[s.BtokU[n]])
                    else:
                        co = 0 if kind == "x0" else 128
                        P.act(lambda e, n=n, src=src, co=co: e.copy(s.xtok[:, n * 4:(n + 1) * 4, co:co + 128], src),
                              reads=[g.bU[5]], writes=[s.xtokU[n]])
                cp2.append(P.end_capture())
        P.replay_merged(cp1[0], [])
        for i in range(len(cp2)):
            P.replay_merged(cp1[i + 1] if i + 1 < len(cp1) else [], [])
            P.replay_merged(cp2[i], [])
        hs4 = slice(4 * grp, 4 * grp + 4)
        fronts, backs = [], []
        for b in range(NB):
            P.capture()
            n = b // 4
            blk = slice(b * 128, (b + 1) * 128)
            hcol = lambda t, b=b: v3(t)[:, b, hs4]
            bch = lambda t, w, b=b: hcol(t, b).unsqueeze(2).to_broadcast([128, 4, w])
            x4 = s.xtok[:, b, :].rearrange("p (h q) -> p h q", h=4)
            pb_ = b % 2
            s.acsTb, s.acsTbU = s.acsTb2[pb_], s.acsTbU2[pb_]
            s.CBm, s.CBmU = s.CBm2[pb_], s.CBmU2[pb_]
            s.Dm, s.DmU = s.Dm2[pb_], s.DmU2[pb_]
            s.E, s.EU = s.E2[pb_], s.EU2[pb_]
            s.t1, s.t1U = s.t12[pb_], s.t1U2[pb_]
            s.t2, s.t2U = s.t22[pb_], s.t2U2[pb_]
            s.ytmp, s.ytmpU = s.ytmp2[pb_], s.ytmpU2[pb_]
            s.zs, s.zsU = s.zs2[pb_], s.zsU2[pb_]
            s.sc, s.scU = s.sc2[pb_], s.scU2[pb_]
            s.Xdt, s.XdtU = s.Xdt2[pb_], s.XdtU2[pb_]
            s.XB, s.XBU = s.XB2[pb_], s.XBU2[pb_]
            for k in range(KC):
                P.pe(lambda e, k=k, blk=blk: e.matmul(bank(g, 6)[:, 0:256], lhsT=g.uT[:, k, blk], rhs=s.w[:, k, 0:256],
                                                      start=(k == 0), stop=(k == KC - 1)),
                     reads=[g.uU[n], s.wU], writes=[g.bU[6]])
            P.act(lambda e: e.activation(out=s.zs, in_=bank(g, 6)[:, 0:256], func=AF.Silu), reads=[g.bU[6]], writes=[s.zsU])
            P.pe(lambda e, blk=blk: e.matmul(bank(g, 7)[:, 0:128], lhsT=s.BT[:, blk], rhs=s.CT[:, blk], start=True, stop=True),
                 reads=[s.BCU[n]], writes=[g.bU[7]])
            P.dve(lambda e: e.tensor_tensor(out=s.CBm, in0=bank(g, 7)[:, 0:128], in1=c["triI"][:], op=ALU.mult),
                  reads=[g.bU[7], g.cU], writes=[s.CBmU])
            P.pe(lambda e, b=b: e.transpose(bank(g, 0)[0:16, 0:128], s.acs[:, b * 16:(b + 1) * 16], ident[:]),
                 reads=du + [g.cU], writes=[g.bU[0]])
            P.act(lambda e: e.copy(s.acsTb[0:16, :], bank(g, 0)[0:16, 0:128]), reads=[g.bU[0]], writes=[s.acsTbU])
            P.dve(lambda e: e.tensor_tensor(out=s.Rbd[0:16, :].rearrange("p (h l) -> p h l", h=4),
                                            in0=s.acsTb[0:16, :].unsqueeze(1).to_broadcast([16, 4, 128]),
                                            in1=c["mg16"][:, hs4].unsqueeze(2).to_broadcast([16, 4, 128]), op=ALU.mult),
                  reads=[s.acsTbU, g.cU], writes=[s.RbdU])
            P.pe(lambda e: e.matmul(bank(g, 1), lhsT=c["ones"][0:16, :], rhs=s.Rbd[0:16, :], start=True, stop=True),
                 reads=[s.RbdU, g.cU], writes=[g.bU[1]])
            P.dve(lambda e, b=b: e.tensor_tensor(out=s.Dm.rearrange("p (h l) -> p h l", h=4),
                                                 in0=bank(g, 1).rearrange("p (h l) -> p h l", h=4),
                                                 in1=bch(s.acs, 128, b), op=ALU.subtract),
                  reads=[g.bU[1]] + du, writes=[s.DmU])
            P.dve(lambda e: e.tensor_scalar(out=s.Dm, in0=s.Dm, scalar1=0.0, scalar2=None, op0=ALU.min), reads=[s.DmU], writes=[s.DmU])
            P.act(lambda e: e.activation(out=s.E, in_=s.Dm, func=AF.Exp), reads=[s.DmU], writes=[s.EU])
            P.dve(lambda e: e.tensor_tensor(out=s.sc, in0=s.E.rearrange("p (h l) -> p h l", h=4),
                                            in1=s.CBm.unsqueeze(1).to_broadcast([128, 4, 128]), op=ALU.mult),
                  reads=[s.EU, s.CBmU], writes=[s.scU])
            P.dve(lambda e, b=b: e.tensor_tensor(out=s.Xdt.rearrange("p (h q) -> p h q", h=4), in0=x4, in1=bch(s.dt, 64, b), op=ALU.mult),
                  reads=[s.xtokU[n]] + du, writes=[s.XdtU])
            P.dve(lambda e, b=b: e.tensor_tensor(out=s.XB.rearrange("p (h q) -> p h q", h=4), in0=x4, in1=bch(s.dtd, 64, b), op=ALU.mult),
                  reads=[s.xtokU[n]] + du, writes=[s.XBU])
            fronts.append(P.end_capture())
            P.capture()
            for h4 in range(4):
                P.pe(lambda e, h4=h4: e.matmul(bank(g, 2)[:, h4 * 64:(h4 + 1) * 64], lhsT=s.sc[:, h4, :], rhs=s.Xdt[:, h4 * 64:(h4 + 1) * 64],
                                               start=True, stop=True), reads=[s.scU, s.XdtU], writes=[g.bU[2]])
            if b > 0:
                P.pe(lambda e, blk=blk: e.matmul(bank(g, 3)[:, 0:256], lhsT=s.CT[:, blk], rhs=s.Sbf, start=True, stop=True),
                     reads=[s.BCU[n], s.SbfU], writes=[g.bU[3]])
            P.pe(lambda e, b=b: e.matmul(bank(g, 4)[:, 0:256], lhsT=s.Btok[:, b, :], rhs=s.XB, start=True, stop=True),
                 reads=[s.BtokU[n], s.XBU], writes=[g.bU[4]])
            if b > 0:
                P.dve(lambda e, b=b: e.tensor_tensor(out=s.t1.rearrange("p (h q) -> p h q", h=4),
                                                     in0=bank(g, 3)[:, 0:256].rearrange("p (h q) -> p h q", h=4),
                                                     in1=bch(s.eacs, 64, b), op=ALU.mult), reads=[g.bU[3]] + du, writes=[s.t1U])
                P.dve(lambda e: e.tensor_tensor(out=s.t2, in0=bank(g, 2)[:, 0:256], in1=s.t1, op=ALU.add), reads=[g.bU[2], s.t1U], writes=[s.t2U])
            else:
                P.dve(lambda e: e.tensor_copy(s.t2, bank(g, 2)[:, 0:256]), reads=[g.bU[2]], writes=[s.t2U])
            P.dve(lambda e: e.tensor_tensor(out=s.t1.rearrange("p (h q) -> p h q", h=4), in0=x4,
                                            in1=s.Dsk[:, hs4].unsqueeze(2).to_broadcast([128, 4, 64]), op=ALU.mult),
                  reads=[s.xtokU[n]] + sm, writes=[s.t1U])
            P.dve(lambda e: e.tensor_tensor(out=s.t2, in0=s.t2, in1=s.t1, op=ALU.add), reads=[s.t1U, s.t2U], writes=[s.t2U])
            P.dve(lambda e: e.tensor_tensor(out=s.ytmp, in0=s.t2, in1=s.zs, op=ALU.mult), reads=[s.t2U, s.zsU], writes=[s.ytmpU])
            for j in range(2):
                P.pe(lambda e, j=j: e.transpose(bank(g, 5)[:, j * 128:(j + 1) * 128], s.ytmp[:, j * 128:(j + 1) * 128], ident[:]),
                     reads=[s.ytmpU, g.cU], writes=[g.bU[5]])
            P.act(lambda e, blk=blk: e.copy(s.yTa[:, 2 * grp:2 * grp + 2, blk], bank(g, 5)[:, 0:256].rearrange("p (j t) -> p j t", j=2)),
                  reads=[g.bU[5]], writes=[s.yTaU[2 * grp][n], s.yTaU[2 * grp + 1][n]])
            if b == 0:
                P.dve(lambda e: e.tensor_copy(s.S, bank(g, 4)[:, 0:256]), reads=[g.bU[4]], writes=[s.SU])
            else:
                P.dve(lambda e, b=b: e.tensor_tensor(out=s.S.rearrange("p (h q) -> p h q", h=4), in0=s.S.rearrange("p (h q) -> p h q", h=4),
                                                     in1=bch(s.edl, 64, b), op=ALU.mult), reads=[s.SU] + du, writes=[s.SU])
                P.dve(lambda e: e.tensor_tensor(out=s.S, in0=s.S, in1=bank(g, 4)[:, 0:256], op=ALU.add), reads=[s.SU, g.bU[4]], writes=[s.SU])
            if b + 1 < NB:
                P.act(lambda e: e.copy(s.Sbf, s.S), reads=[s.SU], writes=[s.SbfU])
            backs.append(P.end_capture())
        P.replay_merged(fronts[0], [])
        for b in range(NB):
            P.replay_merged(fronts[b + 1] if b + 1 < NB else [], backs[b])

    P.barrier()
    for n in range(NT):
        sl = slice(n * 512, (n + 1) * 512)
        rms_rstd_tile(g, lambda k, sl=sl: s.yTa[:, k, sl], lambda k, n=n: [s.yTaU[k][n]], 8, 1024)
        P.dve(lambda e, sl=sl: e.tensor_copy(s.rstd_all[:, sl], g.rstd_t[:]), reads=[g.rstdU], writes=[s.rstdallU])
    cnt = 0
    for m in range(KC):
        P.dma("pool", lambda e, m=m: e.dma_start(out=s.wo.rearrange("p c j -> p (c j)"), in_=g.d["ev_w_out_t"][ei, m, :, 0:1024]),
              "s_wo", writes=[s.woU])
        P.dve(lambda e: e.tensor_tensor(out=s.wos, in0=s.wo, in1=s.snw[:, 0:8].unsqueeze(2).to_broadcast([128, 8, 128]), op=ALU.mult),
              reads=[s.woU] + sm, writes=[s.wosU])
        for n in range(NT):
            sl = slice(n * 512, (n + 1) * 512)
            bk = cnt % 2
            cnt += 1
            for k in range(8):
                P.pe(lambda e, bk=bk, k=k, sl=sl: e.matmul(bank(g, bk), lhsT=s.wos[:, k, :], rhs=s.yTa[:, k, sl], start=(k == 0), stop=(k == 7)),
                     reads=[s.wosU, s.yTaU[k][n]], writes=[g.bU[bk]])
            P.dve(lambda e, bk=bk, sl=sl: e.tensor_tensor(out=s.cacc, in0=bank(g, bk), in1=s.rstd_all[:, sl], op=ALU.mult),
                  reads=[g.bU[bk], s.rstdallU], writes=[s.caccU])
            P.pool(lambda e, m=m, sl=sl: e.tensor_tensor(out=g.hT[:, m, sl], in0=g.hT[:, m, sl], in1=s.cacc, op=ALU.add),
                   reads=[s.caccU, g.hU[m][n]], writes=[g.hU[m][n]])


def even_attn(g, li):
    P, L, NB, NT = g.P, g.L, g.NB, g.NT
    ei = li // 2
    lam_init = 0.8 - 0.6 * math.exp(-0.3 * li)
    c = g.cst
    A = Carver(g)
    a = NS()
    a.corr = A.f(2048).rearrange("p (h d q) -> p h d q", h=8, d=2); a.corrU = P.unit()
    a.b31 = A.f(8); a.nb31 = A.f(8); a.bU_ = P.unit()
    a.lq = [A.f(64) for _ in range(4)]; a.lamU = P.unit()
    a.lam = A.f(8)
    a.slnw = A.f(128); a.slnwU = P.unit()
    a.w = [A.b(KC * 512).rearrange("p (k c) -> p k c", k=KC) for _ in range(2)]; a.wU = P.units(2)
    a.qT = [A.b(L) for _ in range(2)]; a.qTU = [P.unit() for _ in range(NT)]
    a.kT = A.b(L); a.kTU = [P.unit() for _ in range(NT)]
    a.v = A.b(NB * 132).rearrange("p (b c) -> p b c", c=132); a.vU = P.unit()
    a.gs = A.b(NB * 128).rearrange("p (b c) -> p b c", c=128); a.gsU = P.unit()
    a.PT = [[A.b(512) for _ in range(2)] for _ in range(2)]; a.PTU = [P.units(2), P.units(2)]
    a.yT = A.b(4 * L).rearrange("p (j t) -> p j t", j=4); a.yTU = P.units(4)
    a.wo = [A.b(512).rearrange("p (j c) -> p j c", j=4) for _ in range(2)]; a.woU = P.units(2)
    ident = c["ident"]

    P.dma("sp", lambda e: e.dma_start(out=a.corr.rearrange("p h d q -> p (h d q)"), in_=g.d["rel_biasD"]), "a_corr", writes=[a.corrU])
    P.dma("sp", lambda e: e.dma_start(out=a.b31, in_=g.d["rel_b31"].partition_broadcast(128)), "a_b31", writes=[a.bU_])
    P.dve(lambda e: e.tensor_scalar(out=a.nb31, in0=a.b31, scalar1=-1.0, scalar2=None, op0=ALU.mult), reads=[a.bU_], writes=[a.bU_])
    for h in range(8):
        P.act(lambda e, h=h: e.activation(out=a.corr[:, h], in_=a.corr[:, h], func=AF.Exp, bias=a.nb31[:, h:h + 1]),
              reads=[a.corrU, a.bU_], writes=[a.corrU])
    for i, nm in enumerate(("lambda_q1", "lambda_k1", "lambda_q2", "lambda_k2")):
        P.dma("sp", lambda e, i=i, nm=nm: e.dma_start(out=a.lq[i], in_=g.d[nm][ei].partition_broadcast(128)), "a_lam", writes=[a.lamU])
    lu = [a.lamU]
    P.dve(lambda e: e.tensor_tensor(out=a.lq[0], in0=a.lq[0], in1=a.lq[1], op=ALU.mult), reads=lu, writes=lu)
    P.dve(lambda e: e.tensor_tensor(out=a.lq[2], in0=a.lq[2], in1=a.lq[3], op=ALU.mult), reads=lu, writes=lu)
    P.dve(lambda e: e.tensor_reduce(out=a.lam[:, 0:1], in_=a.lq[0], axis=AX.X, op=ALU.add), reads=lu, writes=lu)
    P.dve(lambda e: e.tensor_reduce(out=a.lam[:, 1:2], in_=a.lq[2], axis=AX.X, op=ALU.add), reads=lu, writes=lu)
    P.act(lambda e: e.activation(out=a.lam[:, 2:4], in_=a.lam[:, 0:2], func=AF.Exp), reads=lu, writes=lu)
    P.dve(lambda e: e.tensor_tensor(out=a.lam[:, 4:5], in0=a.lam[:, 3:4], in1=a.lam[:, 2:3], op=ALU.subtract), reads=lu, writes=lu)
    P.dve(lambda e: e.tensor_scalar(out=a.lam[:, 5:6], in0=a.lam[:, 4:5], scalar1=-lam_init, scalar2=None, op0=ALU.add), reads=lu, writes=lu)
    P.dma("sp", lambda e: e.dma_start(out=a.slnw, in_=g.d["subln_w"][ei].partition_broadcast(128)), "a_slnw", writes=[a.slnwU])
    P.dve(lambda e: e.tensor_scalar(out=a.slnw, in0=a.slnw, scalar1=1.0 - lam_init, scalar2=None, op0=ALU.mult),
          reads=[a.slnwU], writes=[a.slnwU])
    P.dve(lambda e: e.memset(a.v, 1.0), writes=[a.vU])

    def load_w(h):
        i = h % 2
        P.dma("pool", lambda e: e.dma_start(out=a.w[i].rearrange("p k c -> p (k c)"), in_=g.d["ev_w_att"][ei, h]),
              f"a_w{i}", writes=[a.wU[i]])

    pc = [0]

    def project(h):
        wi = h % 2
        w = a.w[wi]
        for n in range(NT):
            sl = slice(n * 512, (n + 1) * 512)
            for which in range(2):
                bk = pc[0] % 2
                pc[0] += 1
                for k in range(KC):
                    P.pe(lambda e, bk=bk, k=k, sl=sl, which=which: e.matmul(bank(g, bk), lhsT=w[:, k, which * 128:(which + 1) * 128],
                                                                            rhs=g.uT[:, k, sl], start=(k == 0), stop=(k == KC - 1)),
                         reads=[g.uU[n], a.wU[wi]], writes=[g.bU[bk]])
                if which == 0:
                    for cc in range(2):
                        P.dve(lambda e, bk=bk, sl=sl, cc=cc: e.tensor_scalar(out=a.qT[cc][:, sl], in0=bank(g, bk), scalar1=c["maskq"][:, cc:cc + 1],
                                                                             scalar2=None, op0=ALU.mult),
                              reads=[g.bU[bk], g.cU], writes=[a.qTU[n]])
                else:
                    P.act(lambda e, bk=bk, sl=sl: e.copy(a.kT[:, sl], bank(g, bk)), reads=[g.bU[bk]], writes=[a.kTU[n]])
        for b in range(NB):
            bk = 2 + (b % 2)
            for k in range(KC):
                P.pe(lambda e, bk=bk, k=k, b=b: e.matmul(bank(g, bk)[:, 0:256], lhsT=g.uT[:, k, b * 128:(b + 1) * 128], rhs=w[:, k, 256:512],
                                                         start=(k == 0), stop=(k == KC - 1)),
                     reads=[g.uU[b // 4], a.wU[wi]], writes=[g.bU[bk]])
            P.act(lambda e, bk=bk, b=b: e.copy(a.v[:, b, 0:128], bank(g, bk)[:, 0:128]), reads=[g.bU[bk]], writes=[a.vU])
            P.act(lambda e, bk=bk, b=b: e.activation(out=a.gs[:, b, :], in_=bank(g, bk)[:, 128:256], func=AF.Silu),
                  reads=[g.bU[bk]], writes=[a.gsU])

    gc = [0]
    a.r2 = [A.f(8) for _ in range(2)]; a.rU2 = P.units(2)
    a.t2 = [A.f(128) for _ in range(2)]; a.tU2 = P.units(2)
    a.o2 = [A.f(128) for _ in range(2)]; a.oU2 = P.units(2)
    a.sqo2 = [A.f(128) for _ in range(2)]; a.sqoU2 = P.units(2)
    a.y2 = [A.f(128) for _ in range(2)]; a.yU2 = P.units(2)

    def attend(h):
        hj = h % 4
        groups = []
        for qb in range(NB):
            for gi in range(qb // 4 + 1):
                kbs = [kb for kb in range(gi * 4, gi * 4 + 4) if kb <= qb]
                groups.append((qb, gi, kbs, gc[0] % 2))
                gc[0] += 1

        def accb(qb, cc):
            return (2 + cc) if qb % 2 == 0 else cc

        def S_(grp):
            qb, gi, kbs, buf = grp
            qs = slice(qb * 128, (qb + 1) * 128)
            for cc in range(2):
                bk = 4 + 2 * cc + buf
                for j, kb in enumerate(kbs):
                    P.pe(lambda e, bk=bk, j=j, kb=kb, cc=cc: e.matmul(bank(g, bk)[:, j * 128:(j + 1) * 128],
                                                                      lhsT=a.kT[:, kb * 128:(kb + 1) * 128], rhs=a.qT[cc][:, qs],
                                                                      start=True, stop=True),
                         reads=[a.kTU[kb // 4], a.qTU[qb // 4]], writes=[g.bU[bk]])

        def E_(grp):
            qb, gi, kbs, buf = grp
            nv = len(kbs)
            for cc in range(2):
                bk = 4 + 2 * cc + buf
                pt = a.PT[cc][buf]
                ptu = a.PTU[cc][buf]
                P.act(lambda e, bk=bk, pt=pt, nv=nv: e.activation(out=pt[:, 0:nv * 128], in_=bank(g, bk)[:, 0:nv * 128], func=AF.Exp,
                                                                  scale=0.125, bias=a.b31[:, h:h + 1]),
                      reads=[g.bU[bk], a.bU_], writes=[ptu])
                for j, kb in enumerate(kbs):
                    Dd = qb - kb
                    if Dd <= 1:
                        P.dve(lambda e, pt=pt, j=j, Dd=Dd: e.tensor_tensor(out=pt[:, j * 128:(j + 1) * 128], in0=pt[:, j * 128:(j + 1) * 128],
                                                                           in1=a.corr[:, h, Dd, :], op=ALU.mult),
                              reads=[ptu, a.corrU], writes=[ptu])

        def PV_(grp):
            qb, gi, kbs, buf = grp
            for cc in range(2):
                pt = a.PT[cc][buf]
                ptu = a.PTU[cc][buf]
                ab = accb(qb, cc)
                for j, kb in enumerate(kbs):
                    P.pe(lambda e, ab=ab, pt=pt, j=j, kb=kb: e.matmul(bank(g, ab)[:, 0:129], lhsT=pt[:, j * 128:(j + 1) * 128],
                                                                      rhs=a.v[:, kb, 0:129], start=(kb == 0), stop=(kb == qb)),
                         reads=[ptu, a.vU], writes=[g.bU[ab]])

        def FIN_(qb):
            qs = slice(qb * 128, (qb + 1) * 128)
            pq = qb % 2
            b0, b1 = accb(qb, 0), accb(qb, 1)
            r, t_, o_, sqo, y_ = a.r2[pq], a.t2[pq], a.o2[pq], a.sqo2[pq], a.y2[pq]
            ru = [a.rU2[pq]]
            tU, oU, sqoU, yU = a.tU2[pq], a.oU2[pq], a.sqoU2[pq], a.yU2[pq]
            P.dve(lambda e: e.reciprocal(r[:, 0:1], bank(g, b0)[:, 128:129]), reads=[g.bU[b0]], writes=ru)
            P.dve(lambda e: e.reciprocal(r[:, 1:2], bank(g, b1)[:, 128:129]), reads=[g.bU[b1]], writes=ru)
            P.dve(lambda e: e.tensor_tensor(out=r[:, 2:3], in0=r[:, 1:2], in1=a.lam[:, 5:6], op=ALU.mult), reads=ru + lu, writes=ru)
            P.dve(lambda e: e.tensor_scalar(out=t_, in0=bank(g, b0)[:, 0:128], scalar1=r[:, 0:1], scalar2=None, op0=ALU.mult),
                  reads=[g.bU[b0]] + ru, writes=[tU])
            P.dve(lambda e: e.scalar_tensor_tensor(out=o_, in0=bank(g, b1)[:, 0:128], scalar=r[:, 2:3], in1=t_, op0=ALU.mult, op1=ALU.add),
                  reads=[g.bU[b1], tU] + ru, writes=[oU])
            P.dve(lambda e: e.tensor_tensor(out=sqo, in0=o_, in1=o_, op=ALU.mult), reads=[oU], writes=[sqoU])
            P.dve(lambda e: e.tensor_reduce(out=r[:, 3:4], in_=sqo, axis=AX.X, op=ALU.add), reads=[sqoU], writes=ru)
            P.act(lambda e: e.activation(out=r[:, 4:5], in_=r[:, 3:4], func=AF.Ln, scale=1.0 / 128, bias=EPS), reads=ru, writes=ru)
            P.act(lambda e: e.activation(out=r[:, 5:6], in_=r[:, 4:5], func=AF.Exp, scale=-0.5), reads=ru, writes=ru)
            P.dve(lambda e: e.scalar_tensor_tensor(out=y_, in0=o_, scalar=r[:, 5:6], in1=a.slnw, op0=ALU.mult, op1=ALU.mult),
                  reads=[oU, a.slnwU] + ru, writes=[yU])
            P.dve(lambda e: e.tensor_tensor(out=y_, in0=y_, in1=a.gs[:, qb, :], op=ALU.mult), reads=[yU, a.gsU], writes=[yU])
            P.pe(lambda e: e.transpose(bank(g, b0)[:, 256:384], y_, ident[:]), reads=[yU, g.cU], writes=[g.bU[b0]])
            P.act(lambda e: e.copy(a.yT[:, hj, qs], bank(g, b0)[:, 256:384]), reads=[g.bU[b0]], writes=[a.yTU[hj]])

        M = len(groups)
        S_(groups[0])
        pending_fin = None
        for i in range(M):
            if i + 1 < M:
                S_(groups[i + 1])
            E_(groups[i])
            PV_(groups[i])
            if pending_fin is not None:
                FIN_(pending_fin)
                pending_fin = None
            qb, gi, kbs, buf = groups[i]
            if kbs[-1] == qb:
                pending_fin = qb
        if pending_fin is not None:
            FIN_(pending_fin)

    oc = [0]

    def outproj(hg):
        for m in range(KC):
            wi = oc[0] % 2
            oc[0] += 1
            c0 = (8 + hg * 4) * 128
            P.dma("pool", lambda e, m=m, wi=wi, c0=c0: e.dma_start(out=a.wo[wi].rearrange("p j c -> p (j c)"),
                                                                  in_=g.d["ev_w_out_t"][ei, m, :, c0:c0 + 512]),
                  f"a_wo{wi}", writes=[a.woU[wi]])
            for n in range(NT):
                sl = slice(n * 512, (n + 1) * 512)
                bk = n % 2
                for j in range(4):
                    P.pe(lambda e, bk=bk, j=j, sl=sl, wi=wi: e.matmul(bank(g, bk), lhsT=a.wo[wi][:, j, :], rhs=a.yT[:, j, sl],
                                                                      start=(j == 0), stop=(j == 3)),
                         reads=[a.woU[wi], a.yTU[j]], writes=[g.bU[bk]])
                P.dve(lambda e, bk=bk, m=m, sl=sl: e.tensor_tensor(out=g.hT[:, m, sl], in0=g.hT[:, m, sl], in1=bank(g, bk), op=ALU.add),
                      reads=[g.bU[bk], g.hU[m][n]], writes=[g.hU[m][n]])

    load_w(0)
    for h in range(8):
        if h + 1 < 8:
            load_w(h + 1)
        project(h)
        attend(h)
        if h % 4 == 3:
            outproj(h // 4)


_CACHE = {}


def kernel(**inputs):
    x = np.ascontiguousarray(np.asarray(inputs["x"], dtype=np.float32))
    Bsz, L, _ = x.shape
    n_cores = 8
    nseq = Bsz // n_cores
    key = (L, nseq)
    if key not in _CACHE:
        _CACHE[key] = build(L, nseq, (0, 1, 2, 3))
    nc, _ = _CACHE[key]
    common = host_layout(inputs)
    in_maps = []
    for cidx in range(n_cores):
        m = dict(common)
        m["x"] = x[cidx * nseq:(cidx + 1) * nseq]
        in_maps.append(m)
    res = run_bass_kernel_spmd(nc, in_maps, core_ids=list(range(n_cores)))
    out = np.concatenate([np.asarray(r["out"]) for r in res.results], axis=0)
    return out.astype(np.float32)
```

```python
import math, contextlib
import numpy as np
import concourse.bass as bass
import concourse.mybir as mybir
from concourse.bass_utils import run_bass_kernel_spmd
from concourse.alu_op_type import AluOpType as ALU

F32 = mybir.dt.float32
BF16 = mybir.dt.bfloat16
AF = mybir.ActivationFunctionType
AX = mybir.AxisListType

D = 1024
KC = 8
EPS = 1e-6
DEPTH = 4
HG_W = 2048
ARF_N = 7616
ARB_N = 35840


class Unit:
    __slots__ = ("name", "lw", "rd")

    def __init__(self, name):
        self.name = name
        self.lw = None
        self.rd = []


class _Rec:
    def __init__(self):
        self.call = None

    def __getattr__(self, name):
        def f(*args, **kw):
            assert self.call is None
            self.call = (name, args, kw)
            return None
        return f


class Prog:
    ENGS = ("pe", "act", "dve", "pool", "sp")

    def __init__(self, nc):
        self.nc = nc
        self.ops = []
        self.nunits = 0
        self.last_eng = {}
        self.last_key = {}

    def unit(self, name=None):
        self.nunits += 1
        return Unit(name or f"u{self.nunits}")

    def units(self, n, name="u"):
        return [self.unit(f"{name}{i}") for i in range(n)]

    def capture(self):
        self._cap = []
        return self._cap

    def end_capture(self):
        c, self._cap = self._cap, None
        return c

    def replay_merged(self, A, B):
        na, nb = len(A), len(B)
        ia = ib = 0
        while ia < na or ib < nb:
            if ib >= nb or (ia < na and ia * nb <= ib * na):
                self.op(*A[ia]); ia += 1
            else:
                self.op(*B[ib]); ib += 1

    def op(self, eng, fn, reads=(), writes=(), dma_key=None, extra_deps=()):
        if fn is not None and not isinstance(fn, tuple):
            rec = _Rec()
            fn(rec)
            assert rec.call is not None
            fn = rec.call
        if getattr(self, "_cap", None) is not None:
            self._cap.append((eng, fn, tuple(reads), tuple(writes), dma_key, tuple(extra_deps)))
            return None
        idx = len(self.ops)
        deps = set(extra_deps)
        for u in reads:
            if u.lw is not None:
                deps.add(u.lw)
        for u in writes:
            if u.lw is not None:
                deps.add(u.lw)
            deps.update(u.rd)
        for u in reads:
            u.rd.append(idx)
        for u in writes:
            u.lw = idx
            u.rd = []
        deps.discard(idx)
        self.ops.append(dict(eng=eng, fn=fn, deps=deps, dma_key=dma_key))
        if fn is not None:
            if dma_key is None:
                self.last_eng[eng] = idx
            else:
                self.last_key[dma_key] = idx
        return idx

    def pe(self, fn, reads=(), writes=()):
        return self.op("pe", fn, reads, writes)

    def act(self, fn, reads=(), writes=()):
        return self.op("act", fn, reads, writes)

    def dve(self, fn, reads=(), writes=()):
        return self.op("dve", fn, reads, writes)

    def pool(self, fn, reads=(), writes=()):
        return self.op("pool", fn, reads, writes)

    def dma(self, eng, fn, key, reads=(), writes=()):
        return self.op(eng, fn, reads, writes, dma_key=key)

    def barrier(self):
        deps = set(self.last_eng.values()) | set(self.last_key.values())
        for e in self.ENGS:
            self.op(e, None, extra_deps=deps)

    def emit(self, final_wait_ops=()):
        nc = self.nc
        ops = self.ops
        n = len(ops)

        def skip(od, o):
            return (od["eng"] == "pe" and o["eng"] == "pe" and od["dma_key"] is None
                    and o["dma_key"] is None and o["fn"] is not None)

        needed = [False] * n
        for i, o in enumerate(ops):
            for d in o["deps"]:
                if skip(ops[d], o):
                    continue
                needed[d] = True
        for d in final_wait_ops:
            needed[d] = True
        chan_count = {}
        ev = [None] * n
        for i, o in enumerate(ops):
            if o["fn"] is None:
                continue
            if o["dma_key"] is not None:
                ch = ("dma", o["dma_key"])
                chan_count[ch] = chan_count.get(ch, 0) + 16
                ev[i] = (ch, chan_count[ch])
            elif needed[i]:
                ch = ("eng", o["eng"])
                chan_count[ch] = chan_count.get(ch, 0) + 1
                ev[i] = (ch, chan_count[ch])
        chans = sorted(chan_count.keys(), key=str)
        self.n_sems = len(chans)
        sems = {}
        stack = contextlib.ExitStack()
        for ci, ch in enumerate(chans):
            sems[ch] = stack.enter_context(nc.semaphore(f"s{ci}"))
        known = {e: {} for e in self.ENGS}
        clock = [None] * n
        streams = {e: [] for e in self.ENGS}
        for i, o in enumerate(ops):
            e = o["eng"]
            kn = known[e]
            wd = {}
            for d in sorted(o["deps"]):
                od = ops[d]
                if skip(od, o):
                    continue
                ch, v = ev[d]
                if kn.get(ch, 0) >= v:
                    continue
                for c2, v2 in clock[d].items():
                    if kn.get(c2, 0) < v2:
                        kn[c2] = v2
                wd[ch] = max(wd.get(ch, 0), v)
            ck = dict(kn)
            if ev[i] is not None:
                ch, v = ev[i]
                ck[ch] = v
            clock[i] = ck
            streams[e].append((list(wd.items()), o["fn"], ev[i]))
        final = [ev[d] for d in final_wait_ops]
        for ch, tot in chan_count.items():
            if ch[0] == "dma":
                final.append((ch, tot))
        self.sems, self.streams, self.final, self._stack = sems, streams, final, stack

    def run_block(self):
        nc = self.nc
        sems, streams, final = self.sems, self.streams, self.final
        with nc.Block() as block:
            def mk(ename):
                def body(eng):
                    for waits, fn, e in streams[ename]:
                        for ch, v in waits:
                            eng.wait_ge(sems[ch], v)
                        if fn is None:
                            continue
                        ins = getattr(eng, fn[0])(*fn[1], **fn[2])
                        if e is not None:
                            ins.then_inc(sems[e[0]], 16 if e[0][0] == "dma" else 1)
                    if ename == "sp":
                        for ch, v in final:
                            eng.wait_ge(sems[ch], v)
                return body
            block.tensor(mk("pe"))
            block.scalar(mk("act"))
            block.vector(mk("dve"))
            block.gpsimd(mk("pool"))
            block.sync(mk("sp"))
        self._stack.close()


def _t5_bucket(rel):
    n = np.maximum(rel, 0)
    max_exact = 16
    large = max_exact + (np.log(np.maximum(n, 1).astype(np.float32) / max_exact)
                         / math.log(128 / max_exact) * (32 - max_exact)).astype(np.int32)
    large = np.minimum(large, 31)
    return np.where(n < max_exact, n, large)


def host_consts():
    s = np.arange(128)[:, None]
    t = np.arange(128)[None, :]
    c = {}
    c["ident"] = np.eye(128, dtype=np.float32)
    c["ones"] = np.ones((128, 128), np.float32)
    c["triC"] = ((s <= t).astype(np.float32) - (s <= 63).astype(np.float32))
    c["triU"] = (s > t).astype(np.float32)
    c["triI"] = (s <= t).astype(np.float32)
    sel = np.zeros((128, 2), np.float32)
    sel[:64, 0] = 1.0
    sel[:, 1] = 1.0
    c["sel"] = sel
    c["mg16"] = np.eye(16, dtype=np.float32)
    mq = np.zeros((128, 2), np.float32)
    mq[:64, 0] = 1.0
    mq[64:, 1] = 1.0
    c["maskq"] = mq
    return c


def host_layout(inp):
    f = lambda a: np.ascontiguousarray(np.asarray(a, dtype=np.float32))
    m = dict(host_consts())
    m["final_norm_w"] = f(inp["final_norm_w"])
    m["norm_w_cols"] = f(np.asarray(inp["norm_w"]).reshape(4, 8, 128).transpose(0, 2, 1))
    owin = np.asarray(inp["odd_w_in"])
    t = owin.reshape(2, 8, 128, 4, 16, 128).transpose(0, 4, 2, 1, 3, 5)
    m["odd_w_in_t"] = f(t).reshape(2, 16, 128, 8 * 512)
    m["odd_w_out"] = f(inp["odd_w_out"])
    m["hgrn_lower_bounds"] = f(inp["hgrn_lower_bounds"])
    m["hgrn_norm_w"] = f(inp["hgrn_norm_w"])
    ew = np.asarray(inp["even_w_in"]).reshape(2, 8, 128, 7184)
    z = ew[..., 0:1024]; xs = ew[..., 1024:2048]; Bm = ew[..., 2048:2560]; Cm = ew[..., 2560:3072]
    dt = ew[..., 3072:3088]
    q = ew[..., 3088:4112]; kk = ew[..., 4112:5136]; v = ew[..., 5136:6160]; gg = ew[..., 6160:7184]
    ssd = np.concatenate([z.reshape(2, 8, 128, 4, 256), xs.reshape(2, 8, 128, 4, 256),
                          Bm.reshape(2, 8, 128, 4, 128), Cm.reshape(2, 8, 128, 4, 128)], axis=-1)
    m["ev_w_ssd"] = f(ssd.transpose(0, 3, 2, 1, 4)).reshape(2, 4, 128, 8 * 768)
    m["ev_w_dt"] = f(dt.transpose(0, 2, 1, 3)).reshape(2, 128, 8 * 16)
    att = np.concatenate([q.reshape(2, 8, 128, 8, 128), kk.reshape(2, 8, 128, 8, 128),
                          v.reshape(2, 8, 128, 8, 128), gg.reshape(2, 8, 128, 8, 128)], axis=-1)
    m["ev_w_att"] = f(att.transpose(0, 3, 2, 1, 4)).reshape(2, 8, 128, 8 * 512)
    wo = np.asarray(inp["even_w_out"]).reshape(2, 16, 128, 8, 128)
    m["ev_w_out_t"] = f(wo.transpose(0, 3, 2, 1, 4)).reshape(2, 8, 128, 16 * 128)
    m["conv_w_cols"] = f(np.asarray(inp["conv_w"]).reshape(2, 4, 16, 128).transpose(0, 3, 2, 1)).reshape(2, 128, 64)
    m["conv_b_cols"] = f(np.asarray(inp["conv_b"]).reshape(2, 16, 128).transpose(0, 2, 1))
    for nm in ("dt_bias", "A_log", "D_skip", "lambda_q1", "lambda_k1", "lambda_q2", "lambda_k2", "subln_w"):
        m[nm] = f(inp[nm])
    m["ssd_norm_w_cols"] = f(np.asarray(inp["ssd_norm_w"]).reshape(2, 8, 128).transpose(0, 2, 1))
    rb = np.asarray(inp["rel_bias"], dtype=np.float32)
    kpos = np.arange(128)[:, None]
    qpos = np.arange(128)[None, :]
    bd = np.empty((128, 8, 2, 128), np.float32)
    for Dd in range(2):
        rel = qpos - kpos + 128 * Dd
        bidx = _t5_bucket(rel)
        g_ = rb[bidx]
        g_ = np.where((rel >= 0)[:, :, None], g_, np.float32(-30000.0))
        bd[:, :, Dd, :] = g_.transpose(0, 2, 1)
    m["rel_biasD"] = f(bd).reshape(128, 8 * 2 * 128)
    m["rel_b31"] = f(rb[31])
    return m


class NS:
    pass


class Carver:
    def __init__(self, g):
        self.g = g
        self.fo = 0
        self.bo = 0

    def f(self, n):
        ap = self.g.arf[:, self.fo:self.fo + n]
        self.fo += (n + 7) // 8 * 8
        assert self.fo <= ARF_N, ("ARF overflow", self.fo)
        return ap

    def b(self, n):
        ap = self.g.arb[:, self.bo:self.bo + n]
        self.bo += (n + 15) // 16 * 16
        assert self.bo <= ARB_N, ("ARB overflow", self.bo)
        return ap


def bank(g, i):
    return g.ps[:, i, :]


def build(L=2048, NSEQ=2, layers=(0, 1, 2, 3)):
    nc = bass.Bass("TRN2", target_bir_lowering=False)
    NT, NB = L // 512, L // 128
    g = NS()
    g.nc, g.L, g.NT, g.NB = nc, L, NT, NB
    dr = lambda name, shape, kind="ExternalInput": nc.dram_tensor(name, list(shape), F32, kind=kind).ap()
    g.x_d = dr("x", [NSEQ, L, D])
    g.out_d = dr("out", [NSEQ, L, D], "ExternalOutput")
    g.d = {}
    shapes = {
        "final_norm_w": [D], "norm_w_cols": [DEPTH, 128, KC],
        "ident": [128, 128], "ones": [128, 128], "triC": [128, 128], "triU": [128, 128], "triI": [128, 128],
        "sel": [128, 2], "mg16": [16, 16], "maskq": [128, 2],
        "odd_w_in_t": [2, 16, 128, KC * 512], "odd_w_out": [2, HG_W, D], "hgrn_lower_bounds": [DEPTH, HG_W],
        "hgrn_norm_w": [2, 128],
        "ev_w_ssd": [2, 4, 128, 8 * 768], "ev_w_dt": [2, 128, 8 * 16], "ev_w_att": [2, 8, 128, 8 * 512],
        "ev_w_out_t": [2, 8, 128, 16 * 128], "conv_w_cols": [2, 128, 64], "conv_b_cols": [2, 128, 16],
        "dt_bias": [2, 16], "A_log": [2, 16], "D_skip": [2, 16], "lambda_q1": [2, 64], "lambda_k1": [2, 64],
        "lambda_q2": [2, 64], "lambda_k2": [2, 64], "subln_w": [2, 128], "ssd_norm_w_cols": [2, 128, 8],
        "rel_biasD": [128, 8 * 2 * 128], "rel_b31": [8],
    }
    for nm, shp in shapes.items():
        g.d[nm] = dr(nm, shp)
    g.in_names = ["x"] + list(shapes.keys())

    es = contextlib.ExitStack()
    sb = lambda name, shape, dt=F32: es.enter_context(nc.sbuf_tensor(name, list(shape), dt))
    P = Prog(nc)
    g.P = P
    g.hT = sb("hT", [128, KC, L]); g.hU = [[P.unit(f"h{k}_{n}") for n in range(NT)] for k in range(KC)]
    g.uT = sb("uT", [128, KC, L], BF16); g.uU = [P.unit(f"u{n}") for n in range(NT)]
    g.cst = {}
    g.cU = P.unit("consts")
    for nm in ("ident", "ones", "triI"):
        g.cst[nm] = sb("c_" + nm, [128, 128])
    g.cst["sel"] = sb("c_sel", [128, 2])
    g.cst["maskq"] = sb("c_maskq", [128, 2])
    g.cst["mg16"] = sb("c_mg16", [16, 16])
    g.nwc = sb("nwc", [128, DEPTH, KC])
    g.stat = sb("stat", [128, 16]); g.statU = P.unit("stat")
    g.sq = [sb(f"sq{i}", [128, 512]) for i in range(2)]; g.sqU = P.units(2, "sq")
    g.rstd_t = sb("rstd_t", [128, 512]); g.rstdU = P.unit("rstd_t")
    g.arf = sb("arf", [128, ARF_N])
    g.arb = sb("arb", [128, ARB_N], BF16)
    g.ps = es.enter_context(nc.psum_tensor("ps", [128, 8, 512], F32))
    g.bU = P.units(8, "bank")

    for nm in ("ident", "ones", "triI", "sel", "maskq", "mg16"):
        P.dma("sp", lambda e, nm=nm: e.dma_start(out=g.cst[nm][:], in_=g.d[nm]), "c_" + nm, writes=[g.cU])
    P.dma("sp", lambda e: e.dma_start(out=g.nwc[:], in_=g.d["norm_w_cols"].rearrange("l p k -> p l k")), "c_nwc", writes=[g.cU])

    out_ops = []
    for s in range(NSEQ):
        P.barrier()
        load_x(g, s)
        P.barrier()
        for li in layers:
            rms_to_uT(g, li)
            if li % 2 == 1:
                odd_layer(g, li)
            else:
                even_ssd(g, li)
                P.barrier()
                even_attn(g, li)
            P.barrier()
        out_ops += final_norm_store(g, s)
    P.emit(final_wait_ops=out_ops[-4:])
    P.run_block()
    es.close()
    return nc, P


def load_x(g, s):
    P, NT = g.P, g.NT
    A = Carver(g)
    xst = A.f(4096).rearrange("p (b d) -> p b d", b=4)
    xU = P.unit("xst")
    ident = g.cst["ident"]
    for n in range(NT):
        src = g.x_d[s, n * 512:(n + 1) * 512, :].rearrange("(b p) d -> p b d", p=128)
        P.dma("sp", lambda e, src=src: e.dma_start(out=xst, in_=src), "xst", writes=[xU])
        for k in range(KC):
            bk = k % 8
            for b in range(4):
                P.pe(lambda e, bk=bk, b=b, k=k: e.transpose(
                    bank(g, bk)[:, b * 128:(b + 1) * 128], xst[:, b, k * 128:(k + 1) * 128], ident[:]),
                    reads=[xU, g.cU], writes=[g.bU[bk]])
            if k % 2 == 0:
                P.dve(lambda e, bk=bk, k=k, n=n: e.tensor_copy(g.hT[:, k, n * 512:(n + 1) * 512], bank(g, bk)),
                      reads=[g.bU[bk]], writes=[g.hU[k][n]])
            else:
                P.act(lambda e, bk=bk, k=k, n=n: e.copy(g.hT[:, k, n * 512:(n + 1) * 512], bank(g, bk)),
                      reads=[g.bU[bk]], writes=[g.hU[k][n]])


def final_norm_store(g, s):
    P, NB = g.P, g.NB
    A = Carver(g)
    fnw = A.f(D); fnwU = P.unit("fnw")
    ost = [A.f(D) for _ in range(2)]; ostU = P.units(2, "ost")
    junk = A.f(1024); junkU = P.unit("junk")
    ident = g.cst["ident"]
    P.dma("sp", lambda e: e.dma_start(out=fnw, in_=g.d["final_norm_w"].partition_broadcast(128)), "fnw", writes=[fnwU])
    outs = []
    for b in range(NB):
        n = b // 4
        oi = b % 2
        for k in range(KC):
            bk = k // 4
            P.pe(lambda e, bk=bk, k=k, b=b: e.transpose(
                bank(g, bk)[:, (k % 4) * 128:(k % 4 + 1) * 128], g.hT[:, k, b * 128:(b + 1) * 128], ident[:]),
                reads=[g.hU[k][n], g.cU], writes=[g.bU[bk]])
        for half in range(2):
            P.act(lambda e, half=half: e.activation(
                out=junk[:, half * 512:(half + 1) * 512], in_=bank(g, half), func=AF.Square,
                accum_out=g.stat[:, half:half + 1]),
                reads=[g.bU[half]], writes=[junkU, g.statU])
        P.dve(lambda e: e.tensor_tensor(out=g.stat[:, 2:3], in0=g.stat[:, 0:1], in1=g.stat[:, 1:2], op=ALU.add),
              reads=[g.statU], writes=[g.statU])
        P.act(lambda e: e.activation(out=g.stat[:, 3:4], in_=g.stat[:, 2:3], func=AF.Ln, scale=1.0 / D, bias=EPS),
              reads=[g.statU], writes=[g.statU])
        P.act(lambda e: e.activation(out=g.stat[:, 4:5], in_=g.stat[:, 3:4], func=AF.Exp, scale=-0.5),
              reads=[g.statU], writes=[g.statU])
        for half in range(2):
            P.dve(lambda e, half=half, oi=oi: e.scalar_tensor_tensor(
                out=ost[oi][:, half * 512:(half + 1) * 512], in0=bank(g, half), scalar=g.stat[:, 4:5],
                in1=fnw[:, half * 512:(half + 1) * 512], op0=ALU.mult, op1=ALU.mult),
                reads=[g.bU[half], g.statU, fnwU], writes=[ostU[oi]])
        o = P.dma("sp", lambda e, oi=oi, s=s, b=b: e.dma_start(out=g.out_d[s, b * 128:(b + 1) * 128, :], in_=ost[oi]),
                  f"ost{oi}", reads=[ostU[oi]])
        outs.append(o)
    return outs


def rms_rstd_tile(g, src_fn, reads_fn, nchunks, dim):
    P = g.P
    ones = g.cst["ones"]
    for k in range(nchunks):
        i = k % 2
        P.act(lambda e, i=i, k=k: e.activation(out=g.sq[i][:], in_=src_fn(k), func=AF.Square),
              reads=reads_fn(k), writes=[g.sqU[i]])
        P.pe(lambda e, i=i, k=k: e.matmul(bank(g, 7), lhsT=ones[:], rhs=g.sq[i][:], start=(k == 0), stop=(k == nchunks - 1)),
             reads=[g.sqU[i], g.cU], writes=[g.bU[7]])
    P.act(lambda e: e.activation(out=g.rstd_t[:], in_=bank(g, 7), func=AF.Ln, scale=1.0 / dim, bias=EPS),
          reads=[g.bU[7]], writes=[g.rstdU])
    P.act(lambda e: e.activation(out=g.rstd_t[:], in_=g.rstd_t[:], func=AF.Exp, scale=-0.5),
          reads=[g.rstdU], writes=[g.rstdU])


def rms_to_uT(g, li):
    P, NT = g.P, g.NT
    for n in range(NT):
        sl = slice(n * 512, (n + 1) * 512)
        rms_rstd_tile(g, lambda k, sl=sl: g.hT[:, k, sl], lambda k, n=n: [g.hU[k][n]], KC, D)
        for k in range(KC):
            P.dve(lambda e, k=k, sl=sl: e.scalar_tensor_tensor(
                out=g.uT[:, k, sl], in0=g.hT[:, k, sl], scalar=g.nwc[:, li, k:k + 1], in1=g.rstd_t[:],
                op0=ALU.mult, op1=ALU.mult),
                reads=[g.hU[k][n], g.rstdU, g.cU], writes=[g.uU[n]])


def odd_layer(g, li):
    P, L, NB, NT = g.P, g.L, g.NB, g.NT
    oi = li // 2
    A = Carver(g)
    o = NS()
    c = g.cst
    f3 = lambda: A.f(512).rearrange("p (b d) -> p b d", b=4)
    o.logf = f3(); o.logfU = P.unit()
    o.tA = f3(); o.tAU = P.unit()
    o.kk = f3(); o.kkU = P.unit()
    o.qs = f3(); o.qsU = P.unit()
    o.e13 = A.f(1024).rearrange("p (t b d) -> p t b d", t=2, b=4); o.e13U = P.unit()
    o.e2 = f3(); o.e2U = P.unit()
    o.qt = o.qs; o.qtU = o.qsU
    o.kt = o.e2; o.ktU = o.e2U
    o.lbr = f3(); o.lbrU = P.unit()
    o.lbh = A.f(128); o.omlh = A.f(128); o.den = A.f(128); o.lbU = P.unit()
    o.eb = [A.f(NB * 2).rearrange("p (b t) -> p b t", t=2) for _ in range(2)]
    o.S = A.f(128); o.SU = P.unit()
    o.junk2 = A.f(128); o.junk2U = P.unit()
    o.oall = A.f(NB * 128).rearrange("p (b d) -> p b d", d=128); o.oallU = P.unit()
    o.ssall = A.f(NB); o.rsall = A.f(NB); o.ssU = P.unit()
    o.hnw = A.f(128); o.hnwU = P.unit()
    o.triC = A.f(128); o.triU = A.f(128); o.triUU = P.unit()
    o.bst = A.f(8); o.bstU = P.unit()
    o.w = [A.b(KC * 512).rearrange("p (k c) -> p k c", k=KC) for _ in range(2)]; o.wU = P.units(2)
    o.wout = A.b(2 * D).rearrange("p (j m) -> p j m", j=2); o.woutU = P.units(2)
    o.qT = [A.b(L) for _ in range(2)]
    o.kT = [A.b(L) for _ in range(2)]
    hb3 = lambda: A.b(NB * 128).rearrange("p (b d) -> p b d", d=128)
    o.kh = [hb3() for _ in range(2)]
    o.v = [hb3() for _ in range(2)]
    o.gs = [hb3() for _ in range(2)]
    o.hbU = [[[P.unit() for _ in range(NT)] for _ in range(6)] for _ in range(2)]
    o.attm = [A.b(128) for _ in range(2)]; o.attmU = P.units(2)
    o.Sb2 = [A.b(128) for _ in range(2)]; o.SbU2 = P.units(2)
    o.yT = A.b(2 * L).rearrange("p (j t) -> p j t", j=2); o.yTU = P.units(2)
    QT, KT, KH, VV, GS, EB = range(6)

    P.dma("sp", lambda e: e.dma_start(out=o.hnw, in_=g.d["hgrn_norm_w"][oi].partition_broadcast(128)), "o_hnw", writes=[o.hnwU])
    P.dma("sp", lambda e: e.dma_start(out=o.triC, in_=g.d["triC"]), "o_tri", writes=[o.triUU])
    P.dma("sp", lambda e: e.dma_start(out=o.triU, in_=g.d["triU"]), "o_tri", writes=[o.triUU])
    for i in range(2):
        P.dve(lambda e, i=i: e.memset(o.attm[i], 0.0), writes=[o.attmU[i]])

    def load_w(h):
        i = h % 2
        P.dma("pool", lambda e: e.dma_start(out=o.w[i].rearrange("p k c -> p (k c)"), in_=g.d["odd_w_in_t"][oi, h]),
              f"o_w{i}", writes=[o.wU[i]])

    def head_lb(h):
        hs = slice(h * 128, (h + 1) * 128)
        P.dma("sp", lambda e: e.dma_start(out=o.lbr, in_=g.d["hgrn_lower_bounds"][:, hs].partition_broadcast(128)),
              "o_lbr", writes=[o.lbrU])
        P.act(lambda e: e.activation(out=o.lbr, in_=o.lbr, func=AF.Exp), reads=[o.lbrU], writes=[o.lbrU])
        P.dve(lambda e: e.tensor_tensor(out=o.den, in0=o.lbr[:, 0, :], in1=o.lbr[:, 1, :], op=ALU.add), reads=[o.lbrU], writes=[o.lbU])
        P.dve(lambda e: e.tensor_tensor(out=o.den, in0=o.den, in1=o.lbr[:, 2, :], op=ALU.add), reads=[o.lbrU, o.lbU], writes=[o.lbU])
        P.dve(lambda e: e.tensor_tensor(out=o.den, in0=o.den, in1=o.lbr[:, 3, :], op=ALU.add), reads=[o.lbrU, o.lbU], writes=[o.lbU])
        P.dve(lambda e: e.reciprocal(o.den, o.den), reads=[o.lbU], writes=[o.lbU])
        if li == 1:
            P.dve(lambda e: e.tensor_tensor(out=o.lbh, in0=o.lbr[:, 1, :], in1=o.den, op=ALU.mult), reads=[o.lbrU, o.lbU], writes=[o.lbU])
        else:
            P.dve(lambda e: e.tensor_tensor(out=o.lbh, in0=o.lbr[:, 1, :], in1=o.lbr[:, 2, :], op=ALU.add), reads=[o.lbrU, o.lbU], writes=[o.lbU])
            for j in range(3, li + 1):
                P.dve(lambda e, j=j: e.tensor_tensor(out=o.lbh, in0=o.lbh, in1=o.lbr[:, j, :], op=ALU.add), reads=[o.lbrU, o.lbU], writes=[o.lbU])
            P.dve(lambda e: e.tensor_tensor(out=o.lbh, in0=o.lbh, in1=o.den, op=ALU.mult), reads=[o.lbU], writes=[o.lbU])
        P.dve(lambda e: e.tensor_scalar(out=o.omlh, in0=o.lbh, scalar1=-1.0, scalar2=1.0, op0=ALU.mult, op1=ALU.add),
              reads=[o.lbU], writes=[o.lbU])

    def stageA(h, n):
        hb = h % 2
        wi = h % 2
        U = o.hbU[hb]
        for b in range(4):
            tb = n * 4 + b
            for k in range(KC):
                P.pe(lambda e, b=b, tb=tb, k=k: e.matmul(bank(g, b), lhsT=g.uT[:, k, tb * 128:(tb + 1) * 128], rhs=o.w[wi][:, k, :],
                                                         start=(k == 0), stop=(k == KC - 1)),
                     reads=[g.uU[n], o.wU[wi]], writes=[g.bU[b]])
        pj = g.ps[:, 0:4, :]
        pb = [g.bU[0], g.bU[1], g.bU[2], g.bU[3]]
        bc4 = lambda t: t.unsqueeze(1).to_broadcast([128, 4, 128])
        P.act(lambda e: e.activation(out=o.tA, in_=pj[:, :, 128:256], func=AF.Sigmoid), reads=pb, writes=[o.tAU])
        P.act(lambda e: e.activation(out=o.qs, in_=pj[:, :, 0:128], func=AF.Silu), reads=pb, writes=[o.qsU])
        P.act(lambda e: e.activation(out=o.gs[hb][:, n * 4:(n + 1) * 4, :], in_=pj[:, :, 384:512], func=AF.Silu),
              reads=pb, writes=[U[GS][n]])
        P.act(lambda e: e.copy(o.v[hb][:, n * 4:(n + 1) * 4, :], pj[:, :, 256:384]), reads=pb, writes=[U[VV][n]])
        P.dve(lambda e: e.tensor_tensor(out=o.tA, in0=o.tA, in1=bc4(o.omlh), op=ALU.mult), reads=[o.tAU, o.lbU], writes=[o.tAU])
        P.dve(lambda e: e.tensor_tensor(out=o.tA, in0=o.tA, in1=bc4(o.lbh), op=ALU.add), reads=[o.tAU, o.lbU], writes=[o.tAU])
        P.act(lambda e: e.activation(out=o.logf, in_=o.tA, func=AF.Ln), reads=[o.tAU], writes=[o.logfU])
        P.pool(lambda e: e.tensor_scalar(out=o.kk, in0=o.tA, scalar1=-1.0, scalar2=1.0, op0=ALU.mult, op1=ALU.add),
               reads=[o.tAU], writes=[o.kkU])
        for b in range(4):
            P.pe(lambda e, b=b: e.matmul(bank(g, 0)[:, b * 128:(b + 1) * 128], lhsT=o.triC, rhs=o.logf[:, b, :], start=True, stop=True),
                 reads=[o.logfU, o.triUU], writes=[g.bU[0]])
        for b in range(4):
            P.pe(lambda e, b=b: e.matmul(bank(g, 1)[:, b * 128:(b + 1) * 128], lhsT=o.triU, rhs=o.logf[:, b, :], start=True, stop=True),
                 reads=[o.logfU, o.triUU], writes=[g.bU[1]])
        for b in range(4):
            P.pe(lambda e, b=b: e.matmul(bank(g, 2)[:, b * 2:b * 2 + 2], lhsT=o.logf[:, b, :], rhs=c["sel"][:], start=True, stop=True),
                 reads=[o.logfU, g.cU], writes=[g.bU[2]])
        P.act(lambda e: e.activation(out=o.eb[hb][:, n * 4:(n + 1) * 4, :], in_=bank(g, 2)[:, 0:8].rearrange("p (b t) -> p b t", t=2), func=AF.Exp),
              reads=[g.bU[2]], writes=[U[EB][n]])
        P.act(lambda e: e.activation(out=o.e13, in_=g.ps[:, 0:2, :].rearrange("p t (b d) -> p t b d", b=4), func=AF.Exp),
              reads=[g.bU[0], g.bU[1]], writes=[o.e13U])
        P.act(lambda e: e.activation(out=o.e2, in_=bank(g, 0).rearrange("p (b d) -> p b d", b=4), func=AF.Exp, scale=-1.0),
              reads=[g.bU[0]], writes=[o.e2U])
        P.dve(lambda e: e.tensor_tensor(out=o.qs, in0=o.qs, in1=o.e13[:, 0], op=ALU.mult), reads=[o.qsU, o.e13U], writes=[o.qsU])
        P.pool(lambda e: e.tensor_tensor(out=o.e2, in0=o.kk, in1=o.e2, op=ALU.mult), reads=[o.kkU, o.e2U], writes=[o.e2U])
        P.dve(lambda e: e.tensor_tensor(out=o.kh[hb][:, n * 4:(n + 1) * 4, :], in0=o.kk, in1=o.e13[:, 1], op=ALU.mult),
              reads=[o.kkU, o.e13U], writes=[U[KH][n]])
        idf = c["ident"]
        for b in range(4):
            P.pe(lambda e, b=b: e.transpose(bank(g, 3)[:, b * 128:(b + 1) * 128], o.qt[:, b, :], idf[:]),
                 reads=[o.qtU, g.cU], writes=[g.bU[3]])
        for b in range(4):
            P.pe(lambda e, b=b: e.transpose(bank(g, 2)[:, b * 128:(b + 1) * 128], o.kt[:, b, :], idf[:]),
                 reads=[o.ktU, g.cU], writes=[g.bU[2]])
        P.act(lambda e: e.copy(o.qT[hb][:, n * 512:(n + 1) * 512], bank(g, 3)), reads=[g.bU[3]], writes=[U[QT][n]])
        P.act(lambda e: e.copy(o.kT[hb][:, n * 512:(n + 1) * 512], bank(g, 2)), reads=[g.bU[2]], writes=[U[KT][n]])

    def stageB(h):
        hb = h % 2
        U = o.hbU[hb]

        def att(tb):
            n = tb // 4
            ts = slice(tb * 128, (tb + 1) * 128)
            bk = 5 + 2 * (tb % 2)
            P.pe(lambda e: e.matmul(bank(g, bk)[:, 64:128], lhsT=o.kT[hb][:, ts],
                                    rhs=o.qT[hb][:, tb * 128 + 64:(tb + 1) * 128], start=True, stop=True),
                 reads=[U[QT][n], U[KT][n]], writes=[g.bU[bk]])
            P.pe(lambda e: e.matmul(bank(g, bk)[0:64, 0:64], lhsT=o.kT[hb][:, tb * 128:tb * 128 + 64],
                                    rhs=o.qT[hb][:, tb * 128:tb * 128 + 64], start=True, stop=True),
                 reads=[U[QT][n], U[KT][n]], writes=[g.bU[bk]])

        def mask(tb):
            ai = tb % 2
            bk = 5 + 2 * (tb % 2)
            P.dve(lambda e: e.tensor_tensor(out=o.attm[ai][:, 64:128], in0=bank(g, bk)[:, 64:128], in1=c["triI"][:, 64:128], op=ALU.mult),
                  reads=[g.bU[bk], g.cU], writes=[o.attmU[ai]])
            P.dve(lambda e: e.tensor_tensor(out=o.attm[ai][0:64, 0:64], in0=bank(g, bk)[0:64, 0:64], in1=c["triI"][0:64, 0:64], op=ALU.mult),
                  reads=[g.bU[bk], g.cU], writes=[o.attmU[ai]])

        att(0)
        mask(0)
        for tb in range(NB):
            n = tb // 4
            ts = slice(tb * 128, (tb + 1) * 128)
            ai = tb % 2
            si = tb % 2
            P.pe(lambda e: e.matmul(bank(g, 6)[:, 128:256], lhsT=o.kh[hb][:, tb, :], rhs=o.v[hb][:, tb, :], start=True, stop=True),
                 reads=[U[KH][n], U[VV][n]], writes=[g.bU[6]])
            if tb + 1 < NB:
                att(tb + 1)
            P.pe(lambda e: e.matmul(bank(g, 4)[:, 0:128], lhsT=o.attm[ai], rhs=o.v[hb][:, tb, :], start=True, stop=(tb == 0)),
                 reads=[o.attmU[ai], U[VV][n]], writes=[g.bU[4]])
            if tb > 0:
                P.pe(lambda e: e.matmul(bank(g, 4)[:, 0:128], lhsT=o.qT[hb][:, ts], rhs=o.Sb2[si], start=False, stop=True),
                     reads=[o.SbU2[si], U[QT][n]], writes=[g.bU[4]])
            if tb == 0:
                P.dve(lambda e: e.tensor_copy(o.S, bank(g, 6)[:, 128:256]), reads=[g.bU[6]], writes=[o.SU])
            else:
                P.dve(lambda e: e.scalar_tensor_tensor(out=o.S, in0=o.S, scalar=o.eb[hb][:, tb, 1:2], in1=bank(g, 6)[:, 128:256],
                                                       op0=ALU.mult, op1=ALU.add),
                      reads=[o.SU, g.bU[6], U[EB][n]], writes=[o.SU])
            if tb + 1 < NB:
                nn = (tb + 1) // 4
                sn = (tb + 1) % 2
                P.dve(lambda e: e.tensor_scalar(out=o.Sb2[sn], in0=o.S, scalar1=o.eb[hb][:, tb + 1, 0:1], scalar2=None, op0=ALU.mult),
                      reads=[o.SU, U[EB][nn]], writes=[o.SbU2[sn]])
                mask(tb + 1)
            P.act(lambda e: e.copy(o.oall[:, tb, :], bank(g, 4)[:, 0:128]), reads=[g.bU[4]], writes=[o.oallU])
            P.act(lambda e: e.activation(out=o.junk2, in_=bank(g, 4)[:, 0:128], func=AF.Square, accum_out=o.ssall[:, tb:tb + 1]),
                  reads=[g.bU[4]], writes=[o.junk2U, o.ssU])

    def stageC(h):
        hb = h % 2
        U = o.hbU[hb]
        hj = h % 2
        P.act(lambda e: e.activation(out=o.rsall, in_=o.ssall, func=AF.Ln, scale=1.0 / 128, bias=EPS), reads=[o.ssU], writes=[o.ssU])
        P.act(lambda e: e.activation(out=o.rsall, in_=o.rsall, func=AF.Exp, scale=-0.5), reads=[o.ssU], writes=[o.ssU])
        P.dve(lambda e: e.tensor_tensor(out=o.oall, in0=o.oall, in1=o.rsall.unsqueeze(2).to_broadcast([128, NB, 128]), op=ALU.mult),
              reads=[o.oallU, o.ssU], writes=[o.oallU])
        P.dve(lambda e: e.tensor_tensor(out=o.oall, in0=o.oall, in1=o.hnw.unsqueeze(1).to_broadcast([128, NB, 128]), op=ALU.mult),
              reads=[o.oallU, o.hnwU], writes=[o.oallU])
        P.dve(lambda e: e.tensor_tensor(out=o.oall, in0=o.oall, in1=o.gs[hb], op=ALU.mult),
              reads=[o.oallU] + [U[GS][n] for n in range(NT)], writes=[o.oallU])
        for n in range(NT):
            bk = 4 + (n % 4)
            for b in range(4):
                P.pe(lambda e, bk=bk, b=b, n=n: e.transpose(bank(g, bk)[:, b * 128:(b + 1) * 128], o.oall[:, n * 4 + b, :], c["ident"][:]),
                     reads=[o.oallU, g.cU], writes=[g.bU[bk]])
            P.act(lambda e, bk=bk, n=n: e.copy(o.yT[:, hj, n * 512:(n + 1) * 512], bank(g, bk)), reads=[g.bU[bk]], writes=[o.yTU[hj]])

    def outproj(hp):
        for j in range(2):
            src = g.d["odd_w_out"][oi, (hp * 2 + j) * 128:(hp * 2 + j + 1) * 128, :]
            P.dma("pool", lambda e, j=j, src=src: e.dma_start(out=o.wout[:, j, :], in_=src), f"o_wout{j}", writes=[o.woutU[j]])
        cnt = 0
        for m in range(KC):
            for n in range(NT):
                bk = 4 + cnt % 4
                cnt += 1
                for j in range(2):
                    P.pe(lambda e, bk=bk, m=m, n=n, j=j: e.matmul(bank(g, bk), lhsT=o.wout[:, j, m * 128:(m + 1) * 128],
                                                                     rhs=o.yT[:, j, n * 512:(n + 1) * 512], start=(j == 0), stop=(j == 1)),
                         reads=[o.woutU[j], o.yTU[j]], writes=[g.bU[bk]])
                P.dve(lambda e, bk=bk, m=m, n=n: e.tensor_tensor(out=g.hT[:, m, n * 512:(n + 1) * 512], in0=g.hT[:, m, n * 512:(n + 1) * 512],
                                                                   in1=bank(g, bk), op=ALU.add),
                      reads=[g.bU[bk], g.hU[m][n]], writes=[g.hU[m][n]])

    load_w(0)
    load_w(1)
    head_lb(0)
    for n in range(NT):
        stageA(0, n)
    for h in range(16):
        P.capture()
        stageB(h)
        stageC(h)
        if h % 2 == 1:
            outproj(h // 2)
        LB = P.end_capture()
        P.capture()
        if h + 1 < 16:
            if h + 2 < 16:
                load_w(h + 2)
            head_lb(h + 1)
            for n in range(NT):
                stageA(h + 1, n)
        LA = P.end_capture()
        P.replay_merged(LA, LB)


def even_ssd(g, li):
    P, L, NB, NT = g.P, g.L, g.NB, g.NT
    ei = li // 2
    c = g.cst
    A = Carver(g)
    s = NS()
    HB = NB * 16
    s.xpre = A.f(515); s.xpreU = P.unit()
    s.cacc = A.f(512); s.caccU = P.unit()
    v3 = lambda ap: ap.rearrange("p (b h) -> p b h", h=16)
    s.dt = A.f(HB); s.atok = A.f(HB); s.acs = A.f(HB); s.eacs = A.f(HB); s.dtd = A.f(HB); s.edl = A.f(HB)
    s.dtU = P.unit()
    s.cw = A.f(64); s.cb = A.f(16); s.dtb = A.f(16); s.Abc = A.f(16); s.Dsk = A.f(16); s.snw = A.f(8)
    s.smallU = P.unit()
    s.S = A.f(256); s.SU = P.unit()
    scan_off = A.fo
    s.Rbd = A.f(512); s.RbdU = P.unit()
    D2 = lambda n: ([A.f(n) for _ in range(2)], P.units(2))
    s.acsTb2, s.acsTbU2 = D2(128)
    s.CBm2, s.CBmU2 = D2(128)
    s.Dm2, s.DmU2 = D2(512)
    s.E2, s.EU2 = D2(512)
    s.t12, s.t1U2 = D2(256)
    s.t22, s.t2U2 = D2(256)
    s.ytmp2, s.ytmpU2 = D2(256)
    s.rstd_all = g.arf[:, scan_off:scan_off + L]; s.rstdallU = P.unit()
    assert scan_off + L <= ARF_N
    s.w = A.b(KC * 768).rearrange("p (k c) -> p k c", k=KC); s.wU = P.unit()
    s.wdt = A.b(KC * 16).rearrange("p (k c) -> p k c", k=KC); s.wdtU = P.unit()
    s.BT = A.b(L); s.CT = A.b(L); s.BCU = [P.unit() for _ in range(NT)]
    s.xtok = A.b(NB * 256).rearrange("p (b c) -> p b c", c=256); s.xtokU = [P.unit() for _ in range(NT)]
    s.Btok = A.b(NB * 128).rearrange("p (b c) -> p b c", c=128); s.BtokU = [P.unit() for _ in range(NT)]
    scan_bo = A.bo
    s.zs2 = [A.b(256) for _ in range(2)]; s.zsU2 = P.units(2)
    s.sc2 = [A.b(512).rearrange("p (h l) -> p h l", h=4) for _ in range(2)]; s.scU2 = P.units(2)
    s.Xdt2 = [A.b(256) for _ in range(2)]; s.XdtU2 = P.units(2)
    s.XB2 = [A.b(256) for _ in range(2)]; s.XBU2 = P.units(2)
    s.Sbf = A.b(256); s.SbfU = P.unit()
    s.yTa = A.b(8 * L).rearrange("p (c t) -> p c t", c=8); s.yTaU = [[P.unit() for _ in range(NT)] for _ in range(8)]
    s.wo2 = [g.arb[:, scan_bo + i * 1024:scan_bo + (i + 1) * 1024].rearrange("p (c j) -> p c j", c=8) for i in range(2)]
    s.woU2 = P.units(2)
    assert scan_bo + 2048 <= A.bo
    s.tmp2 = [s.cacc, s.xpre[:, 0:512]]; s.tmpU2 = [s.caccU, s.xpreU]
    ident = c["ident"]

    sm = [s.smallU]
    P.dma("sp", lambda e: e.dma_start(out=s.cw, in_=g.d["conv_w_cols"][ei]), "s_small", writes=sm)
    P.dma("sp", lambda e: e.dma_start(out=s.cb, in_=g.d["conv_b_cols"][ei]), "s_small", writes=sm)
    P.dma("sp", lambda e: e.dma_start(out=s.dtb, in_=g.d["dt_bias"][ei].partition_broadcast(128)), "s_small", writes=sm)
    P.dma("sp", lambda e: e.dma_start(out=s.Abc, in_=g.d["A_log"][ei].partition_broadcast(128)), "s_small", writes=sm)
    P.dma("sp", lambda e: e.dma_start(out=s.Dsk, in_=g.d["D_skip"][ei].partition_broadcast(128)), "s_small", writes=sm)
    P.dma("sp", lambda e: e.dma_start(out=s.snw, in_=g.d["ssd_norm_w_cols"][ei]), "s_small", writes=sm)
    P.act(lambda e: e.activation(out=s.Abc, in_=s.Abc, func=AF.Exp), reads=sm, writes=sm)
    P.dve(lambda e: e.tensor_scalar(out=s.Abc, in0=s.Abc, scalar1=-1.0, scalar2=None, op0=ALU.mult), reads=sm, writes=sm)
    P.dma("pool", lambda e: e.dma_start(out=s.wdt.rearrange("p k c -> p (k c)"), in_=g.d["ev_w_dt"][ei]), "s_wdt", writes=[s.wdtU])

    for b in range(NB):
        for k in range(KC):
            P.pe(lambda e, b=b, k=k: e.matmul(bank(g, 0)[:, b * 16:(b + 1) * 16], lhsT=g.uT[:, k, b * 128:(b + 1) * 128],
                                              rhs=s.wdt[:, k, :], start=(k == 0), stop=(k == KC - 1)),
                 reads=[g.uU[b // 4], s.wdtU], writes=[g.bU[0]])
    bc_h = lambda t: t.unsqueeze(1).to_broadcast([128, NB, 16])
    du = [s.dtU]
    P.dve(lambda e: e.tensor_tensor(out=v3(s.dt), in0=v3(bank(g, 0)[:, 0:HB]), in1=bc_h(s.dtb), op=ALU.add),
          reads=[g.bU[0]] + sm, writes=du)
    P.act(lambda e: e.activation(out=s.dt, in_=s.dt, func=AF.Exp), reads=du, writes=du)
    P.act(lambda e: e.activation(out=s.dt, in_=s.dt, func=AF.Ln, bias=1.0), reads=du, writes=du)
    P.dve(lambda e: e.tensor_tensor(out=v3(s.atok), in0=v3(s.dt), in1=bc_h(s.Abc), op=ALU.mult), reads=du + sm, writes=du)
    for b in range(NB):
        P.pe(lambda e, b=b: e.matmul(bank(g, 1)[:, b * 16:(b + 1) * 16], lhsT=c["triI"][:], rhs=s.atok[:, b * 16:(b + 1) * 16],
                                     start=True, stop=True), reads=du + [g.cU], writes=[g.bU[1]])
    for b in range(NB):
        P.pe(lambda e, b=b: e.matmul(bank(g, 2)[:, b * 16:(b + 1) * 16], lhsT=c["ones"][:], rhs=s.atok[:, b * 16:(b + 1) * 16],
                                     start=True, stop=True), reads=du + [g.cU], writes=[g.bU[2]])
    P.dve(lambda e: e.tensor_copy(s.acs, bank(g, 1)[:, 0:HB]), reads=[g.bU[1]], writes=du)
    P.act(lambda e: e.activation(out=s.eacs, in_=s.acs, func=AF.Exp), reads=du, writes=du)
    P.dve(lambda e: e.tensor_copy(s.edl, bank(g, 2)[:, 0:HB]), reads=[g.bU[2]], writes=du)
    P.dve(lambda e: e.tensor_tensor(out=s.dtd, in0=s.edl, in1=s.acs, op=ALU.subtract), reads=du, writes=du)
    P.act(lambda e: e.activation(out=s.dtd, in_=s.dtd, func=AF.Exp), reads=du, writes=du)
    P.dve(lambda e: e.tensor_tensor(out=s.dtd, in0=s.dtd, in1=s.dt, op=ALU.mult), reads=du, writes=du)
    P.act(lambda e: e.activation(out=s.edl, in_=s.edl, func=AF.Exp), reads=du, writes=du)

    pcnt = [0]
    for grp in range(4):
        P.dma("pool", lambda e, grp=grp: e.dma_start(out=s.w.rearrange("p k c -> p (k c)"), in_=g.d["ev_w_ssd"][ei, grp]),
              "s_w", writes=[s.wU])
        chunks = [(256, 2 * grp, "x0"), (384, 2 * grp + 1, "x1"), (512, 8 + grp, "B"), (640, 12 + grp, "C")]
        cp1, cp2 = [], []
        for wc0, cch, kind in chunks:
            for n in range(NT):
                sl = slice(n * 512, (n + 1) * 512)
                bk = 3 + (pcnt[0] % 2)
                pcnt[0] += 1
                P.capture()
                for k in range(KC):
                    P.pe(lambda e, bk=bk, k=k, wc0=wc0, sl=sl: e.matmul(bank(g, bk), lhsT=s.w[:, k, wc0:wc0 + 128], rhs=g.uT[:, k, sl],
                                                                        start=(k == 0), stop=(k == KC - 1)),
                         reads=[g.uU[n], s.wU], writes=[g.bU[bk]])
                cp1.append(P.end_capture())
                P.capture()
                if n == 0:
                    P.dve(lambda e: e.memset(s.xpre[:, 0:3], 0.0), writes=[s.xpreU])
                else:
                    P.dve(lambda e: e.tensor_copy(s.xpre[:, 0:3], s.xpre[:, 512:515]), reads=[s.xpreU], writes=[s.xpreU])
                P.act(lambda e, bk=bk: e.copy(s.xpre[:, 3:515], bank(g, bk)), reads=[g.bU[bk]], writes=[s.xpreU])
                P.dve(lambda e, cch=cch: e.tensor_scalar(out=s.cacc, in0=s.xpre[:, 3:515], scalar1=s.cw[:, cch * 4 + 3:cch * 4 + 4],
                                                          scalar2=s.cb[:, cch:cch + 1], op0=ALU.mult, op1=ALU.add),
                      reads=[s.xpreU] + sm, writes=[s.caccU])
                for tap in (2, 1, 0):
                    P.dve(lambda e, cch=cch, tap=tap: e.scalar_tensor_tensor(
                        out=s.cacc, in0=s.xpre[:, tap:tap + 512], scalar=s.cw[:, cch * 4 + tap:cch * 4 + tap + 1], in1=s.cacc,
                        op0=ALU.mult, op1=ALU.add), reads=[s.xpreU, s.caccU] + sm, writes=[s.caccU])
                P.act(lambda e: e.activation(out=s.cacc, in_=s.cacc, func=AF.Silu), reads=[s.caccU], writes=[s.caccU])
                if kind in ("B", "C"):
                    dst = s.BT if kind == "B" else s.CT
                    P.dve(lambda e, dst=dst, sl=sl: e.tensor_copy(dst[:, sl], s.cacc), reads=[s.caccU], writes=[s.BCU[n]])
                if kind != "C":
                    for j in range(4):
                        P.pe(lambda e, j=j: e.transpose(bank(g, 5)[:, j * 128:(j + 1) * 128], s.cacc[:, j * 128:(j + 1) * 128], ident[:]),
                             reads=[s.caccU, g.cU], writes=[g.bU[5]])
                    src = bank(g, 5).rearrange("p (b c) -> p b c", b=4)
                    if kind == "B":
                        P.act(lambda e, n=n, src=src: e.copy(s.Btok[:, n * 4:(n + 1) * 4, :], src), reads=[g.bU[5]], writes=[s.BtokU[n]])
                    else:
                        co = 0 if kind == "x0" else 128
                        P.act(lambda e, n=n, src=src, co=co: e.copy(s.xtok[:, n * 4:(n + 1) * 4, co:co + 128], src),
                              reads=[g.bU[5]], writes=[s.xtokU[n]])
                cp2.append(P.end_capture())
        P.replay_merged(cp1[0], [])
        for i in range(len(cp2)):
            P.replay_merged(cp1[i + 1] if i + 1 < len(cp1) else [], [])
            P.replay_merged(cp2[i], [])
        hs4 = slice(4 * grp, 4 * grp + 4)
        fronts, backs = [], []
        for b in range(NB):
            P.capture()
            n = b // 4
            blk = slice(b * 128, (b + 1) * 128)
            hcol = lambda t, b=b: v3(t)[:, b, hs4]
            bch = lambda t, w, b=b: hcol(t, b).unsqueeze(2).to_broadcast([128, 4, w])
            x4 = s.xtok[:, b, :].rearrange("p (h q) -> p h q", h=4)
            pb_ = b % 2
            s.acsTb, s.acsTbU = s.acsTb2[pb_], s.acsTbU2[pb_]
            s.CBm, s.CBmU = s.CBm2[pb_], s.CBmU2[pb_]
            s.Dm, s.DmU = s.Dm2[pb_], s.DmU2[pb_]
            s.E, s.EU = s.E2[pb_], s.EU2[pb_]
            s.t1, s.t1U = s.t12[pb_], s.t1U2[pb_]
            s.t2, s.t2U = s.t22[pb_], s.t2U2[pb_]
            s.ytmp, s.ytmpU = s.ytmp2[pb_], s.ytmpU2[pb_]
            s.zs, s.zsU = s.zs2[pb_], s.zsU2[pb_]
            s.sc, s.scU = s.sc2[pb_], s.scU2[pb_]
            s.Xdt, s.XdtU = s.Xdt2[pb_], s.XdtU2[pb_]
            s.XB, s.XBU = s.XB2[pb_], s.XBU2[pb_]
            for k in range(KC):
                P.pe(lambda e, k=k, blk=blk: e.matmul(bank(g, 6)[:, 0:256], lhsT=g.uT[:, k, blk], rhs=s.w[:, k, 0:256],
                                                      start=(k == 0), stop=(k == KC - 1)),
                     reads=[g.uU[n], s.wU], writes=[g.bU[6]])
            P.act(lambda e: e.activation(out=s.zs, in_=bank(g, 6)[:, 0:256], func=AF.Silu), reads=[g.bU[6]], writes=[s.zsU])
            P.pe(lambda e, blk=blk: e.matmul(bank(g, 7)[:, 0:128], lhsT=s.BT[:, blk], rhs=s.CT[:, blk], start=True, stop=True),
                 reads=[s.BCU[n]], writes=[g.bU[7]])
            P.dve(lambda e: e.tensor_tensor(out=s.CBm, in0=bank(g, 7)[:, 0:128], in1=c["triI"][:], op=ALU.mult),
                  reads=[g.bU[7], g.cU], writes=[s.CBmU])
            P.pe(lambda e, b=b: e.transpose(bank(g, 0)[0:16, 0:128], s.acs[:, b * 16:(b + 1) * 16], ident[:]),
                 reads=du + [g.cU], writes=[g.bU[0]])
            P.act(lambda e: e.copy(s.acsTb[0:16, :], bank(g, 0)[0:16, 0:128]), reads=[g.bU[0]], writes=[s.acsTbU])
            P.dve(lambda e: e.tensor_tensor(out=s.Rbd[0:16, :].rearrange("p (h l) -> p h l", h=4),
                                            in0=s.acsTb[0:16, :].unsqueeze(1).to_broadcast([16, 4, 128]),
                                            in1=c["mg16"][:, hs4].unsqueeze(2).to_broadcast([16, 4, 128]), op=ALU.mult),
                  reads=[s.acsTbU, g.cU], writes=[s.RbdU])
            P.pe(lambda e: e.matmul(bank(g, 1), lhsT=c["ones"][0:16, :], rhs=s.Rbd[0:16, :], start=True, stop=True),
                 reads=[s.RbdU, g.cU], writes=[g.bU[1]])
            P.dve(lambda e, b=b: e.tensor_tensor(out=s.Dm.rearrange("p (h l) -> p h l", h=4),
                                                 in0=bank(g, 1).rearrange("p (h l) -> p h l", h=4),
                                                 in1=bch(s.acs, 128, b), op=ALU.subtract),
                  reads=[g.bU[1]] + du, writes=[s.DmU])
            P.dve(lambda e: e.tensor_scalar(out=s.Dm, in0=s.Dm, scalar1=0.0, scalar2=None, op0=ALU.min), reads=[s.DmU], writes=[s.DmU])
            P.act(lambda e: e.activation(out=s.E, in_=s.Dm, func=AF.Exp), reads=[s.DmU], writes=[s.EU])
            P.dve(lambda e: e.tensor_tensor(out=s.sc, in0=s.E.rearrange("p (h l) -> p h l", h=4),
                                            in1=s.CBm.unsqueeze(1).to_broadcast([128, 4, 128]), op=ALU.mult),
                  reads=[s.EU, s.CBmU], writes=[s.scU])
            P.dve(lambda e, b=b: e.tensor_tensor(out=s.Xdt.rearrange("p (h q) -> p h q", h=4), in0=x4, in1=bch(s.dt, 64, b), op=ALU.mult),
                  reads=[s.xtokU[n]] + du, writes=[s.XdtU])
            P.dve(lambda e, b=b: e.tensor_tensor(out=s.XB.rearrange("p (h q) -> p h q", h=4), in0=x4, in1=bch(s.dtd, 64, b), op=ALU.mult),
                  reads=[s.xtokU[n]] + du, writes=[s.XBU])
            fronts.append(P.end_capture())
            P.capture()
            for h4 in range(4):
                P.pe(lambda e, h4=h4: e.matmul(bank(g, 2)[:, h4 * 64:(h4 + 1) * 64], lhsT=s.sc[:, h4, :], rhs=s.Xdt[:, h4 * 64:(h4 + 1) * 64],
                                               start=True, stop=True), reads=[s.scU, s.XdtU], writes=[g.bU[2]])
            if b > 0:
                P.pe(lambda e, blk=blk: e.matmul(bank(g, 3)[:, 0:256], lhsT=s.CT[:, blk], rhs=s.Sbf, start=True, stop=True),
                     reads=[s.BCU[n], s.SbfU], writes=[g.bU[3]])
            P.pe(lambda e, b=b: e.matmul(bank(g, 4)[:, 0:256], lhsT=s.Btok[:, b, :], rhs=s.XB, start=True, stop=True),
                 reads=[s.BtokU[n], s.XBU], writes=[g.bU[4]])
            if b > 0:
                P.dve(lambda e, b=b: e.tensor_tensor(out=s.t1.rearrange("p (h q) -> p h q", h=4),
                                                     in0=bank(g, 3)[:, 0:256].rearrange("p (h q) -> p h q", h=4),
                                                     in1=bch(s.eacs, 64, b), op=ALU.mult), reads=[g.bU[3]] + du, writes=[s.t1U])
                P.dve(lambda e: e.tensor_tensor(out=s.t2, in0=bank(g, 2)[:, 0:256], in1=s.t1, op=ALU.add), reads=[g.bU[2], s.t1U], writes=[s.t2U])
            else:
                P.dve(lambda e: e.tensor_copy(s.t2, bank(g, 2)[:, 0:256]), reads=[g.bU[2]], writes=[s.t2U])
            P.dve(lambda e: e.tensor_tensor(out=s.t1.rearrange("p (h q) -> p h q", h=4), in0=x4,
                                            in1=s.Dsk[:, hs4].unsqueeze(2).to_broadcast([128, 4, 64]), op=ALU.mult),
                  reads=[s.xtokU[n]] + sm, writes=[s.t1U])
            P.dve(lambda e: e.tensor_tensor(out=s.t2, in0=s.t2, in1=s.t1, op=ALU.add), reads=[s.t1U, s.t2U], writes=[s.t2U])
            P.dve(lambda e: e.tensor_tensor(out=s.ytmp, in0=s.t2, in1=s.zs, op=ALU.mult), reads=[s.t2U, s.zsU], writes=[s.ytmpU])
            for j in range(2):
                P.pe(lambda e, j=j: e.transpose(bank(g, 5)[:, j * 128:(j + 1) * 128], s.ytmp[:, j * 128:(j + 1) * 128], ident[:]),
                     reads=[s.ytmpU, g.cU], writes=[g.bU[5]])
            P.act(lambda e, blk=blk: e.copy(s.yTa[:, 2 * grp:2 * grp + 2, blk], bank(g, 5)[:, 0:256].rearrange("p (j t) -> p j t", j=2)),
                  reads=[g.bU[5]], writes=[s.yTaU[2 * grp][n], s.yTaU[2 * grp + 1][n]])
            if b == 0:
                P.dve(lambda e: e.tensor_copy(s.S, bank(g, 4)[:, 0:256]), reads=[g.bU[4]], writes=[s.SU])
            else:
                P.dve(lambda e, b=b: e.tensor_tensor(out=s.S.rearrange("p (h q) -> p h q", h=4), in0=s.S.rearrange("p (h q) -> p h q", h=4),
                                                     in1=bch(s.edl, 64, b), op=ALU.mult), reads=[s.SU] + du, writes=[s.SU])
                P.dve(lambda e: e.tensor_tensor(out=s.S, in0=s.S, in1=bank(g, 4)[:, 0:256], op=ALU.add), reads=[s.SU, g.bU[4]], writes=[s.SU])
            if b + 1 < NB:
                P.act(lambda e: e.copy(s.Sbf, s.S), reads=[s.SU], writes=[s.SbfU])
            backs.append(P.end_capture())
        P.replay_merged(fronts[0], [])
        for b in range(NB):
            P.replay_merged(fronts[b + 1] if b + 1 < NB else [], backs[b])

    P.barrier()
    for n in range(NT):
        sl = slice(n * 512, (n + 1) * 512)
        rms_rstd_tile(g, lambda k, sl=sl: s.yTa[:, k, sl], lambda k, n=n: [s.yTaU[k][n]], 8, 1024)
        P.dve(lambda e, sl=sl: e.tensor_copy(s.rstd_all[:, sl], g.rstd_t[:]), reads=[g.rstdU], writes=[s.rstdallU])
    cnt = 0

    def load_wo(m):
        wi = m % 2
        P.dma("pool", lambda e: e.dma_start(out=s.wo2[wi].rearrange("p c j -> p (c j)"), in_=g.d["ev_w_out_t"][ei, m, :, 0:1024]),
              f"s_wo{wi}", writes=[s.woU2[wi]])
        P.dve(lambda e: e.tensor_tensor(out=s.wo2[wi], in0=s.wo2[wi], in1=s.snw[:, 0:8].unsqueeze(2).to_broadcast([128, 8, 128]), op=ALU.mult),
              reads=[s.woU2[wi]] + sm, writes=[s.woU2[wi]])

    load_wo(0)
    for m in range(KC):
        if m + 1 < KC:
            load_wo(m + 1)
        wi = m % 2
        for n in range(NT):
            sl = slice(n * 512, (n + 1) * 512)
            bk = cnt % 2
            ti = cnt % 2
            cnt += 1
            for k in range(8):
                P.pe(lambda e, bk=bk, k=k, sl=sl: e.matmul(bank(g, bk), lhsT=s.wo2[wi][:, k, :], rhs=s.yTa[:, k, sl], start=(k == 0), stop=(k == 7)),
                     reads=[s.woU2[wi], s.yTaU[k][n]], writes=[g.bU[bk]])
            P.dve(lambda e, bk=bk, sl=sl, ti=ti: e.tensor_tensor(out=s.tmp2[ti], in0=bank(g, bk), in1=s.rstd_all[:, sl], op=ALU.mult),
                  reads=[g.bU[bk], s.rstdallU], writes=[s.tmpU2[ti]])
            P.pool(lambda e, m=m, sl=sl, ti=ti: e.tensor_tensor(out=g.hT[:, m, sl], in0=g.hT[:, m, sl], in1=s.tmp2[ti], op=ALU.add),
                   reads=[s.tmpU2[ti], g.hU[m][n]], writes=[g.hU[m][n]])


def even_attn(g, li):
    P, L, NB, NT = g.P, g.L, g.NB, g.NT
    ei = li // 2
    lam_init = 0.8 - 0.6 * math.exp(-0.3 * li)
    c = g.cst
    A = Carver(g)
    a = NS()
    a.corr = A.f(2048).rearrange("p (h d q) -> p h d q", h=8, d=2); a.corrU = P.unit()
    a.b31 = A.f(8); a.nb31 = A.f(8); a.bU_ = P.unit()
    a.lq = [A.f(64) for _ in range(4)]; a.lamU = P.unit()
    a.lam = A.f(8)
    a.slnw = A.f(128); a.slnwU = P.unit()
    a.w = [A.b(KC * 512).rearrange("p (k c) -> p k c", k=KC) for _ in range(2)]; a.wU = P.units(2)
    a.qT = [A.b(L) for _ in range(2)]; a.qTU = [P.unit() for _ in range(NT)]
    a.kT = A.b(L); a.kTU = [P.unit() for _ in range(NT)]
    a.v = A.b(NB * 132).rearrange("p (b c) -> p b c", c=132); a.vU = P.unit()
    a.gs = A.b(NB * 128).rearrange("p (b c) -> p b c", c=128); a.gsU = P.unit()
    a.PT = [[A.b(512) for _ in range(2)] for _ in range(2)]; a.PTU = [P.units(2), P.units(2)]
    a.yT = A.b(4 * L).rearrange("p (j t) -> p j t", j=4); a.yTU = P.units(4)
    a.wo = [A.b(512).rearrange("p (j c) -> p j c", j=4) for _ in range(2)]; a.woU = P.units(2)
    ident = c["ident"]

    P.dma("sp", lambda e: e.dma_start(out=a.corr.rearrange("p h d q -> p (h d q)"), in_=g.d["rel_biasD"]), "a_corr", writes=[a.corrU])
    P.dma("sp", lambda e: e.dma_start(out=a.b31, in_=g.d["rel_b31"].partition_broadcast(128)), "a_b31", writes=[a.bU_])
    P.dve(lambda e: e.tensor_scalar(out=a.nb31, in0=a.b31, scalar1=-1.0, scalar2=None, op0=ALU.mult), reads=[a.bU_], writes=[a.bU_])
    for h in range(8):
        P.act(lambda e, h=h: e.activation(out=a.corr[:, h], in_=a.corr[:, h], func=AF.Exp, bias=a.nb31[:, h:h + 1]),
              reads=[a.corrU, a.bU_], writes=[a.corrU])
    for i, nm in enumerate(("lambda_q1", "lambda_k1", "lambda_q2", "lambda_k2")):
        P.dma("sp", lambda e, i=i, nm=nm: e.dma_start(out=a.lq[i], in_=g.d[nm][ei].partition_broadcast(128)), "a_lam", writes=[a.lamU])
    lu = [a.lamU]
    P.dve(lambda e: e.tensor_tensor(out=a.lq[0], in0=a.lq[0], in1=a.lq[1], op=ALU.mult), reads=lu, writes=lu)
    P.dve(lambda e: e.tensor_tensor(out=a.lq[2], in0=a.lq[2], in1=a.lq[3], op=ALU.mult), reads=lu, writes=lu)
    P.dve(lambda e: e.tensor_reduce(out=a.lam[:, 0:1], in_=a.lq[0], axis=AX.X, op=ALU.add), reads=lu, writes=lu)
    P.dve(lambda e: e.tensor_reduce(out=a.lam[:, 1:2], in_=a.lq[2], axis=AX.X, op=ALU.add), reads=lu, writes=lu)
    P.act(lambda e: e.activation(out=a.lam[:, 2:4], in_=a.lam[:, 0:2], func=AF.Exp), reads=lu, writes=lu)
    P.dve(lambda e: e.tensor_tensor(out=a.lam[:, 4:5], in0=a.lam[:, 3:4], in1=a.lam[:, 2:3], op=ALU.subtract), reads=lu, writes=lu)
    P.dve(lambda e: e.tensor_scalar(out=a.lam[:, 5:6], in0=a.lam[:, 4:5], scalar1=-lam_init, scalar2=None, op0=ALU.add), reads=lu, writes=lu)
    P.dma("sp", lambda e: e.dma_start(out=a.slnw, in_=g.d["subln_w"][ei].partition_broadcast(128)), "a_slnw", writes=[a.slnwU])
    P.dve(lambda e: e.tensor_scalar(out=a.slnw, in0=a.slnw, scalar1=1.0 - lam_init, scalar2=None, op0=ALU.mult),
          reads=[a.slnwU], writes=[a.slnwU])
    P.dve(lambda e: e.memset(a.v, 1.0), writes=[a.vU])

    def load_w(h):
        i = h % 2
        P.dma("pool", lambda e: e.dma_start(out=a.w[i].rearrange("p k c -> p (k c)"), in_=g.d["ev_w_att"][ei, h]),
              f"a_w{i}", writes=[a.wU[i]])

    pc = [0]

    def project(h):
        wi = h % 2
        w = a.w[wi]
        for n in range(NT):
            sl = slice(n * 512, (n + 1) * 512)
            for which in range(2):
                bk = pc[0] % 2
                pc[0] += 1
                for k in range(KC):
                    P.pe(lambda e, bk=bk, k=k, sl=sl, which=which: e.matmul(bank(g, bk), lhsT=w[:, k, which * 128:(which + 1) * 128],
                                                                            rhs=g.uT[:, k, sl], start=(k == 0), stop=(k == KC - 1)),
                         reads=[g.uU[n], a.wU[wi]], writes=[g.bU[bk]])
                if which == 0:
                    for cc in range(2):
                        P.dve(lambda e, bk=bk, sl=sl, cc=cc: e.tensor_scalar(out=a.qT[cc][:, sl], in0=bank(g, bk), scalar1=c["maskq"][:, cc:cc + 1],
                                                                             scalar2=None, op0=ALU.mult),
                              reads=[g.bU[bk], g.cU], writes=[a.qTU[n]])
                else:
                    P.act(lambda e, bk=bk, sl=sl: e.copy(a.kT[:, sl], bank(g, bk)), reads=[g.bU[bk]], writes=[a.kTU[n]])
        for b in range(NB):
            bk = 2 + (b % 2)
            for k in range(KC):
                P.pe(lambda e, bk=bk, k=k, b=b: e.matmul(bank(g, bk)[:, 0:256], lhsT=g.uT[:, k, b * 128:(b + 1) * 128], rhs=w[:, k, 256:512],
                                                         start=(k == 0), stop=(k == KC - 1)),
                     reads=[g.uU[b // 4], a.wU[wi]], writes=[g.bU[bk]])
            P.act(lambda e, bk=bk, b=b: e.copy(a.v[:, b, 0:128], bank(g, bk)[:, 0:128]), reads=[g.bU[bk]], writes=[a.vU])
            P.act(lambda e, bk=bk, b=b: e.activation(out=a.gs[:, b, :], in_=bank(g, bk)[:, 128:256], func=AF.Silu),
                  reads=[g.bU[bk]], writes=[a.gsU])

    gc = [0]
    a.r2 = [A.f(8) for _ in range(2)]; a.rU2 = P.units(2)
    a.t2 = [A.f(128) for _ in range(2)]; a.tU2 = P.units(2)
    a.o2 = [A.f(128) for _ in range(2)]; a.oU2 = P.units(2)
    a.sqo2 = [A.f(128) for _ in range(2)]; a.sqoU2 = P.units(2)
    a.y2 = [A.f(128) for _ in range(2)]; a.yU2 = P.units(2)

    def attend(h):
        hj = h % 4
        groups = []
        for qb in range(NB):
            for gi in range(qb // 4 + 1):
                kbs = [kb for kb in range(gi * 4, gi * 4 + 4) if kb <= qb]
                groups.append((qb, gi, kbs, gc[0] % 2))
                gc[0] += 1

        def accb(qb, cc):
            return (2 + cc) if qb % 2 == 0 else cc

        def S_(grp):
            qb, gi, kbs, buf = grp
            qs = slice(qb * 128, (qb + 1) * 128)
            for cc in range(2):
                bk = 4 + 2 * cc + buf
                for j, kb in enumerate(kbs):
                    P.pe(lambda e, bk=bk, j=j, kb=kb, cc=cc: e.matmul(bank(g, bk)[:, j * 128:(j + 1) * 128],
                                                                      lhsT=a.kT[:, kb * 128:(kb + 1) * 128], rhs=a.qT[cc][:, qs],
                                                                      start=True, stop=True),
                         reads=[a.kTU[kb // 4], a.qTU[qb // 4]], writes=[g.bU[bk]])

        def E_(grp):
            qb, gi, kbs, buf = grp
            nv = len(kbs)
            for cc in range(2):
                bk = 4 + 2 * cc + buf
                pt = a.PT[cc][buf]
                ptu = a.PTU[cc][buf]
                P.act(lambda e, bk=bk, pt=pt, nv=nv: e.activation(out=pt[:, 0:nv * 128], in_=bank(g, bk)[:, 0:nv * 128], func=AF.Exp,
                                                                  scale=0.125, bias=a.b31[:, h:h + 1]),
                      reads=[g.bU[bk], a.bU_], writes=[ptu])
                for j, kb in enumerate(kbs):
                    Dd = qb - kb
                    if Dd <= 1:
                        P.dve(lambda e, pt=pt, j=j, Dd=Dd: e.tensor_tensor(out=pt[:, j * 128:(j + 1) * 128], in0=pt[:, j * 128:(j + 1) * 128],
                                                                           in1=a.corr[:, h, Dd, :], op=ALU.mult),
                              reads=[ptu, a.corrU], writes=[ptu])

        def PV_(grp):
            qb, gi, kbs, buf = grp
            for cc in range(2):
                pt = a.PT[cc][buf]
                ptu = a.PTU[cc][buf]
                ab = accb(qb, cc)
                for j, kb in enumerate(kbs):
                    P.pe(lambda e, ab=ab, pt=pt, j=j, kb=kb: e.matmul(bank(g, ab)[:, 0:129], lhsT=pt[:, j * 128:(j + 1) * 128],
                                                                      rhs=a.v[:, kb, 0:129], start=(kb == 0), stop=(kb == qb)),
                         reads=[ptu, a.vU], writes=[g.bU[ab]])

        def FIN_(qb):
            qs = slice(qb * 128, (qb + 1) * 128)
            pq = qb % 2
            b0, b1 = accb(qb, 0), accb(qb, 1)
            r, t_, o_, sqo, y_ = a.r2[pq], a.t2[pq], a.o2[pq], a.sqo2[pq], a.y2[pq]
            ru = [a.rU2[pq]]
            tU, oU, sqoU, yU = a.tU2[pq], a.oU2[pq], a.sqoU2[pq], a.yU2[pq]
            P.dve(lambda e: e.reciprocal(r[:, 0:1], bank(g, b0)[:, 128:129]), reads=[g.bU[b0]], writes=ru)
            P.dve(lambda e: e.reciprocal(r[:, 1:2], bank(g, b1)[:, 128:129]), reads=[g.bU[b1]], writes=ru)
            P.dve(lambda e: e.tensor_tensor(out=r[:, 2:3], in0=r[:, 1:2], in1=a.lam[:, 5:6], op=ALU.mult), reads=ru + lu, writes=ru)
            P.dve(lambda e: e.tensor_scalar(out=t_, in0=bank(g, b0)[:, 0:128], scalar1=r[:, 0:1], scalar2=None, op0=ALU.mult),
                  reads=[g.bU[b0]] + ru, writes=[tU])
            P.dve(lambda e: e.scalar_tensor_tensor(out=o_, in0=bank(g, b1)[:, 0:128], scalar=r[:, 2:3], in1=t_, op0=ALU.mult, op1=ALU.add),
                  reads=[g.bU[b1], tU] + ru, writes=[oU])
            P.dve(lambda e: e.tensor_tensor(out=sqo, in0=o_, in1=o_, op=ALU.mult), reads=[oU], writes=[sqoU])
            P.dve(lambda e: e.tensor_reduce(out=r[:, 3:4], in_=sqo, axis=AX.X, op=ALU.add), reads=[sqoU], writes=ru)
            P.act(lambda e: e.activation(out=r[:, 4:5], in_=r[:, 3:4], func=AF.Ln, scale=1.0 / 128, bias=EPS), reads=ru, writes=ru)
            P.act(lambda e: e.activation(out=r[:, 5:6], in_=r[:, 4:5], func=AF.Exp, scale=-0.5), reads=ru, writes=ru)
            P.dve(lambda e: e.scalar_tensor_tensor(out=y_, in0=o_, scalar=r[:, 5:6], in1=a.slnw, op0=ALU.mult, op1=ALU.mult),
                  reads=[oU, a.slnwU] + ru, writes=[yU])
            P.dve(lambda e: e.tensor_tensor(out=y_, in0=y_, in1=a.gs[:, qb, :], op=ALU.mult), reads=[yU, a.gsU], writes=[yU])
            P.pe(lambda e: e.transpose(bank(g, b0)[:, 256:384], y_, ident[:]), reads=[yU, g.cU], writes=[g.bU[b0]])
            P.act(lambda e: e.copy(a.yT[:, hj, qs], bank(g, b0)[:, 256:384]), reads=[g.bU[b0]], writes=[a.yTU[hj]])

        M = len(groups)
        S_(groups[0])
        pending_fin = None
        for i in range(M):
            if i + 1 < M:
                S_(groups[i + 1])
            E_(groups[i])
            PV_(groups[i])
            if pending_fin is not None:
                FIN_(pending_fin)
                pending_fin = None
            qb, gi, kbs, buf = groups[i]
            if kbs[-1] == qb:
                pending_fin = qb
        if pending_fin is not None:
            FIN_(pending_fin)

    oc = [0]

    def outproj(hg):
        for m in range(KC):
            wi = oc[0] % 2
            oc[0] += 1
            c0 = (8 + hg * 4) * 128
            P.dma("pool", lambda e, m=m, wi=wi, c0=c0: e.dma_start(out=a.wo[wi].rearrange("p j c -> p (j c)"),
                                                                  in_=g.d["ev_w_out_t"][ei, m, :, c0:c0 + 512]),
                  f"a_wo{wi}", writes=[a.woU[wi]])
            for n in range(NT):
                sl = slice(n * 512, (n + 1) * 512)
                bk = n % 2
                for j in range(4):
                    P.pe(lambda e, bk=bk, j=j, sl=sl, wi=wi: e.matmul(bank(g, bk), lhsT=a.wo[wi][:, j, :], rhs=a.yT[:, j, sl],
                                                                      start=(j == 0), stop=(j == 3)),
                         reads=[a.woU[wi], a.yTU[j]], writes=[g.bU[bk]])
                P.dve(lambda e, bk=bk, m=m, sl=sl: e.tensor_tensor(out=g.hT[:, m, sl], in0=g.hT[:, m, sl], in1=bank(g, bk), op=ALU.add),
                      reads=[g.bU[bk], g.hU[m][n]], writes=[g.hU[m][n]])

    load_w(0)
    for h in range(8):
        if h + 1 < 8:
            load_w(h + 1)
        project(h)
        attend(h)
        if h % 4 == 3:
            outproj(h // 4)


_CACHE = {}


def kernel(**inputs):
    x = np.ascontiguousarray(np.asarray(inputs["x"], dtype=np.float32))
    Bsz, L, _ = x.shape
    n_cores = 8
    nseq = Bsz // n_cores
    key = (L, nseq)
    if key not in _CACHE:
        _CACHE[key] = build(L, nseq, (0, 1, 2, 3))
    nc, _ = _CACHE[key]
    common = host_layout(inputs)
    in_maps = []
    for cidx in range(n_cores):
        m = dict(common)
        m["x"] = x[cidx * nseq:(cidx + 1) * nseq]
        in_maps.append(m)
    res = run_bass_kernel_spmd(nc, in_maps, core_ids=list(range(n_cores)))
    out = np.concatenate([np.asarray(r["out"]) for r in res.results], axis=0)
    return out.astype(np.float32)
```

```python
import math, contextlib
import numpy as np
import concourse.bass as bass
import concourse.mybir as mybir
from concourse.bass_utils import run_bass_kernel_spmd
from concourse.alu_op_type import AluOpType as ALU

F32 = mybir.dt.float32
BF16 = mybir.dt.bfloat16
AF = mybir.ActivationFunctionType
AX = mybir.AxisListType

D = 1024
KC = 8
EPS = 1e-6
DEPTH = 4
HG_W = 2048
ARF_N = 11392
ARB_N = 35840


class Unit:
    __slots__ = ("name", "lw", "rd")

    def __init__(self, name):
        self.name = name
        self.lw = None
        self.rd = []


class _Rec:
    def __init__(self):
        self.call = None

    def __getattr__(self, name):
        def f(*args, **kw):
            assert self.call is None
            self.call = (name, args, kw)
            return None
        return f


class Prog:
    ENGS = ("pe", "act", "dve", "pool", "sp")

    def __init__(self, nc):
        self.nc = nc
        self.ops = []
        self.nunits = 0
        self.last_eng = {}
        self.last_key = {}

    def unit(self, name=None):
        self.nunits += 1
        return Unit(name or f"u{self.nunits}")

    def units(self, n, name="u"):
        return [self.unit(f"{name}{i}") for i in range(n)]

    def capture(self):
        self._cap = []
        return self._cap

    def end_capture(self):
        c, self._cap = self._cap, None
        return c

    def replay_merged(self, A, B):
        na, nb = len(A), len(B)
        ia = ib = 0
        while ia < na or ib < nb:
            if ib >= nb or (ia < na and ia * nb <= ib * na):
                self.op(*A[ia]); ia += 1
            else:
                self.op(*B[ib]); ib += 1

    def op(self, eng, fn, reads=(), writes=(), dma_key=None, extra_deps=()):
        if fn is not None and not isinstance(fn, tuple):
            rec = _Rec()
            fn(rec)
            assert rec.call is not None
            fn = rec.call
        if getattr(self, "_cap", None) is not None:
            self._cap.append((eng, fn, tuple(reads), tuple(writes), dma_key, tuple(extra_deps)))
            return None
        idx = len(self.ops)
        deps = set(extra_deps)
        for u in reads:
            if u.lw is not None:
                deps.add(u.lw)
        for u in writes:
            if u.lw is not None:
                deps.add(u.lw)
            deps.update(u.rd)
        for u in reads:
            u.rd.append(idx)
        for u in writes:
            u.lw = idx
            u.rd = []
        deps.discard(idx)
        self.ops.append(dict(eng=eng, fn=fn, deps=deps, dma_key=dma_key))
        if fn is not None:
            if dma_key is None:
                self.last_eng[eng] = idx
            else:
                self.last_key[dma_key] = idx
        return idx

    def pe(self, fn, reads=(), writes=()):
        return self.op("pe", fn, reads, writes)

    def act(self, fn, reads=(), writes=()):
        return self.op("act", fn, reads, writes)

    def dve(self, fn, reads=(), writes=()):
        return self.op("dve", fn, reads, writes)

    def pool(self, fn, reads=(), writes=()):
        return self.op("pool", fn, reads, writes)

    def dma(self, eng, fn, key, reads=(), writes=()):
        return self.op(eng, fn, reads, writes, dma_key=key)

    def barrier(self):
        deps = set(self.last_eng.values()) | set(self.last_key.values())
        for e in self.ENGS:
            self.op(e, None, extra_deps=deps)

    def emit(self, final_wait_ops=()):
        nc = self.nc
        ops = self.ops
        n = len(ops)

        def skip(od, o):
            return (od["eng"] == "pe" and o["eng"] == "pe" and od["dma_key"] is None
                    and o["dma_key"] is None and o["fn"] is not None)

        needed = [False] * n
        for i, o in enumerate(ops):
            for d in o["deps"]:
                if skip(ops[d], o):
                    continue
                needed[d] = True
        for d in final_wait_ops:
            needed[d] = True
        chan_count = {}
        ev = [None] * n
        for i, o in enumerate(ops):
            if o["fn"] is None:
                continue
            if o["dma_key"] is not None:
                ch = ("dma", o["dma_key"])
                chan_count[ch] = chan_count.get(ch, 0) + 16
                ev[i] = (ch, chan_count[ch])
            elif needed[i]:
                ch = ("eng", o["eng"])
                chan_count[ch] = chan_count.get(ch, 0) + 1
                ev[i] = (ch, chan_count[ch])
        chans = sorted(chan_count.keys(), key=str)
        self.n_sems = len(chans)
        sems = {}
        stack = contextlib.ExitStack()
        for ci, ch in enumerate(chans):
            sems[ch] = stack.enter_context(nc.semaphore(f"s{ci}"))
        known = {e: {} for e in self.ENGS}
        clock = [None] * n
        streams = {e: [] for e in self.ENGS}
        for i, o in enumerate(ops):
            e = o["eng"]
            kn = known[e]
            wd = {}
            for d in sorted(o["deps"]):
                od = ops[d]
                if skip(od, o):
                    continue
                ch, v = ev[d]
                if kn.get(ch, 0) >= v:
                    continue
                for c2, v2 in clock[d].items():
                    if kn.get(c2, 0) < v2:
                        kn[c2] = v2
                wd[ch] = max(wd.get(ch, 0), v)
            ck = dict(kn)
            if ev[i] is not None:
                ch, v = ev[i]
                ck[ch] = v
            clock[i] = ck
            streams[e].append((list(wd.items()), o["fn"], ev[i]))
        final = [ev[d] for d in final_wait_ops]
        for ch, tot in chan_count.items():
            if ch[0] == "dma":
                final.append((ch, tot))
        self.sems, self.streams, self.final, self._stack = sems, streams, final, stack

    def run_block(self):
        nc = self.nc
        sems, streams, final = self.sems, self.streams, self.final
        with nc.Block() as block:
            def mk(ename):
                def body(eng):
                    for waits, fn, e in streams[ename]:
                        for ch, v in waits:
                            eng.wait_ge(sems[ch], v)
                        if fn is None:
                            continue
                        ins = getattr(eng, fn[0])(*fn[1], **fn[2])
                        if e is not None:
                            ins.then_inc(sems[e[0]], 16 if e[0][0] == "dma" else 1)
                    if ename == "sp":
                        for ch, v in final:
                            eng.wait_ge(sems[ch], v)
                return body
            block.tensor(mk("pe"))
            block.scalar(mk("act"))
            block.vector(mk("dve"))
            block.gpsimd(mk("pool"))
            block.sync(mk("sp"))
        self._stack.close()


def _t5_bucket(rel):
    n = np.maximum(rel, 0)
    max_exact = 16
    large = max_exact + (np.log(np.maximum(n, 1).astype(np.float32) / max_exact)
                         / math.log(128 / max_exact) * (32 - max_exact)).astype(np.int32)
    large = np.minimum(large, 31)
    return np.where(n < max_exact, n, large)


def host_consts():
    s = np.arange(128)[:, None]
    t = np.arange(128)[None, :]
    c = {}
    c["ident"] = np.eye(128, dtype=np.float32)
    c["ones"] = np.ones((128, 128), np.float32)
    c["triC"] = ((s <= t).astype(np.float32) - (s <= 63).astype(np.float32))
    c["triU"] = (s > t).astype(np.float32)
    c["triI"] = (s <= t).astype(np.float32)
    sel = np.zeros((128, 2), np.float32)
    sel[:64, 0] = 1.0
    sel[:, 1] = 1.0
    c["sel"] = sel
    c["mg16"] = np.eye(16, dtype=np.float32)
    mq = np.zeros((128, 2), np.float32)
    mq[:64, 0] = 1.0
    mq[64:, 1] = 1.0
    c["maskq"] = mq
    return c


def host_layout(inp):
    f = lambda a: np.ascontiguousarray(np.asarray(a, dtype=np.float32))
    m = dict(host_consts())
    m["final_norm_w"] = f(inp["final_norm_w"])
    m["norm_w_cols"] = f(np.asarray(inp["norm_w"]).reshape(4, 8, 128).transpose(0, 2, 1))
    owin = np.asarray(inp["odd_w_in"])
    t = owin.reshape(2, 8, 128, 4, 16, 128).transpose(0, 4, 2, 1, 3, 5)
    m["odd_w_in_t"] = f(t).reshape(2, 16, 128, 8 * 512)
    m["odd_w_out"] = f(inp["odd_w_out"])
    m["hgrn_lower_bounds"] = f(inp["hgrn_lower_bounds"])
    m["hgrn_norm_w"] = f(inp["hgrn_norm_w"])
    ew = np.asarray(inp["even_w_in"]).reshape(2, 8, 128, 7184)
    z = ew[..., 0:1024]; xs = ew[..., 1024:2048]; Bm = ew[..., 2048:2560]; Cm = ew[..., 2560:3072]
    dt = ew[..., 3072:3088]
    q = ew[..., 3088:4112]; kk = ew[..., 4112:5136]; v = ew[..., 5136:6160]; gg = ew[..., 6160:7184]
    ssd = np.concatenate([z.reshape(2, 8, 128, 4, 256), xs.reshape(2, 8, 128, 4, 256),
                          Bm.reshape(2, 8, 128, 4, 128), Cm.reshape(2, 8, 128, 4, 128)], axis=-1)
    m["ev_w_ssd"] = f(ssd.transpose(0, 3, 2, 1, 4)).reshape(2, 4, 128, 8 * 768)
    m["ev_w_dt"] = f(dt.transpose(0, 2, 1, 3)).reshape(2, 128, 8 * 16)
    att = np.concatenate([q.reshape(2, 8, 128, 8, 128), kk.reshape(2, 8, 128, 8, 128),
                          v.reshape(2, 8, 128, 8, 128), gg.reshape(2, 8, 128, 8, 128)], axis=-1)
    m["ev_w_att"] = f(att.transpose(0, 3, 2, 1, 4)).reshape(2, 8, 128, 8 * 512)
    wo = np.asarray(inp["even_w_out"]).reshape(2, 16, 128, 8, 128)
    m["ev_w_out_t"] = f(wo.transpose(0, 3, 2, 1, 4)).reshape(2, 8, 128, 16 * 128)
    m["conv_w_cols"] = f(np.asarray(inp["conv_w"]).reshape(2, 4, 16, 128).transpose(0, 3, 2, 1)).reshape(2, 128, 64)
    m["conv_b_cols"] = f(np.asarray(inp["conv_b"]).reshape(2, 16, 128).transpose(0, 2, 1))
    for nm in ("dt_bias", "A_log", "D_skip", "lambda_q1", "lambda_k1", "lambda_q2", "lambda_k2", "subln_w"):
        m[nm] = f(inp[nm])
    m["ssd_norm_w_cols"] = f(np.asarray(inp["ssd_norm_w"]).reshape(2, 8, 128).transpose(0, 2, 1))
    rb = np.asarray(inp["rel_bias"], dtype=np.float32)
    kpos = np.arange(128)[:, None]
    qpos = np.arange(128)[None, :]
    bd = np.empty((128, 8, 2, 128), np.float32)
    for Dd in range(2):
        rel = qpos - kpos + 128 * Dd
        bidx = _t5_bucket(rel)
        g_ = rb[bidx]
        g_ = np.where((rel >= 0)[:, :, None], g_, np.float32(-30000.0))
        bd[:, :, Dd, :] = g_.transpose(0, 2, 1)
    m["rel_biasD"] = f(bd).reshape(128, 8 * 2 * 128)
    m["rel_b31"] = f(rb[31])
    return m


class NS:
    pass


class Carver:
    def __init__(self, g):
        self.g = g
        self.fo = 0
        self.bo = 0

    def f(self, n):
        ap = self.g.arf[:, self.fo:self.fo + n]
        self.fo += (n + 7) // 8 * 8
        assert self.fo <= ARF_N, ("ARF overflow", self.fo)
        return ap

    def b(self, n):
        ap = self.g.arb[:, self.bo:self.bo + n]
        self.bo += (n + 15) // 16 * 16
        assert self.bo <= ARB_N, ("ARB overflow", self.bo)
        return ap


def bank(g, i):
    return g.ps[:, i, :]


def build(L=2048, NSEQ=2, layers=(0, 1, 2, 3)):
    nc = bass.Bass("TRN2", target_bir_lowering=False, dynamic_dma_scratch_size=4096)
    NT, NB = L // 512, L // 128
    g = NS()
    g.nc, g.L, g.NT, g.NB = nc, L, NT, NB
    dr = lambda name, shape, kind="ExternalInput": nc.dram_tensor(name, list(shape), F32, kind=kind).ap()
    g.x_d = dr("x", [NSEQ, L, D])
    g.out_d = dr("out", [NSEQ, L, D], "ExternalOutput")
    g.d = {}
    shapes = {
        "final_norm_w": [D], "norm_w_cols": [DEPTH, 128, KC],
        "ident": [128, 128], "ones": [128, 128], "triC": [128, 128], "triU": [128, 128], "triI": [128, 128],
        "sel": [128, 2], "mg16": [16, 16], "maskq": [128, 2],
        "odd_w_in_t": [2, 16, 128, KC * 512], "odd_w_out": [2, HG_W, D], "hgrn_lower_bounds": [DEPTH, HG_W],
        "hgrn_norm_w": [2, 128],
        "ev_w_ssd": [2, 4, 128, 8 * 768], "ev_w_dt": [2, 128, 8 * 16], "ev_w_att": [2, 8, 128, 8 * 512],
        "ev_w_out_t": [2, 8, 128, 16 * 128], "conv_w_cols": [2, 128, 64], "conv_b_cols": [2, 128, 16],
        "dt_bias": [2, 16], "A_log": [2, 16], "D_skip": [2, 16], "lambda_q1": [2, 64], "lambda_k1": [2, 64],
        "lambda_q2": [2, 64], "lambda_k2": [2, 64], "subln_w": [2, 128], "ssd_norm_w_cols": [2, 128, 8],
        "rel_biasD": [128, 8 * 2 * 128], "rel_b31": [8],
    }
    for nm, shp in shapes.items():
        g.d[nm] = dr(nm, shp)
    g.in_names = ["x"] + list(shapes.keys())

    es = contextlib.ExitStack()
    sb = lambda name, shape, dt=F32: es.enter_context(nc.sbuf_tensor(name, list(shape), dt))
    P = Prog(nc)
    g.P = P
    g.hT = sb("hT", [128, KC, L]); g.hU = [[P.unit(f"h{k}_{n}") for n in range(NT)] for k in range(KC)]
    g.uT = sb("uT", [128, KC, L], BF16); g.uU = [P.unit(f"u{n}") for n in range(NT)]
    g.cst = {}
    g.cU = P.unit("consts")
    for nm in ("ident", "ones", "triI"):
        g.cst[nm] = sb("c_" + nm, [128, 128])
    g.cst["sel"] = sb("c_sel", [128, 2])
    g.cst["maskq"] = sb("c_maskq", [128, 2])
    g.cst["mg16"] = sb("c_mg16", [16, 16])
    g.nwc = sb("nwc", [128, DEPTH, KC])
    g.stat = sb("stat", [128, 16]); g.statU = P.unit("stat")
    g.sq = [sb(f"sq{i}", [128, 512]) for i in range(2)]; g.sqU = P.units(2, "sq")
    g.rstd_t = sb("rstd_t", [128, 512]); g.rstdU = P.unit("rstd_t")
    g.arf = sb("arf", [128, ARF_N])
    g.arb = sb("arb", [128, ARB_N], BF16)
    g.ps = es.enter_context(nc.psum_tensor("ps", [128, 8, 512], F32))
    g.bU = P.units(8, "bank")

    for nm in ("ident", "ones", "triI", "sel", "maskq", "mg16"):
        P.dma("sp", lambda e, nm=nm: e.dma_start(out=g.cst[nm][:], in_=g.d[nm]), "c_" + nm, writes=[g.cU])
    P.dma("sp", lambda e: e.dma_start(out=g.nwc[:], in_=g.d["norm_w_cols"].rearrange("l p k -> p l k")), "c_nwc", writes=[g.cU])

    out_ops = []
    for s in range(NSEQ):
        P.barrier()
        load_x(g, s)
        P.barrier()
        for li in layers:
            rms_to_uT(g, li)
            if li % 2 == 1:
                odd_layer(g, li)
            else:
                even_ssd(g, li)
                P.barrier()
                even_attn(g, li)
            P.barrier()
        out_ops += final_norm_store(g, s)
    P.emit(final_wait_ops=out_ops[-4:])
    P.run_block()
    es.close()
    return nc, P


def load_x(g, s):
    P, NT = g.P, g.NT
    A = Carver(g)
    xst = A.f(4096).rearrange("p (b d) -> p b d", b=4)
    xU = P.unit("xst")
    ident = g.cst["ident"]
    for n in range(NT):
        src = g.x_d[s, n * 512:(n + 1) * 512, :].rearrange("(b p) d -> p b d", p=128)
        P.dma("sp", lambda e, src=src: e.dma_start(out=xst, in_=src), "xst", writes=[xU])
        for k in range(KC):
            bk = k % 8
            for b in range(4):
                P.pe(lambda e, bk=bk, b=b, k=k: e.transpose(
                    bank(g, bk)[:, b * 128:(b + 1) * 128], xst[:, b, k * 128:(k + 1) * 128], ident[:]),
                    reads=[xU, g.cU], writes=[g.bU[bk]])
            if k % 2 == 0:
                P.dve(lambda e, bk=bk, k=k, n=n: e.tensor_copy(g.hT[:, k, n * 512:(n + 1) * 512], bank(g, bk)),
                      reads=[g.bU[bk]], writes=[g.hU[k][n]])
            else:
                P.act(lambda e, bk=bk, k=k, n=n: e.copy(g.hT[:, k, n * 512:(n + 1) * 512], bank(g, bk)),
                      reads=[g.bU[bk]], writes=[g.hU[k][n]])


def final_norm_store(g, s):
    P, NB = g.P, g.NB
    A = Carver(g)
    fnw = A.f(D); fnwU = P.unit("fnw")
    ost = [A.f(D) for _ in range(2)]; ostU = P.units(2, "ost")
    junk = A.f(1024); junkU = P.unit("junk")
    ident = g.cst["ident"]
    P.dma("sp", lambda e: e.dma_start(out=fnw, in_=g.d["final_norm_w"].partition_broadcast(128)), "fnw", writes=[fnwU])
    outs = []
    for b in range(NB):
        n = b // 4
        oi = b % 2
        for k in range(KC):
            bk = k // 4
            P.pe(lambda e, bk=bk, k=k, b=b: e.transpose(
                bank(g, bk)[:, (k % 4) * 128:(k % 4 + 1) * 128], g.hT[:, k, b * 128:(b + 1) * 128], ident[:]),
                reads=[g.hU[k][n], g.cU], writes=[g.bU[bk]])
        for half in range(2):
            P.act(lambda e, half=half: e.activation(
                out=junk[:, half * 512:(half + 1) * 512], in_=bank(g, half), func=AF.Square,
                accum_out=g.stat[:, half:half + 1]),
                reads=[g.bU[half]], writes=[junkU, g.statU])
        P.dve(lambda e: e.tensor_tensor(out=g.stat[:, 2:3], in0=g.stat[:, 0:1], in1=g.stat[:, 1:2], op=ALU.add),
              reads=[g.statU], writes=[g.statU])
        P.act(lambda e: e.activation(out=g.stat[:, 3:4], in_=g.stat[:, 2:3], func=AF.Ln, scale=1.0 / D, bias=EPS),
              reads=[g.statU], writes=[g.statU])
        P.act(lambda e: e.activation(out=g.stat[:, 4:5], in_=g.stat[:, 3:4], func=AF.Exp, scale=-0.5),
              reads=[g.statU], writes=[g.statU])
        for half in range(2):
            P.dve(lambda e, half=half, oi=oi: e.scalar_tensor_tensor(
                out=ost[oi][:, half * 512:(half + 1) * 512], in0=bank(g, half), scalar=g.stat[:, 4:5],
                in1=fnw[:, half * 512:(half + 1) * 512], op0=ALU.mult, op1=ALU.mult),
                reads=[g.bU[half], g.statU, fnwU], writes=[ostU[oi]])
        o = P.dma("sp", lambda e, oi=oi, s=s, b=b: e.dma_start(out=g.out_d[s, b * 128:(b + 1) * 128, :], in_=ost[oi]),
                  f"ost{oi}", reads=[ostU[oi]])
        outs.append(o)
    return outs


def rms_rstd_tile(g, src_fn, reads_fn, nchunks, dim):
    P = g.P
    ones = g.cst["ones"]
    for k in range(nchunks):
        i = k % 2
        P.act(lambda e, i=i, k=k: e.activation(out=g.sq[i][:], in_=src_fn(k), func=AF.Square),
              reads=reads_fn(k), writes=[g.sqU[i]])
        P.pe(lambda e, i=i, k=k: e.matmul(bank(g, 7), lhsT=ones[:], rhs=g.sq[i][:], start=(k == 0), stop=(k == nchunks - 1)),
             reads=[g.sqU[i], g.cU], writes=[g.bU[7]])
    P.act(lambda e: e.activation(out=g.rstd_t[:], in_=bank(g, 7), func=AF.Ln, scale=1.0 / dim, bias=EPS),
          reads=[g.bU[7]], writes=[g.rstdU])
    P.act(lambda e: e.activation(out=g.rstd_t[:], in_=g.rstd_t[:], func=AF.Exp, scale=-0.5),
          reads=[g.rstdU], writes=[g.rstdU])


def rms_to_uT(g, li):
    P, NT = g.P, g.NT
    for n in range(NT):
        sl = slice(n * 512, (n + 1) * 512)
        rms_rstd_tile(g, lambda k, sl=sl: g.hT[:, k, sl], lambda k, n=n: [g.hU[k][n]], KC, D)
        for k in range(KC):
            P.dve(lambda e, k=k, sl=sl: e.scalar_tensor_tensor(
                out=g.uT[:, k, sl], in0=g.hT[:, k, sl], scalar=g.nwc[:, li, k:k + 1], in1=g.rstd_t[:],
                op0=ALU.mult, op1=ALU.mult),
                reads=[g.hU[k][n], g.rstdU, g.cU], writes=[g.uU[n]])


def odd_layer(g, li):
    P, L, NB, NT = g.P, g.L, g.NB, g.NT
    oi = li // 2
    A = Carver(g)
    o = NS()
    c = g.cst
    f3 = lambda: A.f(512).rearrange("p (b d) -> p b d", b=4)
    f16 = lambda: A.f(NB * 128).rearrange("p (b d) -> p b d", d=128)
    o.fall = f16(); o.fallU = [P.unit() for _ in range(NT)]
    o.kkall = f16(); o.kkallU = [P.unit() for _ in range(NT)]
    o.qsall = f16(); o.qsallU = [P.unit() for _ in range(NT)]
    o.e13 = A.f(1024).rearrange("p (t b d) -> p t b d", t=2, b=4); o.e13U = P.unit()
    o.e2 = f3(); o.e2U = P.unit()
    o.kt = o.e2; o.ktU = o.e2U
    o.lbr = f3(); o.lbrU = P.unit()
    o.lbh = A.f(128); o.omlh = A.f(128); o.den = A.f(128); o.lbU = P.unit()
    o.eb = [A.f(NB * 2).rearrange("p (b t) -> p b t", t=2) for _ in range(2)]
    o.S = A.f(128); o.SU = P.unit()
    o.junk2 = A.f(128); o.junk2U = P.unit()
    o.oall = A.f(NB * 128).rearrange("p (b d) -> p b d", d=128); o.oallU = P.unit()
    o.ssall = A.f(NB); o.rsall = A.f(NB); o.ssU = P.unit()
    o.hnw = A.f(128); o.hnwU = P.unit()
    o.triC = A.f(128); o.triU = A.f(128); o.triUU = P.unit()
    o.bst = A.f(8); o.bstU = P.unit()
    o.w = [A.b(KC * 512).rearrange("p (k c) -> p k c", k=KC) for _ in range(2)]; o.wU = P.units(2)
    o.wout = A.b(2 * D).rearrange("p (j m) -> p j m", j=2); o.woutU = P.units(2)
    o.qT = [A.b(L) for _ in range(2)]
    o.kT = [A.b(L) for _ in range(2)]
    hb3 = lambda: A.b(NB * 128).rearrange("p (b d) -> p b d", d=128)
    o.kh = [hb3() for _ in range(2)]
    o.v = [hb3() for _ in range(2)]
    o.gs = [hb3() for _ in range(2)]
    o.hbU = [[[P.unit() for _ in range(NT)] for _ in range(6)] for _ in range(2)]
    o.attm = [A.b(128) for _ in range(2)]; o.attmU = P.units(2)
    o.Sb2 = [A.b(128) for _ in range(2)]; o.SbU2 = P.units(2)
    o.yT = A.b(2 * L).rearrange("p (j t) -> p j t", j=2); o.yTU = P.units(2)
    QT, KT, KH, VV, GS, EB = range(6)

    P.dma("sp", lambda e: e.dma_start(out=o.hnw, in_=g.d["hgrn_norm_w"][oi].partition_broadcast(128)), "o_hnw", writes=[o.hnwU])
    P.dma("sp", lambda e: e.dma_start(out=o.triC, in_=g.d["triC"]), "o_tri", writes=[o.triUU])
    P.dma("sp", lambda e: e.dma_start(out=o.triU, in_=g.d["triU"]), "o_tri", writes=[o.triUU])
    for i in range(2):
        P.dve(lambda e, i=i: e.memset(o.attm[i], 0.0), writes=[o.attmU[i]])

    def load_w(h):
        i = h % 2
        P.dma("pool", lambda e: e.dma_start(out=o.w[i].rearrange("p k c -> p (k c)"), in_=g.d["odd_w_in_t"][oi, h]),
              f"o_w{i}", writes=[o.wU[i]])

    def head_lb(h):
        hs = slice(h * 128, (h + 1) * 128)
        P.dma("sp", lambda e: e.dma_start(out=o.lbr, in_=g.d["hgrn_lower_bounds"][:, hs].partition_broadcast(128)),
              "o_lbr", writes=[o.lbrU])
        P.act(lambda e: e.activation(out=o.lbr, in_=o.lbr, func=AF.Exp), reads=[o.lbrU], writes=[o.lbrU])
        P.dve(lambda e: e.tensor_tensor(out=o.den, in0=o.lbr[:, 0, :], in1=o.lbr[:, 1, :], op=ALU.add), reads=[o.lbrU], writes=[o.lbU])
        P.dve(lambda e: e.tensor_tensor(out=o.den, in0=o.den, in1=o.lbr[:, 2, :], op=ALU.add), reads=[o.lbrU, o.lbU], writes=[o.lbU])
        P.dve(lambda e: e.tensor_tensor(out=o.den, in0=o.den, in1=o.lbr[:, 3, :], op=ALU.add), reads=[o.lbrU, o.lbU], writes=[o.lbU])
        P.dve(lambda e: e.reciprocal(o.den, o.den), reads=[o.lbU], writes=[o.lbU])
        if li == 1:
            P.dve(lambda e: e.tensor_tensor(out=o.lbh, in0=o.lbr[:, 1, :], in1=o.den, op=ALU.mult), reads=[o.lbrU, o.lbU], writes=[o.lbU])
        else:
            P.dve(lambda e: e.tensor_tensor(out=o.lbh, in0=o.lbr[:, 1, :], in1=o.lbr[:, 2, :], op=ALU.add), reads=[o.lbrU, o.lbU], writes=[o.lbU])
            for j in range(3, li + 1):
                P.dve(lambda e, j=j: e.tensor_tensor(out=o.lbh, in0=o.lbh, in1=o.lbr[:, j, :], op=ALU.add), reads=[o.lbrU, o.lbU], writes=[o.lbU])
            P.dve(lambda e: e.tensor_tensor(out=o.lbh, in0=o.lbh, in1=o.den, op=ALU.mult), reads=[o.lbU], writes=[o.lbU])
        P.dve(lambda e: e.tensor_scalar(out=o.omlh, in0=o.lbh, scalar1=-1.0, scalar2=1.0, op0=ALU.mult, op1=ALU.add),
              reads=[o.lbU], writes=[o.lbU])

    def stageA1(h, n):
        hb = h % 2
        wi = h % 2
        U = o.hbU[hb]
        bs = slice(n * 4, (n + 1) * 4)
        for b in range(4):
            tb = n * 4 + b
            for k in range(KC):
                P.pe(lambda e, b=b, tb=tb, k=k: e.matmul(bank(g, b), lhsT=g.uT[:, k, tb * 128:(tb + 1) * 128], rhs=o.w[wi][:, k, :],
                                                         start=(k == 0), stop=(k == KC - 1)),
                     reads=[g.uU[n], o.wU[wi]], writes=[g.bU[b]])
        pj = g.ps[:, 0:4, :]
        pb = [g.bU[0], g.bU[1], g.bU[2], g.bU[3]]
        bc4 = lambda t: t.unsqueeze(1).to_broadcast([128, 4, 128])
        P.act(lambda e: e.activation(out=o.fall[:, bs, :], in_=pj[:, :, 128:256], func=AF.Sigmoid), reads=pb, writes=[o.fallU[n]])
        P.act(lambda e: e.activation(out=o.qsall[:, bs, :], in_=pj[:, :, 0:128], func=AF.Silu), reads=pb, writes=[o.qsallU[n]])
        P.act(lambda e: e.activation(out=o.gs[hb][:, bs, :], in_=pj[:, :, 384:512], func=AF.Silu), reads=pb, writes=[U[GS][n]])
        P.act(lambda e: e.copy(o.v[hb][:, bs, :], pj[:, :, 256:384]), reads=pb, writes=[U[VV][n]])
        P.dve(lambda e: e.tensor_tensor(out=o.fall[:, bs, :], in0=o.fall[:, bs, :], in1=bc4(o.omlh), op=ALU.mult),
              reads=[o.fallU[n], o.lbU], writes=[o.fallU[n]])
        P.dve(lambda e: e.tensor_tensor(out=o.fall[:, bs, :], in0=o.fall[:, bs, :], in1=bc4(o.lbh), op=ALU.add),
              reads=[o.fallU[n], o.lbU], writes=[o.fallU[n]])
        P.pool(lambda e: e.tensor_scalar(out=o.kkall[:, bs, :], in0=o.fall[:, bs, :], scalar1=-1.0, scalar2=1.0, op0=ALU.mult, op1=ALU.add),
               reads=[o.fallU[n]], writes=[o.kkallU[n]])

    def stageAmid(h):
        P.act(lambda e: e.activation(out=o.fall, in_=o.fall, func=AF.Ln), reads=o.fallU + o.kkallU, writes=o.fallU)

    def stageA2(h, n):
        hb = h % 2
        U = o.hbU[hb]
        bs = slice(n * 4, (n + 1) * 4)
        logf = o.fall[:, bs, :]
        kk = o.kkall[:, bs, :]
        qs = o.qsall[:, bs, :]
        lu = [o.fallU[n]]
        for b in range(4):
            P.pe(lambda e, b=b: e.matmul(bank(g, 0)[:, b * 128:(b + 1) * 128], lhsT=o.triC, rhs=logf[:, b, :], start=True, stop=True),
                 reads=lu + [o.triUU], writes=[g.bU[0]])
        for b in range(4):
            P.pe(lambda e, b=b: e.matmul(bank(g, 1)[:, b * 128:(b + 1) * 128], lhsT=o.triU, rhs=logf[:, b, :], start=True, stop=True),
                 reads=lu + [o.triUU], writes=[g.bU[1]])
        for b in range(4):
            P.pe(lambda e, b=b: e.matmul(bank(g, 2)[:, b * 2:b * 2 + 2], lhsT=logf[:, b, :], rhs=c["sel"][:], start=True, stop=True),
                 reads=lu + [g.cU], writes=[g.bU[2]])
        P.act(lambda e: e.activation(out=o.eb[hb][:, bs, :], in_=bank(g, 2)[:, 0:8].rearrange("p (b t) -> p b t", t=2), func=AF.Exp),
              reads=[g.bU[2]], writes=[U[EB][n]])
        P.act(lambda e: e.activation(out=o.e13, in_=g.ps[:, 0:2, :].rearrange("p t (b d) -> p t b d", b=4), func=AF.Exp),
              reads=[g.bU[0], g.bU[1]], writes=[o.e13U])
        P.act(lambda e: e.activation(out=o.e2, in_=bank(g, 0).rearrange("p (b d) -> p b d", b=4), func=AF.Exp, scale=-1.0),
              reads=[g.bU[0]], writes=[o.e2U])
        P.dve(lambda e: e.tensor_tensor(out=qs, in0=qs, in1=o.e13[:, 0], op=ALU.mult), reads=[o.qsallU[n], o.e13U], writes=[o.qsallU[n]])
        P.pool(lambda e: e.tensor_tensor(out=o.e2, in0=kk, in1=o.e2, op=ALU.mult), reads=[o.kkallU[n], o.e2U], writes=[o.e2U])
        P.dve(lambda e: e.tensor_tensor(out=o.kh[hb][:, bs, :], in0=kk, in1=o.e13[:, 1], op=ALU.mult),
              reads=[o.kkallU[n], o.e13U], writes=[U[KH][n]])
        idf = c["ident"]
        for b in range(4):
            P.pe(lambda e, b=b: e.transpose(bank(g, 3)[:, b * 128:(b + 1) * 128], qs[:, b, :], idf[:]),
                 reads=[o.qsallU[n], g.cU], writes=[g.bU[3]])
        for b in range(4):
            P.pe(lambda e, b=b: e.transpose(bank(g, 2)[:, b * 128:(b + 1) * 128], o.kt[:, b, :], idf[:]),
                 reads=[o.ktU, g.cU], writes=[g.bU[2]])
        P.act(lambda e: e.copy(o.qT[hb][:, n * 512:(n + 1) * 512], bank(g, 3)), reads=[g.bU[3]], writes=[U[QT][n]])
        P.act(lambda e: e.copy(o.kT[hb][:, n * 512:(n + 1) * 512], bank(g, 2)), reads=[g.bU[2]], writes=[U[KT][n]])

    def stageAall(h):
        for n in range(NT):
            stageA1(h, n)
        stageAmid(h)
        for n in range(NT):
            stageA2(h, n)

    def stageB(h):
        hb = h % 2
        U = o.hbU[hb]

        def att(tb):
            n = tb // 4
            ts = slice(tb * 128, (tb + 1) * 128)
            bk = 5 + 2 * (tb % 2)
            P.pe(lambda e: e.matmul(bank(g, bk)[:, 64:128], lhsT=o.kT[hb][:, ts],
                                    rhs=o.qT[hb][:, tb * 128 + 64:(tb + 1) * 128], start=True, stop=True),
                 reads=[U[QT][n], U[KT][n]], writes=[g.bU[bk]])
            P.pe(lambda e: e.matmul(bank(g, bk)[0:64, 0:64], lhsT=o.kT[hb][:, tb * 128:tb * 128 + 64],
                                    rhs=o.qT[hb][:, tb * 128:tb * 128 + 64], start=True, stop=True),
                 reads=[U[QT][n], U[KT][n]], writes=[g.bU[bk]])

        def mask(tb):
            ai = tb % 2
            bk = 5 + 2 * (tb % 2)
            P.dve(lambda e: e.tensor_tensor(out=o.attm[ai][:, 64:128], in0=bank(g, bk)[:, 64:128], in1=c["triI"][:, 64:128], op=ALU.mult),
                  reads=[g.bU[bk], g.cU], writes=[o.attmU[ai]])
            P.dve(lambda e: e.tensor_tensor(out=o.attm[ai][0:64, 0:64], in0=bank(g, bk)[0:64, 0:64], in1=c["triI"][0:64, 0:64], op=ALU.mult),
                  reads=[g.bU[bk], g.cU], writes=[o.attmU[ai]])

        att(0)
        mask(0)
        for tb in range(NB):
            n = tb // 4
            ts = slice(tb * 128, (tb + 1) * 128)
            ai = tb % 2
            si = tb % 2
            P.pe(lambda e: e.matmul(bank(g, 6)[:, 128:256], lhsT=o.kh[hb][:, tb, :], rhs=o.v[hb][:, tb, :], start=True, stop=True),
                 reads=[U[KH][n], U[VV][n]], writes=[g.bU[6]])
            if tb + 1 < NB:
                att(tb + 1)
            P.pe(lambda e: e.matmul(bank(g, 4)[:, 0:128], lhsT=o.attm[ai], rhs=o.v[hb][:, tb, :], start=True, stop=(tb == 0)),
                 reads=[o.attmU[ai], U[VV][n]], writes=[g.bU[4]])
            if tb > 0:
                P.pe(lambda e: e.matmul(bank(g, 4)[:, 0:128], lhsT=o.qT[hb][:, ts], rhs=o.Sb2[si], start=False, stop=True),
                     reads=[o.SbU2[si], U[QT][n]], writes=[g.bU[4]])
            if tb == 0:
                P.dve(lambda e: e.tensor_copy(o.S, bank(g, 6)[:, 128:256]), reads=[g.bU[6]], writes=[o.SU])
            else:
                P.dve(lambda e: e.scalar_tensor_tensor(out=o.S, in0=o.S, scalar=o.eb[hb][:, tb, 1:2], in1=bank(g, 6)[:, 128:256],
                                                       op0=ALU.mult, op1=ALU.add),
                      reads=[o.SU, g.bU[6], U[EB][n]], writes=[o.SU])
            if tb + 1 < NB:
                nn = (tb + 1) // 4
                sn = (tb + 1) % 2
                P.dve(lambda e: e.tensor_scalar(out=o.Sb2[sn], in0=o.S, scalar1=o.eb[hb][:, tb + 1, 0:1], scalar2=None, op0=ALU.mult),
                      reads=[o.SU, U[EB][nn]], writes=[o.SbU2[sn]])
                mask(tb + 1)
            P.act(lambda e: e.copy(o.oall[:, tb, :], bank(g, 4)[:, 0:128]), reads=[g.bU[4]], writes=[o.oallU])
            P.act(lambda e: e.activation(out=o.junk2, in_=bank(g, 4)[:, 0:128], func=AF.Square, accum_out=o.ssall[:, tb:tb + 1]),
                  reads=[g.bU[4]], writes=[o.junk2U, o.ssU])

    def stageC(h):
        hb = h % 2
        U = o.hbU[hb]
        hj = h % 2
        P.act(lambda e: e.activation(out=o.rsall, in_=o.ssall, func=AF.Ln, scale=1.0 / 128, bias=EPS), reads=[o.ssU], writes=[o.ssU])
        P.act(lambda e: e.activation(out=o.rsall, in_=o.rsall, func=AF.Exp, scale=-0.5), reads=[o.ssU], writes=[o.ssU])
        P.dve(lambda e: e.tensor_tensor(out=o.oall, in0=o.oall, in1=o.rsall.unsqueeze(2).to_broadcast([128, NB, 128]), op=ALU.mult),
              reads=[o.oallU, o.ssU], writes=[o.oallU])
        P.dve(lambda e: e.tensor_tensor(out=o.oall, in0=o.oall, in1=o.hnw.unsqueeze(1).to_broadcast([128, NB, 128]), op=ALU.mult),
              reads=[o.oallU, o.hnwU], writes=[o.oallU])
        P.dve(lambda e: e.tensor_tensor(out=o.oall, in0=o.oall, in1=o.gs[hb], op=ALU.mult),
              reads=[o.oallU] + [U[GS][n] for n in range(NT)], writes=[o.oallU])
        for n in range(NT):
            bk = 4 + (n % 4)
            for b in range(4):
                P.pe(lambda e, bk=bk, b=b, n=n: e.transpose(bank(g, bk)[:, b * 128:(b + 1) * 128], o.oall[:, n * 4 + b, :], c["ident"][:]),
                     reads=[o.oallU, g.cU], writes=[g.bU[bk]])
            P.act(lambda e, bk=bk, n=n: e.copy(o.yT[:, hj, n * 512:(n + 1) * 512], bank(g, bk)), reads=[g.bU[bk]], writes=[o.yTU[hj]])

    def outproj(hp):
        for j in range(2):
            src = g.d["odd_w_out"][oi, (hp * 2 + j) * 128:(hp * 2 + j + 1) * 128, :]
            P.dma("pool", lambda e, j=j, src=src: e.dma_start(out=o.wout[:, j, :], in_=src), f"o_wout{j}", writes=[o.woutU[j]])
        cnt = 0
        for m in range(KC):
            for n in range(NT):
                bk = 4 + cnt % 4
                cnt += 1
                for j in range(2):
                    P.pe(lambda e, bk=bk, m=m, n=n, j=j: e.matmul(bank(g, bk), lhsT=o.wout[:, j, m * 128:(m + 1) * 128],
                                                                     rhs=o.yT[:, j, n * 512:(n + 1) * 512], start=(j == 0), stop=(j == 1)),
                         reads=[o.woutU[j], o.yTU[j]], writes=[g.bU[bk]])
                P.dve(lambda e, bk=bk, m=m, n=n: e.tensor_tensor(out=g.hT[:, m, n * 512:(n + 1) * 512], in0=g.hT[:, m, n * 512:(n + 1) * 512],
                                                                   in1=bank(g, bk), op=ALU.add),
                      reads=[g.bU[bk], g.hU[m][n]], writes=[g.hU[m][n]])

    load_w(0)
    load_w(1)
    head_lb(0)
    stageAall(0)
    for h in range(16):
        P.capture()
        stageB(h)
        stageC(h)
        if h % 2 == 1:
            outproj(h // 2)
        LB = P.end_capture()
        P.capture()
        if h + 1 < 16:
            if h + 2 < 16:
                load_w(h + 2)
            head_lb(h + 1)
            stageAall(h + 1)
        LA = P.end_capture()
        P.replay_merged(LA, LB)


def even_ssd(g, li):
    P, L, NB, NT = g.P, g.L, g.NB, g.NT
    ei = li // 2
    c = g.cst
    A = Carver(g)
    s = NS()
    HB = NB * 16
    s.xpre = A.f(515); s.xpreU = P.unit()
    s.cacc = A.f(512); s.caccU = P.unit()
    v3 = lambda ap: ap.rearrange("p (b h) -> p b h", h=16)
    s.dt = A.f(HB); s.atok = A.f(HB); s.acs = A.f(HB); s.eacs = A.f(HB); s.dtd = A.f(HB); s.edl = A.f(HB)
    s.dtU = P.unit()
    s.cw = A.f(64); s.cb = A.f(16); s.dtb = A.f(16); s.Abc = A.f(16); s.Dsk = A.f(16); s.snw = A.f(8)
    s.smallU = P.unit()
    s.S = A.f(256); s.SU = P.unit()
    scan_off = A.fo
    s.Rbd = A.f(512); s.RbdU = P.unit()
    D2 = lambda n: ([A.f(n) for _ in range(2)], P.units(2))
    s.acsTb2, s.acsTbU2 = D2(128)
    s.CBm2, s.CBmU2 = D2(128)
    s.Dm2, s.DmU2 = D2(512)
    s.E2, s.EU2 = D2(512)
    s.t12, s.t1U2 = D2(256)
    s.t22, s.t2U2 = D2(256)
    s.ytmp2, s.ytmpU2 = D2(256)
    s.rstd_all = g.arf[:, scan_off:scan_off + L]; s.rstdallU = P.unit()
    assert scan_off + L <= ARF_N
    s.w = A.b(KC * 768).rearrange("p (k c) -> p k c", k=KC); s.wU = P.unit()
    s.wdt = A.b(KC * 16).rearrange("p (k c) -> p k c", k=KC); s.wdtU = P.unit()
    s.BT = A.b(L); s.CT = A.b(L); s.BCU = [P.unit() for _ in range(NT)]
    s.xtok = A.b(NB * 256).rearrange("p (b c) -> p b c", c=256); s.xtokU = [P.unit() for _ in range(NT)]
    s.Btok = A.b(NB * 128).rearrange("p (b c) -> p b c", c=128); s.BtokU = [P.unit() for _ in range(NT)]
    scan_bo = A.bo
    s.zs2 = [A.b(256) for _ in range(2)]; s.zsU2 = P.units(2)
    s.sc2 = [A.b(512).rearrange("p (h l) -> p h l", h=4) for _ in range(2)]; s.scU2 = P.units(2)
    s.Xdt2 = [A.b(256) for _ in range(2)]; s.XdtU2 = P.units(2)
    s.XB2 = [A.b(256) for _ in range(2)]; s.XBU2 = P.units(2)
    s.Sbf = A.b(256); s.SbfU = P.unit()
    s.yTa = A.b(8 * L).rearrange("p (c t) -> p c t", c=8); s.yTaU = [[P.unit() for _ in range(NT)] for _ in range(8)]
    s.wo2 = [g.arb[:, scan_bo + i * 1024:scan_bo + (i + 1) * 1024].rearrange("p (c j) -> p c j", c=8) for i in range(2)]
    s.woU2 = P.units(2)
    assert scan_bo + 2048 <= A.bo
    s.tmp2 = [s.cacc, s.xpre[:, 0:512]]; s.tmpU2 = [s.caccU, s.xpreU]
    ident = c["ident"]

    sm = [s.smallU]
    P.dma("sp", lambda e: e.dma_start(out=s.cw, in_=g.d["conv_w_cols"][ei]), "s_small", writes=sm)
    P.dma("sp", lambda e: e.dma_start(out=s.cb, in_=g.d["conv_b_cols"][ei]), "s_small", writes=sm)
    P.dma("sp", lambda e: e.dma_start(out=s.dtb, in_=g.d["dt_bias"][ei].partition_broadcast(128)), "s_small", writes=sm)
    P.dma("sp", lambda e: e.dma_start(out=s.Abc, in_=g.d["A_log"][ei].partition_broadcast(128)), "s_small", writes=sm)
    P.dma("sp", lambda e: e.dma_start(out=s.Dsk, in_=g.d["D_skip"][ei].partition_broadcast(128)), "s_small", writes=sm)
    P.dma("sp", lambda e: e.dma_start(out=s.snw, in_=g.d["ssd_norm_w_cols"][ei]), "s_small", writes=sm)
    P.act(lambda e: e.activation(out=s.Abc, in_=s.Abc, func=AF.Exp), reads=sm, writes=sm)
    P.dve(lambda e: e.tensor_scalar(out=s.Abc, in0=s.Abc, scalar1=-1.0, scalar2=None, op0=ALU.mult), reads=sm, writes=sm)
    P.dma("pool", lambda e: e.dma_start(out=s.wdt.rearrange("p k c -> p (k c)"), in_=g.d["ev_w_dt"][ei]), "s_wdt", writes=[s.wdtU])

    for b in range(NB):
        for k in range(KC):
            P.pe(lambda e, b=b, k=k: e.matmul(bank(g, 0)[:, b * 16:(b + 1) * 16], lhsT=g.uT[:, k, b * 128:(b + 1) * 128],
                                              rhs=s.wdt[:, k, :], start=(k == 0), stop=(k == KC - 1)),
                 reads=[g.uU[b // 4], s.wdtU], writes=[g.bU[0]])
    bc_h = lambda t: t.unsqueeze(1).to_broadcast([128, NB, 16])
    du = [s.dtU]
    P.dve(lambda e: e.tensor_tensor(out=v3(s.dt), in0=v3(bank(g, 0)[:, 0:HB]), in1=bc_h(s.dtb), op=ALU.add),
          reads=[g.bU[0]] + sm, writes=du)
    P.act(lambda e: e.activation(out=s.dt, in_=s.dt, func=AF.Exp), reads=du, writes=du)
    P.act(lambda e: e.activation(out=s.dt, in_=s.dt, func=AF.Ln, bias=1.0), reads=du, writes=du)
    P.dve(lambda e: e.tensor_tensor(out=v3(s.atok), in0=v3(s.dt), in1=bc_h(s.Abc), op=ALU.mult), reads=du + sm, writes=du)
    for b in range(NB):
        P.pe(lambda e, b=b: e.matmul(bank(g, 1)[:, b * 16:(b + 1) * 16], lhsT=c["triI"][:], rhs=s.atok[:, b * 16:(b + 1) * 16],
                                     start=True, stop=True), reads=du + [g.cU], writes=[g.bU[1]])
    for b in range(NB):
        P.pe(lambda e, b=b: e.matmul(bank(g, 2)[:, b * 16:(b + 1) * 16], lhsT=c["ones"][:], rhs=s.atok[:, b * 16:(b + 1) * 16],
                                     start=True, stop=True), reads=du + [g.cU], writes=[g.bU[2]])
    P.dve(lambda e: e.tensor_copy(s.acs, bank(g, 1)[:, 0:HB]), reads=[g.bU[1]], writes=du)
    P.act(lambda e: e.activation(out=s.eacs, in_=s.acs, func=AF.Exp), reads=du, writes=du)
    P.dve(lambda e: e.tensor_copy(s.edl, bank(g, 2)[:, 0:HB]), reads=[g.bU[2]], writes=du)
    P.dve(lambda e: e.tensor_tensor(out=s.dtd, in0=s.edl, in1=s.acs, op=ALU.subtract), reads=du, writes=du)
    P.act(lambda e: e.activation(out=s.dtd, in_=s.dtd, func=AF.Exp), reads=du, writes=du)
    P.dve(lambda e: e.tensor_tensor(out=s.dtd, in0=s.dtd, in1=s.dt, op=ALU.mult), reads=du, writes=du)
    P.act(lambda e: e.activation(out=s.edl, in_=s.edl, func=AF.Exp), reads=du, writes=du)

    pcnt = [0]
    for grp in range(4):
        P.dma("pool", lambda e, grp=grp: e.dma_start(out=s.w.rearrange("p k c -> p (k c)"), in_=g.d["ev_w_ssd"][ei, grp]),
              "s_w", writes=[s.wU])
        chunks = [(256, 2 * grp, "x0"), (384, 2 * grp + 1, "x1"), (512, 8 + grp, "B"), (640, 12 + grp, "C")]
        cp1, cp2 = [], []
        for wc0, cch, kind in chunks:
            for n in range(NT):
                sl = slice(n * 512, (n + 1) * 512)
                bk = 3 + (pcnt[0] % 2)
                pcnt[0] += 1
                P.capture()
                for k in range(KC):
                    P.pe(lambda e, bk=bk, k=k, wc0=wc0, sl=sl: e.matmul(bank(g, bk), lhsT=s.w[:, k, wc0:wc0 + 128], rhs=g.uT[:, k, sl],
                                                                        start=(k == 0), stop=(k == KC - 1)),
                         reads=[g.uU[n], s.wU], writes=[g.bU[bk]])
                cp1.append(P.end_capture())
                P.capture()
                if n == 0:
                    P.dve(lambda e: e.memset(s.xpre[:, 0:3], 0.0), writes=[s.xpreU])
                else:
                    P.dve(lambda e: e.tensor_copy(s.xpre[:, 0:3], s.xpre[:, 512:515]), reads=[s.xpreU], writes=[s.xpreU])
                P.act(lambda e, bk=bk: e.copy(s.xpre[:, 3:515], bank(g, bk)), reads=[g.bU[bk]], writes=[s.xpreU])
                P.dve(lambda e, cch=cch: e.tensor_scalar(out=s.cacc, in0=s.xpre[:, 3:515], scalar1=s.cw[:, cch * 4 + 3:cch * 4 + 4],
                                                          scalar2=s.cb[:, cch:cch + 1], op0=ALU.mult, op1=ALU.add),
                      reads=[s.xpreU] + sm, writes=[s.caccU])
                for tap in (2, 1, 0):
                    P.dve(lambda e, cch=cch, tap=tap: e.scalar_tensor_tensor(
                        out=s.cacc, in0=s.xpre[:, tap:tap + 512], scalar=s.cw[:, cch * 4 + tap:cch * 4 + tap + 1], in1=s.cacc,
                        op0=ALU.mult, op1=ALU.add), reads=[s.xpreU, s.caccU] + sm, writes=[s.caccU])
                P.act(lambda e: e.activation(out=s.cacc, in_=s.cacc, func=AF.Silu), reads=[s.caccU], writes=[s.caccU])
                if kind in ("B", "C"):
                    dst = s.BT if kind == "B" else s.CT
                    P.dve(lambda e, dst=dst, sl=sl: e.tensor_copy(dst[:, sl], s.cacc), reads=[s.caccU], writes=[s.BCU[n]])
                if kind != "C":
                    for j in range(4):
                        P.pe(lambda e, j=j: e.transpose(bank(g, 5)[:, j * 128:(j + 1) * 128], s.cacc[:, j * 128:(j + 1) * 128], ident[:]),
                             reads=[s.caccU, g.cU], writes=[g.bU[5]])
                    src = bank(g, 5).rearrange("p (b c) -> p b c", b=4)
                    if kind == "B":
                        P.act(lambda e, n=n, src=src: e.copy(s.Btok[:, n * 4:(n + 1) * 4, :], src), reads=[g.bU[5]], writes=[s.BtokU[n]])
                    else:
                        co = 0 if kind == "x0" else 128
                        P.act(lambda e, n=n, src=src, co=co: e.copy(s.xtok[:, n * 4:(n + 1) * 4, co:co + 128], src),
                              reads=[g.bU[5]], writes=[s.xtokU[n]])
                cp2.append(P.end_capture())
        P.replay_merged(cp1[0], [])
        for i in range(len(cp2)):
            P.replay_merged(cp1[i + 1] if i + 1 < len(cp1) else [], [])
            P.replay_merged(cp2[i], [])
        hs4 = slice(4 * grp, 4 * grp + 4)
        fronts, backs = [], []
        for b in range(NB):
            P.capture()
            n = b // 4
            blk = slice(b * 128, (b + 1) * 128)
            hcol = lambda t, b=b: v3(t)[:, b, hs4]
            bch = lambda t, w, b=b: hcol(t, b).unsqueeze(2).to_broadcast([128, 4, w])
            x4 = s.xtok[:, b, :].rearrange("p (h q) -> p h q", h=4)
            pb_ = b % 2
            s.acsTb, s.acsTbU = s.acsTb2[pb_], s.acsTbU2[pb_]
            s.CBm, s.CBmU = s.CBm2[pb_], s.CBmU2[pb_]
            s.Dm, s.DmU = s.Dm2[pb_], s.DmU2[pb_]
            s.E, s.EU = s.E2[pb_], s.EU2[pb_]
            s.t1, s.t1U = s.t12[pb_], s.t1U2[pb_]
            s.t2, s.t2U = s.t22[pb_], s.t2U2[pb_]
            s.ytmp, s.ytmpU = s.ytmp2[pb_], s.ytmpU2[pb_]
            s.zs, s.zsU = s.zs2[pb_], s.zsU2[pb_]
            s.sc, s.scU = s.sc2[pb_], s.scU2[pb_]
            s.Xdt, s.XdtU = s.Xdt2[pb_], s.XdtU2[pb_]
            s.XB, s.XBU = s.XB2[pb_], s.XBU2[pb_]
            for k in range(KC):
                P.pe(lambda e, k=k, blk=blk: e.matmul(bank(g, 6)[:, 0:256], lhsT=g.uT[:, k, blk], rhs=s.w[:, k, 0:256],
                                                      start=(k == 0), stop=(k == KC - 1)),
                     reads=[g.uU[n], s.wU], writes=[g.bU[6]])
            P.act(lambda e: e.activation(out=s.zs, in_=bank(g, 6)[:, 0:256], func=AF.Silu), reads=[g.bU[6]], writes=[s.zsU])
            P.pe(lambda e, blk=blk: e.matmul(bank(g, 7)[:, 0:128], lhsT=s.BT[:, blk], rhs=s.CT[:, blk], start=True, stop=True),
                 reads=[s.BCU[n]], writes=[g.bU[7]])
            P.dve(lambda e: e.tensor_tensor(out=s.CBm, in0=bank(g, 7)[:, 0:128], in1=c["triI"][:], op=ALU.mult),
                  reads=[g.bU[7], g.cU], writes=[s.CBmU])
            P.pe(lambda e, b=b: e.transpose(bank(g, 0)[0:16, 0:128], s.acs[:, b * 16:(b + 1) * 16], ident[:]),
                 reads=du + [g.cU], writes=[g.bU[0]])
            P.act(lambda e: e.copy(s.acsTb[0:16, :], bank(g, 0)[0:16, 0:128]), reads=[g.bU[0]], writes=[s.acsTbU])
            P.dve(lambda e: e.tensor_tensor(out=s.Rbd[0:16, :].rearrange("p (h l) -> p h l", h=4),
                                            in0=s.acsTb[0:16, :].unsqueeze(1).to_broadcast([16, 4, 128]),
                                            in1=c["mg16"][:, hs4].unsqueeze(2).to_broadcast([16, 4, 128]), op=ALU.mult),
                  reads=[s.acsTbU, g.cU], writes=[s.RbdU])
            P.pe(lambda e: e.matmul(bank(g, 1), lhsT=c["ones"][0:16, :], rhs=s.Rbd[0:16, :], start=True, stop=True),
                 reads=[s.RbdU, g.cU], writes=[g.bU[1]])
            P.dve(lambda e, b=b: e.tensor_tensor(out=s.Dm.rearrange("p (h l) -> p h l", h=4),
                                                 in0=bank(g, 1).rearrange("p (h l) -> p h l", h=4),
                                                 in1=bch(s.acs, 128, b), op=ALU.subtract),
                  reads=[g.bU[1]] + du, writes=[s.DmU])
            P.dve(lambda e: e.tensor_scalar(out=s.Dm, in0=s.Dm, scalar1=0.0, scalar2=None, op0=ALU.min), reads=[s.DmU], writes=[s.DmU])
            P.act(lambda e: e.activation(out=s.E, in_=s.Dm, func=AF.Exp), reads=[s.DmU], writes=[s.EU])
            P.dve(lambda e: e.tensor_tensor(out=s.sc, in0=s.E.rearrange("p (h l) -> p h l", h=4),
                                            in1=s.CBm.unsqueeze(1).to_broadcast([128, 4, 128]), op=ALU.mult),
                  reads=[s.EU, s.CBmU], writes=[s.scU])
            P.dve(lambda e, b=b: e.tensor_tensor(out=s.Xdt.rearrange("p (h q) -> p h q", h=4), in0=x4, in1=bch(s.dt, 64, b), op=ALU.mult),
                  reads=[s.xtokU[n]] + du, writes=[s.XdtU])
            P.dve(lambda e, b=b: e.tensor_tensor(out=s.XB.rearrange("p (h q) -> p h q", h=4), in0=x4, in1=bch(s.dtd, 64, b), op=ALU.mult),
                  reads=[s.xtokU[n]] + du, writes=[s.XBU])
            fronts.append(P.end_capture())
            P.capture()
            for h4 in range(4):
                P.pe(lambda e, h4=h4: e.matmul(bank(g, 2)[:, h4 * 64:(h4 + 1) * 64], lhsT=s.sc[:, h4, :], rhs=s.Xdt[:, h4 * 64:(h4 + 1) * 64],
                                               start=True, stop=True), reads=[s.scU, s.XdtU], writes=[g.bU[2]])
            if b > 0:
                P.pe(lambda e, blk=blk: e.matmul(bank(g, 3)[:, 0:256], lhsT=s.CT[:, blk], rhs=s.Sbf, start=True, stop=True),
                     reads=[s.BCU[n], s.SbfU], writes=[g.bU[3]])
            P.pe(lambda e, b=b: e.matmul(bank(g, 4)[:, 0:256], lhsT=s.Btok[:, b, :], rhs=s.XB, start=True, stop=True),
                 reads=[s.BtokU[n], s.XBU], writes=[g.bU[4]])
            if b > 0:
                P.dve(lambda e, b=b: e.tensor_tensor(out=s.t1.rearrange("p (h q) -> p h q", h=4),
                                                     in0=bank(g, 3)[:, 0:256].rearrange("p (h q) -> p h q", h=4),
                                                     in1=bch(s.eacs, 64, b), op=ALU.mult), reads=[g.bU[3]] + du, writes=[s.t1U])
                P.dve(lambda e: e.tensor_tensor(out=s.t2, in0=bank(g, 2)[:, 0:256], in1=s.t1, op=ALU.add), reads=[g.bU[2], s.t1U], writes=[s.t2U])
            else:
                P.dve(lambda e: e.tensor_copy(s.t2, bank(g, 2)[:, 0:256]), reads=[g.bU[2]], writes=[s.t2U])
            P.dve(lambda e: e.tensor_tensor(out=s.t1.rearrange("p (h q) -> p h q", h=4), in0=x4,
                                            in1=s.Dsk[:, hs4].unsqueeze(2).to_broadcast([128, 4, 64]), op=ALU.mult),
                  reads=[s.xtokU[n]] + sm, writes=[s.t1U])
            P.dve(lambda e: e.tensor_tensor(out=s.t2, in0=s.t2, in1=s.t1, op=ALU.add), reads=[s.t1U, s.t2U], writes=[s.t2U])
            P.dve(lambda e: e.tensor_tensor(out=s.ytmp, in0=s.t2, in1=s.zs, op=ALU.mult), reads=[s.t2U, s.zsU], writes=[s.ytmpU])
            for j in range(2):
                P.pe(lambda e, j=j: e.transpose(bank(g, 5)[:, j * 128:(j + 1) * 128], s.ytmp[:, j * 128:(j + 1) * 128], ident[:]),
                     reads=[s.ytmpU, g.cU], writes=[g.bU[5]])
            P.act(lambda e, blk=blk: e.copy(s.yTa[:, 2 * grp:2 * grp + 2, blk], bank(g, 5)[:, 0:256].rearrange("p (j t) -> p j t", j=2)),
                  reads=[g.bU[5]], writes=[s.yTaU[2 * grp][n], s.yTaU[2 * grp + 1][n]])
            if b == 0:
                P.dve(lambda e: e.tensor_copy(s.S, bank(g, 4)[:, 0:256]), reads=[g.bU[4]], writes=[s.SU])
            else:
                P.dve(lambda e, b=b: e.tensor_tensor(out=s.S.rearrange("p (h q) -> p h q", h=4), in0=s.S.rearrange("p (h q) -> p h q", h=4),
                                                     in1=bch(s.edl, 64, b), op=ALU.mult), reads=[s.SU] + du, writes=[s.SU])
                P.dve(lambda e: e.tensor_tensor(out=s.S, in0=s.S, in1=bank(g, 4)[:, 0:256], op=ALU.add), reads=[s.SU, g.bU[4]], writes=[s.SU])
            if b + 1 < NB:
                P.act(lambda e: e.copy(s.Sbf, s.S), reads=[s.SU], writes=[s.SbfU])
            backs.append(P.end_capture())
        P.replay_merged(fronts[0], [])
        for b in range(NB):
            P.replay_merged(fronts[b + 1] if b + 1 < NB else [], backs[b])

    P.barrier()
    for n in range(NT):
        sl = slice(n * 512, (n + 1) * 512)
        rms_rstd_tile(g, lambda k, sl=sl: s.yTa[:, k, sl], lambda k, n=n: [s.yTaU[k][n]], 8, 1024)
        P.dve(lambda e, sl=sl: e.tensor_copy(s.rstd_all[:, sl], g.rstd_t[:]), reads=[g.rstdU], writes=[s.rstdallU])
    cnt = 0

    def load_wo(m):
        wi = m % 2
        P.dma("pool", lambda e: e.dma_start(out=s.wo2[wi].rearrange("p c j -> p (c j)"), in_=g.d["ev_w_out_t"][ei, m, :, 0:1024]),
              f"s_wo{wi}", writes=[s.woU2[wi]])
        P.dve(lambda e: e.tensor_tensor(out=s.wo2[wi], in0=s.wo2[wi], in1=s.snw[:, 0:8].unsqueeze(2).to_broadcast([128, 8, 128]), op=ALU.mult),
              reads=[s.woU2[wi]] + sm, writes=[s.woU2[wi]])

    load_wo(0)
    for m in range(KC):
        if m + 1 < KC:
            load_wo(m + 1)
        wi = m % 2
        for n in range(NT):
            sl = slice(n * 512, (n + 1) * 512)
            bk = cnt % 2
            ti = cnt % 2
            cnt += 1
            for k in range(8):
                P.pe(lambda e, bk=bk, k=k, sl=sl: e.matmul(bank(g, bk), lhsT=s.wo2[wi][:, k, :], rhs=s.yTa[:, k, sl], start=(k == 0), stop=(k == 7)),
                     reads=[s.woU2[wi], s.yTaU[k][n]], writes=[g.bU[bk]])
            P.dve(lambda e, bk=bk, sl=sl, ti=ti: e.tensor_tensor(out=s.tmp2[ti], in0=bank(g, bk), in1=s.rstd_all[:, sl], op=ALU.mult),
                  reads=[g.bU[bk], s.rstdallU], writes=[s.tmpU2[ti]])
            P.pool(lambda e, m=m, sl=sl, ti=ti: e.tensor_tensor(out=g.hT[:, m, sl], in0=g.hT[:, m, sl], in1=s.tmp2[ti], op=ALU.add),
                   reads=[s.tmpU2[ti], g.hU[m][n]], writes=[g.hU[m][n]])


def even_attn(g, li):
    P, L, NB, NT = g.P, g.L, g.NB, g.NT
    ei = li // 2
    lam_init = 0.8 - 0.6 * math.exp(-0.3 * li)
    c = g.cst
    A = Carver(g)
    a = NS()
    a.corr = A.f(2048).rearrange("p (h d q) -> p h d q", h=8, d=2); a.corrU = P.unit()
    a.b31 = A.f(8); a.nb31 = A.f(8); a.bU_ = P.unit()
    a.lq = [A.f(64) for _ in range(4)]; a.lamU = P.unit()
    a.lam = A.f(8)
    a.slnw = A.f(128); a.slnwU = P.unit()
    a.w = [A.b(KC * 512).rearrange("p (k c) -> p k c", k=KC) for _ in range(2)]; a.wU = P.units(2)
    a.qT = [A.b(L) for _ in range(2)]; a.qTU = [P.unit() for _ in range(NT)]
    a.kT = A.b(L); a.kTU = [P.unit() for _ in range(NT)]
    a.v = A.b(NB * 132).rearrange("p (b c) -> p b c", c=132); a.vU = P.unit()
    a.gs = A.b(NB * 128).rearrange("p (b c) -> p b c", c=128); a.gsU = P.unit()
    a.PT = [[A.b(512) for _ in range(2)] for _ in range(2)]; a.PTU = [P.units(2), P.units(2)]
    a.yT = A.b(4 * L).rearrange("p (j t) -> p j t", j=4); a.yTU = P.units(4)
    a.wo = [A.b(512).rearrange("p (j c) -> p j c", j=4) for _ in range(2)]; a.woU = P.units(2)
    ident = c["ident"]

    P.dma("sp", lambda e: e.dma_start(out=a.corr.rearrange("p h d q -> p (h d q)"), in_=g.d["rel_biasD"]), "a_corr", writes=[a.corrU])
    P.dma("sp", lambda e: e.dma_start(out=a.b31, in_=g.d["rel_b31"].partition_broadcast(128)), "a_b31", writes=[a.bU_])
    P.dve(lambda e: e.tensor_scalar(out=a.nb31, in0=a.b31, scalar1=-1.0, scalar2=None, op0=ALU.mult), reads=[a.bU_], writes=[a.bU_])
    for h in range(8):
        P.act(lambda e, h=h: e.activation(out=a.corr[:, h], in_=a.corr[:, h], func=AF.Exp, bias=a.nb31[:, h:h + 1]),
              reads=[a.corrU, a.bU_], writes=[a.corrU])
    for i, nm in enumerate(("lambda_q1", "lambda_k1", "lambda_q2", "lambda_k2")):
        P.dma("sp", lambda e, i=i, nm=nm: e.dma_start(out=a.lq[i], in_=g.d[nm][ei].partition_broadcast(128)), "a_lam", writes=[a.lamU])
    lu = [a.lamU]
    P.dve(lambda e: e.tensor_tensor(out=a.lq[0], in0=a.lq[0], in1=a.lq[1], op=ALU.mult), reads=lu, writes=lu)
    P.dve(lambda e: e.tensor_tensor(out=a.lq[2], in0=a.lq[2], in1=a.lq[3], op=ALU.mult), reads=lu, writes=lu)
    P.dve(lambda e: e.tensor_reduce(out=a.lam[:, 0:1], in_=a.lq[0], axis=AX.X, op=ALU.add), reads=lu, writes=lu)
    P.dve(lambda e: e.tensor_reduce(out=a.lam[:, 1:2], in_=a.lq[2], axis=AX.X, op=ALU.add), reads=lu, writes=lu)
    P.act(lambda e: e.activation(out=a.lam[:, 2:4], in_=a.lam[:, 0:2], func=AF.Exp), reads=lu, writes=lu)
    P.dve(lambda e: e.tensor_tensor(out=a.lam[:, 4:5], in0=a.lam[:, 3:4], in1=a.lam[:, 2:3], op=ALU.subtract), reads=lu, writes=lu)
    P.dve(lambda e: e.tensor_scalar(out=a.lam[:, 5:6], in0=a.lam[:, 4:5], scalar1=-lam_init, scalar2=None, op0=ALU.add), reads=lu, writes=lu)
    P.dma("sp", lambda e: e.dma_start(out=a.slnw, in_=g.d["subln_w"][ei].partition_broadcast(128)), "a_slnw", writes=[a.slnwU])
    P.dve(lambda e: e.tensor_scalar(out=a.slnw, in0=a.slnw, scalar1=1.0 - lam_init, scalar2=None, op0=ALU.mult),
          reads=[a.slnwU], writes=[a.slnwU])
    P.dve(lambda e: e.memset(a.v, 1.0), writes=[a.vU])

    def load_w(h):
        i = h % 2
        P.dma("pool", lambda e: e.dma_start(out=a.w[i].rearrange("p k c -> p (k c)"), in_=g.d["ev_w_att"][ei, h]),
              f"a_w{i}", writes=[a.wU[i]])

    pc = [0]

    def project(h):
        wi = h % 2
        w = a.w[wi]
        for n in range(NT):
            sl = slice(n * 512, (n + 1) * 512)
            for which in range(2):
                bk = pc[0] % 2
                pc[0] += 1
                for k in range(KC):
                    P.pe(lambda e, bk=bk, k=k, sl=sl, which=which: e.matmul(bank(g, bk), lhsT=w[:, k, which * 128:(which + 1) * 128],
                                                                            rhs=g.uT[:, k, sl], start=(k == 0), stop=(k == KC - 1)),
                         reads=[g.uU[n], a.wU[wi]], writes=[g.bU[bk]])
                if which == 0:
                    for cc in range(2):
                        P.dve(lambda e, bk=bk, sl=sl, cc=cc: e.tensor_scalar(out=a.qT[cc][:, sl], in0=bank(g, bk), scalar1=c["maskq"][:, cc:cc + 1],
                                                                             scalar2=None, op0=ALU.mult),
                              reads=[g.bU[bk], g.cU], writes=[a.qTU[n]])
                else:
                    P.act(lambda e, bk=bk, sl=sl: e.copy(a.kT[:, sl], bank(g, bk)), reads=[g.bU[bk]], writes=[a.kTU[n]])
        for b in range(NB):
            bk = 2 + (b % 2)
            for k in range(KC):
                P.pe(lambda e, bk=bk, k=k, b=b: e.matmul(bank(g, bk)[:, 0:256], lhsT=g.uT[:, k, b * 128:(b + 1) * 128], rhs=w[:, k, 256:512],
                                                         start=(k == 0), stop=(k == KC - 1)),
                     reads=[g.uU[b // 4], a.wU[wi]], writes=[g.bU[bk]])
            P.act(lambda e, bk=bk, b=b: e.copy(a.v[:, b, 0:128], bank(g, bk)[:, 0:128]), reads=[g.bU[bk]], writes=[a.vU])
            P.act(lambda e, bk=bk, b=b: e.activation(out=a.gs[:, b, :], in_=bank(g, bk)[:, 128:256], func=AF.Silu),
                  reads=[g.bU[bk]], writes=[a.gsU])

    gc = [0]
    a.r2 = [A.f(8) for _ in range(2)]; a.rU2 = P.units(2)
    a.t2 = [A.f(128) for _ in range(2)]; a.tU2 = P.units(2)
    a.o2 = [A.f(128) for _ in range(2)]; a.oU2 = P.units(2)
    a.sqo2 = [A.f(128) for _ in range(2)]; a.sqoU2 = P.units(2)
    a.y2 = [A.f(128) for _ in range(2)]; a.yU2 = P.units(2)

    def attend(h):
        hj = h % 4
        groups = []
        for qb in range(NB):
            for gi in range(qb // 4 + 1):
                kbs = [kb for kb in range(gi * 4, gi * 4 + 4) if kb <= qb]
                groups.append((qb, gi, kbs, gc[0] % 2))
                gc[0] += 1

        def accb(qb, cc):
            return (2 + cc) if qb % 2 == 0 else cc

        def S_(grp):
            qb, gi, kbs, buf = grp
            qs = slice(qb * 128, (qb + 1) * 128)
            for cc in range(2):
                bk = 4 + 2 * cc + buf
                for j, kb in enumerate(kbs):
                    P.pe(lambda e, bk=bk, j=j, kb=kb, cc=cc: e.matmul(bank(g, bk)[:, j * 128:(j + 1) * 128],
                                                                      lhsT=a.kT[:, kb * 128:(kb + 1) * 128], rhs=a.qT[cc][:, qs],
                                                                      start=True, stop=True),
                         reads=[a.kTU[kb // 4], a.qTU[qb // 4]], writes=[g.bU[bk]])

        def E_(grp):
            qb, gi, kbs, buf = grp
            nv = len(kbs)
            for cc in range(2):
                bk = 4 + 2 * cc + buf
                pt = a.PT[cc][buf]
                ptu = a.PTU[cc][buf]
                P.act(lambda e, bk=bk, pt=pt, nv=nv: e.activation(out=pt[:, 0:nv * 128], in_=bank(g, bk)[:, 0:nv * 128], func=AF.Exp,
                                                                  scale=0.125, bias=a.b31[:, h:h + 1]),
                      reads=[g.bU[bk], a.bU_], writes=[ptu])
                for j, kb in enumerate(kbs):
                    Dd = qb - kb
                    if Dd <= 1:
                        P.dve(lambda e, pt=pt, j=j, Dd=Dd: e.tensor_tensor(out=pt[:, j * 128:(j + 1) * 128], in0=pt[:, j * 128:(j + 1) * 128],
                                                                           in1=a.corr[:, h, Dd, :], op=ALU.mult),
                              reads=[ptu, a.corrU], writes=[ptu])

        def PV_(grp):
            qb, gi, kbs, buf = grp
            for cc in range(2):
                pt = a.PT[cc][buf]
                ptu = a.PTU[cc][buf]
                ab = accb(qb, cc)
                for j, kb in enumerate(kbs):
                    P.pe(lambda e, ab=ab, pt=pt, j=j, kb=kb: e.matmul(bank(g, ab)[:, 0:129], lhsT=pt[:, j * 128:(j + 1) * 128],
                                                                      rhs=a.v[:, kb, 0:129], start=(kb == 0), stop=(kb == qb)),
                         reads=[ptu, a.vU], writes=[g.bU[ab]])

        def FIN_(qb):
            qs = slice(qb * 128, (qb + 1) * 128)
            pq = qb % 2
            b0, b1 = accb(qb, 0), accb(qb, 1)
            r, t_, o_, sqo, y_ = a.r2[pq], a.t2[pq], a.o2[pq], a.sqo2[pq], a.y2[pq]
            ru = [a.rU2[pq]]
            tU, oU, sqoU, yU = a.tU2[pq], a.oU2[pq], a.sqoU2[pq], a.yU2[pq]
            P.dve(lambda e: e.reciprocal(r[:, 0:1], bank(g, b0)[:, 128:129]), reads=[g.bU[b0]], writes=ru)
            P.dve(lambda e: e.reciprocal(r[:, 1:2], bank(g, b1)[:, 128:129]), reads=[g.bU[b1]], writes=ru)
            P.dve(lambda e: e.tensor_tensor(out=r[:, 2:3], in0=r[:, 1:2], in1=a.lam[:, 5:6], op=ALU.mult), reads=ru + lu, writes=ru)
            P.dve(lambda e: e.tensor_scalar(out=t_, in0=bank(g, b0)[:, 0:128], scalar1=r[:, 0:1], scalar2=None, op0=ALU.mult),
                  reads=[g.bU[b0]] + ru, writes=[tU])
            P.dve(lambda e: e.scalar_tensor_tensor(out=o_, in0=bank(g, b1)[:, 0:128], scalar=r[:, 2:3], in1=t_, op0=ALU.mult, op1=ALU.add),
                  reads=[g.bU[b1], tU] + ru, writes=[oU])
            P.dve(lambda e: e.tensor_tensor(out=sqo, in0=o_, in1=o_, op=ALU.mult), reads=[oU], writes=[sqoU])
            P.dve(lambda e: e.tensor_reduce(out=r[:, 3:4], in_=sqo, axis=AX.X, op=ALU.add), reads=[sqoU], writes=ru)
            P.act(lambda e: e.activation(out=r[:, 4:5], in_=r[:, 3:4], func=AF.Ln, scale=1.0 / 128, bias=EPS), reads=ru, writes=ru)
            P.act(lambda e: e.activation(out=r[:, 5:6], in_=r[:, 4:5], func=AF.Exp, scale=-0.5), reads=ru, writes=ru)
            P.dve(lambda e: e.scalar_tensor_tensor(out=y_, in0=o_, scalar=r[:, 5:6], in1=a.slnw, op0=ALU.mult, op1=ALU.mult),
                  reads=[oU, a.slnwU] + ru, writes=[yU])
            P.dve(lambda e: e.tensor_tensor(out=y_, in0=y_, in1=a.gs[:, qb, :], op=ALU.mult), reads=[yU, a.gsU], writes=[yU])
            P.pe(lambda e: e.transpose(bank(g, b0)[:, 256:384], y_, ident[:]), reads=[yU, g.cU], writes=[g.bU[b0]])
            P.act(lambda e: e.copy(a.yT[:, hj, qs], bank(g, b0)[:, 256:384]), reads=[g.bU[b0]], writes=[a.yTU[hj]])

        M = len(groups)
        S_(groups[0])
        pending_fin = None
        for i in range(M):
            if i + 1 < M:
                S_(groups[i + 1])
            E_(groups[i])
            PV_(groups[i])
            if pending_fin is not None:
                FIN_(pending_fin)
                pending_fin = None
            qb, gi, kbs, buf = groups[i]
            if kbs[-1] == qb:
                pending_fin = qb
        if pending_fin is not None:
            FIN_(pending_fin)

    oc = [0]

    def outproj(hg):
        for m in range(KC):
            wi = oc[0] % 2
            oc[0] += 1
            c0 = (8 + hg * 4) * 128
            P.dma("pool", lambda e, m=m, wi=wi, c0=c0: e.dma_start(out=a.wo[wi].rearrange("p j c -> p (j c)"),
                                                                  in_=g.d["ev_w_out_t"][ei, m, :, c0:c0 + 512]),
                  f"a_wo{wi}", writes=[a.woU[wi]])
            for n in range(NT):
                sl = slice(n * 512, (n + 1) * 512)
                bk = n % 2
                for j in range(4):
                    P.pe(lambda e, bk=bk, j=j, sl=sl, wi=wi: e.matmul(bank(g, bk), lhsT=a.wo[wi][:, j, :], rhs=a.yT[:, j, sl],
                                                                      start=(j == 0), stop=(j == 3)),
                         reads=[a.woU[wi], a.yTU[j]], writes=[g.bU[bk]])
                P.dve(lambda e, bk=bk, m=m, sl=sl: e.tensor_tensor(out=g.hT[:, m, sl], in0=g.hT[:, m, sl], in1=bank(g, bk), op=ALU.add),
                      reads=[g.bU[bk], g.hU[m][n]], writes=[g.hU[m][n]])

    load_w(0)
    for h in range(8):
        if h + 1 < 8:
            load_w(h + 1)
        project(h)
        attend(h)
        if h % 4 == 3:
            outproj(h // 4)


_CACHE = {}


def kernel(**inputs):
    x = np.ascontiguousarray(np.asarray(inputs["x"], dtype=np.float32))
    Bsz, L, _ = x.shape
    n_cores = 8
    nseq = Bsz // n_cores
    key = (L, nseq)
    if key not in _CACHE:
        _CACHE[key] = build(L, nseq, (0, 1, 2, 3))
    nc, _ = _CACHE[key]
    common = host_layout(inputs)
    in_maps = []
    for cidx in range(n_cores):
        m = dict(common)
        m["x"] = x[cidx * nseq:(cidx + 1) * nseq]
        in_maps.append(m)
    res = run_bass_kernel_spmd(nc, in_maps, core_ids=list(range(n_cores)))
    out = np.concatenate([np.asarray(r["out"]) for r in res.results], axis=0)
    return out.astype(np.float32)
```

```python
import math, contextlib
import numpy as np
import concourse.bass as bass
import concourse.mybir as mybir
from concourse.bass_utils import run_bass_kernel_spmd
from concourse.alu_op_type import AluOpType as ALU

F32 = mybir.dt.float32
BF16 = mybir.dt.bfloat16
AF = mybir.ActivationFunctionType
AX = mybir.AxisListType

D = 1024
KC = 8
EPS = 1e-6
DEPTH = 4
HG_W = 2048
ARF_N = 11392
ARB_N = 35840


class Unit:
    __slots__ = ("name", "lw", "rd")

    def __init__(self, name):
        self.name = name
        self.lw = None
        self.rd = []


class _Rec:
    def __init__(self):
        self.call = None

    def __getattr__(self, name):
        def f(*args, **kw):
            assert self.call is None
            self.call = (name, args, kw)
            return None
        return f


class Prog:
    ENGS = ("pe", "act", "dve", "pool", "sp")

    def __init__(self, nc):
        self.nc = nc
        self.ops = []
        self.nunits = 0
        self.last_eng = {}
        self.last_key = {}

    def unit(self, name=None):
        self.nunits += 1
        return Unit(name or f"u{self.nunits}")

    def units(self, n, name="u"):
        return [self.unit(f"{name}{i}") for i in range(n)]

    def capture(self):
        self._cap = []
        return self._cap

    def end_capture(self):
        c, self._cap = self._cap, None
        return c

    def replay_merged(self, A, B):
        na, nb = len(A), len(B)
        ia = ib = 0
        while ia < na or ib < nb:
            if ib >= nb or (ia < na and ia * nb <= ib * na):
                self.op(*A[ia]); ia += 1
            else:
                self.op(*B[ib]); ib += 1

    def op(self, eng, fn, reads=(), writes=(), dma_key=None, extra_deps=()):
        if fn is not None and not isinstance(fn, tuple):
            rec = _Rec()
            fn(rec)
            assert rec.call is not None
            fn = rec.call
        if getattr(self, "_cap", None) is not None:
            self._cap.append((eng, fn, tuple(reads), tuple(writes), dma_key, tuple(extra_deps)))
            return None
        idx = len(self.ops)
        deps = set(extra_deps)
        for u in reads:
            if u.lw is not None:
                deps.add(u.lw)
        for u in writes:
            if u.lw is not None:
                deps.add(u.lw)
            deps.update(u.rd)
        for u in reads:
            u.rd.append(idx)
        for u in writes:
            u.lw = idx
            u.rd = []
        deps.discard(idx)
        self.ops.append(dict(eng=eng, fn=fn, deps=deps, dma_key=dma_key))
        if fn is not None:
            if dma_key is None:
                self.last_eng[eng] = idx
            else:
                self.last_key[dma_key] = idx
        return idx

    def pe(self, fn, reads=(), writes=()):
        return self.op("pe", fn, reads, writes)

    def act(self, fn, reads=(), writes=()):
        return self.op("act", fn, reads, writes)

    def dve(self, fn, reads=(), writes=()):
        return self.op("dve", fn, reads, writes)

    def pool(self, fn, reads=(), writes=()):
        return self.op("pool", fn, reads, writes)

    def dma(self, eng, fn, key, reads=(), writes=()):
        return self.op(eng, fn, reads, writes, dma_key=key)

    def barrier(self):
        deps = set(self.last_eng.values()) | set(self.last_key.values())
        for e in self.ENGS:
            self.op(e, None, extra_deps=deps)

    def emit(self, final_wait_ops=()):
        nc = self.nc
        ops = self.ops
        n = len(ops)

        def skip(od, o):
            return (od["eng"] == "pe" and o["eng"] == "pe" and od["dma_key"] is None
                    and o["dma_key"] is None and o["fn"] is not None)

        needed = [False] * n
        for i, o in enumerate(ops):
            for d in o["deps"]:
                if skip(ops[d], o):
                    continue
                needed[d] = True
        for d in final_wait_ops:
            needed[d] = True
        chan_count = {}
        ev = [None] * n
        for i, o in enumerate(ops):
            if o["fn"] is None:
                continue
            if o["dma_key"] is not None:
                ch = ("dma", o["dma_key"])
                chan_count[ch] = chan_count.get(ch, 0) + 16
                ev[i] = (ch, chan_count[ch])
            elif needed[i]:
                ch = ("eng", o["eng"])
                chan_count[ch] = chan_count.get(ch, 0) + 1
                ev[i] = (ch, chan_count[ch])
        chans = sorted(chan_count.keys(), key=str)
        self.n_sems = len(chans)
        sems = {}
        stack = contextlib.ExitStack()
        for ci, ch in enumerate(chans):
            sems[ch] = stack.enter_context(nc.semaphore(f"s{ci}"))
        known = {e: {} for e in self.ENGS}
        clock = [None] * n
        streams = {e: [] for e in self.ENGS}
        for i, o in enumerate(ops):
            e = o["eng"]
            kn = known[e]
            wd = {}
            for d in sorted(o["deps"]):
                od = ops[d]
                if skip(od, o):
                    continue
                ch, v = ev[d]
                if kn.get(ch, 0) >= v:
                    continue
                for c2, v2 in clock[d].items():
                    if kn.get(c2, 0) < v2:
                        kn[c2] = v2
                wd[ch] = max(wd.get(ch, 0), v)
            ck = dict(kn)
            if ev[i] is not None:
                ch, v = ev[i]
                ck[ch] = v
            clock[i] = ck
            streams[e].append((list(wd.items()), o["fn"], ev[i]))
        final = [ev[d] for d in final_wait_ops]
        for ch, tot in chan_count.items():
            if ch[0] == "dma":
                final.append((ch, tot))
        self.sems, self.streams, self.final, self._stack = sems, streams, final, stack

    def run_block(self):
        nc = self.nc
        sems, streams, final = self.sems, self.streams, self.final
        with nc.Block() as block:
            def mk(ename):
                def body(eng):
                    for waits, fn, e in streams[ename]:
                        for ch, v in waits:
                            eng.wait_ge(sems[ch], v)
                        if fn is None:
                            continue
                        ins = getattr(eng, fn[0])(*fn[1], **fn[2])
                        if e is not None:
                            ins.then_inc(sems[e[0]], 16 if e[0][0] == "dma" else 1)
                    if ename == "sp":
                        for ch, v in final:
                            eng.wait_ge(sems[ch], v)
                return body
            block.tensor(mk("pe"))
            block.scalar(mk("act"))
            block.vector(mk("dve"))
            block.gpsimd(mk("pool"))
            block.sync(mk("sp"))
        self._stack.close()


def _t5_bucket(rel):
    n = np.maximum(rel, 0)
    max_exact = 16
    large = max_exact + (np.log(np.maximum(n, 1).astype(np.float32) / max_exact)
                         / math.log(128 / max_exact) * (32 - max_exact)).astype(np.int32)
    large = np.minimum(large, 31)
    return np.where(n < max_exact, n, large)


def host_consts():
    s = np.arange(128)[:, None]
    t = np.arange(128)[None, :]
    c = {}
    c["ident"] = np.eye(128, dtype=np.float32)
    c["ones"] = np.ones((128, 128), np.float32)
    c["triC"] = ((s <= t).astype(np.float32) - (s <= 63).astype(np.float32))
    c["triU"] = (s > t).astype(np.float32)
    c["triI"] = (s <= t).astype(np.float32)
    sel = np.zeros((128, 2), np.float32)
    sel[:64, 0] = 1.0
    sel[:, 1] = 1.0
    c["sel"] = sel
    c["mg16"] = np.eye(16, dtype=np.float32)
    mq = np.zeros((128, 2), np.float32)
    mq[:64, 0] = 1.0
    mq[64:, 1] = 1.0
    c["maskq"] = mq
    return c


def host_layout(inp):
    f = lambda a: np.ascontiguousarray(np.asarray(a, dtype=np.float32))
    m = dict(host_consts())
    m["final_norm_w"] = f(inp["final_norm_w"])
    m["norm_w_cols"] = f(np.asarray(inp["norm_w"]).reshape(4, 8, 128).transpose(0, 2, 1))
    owin = np.asarray(inp["odd_w_in"])
    t = owin.reshape(2, 8, 128, 4, 16, 128).transpose(0, 4, 2, 1, 3, 5)
    m["odd_w_in_t"] = f(t).reshape(2, 16, 128, 8 * 512)
    m["odd_w_out"] = f(inp["odd_w_out"])
    m["hgrn_lower_bounds"] = f(inp["hgrn_lower_bounds"])
    m["hgrn_norm_w"] = f(inp["hgrn_norm_w"])
    ew = np.asarray(inp["even_w_in"]).reshape(2, 8, 128, 7184)
    z = ew[..., 0:1024]; xs = ew[..., 1024:2048]; Bm = ew[..., 2048:2560]; Cm = ew[..., 2560:3072]
    dt = ew[..., 3072:3088]
    q = ew[..., 3088:4112]; kk = ew[..., 4112:5136]; v = ew[..., 5136:6160]; gg = ew[..., 6160:7184]
    ssd = np.concatenate([z.reshape(2, 8, 128, 4, 256), xs.reshape(2, 8, 128, 4, 256),
                          Bm.reshape(2, 8, 128, 4, 128), Cm.reshape(2, 8, 128, 4, 128)], axis=-1)
    m["ev_w_ssd"] = f(ssd.transpose(0, 3, 2, 1, 4)).reshape(2, 4, 128, 8 * 768)
    m["ev_w_dt"] = f(dt.transpose(0, 2, 1, 3)).reshape(2, 128, 8 * 16)
    att = np.concatenate([q.reshape(2, 8, 128, 8, 128), kk.reshape(2, 8, 128, 8, 128),
                          v.reshape(2, 8, 128, 8, 128), gg.reshape(2, 8, 128, 8, 128)], axis=-1)
    m["ev_w_att"] = f(att.transpose(0, 3, 2, 1, 4)).reshape(2, 8, 128, 8 * 512)
    wo = np.asarray(inp["even_w_out"]).reshape(2, 16, 128, 8, 128)
    m["ev_w_out_t"] = f(wo.transpose(0, 3, 2, 1, 4)).reshape(2, 8, 128, 16 * 128)
    m["conv_w_cols"] = f(np.asarray(inp["conv_w"]).reshape(2, 4, 16, 128).transpose(0, 3, 2, 1)).reshape(2, 128, 64)
    m["conv_b_cols"] = f(np.asarray(inp["conv_b"]).reshape(2, 16, 128).transpose(0, 2, 1))
    for nm in ("dt_bias", "A_log", "D_skip", "lambda_q1", "lambda_k1", "lambda_q2", "lambda_k2", "subln_w"):
        m[nm] = f(inp[nm])
    m["ssd_norm_w_cols"] = f(np.asarray(inp["ssd_norm_w"]).reshape(2, 8, 128).transpose(0, 2, 1))
    rb = np.asarray(inp["rel_bias"], dtype=np.float32)
    kpos = np.arange(128)[:, None]
    qpos = np.arange(128)[None, :]
    bd = np.empty((128, 8, 2, 128), np.float32)
    for Dd in range(2):
        rel = qpos - kpos + 128 * Dd
        bidx = _t5_bucket(rel)
        g_ = rb[bidx]
        g_ = np.where((rel >= 0)[:, :, None], g_, np.float32(-30000.0))
        bd[:, :, Dd, :] = g_.transpose(0, 2, 1)
    m["rel_biasD"] = f(bd).reshape(128, 8 * 2 * 128)
    m["rel_b31"] = f(rb[31])
    return m


class NS:
    pass


class Carver:
    def __init__(self, g):
        self.g = g
        self.fo = 0
        self.bo = 0

    def f(self, n):
        ap = self.g.arf[:, self.fo:self.fo + n]
        self.fo += (n + 7) // 8 * 8
        assert self.fo <= ARF_N, ("ARF overflow", self.fo)
        return ap

    def b(self, n):
        ap = self.g.arb[:, self.bo:self.bo + n]
        self.bo += (n + 15) // 16 * 16
        assert self.bo <= ARB_N, ("ARB overflow", self.bo)
        return ap


def bank(g, i):
    return g.ps[:, i, :]


def build(L=2048, NSEQ=2, layers=(0, 1, 2, 3)):
    nc = bass.Bass("TRN2", target_bir_lowering=False, dynamic_dma_scratch_size=4096)
    NT, NB = L // 512, L // 128
    g = NS()
    g.nc, g.L, g.NT, g.NB = nc, L, NT, NB
    dr = lambda name, shape, kind="ExternalInput": nc.dram_tensor(name, list(shape), F32, kind=kind).ap()
    g.x_d = dr("x", [NSEQ, L, D])
    g.out_d = dr("out", [NSEQ, L, D], "ExternalOutput")
    g.d = {}
    shapes = {
        "final_norm_w": [D], "norm_w_cols": [DEPTH, 128, KC],
        "ident": [128, 128], "ones": [128, 128], "triC": [128, 128], "triU": [128, 128], "triI": [128, 128],
        "sel": [128, 2], "mg16": [16, 16], "maskq": [128, 2],
        "odd_w_in_t": [2, 16, 128, KC * 512], "odd_w_out": [2, HG_W, D], "hgrn_lower_bounds": [DEPTH, HG_W],
        "hgrn_norm_w": [2, 128],
        "ev_w_ssd": [2, 4, 128, 8 * 768], "ev_w_dt": [2, 128, 8 * 16], "ev_w_att": [2, 8, 128, 8 * 512],
        "ev_w_out_t": [2, 8, 128, 16 * 128], "conv_w_cols": [2, 128, 64], "conv_b_cols": [2, 128, 16],
        "dt_bias": [2, 16], "A_log": [2, 16], "D_skip": [2, 16], "lambda_q1": [2, 64], "lambda_k1": [2, 64],
        "lambda_q2": [2, 64], "lambda_k2": [2, 64], "subln_w": [2, 128], "ssd_norm_w_cols": [2, 128, 8],
        "rel_biasD": [128, 8 * 2 * 128], "rel_b31": [8],
    }
    for nm, shp in shapes.items():
        g.d[nm] = dr(nm, shp)
    g.in_names = ["x"] + list(shapes.keys())

    es = contextlib.ExitStack()
    sb = lambda name, shape, dt=F32: es.enter_context(nc.sbuf_tensor(name, list(shape), dt))
    P = Prog(nc)
    g.P = P
    g.hT = sb("hT", [128, KC, L]); g.hU = [[P.unit(f"h{k}_{n}") for n in range(NT)] for k in range(KC)]
    g.uT = sb("uT", [128, KC, L], BF16); g.uU = [P.unit(f"u{n}") for n in range(NT)]
    g.cst = {}
    g.cU = P.unit("consts")
    for nm in ("ident", "ones", "triI"):
        g.cst[nm] = sb("c_" + nm, [128, 128])
    g.cst["sel"] = sb("c_sel", [128, 2])
    g.cst["maskq"] = sb("c_maskq", [128, 2])
    g.cst["mg16"] = sb("c_mg16", [16, 16])
    g.nwc = sb("nwc", [128, DEPTH, KC])
    g.stat = sb("stat", [128, 16]); g.statU = P.unit("stat")
    g.sq = [sb(f"sq{i}", [128, 512]) for i in range(2)]; g.sqU = P.units(2, "sq")
    g.rstd_t = sb("rstd_t", [128, 512]); g.rstdU = P.unit("rstd_t")
    g.arf = sb("arf", [128, ARF_N])
    g.arb = sb("arb", [128, ARB_N], BF16)
    g.ps = es.enter_context(nc.psum_tensor("ps", [128, 8, 512], F32))
    g.bU = P.units(8, "bank")

    for nm in ("ident", "ones", "triI", "sel", "maskq", "mg16"):
        P.dma("sp", lambda e, nm=nm: e.dma_start(out=g.cst[nm][:], in_=g.d[nm]), "c_" + nm, writes=[g.cU])
    P.dma("sp", lambda e: e.dma_start(out=g.nwc[:], in_=g.d["norm_w_cols"].rearrange("l p k -> p l k")), "c_nwc", writes=[g.cU])

    out_ops = []
    for s in range(NSEQ):
        P.barrier()
        load_x(g, s)
        P.barrier()
        for li in layers:
            rms_to_uT(g, li)
            if li % 2 == 1:
                odd_layer(g, li)
            else:
                even_ssd(g, li)
                P.barrier()
                even_attn(g, li)
            P.barrier()
        out_ops += final_norm_store(g, s)
    P.emit(final_wait_ops=out_ops[-4:])
    P.run_block()
    es.close()
    return nc, P


def load_x(g, s):
    P, NT = g.P, g.NT
    A = Carver(g)
    xst = A.f(4096).rearrange("p (b d) -> p b d", b=4)
    xU = P.unit("xst")
    ident = g.cst["ident"]
    for n in range(NT):
        src = g.x_d[s, n * 512:(n + 1) * 512, :].rearrange("(b p) d -> p b d", p=128)
        P.dma("sp", lambda e, src=src: e.dma_start(out=xst, in_=src), "xst", writes=[xU])
        for k in range(KC):
            bk = k % 8
            for b in range(4):
                P.pe(lambda e, bk=bk, b=b, k=k: e.transpose(
                    bank(g, bk)[:, b * 128:(b + 1) * 128], xst[:, b, k * 128:(k + 1) * 128], ident[:]),
                    reads=[xU, g.cU], writes=[g.bU[bk]])
            if k % 2 == 0:
                P.dve(lambda e, bk=bk, k=k, n=n: e.tensor_copy(g.hT[:, k, n * 512:(n + 1) * 512], bank(g, bk)),
                      reads=[g.bU[bk]], writes=[g.hU[k][n]])
            else:
                P.act(lambda e, bk=bk, k=k, n=n: e.copy(g.hT[:, k, n * 512:(n + 1) * 512], bank(g, bk)),
                      reads=[g.bU[bk]], writes=[g.hU[k][n]])


def final_norm_store(g, s):
    P, NB = g.P, g.NB
    A = Carver(g)
    fnw = A.f(D); fnwU = P.unit("fnw")
    ost = [A.f(D) for _ in range(2)]; ostU = P.units(2, "ost")
    junk = A.f(1024); junkU = P.unit("junk")
    ident = g.cst["ident"]
    P.dma("sp", lambda e: e.dma_start(out=fnw, in_=g.d["final_norm_w"].partition_broadcast(128)), "fnw", writes=[fnwU])
    outs = []
    for b in range(NB):
        n = b // 4
        oi = b % 2
        for k in range(KC):
            bk = k // 4
            P.pe(lambda e, bk=bk, k=k, b=b: e.transpose(
                bank(g, bk)[:, (k % 4) * 128:(k % 4 + 1) * 128], g.hT[:, k, b * 128:(b + 1) * 128], ident[:]),
                reads=[g.hU[k][n], g.cU], writes=[g.bU[bk]])
        for half in range(2):
            P.act(lambda e, half=half: e.activation(
                out=junk[:, half * 512:(half + 1) * 512], in_=bank(g, half), func=AF.Square,
                accum_out=g.stat[:, half:half + 1]),
                reads=[g.bU[half]], writes=[junkU, g.statU])
        P.dve(lambda e: e.tensor_tensor(out=g.stat[:, 2:3], in0=g.stat[:, 0:1], in1=g.stat[:, 1:2], op=ALU.add),
              reads=[g.statU], writes=[g.statU])
        P.act(lambda e: e.activation(out=g.stat[:, 3:4], in_=g.stat[:, 2:3], func=AF.Ln, scale=1.0 / D, bias=EPS),
              reads=[g.statU], writes=[g.statU])
        P.act(lambda e: e.activation(out=g.stat[:, 4:5], in_=g.stat[:, 3:4], func=AF.Exp, scale=-0.5),
              reads=[g.statU], writes=[g.statU])
        for half in range(2):
            P.dve(lambda e, half=half, oi=oi: e.scalar_tensor_tensor(
                out=ost[oi][:, half * 512:(half + 1) * 512], in0=bank(g, half), scalar=g.stat[:, 4:5],
                in1=fnw[:, half * 512:(half + 1) * 512], op0=ALU.mult, op1=ALU.mult),
                reads=[g.bU[half], g.statU, fnwU], writes=[ostU[oi]])
        o = P.dma("sp", lambda e, oi=oi, s=s, b=b: e.dma_start(out=g.out_d[s, b * 128:(b + 1) * 128, :], in_=ost[oi]),
                  f"ost{oi}", reads=[ostU[oi]])
        outs.append(o)
    return outs


def rms_rstd_tile(g, src_fn, reads_fn, nchunks, dim):
    P = g.P
    ones = g.cst["ones"]
    for k in range(nchunks):
        i = k % 2
        P.act(lambda e, i=i, k=k: e.activation(out=g.sq[i][:], in_=src_fn(k), func=AF.Square),
              reads=reads_fn(k), writes=[g.sqU[i]])
        P.pe(lambda e, i=i, k=k: e.matmul(bank(g, 7), lhsT=ones[:], rhs=g.sq[i][:], start=(k == 0), stop=(k == nchunks - 1)),
             reads=[g.sqU[i], g.cU], writes=[g.bU[7]])
    P.act(lambda e: e.activation(out=g.rstd_t[:], in_=bank(g, 7), func=AF.Ln, scale=1.0 / dim, bias=EPS),
          reads=[g.bU[7]], writes=[g.rstdU])
    P.act(lambda e: e.activation(out=g.rstd_t[:], in_=g.rstd_t[:], func=AF.Exp, scale=-0.5),
          reads=[g.rstdU], writes=[g.rstdU])


def rms_to_uT(g, li):
    P, NT = g.P, g.NT
    for n in range(NT):
        sl = slice(n * 512, (n + 1) * 512)
        rms_rstd_tile(g, lambda k, sl=sl: g.hT[:, k, sl], lambda k, n=n: [g.hU[k][n]], KC, D)
        for k in range(KC):
            P.dve(lambda e, k=k, sl=sl: e.scalar_tensor_tensor(
                out=g.uT[:, k, sl], in0=g.hT[:, k, sl], scalar=g.nwc[:, li, k:k + 1], in1=g.rstd_t[:],
                op0=ALU.mult, op1=ALU.mult),
                reads=[g.hU[k][n], g.rstdU, g.cU], writes=[g.uU[n]])


def odd_layer(g, li):
    P, L, NB, NT = g.P, g.L, g.NB, g.NT
    oi = li // 2
    A = Carver(g)
    o = NS()
    c = g.cst
    f3 = lambda: A.f(512).rearrange("p (b d) -> p b d", b=4)
    f16 = lambda: A.f(NB * 128).rearrange("p (b d) -> p b d", d=128)
    o.fall = f16(); o.fallU = [P.unit() for _ in range(NT)]
    o.kkall = f16(); o.kkallU = [P.unit() for _ in range(NT)]
    o.qsall = f16(); o.qsallU = [P.unit() for _ in range(NT)]
    o.e13 = A.f(1024).rearrange("p (t b d) -> p t b d", t=2, b=4); o.e13U = P.unit()
    o.e2 = f3(); o.e2U = P.unit()
    o.kt = o.e2; o.ktU = o.e2U
    o.lbr = f3(); o.lbrU = P.unit()
    o.lbh = A.f(128); o.omlh = A.f(128); o.den = A.f(128); o.lbU = P.unit()
    o.eb = [A.f(NB * 2).rearrange("p (b t) -> p b t", t=2) for _ in range(2)]
    o.S = A.f(128); o.SU = P.unit()
    o.junk2 = A.f(128); o.junk2U = P.unit()
    o.oall = A.f(NB * 128).rearrange("p (b d) -> p b d", d=128); o.oallU = P.unit()
    o.ssall = A.f(NB); o.rsall = A.f(NB); o.ssU = P.unit()
    o.hnw = A.f(128); o.hnwU = P.unit()
    o.triC = A.f(128); o.triU = A.f(128); o.triUU = P.unit()
    o.bst = A.f(8); o.bstU = P.unit()
    o.w = [A.b(KC * 512).rearrange("p (k c) -> p k c", k=KC) for _ in range(2)]; o.wU = P.units(2)
    o.wout = A.b(2 * D).rearrange("p (j m) -> p j m", j=2); o.woutU = P.units(2)
    o.qT = [A.b(L) for _ in range(2)]
    o.kT = [A.b(L) for _ in range(2)]
    hb3 = lambda: A.b(NB * 128).rearrange("p (b d) -> p b d", d=128)
    o.kh = [hb3() for _ in range(2)]
    o.v = [hb3() for _ in range(2)]
    o.gs = [hb3() for _ in range(2)]
    o.hbU = [[[P.unit() for _ in range(NT)] for _ in range(6)] for _ in range(2)]
    o.attm = [A.b(128) for _ in range(2)]; o.attmU = P.units(2)
    o.Sb2 = [A.b(128) for _ in range(2)]; o.SbU2 = P.units(2)
    o.yT = A.b(2 * L).rearrange("p (j t) -> p j t", j=2); o.yTU = P.units(2)
    QT, KT, KH, VV, GS, EB = range(6)

    P.dma("sp", lambda e: e.dma_start(out=o.hnw, in_=g.d["hgrn_norm_w"][oi].partition_broadcast(128)), "o_hnw", writes=[o.hnwU])
    P.dma("sp", lambda e: e.dma_start(out=o.triC, in_=g.d["triC"]), "o_tri", writes=[o.triUU])
    P.dma("sp", lambda e: e.dma_start(out=o.triU, in_=g.d["triU"]), "o_tri", writes=[o.triUU])
    for i in range(2):
        P.dve(lambda e, i=i: e.memset(o.attm[i], 0.0), writes=[o.attmU[i]])

    def load_w(h):
        i = h % 2
        P.dma("pool", lambda e: e.dma_start(out=o.w[i].rearrange("p k c -> p (k c)"), in_=g.d["odd_w_in_t"][oi, h]),
              f"o_w{i}", writes=[o.wU[i]])

    def head_lb(h):
        hs = slice(h * 128, (h + 1) * 128)
        P.dma("sp", lambda e: e.dma_start(out=o.lbr, in_=g.d["hgrn_lower_bounds"][:, hs].partition_broadcast(128)),
              "o_lbr", writes=[o.lbrU])
        P.act(lambda e: e.activation(out=o.lbr, in_=o.lbr, func=AF.Exp), reads=[o.lbrU], writes=[o.lbrU])
        P.dve(lambda e: e.tensor_tensor(out=o.den, in0=o.lbr[:, 0, :], in1=o.lbr[:, 1, :], op=ALU.add), reads=[o.lbrU], writes=[o.lbU])
        P.dve(lambda e: e.tensor_tensor(out=o.den, in0=o.den, in1=o.lbr[:, 2, :], op=ALU.add), reads=[o.lbrU, o.lbU], writes=[o.lbU])
        P.dve(lambda e: e.tensor_tensor(out=o.den, in0=o.den, in1=o.lbr[:, 3, :], op=ALU.add), reads=[o.lbrU, o.lbU], writes=[o.lbU])
        P.dve(lambda e: e.reciprocal(o.den, o.den), reads=[o.lbU], writes=[o.lbU])
        if li == 1:
            P.dve(lambda e: e.tensor_tensor(out=o.lbh, in0=o.lbr[:, 1, :], in1=o.den, op=ALU.mult), reads=[o.lbrU, o.lbU], writes=[o.lbU])
        else:
            P.dve(lambda e: e.tensor_tensor(out=o.lbh, in0=o.lbr[:, 1, :], in1=o.lbr[:, 2, :], op=ALU.add), reads=[o.lbrU, o.lbU], writes=[o.lbU])
            for j in range(3, li + 1):
                P.dve(lambda e, j=j: e.tensor_tensor(out=o.lbh, in0=o.lbh, in1=o.lbr[:, j, :], op=ALU.add), reads=[o.lbrU, o.lbU], writes=[o.lbU])
            P.dve(lambda e: e.tensor_tensor(out=o.lbh, in0=o.lbh, in1=o.den, op=ALU.mult), reads=[o.lbU], writes=[o.lbU])
        P.dve(lambda e: e.tensor_scalar(out=o.omlh, in0=o.lbh, scalar1=-1.0, scalar2=1.0, op0=ALU.mult, op1=ALU.add),
              reads=[o.lbU], writes=[o.lbU])

    def stageA1(h, n):
        hb = h % 2
        wi = h % 2
        U = o.hbU[hb]
        bs = slice(n * 4, (n + 1) * 4)
        for b in range(4):
            tb = n * 4 + b
            for k in range(KC):
                P.pe(lambda e, b=b, tb=tb, k=k: e.matmul(bank(g, b), lhsT=g.uT[:, k, tb * 128:(tb + 1) * 128], rhs=o.w[wi][:, k, :],
                                                         start=(k == 0), stop=(k == KC - 1)),
                     reads=[g.uU[n], o.wU[wi]], writes=[g.bU[b]])
        pj = g.ps[:, 0:4, :]
        pb = [g.bU[0], g.bU[1], g.bU[2], g.bU[3]]
        bc4 = lambda t: t.unsqueeze(1).to_broadcast([128, 4, 128])
        P.act(lambda e: e.activation(out=o.fall[:, bs, :], in_=pj[:, :, 128:256], func=AF.Sigmoid), reads=pb, writes=[o.fallU[n]])
        P.act(lambda e: e.activation(out=o.qsall[:, bs, :], in_=pj[:, :, 0:128], func=AF.Silu), reads=pb, writes=[o.qsallU[n]])
        P.act(lambda e: e.activation(out=o.gs[hb][:, bs, :], in_=pj[:, :, 384:512], func=AF.Silu), reads=pb, writes=[U[GS][n]])
        P.act(lambda e: e.copy(o.v[hb][:, bs, :], pj[:, :, 256:384]), reads=pb, writes=[U[VV][n]])
        P.dve(lambda e: e.tensor_tensor(out=o.fall[:, bs, :], in0=o.fall[:, bs, :], in1=bc4(o.omlh), op=ALU.mult),
              reads=[o.fallU[n], o.lbU], writes=[o.fallU[n]])
        P.dve(lambda e: e.tensor_tensor(out=o.fall[:, bs, :], in0=o.fall[:, bs, :], in1=bc4(o.lbh), op=ALU.add),
              reads=[o.fallU[n], o.lbU], writes=[o.fallU[n]])
        P.pool(lambda e: e.tensor_scalar(out=o.kkall[:, bs, :], in0=o.fall[:, bs, :], scalar1=-1.0, scalar2=1.0, op0=ALU.mult, op1=ALU.add),
               reads=[o.fallU[n]], writes=[o.kkallU[n]])

    def stageAmid(h):
        P.act(lambda e: e.activation(out=o.fall, in_=o.fall, func=AF.Ln), reads=o.fallU + o.kkallU, writes=o.fallU)

    def stageA2(h, n):
        hb = h % 2
        U = o.hbU[hb]
        bs = slice(n * 4, (n + 1) * 4)
        logf = o.fall[:, bs, :]
        kk = o.kkall[:, bs, :]
        qs = o.qsall[:, bs, :]
        lu = [o.fallU[n]]
        lf4 = logf.rearrange("p b d -> p (b d)")
        P.pe(lambda e: e.matmul(bank(g, 0), lhsT=o.triC, rhs=lf4, start=True, stop=True), reads=lu + [o.triUU], writes=[g.bU[0]])
        P.pe(lambda e: e.matmul(bank(g, 1), lhsT=o.triU, rhs=lf4, start=True, stop=True), reads=lu + [o.triUU], writes=[g.bU[1]])
        for b in range(4):
            P.pe(lambda e, b=b: e.matmul(bank(g, 2)[:, b * 2:b * 2 + 2], lhsT=logf[:, b, :], rhs=c["sel"][:], start=True, stop=True),
                 reads=lu + [g.cU], writes=[g.bU[2]])
        P.act(lambda e: e.activation(out=o.eb[hb][:, bs, :], in_=bank(g, 2)[:, 0:8].rearrange("p (b t) -> p b t", t=2), func=AF.Exp),
              reads=[g.bU[2]], writes=[U[EB][n]])
        P.act(lambda e: e.activation(out=o.e13, in_=g.ps[:, 0:2, :].rearrange("p t (b d) -> p t b d", b=4), func=AF.Exp),
              reads=[g.bU[0], g.bU[1]], writes=[o.e13U])
        P.act(lambda e: e.activation(out=o.e2, in_=bank(g, 0).rearrange("p (b d) -> p b d", b=4), func=AF.Exp, scale=-1.0),
              reads=[g.bU[0]], writes=[o.e2U])
        P.dve(lambda e: e.tensor_tensor(out=qs, in0=qs, in1=o.e13[:, 0], op=ALU.mult), reads=[o.qsallU[n], o.e13U], writes=[o.qsallU[n]])
        P.pool(lambda e: e.tensor_tensor(out=o.e2, in0=kk, in1=o.e2, op=ALU.mult), reads=[o.kkallU[n], o.e2U], writes=[o.e2U])
        P.dve(lambda e: e.tensor_tensor(out=o.kh[hb][:, bs, :], in0=kk, in1=o.e13[:, 1], op=ALU.mult),
              reads=[o.kkallU[n], o.e13U], writes=[U[KH][n]])
        idf = c["ident"]
        for b in range(4):
            P.pe(lambda e, b=b: e.transpose(bank(g, 3)[:, b * 128:(b + 1) * 128], qs[:, b, :], idf[:]),
                 reads=[o.qsallU[n], g.cU], writes=[g.bU[3]])
        for b in range(4):
            P.pe(lambda e, b=b: e.transpose(bank(g, 2)[:, b * 128:(b + 1) * 128], o.kt[:, b, :], idf[:]),
                 reads=[o.ktU, g.cU], writes=[g.bU[2]])
        P.act(lambda e: e.copy(o.qT[hb][:, n * 512:(n + 1) * 512], bank(g, 3)), reads=[g.bU[3]], writes=[U[QT][n]])
        P.act(lambda e: e.copy(o.kT[hb][:, n * 512:(n + 1) * 512], bank(g, 2)), reads=[g.bU[2]], writes=[U[KT][n]])

    def stageAall(h):
        for n in range(NT):
            stageA1(h, n)
        stageAmid(h)
        for n in range(NT):
            stageA2(h, n)

    def stageB(h):
        hb = h % 2
        U = o.hbU[hb]

        def att(tb):
            n = tb // 4
            ts = slice(tb * 128, (tb + 1) * 128)
            bk = 5 + 2 * (tb % 2)
            P.pe(lambda e: e.matmul(bank(g, bk)[:, 64:128], lhsT=o.kT[hb][:, ts],
                                    rhs=o.qT[hb][:, tb * 128 + 64:(tb + 1) * 128], start=True, stop=True),
                 reads=[U[QT][n], U[KT][n]], writes=[g.bU[bk]])
            P.pe(lambda e: e.matmul(bank(g, bk)[0:64, 0:64], lhsT=o.kT[hb][:, tb * 128:tb * 128 + 64],
                                    rhs=o.qT[hb][:, tb * 128:tb * 128 + 64], start=True, stop=True),
                 reads=[U[QT][n], U[KT][n]], writes=[g.bU[bk]])

        def mask(tb):
            ai = tb % 2
            bk = 5 + 2 * (tb % 2)
            P.dve(lambda e: e.tensor_tensor(out=o.attm[ai][:, 64:128], in0=bank(g, bk)[:, 64:128], in1=c["triI"][:, 64:128], op=ALU.mult),
                  reads=[g.bU[bk], g.cU], writes=[o.attmU[ai]])
            P.dve(lambda e: e.tensor_tensor(out=o.attm[ai][0:64, 0:64], in0=bank(g, bk)[0:64, 0:64], in1=c["triI"][0:64, 0:64], op=ALU.mult),
                  reads=[g.bU[bk], g.cU], writes=[o.attmU[ai]])

        att(0)
        mask(0)
        for tb in range(NB):
            n = tb // 4
            ts = slice(tb * 128, (tb + 1) * 128)
            ai = tb % 2
            si = tb % 2
            P.pe(lambda e: e.matmul(bank(g, 6)[:, 128:256], lhsT=o.kh[hb][:, tb, :], rhs=o.v[hb][:, tb, :], start=True, stop=True),
                 reads=[U[KH][n], U[VV][n]], writes=[g.bU[6]])
            if tb + 1 < NB:
                att(tb + 1)
            P.pe(lambda e: e.matmul(bank(g, 4)[:, 0:128], lhsT=o.attm[ai], rhs=o.v[hb][:, tb, :], start=True, stop=(tb == 0)),
                 reads=[o.attmU[ai], U[VV][n]], writes=[g.bU[4]])
            if tb > 0:
                P.pe(lambda e: e.matmul(bank(g, 4)[:, 0:128], lhsT=o.qT[hb][:, ts], rhs=o.Sb2[si], start=False, stop=True),
                     reads=[o.SbU2[si], U[QT][n]], writes=[g.bU[4]])
            if tb == 0:
                P.dve(lambda e: e.tensor_copy(o.S, bank(g, 6)[:, 128:256]), reads=[g.bU[6]], writes=[o.SU])
            else:
                P.dve(lambda e: e.scalar_tensor_tensor(out=o.S, in0=o.S, scalar=o.eb[hb][:, tb, 1:2], in1=bank(g, 6)[:, 128:256],
                                                       op0=ALU.mult, op1=ALU.add),
                      reads=[o.SU, g.bU[6], U[EB][n]], writes=[o.SU])
            if tb + 1 < NB:
                nn = (tb + 1) // 4
                sn = (tb + 1) % 2
                P.dve(lambda e: e.tensor_scalar(out=o.Sb2[sn], in0=o.S, scalar1=o.eb[hb][:, tb + 1, 0:1], scalar2=None, op0=ALU.mult),
                      reads=[o.SU, U[EB][nn]], writes=[o.SbU2[sn]])
                mask(tb + 1)
            P.act(lambda e: e.copy(o.oall[:, tb, :], bank(g, 4)[:, 0:128]), reads=[g.bU[4]], writes=[o.oallU])
            P.act(lambda e: e.activation(out=o.junk2, in_=bank(g, 4)[:, 0:128], func=AF.Square, accum_out=o.ssall[:, tb:tb + 1]),
                  reads=[g.bU[4]], writes=[o.junk2U, o.ssU])

    def stageC(h):
        hb = h % 2
        U = o.hbU[hb]
        hj = h % 2
        P.act(lambda e: e.activation(out=o.rsall, in_=o.ssall, func=AF.Ln, scale=1.0 / 128, bias=EPS), reads=[o.ssU], writes=[o.ssU])
        P.act(lambda e: e.activation(out=o.rsall, in_=o.rsall, func=AF.Exp, scale=-0.5), reads=[o.ssU], writes=[o.ssU])
        P.dve(lambda e: e.tensor_tensor(out=o.oall, in0=o.oall, in1=o.rsall.unsqueeze(2).to_broadcast([128, NB, 128]), op=ALU.mult),
              reads=[o.oallU, o.ssU], writes=[o.oallU])
        P.dve(lambda e: e.tensor_tensor(out=o.oall, in0=o.oall, in1=o.hnw.unsqueeze(1).to_broadcast([128, NB, 128]), op=ALU.mult),
              reads=[o.oallU, o.hnwU], writes=[o.oallU])
        P.dve(lambda e: e.tensor_tensor(out=o.oall, in0=o.oall, in1=o.gs[hb], op=ALU.mult),
              reads=[o.oallU] + [U[GS][n] for n in range(NT)], writes=[o.oallU])
        for n in range(NT):
            bk = 4 + (n % 4)
            for b in range(4):
                P.pe(lambda e, bk=bk, b=b, n=n: e.transpose(bank(g, bk)[:, b * 128:(b + 1) * 128], o.oall[:, n * 4 + b, :], c["ident"][:]),
                     reads=[o.oallU, g.cU], writes=[g.bU[bk]])
            P.act(lambda e, bk=bk, n=n: e.copy(o.yT[:, hj, n * 512:(n + 1) * 512], bank(g, bk)), reads=[g.bU[bk]], writes=[o.yTU[hj]])

    def outproj(hp):
        for j in range(2):
            src = g.d["odd_w_out"][oi, (hp * 2 + j) * 128:(hp * 2 + j + 1) * 128, :]
            P.dma("pool", lambda e, j=j, src=src: e.dma_start(out=o.wout[:, j, :], in_=src), f"o_wout{j}", writes=[o.woutU[j]])
        cnt = 0
        for m in range(KC):
            for n in range(NT):
                bk = 4 + cnt % 4
                cnt += 1
                for j in range(2):
                    P.pe(lambda e, bk=bk, m=m, n=n, j=j: e.matmul(bank(g, bk), lhsT=o.wout[:, j, m * 128:(m + 1) * 128],
                                                                     rhs=o.yT[:, j, n * 512:(n + 1) * 512], start=(j == 0), stop=(j == 1)),
                         reads=[o.woutU[j], o.yTU[j]], writes=[g.bU[bk]])
                P.dve(lambda e, bk=bk, m=m, n=n: e.tensor_tensor(out=g.hT[:, m, n * 512:(n + 1) * 512], in0=g.hT[:, m, n * 512:(n + 1) * 512],
                                                                   in1=bank(g, bk), op=ALU.add),
                      reads=[g.bU[bk], g.hU[m][n]], writes=[g.hU[m][n]])

    load_w(0)
    load_w(1)
    head_lb(0)
    stageAall(0)
    for h in range(16):
        P.capture()
        stageB(h)
        stageC(h)
        if h % 2 == 1:
            outproj(h // 2)
        LB = P.end_capture()
        P.capture()
        if h + 1 < 16:
            if h + 2 < 16:
                load_w(h + 2)
            head_lb(h + 1)
            stageAall(h + 1)
        LA = P.end_capture()
        P.replay_merged(LA, LB)


def even_ssd(g, li):
    P, L, NB, NT = g.P, g.L, g.NB, g.NT
    ei = li // 2
    c = g.cst
    A = Carver(g)
    s = NS()
    HB = NB * 16
    s.xpre = A.f(515); s.xpreU = P.unit()
    s.cacc = A.f(512); s.caccU = P.unit()
    v3 = lambda ap: ap.rearrange("p (b h) -> p b h", h=16)
    s.dt = A.f(HB); s.atok = A.f(HB); s.acs = A.f(HB); s.eacs = A.f(HB); s.dtd = A.f(HB); s.edl = A.f(HB)
    s.dtU = P.unit()
    s.cw = A.f(64); s.cb = A.f(16); s.dtb = A.f(16); s.Abc = A.f(16); s.Dsk = A.f(16); s.snw = A.f(8)
    s.smallU = P.unit()
    s.S = A.f(256); s.SU = P.unit()
    scan_off = A.fo
    s.Rbd = A.f(512); s.RbdU = P.unit()
    D2 = lambda n: ([A.f(n) for _ in range(2)], P.units(2))
    s.acsTb2, s.acsTbU2 = D2(128)
    s.CBm2, s.CBmU2 = D2(128)
    s.Dm2, s.DmU2 = D2(512)
    s.E2, s.EU2 = D2(512)
    s.t12, s.t1U2 = D2(256)
    s.t22, s.t2U2 = D2(256)
    s.ytmp2, s.ytmpU2 = D2(256)
    s.rstd_all = g.arf[:, scan_off:scan_off + L]; s.rstdallU = P.unit()
    assert scan_off + L <= ARF_N
    s.w = A.b(KC * 768).rearrange("p (k c) -> p k c", k=KC); s.wU = P.unit()
    s.wdt = A.b(KC * 16).rearrange("p (k c) -> p k c", k=KC); s.wdtU = P.unit()
    s.BT = A.b(L); s.CT = A.b(L); s.BCU = [P.unit() for _ in range(NT)]
    s.xtok = A.b(NB * 256).rearrange("p (b c) -> p b c", c=256); s.xtokU = [P.unit() for _ in range(NT)]
    s.Btok = A.b(NB * 128).rearrange("p (b c) -> p b c", c=128); s.BtokU = [P.unit() for _ in range(NT)]
    scan_bo = A.bo
    s.zs2 = [A.b(256) for _ in range(2)]; s.zsU2 = P.units(2)
    s.sc2 = [A.b(512).rearrange("p (h l) -> p h l", h=4) for _ in range(2)]; s.scU2 = P.units(2)
    s.Xdt2 = [A.b(256) for _ in range(2)]; s.XdtU2 = P.units(2)
    s.XB2 = [A.b(256) for _ in range(2)]; s.XBU2 = P.units(2)
    s.Sbf = A.b(256); s.SbfU = P.unit()
    s.yTa = A.b(8 * L).rearrange("p (c t) -> p c t", c=8); s.yTaU = [[P.unit() for _ in range(NT)] for _ in range(8)]
    s.wo2 = [g.arb[:, scan_bo + i * 1024:scan_bo + (i + 1) * 1024].rearrange("p (c j) -> p c j", c=8) for i in range(2)]
    s.woU2 = P.units(2)
    assert scan_bo + 2048 <= A.bo
    s.tmp2 = [s.cacc, s.xpre[:, 0:512]]; s.tmpU2 = [s.caccU, s.xpreU]
    ident = c["ident"]

    sm = [s.smallU]
    P.dma("sp", lambda e: e.dma_start(out=s.cw, in_=g.d["conv_w_cols"][ei]), "s_small", writes=sm)
    P.dma("sp", lambda e: e.dma_start(out=s.cb, in_=g.d["conv_b_cols"][ei]), "s_small", writes=sm)
    P.dma("sp", lambda e: e.dma_start(out=s.dtb, in_=g.d["dt_bias"][ei].partition_broadcast(128)), "s_small", writes=sm)
    P.dma("sp", lambda e: e.dma_start(out=s.Abc, in_=g.d["A_log"][ei].partition_broadcast(128)), "s_small", writes=sm)
    P.dma("sp", lambda e: e.dma_start(out=s.Dsk, in_=g.d["D_skip"][ei].partition_broadcast(128)), "s_small", writes=sm)
    P.dma("sp", lambda e: e.dma_start(out=s.snw, in_=g.d["ssd_norm_w_cols"][ei]), "s_small", writes=sm)
    P.act(lambda e: e.activation(out=s.Abc, in_=s.Abc, func=AF.Exp), reads=sm, writes=sm)
    P.dve(lambda e: e.tensor_scalar(out=s.Abc, in0=s.Abc, scalar1=-1.0, scalar2=None, op0=ALU.mult), reads=sm, writes=sm)
    P.dma("pool", lambda e: e.dma_start(out=s.wdt.rearrange("p k c -> p (k c)"), in_=g.d["ev_w_dt"][ei]), "s_wdt", writes=[s.wdtU])

    for b in range(NB):
        for k in range(KC):
            P.pe(lambda e, b=b, k=k: e.matmul(bank(g, 0)[:, b * 16:(b + 1) * 16], lhsT=g.uT[:, k, b * 128:(b + 1) * 128],
                                              rhs=s.wdt[:, k, :], start=(k == 0), stop=(k == KC - 1)),
                 reads=[g.uU[b // 4], s.wdtU], writes=[g.bU[0]])
    bc_h = lambda t: t.unsqueeze(1).to_broadcast([128, NB, 16])
    du = [s.dtU]
    P.dve(lambda e: e.tensor_tensor(out=v3(s.dt), in0=v3(bank(g, 0)[:, 0:HB]), in1=bc_h(s.dtb), op=ALU.add),
          reads=[g.bU[0]] + sm, writes=du)
    P.act(lambda e: e.activation(out=s.dt, in_=s.dt, func=AF.Exp), reads=du, writes=du)
    P.act(lambda e: e.activation(out=s.dt, in_=s.dt, func=AF.Ln, bias=1.0), reads=du, writes=du)
    P.dve(lambda e: e.tensor_tensor(out=v3(s.atok), in0=v3(s.dt), in1=bc_h(s.Abc), op=ALU.mult), reads=du + sm, writes=du)
    P.pe(lambda e: e.matmul(bank(g, 1)[:, 0:HB], lhsT=c["triI"][:], rhs=s.atok, start=True, stop=True), reads=du + [g.cU], writes=[g.bU[1]])
    P.pe(lambda e: e.matmul(bank(g, 2)[:, 0:HB], lhsT=c["ones"][:], rhs=s.atok, start=True, stop=True), reads=du + [g.cU], writes=[g.bU[2]])
    P.dve(lambda e: e.tensor_copy(s.acs, bank(g, 1)[:, 0:HB]), reads=[g.bU[1]], writes=du)
    P.act(lambda e: e.activation(out=s.eacs, in_=s.acs, func=AF.Exp), reads=du, writes=du)
    P.dve(lambda e: e.tensor_copy(s.edl, bank(g, 2)[:, 0:HB]), reads=[g.bU[2]], writes=du)
    P.dve(lambda e: e.tensor_tensor(out=s.dtd, in0=s.edl, in1=s.acs, op=ALU.subtract), reads=du, writes=du)
    P.act(lambda e: e.activation(out=s.dtd, in_=s.dtd, func=AF.Exp), reads=du, writes=du)
    P.dve(lambda e: e.tensor_tensor(out=s.dtd, in0=s.dtd, in1=s.dt, op=ALU.mult), reads=du, writes=du)
    P.act(lambda e: e.activation(out=s.edl, in_=s.edl, func=AF.Exp), reads=du, writes=du)

    pcnt = [0]
    for grp in range(4):
        P.dma("pool", lambda e, grp=grp: e.dma_start(out=s.w.rearrange("p k c -> p (k c)"), in_=g.d["ev_w_ssd"][ei, grp]),
              "s_w", writes=[s.wU])
        chunks = [(256, 2 * grp, "x0"), (384, 2 * grp + 1, "x1"), (512, 8 + grp, "B"), (640, 12 + grp, "C")]
        cp1, cp2 = [], []
        for wc0, cch, kind in chunks:
            for n in range(NT):
                sl = slice(n * 512, (n + 1) * 512)
                bk = 3 + (pcnt[0] % 2)
                pcnt[0] += 1
                P.capture()
                for k in range(KC):
                    P.pe(lambda e, bk=bk, k=k, wc0=wc0, sl=sl: e.matmul(bank(g, bk), lhsT=s.w[:, k, wc0:wc0 + 128], rhs=g.uT[:, k, sl],
                                                                        start=(k == 0), stop=(k == KC - 1)),
                         reads=[g.uU[n], s.wU], writes=[g.bU[bk]])
                cp1.append(P.end_capture())
                P.capture()
                if n == 0:
                    P.dve(lambda e: e.memset(s.xpre[:, 0:3], 0.0), writes=[s.xpreU])
                else:
                    P.dve(lambda e: e.tensor_copy(s.xpre[:, 0:3], s.xpre[:, 512:515]), reads=[s.xpreU], writes=[s.xpreU])
                P.act(lambda e, bk=bk: e.copy(s.xpre[:, 3:515], bank(g, bk)), reads=[g.bU[bk]], writes=[s.xpreU])
                P.dve(lambda e, cch=cch: e.tensor_scalar(out=s.cacc, in0=s.xpre[:, 3:515], scalar1=s.cw[:, cch * 4 + 3:cch * 4 + 4],
                                                          scalar2=s.cb[:, cch:cch + 1], op0=ALU.mult, op1=ALU.add),
                      reads=[s.xpreU] + sm, writes=[s.caccU])
                for tap in (2, 1, 0):
                    P.dve(lambda e, cch=cch, tap=tap: e.scalar_tensor_tensor(
                        out=s.cacc, in0=s.xpre[:, tap:tap + 512], scalar=s.cw[:, cch * 4 + tap:cch * 4 + tap + 1], in1=s.cacc,
                        op0=ALU.mult, op1=ALU.add), reads=[s.xpreU, s.caccU] + sm, writes=[s.caccU])
                P.act(lambda e: e.activation(out=s.cacc, in_=s.cacc, func=AF.Silu), reads=[s.caccU], writes=[s.caccU])
                if kind in ("B", "C"):
                    dst = s.BT if kind == "B" else s.CT
                    P.dve(lambda e, dst=dst, sl=sl: e.tensor_copy(dst[:, sl], s.cacc), reads=[s.caccU], writes=[s.BCU[n]])
                if kind != "C":
                    for j in range(4):
                        P.pe(lambda e, j=j: e.transpose(bank(g, 5)[:, j * 128:(j + 1) * 128], s.cacc[:, j * 128:(j + 1) * 128], ident[:]),
                             reads=[s.caccU, g.cU], writes=[g.bU[5]])
                    src = bank(g, 5).rearrange("p (b c) -> p b c", b=4)
                    if kind == "B":
                        P.act(lambda e, n=n, src=src: e.copy(s.Btok[:, n * 4:(n + 1) * 4, :], src), reads=[g.bU[5]], writes=[s.BtokU[n]])
                    else:
                        co = 0 if kind == "x0" else 128
                        P.act(lambda e, n=n, src=src, co=co: e.copy(s.xtok[:, n * 4:(n + 1) * 4, co:co + 128], src),
                              reads=[g.bU[5]], writes=[s.xtokU[n]])
                cp2.append(P.end_capture())
        P.replay_merged(cp1[0], [])
        for i in range(len(cp2)):
            P.replay_merged(cp1[i + 1] if i + 1 < len(cp1) else [], [])
            P.replay_merged(cp2[i], [])
        hs4 = slice(4 * grp, 4 * grp + 4)
        fronts, backs = [], []
        for b in range(NB):
            P.capture()
            n = b // 4
            blk = slice(b * 128, (b + 1) * 128)
            hcol = lambda t, b=b: v3(t)[:, b, hs4]
            bch = lambda t, w, b=b: hcol(t, b).unsqueeze(2).to_broadcast([128, 4, w])
            x4 = s.xtok[:, b, :].rearrange("p (h q) -> p h q", h=4)
            pb_ = b % 2
            s.acsTb, s.acsTbU = s.acsTb2[pb_], s.acsTbU2[pb_]
            s.CBm, s.CBmU = s.CBm2[pb_], s.CBmU2[pb_]
            s.Dm, s.DmU = s.Dm2[pb_], s.DmU2[pb_]
            s.E, s.EU = s.E2[pb_], s.EU2[pb_]
            s.t1, s.t1U = s.t12[pb_], s.t1U2[pb_]
            s.t2, s.t2U = s.t22[pb_], s.t2U2[pb_]
            s.ytmp, s.ytmpU = s.ytmp2[pb_], s.ytmpU2[pb_]
            s.zs, s.zsU = s.zs2[pb_], s.zsU2[pb_]
            s.sc, s.scU = s.sc2[pb_], s.scU2[pb_]
            s.Xdt, s.XdtU = s.Xdt2[pb_], s.XdtU2[pb_]
            s.XB, s.XBU = s.XB2[pb_], s.XBU2[pb_]
            for k in range(KC):
                P.pe(lambda e, k=k, blk=blk: e.matmul(bank(g, 6)[:, 0:256], lhsT=g.uT[:, k, blk], rhs=s.w[:, k, 0:256],
                                                      start=(k == 0), stop=(k == KC - 1)),
                     reads=[g.uU[n], s.wU], writes=[g.bU[6]])
            P.act(lambda e: e.activation(out=s.zs, in_=bank(g, 6)[:, 0:256], func=AF.Silu), reads=[g.bU[6]], writes=[s.zsU])
            P.pe(lambda e, blk=blk: e.matmul(bank(g, 7)[:, 0:128], lhsT=s.BT[:, blk], rhs=s.CT[:, blk], start=True, stop=True),
                 reads=[s.BCU[n]], writes=[g.bU[7]])
            P.dve(lambda e: e.tensor_tensor(out=s.CBm, in0=bank(g, 7)[:, 0:128], in1=c["triI"][:], op=ALU.mult),
                  reads=[g.bU[7], g.cU], writes=[s.CBmU])
            P.pe(lambda e, b=b: e.transpose(bank(g, 0)[0:16, 0:128], s.acs[:, b * 16:(b + 1) * 16], ident[:]),
                 reads=du + [g.cU], writes=[g.bU[0]])
            P.act(lambda e: e.copy(s.acsTb[0:16, :], bank(g, 0)[0:16, 0:128]), reads=[g.bU[0]], writes=[s.acsTbU])
            P.dve(lambda e: e.tensor_tensor(out=s.Rbd[0:16, :].rearrange("p (h l) -> p h l", h=4),
                                            in0=s.acsTb[0:16, :].unsqueeze(1).to_broadcast([16, 4, 128]),
                                            in1=c["mg16"][:, hs4].unsqueeze(2).to_broadcast([16, 4, 128]), op=ALU.mult),
                  reads=[s.acsTbU, g.cU], writes=[s.RbdU])
            P.pe(lambda e: e.matmul(bank(g, 1), lhsT=c["ones"][0:16, :], rhs=s.Rbd[0:16, :], start=True, stop=True),
                 reads=[s.RbdU, g.cU], writes=[g.bU[1]])
            P.dve(lambda e, b=b: e.tensor_tensor(out=s.Dm.rearrange("p (h l) -> p h l", h=4),
                                                 in0=bank(g, 1).rearrange("p (h l) -> p h l", h=4),
                                                 in1=bch(s.acs, 128, b), op=ALU.subtract),
                  reads=[g.bU[1]] + du, writes=[s.DmU])
            P.dve(lambda e: e.tensor_scalar(out=s.Dm, in0=s.Dm, scalar1=0.0, scalar2=None, op0=ALU.min), reads=[s.DmU], writes=[s.DmU])
            P.act(lambda e: e.activation(out=s.E, in_=s.Dm, func=AF.Exp), reads=[s.DmU], writes=[s.EU])
            P.dve(lambda e: e.tensor_tensor(out=s.sc, in0=s.E.rearrange("p (h l) -> p h l", h=4),
                                            in1=s.CBm.unsqueeze(1).to_broadcast([128, 4, 128]), op=ALU.mult),
                  reads=[s.EU, s.CBmU], writes=[s.scU])
            P.dve(lambda e, b=b: e.tensor_tensor(out=s.Xdt.rearrange("p (h q) -> p h q", h=4), in0=x4, in1=bch(s.dt, 64, b), op=ALU.mult),
                  reads=[s.xtokU[n]] + du, writes=[s.XdtU])
            P.dve(lambda e, b=b: e.tensor_tensor(out=s.XB.rearrange("p (h q) -> p h q", h=4), in0=x4, in1=bch(s.dtd, 64, b), op=ALU.mult),
                  reads=[s.xtokU[n]] + du, writes=[s.XBU])
            fronts.append(P.end_capture())
            P.capture()
            for h4 in range(4):
                P.pe(lambda e, h4=h4: e.matmul(bank(g, 2)[:, h4 * 64:(h4 + 1) * 64], lhsT=s.sc[:, h4, :], rhs=s.Xdt[:, h4 * 64:(h4 + 1) * 64],
                                               start=True, stop=True), reads=[s.scU, s.XdtU], writes=[g.bU[2]])
            if b > 0:
                P.pe(lambda e, blk=blk: e.matmul(bank(g, 3)[:, 0:256], lhsT=s.CT[:, blk], rhs=s.Sbf, start=True, stop=True),
                     reads=[s.BCU[n], s.SbfU], writes=[g.bU[3]])
            P.pe(lambda e, b=b: e.matmul(bank(g, 4)[:, 0:256], lhsT=s.Btok[:, b, :], rhs=s.XB, start=True, stop=True),
                 reads=[s.BtokU[n], s.XBU], writes=[g.bU[4]])
            if b > 0:
                P.dve(lambda e, b=b: e.tensor_tensor(out=s.t1.rearrange("p (h q) -> p h q", h=4),
                                                     in0=bank(g, 3)[:, 0:256].rearrange("p (h q) -> p h q", h=4),
                                                     in1=bch(s.eacs, 64, b), op=ALU.mult), reads=[g.bU[3]] + du, writes=[s.t1U])
                P.dve(lambda e: e.tensor_tensor(out=s.t2, in0=bank(g, 2)[:, 0:256], in1=s.t1, op=ALU.add), reads=[g.bU[2], s.t1U], writes=[s.t2U])
            else:
                P.dve(lambda e: e.tensor_copy(s.t2, bank(g, 2)[:, 0:256]), reads=[g.bU[2]], writes=[s.t2U])
            P.dve(lambda e: e.tensor_tensor(out=s.t1.rearrange("p (h q) -> p h q", h=4), in0=x4,
                                            in1=s.Dsk[:, hs4].unsqueeze(2).to_broadcast([128, 4, 64]), op=ALU.mult),
                  reads=[s.xtokU[n]] + sm, writes=[s.t1U])
            P.dve(lambda e: e.tensor_tensor(out=s.t2, in0=s.t2, in1=s.t1, op=ALU.add), reads=[s.t1U, s.t2U], writes=[s.t2U])
            P.dve(lambda e: e.tensor_tensor(out=s.ytmp, in0=s.t2, in1=s.zs, op=ALU.mult), reads=[s.t2U, s.zsU], writes=[s.ytmpU])
            for j in range(2):
                P.pe(lambda e, j=j: e.transpose(bank(g, 5)[:, j * 128:(j + 1) * 128], s.ytmp[:, j * 128:(j + 1) * 128], ident[:]),
                     reads=[s.ytmpU, g.cU], writes=[g.bU[5]])
            P.act(lambda e, blk=blk: e.copy(s.yTa[:, 2 * grp:2 * grp + 2, blk], bank(g, 5)[:, 0:256].rearrange("p (j t) -> p j t", j=2)),
                  reads=[g.bU[5]], writes=[s.yTaU[2 * grp][n], s.yTaU[2 * grp + 1][n]])
            if b == 0:
                P.dve(lambda e: e.tensor_copy(s.S, bank(g, 4)[:, 0:256]), reads=[g.bU[4]], writes=[s.SU])
            else:
                P.dve(lambda e, b=b: e.tensor_tensor(out=s.S.rearrange("p (h q) -> p h q", h=4), in0=s.S.rearrange("p (h q) -> p h q", h=4),
                                                     in1=bch(s.edl, 64, b), op=ALU.mult), reads=[s.SU] + du, writes=[s.SU])
                P.dve(lambda e: e.tensor_tensor(out=s.S, in0=s.S, in1=bank(g, 4)[:, 0:256], op=ALU.add), reads=[s.SU, g.bU[4]], writes=[s.SU])
            if b + 1 < NB:
                P.act(lambda e: e.copy(s.Sbf, s.S), reads=[s.SU], writes=[s.SbfU])
            backs.append(P.end_capture())
        P.replay_merged(fronts[0], [])
        for b in range(NB):
            P.replay_merged(fronts[b + 1] if b + 1 < NB else [], backs[b])

    P.barrier()
    for n in range(NT):
        sl = slice(n * 512, (n + 1) * 512)
        rms_rstd_tile(g, lambda k, sl=sl: s.yTa[:, k, sl], lambda k, n=n: [s.yTaU[k][n]], 8, 1024)
        P.dve(lambda e, sl=sl: e.tensor_copy(s.rstd_all[:, sl], g.rstd_t[:]), reads=[g.rstdU], writes=[s.rstdallU])
    cnt = 0

    def load_wo(m):
        wi = m % 2
        P.dma("pool", lambda e: e.dma_start(out=s.wo2[wi].rearrange("p c j -> p (c j)"), in_=g.d["ev_w_out_t"][ei, m, :, 0:1024]),
              f"s_wo{wi}", writes=[s.woU2[wi]])
        P.dve(lambda e: e.tensor_tensor(out=s.wo2[wi], in0=s.wo2[wi], in1=s.snw[:, 0:8].unsqueeze(2).to_broadcast([128, 8, 128]), op=ALU.mult),
              reads=[s.woU2[wi]] + sm, writes=[s.woU2[wi]])

    load_wo(0)
    for m in range(KC):
        if m + 1 < KC:
            load_wo(m + 1)
        wi = m % 2
        for n in range(NT):
            sl = slice(n * 512, (n + 1) * 512)
            bk = cnt % 2
            ti = cnt % 2
            cnt += 1
            for k in range(8):
                P.pe(lambda e, bk=bk, k=k, sl=sl: e.matmul(bank(g, bk), lhsT=s.wo2[wi][:, k, :], rhs=s.yTa[:, k, sl], start=(k == 0), stop=(k == 7)),
                     reads=[s.woU2[wi], s.yTaU[k][n]], writes=[g.bU[bk]])
            P.dve(lambda e, bk=bk, sl=sl, ti=ti: e.tensor_tensor(out=s.tmp2[ti], in0=bank(g, bk), in1=s.rstd_all[:, sl], op=ALU.mult),
                  reads=[g.bU[bk], s.rstdallU], writes=[s.tmpU2[ti]])
            P.pool(lambda e, m=m, sl=sl, ti=ti: e.tensor_tensor(out=g.hT[:, m, sl], in0=g.hT[:, m, sl], in1=s.tmp2[ti], op=ALU.add),
                   reads=[s.tmpU2[ti], g.hU[m][n]], writes=[g.hU[m][n]])


def even_attn(g, li):
    P, L, NB, NT = g.P, g.L, g.NB, g.NT
    ei = li // 2
    lam_init = 0.8 - 0.6 * math.exp(-0.3 * li)
    c = g.cst
    A = Carver(g)
    a = NS()
    a.corr = A.f(2048).rearrange("p (h d q) -> p h d q", h=8, d=2); a.corrU = P.unit()
    a.b31 = A.f(8); a.nb31 = A.f(8); a.bU_ = P.unit()
    a.lq = [A.f(64) for _ in range(4)]; a.lamU = P.unit()
    a.lam = A.f(8)
    a.slnw = A.f(128); a.slnwU = P.unit()
    a.w = [A.b(KC * 512).rearrange("p (k c) -> p k c", k=KC) for _ in range(2)]; a.wU = P.units(2)
    a.qT = [A.b(L) for _ in range(2)]; a.qTU = [P.unit() for _ in range(NT)]
    a.kT = A.b(L); a.kTU = [P.unit() for _ in range(NT)]
    a.v = A.b(NB * 132).rearrange("p (b c) -> p b c", c=132); a.vU = P.unit()
    a.gs = A.b(NB * 128).rearrange("p (b c) -> p b c", c=128); a.gsU = P.unit()
    a.PT = [[A.b(512) for _ in range(2)] for _ in range(2)]; a.PTU = [P.units(2), P.units(2)]
    a.yT = A.b(4 * L).rearrange("p (j t) -> p j t", j=4); a.yTU = P.units(4)
    a.wo = [A.b(512).rearrange("p (j c) -> p j c", j=4) for _ in range(2)]; a.woU = P.units(2)
    ident = c["ident"]

    P.dma("sp", lambda e: e.dma_start(out=a.corr.rearrange("p h d q -> p (h d q)"), in_=g.d["rel_biasD"]), "a_corr", writes=[a.corrU])
    P.dma("sp", lambda e: e.dma_start(out=a.b31, in_=g.d["rel_b31"].partition_broadcast(128)), "a_b31", writes=[a.bU_])
    P.dve(lambda e: e.tensor_scalar(out=a.nb31, in0=a.b31, scalar1=-1.0, scalar2=None, op0=ALU.mult), reads=[a.bU_], writes=[a.bU_])
    for h in range(8):
        P.act(lambda e, h=h: e.activation(out=a.corr[:, h], in_=a.corr[:, h], func=AF.Exp, bias=a.nb31[:, h:h + 1]),
              reads=[a.corrU, a.bU_], writes=[a.corrU])
    for i, nm in enumerate(("lambda_q1", "lambda_k1", "lambda_q2", "lambda_k2")):
        P.dma("sp", lambda e, i=i, nm=nm: e.dma_start(out=a.lq[i], in_=g.d[nm][ei].partition_broadcast(128)), "a_lam", writes=[a.lamU])
    lu = [a.lamU]
    P.dve(lambda e: e.tensor_tensor(out=a.lq[0], in0=a.lq[0], in1=a.lq[1], op=ALU.mult), reads=lu, writes=lu)
    P.dve(lambda e: e.tensor_tensor(out=a.lq[2], in0=a.lq[2], in1=a.lq[3], op=ALU.mult), reads=lu, writes=lu)
    P.dve(lambda e: e.tensor_reduce(out=a.lam[:, 0:1], in_=a.lq[0], axis=AX.X, op=ALU.add), reads=lu, writes=lu)
    P.dve(lambda e: e.tensor_reduce(out=a.lam[:, 1:2], in_=a.lq[2], axis=AX.X, op=ALU.add), reads=lu, writes=lu)
    P.act(lambda e: e.activation(out=a.lam[:, 2:4], in_=a.lam[:, 0:2], func=AF.Exp), reads=lu, writes=lu)
    P.dve(lambda e: e.tensor_tensor(out=a.lam[:, 4:5], in0=a.lam[:, 3:4], in1=a.lam[:, 2:3], op=ALU.subtract), reads=lu, writes=lu)
    P.dve(lambda e: e.tensor_scalar(out=a.lam[:, 5:6], in0=a.lam[:, 4:5], scalar1=-lam_init, scalar2=None, op0=ALU.add), reads=lu, writes=lu)
    P.dma("sp", lambda e: e.dma_start(out=a.slnw, in_=g.d["subln_w"][ei].partition_broadcast(128)), "a_slnw", writes=[a.slnwU])
    P.dve(lambda e: e.tensor_scalar(out=a.slnw, in0=a.slnw, scalar1=1.0 - lam_init, scalar2=None, op0=ALU.mult),
          reads=[a.slnwU], writes=[a.slnwU])
    P.dve(lambda e: e.memset(a.v, 1.0), writes=[a.vU])

    def load_w(h):
        i = h % 2
        P.dma("pool", lambda e: e.dma_start(out=a.w[i].rearrange("p k c -> p (k c)"), in_=g.d["ev_w_att"][ei, h]),
              f"a_w{i}", writes=[a.wU[i]])

    pc = [0]

    def project(h):
        wi = h % 2
        w = a.w[wi]
        for n in range(NT):
            sl = slice(n * 512, (n + 1) * 512)
            for which in range(2):
                bk = pc[0] % 2
                pc[0] += 1
                for k in range(KC):
                    P.pe(lambda e, bk=bk, k=k, sl=sl, which=which: e.matmul(bank(g, bk), lhsT=w[:, k, which * 128:(which + 1) * 128],
                                                                            rhs=g.uT[:, k, sl], start=(k == 0), stop=(k == KC - 1)),
                         reads=[g.uU[n], a.wU[wi]], writes=[g.bU[bk]])
                if which == 0:
                    for cc in range(2):
                        P.dve(lambda e, bk=bk, sl=sl, cc=cc: e.tensor_scalar(out=a.qT[cc][:, sl], in0=bank(g, bk), scalar1=c["maskq"][:, cc:cc + 1],
                                                                             scalar2=None, op0=ALU.mult),
                              reads=[g.bU[bk], g.cU], writes=[a.qTU[n]])
                else:
                    P.act(lambda e, bk=bk, sl=sl: e.copy(a.kT[:, sl], bank(g, bk)), reads=[g.bU[bk]], writes=[a.kTU[n]])
        for b in range(NB):
            bk = 2 + (b % 2)
            for k in range(KC):
                P.pe(lambda e, bk=bk, k=k, b=b: e.matmul(bank(g, bk)[:, 0:256], lhsT=g.uT[:, k, b * 128:(b + 1) * 128], rhs=w[:, k, 256:512],
                                                         start=(k == 0), stop=(k == KC - 1)),
                     reads=[g.uU[b // 4], a.wU[wi]], writes=[g.bU[bk]])
            P.act(lambda e, bk=bk, b=b: e.copy(a.v[:, b, 0:128], bank(g, bk)[:, 0:128]), reads=[g.bU[bk]], writes=[a.vU])
            P.act(lambda e, bk=bk, b=b: e.activation(out=a.gs[:, b, :], in_=bank(g, bk)[:, 128:256], func=AF.Silu),
                  reads=[g.bU[bk]], writes=[a.gsU])

    gc = [0]
    a.r2 = [A.f(8) for _ in range(2)]; a.rU2 = P.units(2)
    a.t2 = [A.f(128) for _ in range(2)]; a.tU2 = P.units(2)
    a.o2 = [A.f(128) for _ in range(2)]; a.oU2 = P.units(2)
    a.sqo2 = [A.f(128) for _ in range(2)]; a.sqoU2 = P.units(2)
    a.y2 = [A.f(128) for _ in range(2)]; a.yU2 = P.units(2)

    def attend(h):
        hj = h % 4
        groups = []
        for qb in range(NB):
            for gi in range(qb // 4 + 1):
                kbs = [kb for kb in range(gi * 4, gi * 4 + 4) if kb <= qb]
                groups.append((qb, gi, kbs, gc[0] % 2))
                gc[0] += 1

        def accb(qb, cc):
            return (2 + cc) if qb % 2 == 0 else cc

        def S_(grp):
            qb, gi, kbs, buf = grp
            qs = slice(qb * 128, (qb + 1) * 128)
            for cc in range(2):
                bk = 4 + 2 * cc + buf
                for j, kb in enumerate(kbs):
                    P.pe(lambda e, bk=bk, j=j, kb=kb, cc=cc: e.matmul(bank(g, bk)[:, j * 128:(j + 1) * 128],
                                                                      lhsT=a.kT[:, kb * 128:(kb + 1) * 128], rhs=a.qT[cc][:, qs],
                                                                      start=True, stop=True),
                         reads=[a.kTU[kb // 4], a.qTU[qb // 4]], writes=[g.bU[bk]])

        def E_(grp):
            qb, gi, kbs, buf = grp
            nv = len(kbs)
            for cc in range(2):
                bk = 4 + 2 * cc + buf
                pt = a.PT[cc][buf]
                ptu = a.PTU[cc][buf]
                P.act(lambda e, bk=bk, pt=pt, nv=nv: e.activation(out=pt[:, 0:nv * 128], in_=bank(g, bk)[:, 0:nv * 128], func=AF.Exp,
                                                                  scale=0.125, bias=a.b31[:, h:h + 1]),
                      reads=[g.bU[bk], a.bU_], writes=[ptu])
                for j, kb in enumerate(kbs):
                    Dd = qb - kb
                    if Dd <= 1:
                        P.dve(lambda e, pt=pt, j=j, Dd=Dd: e.tensor_tensor(out=pt[:, j * 128:(j + 1) * 128], in0=pt[:, j * 128:(j + 1) * 128],
                                                                           in1=a.corr[:, h, Dd, :], op=ALU.mult),
                              reads=[ptu, a.corrU], writes=[ptu])

        def PV_(grp):
            qb, gi, kbs, buf = grp
            for cc in range(2):
                pt = a.PT[cc][buf]
                ptu = a.PTU[cc][buf]
                ab = accb(qb, cc)
                for j, kb in enumerate(kbs):
                    P.pe(lambda e, ab=ab, pt=pt, j=j, kb=kb: e.matmul(bank(g, ab)[:, 0:129], lhsT=pt[:, j * 128:(j + 1) * 128],
                                                                      rhs=a.v[:, kb, 0:129], start=(kb == 0), stop=(kb == qb)),
                         reads=[ptu, a.vU], writes=[g.bU[ab]])

        def FIN_(qb):
            qs = slice(qb * 128, (qb + 1) * 128)
            pq = qb % 2
            b0, b1 = accb(qb, 0), accb(qb, 1)
            r, t_, o_, sqo, y_ = a.r2[pq], a.t2[pq], a.o2[pq], a.sqo2[pq], a.y2[pq]
            ru = [a.rU2[pq]]
            tU, oU, sqoU, yU = a.tU2[pq], a.oU2[pq], a.sqoU2[pq], a.yU2[pq]
            P.dve(lambda e: e.reciprocal(r[:, 0:1], bank(g, b0)[:, 128:129]), reads=[g.bU[b0]], writes=ru)
            P.dve(lambda e: e.reciprocal(r[:, 1:2], bank(g, b1)[:, 128:129]), reads=[g.bU[b1]], writes=ru)
            P.dve(lambda e: e.tensor_tensor(out=r[:, 2:3], in0=r[:, 1:2], in1=a.lam[:, 5:6], op=ALU.mult), reads=ru + lu, writes=ru)
            P.dve(lambda e: e.tensor_scalar(out=t_, in0=bank(g, b0)[:, 0:128], scalar1=r[:, 0:1], scalar2=None, op0=ALU.mult),
                  reads=[g.bU[b0]] + ru, writes=[tU])
            P.dve(lambda e: e.scalar_tensor_tensor(out=o_, in0=bank(g, b1)[:, 0:128], scalar=r[:, 2:3], in1=t_, op0=ALU.mult, op1=ALU.add),
                  reads=[g.bU[b1], tU] + ru, writes=[oU])
            P.dve(lambda e: e.tensor_tensor(out=sqo, in0=o_, in1=o_, op=ALU.mult), reads=[oU], writes=[sqoU])
            P.dve(lambda e: e.tensor_reduce(out=r[:, 3:4], in_=sqo, axis=AX.X, op=ALU.add), reads=[sqoU], writes=ru)
            P.act(lambda e: e.activation(out=r[:, 4:5], in_=r[:, 3:4], func=AF.Ln, scale=1.0 / 128, bias=EPS), reads=ru, writes=ru)
            P.act(lambda e: e.activation(out=r[:, 5:6], in_=r[:, 4:5], func=AF.Exp, scale=-0.5), reads=ru, writes=ru)
            P.dve(lambda e: e.scalar_tensor_tensor(out=y_, in0=o_, scalar=r[:, 5:6], in1=a.slnw, op0=ALU.mult, op1=ALU.mult),
                  reads=[oU, a.slnwU] + ru, writes=[yU])
            P.dve(lambda e: e.tensor_tensor(out=y_, in0=y_, in1=a.gs[:, qb, :], op=ALU.mult), reads=[yU, a.gsU], writes=[yU])
            P.pe(lambda e: e.transpose(bank(g, b0)[:, 256:384], y_, ident[:]), reads=[yU, g.cU], writes=[g.bU[b0]])
            P.act(lambda e: e.copy(a.yT[:, hj, qs], bank(g, b0)[:, 256:384]), reads=[g.bU[b0]], writes=[a.yTU[hj]])

        M = len(groups)
        S_(groups[0])
        pending_fin = None
        for i in range(M):
            if i + 1 < M:
                S_(groups[i + 1])
            E_(groups[i])
            PV_(groups[i])
            if pending_fin is not None:
                FIN_(pending_fin)
                pending_fin = None
            qb, gi, kbs, buf = groups[i]
            if kbs[-1] == qb:
                pending_fin = qb
        if pending_fin is not None:
            FIN_(pending_fin)

    oc = [0]

    def outproj(hg):
        for m in range(KC):
            wi = oc[0] % 2
            oc[0] += 1
            c0 = (8 + hg * 4) * 128
            P.dma("pool", lambda e, m=m, wi=wi, c0=c0: e.dma_start(out=a.wo[wi].rearrange("p j c -> p (j c)"),
                                                                  in_=g.d["ev_w_out_t"][ei, m, :, c0:c0 + 512]),
                  f"a_wo{wi}", writes=[a.woU[wi]])
            for n in range(NT):
                sl = slice(n * 512, (n + 1) * 512)
                bk = n % 2
                for j in range(4):
                    P.pe(lambda e, bk=bk, j=j, sl=sl, wi=wi: e.matmul(bank(g, bk), lhsT=a.wo[wi][:, j, :], rhs=a.yT[:, j, sl],
                                                                      start=(j == 0), stop=(j == 3)),
                         reads=[a.woU[wi], a.yTU[j]], writes=[g.bU[bk]])
                P.dve(lambda e, bk=bk, m=m, sl=sl: e.tensor_tensor(out=g.hT[:, m, sl], in0=g.hT[:, m, sl], in1=bank(g, bk), op=ALU.add),
                      reads=[g.bU[bk], g.hU[m][n]], writes=[g.hU[m][n]])

    load_w(0)
    for h in range(8):
        if h + 1 < 8:
            load_w(h + 1)
        project(h)
        attend(h)
        if h % 4 == 3:
            outproj(h // 4)


_CACHE = {}


def kernel(**inputs):
    x = np.ascontiguousarray(np.asarray(inputs["x"], dtype=np.float32))
    Bsz, L, _ = x.shape
    n_cores = 8
    nseq = Bsz // n_cores
    key = (L, nseq)
    if key not in _CACHE:
        _CACHE[key] = build(L, nseq, (0, 1, 2, 3))
    nc, _ = _CACHE[key]
    common = host_layout(inputs)
    in_maps = []
    for cidx in range(n_cores):
        m = dict(common)
        m["x"] = x[cidx * nseq:(cidx + 1) * nseq]
        in_maps.append(m)
    res = run_bass_kernel_spmd(nc, in_maps, core_ids=list(range(n_cores)))
    out = np.concatenate([np.asarray(r["out"]) for r in res.results], axis=0)
    return out.astype(np.float32)
```

```python
import math, contextlib
import numpy as np
import concourse.bass as bass
import concourse.mybir as mybir
from concourse.bass_utils import run_bass_kernel_spmd
from concourse.alu_op_type import AluOpType as ALU

F32 = mybir.dt.float32
BF16 = mybir.dt.bfloat16
AF = mybir.ActivationFunctionType
AX = mybir.AxisListType

D = 1024
KC = 8
EPS = 1e-6
DEPTH = 4
HG_W = 2048
ARF_N = 11392
ARB_N = 35840


class Unit:
    __slots__ = ("name", "lw", "rd")

    def __init__(self, name):
        self.name = name
        self.lw = None
        self.rd = []


class _Rec:
    def __init__(self):
        self.call = None

    def __getattr__(self, name):
        def f(*args, **kw):
            assert self.call is None
            self.call = (name, args, kw)
            return None
        return f


class Prog:
    ENGS = ("pe", "act", "dve", "pool", "sp")

    def __init__(self, nc):
        self.nc = nc
        self.ops = []
        self.nunits = 0
        self.last_eng = {}
        self.last_key = {}

    def unit(self, name=None):
        self.nunits += 1
        return Unit(name or f"u{self.nunits}")

    def units(self, n, name="u"):
        return [self.unit(f"{name}{i}") for i in range(n)]

    def capture(self):
        self._cap = []
        return self._cap

    def end_capture(self):
        c, self._cap = self._cap, None
        return c

    def replay_merged(self, A, B):
        na, nb = len(A), len(B)
        ia = ib = 0
        while ia < na or ib < nb:
            if ib >= nb or (ia < na and ia * nb <= ib * na):
                self.op(*A[ia]); ia += 1
            else:
                self.op(*B[ib]); ib += 1

    def op(self, eng, fn, reads=(), writes=(), dma_key=None, extra_deps=()):
        if fn is not None and not isinstance(fn, tuple):
            rec = _Rec()
            fn(rec)
            assert rec.call is not None
            fn = rec.call
        if getattr(self, "_cap", None) is not None:
            self._cap.append((eng, fn, tuple(reads), tuple(writes), dma_key, tuple(extra_deps)))
            return None
        idx = len(self.ops)
        deps = set(extra_deps)
        for u in reads:
            if u.lw is not None:
                deps.add(u.lw)
        for u in writes:
            if u.lw is not None:
                deps.add(u.lw)
            deps.update(u.rd)
        for u in reads:
            u.rd.append(idx)
        for u in writes:
            u.lw = idx
            u.rd = []
        deps.discard(idx)
        self.ops.append(dict(eng=eng, fn=fn, deps=deps, dma_key=dma_key))
        if fn is not None:
            if dma_key is None:
                self.last_eng[eng] = idx
            else:
                self.last_key[dma_key] = idx
        return idx

    def pe(self, fn, reads=(), writes=()):
        return self.op("pe", fn, reads, writes)

    def act(self, fn, reads=(), writes=()):
        return self.op("act", fn, reads, writes)

    def dve(self, fn, reads=(), writes=()):
        return self.op("dve", fn, reads, writes)

    def pool(self, fn, reads=(), writes=()):
        return self.op("pool", fn, reads, writes)

    def dma(self, eng, fn, key, reads=(), writes=()):
        return self.op(eng, fn, reads, writes, dma_key=key)

    def barrier(self):
        deps = set(self.last_eng.values()) | set(self.last_key.values())
        for e in self.ENGS:
            self.op(e, None, extra_deps=deps)

    def emit(self, final_wait_ops=()):
        nc = self.nc
        ops = self.ops
        n = len(ops)

        def skip(od, o):
            return (od["eng"] == "pe" and o["eng"] == "pe" and od["dma_key"] is None
                    and o["dma_key"] is None and o["fn"] is not None)

        needed = [False] * n
        for i, o in enumerate(ops):
            for d in o["deps"]:
                if skip(ops[d], o):
                    continue
                needed[d] = True
        for d in final_wait_ops:
            needed[d] = True
        chan_count = {}
        ev = [None] * n
        for i, o in enumerate(ops):
            if o["fn"] is None:
                continue
            if o["dma_key"] is not None:
                ch = ("dma", o["dma_key"])
                chan_count[ch] = chan_count.get(ch, 0) + 16
                ev[i] = (ch, chan_count[ch])
            elif needed[i]:
                ch = ("eng", o["eng"])
                chan_count[ch] = chan_count.get(ch, 0) + 1
                ev[i] = (ch, chan_count[ch])
        chans = sorted(chan_count.keys(), key=str)
        self.n_sems = len(chans)
        sems = {}
        stack = contextlib.ExitStack()
        for ci, ch in enumerate(chans):
            sems[ch] = stack.enter_context(nc.semaphore(f"s{ci}"))
        known = {e: {} for e in self.ENGS}
        clock = [None] * n
        streams = {e: [] for e in self.ENGS}
        for i, o in enumerate(ops):
            e = o["eng"]
            kn = known[e]
            wd = {}
            for d in sorted(o["deps"]):
                od = ops[d]
                if skip(od, o):
                    continue
                ch, v = ev[d]
                if kn.get(ch, 0) >= v:
                    continue
                for c2, v2 in clock[d].items():
                    if kn.get(c2, 0) < v2:
                        kn[c2] = v2
                wd[ch] = max(wd.get(ch, 0), v)
            ck = dict(kn)
            if ev[i] is not None:
                ch, v = ev[i]
                ck[ch] = v
            clock[i] = ck
            streams[e].append((list(wd.items()), o["fn"], ev[i]))
        final = [ev[d] for d in final_wait_ops]
        for ch, tot in chan_count.items():
            if ch[0] == "dma":
                final.append((ch, tot))
        self.sems, self.streams, self.final, self._stack = sems, streams, final, stack

    def run_block(self):
        nc = self.nc
        sems, streams, final = self.sems, self.streams, self.final
        with nc.Block() as block:
            def mk(ename):
                def body(eng):
                    for waits, fn, e in streams[ename]:
                        for ch, v in waits:
                            eng.wait_ge(sems[ch], v)
                        if fn is None:
                            continue
                        ins = getattr(eng, fn[0])(*fn[1], **fn[2])
                        if e is not None:
                            ins.then_inc(sems[e[0]], 16 if e[0][0] == "dma" else 1)
                    if ename == "sp":
                        for ch, v in final:
                            eng.wait_ge(sems[ch], v)
                return body
            block.tensor(mk("pe"))
            block.scalar(mk("act"))
            block.vector(mk("dve"))
            block.gpsimd(mk("pool"))
            block.sync(mk("sp"))
        self._stack.close()


def _t5_bucket(rel):
    n = np.maximum(rel, 0)
    max_exact = 16
    large = max_exact + (np.log(np.maximum(n, 1).astype(np.float32) / max_exact)
                         / math.log(128 / max_exact) * (32 - max_exact)).astype(np.int32)
    large = np.minimum(large, 31)
    return np.where(n < max_exact, n, large)


def host_consts():
    s = np.arange(128)[:, None]
    t = np.arange(128)[None, :]
    c = {}
    c["ident"] = np.eye(128, dtype=np.float32)
    c["ones"] = np.ones((128, 128), np.float32)
    c["triC"] = ((s <= t).astype(np.float32) - (s <= 63).astype(np.float32))
    c["triU"] = (s > t).astype(np.float32)
    c["triI"] = (s <= t).astype(np.float32)
    sel = np.zeros((128, 2), np.float32)
    sel[:64, 0] = 1.0
    sel[:, 1] = 1.0
    c["sel"] = sel
    c["mg16"] = np.eye(16, dtype=np.float32)
    mq = np.zeros((128, 2), np.float32)
    mq[:64, 0] = 1.0
    mq[64:, 1] = 1.0
    c["maskq"] = mq
    return c


def host_layout(inp):
    f = lambda a: np.ascontiguousarray(np.asarray(a, dtype=np.float32))
    m = dict(host_consts())
    m["final_norm_w"] = f(inp["final_norm_w"])
    m["norm_w_cols"] = f(np.asarray(inp["norm_w"]).reshape(4, 8, 128).transpose(0, 2, 1))
    owin = np.asarray(inp["odd_w_in"])
    t = owin.reshape(2, 8, 128, 4, 16, 128).transpose(0, 4, 2, 1, 3, 5)
    m["odd_w_in_t"] = f(t).reshape(2, 16, 128, 8 * 512)
    m["odd_w_out"] = f(inp["odd_w_out"])
    m["hgrn_lower_bounds"] = f(inp["hgrn_lower_bounds"])
    m["hgrn_norm_w"] = f(inp["hgrn_norm_w"])
    ew = np.asarray(inp["even_w_in"]).reshape(2, 8, 128, 7184)
    z = ew[..., 0:1024]; xs = ew[..., 1024:2048]; Bm = ew[..., 2048:2560]; Cm = ew[..., 2560:3072]
    dt = ew[..., 3072:3088]
    q = ew[..., 3088:4112]; kk = ew[..., 4112:5136]; v = ew[..., 5136:6160]; gg = ew[..., 6160:7184]
    ssd = np.concatenate([z.reshape(2, 8, 128, 4, 256), xs.reshape(2, 8, 128, 4, 256),
                          Bm.reshape(2, 8, 128, 4, 128), Cm.reshape(2, 8, 128, 4, 128)], axis=-1)
    m["ev_w_ssd"] = f(ssd.transpose(0, 3, 2, 1, 4)).reshape(2, 4, 128, 8 * 768)
    m["ev_w_dt"] = f(dt.transpose(0, 2, 1, 3)).reshape(2, 128, 8 * 16)
    att = np.concatenate([q.reshape(2, 8, 128, 8, 128), kk.reshape(2, 8, 128, 8, 128),
                          v.reshape(2, 8, 128, 8, 128), gg.reshape(2, 8, 128, 8, 128)], axis=-1)
    m["ev_w_att"] = f(att.transpose(0, 3, 2, 1, 4)).reshape(2, 8, 128, 8 * 512)
    wo = np.asarray(inp["even_w_out"]).reshape(2, 16, 128, 8, 128)
    m["ev_w_out_t"] = f(wo.transpose(0, 3, 2, 1, 4)).reshape(2, 8, 128, 16 * 128)
    m["conv_w_cols"] = f(np.asarray(inp["conv_w"]).reshape(2, 4, 16, 128).transpose(0, 3, 2, 1)).reshape(2, 128, 64)
    m["conv_b_cols"] = f(np.asarray(inp["conv_b"]).reshape(2, 16, 128).transpose(0, 2, 1))
    for nm in ("dt_bias", "A_log", "D_skip", "lambda_q1", "lambda_k1", "lambda_q2", "lambda_k2", "subln_w"):
        m[nm] = f(inp[nm])
    m["ssd_norm_w_cols"] = f(np.asarray(inp["ssd_norm_w"]).reshape(2, 8, 128).transpose(0, 2, 1))
    rb = np.asarray(inp["rel_bias"], dtype=np.float32)
    kpos = np.arange(128)[:, None]
    qpos = np.arange(128)[None, :]
    bd = np.empty((128, 8, 2, 128), np.float32)
    for Dd in range(2):
        rel = qpos - kpos + 128 * Dd
        bidx = _t5_bucket(rel)
        g_ = rb[bidx]
        g_ = np.where((rel >= 0)[:, :, None], g_, np.float32(-30000.0))
        bd[:, :, Dd, :] = g_.transpose(0, 2, 1)
    m["rel_biasD"] = f(bd).reshape(128, 8 * 2 * 128)
    m["rel_b31"] = f(rb[31])
    return m


class NS:
    pass


class Carver:
    def __init__(self, g):
        self.g = g
        self.fo = 0
        self.bo = 0

    def f(self, n):
        ap = self.g.arf[:, self.fo:self.fo + n]
        self.fo += (n + 7) // 8 * 8
        assert self.fo <= ARF_N, ("ARF overflow", self.fo)
        return ap

    def b(self, n):
        ap = self.g.arb[:, self.bo:self.bo + n]
        self.bo += (n + 15) // 16 * 16
        assert self.bo <= ARB_N, ("ARB overflow", self.bo)
        return ap


def bank(g, i):
    return g.ps[:, i, :]


def build(L=2048, NSEQ=2, layers=(0, 1, 2, 3)):
    nc = bass.Bass("TRN2", target_bir_lowering=False, dynamic_dma_scratch_size=4096)
    NT, NB = L // 512, L // 128
    g = NS()
    g.nc, g.L, g.NT, g.NB = nc, L, NT, NB
    dr = lambda name, shape, kind="ExternalInput": nc.dram_tensor(name, list(shape), F32, kind=kind).ap()
    g.x_d = dr("x", [NSEQ, L, D])
    g.out_d = dr("out", [NSEQ, L, D], "ExternalOutput")
    g.d = {}
    shapes = {
        "final_norm_w": [D], "norm_w_cols": [DEPTH, 128, KC],
        "ident": [128, 128], "ones": [128, 128], "triC": [128, 128], "triU": [128, 128], "triI": [128, 128],
        "sel": [128, 2], "mg16": [16, 16], "maskq": [128, 2],
        "odd_w_in_t": [2, 16, 128, KC * 512], "odd_w_out": [2, HG_W, D], "hgrn_lower_bounds": [DEPTH, HG_W],
        "hgrn_norm_w": [2, 128],
        "ev_w_ssd": [2, 4, 128, 8 * 768], "ev_w_dt": [2, 128, 8 * 16], "ev_w_att": [2, 8, 128, 8 * 512],
        "ev_w_out_t": [2, 8, 128, 16 * 128], "conv_w_cols": [2, 128, 64], "conv_b_cols": [2, 128, 16],
        "dt_bias": [2, 16], "A_log": [2, 16], "D_skip": [2, 16], "lambda_q1": [2, 64], "lambda_k1": [2, 64],
        "lambda_q2": [2, 64], "lambda_k2": [2, 64], "subln_w": [2, 128], "ssd_norm_w_cols": [2, 128, 8],
        "rel_biasD": [128, 8 * 2 * 128], "rel_b31": [8],
    }
    for nm, shp in shapes.items():
        g.d[nm] = dr(nm, shp)
    g.in_names = ["x"] + list(shapes.keys())

    es = contextlib.ExitStack()
    sb = lambda name, shape, dt=F32: es.enter_context(nc.sbuf_tensor(name, list(shape), dt))
    P = Prog(nc)
    g.P = P
    g.hT = sb("hT", [128, KC, L]); g.hU = [[P.unit(f"h{k}_{n}") for n in range(NT)] for k in range(KC)]
    g.uT = sb("uT", [128, KC, L], BF16); g.uU = [P.unit(f"u{n}") for n in range(NT)]
    g.cst = {}
    g.cU = P.unit("consts")
    for nm in ("ident", "ones", "triI"):
        g.cst[nm] = sb("c_" + nm, [128, 128])
    g.cst["sel"] = sb("c_sel", [128, 2])
    g.cst["maskq"] = sb("c_maskq", [128, 2])
    g.cst["mg16"] = sb("c_mg16", [16, 16])
    g.nwc = sb("nwc", [128, DEPTH, KC])
    g.stat = sb("stat", [128, 16]); g.statU = P.unit("stat")
    g.sq = [sb(f"sq{i}", [128, 512]) for i in range(2)]; g.sqU = P.units(2, "sq")
    g.rstd_t = sb("rstd_t", [128, 512]); g.rstdU = P.unit("rstd_t")
    g.arf = sb("arf", [128, ARF_N])
    g.arb = sb("arb", [128, ARB_N], BF16)
    g.ps = es.enter_context(nc.psum_tensor("ps", [128, 8, 512], F32))
    g.bU = P.units(8, "bank")

    for nm in ("ident", "ones", "triI", "sel", "maskq", "mg16"):
        P.dma("sp", lambda e, nm=nm: e.dma_start(out=g.cst[nm][:], in_=g.d[nm]), "c_" + nm, writes=[g.cU])
    P.dma("sp", lambda e: e.dma_start(out=g.nwc[:], in_=g.d["norm_w_cols"].rearrange("l p k -> p l k")), "c_nwc", writes=[g.cU])

    out_ops = []
    for s in range(NSEQ):
        P.barrier()
        load_x(g, s)
        P.barrier()
        for li in layers:
            rms_to_uT(g, li)
            if li % 2 == 1:
                odd_layer(g, li)
            else:
                even_ssd(g, li)
                P.barrier()
                even_attn(g, li)
            P.barrier()
        out_ops += final_norm_store(g, s)
    P.emit(final_wait_ops=out_ops[-4:])
    P.run_block()
    es.close()
    return nc, P


def load_x(g, s):
    P, NT = g.P, g.NT
    A = Carver(g)
    xst = A.f(4096).rearrange("p (b d) -> p b d", b=4)
    xU = P.unit("xst")
    ident = g.cst["ident"]
    for n in range(NT):
        src = g.x_d[s, n * 512:(n + 1) * 512, :].rearrange("(b p) d -> p b d", p=128)
        P.dma("sp", lambda e, src=src: e.dma_start(out=xst, in_=src), "xst", writes=[xU])
        for k in range(KC):
            bk = k % 8
            for b in range(4):
                P.pe(lambda e, bk=bk, b=b, k=k: e.transpose(
                    bank(g, bk)[:, b * 128:(b + 1) * 128], xst[:, b, k * 128:(k + 1) * 128], ident[:]),
                    reads=[xU, g.cU], writes=[g.bU[bk]])
            if k % 2 == 0:
                P.dve(lambda e, bk=bk, k=k, n=n: e.tensor_copy(g.hT[:, k, n * 512:(n + 1) * 512], bank(g, bk)),
                      reads=[g.bU[bk]], writes=[g.hU[k][n]])
            else:
                P.act(lambda e, bk=bk, k=k, n=n: e.copy(g.hT[:, k, n * 512:(n + 1) * 512], bank(g, bk)),
                      reads=[g.bU[bk]], writes=[g.hU[k][n]])


def final_norm_store(g, s):
    P, NB = g.P, g.NB
    A = Carver(g)
    fnw = A.f(D); fnwU = P.unit("fnw")
    ost = [A.f(D) for _ in range(2)]; ostU = P.units(2, "ost")
    junk = A.f(1024); junkU = P.unit("junk")
    ident = g.cst["ident"]
    P.dma("sp", lambda e: e.dma_start(out=fnw, in_=g.d["final_norm_w"].partition_broadcast(128)), "fnw", writes=[fnwU])
    outs = []
    for b in range(NB):
        n = b // 4
        oi = b % 2
        for k in range(KC):
            bk = k // 4
            P.pe(lambda e, bk=bk, k=k, b=b: e.transpose(
                bank(g, bk)[:, (k % 4) * 128:(k % 4 + 1) * 128], g.hT[:, k, b * 128:(b + 1) * 128], ident[:]),
                reads=[g.hU[k][n], g.cU], writes=[g.bU[bk]])
        for half in range(2):
            P.act(lambda e, half=half: e.activation(
                out=junk[:, half * 512:(half + 1) * 512], in_=bank(g, half), func=AF.Square,
                accum_out=g.stat[:, half:half + 1]),
                reads=[g.bU[half]], writes=[junkU, g.statU])
        P.dve(lambda e: e.tensor_tensor(out=g.stat[:, 2:3], in0=g.stat[:, 0:1], in1=g.stat[:, 1:2], op=ALU.add),
              reads=[g.statU], writes=[g.statU])
        P.act(lambda e: e.activation(out=g.stat[:, 3:4], in_=g.stat[:, 2:3], func=AF.Ln, scale=1.0 / D, bias=EPS),
              reads=[g.statU], writes=[g.statU])
        P.act(lambda e: e.activation(out=g.stat[:, 4:5], in_=g.stat[:, 3:4], func=AF.Exp, scale=-0.5),
              reads=[g.statU], writes=[g.statU])
        for half in range(2):
            P.dve(lambda e, half=half, oi=oi: e.scalar_tensor_tensor(
                out=ost[oi][:, half * 512:(half + 1) * 512], in0=bank(g, half), scalar=g.stat[:, 4:5],
                in1=fnw[:, half * 512:(half + 1) * 512], op0=ALU.mult, op1=ALU.mult),
                reads=[g.bU[half], g.statU, fnwU], writes=[ostU[oi]])
        o = P.dma("sp", lambda e, oi=oi, s=s, b=b: e.dma_start(out=g.out_d[s, b * 128:(b + 1) * 128, :], in_=ost[oi]),
                  f"ost{oi}", reads=[ostU[oi]])
        outs.append(o)
    return outs


def rms_rstd_tile(g, src_fn, reads_fn, nchunks, dim):
    P = g.P
    ones = g.cst["ones"]
    for k in range(nchunks):
        i = k % 2
        P.act(lambda e, i=i, k=k: e.activation(out=g.sq[i][:], in_=src_fn(k), func=AF.Square),
              reads=reads_fn(k), writes=[g.sqU[i]])
        P.pe(lambda e, i=i, k=k: e.matmul(bank(g, 7), lhsT=ones[:], rhs=g.sq[i][:], start=(k == 0), stop=(k == nchunks - 1)),
             reads=[g.sqU[i], g.cU], writes=[g.bU[7]])
    P.act(lambda e: e.activation(out=g.rstd_t[:], in_=bank(g, 7), func=AF.Ln, scale=1.0 / dim, bias=EPS),
          reads=[g.bU[7]], writes=[g.rstdU])
    P.act(lambda e: e.activation(out=g.rstd_t[:], in_=g.rstd_t[:], func=AF.Exp, scale=-0.5),
          reads=[g.rstdU], writes=[g.rstdU])


def rms_to_uT(g, li):
    P, NT = g.P, g.NT
    for n in range(NT):
        sl = slice(n * 512, (n + 1) * 512)
        rms_rstd_tile(g, lambda k, sl=sl: g.hT[:, k, sl], lambda k, n=n: [g.hU[k][n]], KC, D)
        for k in range(KC):
            P.dve(lambda e, k=k, sl=sl: e.scalar_tensor_tensor(
                out=g.uT[:, k, sl], in0=g.hT[:, k, sl], scalar=g.nwc[:, li, k:k + 1], in1=g.rstd_t[:],
                op0=ALU.mult, op1=ALU.mult),
                reads=[g.hU[k][n], g.rstdU, g.cU], writes=[g.uU[n]])


def odd_layer(g, li):
    P, L, NB, NT = g.P, g.L, g.NB, g.NT
    oi = li // 2
    A = Carver(g)
    o = NS()
    c = g.cst
    f3 = lambda: A.f(512).rearrange("p (b d) -> p b d", b=4)
    f16 = lambda: A.f(NB * 128).rearrange("p (b d) -> p b d", d=128)
    o.fall = f16(); o.fallU = [P.unit() for _ in range(NT)]
    o.kkall = f16(); o.kkallU = [P.unit() for _ in range(NT)]
    o.qsall = f16(); o.qsallU = [P.unit() for _ in range(NT)]
    o.e13 = A.f(1024).rearrange("p (t b d) -> p t b d", t=2, b=4); o.e13U = P.unit()
    o.e2 = f3(); o.e2U = P.unit()
    o.kt = o.e2; o.ktU = o.e2U
    o.lbr = f3(); o.lbrU = P.unit()
    o.lbh = A.f(128); o.omlh = A.f(128); o.den = A.f(128); o.lbU = P.unit()
    o.eb = [A.f(NB * 2).rearrange("p (b t) -> p b t", t=2) for _ in range(2)]
    o.S = A.f(128); o.SU = P.unit()
    o.junk2 = A.f(128); o.junk2U = P.unit()
    o.oall = A.f(NB * 128).rearrange("p (b d) -> p b d", d=128); o.oallU = P.unit()
    o.ssall = A.f(NB); o.rsall = A.f(NB); o.ssU = P.unit()
    o.hnw = A.f(128); o.hnwU = P.unit()
    o.triC = A.f(128); o.triU = A.f(128); o.triUU = P.unit()
    o.bst = A.f(8); o.bstU = P.unit()
    o.w = [A.b(KC * 512).rearrange("p (k c) -> p k c", k=KC) for _ in range(2)]; o.wU = P.units(2)
    o.wout = A.b(2 * D).rearrange("p (j m) -> p j m", j=2); o.woutU = P.units(2)
    o.qT = [A.b(L) for _ in range(2)]
    o.kT = [A.b(L) for _ in range(2)]
    hb3 = lambda: A.b(NB * 128).rearrange("p (b d) -> p b d", d=128)
    o.kh = [hb3() for _ in range(2)]
    o.v = [hb3() for _ in range(2)]
    o.gs = [hb3() for _ in range(2)]
    o.hbU = [[[P.unit() for _ in range(NT)] for _ in range(6)] for _ in range(2)]
    o.attm = [A.b(128) for _ in range(2)]; o.attmU = P.units(2)
    o.Sb2 = [A.b(128) for _ in range(2)]; o.SbU2 = P.units(2)
    o.yT = A.b(2 * L).rearrange("p (j t) -> p j t", j=2); o.yTU = P.units(2)
    QT, KT, KH, VV, GS, EB = range(6)

    P.dma("sp", lambda e: e.dma_start(out=o.hnw, in_=g.d["hgrn_norm_w"][oi].partition_broadcast(128)), "o_hnw", writes=[o.hnwU])
    P.dma("sp", lambda e: e.dma_start(out=o.triC, in_=g.d["triC"]), "o_tri", writes=[o.triUU])
    P.dma("sp", lambda e: e.dma_start(out=o.triU, in_=g.d["triU"]), "o_tri", writes=[o.triUU])
    for i in range(2):
        P.dve(lambda e, i=i: e.memset(o.attm[i], 0.0), writes=[o.attmU[i]])

    def load_w(h):
        i = h % 2
        P.dma("pool", lambda e: e.dma_start(out=o.w[i].rearrange("p k c -> p (k c)"), in_=g.d["odd_w_in_t"][oi, h]),
              f"o_w{i}", writes=[o.wU[i]])

    def head_lb(h):
        hs = slice(h * 128, (h + 1) * 128)
        P.dma("sp", lambda e: e.dma_start(out=o.lbr, in_=g.d["hgrn_lower_bounds"][:, hs].partition_broadcast(128)),
              "o_lbr", writes=[o.lbrU])
        P.act(lambda e: e.activation(out=o.lbr, in_=o.lbr, func=AF.Exp), reads=[o.lbrU], writes=[o.lbrU])
        P.dve(lambda e: e.tensor_tensor(out=o.den, in0=o.lbr[:, 0, :], in1=o.lbr[:, 1, :], op=ALU.add), reads=[o.lbrU], writes=[o.lbU])
        P.dve(lambda e: e.tensor_tensor(out=o.den, in0=o.den, in1=o.lbr[:, 2, :], op=ALU.add), reads=[o.lbrU, o.lbU], writes=[o.lbU])
        P.dve(lambda e: e.tensor_tensor(out=o.den, in0=o.den, in1=o.lbr[:, 3, :], op=ALU.add), reads=[o.lbrU, o.lbU], writes=[o.lbU])
        P.dve(lambda e: e.reciprocal(o.den, o.den), reads=[o.lbU], writes=[o.lbU])
        if li == 1:
            P.dve(lambda e: e.tensor_tensor(out=o.lbh, in0=o.lbr[:, 1, :], in1=o.den, op=ALU.mult), reads=[o.lbrU, o.lbU], writes=[o.lbU])
        else:
            P.dve(lambda e: e.tensor_tensor(out=o.lbh, in0=o.lbr[:, 1, :], in1=o.lbr[:, 2, :], op=ALU.add), reads=[o.lbrU, o.lbU], writes=[o.lbU])
            for j in range(3, li + 1):
                P.dve(lambda e, j=j: e.tensor_tensor(out=o.lbh, in0=o.lbh, in1=o.lbr[:, j, :], op=ALU.add), reads=[o.lbrU, o.lbU], writes=[o.lbU])
            P.dve(lambda e: e.tensor_tensor(out=o.lbh, in0=o.lbh, in1=o.den, op=ALU.mult), reads=[o.lbU], writes=[o.lbU])
        P.dve(lambda e: e.tensor_scalar(out=o.omlh, in0=o.lbh, scalar1=-1.0, scalar2=1.0, op0=ALU.mult, op1=ALU.add),
              reads=[o.lbU], writes=[o.lbU])

    def stageA1(h, n):
        hb = h % 2
        wi = h % 2
        U = o.hbU[hb]
        bs = slice(n * 4, (n + 1) * 4)
        for b in range(4):
            tb = n * 4 + b
            for k in range(KC):
                P.pe(lambda e, b=b, tb=tb, k=k: e.matmul(bank(g, b), lhsT=g.uT[:, k, tb * 128:(tb + 1) * 128], rhs=o.w[wi][:, k, :],
                                                         start=(k == 0), stop=(k == KC - 1)),
                     reads=[g.uU[n], o.wU[wi]], writes=[g.bU[b]])
        pj = g.ps[:, 0:4, :]
        pb = [g.bU[0], g.bU[1], g.bU[2], g.bU[3]]
        bc4 = lambda t: t.unsqueeze(1).to_broadcast([128, 4, 128])
        P.act(lambda e: e.activation(out=o.fall[:, bs, :], in_=pj[:, :, 128:256], func=AF.Sigmoid), reads=pb, writes=[o.fallU[n]])
        P.act(lambda e: e.activation(out=o.qsall[:, bs, :], in_=pj[:, :, 0:128], func=AF.Silu), reads=pb, writes=[o.qsallU[n]])
        P.act(lambda e: e.activation(out=o.gs[hb][:, bs, :], in_=pj[:, :, 384:512], func=AF.Silu), reads=pb, writes=[U[GS][n]])
        P.act(lambda e: e.copy(o.v[hb][:, bs, :], pj[:, :, 256:384]), reads=pb, writes=[U[VV][n]])
        P.dve(lambda e: e.tensor_tensor(out=o.fall[:, bs, :], in0=o.fall[:, bs, :], in1=bc4(o.omlh), op=ALU.mult),
              reads=[o.fallU[n], o.lbU], writes=[o.fallU[n]])
        P.dve(lambda e: e.tensor_tensor(out=o.fall[:, bs, :], in0=o.fall[:, bs, :], in1=bc4(o.lbh), op=ALU.add),
              reads=[o.fallU[n], o.lbU], writes=[o.fallU[n]])
        P.pool(lambda e: e.tensor_scalar(out=o.kkall[:, bs, :], in0=o.fall[:, bs, :], scalar1=-1.0, scalar2=1.0, op0=ALU.mult, op1=ALU.add),
               reads=[o.fallU[n]], writes=[o.kkallU[n]])

    def stageAmid(h):
        P.act(lambda e: e.activation(out=o.fall, in_=o.fall, func=AF.Ln), reads=o.fallU + o.kkallU, writes=o.fallU)

    def stageA2(h, n):
        hb = h % 2
        U = o.hbU[hb]
        bs = slice(n * 4, (n + 1) * 4)
        logf = o.fall[:, bs, :]
        kk = o.kkall[:, bs, :]
        qs = o.qsall[:, bs, :]
        lu = [o.fallU[n]]
        lf4 = logf.rearrange("p b d -> p (b d)")
        P.pe(lambda e: e.matmul(bank(g, 0), lhsT=o.triC, rhs=lf4, start=True, stop=True), reads=lu + [o.triUU], writes=[g.bU[0]])
        P.pe(lambda e: e.matmul(bank(g, 1), lhsT=o.triU, rhs=lf4, start=True, stop=True), reads=lu + [o.triUU], writes=[g.bU[1]])
        for b in range(4):
            P.pe(lambda e, b=b: e.matmul(bank(g, 2)[:, b * 2:b * 2 + 2], lhsT=logf[:, b, :], rhs=c["sel"][:], start=True, stop=True),
                 reads=lu + [g.cU], writes=[g.bU[2]])
        P.act(lambda e: e.activation(out=o.eb[hb][:, bs, :], in_=bank(g, 2)[:, 0:8].rearrange("p (b t) -> p b t", t=2), func=AF.Exp),
              reads=[g.bU[2]], writes=[U[EB][n]])
        P.act(lambda e: e.activation(out=o.e13, in_=g.ps[:, 0:2, :].rearrange("p t (b d) -> p t b d", b=4), func=AF.Exp),
              reads=[g.bU[0], g.bU[1]], writes=[o.e13U])
        P.act(lambda e: e.activation(out=o.e2, in_=bank(g, 0).rearrange("p (b d) -> p b d", b=4), func=AF.Exp, scale=-1.0),
              reads=[g.bU[0]], writes=[o.e2U])
        P.dve(lambda e: e.tensor_tensor(out=qs, in0=qs, in1=o.e13[:, 0], op=ALU.mult), reads=[o.qsallU[n], o.e13U], writes=[o.qsallU[n]])
        P.pool(lambda e: e.tensor_tensor(out=o.e2, in0=kk, in1=o.e2, op=ALU.mult), reads=[o.kkallU[n], o.e2U], writes=[o.e2U])
        P.dve(lambda e: e.tensor_tensor(out=o.kh[hb][:, bs, :], in0=kk, in1=o.e13[:, 1], op=ALU.mult),
              reads=[o.kkallU[n], o.e13U], writes=[U[KH][n]])
        idf = c["ident"]
        for b in range(4):
            P.pe(lambda e, b=b: e.transpose(bank(g, 3)[:, b * 128:(b + 1) * 128], qs[:, b, :], idf[:]),
                 reads=[o.qsallU[n], g.cU], writes=[g.bU[3]])
        for b in range(4):
            P.pe(lambda e, b=b: e.transpose(bank(g, 2)[:, b * 128:(b + 1) * 128], o.kt[:, b, :], idf[:]),
                 reads=[o.ktU, g.cU], writes=[g.bU[2]])
        P.act(lambda e: e.copy(o.qT[hb][:, n * 512:(n + 1) * 512], bank(g, 3)), reads=[g.bU[3]], writes=[U[QT][n]])
        P.act(lambda e: e.copy(o.kT[hb][:, n * 512:(n + 1) * 512], bank(g, 2)), reads=[g.bU[2]], writes=[U[KT][n]])

    def stageAall(h):
        for n in range(NT):
            stageA1(h, n)
        stageAmid(h)
        for n in range(NT):
            stageA2(h, n)

    def stageB(h):
        hb = h % 2
        U = o.hbU[hb]

        def att(tb):
            n = tb // 4
            ts = slice(tb * 128, (tb + 1) * 128)
            bk = 5 + 2 * (tb % 2)
            P.pe(lambda e: e.matmul(bank(g, bk)[:, 64:128], lhsT=o.kT[hb][:, ts],
                                    rhs=o.qT[hb][:, tb * 128 + 64:(tb + 1) * 128], start=True, stop=True),
                 reads=[U[QT][n], U[KT][n]], writes=[g.bU[bk]])
            P.pe(lambda e: e.matmul(bank(g, bk)[0:64, 0:64], lhsT=o.kT[hb][:, tb * 128:tb * 128 + 64],
                                    rhs=o.qT[hb][:, tb * 128:tb * 128 + 64], start=True, stop=True),
                 reads=[U[QT][n], U[KT][n]], writes=[g.bU[bk]])

        def mask(tb):
            ai = tb % 2
            bk = 5 + 2 * (tb % 2)
            P.dve(lambda e: e.tensor_tensor(out=o.attm[ai][:, 64:128], in0=bank(g, bk)[:, 64:128], in1=c["triI"][:, 64:128], op=ALU.mult),
                  reads=[g.bU[bk], g.cU], writes=[o.attmU[ai]])
            P.dve(lambda e: e.tensor_tensor(out=o.attm[ai][0:64, 0:64], in0=bank(g, bk)[0:64, 0:64], in1=c["triI"][0:64, 0:64], op=ALU.mult),
                  reads=[g.bU[bk], g.cU], writes=[o.attmU[ai]])

        att(0)
        mask(0)
        for tb in range(NB):
            n = tb // 4
            ts = slice(tb * 128, (tb + 1) * 128)
            ai = tb % 2
            si = tb % 2
            P.pe(lambda e: e.matmul(bank(g, 6)[:, 128:256], lhsT=o.kh[hb][:, tb, :], rhs=o.v[hb][:, tb, :], start=True, stop=True),
                 reads=[U[KH][n], U[VV][n]], writes=[g.bU[6]])
            if tb + 1 < NB:
                att(tb + 1)
            P.pe(lambda e: e.matmul(bank(g, 4)[:, 0:128], lhsT=o.attm[ai], rhs=o.v[hb][:, tb, :], start=True, stop=(tb == 0)),
                 reads=[o.attmU[ai], U[VV][n]], writes=[g.bU[4]])
            if tb > 0:
                P.pe(lambda e: e.matmul(bank(g, 4)[:, 0:128], lhsT=o.qT[hb][:, ts], rhs=o.Sb2[si], start=False, stop=True),
                     reads=[o.SbU2[si], U[QT][n]], writes=[g.bU[4]])
            if tb == 0:
                P.dve(lambda e: e.tensor_copy(o.S, bank(g, 6)[:, 128:256]), reads=[g.bU[6]], writes=[o.SU])
            else:
                P.dve(lambda e: e.scalar_tensor_tensor(out=o.S, in0=o.S, scalar=o.eb[hb][:, tb, 1:2], in1=bank(g, 6)[:, 128:256],
                                                       op0=ALU.mult, op1=ALU.add),
                      reads=[o.SU, g.bU[6], U[EB][n]], writes=[o.SU])
            if tb + 1 < NB:
                nn = (tb + 1) // 4
                sn = (tb + 1) % 2
                P.dve(lambda e: e.tensor_scalar(out=o.Sb2[sn], in0=o.S, scalar1=o.eb[hb][:, tb + 1, 0:1], scalar2=None, op0=ALU.mult),
                      reads=[o.SU, U[EB][nn]], writes=[o.SbU2[sn]])
                mask(tb + 1)
            P.act(lambda e: e.copy(o.oall[:, tb, :], bank(g, 4)[:, 0:128]), reads=[g.bU[4]], writes=[o.oallU])
            P.act(lambda e: e.activation(out=o.junk2, in_=bank(g, 4)[:, 0:128], func=AF.Square, accum_out=o.ssall[:, tb:tb + 1]),
                  reads=[g.bU[4]], writes=[o.junk2U, o.ssU])

    def stageC(h):
        hb = h % 2
        U = o.hbU[hb]
        hj = h % 2
        P.act(lambda e: e.activation(out=o.rsall, in_=o.ssall, func=AF.Ln, scale=1.0 / 128, bias=EPS), reads=[o.ssU], writes=[o.ssU])
        P.act(lambda e: e.activation(out=o.rsall, in_=o.rsall, func=AF.Exp, scale=-0.5), reads=[o.ssU], writes=[o.ssU])
        P.dve(lambda e: e.tensor_tensor(out=o.oall, in0=o.oall, in1=o.rsall.unsqueeze(2).to_broadcast([128, NB, 128]), op=ALU.mult),
              reads=[o.oallU, o.ssU], writes=[o.oallU])
        P.dve(lambda e: e.tensor_tensor(out=o.oall, in0=o.oall, in1=o.hnw.unsqueeze(1).to_broadcast([128, NB, 128]), op=ALU.mult),
              reads=[o.oallU, o.hnwU], writes=[o.oallU])
        P.dve(lambda e: e.tensor_tensor(out=o.oall, in0=o.oall, in1=o.gs[hb], op=ALU.mult),
              reads=[o.oallU] + [U[GS][n] for n in range(NT)], writes=[o.oallU])
        for n in range(NT):
            bk = 4 + (n % 4)
            for b in range(4):
                P.pe(lambda e, bk=bk, b=b, n=n: e.transpose(bank(g, bk)[:, b * 128:(b + 1) * 128], o.oall[:, n * 4 + b, :], c["ident"][:]),
                     reads=[o.oallU, g.cU], writes=[g.bU[bk]])
            P.act(lambda e, bk=bk, n=n: e.copy(o.yT[:, hj, n * 512:(n + 1) * 512], bank(g, bk)), reads=[g.bU[bk]], writes=[o.yTU[hj]])

    def outproj(hp):
        for j in range(2):
            src = g.d["odd_w_out"][oi, (hp * 2 + j) * 128:(hp * 2 + j + 1) * 128, :]
            P.dma("pool", lambda e, j=j, src=src: e.dma_start(out=o.wout[:, j, :], in_=src), f"o_wout{j}", writes=[o.woutU[j]])
        cnt = 0
        for m in range(KC):
            for n in range(NT):
                bk = 4 + cnt % 4
                cnt += 1
                for j in range(2):
                    P.pe(lambda e, bk=bk, m=m, n=n, j=j: e.matmul(bank(g, bk), lhsT=o.wout[:, j, m * 128:(m + 1) * 128],
                                                                     rhs=o.yT[:, j, n * 512:(n + 1) * 512], start=(j == 0), stop=(j == 1)),
                         reads=[o.woutU[j], o.yTU[j]], writes=[g.bU[bk]])
                P.dve(lambda e, bk=bk, m=m, n=n: e.tensor_tensor(out=g.hT[:, m, n * 512:(n + 1) * 512], in0=g.hT[:, m, n * 512:(n + 1) * 512],
                                                                   in1=bank(g, bk), op=ALU.add),
                      reads=[g.bU[bk], g.hU[m][n]], writes=[g.hU[m][n]])

    load_w(0)
    load_w(1)
    head_lb(0)
    stageAall(0)
    for h in range(16):
        P.capture()
        stageB(h)
        stageC(h)
        if h % 2 == 1:
            outproj(h // 2)
        LB = P.end_capture()
        P.capture()
        if h + 1 < 16:
            if h + 2 < 16:
                load_w(h + 2)
            head_lb(h + 1)
            stageAall(h + 1)
        LA = P.end_capture()
        P.replay_merged(LA, LB)


def even_ssd(g, li):
    P, L, NB, NT = g.P, g.L, g.NB, g.NT
    ei = li // 2
    c = g.cst
    A = Carver(g)
    s = NS()
    HB = NB * 16
    s.xpre = A.f(515); s.xpreU = P.unit()
    s.cacc = A.f(512); s.caccU = P.unit()
    v3 = lambda ap: ap.rearrange("p (b h) -> p b h", h=16)
    s.dt = A.f(HB); s.atok = A.f(HB); s.acs = A.f(HB); s.eacs = A.f(HB); s.dtd = A.f(HB); s.edl = A.f(HB)
    s.dtU = P.unit()
    s.acsT = A.f(L); s.acsTU = P.unit()
    s.cw = A.f(64); s.cb = A.f(16); s.dtb = A.f(16); s.Abc = A.f(16); s.Dsk = A.f(16); s.snw = A.f(8)
    s.smallU = P.unit()
    s.S = A.f(256); s.SU = P.unit()
    scan_off = A.fo
    s.Rbd = A.f(512); s.RbdU = P.unit()
    D2 = lambda n: ([A.f(n) for _ in range(2)], P.units(2))
    s.acsTb2, s.acsTbU2 = D2(128)
    s.CBm2, s.CBmU2 = D2(128)
    s.Dm2, s.DmU2 = D2(512)
    s.E2, s.EU2 = D2(512)
    s.t12, s.t1U2 = D2(256)
    s.t22, s.t2U2 = D2(256)
    s.ytmp2, s.ytmpU2 = D2(256)
    s.rstd_all = g.arf[:, scan_off:scan_off + L]; s.rstdallU = P.unit()
    assert scan_off + L <= ARF_N
    s.w = A.b(KC * 768).rearrange("p (k c) -> p k c", k=KC); s.wU = P.unit()
    s.wdt = A.b(KC * 16).rearrange("p (k c) -> p k c", k=KC); s.wdtU = P.unit()
    s.BT = A.b(L); s.CT = A.b(L); s.BCU = [P.unit() for _ in range(NT)]
    s.xtok = A.b(NB * 256).rearrange("p (b c) -> p b c", c=256); s.xtokU = [P.unit() for _ in range(NT)]
    s.Btok = A.b(NB * 128).rearrange("p (b c) -> p b c", c=128); s.BtokU = [P.unit() for _ in range(NT)]
    scan_bo = A.bo
    s.zs2 = [A.b(256) for _ in range(2)]; s.zsU2 = P.units(2)
    s.sc2 = [A.b(512).rearrange("p (h l) -> p h l", h=4) for _ in range(2)]; s.scU2 = P.units(2)
    s.Xdt2 = [A.b(256) for _ in range(2)]; s.XdtU2 = P.units(2)
    s.XB2 = [A.b(256) for _ in range(2)]; s.XBU2 = P.units(2)
    s.Sbf = A.b(256); s.SbfU = P.unit()
    s.yTa = A.b(8 * L).rearrange("p (c t) -> p c t", c=8); s.yTaU = [[P.unit() for _ in range(NT)] for _ in range(8)]
    s.wo2 = [g.arb[:, scan_bo + i * 1024:scan_bo + (i + 1) * 1024].rearrange("p (c j) -> p c j", c=8) for i in range(2)]
    s.woU2 = P.units(2)
    assert scan_bo + 2048 <= A.bo
    s.tmp2 = [s.cacc, s.xpre[:, 0:512]]; s.tmpU2 = [s.caccU, s.xpreU]
    ident = c["ident"]

    sm = [s.smallU]
    P.dma("sp", lambda e: e.dma_start(out=s.cw, in_=g.d["conv_w_cols"][ei]), "s_small", writes=sm)
    P.dma("sp", lambda e: e.dma_start(out=s.cb, in_=g.d["conv_b_cols"][ei]), "s_small", writes=sm)
    P.dma("sp", lambda e: e.dma_start(out=s.dtb, in_=g.d["dt_bias"][ei].partition_broadcast(128)), "s_small", writes=sm)
    P.dma("sp", lambda e: e.dma_start(out=s.Abc, in_=g.d["A_log"][ei].partition_broadcast(128)), "s_small", writes=sm)
    P.dma("sp", lambda e: e.dma_start(out=s.Dsk, in_=g.d["D_skip"][ei].partition_broadcast(128)), "s_small", writes=sm)
    P.dma("sp", lambda e: e.dma_start(out=s.snw, in_=g.d["ssd_norm_w_cols"][ei]), "s_small", writes=sm)
    P.act(lambda e: e.activation(out=s.Abc, in_=s.Abc, func=AF.Exp), reads=sm, writes=sm)
    P.dve(lambda e: e.tensor_scalar(out=s.Abc, in0=s.Abc, scalar1=-1.0, scalar2=None, op0=ALU.mult), reads=sm, writes=sm)
    P.dma("pool", lambda e: e.dma_start(out=s.wdt.rearrange("p k c -> p (k c)"), in_=g.d["ev_w_dt"][ei]), "s_wdt", writes=[s.wdtU])

    for b in range(NB):
        for k in range(KC):
            P.pe(lambda e, b=b, k=k: e.matmul(bank(g, 0)[:, b * 16:(b + 1) * 16], lhsT=g.uT[:, k, b * 128:(b + 1) * 128],
                                              rhs=s.wdt[:, k, :], start=(k == 0), stop=(k == KC - 1)),
                 reads=[g.uU[b // 4], s.wdtU], writes=[g.bU[0]])
    bc_h = lambda t: t.unsqueeze(1).to_broadcast([128, NB, 16])
    du = [s.dtU]
    P.dve(lambda e: e.tensor_tensor(out=v3(s.dt), in0=v3(bank(g, 0)[:, 0:HB]), in1=bc_h(s.dtb), op=ALU.add),
          reads=[g.bU[0]] + sm, writes=du)
    P.act(lambda e: e.activation(out=s.dt, in_=s.dt, func=AF.Exp), reads=du, writes=du)
    P.act(lambda e: e.activation(out=s.dt, in_=s.dt, func=AF.Ln, bias=1.0), reads=du, writes=du)
    P.dve(lambda e: e.tensor_tensor(out=v3(s.atok), in0=v3(s.dt), in1=bc_h(s.Abc), op=ALU.mult), reads=du + sm, writes=du)
    P.pe(lambda e: e.matmul(bank(g, 1)[:, 0:HB], lhsT=c["triI"][:], rhs=s.atok, start=True, stop=True), reads=du + [g.cU], writes=[g.bU[1]])
    P.pe(lambda e: e.matmul(bank(g, 2)[:, 0:HB], lhsT=c["ones"][:], rhs=s.atok, start=True, stop=True), reads=du + [g.cU], writes=[g.bU[2]])
    P.dve(lambda e: e.tensor_copy(s.acs, bank(g, 1)[:, 0:HB]), reads=[g.bU[1]], writes=du)
    P.act(lambda e: e.activation(out=s.eacs, in_=s.acs, func=AF.Exp), reads=du, writes=du)
    P.dve(lambda e: e.tensor_copy(s.edl, bank(g, 2)[:, 0:HB]), reads=[g.bU[2]], writes=du)
    P.dve(lambda e: e.tensor_tensor(out=s.dtd, in0=s.edl, in1=s.acs, op=ALU.subtract), reads=du, writes=du)
    P.act(lambda e: e.activation(out=s.dtd, in_=s.dtd, func=AF.Exp), reads=du, writes=du)
    P.dve(lambda e: e.tensor_tensor(out=s.dtd, in0=s.dtd, in1=s.dt, op=ALU.mult), reads=du, writes=du)
    P.act(lambda e: e.activation(out=s.edl, in_=s.edl, func=AF.Exp), reads=du, writes=du)
    for n in range(NT):
        for j in range(4):
            b = n * 4 + j
            P.pe(lambda e, b=b, j=j: e.transpose(bank(g, 3)[0:16, j * 128:(j + 1) * 128], s.acs[:, b * 16:(b + 1) * 16], c["ident"][:]),
                 reads=du + [g.cU], writes=[g.bU[3]])
        P.act(lambda e, n=n: e.copy(s.acsT[0:16, n * 512:(n + 1) * 512], bank(g, 3)[0:16, :]), reads=[g.bU[3]], writes=[s.acsTU])

    pcnt = [0]
    for grp in range(4):
        P.dma("pool", lambda e, grp=grp: e.dma_start(out=s.w.rearrange("p k c -> p (k c)"), in_=g.d["ev_w_ssd"][ei, grp]),
              "s_w", writes=[s.wU])
        chunks = [(256, 2 * grp, "x0"), (384, 2 * grp + 1, "x1"), (512, 8 + grp, "B"), (640, 12 + grp, "C")]
        cp1, cp2 = [], []
        for wc0, cch, kind in chunks:
            for n in range(NT):
                sl = slice(n * 512, (n + 1) * 512)
                bk = 3 + (pcnt[0] % 2)
                pcnt[0] += 1
                P.capture()
                for k in range(KC):
                    P.pe(lambda e, bk=bk, k=k, wc0=wc0, sl=sl: e.matmul(bank(g, bk), lhsT=s.w[:, k, wc0:wc0 + 128], rhs=g.uT[:, k, sl],
                                                                        start=(k == 0), stop=(k == KC - 1)),
                         reads=[g.uU[n], s.wU], writes=[g.bU[bk]])
                cp1.append(P.end_capture())
                P.capture()
                if n == 0:
                    P.dve(lambda e: e.memset(s.xpre[:, 0:3], 0.0), writes=[s.xpreU])
                else:
                    P.dve(lambda e: e.tensor_copy(s.xpre[:, 0:3], s.xpre[:, 512:515]), reads=[s.xpreU], writes=[s.xpreU])
                P.act(lambda e, bk=bk: e.copy(s.xpre[:, 3:515], bank(g, bk)), reads=[g.bU[bk]], writes=[s.xpreU])
                P.dve(lambda e, cch=cch: e.tensor_scalar(out=s.cacc, in0=s.xpre[:, 3:515], scalar1=s.cw[:, cch * 4 + 3:cch * 4 + 4],
                                                          scalar2=s.cb[:, cch:cch + 1], op0=ALU.mult, op1=ALU.add),
                      reads=[s.xpreU] + sm, writes=[s.caccU])
                for tap in (2, 1, 0):
                    P.dve(lambda e, cch=cch, tap=tap: e.scalar_tensor_tensor(
                        out=s.cacc, in0=s.xpre[:, tap:tap + 512], scalar=s.cw[:, cch * 4 + tap:cch * 4 + tap + 1], in1=s.cacc,
                        op0=ALU.mult, op1=ALU.add), reads=[s.xpreU, s.caccU] + sm, writes=[s.caccU])
                P.act(lambda e: e.activation(out=s.cacc, in_=s.cacc, func=AF.Silu), reads=[s.caccU], writes=[s.caccU])
                if kind in ("B", "C"):
                    dst = s.BT if kind == "B" else s.CT
                    P.dve(lambda e, dst=dst, sl=sl: e.tensor_copy(dst[:, sl], s.cacc), reads=[s.caccU], writes=[s.BCU[n]])
                if kind != "C":
                    for j in range(4):
                        P.pe(lambda e, j=j: e.transpose(bank(g, 5)[:, j * 128:(j + 1) * 128], s.cacc[:, j * 128:(j + 1) * 128], ident[:]),
                             reads=[s.caccU, g.cU], writes=[g.bU[5]])
                    src = bank(g, 5).rearrange("p (b c) -> p b c", b=4)
                    if kind == "B":
                        P.act(lambda e, n=n, src=src: e.copy(s.Btok[:, n * 4:(n + 1) * 4, :], src), reads=[g.bU[5]], writes=[s.BtokU[n]])
                    else:
                        co = 0 if kind == "x0" else 128
                        P.act(lambda e, n=n, src=src, co=co: e.copy(s.xtok[:, n * 4:(n + 1) * 4, co:co + 128], src),
                              reads=[g.bU[5]], writes=[s.xtokU[n]])
                cp2.append(P.end_capture())
        P.replay_merged(cp1[0], [])
        for i in range(len(cp2)):
            P.replay_merged(cp1[i + 1] if i + 1 < len(cp1) else [], [])
            P.replay_merged(cp2[i], [])
        hs4 = slice(4 * grp, 4 * grp + 4)
        fronts, backs = [], []
        for b in range(NB):
            P.capture()
            n = b // 4
            blk = slice(b * 128, (b + 1) * 128)
            hcol = lambda t, b=b: v3(t)[:, b, hs4]
            bch = lambda t, w, b=b: hcol(t, b).unsqueeze(2).to_broadcast([128, 4, w])
            x4 = s.xtok[:, b, :].rearrange("p (h q) -> p h q", h=4)
            pb_ = b % 2
            s.acsTb, s.acsTbU = s.acsTb2[pb_], s.acsTbU2[pb_]
            s.CBm, s.CBmU = s.CBm2[pb_], s.CBmU2[pb_]
            s.Dm, s.DmU = s.Dm2[pb_], s.DmU2[pb_]
            s.E, s.EU = s.E2[pb_], s.EU2[pb_]
            s.t1, s.t1U = s.t12[pb_], s.t1U2[pb_]
            s.t2, s.t2U = s.t22[pb_], s.t2U2[pb_]
            s.ytmp, s.ytmpU = s.ytmp2[pb_], s.ytmpU2[pb_]
            s.zs, s.zsU = s.zs2[pb_], s.zsU2[pb_]
            s.sc, s.scU = s.sc2[pb_], s.scU2[pb_]
            s.Xdt, s.XdtU = s.Xdt2[pb_], s.XdtU2[pb_]
            s.XB, s.XBU = s.XB2[pb_], s.XBU2[pb_]
            for k in range(KC):
                P.pe(lambda e, k=k, blk=blk: e.matmul(bank(g, 6)[:, 0:256], lhsT=g.uT[:, k, blk], rhs=s.w[:, k, 0:256],
                                                      start=(k == 0), stop=(k == KC - 1)),
                     reads=[g.uU[n], s.wU], writes=[g.bU[6]])
            P.act(lambda e: e.activation(out=s.zs, in_=bank(g, 6)[:, 0:256], func=AF.Silu), reads=[g.bU[6]], writes=[s.zsU])
            P.pe(lambda e, blk=blk: e.matmul(bank(g, 7)[:, 0:128], lhsT=s.BT[:, blk], rhs=s.CT[:, blk], start=True, stop=True),
                 reads=[s.BCU[n]], writes=[g.bU[7]])
            P.dve(lambda e: e.tensor_tensor(out=s.CBm, in0=bank(g, 7)[:, 0:128], in1=c["triI"][:], op=ALU.mult),
                  reads=[g.bU[7], g.cU], writes=[s.CBmU])
            P.dve(lambda e, blk=blk: e.tensor_tensor(out=s.Rbd[0:16, :].rearrange("p (h l) -> p h l", h=4),
                                                     in0=s.acsT[0:16, blk].unsqueeze(1).to_broadcast([16, 4, 128]),
                                                     in1=c["mg16"][:, hs4].unsqueeze(2).to_broadcast([16, 4, 128]), op=ALU.mult),
                  reads=[s.acsTU, g.cU], writes=[s.RbdU])
            P.pe(lambda e: e.matmul(bank(g, 1), lhsT=c["ones"][0:16, :], rhs=s.Rbd[0:16, :], start=True, stop=True),
                 reads=[s.RbdU, g.cU], writes=[g.bU[1]])
            P.dve(lambda e, b=b: e.tensor_tensor(out=s.Dm.rearrange("p (h l) -> p h l", h=4),
                                                 in0=bank(g, 1).rearrange("p (h l) -> p h l", h=4),
                                                 in1=bch(s.acs, 128, b), op=ALU.subtract),
                  reads=[g.bU[1]] + du, writes=[s.DmU])
            P.dve(lambda e: e.tensor_scalar(out=s.Dm, in0=s.Dm, scalar1=0.0, scalar2=None, op0=ALU.min), reads=[s.DmU], writes=[s.DmU])
            P.act(lambda e: e.activation(out=s.E, in_=s.Dm, func=AF.Exp), reads=[s.DmU], writes=[s.EU])
            P.dve(lambda e: e.tensor_tensor(out=s.sc, in0=s.E.rearrange("p (h l) -> p h l", h=4),
                                            in1=s.CBm.unsqueeze(1).to_broadcast([128, 4, 128]), op=ALU.mult),
                  reads=[s.EU, s.CBmU], writes=[s.scU])
            P.dve(lambda e, b=b: e.tensor_tensor(out=s.Xdt.rearrange("p (h q) -> p h q", h=4), in0=x4, in1=bch(s.dt, 64, b), op=ALU.mult),
                  reads=[s.xtokU[n]] + du, writes=[s.XdtU])
            P.dve(lambda e, b=b: e.tensor_tensor(out=s.XB.rearrange("p (h q) -> p h q", h=4), in0=x4, in1=bch(s.dtd, 64, b), op=ALU.mult),
                  reads=[s.xtokU[n]] + du, writes=[s.XBU])
            fronts.append(P.end_capture())
            P.capture()
            for h4 in range(4):
                P.pe(lambda e, h4=h4: e.matmul(bank(g, 2)[:, h4 * 64:(h4 + 1) * 64], lhsT=s.sc[:, h4, :], rhs=s.Xdt[:, h4 * 64:(h4 + 1) * 64],
                                               start=True, stop=True), reads=[s.scU, s.XdtU], writes=[g.bU[2]])
            if b > 0:
                P.pe(lambda e, blk=blk: e.matmul(bank(g, 3)[:, 0:256], lhsT=s.CT[:, blk], rhs=s.Sbf, start=True, stop=True),
                     reads=[s.BCU[n], s.SbfU], writes=[g.bU[3]])
            P.pe(lambda e, b=b: e.matmul(bank(g, 4)[:, 0:256], lhsT=s.Btok[:, b, :], rhs=s.XB, start=True, stop=True),
                 reads=[s.BtokU[n], s.XBU], writes=[g.bU[4]])
            if b > 0:
                P.dve(lambda e, b=b: e.tensor_tensor(out=s.t1.rearrange("p (h q) -> p h q", h=4),
                                                     in0=bank(g, 3)[:, 0:256].rearrange("p (h q) -> p h q", h=4),
                                                     in1=bch(s.eacs, 64, b), op=ALU.mult), reads=[g.bU[3]] + du, writes=[s.t1U])
                P.dve(lambda e: e.tensor_tensor(out=s.t2, in0=bank(g, 2)[:, 0:256], in1=s.t1, op=ALU.add), reads=[g.bU[2], s.t1U], writes=[s.t2U])
            else:
                P.dve(lambda e: e.tensor_copy(s.t2, bank(g, 2)[:, 0:256]), reads=[g.bU[2]], writes=[s.t2U])
            P.dve(lambda e: e.tensor_tensor(out=s.t1.rearrange("p (h q) -> p h q", h=4), in0=x4,
                                            in1=s.Dsk[:, hs4].unsqueeze(2).to_broadcast([128, 4, 64]), op=ALU.mult),
                  reads=[s.xtokU[n]] + sm, writes=[s.t1U])
            P.dve(lambda e: e.tensor_tensor(out=s.t2, in0=s.t2, in1=s.t1, op=ALU.add), reads=[s.t1U, s.t2U], writes=[s.t2U])
            P.dve(lambda e: e.tensor_tensor(out=s.ytmp, in0=s.t2, in1=s.zs, op=ALU.mult), reads=[s.t2U, s.zsU], writes=[s.ytmpU])
            for j in range(2):
                P.pe(lambda e, j=j: e.transpose(bank(g, 5)[:, j * 128:(j + 1) * 128], s.ytmp[:, j * 128:(j + 1) * 128], ident[:]),
                     reads=[s.ytmpU, g.cU], writes=[g.bU[5]])
            P.act(lambda e, blk=blk: e.copy(s.yTa[:, 2 * grp:2 * grp + 2, blk], bank(g, 5)[:, 0:256].rearrange("p (j t) -> p j t", j=2)),
                  reads=[g.bU[5]], writes=[s.yTaU[2 * grp][n], s.yTaU[2 * grp + 1][n]])
            if b == 0:
                P.dve(lambda e: e.tensor_copy(s.S, bank(g, 4)[:, 0:256]), reads=[g.bU[4]], writes=[s.SU])
            else:
                P.dve(lambda e, b=b: e.tensor_tensor(out=s.S.rearrange("p (h q) -> p h q", h=4), in0=s.S.rearrange("p (h q) -> p h q", h=4),
                                                     in1=bch(s.edl, 64, b), op=ALU.mult), reads=[s.SU] + du, writes=[s.SU])
                P.dve(lambda e: e.tensor_tensor(out=s.S, in0=s.S, in1=bank(g, 4)[:, 0:256], op=ALU.add), reads=[s.SU, g.bU[4]], writes=[s.SU])
            if b + 1 < NB:
                P.act(lambda e: e.copy(s.Sbf, s.S), reads=[s.SU], writes=[s.SbfU])
            backs.append(P.end_capture())
        P.replay_merged(fronts[0], [])
        for b in range(NB):
            P.replay_merged(fronts[b + 1] if b + 1 < NB else [], backs[b])

    P.barrier()
    for n in range(NT):
        sl = slice(n * 512, (n + 1) * 512)
        rms_rstd_tile(g, lambda k, sl=sl: s.yTa[:, k, sl], lambda k, n=n: [s.yTaU[k][n]], 8, 1024)
        P.dve(lambda e, sl=sl: e.tensor_copy(s.rstd_all[:, sl], g.rstd_t[:]), reads=[g.rstdU], writes=[s.rstdallU])
    cnt = 0

    def load_wo(m):
        wi = m % 2
        P.dma("pool", lambda e: e.dma_start(out=s.wo2[wi].rearrange("p c j -> p (c j)"), in_=g.d["ev_w_out_t"][ei, m, :, 0:1024]),
              f"s_wo{wi}", writes=[s.woU2[wi]])
        P.dve(lambda e: e.tensor_tensor(out=s.wo2[wi], in0=s.wo2[wi], in1=s.snw[:, 0:8].unsqueeze(2).to_broadcast([128, 8, 128]), op=ALU.mult),
              reads=[s.woU2[wi]] + sm, writes=[s.woU2[wi]])

    load_wo(0)
    for m in range(KC):
        if m + 1 < KC:
            load_wo(m + 1)
        wi = m % 2
        for n in range(NT):
            sl = slice(n * 512, (n + 1) * 512)
            bk = cnt % 2
            ti = cnt % 2
            cnt += 1
            for k in range(8):
                P.pe(lambda e, bk=bk, k=k, sl=sl: e.matmul(bank(g, bk), lhsT=s.wo2[wi][:, k, :], rhs=s.yTa[:, k, sl], start=(k == 0), stop=(k == 7)),
                     reads=[s.woU2[wi], s.yTaU[k][n]], writes=[g.bU[bk]])
            P.dve(lambda e, bk=bk, sl=sl, ti=ti: e.tensor_tensor(out=s.tmp2[ti], in0=bank(g, bk), in1=s.rstd_all[:, sl], op=ALU.mult),
                  reads=[g.bU[bk], s.rstdallU], writes=[s.tmpU2[ti]])
            P.pool(lambda e, m=m, sl=sl, ti=ti: e.tensor_tensor(out=g.hT[:, m, sl], in0=g.hT[:, m, sl], in1=s.tmp2[ti], op=ALU.add),
                   reads=[s.tmpU2[ti], g.hU[m][n]], writes=[g.hU[m][n]])


def even_attn(g, li):
    P, L, NB, NT = g.P, g.L, g.NB, g.NT
    ei = li // 2
    lam_init = 0.8 - 0.6 * math.exp(-0.3 * li)
    c = g.cst
    A = Carver(g)
    a = NS()
    a.corr = A.f(2048).rearrange("p (h d q) -> p h d q", h=8, d=2); a.corrU = P.unit()
    a.b31 = A.f(8); a.nb31 = A.f(8); a.bU_ = P.unit()
    a.lq = [A.f(64) for _ in range(4)]; a.lamU = P.unit()
    a.lam = A.f(8)
    a.slnw = A.f(128); a.slnwU = P.unit()
    a.w = [A.b(KC * 512).rearrange("p (k c) -> p k c", k=KC) for _ in range(2)]; a.wU = P.units(2)
    a.qT = [A.b(L) for _ in range(2)]; a.qTU = [P.unit() for _ in range(NT)]
    a.kT = A.b(L); a.kTU = [P.unit() for _ in range(NT)]
    a.v = A.b(NB * 132).rearrange("p (b c) -> p b c", c=132); a.vU = P.unit()
    a.gs = A.b(NB * 128).rearrange("p (b c) -> p b c", c=128); a.gsU = P.unit()
    a.PT = [[A.b(512) for _ in range(2)] for _ in range(2)]; a.PTU = [P.units(2), P.units(2)]
    a.yT = A.b(4 * L).rearrange("p (j t) -> p j t", j=4); a.yTU = P.units(4)
    a.wo = [A.b(512).rearrange("p (j c) -> p j c", j=4) for _ in range(2)]; a.woU = P.units(2)
    ident = c["ident"]

    P.dma("sp", lambda e: e.dma_start(out=a.corr.rearrange("p h d q -> p (h d q)"), in_=g.d["rel_biasD"]), "a_corr", writes=[a.corrU])
    P.dma("sp", lambda e: e.dma_start(out=a.b31, in_=g.d["rel_b31"].partition_broadcast(128)), "a_b31", writes=[a.bU_])
    P.dve(lambda e: e.tensor_scalar(out=a.nb31, in0=a.b31, scalar1=-1.0, scalar2=None, op0=ALU.mult), reads=[a.bU_], writes=[a.bU_])
    for h in range(8):
        P.act(lambda e, h=h: e.activation(out=a.corr[:, h], in_=a.corr[:, h], func=AF.Exp, bias=a.nb31[:, h:h + 1]),
              reads=[a.corrU, a.bU_], writes=[a.corrU])
    for i, nm in enumerate(("lambda_q1", "lambda_k1", "lambda_q2", "lambda_k2")):
        P.dma("sp", lambda e, i=i, nm=nm: e.dma_start(out=a.lq[i], in_=g.d[nm][ei].partition_broadcast(128)), "a_lam", writes=[a.lamU])
    lu = [a.lamU]
    P.dve(lambda e: e.tensor_tensor(out=a.lq[0], in0=a.lq[0], in1=a.lq[1], op=ALU.mult), reads=lu, writes=lu)
    P.dve(lambda e: e.tensor_tensor(out=a.lq[2], in0=a.lq[2], in1=a.lq[3], op=ALU.mult), reads=lu, writes=lu)
    P.dve(lambda e: e.tensor_reduce(out=a.lam[:, 0:1], in_=a.lq[0], axis=AX.X, op=ALU.add), reads=lu, writes=lu)
    P.dve(lambda e: e.tensor_reduce(out=a.lam[:, 1:2], in_=a.lq[2], axis=AX.X, op=ALU.add), reads=lu, writes=lu)
    P.act(lambda e: e.activation(out=a.lam[:, 2:4], in_=a.lam[:, 0:2], func=AF.Exp), reads=lu, writes=lu)
    P.dve(lambda e: e.tensor_tensor(out=a.lam[:, 4:5], in0=a.lam[:, 3:4], in1=a.lam[:, 2:3], op=ALU.subtract), reads=lu, writes=lu)
    P.dve(lambda e: e.tensor_scalar(out=a.lam[:, 5:6], in0=a.lam[:, 4:5], scalar1=-lam_init, scalar2=None, op0=ALU.add), reads=lu, writes=lu)
    P.dma("sp", lambda e: e.dma_start(out=a.slnw, in_=g.d["subln_w"][ei].partition_broadcast(128)), "a_slnw", writes=[a.slnwU])
    P.dve(lambda e: e.tensor_scalar(out=a.slnw, in0=a.slnw, scalar1=1.0 - lam_init, scalar2=None, op0=ALU.mult),
          reads=[a.slnwU], writes=[a.slnwU])
    P.dve(lambda e: e.memset(a.v, 1.0), writes=[a.vU])

    def load_w(h):
        i = h % 2
        P.dma("pool", lambda e: e.dma_start(out=a.w[i].rearrange("p k c -> p (k c)"), in_=g.d["ev_w_att"][ei, h]),
              f"a_w{i}", writes=[a.wU[i]])

    pc = [0]

    def project(h):
        wi = h % 2
        w = a.w[wi]
        for n in range(NT):
            sl = slice(n * 512, (n + 1) * 512)
            for which in range(2):
                bk = pc[0] % 2
                pc[0] += 1
                for k in range(KC):
                    P.pe(lambda e, bk=bk, k=k, sl=sl, which=which: e.matmul(bank(g, bk), lhsT=w[:, k, which * 128:(which + 1) * 128],
                                                                            rhs=g.uT[:, k, sl], start=(k == 0), stop=(k == KC - 1)),
                         reads=[g.uU[n], a.wU[wi]], writes=[g.bU[bk]])
                if which == 0:
                    for cc in range(2):
                        P.dve(lambda e, bk=bk, sl=sl, cc=cc: e.tensor_scalar(out=a.qT[cc][:, sl], in0=bank(g, bk), scalar1=c["maskq"][:, cc:cc + 1],
                                                                             scalar2=None, op0=ALU.mult),
                              reads=[g.bU[bk], g.cU], writes=[a.qTU[n]])
                else:
                    P.act(lambda e, bk=bk, sl=sl: e.copy(a.kT[:, sl], bank(g, bk)), reads=[g.bU[bk]], writes=[a.kTU[n]])
        for b in range(NB):
            bk = 2 + (b % 2)
            for k in range(KC):
                P.pe(lambda e, bk=bk, k=k, b=b: e.matmul(bank(g, bk)[:, 0:256], lhsT=g.uT[:, k, b * 128:(b + 1) * 128], rhs=w[:, k, 256:512],
                                                         start=(k == 0), stop=(k == KC - 1)),
                     reads=[g.uU[b // 4], a.wU[wi]], writes=[g.bU[bk]])
            P.act(lambda e, bk=bk, b=b: e.copy(a.v[:, b, 0:128], bank(g, bk)[:, 0:128]), reads=[g.bU[bk]], writes=[a.vU])
            P.act(lambda e, bk=bk, b=b: e.activation(out=a.gs[:, b, :], in_=bank(g, bk)[:, 128:256], func=AF.Silu),
                  reads=[g.bU[bk]], writes=[a.gsU])

    gc = [0]
    a.r2 = [A.f(8) for _ in range(2)]; a.rU2 = P.units(2)
    a.t2 = [A.f(128) for _ in range(2)]; a.tU2 = P.units(2)
    a.o2 = [A.f(128) for _ in range(2)]; a.oU2 = P.units(2)
    a.sqo2 = [A.f(128) for _ in range(2)]; a.sqoU2 = P.units(2)
    a.y2 = [A.f(128) for _ in range(2)]; a.yU2 = P.units(2)

    def attend(h):
        hj = h % 4
        groups = []
        for qb in range(NB):
            for gi in range(qb // 4 + 1):
                kbs = [kb for kb in range(gi * 4, gi * 4 + 4) if kb <= qb]
                groups.append((qb, gi, kbs, gc[0] % 2))
                gc[0] += 1

        def accb(qb, cc):
            return (2 + cc) if qb % 2 == 0 else cc

        def S_(grp):
            qb, gi, kbs, buf = grp
            qs = slice(qb * 128, (qb + 1) * 128)
            for cc in range(2):
                bk = 4 + 2 * cc + buf
                for j, kb in enumerate(kbs):
                    P.pe(lambda e, bk=bk, j=j, kb=kb, cc=cc: e.matmul(bank(g, bk)[:, j * 128:(j + 1) * 128],
                                                                      lhsT=a.kT[:, kb * 128:(kb + 1) * 128], rhs=a.qT[cc][:, qs],
                                                                      start=True, stop=True),
                         reads=[a.kTU[kb // 4], a.qTU[qb // 4]], writes=[g.bU[bk]])

        def E_(grp):
            qb, gi, kbs, buf = grp
            nv = len(kbs)
            for cc in range(2):
                bk = 4 + 2 * cc + buf
                pt = a.PT[cc][buf]
                ptu = a.PTU[cc][buf]
                P.act(lambda e, bk=bk, pt=pt, nv=nv: e.activation(out=pt[:, 0:nv * 128], in_=bank(g, bk)[:, 0:nv * 128], func=AF.Exp,
                                                                  scale=0.125, bias=a.b31[:, h:h + 1]),
                      reads=[g.bU[bk], a.bU_], writes=[ptu])
                for j, kb in enumerate(kbs):
                    Dd = qb - kb
                    if Dd <= 1:
                        P.dve(lambda e, pt=pt, j=j, Dd=Dd: e.tensor_tensor(out=pt[:, j * 128:(j + 1) * 128], in0=pt[:, j * 128:(j + 1) * 128],
                                                                           in1=a.corr[:, h, Dd, :], op=ALU.mult),
                              reads=[ptu, a.corrU], writes=[ptu])

        def PV_(grp):
            qb, gi, kbs, buf = grp
            for cc in range(2):
                pt = a.PT[cc][buf]
                ptu = a.PTU[cc][buf]
                ab = accb(qb, cc)
                for j, kb in enumerate(kbs):
                    P.pe(lambda e, ab=ab, pt=pt, j=j, kb=kb: e.matmul(bank(g, ab)[:, 0:129], lhsT=pt[:, j * 128:(j + 1) * 128],
                                                                      rhs=a.v[:, kb, 0:129], start=(kb == 0), stop=(kb == qb)),
                         reads=[ptu, a.vU], writes=[g.bU[ab]])

        def FIN_(qb):
            qs = slice(qb * 128, (qb + 1) * 128)
            pq = qb % 2
            b0, b1 = accb(qb, 0), accb(qb, 1)
            r, t_, o_, sqo, y_ = a.r2[pq], a.t2[pq], a.o2[pq], a.sqo2[pq], a.y2[pq]
            ru = [a.rU2[pq]]
            tU, oU, sqoU, yU = a.tU2[pq], a.oU2[pq], a.sqoU2[pq], a.yU2[pq]
            P.dve(lambda e: e.reciprocal(r[:, 0:1], bank(g, b0)[:, 128:129]), reads=[g.bU[b0]], writes=ru)
            P.dve(lambda e: e.reciprocal(r[:, 1:2], bank(g, b1)[:, 128:129]), reads=[g.bU[b1]], writes=ru)
            P.dve(lambda e: e.tensor_tensor(out=r[:, 2:3], in0=r[:, 1:2], in1=a.lam[:, 5:6], op=ALU.mult), reads=ru + lu, writes=ru)
            P.dve(lambda e: e.tensor_scalar(out=t_, in0=bank(g, b0)[:, 0:128], scalar1=r[:, 0:1], scalar2=None, op0=ALU.mult),
                  reads=[g.bU[b0]] + ru, writes=[tU])
            P.dve(lambda e: e.scalar_tensor_tensor(out=o_, in0=bank(g, b1)[:, 0:128], scalar=r[:, 2:3], in1=t_, op0=ALU.mult, op1=ALU.add),
                  reads=[g.bU[b1], tU] + ru, writes=[oU])
            P.dve(lambda e: e.tensor_tensor(out=sqo, in0=o_, in1=o_, op=ALU.mult), reads=[oU], writes=[sqoU])
            P.dve(lambda e: e.tensor_reduce(out=r[:, 3:4], in_=sqo, axis=AX.X, op=ALU.add), reads=[sqoU], writes=ru)
            P.act(lambda e: e.activation(out=r[:, 4:5], in_=r[:, 3:4], func=AF.Ln, scale=1.0 / 128, bias=EPS), reads=ru, writes=ru)
            P.act(lambda e: e.activation(out=r[:, 5:6], in_=r[:, 4:5], func=AF.Exp, scale=-0.5), reads=ru, writes=ru)
            P.dve(lambda e: e.scalar_tensor_tensor(out=y_, in0=o_, scalar=r[:, 5:6], in1=a.slnw, op0=ALU.mult, op1=ALU.mult),
                  reads=[oU, a.slnwU] + ru, writes=[yU])
            P.dve(lambda e: e.tensor_tensor(out=y_, in0=y_, in1=a.gs[:, qb, :], op=ALU.mult), reads=[yU, a.gsU], writes=[yU])
            P.pe(lambda e: e.transpose(bank(g, b0)[:, 256:384], y_, ident[:]), reads=[yU, g.cU], writes=[g.bU[b0]])
            P.act(lambda e: e.copy(a.yT[:, hj, qs], bank(g, b0)[:, 256:384]), reads=[g.bU[b0]], writes=[a.yTU[hj]])

        M = len(groups)
        S_(groups[0])
        pending_fin = None
        for i in range(M):
            if i + 1 < M:
                S_(groups[i + 1])
            E_(groups[i])
            PV_(groups[i])
            if pending_fin is not None:
                FIN_(pending_fin)
                pending_fin = None
            qb, gi, kbs, buf = groups[i]
            if kbs[-1] == qb:
                pending_fin = qb
        if pending_fin is not None:
            FIN_(pending_fin)

    oc = [0]

    def outproj(hg):
        for m in range(KC):
            wi = oc[0] % 2
            oc[0] += 1
            c0 = (8 + hg * 4) * 128
            P.dma("pool", lambda e, m=m, wi=wi, c0=c0: e.dma_start(out=a.wo[wi].rearrange("p j c -> p (j c)"),
                                                                  in_=g.d["ev_w_out_t"][ei, m, :, c0:c0 + 512]),
                  f"a_wo{wi}", writes=[a.woU[wi]])
            for n in range(NT):
                sl = slice(n * 512, (n + 1) * 512)
                bk = n % 2
                for j in range(4):
                    P.pe(lambda e, bk=bk, j=j, sl=sl, wi=wi: e.matmul(bank(g, bk), lhsT=a.wo[wi][:, j, :], rhs=a.yT[:, j, sl],
                                                                      start=(j == 0), stop=(j == 3)),
                         reads=[a.woU[wi], a.yTU[j]], writes=[g.bU[bk]])
                P.dve(lambda e, bk=bk, m=m, sl=sl: e.tensor_tensor(out=g.hT[:, m, sl], in0=g.hT[:, m, sl], in1=bank(g, bk), op=ALU.add),
                      reads=[g.bU[bk], g.hU[m][n]], writes=[g.hU[m][n]])

    load_w(0)
    for h in range(8):
        if h + 1 < 8:
            load_w(h + 1)
        project(h)
        attend(h)
        if h % 4 == 3:
            outproj(h // 4)


_CACHE = {}


def kernel(**inputs):
    x = np.ascontiguousarray(np.asarray(inputs["x"], dtype=np.float32))
    Bsz, L, _ = x.shape
    n_cores = 8
    nseq = Bsz // n_cores
    key = (L, nseq)
    if key not in _CACHE:
        _CACHE[key] = build(L, nseq, (0, 1, 2, 3))
    nc, _ = _CACHE[key]
    common = host_layout(inputs)
    in_maps = []
    for cidx in range(n_cores):
        m = dict(common)
        m["x"] = x[cidx * nseq:(cidx + 1) * nseq]
        in_maps.append(m)
    res = run_bass_kernel_spmd(nc, in_maps, core_ids=list(range(n_cores)))
    out = np.concatenate([np.asarray(r["out"]) for r in res.results], axis=0)
    return out.astype(np.float32)
```

```python
import math, contextlib
import numpy as np
import concourse.bass as bass
import concourse.mybir as mybir
from concourse.bass_utils import run_bass_kernel_spmd
from concourse.alu_op_type import AluOpType as ALU

F32 = mybir.dt.float32
BF16 = mybir.dt.bfloat16
AF = mybir.ActivationFunctionType
AX = mybir.AxisListType

D = 1024
KC = 8
EPS = 1e-6
DEPTH = 4
HG_W = 2048
ARF_N = 11392
ARB_N = 35840


class Unit:
    __slots__ = ("name", "lw", "rd")

    def __init__(self, name):
        self.name = name
        self.lw = None
        self.rd = []


class _Rec:
    def __init__(self):
        self.call = None

    def __getattr__(self, name):
        def f(*args, **kw):
            assert self.call is None
            self.call = (name, args, kw)
            return None
        return f


class Prog:
    ENGS = ("pe", "act", "dve", "pool", "sp")

    def __init__(self, nc):
        self.nc = nc
        self.ops = []
        self.nunits = 0
        self.last_eng = {}
        self.last_key = {}

    def unit(self, name=None):
        self.nunits += 1
        return Unit(name or f"u{self.nunits}")

    def units(self, n, name="u"):
        return [self.unit(f"{name}{i}") for i in range(n)]

    def capture(self):
        self._cap = []
        return self._cap

    def end_capture(self):
        c, self._cap = self._cap, None
        return c

    def replay_merged(self, A, B):
        na, nb = len(A), len(B)
        ia = ib = 0
        while ia < na or ib < nb:
            if ib >= nb or (ia < na and ia * nb <= ib * na):
                self.op(*A[ia]); ia += 1
            else:
                self.op(*B[ib]); ib += 1

    def op(self, eng, fn, reads=(), writes=(), dma_key=None, extra_deps=()):
        if fn is not None and not isinstance(fn, tuple):
            rec = _Rec()
            fn(rec)
            assert rec.call is not None
            fn = rec.call
        if getattr(self, "_cap", None) is not None:
            self._cap.append((eng, fn, tuple(reads), tuple(writes), dma_key, tuple(extra_deps)))
            return None
        idx = len(self.ops)
        deps = set(extra_deps)
        for u in reads:
            if u.lw is not None:
                deps.add(u.lw)
        for u in writes:
            if u.lw is not None:
                deps.add(u.lw)
            deps.update(u.rd)
        for u in reads:
            u.rd.append(idx)
        for u in writes:
            u.lw = idx
            u.rd = []
        deps.discard(idx)
        self.ops.append(dict(eng=eng, fn=fn, deps=deps, dma_key=dma_key))
        if fn is not None:
            if dma_key is None:
                self.last_eng[eng] = idx
            else:
                self.last_key[dma_key] = idx
        return idx

    def pe(self, fn, reads=(), writes=()):
        return self.op("pe", fn, reads, writes)

    def act(self, fn, reads=(), writes=()):
        return self.op("act", fn, reads, writes)

    def dve(self, fn, reads=(), writes=()):
        return self.op("dve", fn, reads, writes)

    def pool(self, fn, reads=(), writes=()):
        return self.op("pool", fn, reads, writes)

    def dma(self, eng, fn, key, reads=(), writes=()):
        return self.op(eng, fn, reads, writes, dma_key=key)

    def barrier(self):
        deps = set(self.last_eng.values()) | set(self.last_key.values())
        for e in self.ENGS:
            self.op(e, None, extra_deps=deps)

    def emit(self, final_wait_ops=()):
        nc = self.nc
        ops = self.ops
        n = len(ops)

        def skip(od, o):
            return (od["eng"] == "pe" and o["eng"] == "pe" and od["dma_key"] is None
                    and o["dma_key"] is None and o["fn"] is not None)

        needed = [False] * n
        for i, o in enumerate(ops):
            for d in o["deps"]:
                if skip(ops[d], o):
                    continue
                needed[d] = True
        for d in final_wait_ops:
            needed[d] = True
        chan_count = {}
        ev = [None] * n
        for i, o in enumerate(ops):
            if o["fn"] is None:
                continue
            if o["dma_key"] is not None:
                ch = ("dma", o["dma_key"])
                chan_count[ch] = chan_count.get(ch, 0) + 16
                ev[i] = (ch, chan_count[ch])
            elif needed[i]:
                ch = ("eng", o["eng"])
                chan_count[ch] = chan_count.get(ch, 0) + 1
                ev[i] = (ch, chan_count[ch])
        chans = sorted(chan_count.keys(), key=str)
        self.n_sems = len(chans)
        sems = {}
        stack = contextlib.ExitStack()
        for ci, ch in enumerate(chans):
            sems[ch] = stack.enter_context(nc.semaphore(f"s{ci}"))
        known = {e: {} for e in self.ENGS}
        clock = [None] * n
        streams = {e: [] for e in self.ENGS}
        for i, o in enumerate(ops):
            e = o["eng"]
            kn = known[e]
            wd = {}
            for d in sorted(o["deps"]):
                od = ops[d]
                if skip(od, o):
                    continue
                ch, v = ev[d]
                if kn.get(ch, 0) >= v:
                    continue
                for c2, v2 in clock[d].items():
                    if kn.get(c2, 0) < v2:
                        kn[c2] = v2
                wd[ch] = max(wd.get(ch, 0), v)
            ck = dict(kn)
            if ev[i] is not None:
                ch, v = ev[i]
                ck[ch] = v
            clock[i] = ck
            streams[e].append((list(wd.items()), o["fn"], ev[i]))
        final = [ev[d] for d in final_wait_ops]
        for ch, tot in chan_count.items():
            if ch[0] == "dma":
                final.append((ch, tot))
        self.sems, self.streams, self.final, self._stack = sems, streams, final, stack

    def run_block(self):
        nc = self.nc
        sems, streams, final = self.sems, self.streams, self.final
        with nc.Block() as block:
            def mk(ename):
                def body(eng):
                    for waits, fn, e in streams[ename]:
                        for ch, v in waits:
                            eng.wait_ge(sems[ch], v)
                        if fn is None:
                            continue
                        ins = getattr(eng, fn[0])(*fn[1], **fn[2])
                        if e is not None:
                            ins.then_inc(sems[e[0]], 16 if e[0][0] == "dma" else 1)
                    if ename == "sp":
                        for ch, v in final:
                            eng.wait_ge(sems[ch], v)
                return body
            block.tensor(mk("pe"))
            block.scalar(mk("act"))
            block.vector(mk("dve"))
            block.gpsimd(mk("pool"))
            block.sync(mk("sp"))
        self._stack.close()


def _t5_bucket(rel):
    n = np.maximum(rel, 0)
    max_exact = 16
    large = max_exact + (np.log(np.maximum(n, 1).astype(np.float32) / max_exact)
                         / math.log(128 / max_exact) * (32 - max_exact)).astype(np.int32)
    large = np.minimum(large, 31)
    return np.where(n < max_exact, n, large)


def host_consts():
    s = np.arange(128)[:, None]
    t = np.arange(128)[None, :]
    c = {}
    c["ident"] = np.eye(128, dtype=np.float32)
    c["ones"] = np.ones((128, 128), np.float32)
    c["triC"] = ((s <= t).astype(np.float32) - (s <= 63).astype(np.float32))
    c["triU"] = (s > t).astype(np.float32)
    c["triI"] = (s <= t).astype(np.float32)
    sel = np.zeros((128, 2), np.float32)
    sel[:64, 0] = 1.0
    sel[:, 1] = 1.0
    c["sel"] = sel
    c["mg16"] = np.eye(16, dtype=np.float32)
    mq = np.zeros((128, 2), np.float32)
    mq[:64, 0] = 1.0
    mq[64:, 1] = 1.0
    c["maskq"] = mq
    return c


def host_layout(inp):
    f = lambda a: np.ascontiguousarray(np.asarray(a, dtype=np.float32))
    m = dict(host_consts())
    m["final_norm_w"] = f(inp["final_norm_w"])
    m["norm_w_cols"] = f(np.asarray(inp["norm_w"]).reshape(4, 8, 128).transpose(0, 2, 1))
    owin = np.asarray(inp["odd_w_in"])
    t = owin.reshape(2, 8, 128, 4, 16, 128).transpose(0, 4, 2, 1, 3, 5)
    m["odd_w_in_t"] = f(t).reshape(2, 16, 128, 8 * 512)
    m["odd_w_out"] = f(inp["odd_w_out"])
    m["hgrn_lower_bounds"] = f(inp["hgrn_lower_bounds"])
    m["hgrn_norm_w"] = f(inp["hgrn_norm_w"])
    ew = np.asarray(inp["even_w_in"]).reshape(2, 8, 128, 7184)
    z = ew[..., 0:1024]; xs = ew[..., 1024:2048]; Bm = ew[..., 2048:2560]; Cm = ew[..., 2560:3072]
    dt = ew[..., 3072:3088]
    q = ew[..., 3088:4112]; kk = ew[..., 4112:5136]; v = ew[..., 5136:6160]; gg = ew[..., 6160:7184]
    ssd = np.concatenate([z.reshape(2, 8, 128, 4, 256), xs.reshape(2, 8, 128, 4, 256),
                          Bm.reshape(2, 8, 128, 4, 128), Cm.reshape(2, 8, 128, 4, 128)], axis=-1)
    m["ev_w_ssd"] = f(ssd.transpose(0, 3, 2, 1, 4)).reshape(2, 4, 128, 8 * 768)
    m["ev_w_dt"] = f(dt.transpose(0, 2, 1, 3)).reshape(2, 128, 8 * 16)
    att = np.concatenate([q.reshape(2, 8, 128, 8, 128), kk.reshape(2, 8, 128, 8, 128),
                          v.reshape(2, 8, 128, 8, 128), gg.reshape(2, 8, 128, 8, 128)], axis=-1)
    m["ev_w_att"] = f(att.transpose(0, 3, 2, 1, 4)).reshape(2, 8, 128, 8 * 512)
    wo = np.asarray(inp["even_w_out"]).reshape(2, 16, 128, 8, 128)
    m["ev_w_out_t"] = f(wo.transpose(0, 3, 2, 1, 4)).reshape(2, 8, 128, 16 * 128)
    m["conv_w_cols"] = f(np.asarray(inp["conv_w"]).reshape(2, 4, 16, 128).transpose(0, 3, 2, 1)).reshape(2, 128, 64)
    m["conv_b_cols"] = f(np.asarray(inp["conv_b"]).reshape(2, 16, 128).transpose(0, 2, 1))
    for nm in ("dt_bias", "A_log", "D_skip", "lambda_q1", "lambda_k1", "lambda_q2", "lambda_k2", "subln_w"):
        m[nm] = f(inp[nm])
    m["ssd_norm_w_cols"] = f(np.asarray(inp["ssd_norm_w"]).reshape(2, 8, 128).transpose(0, 2, 1))
    rb = np.asarray(inp["rel_bias"], dtype=np.float32)
    kpos = np.arange(128)[:, None]
    qpos = np.arange(128)[None, :]
    bd = np.empty((128, 8, 2, 128), np.float32)
    for Dd in range(2):
        rel = qpos - kpos + 128 * Dd
        bidx = _t5_bucket(rel)
        g_ = rb[bidx]
        g_ = np.where((rel >= 0)[:, :, None], g_, np.float32(-30000.0))
        bd[:, :, Dd, :] = g_.transpose(0, 2, 1)
    m["rel_biasD"] = f(bd).reshape(128, 8 * 2 * 128)
    m["rel_b31"] = f(rb[31])
    return m


class NS:
    pass


class Carver:
    def __init__(self, g):
        self.g = g
        self.fo = 0
        self.bo = 0

    def f(self, n):
        ap = self.g.arf[:, self.fo:self.fo + n]
        self.fo += (n + 7) // 8 * 8
        assert self.fo <= ARF_N, ("ARF overflow", self.fo)
        return ap

    def b(self, n):
        ap = self.g.arb[:, self.bo:self.bo + n]
        self.bo += (n + 15) // 16 * 16
        assert self.bo <= ARB_N, ("ARB overflow", self.bo)
        return ap


def bank(g, i):
    return g.ps[:, i, :]


def build(L=2048, NSEQ=2, layers=(0, 1, 2, 3)):
    nc = bass.Bass("TRN2", target_bir_lowering=False, dynamic_dma_scratch_size=4096)
    NT, NB = L // 512, L // 128
    g = NS()
    g.nc, g.L, g.NT, g.NB = nc, L, NT, NB
    dr = lambda name, shape, kind="ExternalInput": nc.dram_tensor(name, list(shape), F32, kind=kind).ap()
    g.x_d = dr("x", [NSEQ, L, D])
    g.out_d = dr("out", [NSEQ, L, D], "ExternalOutput")
    g.d = {}
    shapes = {
        "final_norm_w": [D], "norm_w_cols": [DEPTH, 128, KC],
        "ident": [128, 128], "ones": [128, 128], "triC": [128, 128], "triU": [128, 128], "triI": [128, 128],
        "sel": [128, 2], "mg16": [16, 16], "maskq": [128, 2],
        "odd_w_in_t": [2, 16, 128, KC * 512], "odd_w_out": [2, HG_W, D], "hgrn_lower_bounds": [DEPTH, HG_W],
        "hgrn_norm_w": [2, 128],
        "ev_w_ssd": [2, 4, 128, 8 * 768], "ev_w_dt": [2, 128, 8 * 16], "ev_w_att": [2, 8, 128, 8 * 512],
        "ev_w_out_t": [2, 8, 128, 16 * 128], "conv_w_cols": [2, 128, 64], "conv_b_cols": [2, 128, 16],
        "dt_bias": [2, 16], "A_log": [2, 16], "D_skip": [2, 16], "lambda_q1": [2, 64], "lambda_k1": [2, 64],
        "lambda_q2": [2, 64], "lambda_k2": [2, 64], "subln_w": [2, 128], "ssd_norm_w_cols": [2, 128, 8],
        "rel_biasD": [128, 8 * 2 * 128], "rel_b31": [8],
    }
    for nm, shp in shapes.items():
        g.d[nm] = dr(nm, shp)
    g.in_names = ["x"] + list(shapes.keys())

    es = contextlib.ExitStack()
    sb = lambda name, shape, dt=F32: es.enter_context(nc.sbuf_tensor(name, list(shape), dt))
    P = Prog(nc)
    g.P = P
    g.hT = sb("hT", [128, KC, L]); g.hU = [[P.unit(f"h{k}_{n}") for n in range(NT)] for k in range(KC)]
    g.uT = sb("uT", [128, KC, L], BF16); g.uU = [P.unit(f"u{n}") for n in range(NT)]
    g.cst = {}
    g.cU = P.unit("consts")
    for nm in ("ident", "ones", "triI"):
        g.cst[nm] = sb("c_" + nm, [128, 128])
    g.cst["sel"] = sb("c_sel", [128, 2])
    g.cst["maskq"] = sb("c_maskq", [128, 2])
    g.cst["mg16"] = sb("c_mg16", [16, 16])
    g.nwc = sb("nwc", [128, DEPTH, KC])
    g.stat = sb("stat", [128, 16]); g.statU = P.unit("stat")
    g.sq = [sb(f"sq{i}", [128, 512], BF16) for i in range(2)]; g.sqU = P.units(2, "sq")
    g.onesb = sb("onesb", [128, 128], BF16); g.onesbU = P.unit("onesb")
    g.rstd_t = sb("rstd_t", [128, 512]); g.rstdU = P.unit("rstd_t")
    g.arf = sb("arf", [128, ARF_N])
    g.arb = sb("arb", [128, ARB_N], BF16)
    g.ps = es.enter_context(nc.psum_tensor("ps", [128, 8, 512], F32))
    g.bU = P.units(8, "bank")

    for nm in ("ident", "ones", "triI", "sel", "maskq", "mg16"):
        P.dma("sp", lambda e, nm=nm: e.dma_start(out=g.cst[nm][:], in_=g.d[nm]), "c_" + nm, writes=[g.cU])
    P.dma("sp", lambda e: e.dma_start(out=g.nwc[:], in_=g.d["norm_w_cols"].rearrange("l p k -> p l k")), "c_nwc", writes=[g.cU])

    P.dve(lambda e: e.tensor_copy(g.onesb[:], g.cst["ones"][:]), reads=[g.cU], writes=[g.onesbU])
    out_ops = []
    for s in range(NSEQ):
        P.barrier()
        load_x(g, s)
        P.barrier()
        for li in layers:
            rms_to_uT(g, li)
            if li % 2 == 1:
                odd_layer(g, li)
            else:
                even_ssd(g, li)
                P.barrier()
                even_attn(g, li)
            P.barrier()
        out_ops += final_norm_store(g, s)
    P.emit(final_wait_ops=out_ops[-4:])
    P.run_block()
    es.close()
    return nc, P


def load_x(g, s):
    P, NT = g.P, g.NT
    A = Carver(g)
    xst = A.f(4096).rearrange("p (b d) -> p b d", b=4)
    xU = P.unit("xst")
    ident = g.cst["ident"]
    for n in range(NT):
        src = g.x_d[s, n * 512:(n + 1) * 512, :].rearrange("(b p) d -> p b d", p=128)
        P.dma("sp", lambda e, src=src: e.dma_start(out=xst, in_=src), "xst", writes=[xU])
        for k in range(KC):
            bk = k % 8
            for b in range(4):
                P.pe(lambda e, bk=bk, b=b, k=k: e.transpose(
                    bank(g, bk)[:, b * 128:(b + 1) * 128], xst[:, b, k * 128:(k + 1) * 128], ident[:]),
                    reads=[xU, g.cU], writes=[g.bU[bk]])
            if k % 2 == 0:
                P.dve(lambda e, bk=bk, k=k, n=n: e.tensor_copy(g.hT[:, k, n * 512:(n + 1) * 512], bank(g, bk)),
                      reads=[g.bU[bk]], writes=[g.hU[k][n]])
            else:
                P.act(lambda e, bk=bk, k=k, n=n: e.copy(g.hT[:, k, n * 512:(n + 1) * 512], bank(g, bk)),
                      reads=[g.bU[bk]], writes=[g.hU[k][n]])


def final_norm_store(g, s):
    P, NB = g.P, g.NB
    A = Carver(g)
    fnw = A.f(D); fnwU = P.unit("fnw")
    ost = [A.f(D) for _ in range(2)]; ostU = P.units(2, "ost")
    junk = A.f(1024); junkU = P.unit("junk")
    ident = g.cst["ident"]
    P.dma("sp", lambda e: e.dma_start(out=fnw, in_=g.d["final_norm_w"].partition_broadcast(128)), "fnw", writes=[fnwU])
    outs = []
    for b in range(NB):
        n = b // 4
        oi = b % 2
        for k in range(KC):
            bk = k // 4
            P.pe(lambda e, bk=bk, k=k, b=b: e.transpose(
                bank(g, bk)[:, (k % 4) * 128:(k % 4 + 1) * 128], g.hT[:, k, b * 128:(b + 1) * 128], ident[:]),
                reads=[g.hU[k][n], g.cU], writes=[g.bU[bk]])
        for half in range(2):
            P.act(lambda e, half=half: e.activation(
                out=junk[:, half * 512:(half + 1) * 512], in_=bank(g, half), func=AF.Square,
                accum_out=g.stat[:, half:half + 1]),
                reads=[g.bU[half]], writes=[junkU, g.statU])
        P.dve(lambda e: e.tensor_tensor(out=g.stat[:, 2:3], in0=g.stat[:, 0:1], in1=g.stat[:, 1:2], op=ALU.add),
              reads=[g.statU], writes=[g.statU])
        P.act(lambda e: e.activation(out=g.stat[:, 3:4], in_=g.stat[:, 2:3], func=AF.Ln, scale=1.0 / D, bias=EPS),
              reads=[g.statU], writes=[g.statU])
        P.act(lambda e: e.activation(out=g.stat[:, 4:5], in_=g.stat[:, 3:4], func=AF.Exp, scale=-0.5),
              reads=[g.statU], writes=[g.statU])
        for half in range(2):
            P.dve(lambda e, half=half, oi=oi: e.scalar_tensor_tensor(
                out=ost[oi][:, half * 512:(half + 1) * 512], in0=bank(g, half), scalar=g.stat[:, 4:5],
                in1=fnw[:, half * 512:(half + 1) * 512], op0=ALU.mult, op1=ALU.mult),
                reads=[g.bU[half], g.statU, fnwU], writes=[ostU[oi]])
        o = P.dma("sp", lambda e, oi=oi, s=s, b=b: e.dma_start(out=g.out_d[s, b * 128:(b + 1) * 128, :], in_=ost[oi]),
                  f"ost{oi}", reads=[ostU[oi]])
        outs.append(o)
    return outs


def rms_rstd_tile(g, src_fn, reads_fn, nchunks, dim):
    P = g.P
    ones = g.cst["ones"]
    for k in range(nchunks):
        i = k % 2
        P.act(lambda e, i=i, k=k: e.activation(out=g.sq[i][:], in_=src_fn(k), func=AF.Square),
              reads=reads_fn(k), writes=[g.sqU[i]])
        P.pe(lambda e, i=i, k=k: e.matmul(bank(g, 7), lhsT=g.onesb[:], rhs=g.sq[i][:], start=(k == 0), stop=(k == nchunks - 1)),
             reads=[g.sqU[i], g.onesbU], writes=[g.bU[7]])
    P.act(lambda e: e.activation(out=g.rstd_t[:], in_=bank(g, 7), func=AF.Ln, scale=1.0 / dim, bias=EPS),
          reads=[g.bU[7]], writes=[g.rstdU])
    P.act(lambda e: e.activation(out=g.rstd_t[:], in_=g.rstd_t[:], func=AF.Exp, scale=-0.5),
          reads=[g.rstdU], writes=[g.rstdU])


def rms_to_uT(g, li):
    P, NT = g.P, g.NT
    for n in range(NT):
        sl = slice(n * 512, (n + 1) * 512)
        rms_rstd_tile(g, lambda k, sl=sl: g.hT[:, k, sl], lambda k, n=n: [g.hU[k][n]], KC, D)
        for k in range(KC):
            P.dve(lambda e, k=k, sl=sl: e.scalar_tensor_tensor(
                out=g.uT[:, k, sl], in0=g.hT[:, k, sl], scalar=g.nwc[:, li, k:k + 1], in1=g.rstd_t[:],
                op0=ALU.mult, op1=ALU.mult),
                reads=[g.hU[k][n], g.rstdU, g.cU], writes=[g.uU[n]])


def odd_layer(g, li):
    P, L, NB, NT = g.P, g.L, g.NB, g.NT
    oi = li // 2
    A = Carver(g)
    o = NS()
    c = g.cst
    f3 = lambda: A.f(512).rearrange("p (b d) -> p b d", b=4)
    f16 = lambda: A.f(NB * 128).rearrange("p (b d) -> p b d", d=128)
    o.fall = f16(); o.fallU = [P.unit() for _ in range(NT)]
    o.kkall = f16(); o.kkallU = [P.unit() for _ in range(NT)]
    o.qsall = f16(); o.qsallU = [P.unit() for _ in range(NT)]
    o.e13 = A.f(1024).rearrange("p (t b d) -> p t b d", t=2, b=4); o.e13U = P.unit()
    o.e2 = f3(); o.e2U = P.unit()
    o.kt = o.e2; o.ktU = o.e2U
    o.lbr = f3(); o.lbrU = P.unit()
    o.lbh = A.f(128); o.omlh = A.f(128); o.den = A.f(128); o.lbU = P.unit()
    o.eb = [A.f(NB * 2).rearrange("p (b t) -> p b t", t=2) for _ in range(2)]
    o.S = A.f(128); o.SU = P.unit()
    o.junk2 = A.f(128); o.junk2U = P.unit()
    o.oall = A.f(NB * 128).rearrange("p (b d) -> p b d", d=128); o.oallU = P.unit()
    o.ssall = A.f(NB); o.rsall = A.f(NB); o.ssU = P.unit()
    o.hnw = A.f(128); o.hnwU = P.unit()
    o.triC = A.f(128); o.triU = A.f(128); o.triUU = P.unit()
    o.bst = A.f(8); o.bstU = P.unit()
    o.w = [A.b(KC * 512).rearrange("p (k c) -> p k c", k=KC) for _ in range(2)]; o.wU = P.units(2)
    o.wout = A.b(2 * D).rearrange("p (j m) -> p j m", j=2); o.woutU = P.units(2)
    o.qT = [A.b(L) for _ in range(2)]
    o.kT = [A.b(L) for _ in range(2)]
    hb3 = lambda: A.b(NB * 128).rearrange("p (b d) -> p b d", d=128)
    o.kh = [hb3() for _ in range(2)]
    o.v = [hb3() for _ in range(2)]
    o.gs = [hb3() for _ in range(2)]
    o.hbU = [[[P.unit() for _ in range(NT)] for _ in range(6)] for _ in range(2)]
    o.attm = [A.b(128) for _ in range(2)]; o.attmU = P.units(2)
    o.Sb2 = [A.b(128) for _ in range(2)]; o.SbU2 = P.units(2)
    o.yT = A.b(2 * L).rearrange("p (j t) -> p j t", j=2); o.yTU = P.units(2)
    QT, KT, KH, VV, GS, EB = range(6)

    P.dma("sp", lambda e: e.dma_start(out=o.hnw, in_=g.d["hgrn_norm_w"][oi].partition_broadcast(128)), "o_hnw", writes=[o.hnwU])
    P.dma("sp", lambda e: e.dma_start(out=o.triC, in_=g.d["triC"]), "o_tri", writes=[o.triUU])
    P.dma("sp", lambda e: e.dma_start(out=o.triU, in_=g.d["triU"]), "o_tri", writes=[o.triUU])
    for i in range(2):
        P.dve(lambda e, i=i: e.memset(o.attm[i], 0.0), writes=[o.attmU[i]])

    def load_w(h):
        i = h % 2
        P.dma("pool", lambda e: e.dma_start(out=o.w[i].rearrange("p k c -> p (k c)"), in_=g.d["odd_w_in_t"][oi, h]),
              f"o_w{i}", writes=[o.wU[i]])

    def head_lb(h):
        hs = slice(h * 128, (h + 1) * 128)
        P.dma("sp", lambda e: e.dma_start(out=o.lbr, in_=g.d["hgrn_lower_bounds"][:, hs].partition_broadcast(128)),
              "o_lbr", writes=[o.lbrU])
        P.act(lambda e: e.activation(out=o.lbr, in_=o.lbr, func=AF.Exp), reads=[o.lbrU], writes=[o.lbrU])
        P.dve(lambda e: e.tensor_tensor(out=o.den, in0=o.lbr[:, 0, :], in1=o.lbr[:, 1, :], op=ALU.add), reads=[o.lbrU], writes=[o.lbU])
        P.dve(lambda e: e.tensor_tensor(out=o.den, in0=o.den, in1=o.lbr[:, 2, :], op=ALU.add), reads=[o.lbrU, o.lbU], writes=[o.lbU])
        P.dve(lambda e: e.tensor_tensor(out=o.den, in0=o.den, in1=o.lbr[:, 3, :], op=ALU.add), reads=[o.lbrU, o.lbU], writes=[o.lbU])
        P.dve(lambda e: e.reciprocal(o.den, o.den), reads=[o.lbU], writes=[o.lbU])
        if li == 1:
            P.dve(lambda e: e.tensor_tensor(out=o.lbh, in0=o.lbr[:, 1, :], in1=o.den, op=ALU.mult), reads=[o.lbrU, o.lbU], writes=[o.lbU])
        else:
            P.dve(lambda e: e.tensor_tensor(out=o.lbh, in0=o.lbr[:, 1, :], in1=o.lbr[:, 2, :], op=ALU.add), reads=[o.lbrU, o.lbU], writes=[o.lbU])
            for j in range(3, li + 1):
                P.dve(lambda e, j=j: e.tensor_tensor(out=o.lbh, in0=o.lbh, in1=o.lbr[:, j, :], op=ALU.add), reads=[o.lbrU, o.lbU], writes=[o.lbU])
            P.dve(lambda e: e.tensor_tensor(out=o.lbh, in0=o.lbh, in1=o.den, op=ALU.mult), reads=[o.lbU], writes=[o.lbU])
        P.dve(lambda e: e.tensor_scalar(out=o.omlh, in0=o.lbh, scalar1=-1.0, scalar2=1.0, op0=ALU.mult, op1=ALU.add),
              reads=[o.lbU], writes=[o.lbU])

    def stageA1(h, n):
        hb = h % 2
        wi = h % 2
        U = o.hbU[hb]
        bs = slice(n * 4, (n + 1) * 4)
        for b in range(4):
            tb = n * 4 + b
            for k in range(KC):
                P.pe(lambda e, b=b, tb=tb, k=k: e.matmul(bank(g, b), lhsT=g.uT[:, k, tb * 128:(tb + 1) * 128], rhs=o.w[wi][:, k, :],
                                                         start=(k == 0), stop=(k == KC - 1)),
                     reads=[g.uU[n], o.wU[wi]], writes=[g.bU[b]])
        pj = g.ps[:, 0:4, :]
        pb = [g.bU[0], g.bU[1], g.bU[2], g.bU[3]]
        bc4 = lambda t: t.unsqueeze(1).to_broadcast([128, 4, 128])
        P.act(lambda e: e.activation(out=o.fall[:, bs, :], in_=pj[:, :, 128:256], func=AF.Sigmoid), reads=pb, writes=[o.fallU[n]])
        P.act(lambda e: e.activation(out=o.qsall[:, bs, :], in_=pj[:, :, 0:128], func=AF.Silu), reads=pb, writes=[o.qsallU[n]])
        P.act(lambda e: e.activation(out=o.gs[hb][:, bs, :], in_=pj[:, :, 384:512], func=AF.Silu), reads=pb, writes=[U[GS][n]])
        P.act(lambda e: e.copy(o.v[hb][:, bs, :], pj[:, :, 256:384]), reads=pb, writes=[U[VV][n]])
        P.dve(lambda e: e.tensor_tensor(out=o.fall[:, bs, :], in0=o.fall[:, bs, :], in1=bc4(o.omlh), op=ALU.mult),
              reads=[o.fallU[n], o.lbU], writes=[o.fallU[n]])
        P.dve(lambda e: e.tensor_tensor(out=o.fall[:, bs, :], in0=o.fall[:, bs, :], in1=bc4(o.lbh), op=ALU.add),
              reads=[o.fallU[n], o.lbU], writes=[o.fallU[n]])
        P.pool(lambda e: e.tensor_scalar(out=o.kkall[:, bs, :], in0=o.fall[:, bs, :], scalar1=-1.0, scalar2=1.0, op0=ALU.mult, op1=ALU.add),
               reads=[o.fallU[n]], writes=[o.kkallU[n]])

    def stageAmid(h):
        P.act(lambda e: e.activation(out=o.fall, in_=o.fall, func=AF.Ln), reads=o.fallU + o.kkallU, writes=o.fallU)

    def stageA2(h, n):
        hb = h % 2
        U = o.hbU[hb]
        bs = slice(n * 4, (n + 1) * 4)
        logf = o.fall[:, bs, :]
        kk = o.kkall[:, bs, :]
        qs = o.qsall[:, bs, :]
        lu = [o.fallU[n]]
        lf4 = logf.rearrange("p b d -> p (b d)")
        P.pe(lambda e: e.matmul(bank(g, 0), lhsT=o.triC, rhs=lf4, start=True, stop=True), reads=lu + [o.triUU], writes=[g.bU[0]])
        P.pe(lambda e: e.matmul(bank(g, 1), lhsT=o.triU, rhs=lf4, start=True, stop=True), reads=lu + [o.triUU], writes=[g.bU[1]])
        for b in range(4):
            P.pe(lambda e, b=b: e.matmul(bank(g, 2)[:, b * 2:b * 2 + 2], lhsT=logf[:, b, :], rhs=c["sel"][:], start=True, stop=True),
                 reads=lu + [g.cU], writes=[g.bU[2]])
        P.act(lambda e: e.activation(out=o.eb[hb][:, bs, :], in_=bank(g, 2)[:, 0:8].rearrange("p (b t) -> p b t", t=2), func=AF.Exp),
              reads=[g.bU[2]], writes=[U[EB][n]])
        P.act(lambda e: e.activation(out=o.e13, in_=g.ps[:, 0:2, :].rearrange("p t (b d) -> p t b d", b=4), func=AF.Exp),
              reads=[g.bU[0], g.bU[1]], writes=[o.e13U])
        P.act(lambda e: e.activation(out=o.e2, in_=bank(g, 0).rearrange("p (b d) -> p b d", b=4), func=AF.Exp, scale=-1.0),
              reads=[g.bU[0]], writes=[o.e2U])
        P.dve(lambda e: e.tensor_tensor(out=qs, in0=qs, in1=o.e13[:, 0], op=ALU.mult), reads=[o.qsallU[n], o.e13U], writes=[o.qsallU[n]])
        P.pool(lambda e: e.tensor_tensor(out=o.e2, in0=kk, in1=o.e2, op=ALU.mult), reads=[o.kkallU[n], o.e2U], writes=[o.e2U])
        P.dve(lambda e: e.tensor_tensor(out=o.kh[hb][:, bs, :], in0=kk, in1=o.e13[:, 1], op=ALU.mult),
              reads=[o.kkallU[n], o.e13U], writes=[U[KH][n]])
        idf = c["ident"]
        for b in range(4):
            P.pe(lambda e, b=b: e.transpose(bank(g, 3)[:, b * 128:(b + 1) * 128], qs[:, b, :], idf[:]),
                 reads=[o.qsallU[n], g.cU], writes=[g.bU[3]])
        for b in range(4):
            P.pe(lambda e, b=b: e.transpose(bank(g, 2)[:, b * 128:(b + 1) * 128], o.kt[:, b, :], idf[:]),
                 reads=[o.ktU, g.cU], writes=[g.bU[2]])
        P.act(lambda e: e.copy(o.qT[hb][:, n * 512:(n + 1) * 512], bank(g, 3)), reads=[g.bU[3]], writes=[U[QT][n]])
        P.act(lambda e: e.copy(o.kT[hb][:, n * 512:(n + 1) * 512], bank(g, 2)), reads=[g.bU[2]], writes=[U[KT][n]])

    def stageAall(h):
        for n in range(NT):
            stageA1(h, n)
        stageAmid(h)
        for n in range(NT):
            stageA2(h, n)

    def stageB(h):
        hb = h % 2
        U = o.hbU[hb]

        def att(tb):
            n = tb // 4
            ts = slice(tb * 128, (tb + 1) * 128)
            bk = 5 + 2 * (tb % 2)
            P.pe(lambda e: e.matmul(bank(g, bk)[:, 64:128], lhsT=o.kT[hb][:, ts],
                                    rhs=o.qT[hb][:, tb * 128 + 64:(tb + 1) * 128], start=True, stop=True),
                 reads=[U[QT][n], U[KT][n]], writes=[g.bU[bk]])
            P.pe(lambda e: e.matmul(bank(g, bk)[0:64, 0:64], lhsT=o.kT[hb][:, tb * 128:tb * 128 + 64],
                                    rhs=o.qT[hb][:, tb * 128:tb * 128 + 64], start=True, stop=True),
                 reads=[U[QT][n], U[KT][n]], writes=[g.bU[bk]])

        def mask(tb):
            ai = tb % 2
            bk = 5 + 2 * (tb % 2)
            P.dve(lambda e: e.tensor_tensor(out=o.attm[ai][:, 64:128], in0=bank(g, bk)[:, 64:128], in1=c["triI"][:, 64:128], op=ALU.mult),
                  reads=[g.bU[bk], g.cU], writes=[o.attmU[ai]])
            P.dve(lambda e: e.tensor_tensor(out=o.attm[ai][0:64, 0:64], in0=bank(g, bk)[0:64, 0:64], in1=c["triI"][0:64, 0:64], op=ALU.mult),
                  reads=[g.bU[bk], g.cU], writes=[o.attmU[ai]])

        att(0)
        mask(0)
        for tb in range(NB):
            n = tb // 4
            ts = slice(tb * 128, (tb + 1) * 128)
            ai = tb % 2
            si = tb % 2
            P.pe(lambda e: e.matmul(bank(g, 6)[:, 128:256], lhsT=o.kh[hb][:, tb, :], rhs=o.v[hb][:, tb, :], start=True, stop=True),
                 reads=[U[KH][n], U[VV][n]], writes=[g.bU[6]])
            if tb + 1 < NB:
                att(tb + 1)
            P.pe(lambda e: e.matmul(bank(g, 4)[:, 0:128], lhsT=o.attm[ai], rhs=o.v[hb][:, tb, :], start=True, stop=(tb == 0)),
                 reads=[o.attmU[ai], U[VV][n]], writes=[g.bU[4]])
            if tb > 0:
                P.pe(lambda e: e.matmul(bank(g, 4)[:, 0:128], lhsT=o.qT[hb][:, ts], rhs=o.Sb2[si], start=False, stop=True),
                     reads=[o.SbU2[si], U[QT][n]], writes=[g.bU[4]])
            if tb == 0:
                P.dve(lambda e: e.tensor_copy(o.S, bank(g, 6)[:, 128:256]), reads=[g.bU[6]], writes=[o.SU])
            else:
                P.dve(lambda e: e.scalar_tensor_tensor(out=o.S, in0=o.S, scalar=o.eb[hb][:, tb, 1:2], in1=bank(g, 6)[:, 128:256],
                                                       op0=ALU.mult, op1=ALU.add),
                      reads=[o.SU, g.bU[6], U[EB][n]], writes=[o.SU])
            if tb + 1 < NB:
                nn = (tb + 1) // 4
                sn = (tb + 1) % 2
                P.dve(lambda e: e.tensor_scalar(out=o.Sb2[sn], in0=o.S, scalar1=o.eb[hb][:, tb + 1, 0:1], scalar2=None, op0=ALU.mult),
                      reads=[o.SU, U[EB][nn]], writes=[o.SbU2[sn]])
                mask(tb + 1)
            P.act(lambda e: e.copy(o.oall[:, tb, :], bank(g, 4)[:, 0:128]), reads=[g.bU[4]], writes=[o.oallU])
            P.act(lambda e: e.activation(out=o.junk2, in_=bank(g, 4)[:, 0:128], func=AF.Square, accum_out=o.ssall[:, tb:tb + 1]),
                  reads=[g.bU[4]], writes=[o.junk2U, o.ssU])

    def stageC(h):
        hb = h % 2
        U = o.hbU[hb]
        hj = h % 2
        P.act(lambda e: e.activation(out=o.rsall, in_=o.ssall, func=AF.Ln, scale=1.0 / 128, bias=EPS), reads=[o.ssU], writes=[o.ssU])
        P.act(lambda e: e.activation(out=o.rsall, in_=o.rsall, func=AF.Exp, scale=-0.5), reads=[o.ssU], writes=[o.ssU])
        P.dve(lambda e: e.tensor_tensor(out=o.oall, in0=o.oall, in1=o.rsall.unsqueeze(2).to_broadcast([128, NB, 128]), op=ALU.mult),
              reads=[o.oallU, o.ssU], writes=[o.oallU])
        P.dve(lambda e: e.tensor_tensor(out=o.oall, in0=o.oall, in1=o.hnw.unsqueeze(1).to_broadcast([128, NB, 128]), op=ALU.mult),
              reads=[o.oallU, o.hnwU], writes=[o.oallU])
        P.dve(lambda e: e.tensor_tensor(out=o.oall, in0=o.oall, in1=o.gs[hb], op=ALU.mult),
              reads=[o.oallU] + [U[GS][n] for n in range(NT)], writes=[o.oallU])
        for n in range(NT):
            bk = 4 + (n % 4)
            for b in range(4):
                P.pe(lambda e, bk=bk, b=b, n=n: e.transpose(bank(g, bk)[:, b * 128:(b + 1) * 128], o.oall[:, n * 4 + b, :], c["ident"][:]),
                     reads=[o.oallU, g.cU], writes=[g.bU[bk]])
            P.act(lambda e, bk=bk, n=n: e.copy(o.yT[:, hj, n * 512:(n + 1) * 512], bank(g, bk)), reads=[g.bU[bk]], writes=[o.yTU[hj]])

    def outproj(hp):
        for j in range(2):
            src = g.d["odd_w_out"][oi, (hp * 2 + j) * 128:(hp * 2 + j + 1) * 128, :]
            P.dma("pool", lambda e, j=j, src=src: e.dma_start(out=o.wout[:, j, :], in_=src), f"o_wout{j}", writes=[o.woutU[j]])
        cnt = 0
        for m in range(KC):
            for n in range(NT):
                bk = 4 + cnt % 4
                cnt += 1
                for j in range(2):
                    P.pe(lambda e, bk=bk, m=m, n=n, j=j: e.matmul(bank(g, bk), lhsT=o.wout[:, j, m * 128:(m + 1) * 128],
                                                                     rhs=o.yT[:, j, n * 512:(n + 1) * 512], start=(j == 0), stop=(j == 1)),
                         reads=[o.woutU[j], o.yTU[j]], writes=[g.bU[bk]])
                P.dve(lambda e, bk=bk, m=m, n=n: e.tensor_tensor(out=g.hT[:, m, n * 512:(n + 1) * 512], in0=g.hT[:, m, n * 512:(n + 1) * 512],
                                                                   in1=bank(g, bk), op=ALU.add),
                      reads=[g.bU[bk], g.hU[m][n]], writes=[g.hU[m][n]])

    load_w(0)
    load_w(1)
    head_lb(0)
    stageAall(0)
    for h in range(16):
        P.capture()
        stageB(h)
        stageC(h)
        if h % 2 == 1:
            outproj(h // 2)
        LB = P.end_capture()
        P.capture()
        if h + 1 < 16:
            if h + 2 < 16:
                load_w(h + 2)
            head_lb(h + 1)
            stageAall(h + 1)
        LA = P.end_capture()
        P.replay_merged(LA, LB)


def even_ssd(g, li):
    P, L, NB, NT = g.P, g.L, g.NB, g.NT
    ei = li // 2
    c = g.cst
    A = Carver(g)
    s = NS()
    HB = NB * 16
    s.xpre = A.f(515); s.xpreU = P.unit()
    s.cacc = A.f(512); s.caccU = P.unit()
    v3 = lambda ap: ap.rearrange("p (b h) -> p b h", h=16)
    s.dt = A.f(HB); s.atok = A.f(HB); s.acs = A.f(HB); s.eacs = A.f(HB); s.dtd = A.f(HB); s.edl = A.f(HB)
    s.dtU = P.unit()
    s.acsT = A.f(L); s.acsTU = P.unit()
    s.cw = A.f(64); s.cb = A.f(16); s.dtb = A.f(16); s.Abc = A.f(16); s.Dsk = A.f(16); s.snw = A.f(8)
    s.smallU = P.unit()
    s.S = A.f(256); s.SU = P.unit()
    scan_off = A.fo
    s.Rbd = A.f(512); s.RbdU = P.unit()
    D2 = lambda n: ([A.f(n) for _ in range(2)], P.units(2))
    s.acsTb2, s.acsTbU2 = D2(128)
    s.CBm2, s.CBmU2 = D2(128)
    s.Dm2, s.DmU2 = D2(512)
    s.E2, s.EU2 = D2(512)
    s.t12, s.t1U2 = D2(256)
    s.t22, s.t2U2 = D2(256)
    s.ytmp2, s.ytmpU2 = D2(256)
    s.rstd_all = g.arf[:, scan_off:scan_off + L]; s.rstdallU = P.unit()
    assert scan_off + L <= ARF_N
    s.w = A.b(KC * 768).rearrange("p (k c) -> p k c", k=KC); s.wU = P.unit()
    s.wdt = A.b(KC * 16).rearrange("p (k c) -> p k c", k=KC); s.wdtU = P.unit()
    s.BT = A.b(L); s.CT = A.b(L); s.BCU = [P.unit() for _ in range(NT)]
    s.xtok = A.b(NB * 256).rearrange("p (b c) -> p b c", c=256); s.xtokU = [P.unit() for _ in range(NT)]
    s.Btok = A.b(NB * 128).rearrange("p (b c) -> p b c", c=128); s.BtokU = [P.unit() for _ in range(NT)]
    scan_bo = A.bo
    s.zs2 = [A.b(256) for _ in range(2)]; s.zsU2 = P.units(2)
    s.sc2 = [A.b(512).rearrange("p (h l) -> p h l", h=4) for _ in range(2)]; s.scU2 = P.units(2)
    s.Xdt2 = [A.b(256) for _ in range(2)]; s.XdtU2 = P.units(2)
    s.XB2 = [A.b(256) for _ in range(2)]; s.XBU2 = P.units(2)
    s.Sbf = A.b(256); s.SbfU = P.unit()
    s.yTa = A.b(8 * L).rearrange("p (c t) -> p c t", c=8); s.yTaU = [[P.unit() for _ in range(NT)] for _ in range(8)]
    s.wo2 = [g.arb[:, scan_bo + i * 1024:scan_bo + (i + 1) * 1024].rearrange("p (c j) -> p c j", c=8) for i in range(2)]
    s.woU2 = P.units(2)
    assert scan_bo + 2048 <= A.bo
    s.tmp2 = [s.cacc, s.xpre[:, 0:512]]; s.tmpU2 = [s.caccU, s.xpreU]
    ident = c["ident"]

    sm = [s.smallU]
    P.dma("sp", lambda e: e.dma_start(out=s.cw, in_=g.d["conv_w_cols"][ei]), "s_small", writes=sm)
    P.dma("sp", lambda e: e.dma_start(out=s.cb, in_=g.d["conv_b_cols"][ei]), "s_small", writes=sm)
    P.dma("sp", lambda e: e.dma_start(out=s.dtb, in_=g.d["dt_bias"][ei].partition_broadcast(128)), "s_small", writes=sm)
    P.dma("sp", lambda e: e.dma_start(out=s.Abc, in_=g.d["A_log"][ei].partition_broadcast(128)), "s_small", writes=sm)
    P.dma("sp", lambda e: e.dma_start(out=s.Dsk, in_=g.d["D_skip"][ei].partition_broadcast(128)), "s_small", writes=sm)
    P.dma("sp", lambda e: e.dma_start(out=s.snw, in_=g.d["ssd_norm_w_cols"][ei]), "s_small", writes=sm)
    P.act(lambda e: e.activation(out=s.Abc, in_=s.Abc, func=AF.Exp), reads=sm, writes=sm)
    P.dve(lambda e: e.tensor_scalar(out=s.Abc, in0=s.Abc, scalar1=-1.0, scalar2=None, op0=ALU.mult), reads=sm, writes=sm)
    P.dma("pool", lambda e: e.dma_start(out=s.wdt.rearrange("p k c -> p (k c)"), in_=g.d["ev_w_dt"][ei]), "s_wdt", writes=[s.wdtU])

    for b in range(NB):
        for k in range(KC):
            P.pe(lambda e, b=b, k=k: e.matmul(bank(g, 0)[:, b * 16:(b + 1) * 16], lhsT=g.uT[:, k, b * 128:(b + 1) * 128],
                                              rhs=s.wdt[:, k, :], start=(k == 0), stop=(k == KC - 1)),
                 reads=[g.uU[b // 4], s.wdtU], writes=[g.bU[0]])
    bc_h = lambda t: t.unsqueeze(1).to_broadcast([128, NB, 16])
    du = [s.dtU]
    P.dve(lambda e: e.tensor_tensor(out=v3(s.dt), in0=v3(bank(g, 0)[:, 0:HB]), in1=bc_h(s.dtb), op=ALU.add),
          reads=[g.bU[0]] + sm, writes=du)
    P.act(lambda e: e.activation(out=s.dt, in_=s.dt, func=AF.Exp), reads=du, writes=du)
    P.act(lambda e: e.activation(out=s.dt, in_=s.dt, func=AF.Ln, bias=1.0), reads=du, writes=du)
    P.dve(lambda e: e.tensor_tensor(out=v3(s.atok), in0=v3(s.dt), in1=bc_h(s.Abc), op=ALU.mult), reads=du + sm, writes=du)
    P.pe(lambda e: e.matmul(bank(g, 1)[:, 0:HB], lhsT=c["triI"][:], rhs=s.atok, start=True, stop=True), reads=du + [g.cU], writes=[g.bU[1]])
    P.pe(lambda e: e.matmul(bank(g, 2)[:, 0:HB], lhsT=c["ones"][:], rhs=s.atok, start=True, stop=True), reads=du + [g.cU], writes=[g.bU[2]])
    P.dve(lambda e: e.tensor_copy(s.acs, bank(g, 1)[:, 0:HB]), reads=[g.bU[1]], writes=du)
    P.act(lambda e: e.activation(out=s.eacs, in_=s.acs, func=AF.Exp), reads=du, writes=du)
    P.dve(lambda e: e.tensor_copy(s.edl, bank(g, 2)[:, 0:HB]), reads=[g.bU[2]], writes=du)
    P.dve(lambda e: e.tensor_tensor(out=s.dtd, in0=s.edl, in1=s.acs, op=ALU.subtract), reads=du, writes=du)
    P.act(lambda e: e.activation(out=s.dtd, in_=s.dtd, func=AF.Exp), reads=du, writes=du)
    P.dve(lambda e: e.tensor_tensor(out=s.dtd, in0=s.dtd, in1=s.dt, op=ALU.mult), reads=du, writes=du)
    P.act(lambda e: e.activation(out=s.edl, in_=s.edl, func=AF.Exp), reads=du, writes=du)
    for n in range(NT):
        for j in range(4):
            b = n * 4 + j
            P.pe(lambda e, b=b, j=j: e.transpose(bank(g, 3)[0:16, j * 128:(j + 1) * 128], s.acs[:, b * 16:(b + 1) * 16], c["ident"][:]),
                 reads=du + [g.cU], writes=[g.bU[3]])
        P.act(lambda e, n=n: e.copy(s.acsT[0:16, n * 512:(n + 1) * 512], bank(g, 3)[0:16, :]), reads=[g.bU[3]], writes=[s.acsTU])

    pcnt = [0]
    for grp in range(4):
        P.dma("pool", lambda e, grp=grp: e.dma_start(out=s.w.rearrange("p k c -> p (k c)"), in_=g.d["ev_w_ssd"][ei, grp]),
              "s_w", writes=[s.wU])
        chunks = [(256, 2 * grp, "x0"), (384, 2 * grp + 1, "x1"), (512, 8 + grp, "B"), (640, 12 + grp, "C")]
        cp1, cp2 = [], []
        for wc0, cch, kind in chunks:
            for n in range(NT):
                sl = slice(n * 512, (n + 1) * 512)
                bk = 3 + (pcnt[0] % 2)
                pcnt[0] += 1
                P.capture()
                for k in range(KC):
                    P.pe(lambda e, bk=bk, k=k, wc0=wc0, sl=sl: e.matmul(bank(g, bk), lhsT=s.w[:, k, wc0:wc0 + 128], rhs=g.uT[:, k, sl],
                                                                        start=(k == 0), stop=(k == KC - 1)),
                         reads=[g.uU[n], s.wU], writes=[g.bU[bk]])
                cp1.append(P.end_capture())
                P.capture()
                if n == 0:
                    P.dve(lambda e: e.memset(s.xpre[:, 0:3], 0.0), writes=[s.xpreU])
                else:
                    P.dve(lambda e: e.tensor_copy(s.xpre[:, 0:3], s.xpre[:, 512:515]), reads=[s.xpreU], writes=[s.xpreU])
                P.act(lambda e, bk=bk: e.copy(s.xpre[:, 3:515], bank(g, bk)), reads=[g.bU[bk]], writes=[s.xpreU])
                P.dve(lambda e, cch=cch: e.tensor_scalar(out=s.cacc, in0=s.xpre[:, 3:515], scalar1=s.cw[:, cch * 4 + 3:cch * 4 + 4],
                                                          scalar2=s.cb[:, cch:cch + 1], op0=ALU.mult, op1=ALU.add),
                      reads=[s.xpreU] + sm, writes=[s.caccU])
                for tap in (2, 1, 0):
                    P.dve(lambda e, cch=cch, tap=tap: e.scalar_tensor_tensor(
                        out=s.cacc, in0=s.xpre[:, tap:tap + 512], scalar=s.cw[:, cch * 4 + tap:cch * 4 + tap + 1], in1=s.cacc,
                        op0=ALU.mult, op1=ALU.add), reads=[s.xpreU, s.caccU] + sm, writes=[s.caccU])
                P.act(lambda e: e.activation(out=s.cacc, in_=s.cacc, func=AF.Silu), reads=[s.caccU], writes=[s.caccU])
                if kind in ("B", "C"):
                    dst = s.BT if kind == "B" else s.CT
                    P.dve(lambda e, dst=dst, sl=sl: e.tensor_copy(dst[:, sl], s.cacc), reads=[s.caccU], writes=[s.BCU[n]])
                if kind != "C":
                    for j in range(4):
                        P.pe(lambda e, j=j: e.transpose(bank(g, 5)[:, j * 128:(j + 1) * 128], s.cacc[:, j * 128:(j + 1) * 128], ident[:]),
                             reads=[s.caccU, g.cU], writes=[g.bU[5]])
                    src = bank(g, 5).rearrange("p (b c) -> p b c", b=4)
                    if kind == "B":
                        P.act(lambda e, n=n, src=src: e.copy(s.Btok[:, n * 4:(n + 1) * 4, :], src), reads=[g.bU[5]], writes=[s.BtokU[n]])
                    else:
                        co = 0 if kind == "x0" else 128
                        P.act(lambda e, n=n, src=src, co=co: e.copy(s.xtok[:, n * 4:(n + 1) * 4, co:co + 128], src),
                              reads=[g.bU[5]], writes=[s.xtokU[n]])
                cp2.append(P.end_capture())
        P.replay_merged(cp1[0], [])
        for i in range(len(cp2)):
            P.replay_merged(cp1[i + 1] if i + 1 < len(cp1) else [], [])
            P.replay_merged(cp2[i], [])
        hs4 = slice(4 * grp, 4 * grp + 4)
        fronts, backs = [], []
        for b in range(NB):
            P.capture()
            n = b // 4
            blk = slice(b * 128, (b + 1) * 128)
            hcol = lambda t, b=b: v3(t)[:, b, hs4]
            bch = lambda t, w, b=b: hcol(t, b).unsqueeze(2).to_broadcast([128, 4, w])
            x4 = s.xtok[:, b, :].rearrange("p (h q) -> p h q", h=4)
            pb_ = b % 2
            s.acsTb, s.acsTbU = s.acsTb2[pb_], s.acsTbU2[pb_]
            s.CBm, s.CBmU = s.CBm2[pb_], s.CBmU2[pb_]
            s.Dm, s.DmU = s.Dm2[pb_], s.DmU2[pb_]
            s.E, s.EU = s.E2[pb_], s.EU2[pb_]
            s.t1, s.t1U = s.t12[pb_], s.t1U2[pb_]
            s.t2, s.t2U = s.t22[pb_], s.t2U2[pb_]
            s.ytmp, s.ytmpU = s.ytmp2[pb_], s.ytmpU2[pb_]
            s.zs, s.zsU = s.zs2[pb_], s.zsU2[pb_]
            s.sc, s.scU = s.sc2[pb_], s.scU2[pb_]
            s.Xdt, s.XdtU = s.Xdt2[pb_], s.XdtU2[pb_]
            s.XB, s.XBU = s.XB2[pb_], s.XBU2[pb_]
            for k in range(KC):
                P.pe(lambda e, k=k, blk=blk: e.matmul(bank(g, 6)[:, 0:256], lhsT=g.uT[:, k, blk], rhs=s.w[:, k, 0:256],
                                                      start=(k == 0), stop=(k == KC - 1)),
                     reads=[g.uU[n], s.wU], writes=[g.bU[6]])
            P.act(lambda e: e.activation(out=s.zs, in_=bank(g, 6)[:, 0:256], func=AF.Silu), reads=[g.bU[6]], writes=[s.zsU])
            P.pe(lambda e, blk=blk: e.matmul(bank(g, 7)[:, 0:128], lhsT=s.BT[:, blk], rhs=s.CT[:, blk], start=True, stop=True),
                 reads=[s.BCU[n]], writes=[g.bU[7]])
            P.dve(lambda e: e.tensor_tensor(out=s.CBm, in0=bank(g, 7)[:, 0:128], in1=c["triI"][:], op=ALU.mult),
                  reads=[g.bU[7], g.cU], writes=[s.CBmU])
            P.dve(lambda e, blk=blk: e.tensor_tensor(out=s.Rbd[0:16, :].rearrange("p (h l) -> p h l", h=4),
                                                     in0=s.acsT[0:16, blk].unsqueeze(1).to_broadcast([16, 4, 128]),
                                                     in1=c["mg16"][:, hs4].unsqueeze(2).to_broadcast([16, 4, 128]), op=ALU.mult),
                  reads=[s.acsTU, g.cU], writes=[s.RbdU])
            P.pe(lambda e: e.matmul(bank(g, 1), lhsT=c["ones"][0:16, :], rhs=s.Rbd[0:16, :], start=True, stop=True),
                 reads=[s.RbdU, g.cU], writes=[g.bU[1]])
            P.dve(lambda e, b=b: e.tensor_tensor(out=s.Dm.rearrange("p (h l) -> p h l", h=4),
                                                 in0=bank(g, 1).rearrange("p (h l) -> p h l", h=4),
                                                 in1=bch(s.acs, 128, b), op=ALU.subtract),
                  reads=[g.bU[1]] + du, writes=[s.DmU])
            P.dve(lambda e: e.tensor_scalar(out=s.Dm, in0=s.Dm, scalar1=0.0, scalar2=None, op0=ALU.min), reads=[s.DmU], writes=[s.DmU])
            P.act(lambda e: e.activation(out=s.E, in_=s.Dm, func=AF.Exp), reads=[s.DmU], writes=[s.EU])
            P.dve(lambda e: e.tensor_tensor(out=s.sc, in0=s.E.rearrange("p (h l) -> p h l", h=4),
                                            in1=s.CBm.unsqueeze(1).to_broadcast([128, 4, 128]), op=ALU.mult),
                  reads=[s.EU, s.CBmU], writes=[s.scU])
            P.dve(lambda e, b=b: e.tensor_tensor(out=s.Xdt.rearrange("p (h q) -> p h q", h=4), in0=x4, in1=bch(s.dt, 64, b), op=ALU.mult),
                  reads=[s.xtokU[n]] + du, writes=[s.XdtU])
            P.dve(lambda e, b=b: e.tensor_tensor(out=s.XB.rearrange("p (h q) -> p h q", h=4), in0=x4, in1=bch(s.dtd, 64, b), op=ALU.mult),
                  reads=[s.xtokU[n]] + du, writes=[s.XBU])
            fronts.append(P.end_capture())
            P.capture()
            for h4 in range(4):
                P.pe(lambda e, h4=h4: e.matmul(bank(g, 2)[:, h4 * 64:(h4 + 1) * 64], lhsT=s.sc[:, h4, :], rhs=s.Xdt[:, h4 * 64:(h4 + 1) * 64],
                                               start=True, stop=True), reads=[s.scU, s.XdtU], writes=[g.bU[2]])
            if b > 0:
                P.pe(lambda e, blk=blk: e.matmul(bank(g, 3)[:, 0:256], lhsT=s.CT[:, blk], rhs=s.Sbf, start=True, stop=True),
                     reads=[s.BCU[n], s.SbfU], writes=[g.bU[3]])
            P.pe(lambda e, b=b: e.matmul(bank(g, 4)[:, 0:256], lhsT=s.Btok[:, b, :], rhs=s.XB, start=True, stop=True),
                 reads=[s.BtokU[n], s.XBU], writes=[g.bU[4]])
            if b > 0:
                P.dve(lambda e, b=b: e.tensor_tensor(out=s.t1.rearrange("p (h q) -> p h q", h=4),
                                                     in0=bank(g, 3)[:, 0:256].rearrange("p (h q) -> p h q", h=4),
                                                     in1=bch(s.eacs, 64, b), op=ALU.mult), reads=[g.bU[3]] + du, writes=[s.t1U])
                P.dve(lambda e: e.tensor_tensor(out=s.t2, in0=bank(g, 2)[:, 0:256], in1=s.t1, op=ALU.add), reads=[g.bU[2], s.t1U], writes=[s.t2U])
            else:
                P.dve(lambda e: e.tensor_copy(s.t2, bank(g, 2)[:, 0:256]), reads=[g.bU[2]], writes=[s.t2U])
            P.dve(lambda e: e.tensor_tensor(out=s.t1.rearrange("p (h q) -> p h q", h=4), in0=x4,
                                            in1=s.Dsk[:, hs4].unsqueeze(2).to_broadcast([128, 4, 64]), op=ALU.mult),
                  reads=[s.xtokU[n]] + sm, writes=[s.t1U])
            P.dve(lambda e: e.tensor_tensor(out=s.t2, in0=s.t2, in1=s.t1, op=ALU.add), reads=[s.t1U, s.t2U], writes=[s.t2U])
            P.dve(lambda e: e.tensor_tensor(out=s.ytmp, in0=s.t2, in1=s.zs, op=ALU.mult), reads=[s.t2U, s.zsU], writes=[s.ytmpU])
            for j in range(2):
                P.pe(lambda e, j=j: e.transpose(bank(g, 5)[:, j * 128:(j + 1) * 128], s.ytmp[:, j * 128:(j + 1) * 128], ident[:]),
                     reads=[s.ytmpU, g.cU], writes=[g.bU[5]])
            P.act(lambda e, blk=blk: e.copy(s.yTa[:, 2 * grp:2 * grp + 2, blk], bank(g, 5)[:, 0:256].rearrange("p (j t) -> p j t", j=2)),
                  reads=[g.bU[5]], writes=[s.yTaU[2 * grp][n], s.yTaU[2 * grp + 1][n]])
            if b == 0:
                P.dve(lambda e: e.tensor_copy(s.S, bank(g, 4)[:, 0:256]), reads=[g.bU[4]], writes=[s.SU])
            else:
                P.dve(lambda e, b=b: e.tensor_tensor(out=s.S.rearrange("p (h q) -> p h q", h=4), in0=s.S.rearrange("p (h q) -> p h q", h=4),
                                                     in1=bch(s.edl, 64, b), op=ALU.mult), reads=[s.SU] + du, writes=[s.SU])
                P.dve(lambda e: e.tensor_tensor(out=s.S, in0=s.S, in1=bank(g, 4)[:, 0:256], op=ALU.add), reads=[s.SU, g.bU[4]], writes=[s.SU])
            if b + 1 < NB:
                P.act(lambda e: e.copy(s.Sbf, s.S), reads=[s.SU], writes=[s.SbfU])
            backs.append(P.end_capture())
        P.replay_merged(fronts[0], [])
        for b in range(NB):
            P.replay_merged(fronts[b + 1] if b + 1 < NB else [], backs[b])

    P.barrier()
    for n in range(NT):
        sl = slice(n * 512, (n + 1) * 512)
        rms_rstd_tile(g, lambda k, sl=sl: s.yTa[:, k, sl], lambda k, n=n: [s.yTaU[k][n]], 8, 1024)
        P.dve(lambda e, sl=sl: e.tensor_copy(s.rstd_all[:, sl], g.rstd_t[:]), reads=[g.rstdU], writes=[s.rstdallU])
    cnt = 0

    def load_wo(m):
        wi = m % 2
        P.dma("pool", lambda e: e.dma_start(out=s.wo2[wi].rearrange("p c j -> p (c j)"), in_=g.d["ev_w_out_t"][ei, m, :, 0:1024]),
              f"s_wo{wi}", writes=[s.woU2[wi]])
        P.dve(lambda e: e.tensor_tensor(out=s.wo2[wi], in0=s.wo2[wi], in1=s.snw[:, 0:8].unsqueeze(2).to_broadcast([128, 8, 128]), op=ALU.mult),
              reads=[s.woU2[wi]] + sm, writes=[s.woU2[wi]])

    load_wo(0)
    for m in range(KC):
        if m + 1 < KC:
            load_wo(m + 1)
        wi = m % 2
        for n in range(NT):
            sl = slice(n * 512, (n + 1) * 512)
            bk = cnt % 2
            ti = cnt % 2
            cnt += 1
            for k in range(8):
                P.pe(lambda e, bk=bk, k=k, sl=sl: e.matmul(bank(g, bk), lhsT=s.wo2[wi][:, k, :], rhs=s.yTa[:, k, sl], start=(k == 0), stop=(k == 7)),
                     reads=[s.woU2[wi], s.yTaU[k][n]], writes=[g.bU[bk]])
            P.dve(lambda e, bk=bk, sl=sl, ti=ti: e.tensor_tensor(out=s.tmp2[ti], in0=bank(g, bk), in1=s.rstd_all[:, sl], op=ALU.mult),
                  reads=[g.bU[bk], s.rstdallU], writes=[s.tmpU2[ti]])
            P.pool(lambda e, m=m, sl=sl, ti=ti: e.tensor_tensor(out=g.hT[:, m, sl], in0=g.hT[:, m, sl], in1=s.tmp2[ti], op=ALU.add),
                   reads=[s.tmpU2[ti], g.hU[m][n]], writes=[g.hU[m][n]])


def even_attn(g, li):
    P, L, NB, NT = g.P, g.L, g.NB, g.NT
    ei = li // 2
    lam_init = 0.8 - 0.6 * math.exp(-0.3 * li)
    c = g.cst
    A = Carver(g)
    a = NS()
    a.corr = A.f(2048).rearrange("p (h d q) -> p h d q", h=8, d=2); a.corrU = P.unit()
    a.b31 = A.f(8); a.nb31 = A.f(8); a.bU_ = P.unit()
    a.lq = [A.f(64) for _ in range(4)]; a.lamU = P.unit()
    a.lam = A.f(8)
    a.slnw = A.f(128); a.slnwU = P.unit()
    a.w = [A.b(KC * 512).rearrange("p (k c) -> p k c", k=KC) for _ in range(2)]; a.wU = P.units(2)
    a.qT = [A.b(L) for _ in range(2)]; a.qTU = [P.unit() for _ in range(NT)]
    a.kT = A.b(L); a.kTU = [P.unit() for _ in range(NT)]
    a.v = A.b(NB * 132).rearrange("p (b c) -> p b c", c=132); a.vU = P.unit()
    a.gs = A.b(NB * 128).rearrange("p (b c) -> p b c", c=128); a.gsU = P.unit()
    a.PT = [[A.b(512) for _ in range(2)] for _ in range(2)]; a.PTU = [P.units(2), P.units(2)]
    a.yT = A.b(4 * L).rearrange("p (j t) -> p j t", j=4); a.yTU = P.units(4)
    a.wo = [A.b(512).rearrange("p (j c) -> p j c", j=4) for _ in range(2)]; a.woU = P.units(2)
    ident = c["ident"]

    P.dma("sp", lambda e: e.dma_start(out=a.corr.rearrange("p h d q -> p (h d q)"), in_=g.d["rel_biasD"]), "a_corr", writes=[a.corrU])
    P.dma("sp", lambda e: e.dma_start(out=a.b31, in_=g.d["rel_b31"].partition_broadcast(128)), "a_b31", writes=[a.bU_])
    P.dve(lambda e: e.tensor_scalar(out=a.nb31, in0=a.b31, scalar1=-1.0, scalar2=None, op0=ALU.mult), reads=[a.bU_], writes=[a.bU_])
    for h in range(8):
        P.act(lambda e, h=h: e.activation(out=a.corr[:, h], in_=a.corr[:, h], func=AF.Exp, bias=a.nb31[:, h:h + 1]),
              reads=[a.corrU, a.bU_], writes=[a.corrU])
    for i, nm in enumerate(("lambda_q1", "lambda_k1", "lambda_q2", "lambda_k2")):
        P.dma("sp", lambda e, i=i, nm=nm: e.dma_start(out=a.lq[i], in_=g.d[nm][ei].partition_broadcast(128)), "a_lam", writes=[a.lamU])
    lu = [a.lamU]
    P.dve(lambda e: e.tensor_tensor(out=a.lq[0], in0=a.lq[0], in1=a.lq[1], op=ALU.mult), reads=lu, writes=lu)
    P.dve(lambda e: e.tensor_tensor(out=a.lq[2], in0=a.lq[2], in1=a.lq[3], op=ALU.mult), reads=lu, writes=lu)
    P.dve(lambda e: e.tensor_reduce(out=a.lam[:, 0:1], in_=a.lq[0], axis=AX.X, op=ALU.add), reads=lu, writes=lu)
    P.dve(lambda e: e.tensor_reduce(out=a.lam[:, 1:2], in_=a.lq[2], axis=AX.X, op=ALU.add), reads=lu, writes=lu)
    P.act(lambda e: e.activation(out=a.lam[:, 2:4], in_=a.lam[:, 0:2], func=AF.Exp), reads=lu, writes=lu)
    P.dve(lambda e: e.tensor_tensor(out=a.lam[:, 4:5], in0=a.lam[:, 3:4], in1=a.lam[:, 2:3], op=ALU.subtract), reads=lu, writes=lu)
    P.dve(lambda e: e.tensor_scalar(out=a.lam[:, 5:6], in0=a.lam[:, 4:5], scalar1=-lam_init, scalar2=None, op0=ALU.add), reads=lu, writes=lu)
    P.dma("sp", lambda e: e.dma_start(out=a.slnw, in_=g.d["subln_w"][ei].partition_broadcast(128)), "a_slnw", writes=[a.slnwU])
    P.dve(lambda e: e.tensor_scalar(out=a.slnw, in0=a.slnw, scalar1=1.0 - lam_init, scalar2=None, op0=ALU.mult),
          reads=[a.slnwU], writes=[a.slnwU])
    P.dve(lambda e: e.memset(a.v, 1.0), writes=[a.vU])

    def load_w(h):
        i = h % 2
        P.dma("pool", lambda e: e.dma_start(out=a.w[i].rearrange("p k c -> p (k c)"), in_=g.d["ev_w_att"][ei, h]),
              f"a_w{i}", writes=[a.wU[i]])

    pc = [0]

    def project(h):
        wi = h % 2
        w = a.w[wi]
        for n in range(NT):
            sl = slice(n * 512, (n + 1) * 512)
            for which in range(2):
                bk = pc[0] % 2
                pc[0] += 1
                for k in range(KC):
                    P.pe(lambda e, bk=bk, k=k, sl=sl, which=which: e.matmul(bank(g, bk), lhsT=w[:, k, which * 128:(which + 1) * 128],
                                                                            rhs=g.uT[:, k, sl], start=(k == 0), stop=(k == KC - 1)),
                         reads=[g.uU[n], a.wU[wi]], writes=[g.bU[bk]])
                if which == 0:
                    for cc in range(2):
                        P.dve(lambda e, bk=bk, sl=sl, cc=cc: e.tensor_scalar(out=a.qT[cc][:, sl], in0=bank(g, bk), scalar1=c["maskq"][:, cc:cc + 1],
                                                                             scalar2=None, op0=ALU.mult),
                              reads=[g.bU[bk], g.cU], writes=[a.qTU[n]])
                else:
                    P.act(lambda e, bk=bk, sl=sl: e.copy(a.kT[:, sl], bank(g, bk)), reads=[g.bU[bk]], writes=[a.kTU[n]])
        for b in range(NB):
            bk = 2 + (b % 2)
            for k in range(KC):
                P.pe(lambda e, bk=bk, k=k, b=b: e.matmul(bank(g, bk)[:, 0:256], lhsT=g.uT[:, k, b * 128:(b + 1) * 128], rhs=w[:, k, 256:512],
                                                         start=(k == 0), stop=(k == KC - 1)),
                     reads=[g.uU[b // 4], a.wU[wi]], writes=[g.bU[bk]])
            P.act(lambda e, bk=bk, b=b: e.copy(a.v[:, b, 0:128], bank(g, bk)[:, 0:128]), reads=[g.bU[bk]], writes=[a.vU])
            P.act(lambda e, bk=bk, b=b: e.activation(out=a.gs[:, b, :], in_=bank(g, bk)[:, 128:256], func=AF.Silu),
                  reads=[g.bU[bk]], writes=[a.gsU])

    gc = [0]
    a.r2 = [A.f(8) for _ in range(2)]; a.rU2 = P.units(2)
    a.t2 = [A.f(128) for _ in range(2)]; a.tU2 = P.units(2)
    a.o2 = [A.f(128) for _ in range(2)]; a.oU2 = P.units(2)
    a.sqo2 = [A.f(128) for _ in range(2)]; a.sqoU2 = P.units(2)
    a.y2 = [A.f(128) for _ in range(2)]; a.yU2 = P.units(2)

    def attend(h):
        hj = h % 4
        groups = []
        for qb in range(NB):
            for gi in range(qb // 4 + 1):
                kbs = [kb for kb in range(gi * 4, gi * 4 + 4) if kb <= qb]
                groups.append((qb, gi, kbs, gc[0] % 2))
                gc[0] += 1

        def accb(qb, cc):
            return (2 + cc) if qb % 2 == 0 else cc

        def S_(grp):
            qb, gi, kbs, buf = grp
            qs = slice(qb * 128, (qb + 1) * 128)
            for cc in range(2):
                bk = 4 + 2 * cc + buf
                for j, kb in enumerate(kbs):
                    P.pe(lambda e, bk=bk, j=j, kb=kb, cc=cc: e.matmul(bank(g, bk)[:, j * 128:(j + 1) * 128],
                                                                      lhsT=a.kT[:, kb * 128:(kb + 1) * 128], rhs=a.qT[cc][:, qs],
                                                                      start=True, stop=True),
                         reads=[a.kTU[kb // 4], a.qTU[qb // 4]], writes=[g.bU[bk]])

        def E_(grp):
            qb, gi, kbs, buf = grp
            nv = len(kbs)
            for cc in range(2):
                bk = 4 + 2 * cc + buf
                pt = a.PT[cc][buf]
                ptu = a.PTU[cc][buf]
                P.act(lambda e, bk=bk, pt=pt, nv=nv: e.activation(out=pt[:, 0:nv * 128], in_=bank(g, bk)[:, 0:nv * 128], func=AF.Exp,
                                                                  scale=0.125, bias=a.b31[:, h:h + 1]),
                      reads=[g.bU[bk], a.bU_], writes=[ptu])
                for j, kb in enumerate(kbs):
                    Dd = qb - kb
                    if Dd <= 1:
                        P.dve(lambda e, pt=pt, j=j, Dd=Dd: e.tensor_tensor(out=pt[:, j * 128:(j + 1) * 128], in0=pt[:, j * 128:(j + 1) * 128],
                                                                           in1=a.corr[:, h, Dd, :], op=ALU.mult),
                              reads=[ptu, a.corrU], writes=[ptu])

        def PV_(grp):
            qb, gi, kbs, buf = grp
            for cc in range(2):
                pt = a.PT[cc][buf]
                ptu = a.PTU[cc][buf]
                ab = accb(qb, cc)
                for j, kb in enumerate(kbs):
                    P.pe(lambda e, ab=ab, pt=pt, j=j, kb=kb: e.matmul(bank(g, ab)[:, 0:129], lhsT=pt[:, j * 128:(j + 1) * 128],
                                                                      rhs=a.v[:, kb, 0:129], start=(kb == 0), stop=(kb == qb)),
                         reads=[ptu, a.vU], writes=[g.bU[ab]])

        def FIN_(qb):
            qs = slice(qb * 128, (qb + 1) * 128)
            pq = qb % 2
            b0, b1 = accb(qb, 0), accb(qb, 1)
            r, t_, o_, sqo, y_ = a.r2[pq], a.t2[pq], a.o2[pq], a.sqo2[pq], a.y2[pq]
            ru = [a.rU2[pq]]
            tU, oU, sqoU, yU = a.tU2[pq], a.oU2[pq], a.sqoU2[pq], a.yU2[pq]
            P.dve(lambda e: e.reciprocal(r[:, 0:1], bank(g, b0)[:, 128:129]), reads=[g.bU[b0]], writes=ru)
            P.dve(lambda e: e.reciprocal(r[:, 1:2], bank(g, b1)[:, 128:129]), reads=[g.bU[b1]], writes=ru)
            P.dve(lambda e: e.tensor_tensor(out=r[:, 2:3], in0=r[:, 1:2], in1=a.lam[:, 5:6], op=ALU.mult), reads=ru + lu, writes=ru)
            P.dve(lambda e: e.tensor_scalar(out=t_, in0=bank(g, b0)[:, 0:128], scalar1=r[:, 0:1], scalar2=None, op0=ALU.mult),
                  reads=[g.bU[b0]] + ru, writes=[tU])
            P.dve(lambda e: e.scalar_tensor_tensor(out=o_, in0=bank(g, b1)[:, 0:128], scalar=r[:, 2:3], in1=t_, op0=ALU.mult, op1=ALU.add),
                  reads=[g.bU[b1], tU] + ru, writes=[oU])
            P.dve(lambda e: e.tensor_tensor(out=sqo, in0=o_, in1=o_, op=ALU.mult), reads=[oU], writes=[sqoU])
            P.dve(lambda e: e.tensor_reduce(out=r[:, 3:4], in_=sqo, axis=AX.X, op=ALU.add), reads=[sqoU], writes=ru)
            P.act(lambda e: e.activation(out=r[:, 4:5], in_=r[:, 3:4], func=AF.Ln, scale=1.0 / 128, bias=EPS), reads=ru, writes=ru)
            P.act(lambda e: e.activation(out=r[:, 5:6], in_=r[:, 4:5], func=AF.Exp, scale=-0.5), reads=ru, writes=ru)
            P.dve(lambda e: e.scalar_tensor_tensor(out=y_, in0=o_, scalar=r[:, 5:6], in1=a.slnw, op0=ALU.mult, op1=ALU.mult),
                  reads=[oU, a.slnwU] + ru, writes=[yU])
            P.dve(lambda e: e.tensor_tensor(out=y_, in0=y_, in1=a.gs[:, qb, :], op=ALU.mult), reads=[yU, a.gsU], writes=[yU])
            P.pe(lambda e: e.transpose(bank(g, b0)[:, 256:384], y_, ident[:]), reads=[yU, g.cU], writes=[g.bU[b0]])
            P.act(lambda e: e.copy(a.yT[:, hj, qs], bank(g, b0)[:, 256:384]), reads=[g.bU[b0]], writes=[a.yTU[hj]])

        M = len(groups)
        S_(groups[0])
        pending_fin = None
        for i in range(M):
            if i + 1 < M:
                S_(groups[i + 1])
            E_(groups[i])
            PV_(groups[i])
            if pending_fin is not None:
                FIN_(pending_fin)
                pending_fin = None
            qb, gi, kbs, buf = groups[i]
            if kbs[-1] == qb:
                pending_fin = qb
        if pending_fin is not None:
            FIN_(pending_fin)

    oc = [0]

    def outproj(hg):
        for m in range(KC):
            wi = oc[0] % 2
            oc[0] += 1
            c0 = (8 + hg * 4) * 128
            P.dma("pool", lambda e, m=m, wi=wi, c0=c0: e.dma_start(out=a.wo[wi].rearrange("p j c -> p (j c)"),
                                                                  in_=g.d["ev_w_out_t"][ei, m, :, c0:c0 + 512]),
                  f"a_wo{wi}", writes=[a.woU[wi]])
            for n in range(NT):
                sl = slice(n * 512, (n + 1) * 512)
                bk = n % 2
                for j in range(4):
                    P.pe(lambda e, bk=bk, j=j, sl=sl, wi=wi: e.matmul(bank(g, bk), lhsT=a.wo[wi][:, j, :], rhs=a.yT[:, j, sl],
                                                                      start=(j == 0), stop=(j == 3)),
                         reads=[a.woU[wi], a.yTU[j]], writes=[g.bU[bk]])
                P.dve(lambda e, bk=bk, m=m, sl=sl: e.tensor_tensor(out=g.hT[:, m, sl], in0=g.hT[:, m, sl], in1=bank(g, bk), op=ALU.add),
                      reads=[g.bU[bk], g.hU[m][n]], writes=[g.hU[m][n]])

    load_w(0)
    for h in range(8):
        if h + 1 < 8:
            load_w(h + 1)
        project(h)
        attend(h)
        if h % 4 == 3:
            outproj(h // 4)


_CACHE = {}


def kernel(**inputs):
    x = np.ascontiguousarray(np.asarray(inputs["x"], dtype=np.float32))
    Bsz, L, _ = x.shape
    n_cores = 8
    nseq = Bsz // n_cores
    key = (L, nseq)
    if key not in _CACHE:
        _CACHE[key] = build(L, nseq, (0, 1, 2, 3))
    nc, _ = _CACHE[key]
    common = host_layout(inputs)
    in_maps = []
    for cidx in range(n_cores):
        m = dict(common)
        m["x"] = x[cidx * nseq:(cidx + 1) * nseq]
        in_maps.append(m)
    res = run_bass_kernel_spmd(nc, in_maps, core_ids=list(range(n_cores)))
    out = np.concatenate([np.asarray(r["out"]) for r in res.results], axis=0)
    return out.astype(np.float32)
```

```python
import math, contextlib
import numpy as np
import concourse.bass as bass
import concourse.mybir as mybir
from concourse.bass_utils import run_bass_kernel_spmd
from concourse.alu_op_type import AluOpType as ALU

F32 = mybir.dt.float32
BF16 = mybir.dt.bfloat16
AF = mybir.ActivationFunctionType
AX = mybir.AxisListType

D = 1024
KC = 8
EPS = 1e-6
DEPTH = 4
HG_W = 2048
ARF_N = 11392
ARB_N = 35840


class Unit:
    __slots__ = ("name", "lw", "rd")

    def __init__(self, name):
        self.name = name
        self.lw = None
        self.rd = []


class _Rec:
    def __init__(self):
        self.call = None

    def __getattr__(self, name):
        def f(*args, **kw):
            assert self.call is None
            self.call = (name, args, kw)
            return None
        return f


class Prog:
    ENGS = ("pe", "act", "dve", "pool", "sp")

    def __init__(self, nc):
        self.nc = nc
        self.ops = []
        self.nunits = 0
        self.last_eng = {}
        self.last_key = {}

    def unit(self, name=None):
        self.nunits += 1
        return Unit(name or f"u{self.nunits}")

    def units(self, n, name="u"):
        return [self.unit(f"{name}{i}") for i in range(n)]

    def capture(self):
        self._cap = []
        return self._cap

    def end_capture(self):
        c, self._cap = self._cap, None
        return c

    def replay_merged(self, A, B):
        na, nb = len(A), len(B)
        ia = ib = 0
        while ia < na or ib < nb:
            if ib >= nb or (ia < na and ia * nb <= ib * na):
                self.op(*A[ia]); ia += 1
            else:
                self.op(*B[ib]); ib += 1

    def op(self, eng, fn, reads=(), writes=(), dma_key=None, extra_deps=()):
        if fn is not None and not isinstance(fn, tuple):
            rec = _Rec()
            fn(rec)
            assert rec.call is not None
            fn = rec.call
        if getattr(self, "_cap", None) is not None:
            self._cap.append((eng, fn, tuple(reads), tuple(writes), dma_key, tuple(extra_deps)))
            return None
        idx = len(self.ops)
        deps = set(extra_deps)
        for u in reads:
            if u.lw is not None:
                deps.add(u.lw)
        for u in writes:
            if u.lw is not None:
                deps.add(u.lw)
            deps.update(u.rd)
        for u in reads:
            u.rd.append(idx)
        for u in writes:
            u.lw = idx
            u.rd = []
        deps.discard(idx)
        self.ops.append(dict(eng=eng, fn=fn, deps=deps, dma_key=dma_key))
        if fn is not None:
            if dma_key is None:
                self.last_eng[eng] = idx
            else:
                self.last_key[dma_key] = idx
        return idx

    def pe(self, fn, reads=(), writes=()):
        return self.op("pe", fn, reads, writes)

    def act(self, fn, reads=(), writes=()):
        return self.op("act", fn, reads, writes)

    def dve(self, fn, reads=(), writes=()):
        return self.op("dve", fn, reads, writes)

    def pool(self, fn, reads=(), writes=()):
        return self.op("pool", fn, reads, writes)

    def dma(self, eng, fn, key, reads=(), writes=()):
        return self.op(eng, fn, reads, writes, dma_key=key)

    def barrier(self):
        deps = set(self.last_eng.values()) | set(self.last_key.values())
        for e in self.ENGS:
            self.op(e, None, extra_deps=deps)

    def emit(self, final_wait_ops=()):
        nc = self.nc
        ops = self.ops
        n = len(ops)

        def skip(od, o):
            return (od["eng"] == "pe" and o["eng"] == "pe" and od["dma_key"] is None
                    and o["dma_key"] is None and o["fn"] is not None)

        needed = [False] * n
        for i, o in enumerate(ops):
            for d in o["deps"]:
                if skip(ops[d], o):
                    continue
                needed[d] = True
        for d in final_wait_ops:
            needed[d] = True
        chan_count = {}
        ev = [None] * n
        for i, o in enumerate(ops):
            if o["fn"] is None:
                continue
            if o["dma_key"] is not None:
                ch = ("dma", o["dma_key"])
                chan_count[ch] = chan_count.get(ch, 0) + 16
                ev[i] = (ch, chan_count[ch])
            elif needed[i]:
                ch = ("eng", o["eng"])
                chan_count[ch] = chan_count.get(ch, 0) + 1
                ev[i] = (ch, chan_count[ch])
        chans = sorted(chan_count.keys(), key=str)
        self.n_sems = len(chans)
        sems = {}
        stack = contextlib.ExitStack()
        for ci, ch in enumerate(chans):
            sems[ch] = stack.enter_context(nc.semaphore(f"s{ci}"))
        known = {e: {} for e in self.ENGS}
        clock = [None] * n
        streams = {e: [] for e in self.ENGS}
        for i, o in enumerate(ops):
            e = o["eng"]
            kn = known[e]
            wd = {}
            for d in sorted(o["deps"]):
                od = ops[d]
                if skip(od, o):
                    continue
                ch, v = ev[d]
                if kn.get(ch, 0) >= v:
                    continue
                for c2, v2 in clock[d].items():
                    if kn.get(c2, 0) < v2:
                        kn[c2] = v2
                wd[ch] = max(wd.get(ch, 0), v)
            ck = dict(kn)
            if ev[i] is not None:
                ch, v = ev[i]
                ck[ch] = v
            clock[i] = ck
            streams[e].append((list(wd.items()), o["fn"], ev[i]))
        final = [ev[d] for d in final_wait_ops]
        for ch, tot in chan_count.items():
            if ch[0] == "dma":
                final.append((ch, tot))
        self.sems, self.streams, self.final, self._stack = sems, streams, final, stack

    def run_block(self):
        nc = self.nc
        sems, streams, final = self.sems, self.streams, self.final
        with nc.Block() as block:
            def mk(ename):
                def body(eng):
                    for waits, fn, e in streams[ename]:
                        for ch, v in waits:
                            eng.wait_ge(sems[ch], v)
                        if fn is None:
                            continue
                        ins = getattr(eng, fn[0])(*fn[1], **fn[2])
                        if e is not None:
                            ins.then_inc(sems[e[0]], 16 if e[0][0] == "dma" else 1)
                    if ename == "sp":
                        for ch, v in final:
                            eng.wait_ge(sems[ch], v)
                return body
            block.tensor(mk("pe"))
            block.scalar(mk("act"))
            block.vector(mk("dve"))
            block.gpsimd(mk("pool"))
            block.sync(mk("sp"))
        self._stack.close()


def _t5_bucket(rel):
    n = np.maximum(rel, 0)
    max_exact = 16
    large = max_exact + (np.log(np.maximum(n, 1).astype(np.float32) / max_exact)
                         / math.log(128 / max_exact) * (32 - max_exact)).astype(np.int32)
    large = np.minimum(large, 31)
    return np.where(n < max_exact, n, large)


def host_consts():
    s = np.arange(128)[:, None]
    t = np.arange(128)[None, :]
    c = {}
    c["ident"] = np.eye(128, dtype=np.float32)
    c["ones"] = np.ones((128, 128), np.float32)
    c["triC"] = ((s <= t).astype(np.float32) - (s <= 63).astype(np.float32))
    c["triU"] = (s > t).astype(np.float32)
    c["triI"] = (s <= t).astype(np.float32)
    sel = np.zeros((128, 2), np.float32)
    sel[:64, 0] = 1.0
    sel[:, 1] = 1.0
    c["sel"] = sel
    c["mg16"] = np.eye(16, dtype=np.float32)
    mq = np.zeros((128, 2), np.float32)
    mq[:64, 0] = 1.0
    mq[64:, 1] = 1.0
    c["maskq"] = mq
    return c


def host_layout(inp):
    f = lambda a: np.ascontiguousarray(np.asarray(a, dtype=np.float32))
    m = dict(host_consts())
    m["final_norm_w"] = f(inp["final_norm_w"])
    m["norm_w_cols"] = f(np.asarray(inp["norm_w"]).reshape(4, 8, 128).transpose(0, 2, 1))
    owin = np.asarray(inp["odd_w_in"])
    t = owin.reshape(2, 8, 128, 4, 16, 128).transpose(0, 4, 2, 1, 3, 5)
    m["odd_w_in_t"] = f(t).reshape(2, 16, 128, 8 * 512)
    m["odd_w_out"] = f(inp["odd_w_out"])
    m["hgrn_lower_bounds"] = f(inp["hgrn_lower_bounds"])
    m["hgrn_norm_w"] = f(inp["hgrn_norm_w"])
    ew = np.asarray(inp["even_w_in"]).reshape(2, 8, 128, 7184)
    z = ew[..., 0:1024]; xs = ew[..., 1024:2048]; Bm = ew[..., 2048:2560]; Cm = ew[..., 2560:3072]
    dt = ew[..., 3072:3088]
    q = ew[..., 3088:4112]; kk = ew[..., 4112:5136]; v = ew[..., 5136:6160]; gg = ew[..., 6160:7184]
    ssd = np.concatenate([z.reshape(2, 8, 128, 4, 256), xs.reshape(2, 8, 128, 4, 256),
                          Bm.reshape(2, 8, 128, 4, 128), Cm.reshape(2, 8, 128, 4, 128)], axis=-1)
    m["ev_w_ssd"] = f(ssd.transpose(0, 3, 2, 1, 4)).reshape(2, 4, 128, 8 * 768)
    m["ev_w_dt"] = f(dt.transpose(0, 2, 1, 3)).reshape(2, 128, 8 * 16)
    att = np.concatenate([q.reshape(2, 8, 128, 8, 128), kk.reshape(2, 8, 128, 8, 128),
                          v.reshape(2, 8, 128, 8, 128), gg.reshape(2, 8, 128, 8, 128)], axis=-1)
    m["ev_w_att"] = f(att.transpose(0, 3, 2, 1, 4)).reshape(2, 8, 128, 8 * 512)
    wo = np.asarray(inp["even_w_out"]).reshape(2, 16, 128, 8, 128)
    m["ev_w_out_t"] = f(wo.transpose(0, 3, 2, 1, 4)).reshape(2, 8, 128, 16 * 128)
    m["conv_w_cols"] = f(np.asarray(inp["conv_w"]).reshape(2, 4, 16, 128).transpose(0, 3, 2, 1)).reshape(2, 128, 64)
    m["conv_b_cols"] = f(np.asarray(inp["conv_b"]).reshape(2, 16, 128).transpose(0, 2, 1))
    for nm in ("dt_bias", "A_log", "D_skip", "lambda_q1", "lambda_k1", "lambda_q2", "lambda_k2", "subln_w"):
        m[nm] = f(inp[nm])
    m["ssd_norm_w_cols"] = f(np.asarray(inp["ssd_norm_w"]).reshape(2, 8, 128).transpose(0, 2, 1))
    rb = np.asarray(inp["rel_bias"], dtype=np.float32)
    kpos = np.arange(128)[:, None]
    qpos = np.arange(128)[None, :]
    bd = np.empty((128, 8, 2, 128), np.float32)
    for Dd in range(2):
        rel = qpos - kpos + 128 * Dd
        bidx = _t5_bucket(rel)
        g_ = rb[bidx]
        g_ = np.where((rel >= 0)[:, :, None], g_, np.float32(-30000.0))
        bd[:, :, Dd, :] = g_.transpose(0, 2, 1)
    m["rel_biasD"] = f(bd).reshape(128, 8 * 2 * 128)
    m["rel_b31"] = f(rb[31])
    return m


class NS:
    pass


class Carver:
    def __init__(self, g):
        self.g = g
        self.fo = 0
        self.bo = 0

    def f(self, n):
        ap = self.g.arf[:, self.fo:self.fo + n]
        self.fo += (n + 7) // 8 * 8
        assert self.fo <= ARF_N, ("ARF overflow", self.fo)
        return ap

    def b(self, n):
        ap = self.g.arb[:, self.bo:self.bo + n]
        self.bo += (n + 15) // 16 * 16
        assert self.bo <= ARB_N, ("ARB overflow", self.bo)
        return ap


def bank(g, i):
    return g.ps[:, i, :]


def build(L=2048, NSEQ=2, layers=(0, 1, 2, 3)):
    nc = bass.Bass("TRN2", target_bir_lowering=False, dynamic_dma_scratch_size=4096)
    NT, NB = L // 512, L // 128
    g = NS()
    g.nc, g.L, g.NT, g.NB = nc, L, NT, NB
    dr = lambda name, shape, kind="ExternalInput": nc.dram_tensor(name, list(shape), F32, kind=kind).ap()
    g.x_d = dr("x", [NSEQ, L, D])
    g.out_d = dr("out", [NSEQ, L, D], "ExternalOutput")
    g.d = {}
    shapes = {
        "final_norm_w": [D], "norm_w_cols": [DEPTH, 128, KC],
        "ident": [128, 128], "ones": [128, 128], "triC": [128, 128], "triU": [128, 128], "triI": [128, 128],
        "sel": [128, 2], "mg16": [16, 16], "maskq": [128, 2],
        "odd_w_in_t": [2, 16, 128, KC * 512], "odd_w_out": [2, HG_W, D], "hgrn_lower_bounds": [DEPTH, HG_W],
        "hgrn_norm_w": [2, 128],
        "ev_w_ssd": [2, 4, 128, 8 * 768], "ev_w_dt": [2, 128, 8 * 16], "ev_w_att": [2, 8, 128, 8 * 512],
        "ev_w_out_t": [2, 8, 128, 16 * 128], "conv_w_cols": [2, 128, 64], "conv_b_cols": [2, 128, 16],
        "dt_bias": [2, 16], "A_log": [2, 16], "D_skip": [2, 16], "lambda_q1": [2, 64], "lambda_k1": [2, 64],
        "lambda_q2": [2, 64], "lambda_k2": [2, 64], "subln_w": [2, 128], "ssd_norm_w_cols": [2, 128, 8],
        "rel_biasD": [128, 8 * 2 * 128], "rel_b31": [8],
    }
    for nm, shp in shapes.items():
        g.d[nm] = dr(nm, shp)
    g.in_names = ["x"] + list(shapes.keys())

    es = contextlib.ExitStack()
    sb = lambda name, shape, dt=F32: es.enter_context(nc.sbuf_tensor(name, list(shape), dt))
    P = Prog(nc)
    g.P = P
    g.hT = sb("hT", [128, KC, L]); g.hU = [[P.unit(f"h{k}_{n}") for n in range(NT)] for k in range(KC)]
    g.uT = sb("uT", [128, KC, L], BF16); g.uU = [P.unit(f"u{n}") for n in range(NT)]
    g.cst = {}
    g.cU = P.unit("consts")
    for nm in ("ident", "ones", "triI"):
        g.cst[nm] = sb("c_" + nm, [128, 128])
    g.cst["sel"] = sb("c_sel", [128, 2])
    g.cst["maskq"] = sb("c_maskq", [128, 2])
    g.cst["mg16"] = sb("c_mg16", [16, 16])
    g.nwc = sb("nwc", [128, DEPTH, KC])
    g.stat = sb("stat", [128, 16]); g.statU = P.unit("stat")
    g.sq = [sb(f"sq{i}", [128, 512], BF16) for i in range(2)]; g.sqU = P.units(2, "sq")
    g.onesb = sb("onesb", [128, 128], BF16); g.onesbU = P.unit("onesb")
    g.rstd_t2 = [sb(f"rstd_t{i}", [128, 512]) for i in range(2)]; g.rstdU2 = P.units(2, "rstd_t")
    g.rstd_t, g.rstdU = g.rstd_t2[0], g.rstdU2[0]
    g.arf = sb("arf", [128, ARF_N])
    g.arb = sb("arb", [128, ARB_N], BF16)
    g.ps = es.enter_context(nc.psum_tensor("ps", [128, 8, 512], F32))
    g.bU = P.units(8, "bank")

    for nm in ("ident", "ones", "triI", "sel", "maskq", "mg16"):
        P.dma("sp", lambda e, nm=nm: e.dma_start(out=g.cst[nm][:], in_=g.d[nm]), "c_" + nm, writes=[g.cU])
    P.dma("sp", lambda e: e.dma_start(out=g.nwc[:], in_=g.d["norm_w_cols"].rearrange("l p k -> p l k")), "c_nwc", writes=[g.cU])

    P.dve(lambda e: e.tensor_copy(g.onesb[:], g.cst["ones"][:]), reads=[g.cU], writes=[g.onesbU])
    out_ops = []
    for s in range(NSEQ):
        P.barrier()
        load_x(g, s)
        P.barrier()
        for li in layers:
            rms_to_uT(g, li)
            if li % 2 == 1:
                odd_layer(g, li)
            else:
                even_ssd(g, li)
                P.barrier()
                even_attn(g, li)
            P.barrier()
        out_ops += final_norm_store(g, s)
    P.emit(final_wait_ops=out_ops[-4:])
    P.run_block()
    es.close()
    return nc, P


def load_x(g, s):
    P, NT = g.P, g.NT
    A = Carver(g)
    xst = A.f(4096).rearrange("p (b d) -> p b d", b=4)
    xU = P.unit("xst")
    ident = g.cst["ident"]
    for n in range(NT):
        src = g.x_d[s, n * 512:(n + 1) * 512, :].rearrange("(b p) d -> p b d", p=128)
        P.dma("sp", lambda e, src=src: e.dma_start(out=xst, in_=src), "xst", writes=[xU])
        for k in range(KC):
            bk = k % 8
            for b in range(4):
                P.pe(lambda e, bk=bk, b=b, k=k: e.transpose(
                    bank(g, bk)[:, b * 128:(b + 1) * 128], xst[:, b, k * 128:(k + 1) * 128], ident[:]),
                    reads=[xU, g.cU], writes=[g.bU[bk]])
            if k % 2 == 0:
                P.dve(lambda e, bk=bk, k=k, n=n: e.tensor_copy(g.hT[:, k, n * 512:(n + 1) * 512], bank(g, bk)),
                      reads=[g.bU[bk]], writes=[g.hU[k][n]])
            else:
                P.act(lambda e, bk=bk, k=k, n=n: e.copy(g.hT[:, k, n * 512:(n + 1) * 512], bank(g, bk)),
                      reads=[g.bU[bk]], writes=[g.hU[k][n]])


def final_norm_store(g, s):
    P, NB = g.P, g.NB
    A = Carver(g)
    fnw = A.f(D); fnwU = P.unit("fnw")
    ost = [A.f(D) for _ in range(2)]; ostU = P.units(2, "ost")
    junk = A.f(1024); junkU = P.unit("junk")
    ident = g.cst["ident"]
    P.dma("sp", lambda e: e.dma_start(out=fnw, in_=g.d["final_norm_w"].partition_broadcast(128)), "fnw", writes=[fnwU])
    outs = []
    for b in range(NB):
        n = b // 4
        oi = b % 2
        for k in range(KC):
            bk = k // 4
            P.pe(lambda e, bk=bk, k=k, b=b: e.transpose(
                bank(g, bk)[:, (k % 4) * 128:(k % 4 + 1) * 128], g.hT[:, k, b * 128:(b + 1) * 128], ident[:]),
                reads=[g.hU[k][n], g.cU], writes=[g.bU[bk]])
        for half in range(2):
            P.act(lambda e, half=half: e.activation(
                out=junk[:, half * 512:(half + 1) * 512], in_=bank(g, half), func=AF.Square,
                accum_out=g.stat[:, half:half + 1]),
                reads=[g.bU[half]], writes=[junkU, g.statU])
        P.dve(lambda e: e.tensor_tensor(out=g.stat[:, 2:3], in0=g.stat[:, 0:1], in1=g.stat[:, 1:2], op=ALU.add),
              reads=[g.statU], writes=[g.statU])
        P.act(lambda e: e.activation(out=g.stat[:, 3:4], in_=g.stat[:, 2:3], func=AF.Ln, scale=1.0 / D, bias=EPS),
              reads=[g.statU], writes=[g.statU])
        P.act(lambda e: e.activation(out=g.stat[:, 4:5], in_=g.stat[:, 3:4], func=AF.Exp, scale=-0.5),
              reads=[g.statU], writes=[g.statU])
        for half in range(2):
            P.dve(lambda e, half=half, oi=oi: e.scalar_tensor_tensor(
                out=ost[oi][:, half * 512:(half + 1) * 512], in0=bank(g, half), scalar=g.stat[:, 4:5],
                in1=fnw[:, half * 512:(half + 1) * 512], op0=ALU.mult, op1=ALU.mult),
                reads=[g.bU[half], g.statU, fnwU], writes=[ostU[oi]])
        o = P.dma("sp", lambda e, oi=oi, s=s, b=b: e.dma_start(out=g.out_d[s, b * 128:(b + 1) * 128, :], in_=ost[oi]),
                  f"ost{oi}", reads=[ostU[oi]])
        outs.append(o)
    return outs


def rms_rstd_tile(g, src_fn, reads_fn, nchunks, dim, slot=0):
    P = g.P
    ones = g.cst["ones"]
    g.rstd_t, g.rstdU = g.rstd_t2[slot], g.rstdU2[slot]
    _bank7 = 7 - slot
    for k in range(nchunks):
        i = k % 2
        P.act(lambda e, i=i, k=k: e.activation(out=g.sq[i][:], in_=src_fn(k), func=AF.Square),
              reads=reads_fn(k), writes=[g.sqU[i]])
        P.pe(lambda e, i=i, k=k: e.matmul(bank(g, _bank7), lhsT=g.onesb[:], rhs=g.sq[i][:], start=(k == 0), stop=(k == nchunks - 1)),
             reads=[g.sqU[i], g.onesbU], writes=[g.bU[_bank7]])
    P.act(lambda e: e.activation(out=g.rstd_t[:], in_=bank(g, _bank7), func=AF.Ln, scale=1.0 / dim, bias=EPS),
          reads=[g.bU[_bank7]], writes=[g.rstdU])
    P.act(lambda e: e.activation(out=g.rstd_t[:], in_=g.rstd_t[:], func=AF.Exp, scale=-0.5),
          reads=[g.rstdU], writes=[g.rstdU])


def rms_to_uT(g, li):
    P, NT = g.P, g.NT
    for n in range(NT):
        sl = slice(n * 512, (n + 1) * 512)
        rms_rstd_tile(g, lambda k, sl=sl: g.hT[:, k, sl], lambda k, n=n: [g.hU[k][n]], KC, D, slot=n % 2)
        for k in range(KC):
            P.dve(lambda e, k=k, sl=sl: e.scalar_tensor_tensor(
                out=g.uT[:, k, sl], in0=g.hT[:, k, sl], scalar=g.nwc[:, li, k:k + 1], in1=g.rstd_t[:],
                op0=ALU.mult, op1=ALU.mult),
                reads=[g.hU[k][n], g.rstdU, g.cU], writes=[g.uU[n]])


def odd_layer(g, li):
    P, L, NB, NT = g.P, g.L, g.NB, g.NT
    oi = li // 2
    A = Carver(g)
    o = NS()
    c = g.cst
    f3 = lambda: A.f(512).rearrange("p (b d) -> p b d", b=4)
    f16 = lambda: A.f(NB * 128).rearrange("p (b d) -> p b d", d=128)
    o.fall = f16(); o.fallU = [P.unit() for _ in range(NT)]
    o.kkall = f16(); o.kkallU = [P.unit() for _ in range(NT)]
    o.qsall = f16(); o.qsallU = [P.unit() for _ in range(NT)]
    o.e13 = A.f(1024).rearrange("p (t b d) -> p t b d", t=2, b=4); o.e13U = P.unit()
    o.e2 = f3(); o.e2U = P.unit()
    o.kt = o.e2; o.ktU = o.e2U
    o.lbr = f3(); o.lbrU = P.unit()
    o.lbh = A.f(128); o.omlh = A.f(128); o.den = A.f(128); o.lbU = P.unit()
    o.eb = [A.f(NB * 2).rearrange("p (b t) -> p b t", t=2) for _ in range(2)]
    o.S = A.f(128); o.SU = P.unit()
    o.junk2 = A.f(128); o.junk2U = P.unit()
    o.oall = A.f(NB * 128).rearrange("p (b d) -> p b d", d=128); o.oallU = P.unit()
    o.ssall = A.f(NB); o.rsall = A.f(NB); o.ssU = P.unit()
    o.hnw = A.f(128); o.hnwU = P.unit()
    o.triC = A.f(128); o.triU = A.f(128); o.triUU = P.unit()
    o.bst = A.f(8); o.bstU = P.unit()
    o.w = [A.b(KC * 512).rearrange("p (k c) -> p k c", k=KC) for _ in range(2)]; o.wU = P.units(2)
    o.wout = A.b(2 * D).rearrange("p (j m) -> p j m", j=2); o.woutU = P.units(2)
    o.qT = [A.b(L) for _ in range(2)]
    o.kT = [A.b(L) for _ in range(2)]
    hb3 = lambda: A.b(NB * 128).rearrange("p (b d) -> p b d", d=128)
    o.kh = [hb3() for _ in range(2)]
    o.v = [hb3() for _ in range(2)]
    o.gs = [hb3() for _ in range(2)]
    o.hbU = [[[P.unit() for _ in range(NT)] for _ in range(6)] for _ in range(2)]
    o.attm = [A.b(128) for _ in range(2)]; o.attmU = P.units(2)
    o.Sb2 = [A.b(128) for _ in range(2)]; o.SbU2 = P.units(2)
    o.yT = A.b(2 * L).rearrange("p (j t) -> p j t", j=2); o.yTU = P.units(2)
    QT, KT, KH, VV, GS, EB = range(6)

    P.dma("sp", lambda e: e.dma_start(out=o.hnw, in_=g.d["hgrn_norm_w"][oi].partition_broadcast(128)), "o_hnw", writes=[o.hnwU])
    P.dma("sp", lambda e: e.dma_start(out=o.triC, in_=g.d["triC"]), "o_tri", writes=[o.triUU])
    P.dma("sp", lambda e: e.dma_start(out=o.triU, in_=g.d["triU"]), "o_tri", writes=[o.triUU])
    for i in range(2):
        P.dve(lambda e, i=i: e.memset(o.attm[i], 0.0), writes=[o.attmU[i]])

    def load_w(h):
        i = h % 2
        P.dma("pool", lambda e: e.dma_start(out=o.w[i].rearrange("p k c -> p (k c)"), in_=g.d["odd_w_in_t"][oi, h]),
              f"o_w{i}", writes=[o.wU[i]])

    def head_lb(h):
        hs = slice(h * 128, (h + 1) * 128)
        P.dma("sp", lambda e: e.dma_start(out=o.lbr, in_=g.d["hgrn_lower_bounds"][:, hs].partition_broadcast(128)),
              "o_lbr", writes=[o.lbrU])
        P.act(lambda e: e.activation(out=o.lbr, in_=o.lbr, func=AF.Exp), reads=[o.lbrU], writes=[o.lbrU])
        P.dve(lambda e: e.tensor_tensor(out=o.den, in0=o.lbr[:, 0, :], in1=o.lbr[:, 1, :], op=ALU.add), reads=[o.lbrU], writes=[o.lbU])
        P.dve(lambda e: e.tensor_tensor(out=o.den, in0=o.den, in1=o.lbr[:, 2, :], op=ALU.add), reads=[o.lbrU, o.lbU], writes=[o.lbU])
        P.dve(lambda e: e.tensor_tensor(out=o.den, in0=o.den, in1=o.lbr[:, 3, :], op=ALU.add), reads=[o.lbrU, o.lbU], writes=[o.lbU])
        P.dve(lambda e: e.reciprocal(o.den, o.den), reads=[o.lbU], writes=[o.lbU])
        if li == 1:
            P.dve(lambda e: e.tensor_tensor(out=o.lbh, in0=o.lbr[:, 1, :], in1=o.den, op=ALU.mult), reads=[o.lbrU, o.lbU], writes=[o.lbU])
        else:
            P.dve(lambda e: e.tensor_tensor(out=o.lbh, in0=o.lbr[:, 1, :], in1=o.lbr[:, 2, :], op=ALU.add), reads=[o.lbrU, o.lbU], writes=[o.lbU])
            for j in range(3, li + 1):
                P.dve(lambda e, j=j: e.tensor_tensor(out=o.lbh, in0=o.lbh, in1=o.lbr[:, j, :], op=ALU.add), reads=[o.lbrU, o.lbU], writes=[o.lbU])
            P.dve(lambda e: e.tensor_tensor(out=o.lbh, in0=o.lbh, in1=o.den, op=ALU.mult), reads=[o.lbU], writes=[o.lbU])
        P.dve(lambda e: e.tensor_scalar(out=o.omlh, in0=o.lbh, scalar1=-1.0, scalar2=1.0, op0=ALU.mult, op1=ALU.add),
              reads=[o.lbU], writes=[o.lbU])

    def stageA1(h, n):
        hb = h % 2
        wi = h % 2
        U = o.hbU[hb]
        bs = slice(n * 4, (n + 1) * 4)
        for b in range(4):
            tb = n * 4 + b
            for k in range(KC):
                P.pe(lambda e, b=b, tb=tb, k=k: e.matmul(bank(g, b), lhsT=g.uT[:, k, tb * 128:(tb + 1) * 128], rhs=o.w[wi][:, k, :],
                                                         start=(k == 0), stop=(k == KC - 1)),
                     reads=[g.uU[n], o.wU[wi]], writes=[g.bU[b]])
        pj = g.ps[:, 0:4, :]
        pb = [g.bU[0], g.bU[1], g.bU[2], g.bU[3]]
        bc4 = lambda t: t.unsqueeze(1).to_broadcast([128, 4, 128])
        P.act(lambda e: e.activation(out=o.fall[:, bs, :], in_=pj[:, :, 128:256], func=AF.Sigmoid), reads=pb, writes=[o.fallU[n]])
        P.act(lambda e: e.activation(out=o.qsall[:, bs, :], in_=pj[:, :, 0:128], func=AF.Silu), reads=pb, writes=[o.qsallU[n]])
        P.act(lambda e: e.activation(out=o.gs[hb][:, bs, :], in_=pj[:, :, 384:512], func=AF.Silu), reads=pb, writes=[U[GS][n]])
        P.act(lambda e: e.copy(o.v[hb][:, bs, :], pj[:, :, 256:384]), reads=pb, writes=[U[VV][n]])
        P.dve(lambda e: e.tensor_tensor(out=o.fall[:, bs, :], in0=o.fall[:, bs, :], in1=bc4(o.omlh), op=ALU.mult),
              reads=[o.fallU[n], o.lbU], writes=[o.fallU[n]])
        P.dve(lambda e: e.tensor_tensor(out=o.fall[:, bs, :], in0=o.fall[:, bs, :], in1=bc4(o.lbh), op=ALU.add),
              reads=[o.fallU[n], o.lbU], writes=[o.fallU[n]])
        P.pool(lambda e: e.tensor_scalar(out=o.kkall[:, bs, :], in0=o.fall[:, bs, :], scalar1=-1.0, scalar2=1.0, op0=ALU.mult, op1=ALU.add),
               reads=[o.fallU[n]], writes=[o.kkallU[n]])

    def stageAmid(h):
        P.act(lambda e: e.activation(out=o.fall, in_=o.fall, func=AF.Ln), reads=o.fallU + o.kkallU, writes=o.fallU)

    def stageA2(h, n):
        hb = h % 2
        U = o.hbU[hb]
        bs = slice(n * 4, (n + 1) * 4)
        logf = o.fall[:, bs, :]
        kk = o.kkall[:, bs, :]
        qs = o.qsall[:, bs, :]
        lu = [o.fallU[n]]
        lf4 = logf.rearrange("p b d -> p (b d)")
        P.pe(lambda e: e.matmul(bank(g, 0), lhsT=o.triC, rhs=lf4, start=True, stop=True), reads=lu + [o.triUU], writes=[g.bU[0]])
        P.pe(lambda e: e.matmul(bank(g, 1), lhsT=o.triU, rhs=lf4, start=True, stop=True), reads=lu + [o.triUU], writes=[g.bU[1]])
        for b in range(4):
            P.pe(lambda e, b=b: e.matmul(bank(g, 2)[:, b * 2:b * 2 + 2], lhsT=logf[:, b, :], rhs=c["sel"][:], start=True, stop=True),
                 reads=lu + [g.cU], writes=[g.bU[2]])
        P.act(lambda e: e.activation(out=o.eb[hb][:, bs, :], in_=bank(g, 2)[:, 0:8].rearrange("p (b t) -> p b t", t=2), func=AF.Exp),
              reads=[g.bU[2]], writes=[U[EB][n]])
        P.act(lambda e: e.activation(out=o.e13, in_=g.ps[:, 0:2, :].rearrange("p t (b d) -> p t b d", b=4), func=AF.Exp),
              reads=[g.bU[0], g.bU[1]], writes=[o.e13U])
        P.act(lambda e: e.activation(out=o.e2, in_=bank(g, 0).rearrange("p (b d) -> p b d", b=4), func=AF.Exp, scale=-1.0),
              reads=[g.bU[0]], writes=[o.e2U])
        P.dve(lambda e: e.tensor_tensor(out=qs, in0=qs, in1=o.e13[:, 0], op=ALU.mult), reads=[o.qsallU[n], o.e13U], writes=[o.qsallU[n]])
        P.pool(lambda e: e.tensor_tensor(out=o.e2, in0=kk, in1=o.e2, op=ALU.mult), reads=[o.kkallU[n], o.e2U], writes=[o.e2U])
        P.dve(lambda e: e.tensor_tensor(out=o.kh[hb][:, bs, :], in0=kk, in1=o.e13[:, 1], op=ALU.mult),
              reads=[o.kkallU[n], o.e13U], writes=[U[KH][n]])
        idf = c["ident"]
        for b in range(4):
            P.pe(lambda e, b=b: e.transpose(bank(g, 3)[:, b * 128:(b + 1) * 128], qs[:, b, :], idf[:]),
                 reads=[o.qsallU[n], g.cU], writes=[g.bU[3]])
        for b in range(4):
            P.pe(lambda e, b=b: e.transpose(bank(g, 2)[:, b * 128:(b + 1) * 128], o.kt[:, b, :], idf[:]),
                 reads=[o.ktU, g.cU], writes=[g.bU[2]])
        P.act(lambda e: e.copy(o.qT[hb][:, n * 512:(n + 1) * 512], bank(g, 3)), reads=[g.bU[3]], writes=[U[QT][n]])
        P.act(lambda e: e.copy(o.kT[hb][:, n * 512:(n + 1) * 512], bank(g, 2)), reads=[g.bU[2]], writes=[U[KT][n]])

    def stageAall(h):
        for n in range(NT):
            stageA1(h, n)
        stageAmid(h)
        for n in range(NT):
            stageA2(h, n)

    def stageB(h):
        hb = h % 2
        U = o.hbU[hb]

        def att(tb):
            n = tb // 4
            ts = slice(tb * 128, (tb + 1) * 128)
            bk = 5 + 2 * (tb % 2)
            P.pe(lambda e: e.matmul(bank(g, bk)[:, 64:128], lhsT=o.kT[hb][:, ts],
                                    rhs=o.qT[hb][:, tb * 128 + 64:(tb + 1) * 128], start=True, stop=True),
                 reads=[U[QT][n], U[KT][n]], writes=[g.bU[bk]])
            P.pe(lambda e: e.matmul(bank(g, bk)[0:64, 0:64], lhsT=o.kT[hb][:, tb * 128:tb * 128 + 64],
                                    rhs=o.qT[hb][:, tb * 128:tb * 128 + 64], start=True, stop=True),
                 reads=[U[QT][n], U[KT][n]], writes=[g.bU[bk]])

        def mask(tb):
            ai = tb % 2
            bk = 5 + 2 * (tb % 2)
            P.dve(lambda e: e.tensor_tensor(out=o.attm[ai][:, 64:128], in0=bank(g, bk)[:, 64:128], in1=c["triI"][:, 64:128], op=ALU.mult),
                  reads=[g.bU[bk], g.cU], writes=[o.attmU[ai]])
            P.dve(lambda e: e.tensor_tensor(out=o.attm[ai][0:64, 0:64], in0=bank(g, bk)[0:64, 0:64], in1=c["triI"][0:64, 0:64], op=ALU.mult),
                  reads=[g.bU[bk], g.cU], writes=[o.attmU[ai]])

        att(0)
        mask(0)
        for tb in range(NB):
            n = tb // 4
            ts = slice(tb * 128, (tb + 1) * 128)
            ai = tb % 2
            si = tb % 2
            P.pe(lambda e: e.matmul(bank(g, 6)[:, 128:256], lhsT=o.kh[hb][:, tb, :], rhs=o.v[hb][:, tb, :], start=True, stop=True),
                 reads=[U[KH][n], U[VV][n]], writes=[g.bU[6]])
            if tb + 1 < NB:
                att(tb + 1)
            P.pe(lambda e: e.matmul(bank(g, 4)[:, 0:128], lhsT=o.attm[ai], rhs=o.v[hb][:, tb, :], start=True, stop=(tb == 0)),
                 reads=[o.attmU[ai], U[VV][n]], writes=[g.bU[4]])
            if tb > 0:
                P.pe(lambda e: e.matmul(bank(g, 4)[:, 0:128], lhsT=o.qT[hb][:, ts], rhs=o.Sb2[si], start=False, stop=True),
                     reads=[o.SbU2[si], U[QT][n]], writes=[g.bU[4]])
            if tb == 0:
                P.dve(lambda e: e.tensor_copy(o.S, bank(g, 6)[:, 128:256]), reads=[g.bU[6]], writes=[o.SU])
            else:
                P.dve(lambda e: e.scalar_tensor_tensor(out=o.S, in0=o.S, scalar=o.eb[hb][:, tb, 1:2], in1=bank(g, 6)[:, 128:256],
                                                       op0=ALU.mult, op1=ALU.add),
                      reads=[o.SU, g.bU[6], U[EB][n]], writes=[o.SU])
            if tb + 1 < NB:
                nn = (tb + 1) // 4
                sn = (tb + 1) % 2
                P.dve(lambda e: e.tensor_scalar(out=o.Sb2[sn], in0=o.S, scalar1=o.eb[hb][:, tb + 1, 0:1], scalar2=None, op0=ALU.mult),
                      reads=[o.SU, U[EB][nn]], writes=[o.SbU2[sn]])
                mask(tb + 1)
            P.act(lambda e: e.copy(o.oall[:, tb, :], bank(g, 4)[:, 0:128]), reads=[g.bU[4]], writes=[o.oallU])
            P.act(lambda e: e.activation(out=o.junk2, in_=bank(g, 4)[:, 0:128], func=AF.Square, accum_out=o.ssall[:, tb:tb + 1]),
                  reads=[g.bU[4]], writes=[o.junk2U, o.ssU])

    def stageC(h):
        hb = h % 2
        U = o.hbU[hb]
        hj = h % 2
        P.act(lambda e: e.activation(out=o.rsall, in_=o.ssall, func=AF.Ln, scale=1.0 / 128, bias=EPS), reads=[o.ssU], writes=[o.ssU])
        P.act(lambda e: e.activation(out=o.rsall, in_=o.rsall, func=AF.Exp, scale=-0.5), reads=[o.ssU], writes=[o.ssU])
        P.dve(lambda e: e.tensor_tensor(out=o.oall, in0=o.oall, in1=o.rsall.unsqueeze(2).to_broadcast([128, NB, 128]), op=ALU.mult),
              reads=[o.oallU, o.ssU], writes=[o.oallU])
        P.dve(lambda e: e.tensor_tensor(out=o.oall, in0=o.oall, in1=o.hnw.unsqueeze(1).to_broadcast([128, NB, 128]), op=ALU.mult),
              reads=[o.oallU, o.hnwU], writes=[o.oallU])
        P.dve(lambda e: e.tensor_tensor(out=o.oall, in0=o.oall, in1=o.gs[hb], op=ALU.mult),
              reads=[o.oallU] + [U[GS][n] for n in range(NT)], writes=[o.oallU])
        for n in range(NT):
            bk = 4 + (n % 4)
            for b in range(4):
                P.pe(lambda e, bk=bk, b=b, n=n: e.transpose(bank(g, bk)[:, b * 128:(b + 1) * 128], o.oall[:, n * 4 + b, :], c["ident"][:]),
                     reads=[o.oallU, g.cU], writes=[g.bU[bk]])
            P.act(lambda e, bk=bk, n=n: e.copy(o.yT[:, hj, n * 512:(n + 1) * 512], bank(g, bk)), reads=[g.bU[bk]], writes=[o.yTU[hj]])

    def outproj(hp):
        for j in range(2):
            src = g.d["odd_w_out"][oi, (hp * 2 + j) * 128:(hp * 2 + j + 1) * 128, :]
            P.dma("pool", lambda e, j=j, src=src: e.dma_start(out=o.wout[:, j, :], in_=src), f"o_wout{j}", writes=[o.woutU[j]])
        cnt = 0
        for m in range(KC):
            for n in range(NT):
                bk = 4 + cnt % 4
                cnt += 1
                for j in range(2):
                    P.pe(lambda e, bk=bk, m=m, n=n, j=j: e.matmul(bank(g, bk), lhsT=o.wout[:, j, m * 128:(m + 1) * 128],
                                                                     rhs=o.yT[:, j, n * 512:(n + 1) * 512], start=(j == 0), stop=(j == 1)),
                         reads=[o.woutU[j], o.yTU[j]], writes=[g.bU[bk]])
                P.dve(lambda e, bk=bk, m=m, n=n: e.tensor_tensor(out=g.hT[:, m, n * 512:(n + 1) * 512], in0=g.hT[:, m, n * 512:(n + 1) * 512],
                                                                   in1=bank(g, bk), op=ALU.add),
                      reads=[g.bU[bk], g.hU[m][n]], writes=[g.hU[m][n]])

    load_w(0)
    load_w(1)
    head_lb(0)
    stageAall(0)
    for h in range(16):
        P.capture()
        stageB(h)
        stageC(h)
        if h % 2 == 1:
            outproj(h // 2)
        LB = P.end_capture()
        P.capture()
        if h + 1 < 16:
            if h + 2 < 16:
                load_w(h + 2)
            head_lb(h + 1)
            stageAall(h + 1)
        LA = P.end_capture()
        P.replay_merged(LA, LB)


def even_ssd(g, li):
    P, L, NB, NT = g.P, g.L, g.NB, g.NT
    ei = li // 2
    c = g.cst
    A = Carver(g)
    s = NS()
    HB = NB * 16
    s.xpre = A.f(515); s.xpreU = P.unit()
    s.cacc = A.f(512); s.caccU = P.unit()
    v3 = lambda ap: ap.rearrange("p (b h) -> p b h", h=16)
    s.dt = A.f(HB); s.atok = A.f(HB); s.acs = A.f(HB); s.eacs = A.f(HB); s.dtd = A.f(HB); s.edl = A.f(HB)
    s.dtU = P.unit()
    s.acsT = A.f(L); s.acsTU = P.unit()
    s.cw = A.f(64); s.cb = A.f(16); s.dtb = A.f(16); s.Abc = A.f(16); s.Dsk = A.f(16); s.snw = A.f(8)
    s.smallU = P.unit()
    s.S = A.f(256); s.SU = P.unit()
    scan_off = A.fo
    s.Rbd = A.f(512); s.RbdU = P.unit()
    D2 = lambda n: ([A.f(n) for _ in range(2)], P.units(2))
    s.acsTb2, s.acsTbU2 = D2(128)
    s.CBm2, s.CBmU2 = D2(128)
    s.Dm2, s.DmU2 = D2(512)
    s.E2, s.EU2 = D2(512)
    s.t12, s.t1U2 = D2(256)
    s.t22, s.t2U2 = D2(256)
    s.ytmp2, s.ytmpU2 = D2(256)
    s.rstd_all = g.arf[:, scan_off:scan_off + L]; s.rstdallU = P.unit()
    assert scan_off + L <= ARF_N
    s.w = A.b(KC * 768).rearrange("p (k c) -> p k c", k=KC); s.wU = P.unit()
    s.wdt = A.b(KC * 16).rearrange("p (k c) -> p k c", k=KC); s.wdtU = P.unit()
    s.BT = A.b(L); s.CT = A.b(L); s.BCU = [P.unit() for _ in range(NT)]
    s.xtok = A.b(NB * 256).rearrange("p (b c) -> p b c", c=256); s.xtokU = [P.unit() for _ in range(NT)]
    s.Btok = A.b(NB * 128).rearrange("p (b c) -> p b c", c=128); s.BtokU = [P.unit() for _ in range(NT)]
    scan_bo = A.bo
    s.zs2 = [A.b(256) for _ in range(2)]; s.zsU2 = P.units(2)
    s.sc2 = [A.b(512).rearrange("p (h l) -> p h l", h=4) for _ in range(2)]; s.scU2 = P.units(2)
    s.Xdt2 = [A.b(256) for _ in range(2)]; s.XdtU2 = P.units(2)
    s.XB2 = [A.b(256) for _ in range(2)]; s.XBU2 = P.units(2)
    s.Sbf = A.b(256); s.SbfU = P.unit()
    s.yTa = A.b(8 * L).rearrange("p (c t) -> p c t", c=8); s.yTaU = [[P.unit() for _ in range(NT)] for _ in range(8)]
    s.wo2 = [g.arb[:, scan_bo + i * 1024:scan_bo + (i + 1) * 1024].rearrange("p (c j) -> p c j", c=8) for i in range(2)]
    s.woU2 = P.units(2)
    assert scan_bo + 2048 <= A.bo
    s.tmp2 = [s.cacc, s.xpre[:, 0:512]]; s.tmpU2 = [s.caccU, s.xpreU]
    ident = c["ident"]

    sm = [s.smallU]
    P.dma("sp", lambda e: e.dma_start(out=s.cw, in_=g.d["conv_w_cols"][ei]), "s_small", writes=sm)
    P.dma("sp", lambda e: e.dma_start(out=s.cb, in_=g.d["conv_b_cols"][ei]), "s_small", writes=sm)
    P.dma("sp", lambda e: e.dma_start(out=s.dtb, in_=g.d["dt_bias"][ei].partition_broadcast(128)), "s_small", writes=sm)
    P.dma("sp", lambda e: e.dma_start(out=s.Abc, in_=g.d["A_log"][ei].partition_broadcast(128)), "s_small", writes=sm)
    P.dma("sp", lambda e: e.dma_start(out=s.Dsk, in_=g.d["D_skip"][ei].partition_broadcast(128)), "s_small", writes=sm)
    P.dma("sp", lambda e: e.dma_start(out=s.snw, in_=g.d["ssd_norm_w_cols"][ei]), "s_small", writes=sm)
    P.act(lambda e: e.activation(out=s.Abc, in_=s.Abc, func=AF.Exp), reads=sm, writes=sm)
    P.dve(lambda e: e.tensor_scalar(out=s.Abc, in0=s.Abc, scalar1=-1.0, scalar2=None, op0=ALU.mult), reads=sm, writes=sm)
    P.dma("pool", lambda e: e.dma_start(out=s.wdt.rearrange("p k c -> p (k c)"), in_=g.d["ev_w_dt"][ei]), "s_wdt", writes=[s.wdtU])

    for b in range(NB):
        for k in range(KC):
            P.pe(lambda e, b=b, k=k: e.matmul(bank(g, 0)[:, b * 16:(b + 1) * 16], lhsT=g.uT[:, k, b * 128:(b + 1) * 128],
                                              rhs=s.wdt[:, k, :], start=(k == 0), stop=(k == KC - 1)),
                 reads=[g.uU[b // 4], s.wdtU], writes=[g.bU[0]])
    bc_h = lambda t: t.unsqueeze(1).to_broadcast([128, NB, 16])
    du = [s.dtU]
    P.dve(lambda e: e.tensor_tensor(out=v3(s.dt), in0=v3(bank(g, 0)[:, 0:HB]), in1=bc_h(s.dtb), op=ALU.add),
          reads=[g.bU[0]] + sm, writes=du)
    P.act(lambda e: e.activation(out=s.dt, in_=s.dt, func=AF.Exp), reads=du, writes=du)
    P.act(lambda e: e.activation(out=s.dt, in_=s.dt, func=AF.Ln, bias=1.0), reads=du, writes=du)
    P.dve(lambda e: e.tensor_tensor(out=v3(s.atok), in0=v3(s.dt), in1=bc_h(s.Abc), op=ALU.mult), reads=du + sm, writes=du)
    P.pe(lambda e: e.matmul(bank(g, 1)[:, 0:HB], lhsT=c["triI"][:], rhs=s.atok, start=True, stop=True), reads=du + [g.cU], writes=[g.bU[1]])
    P.pe(lambda e: e.matmul(bank(g, 2)[:, 0:HB], lhsT=c["ones"][:], rhs=s.atok, start=True, stop=True), reads=du + [g.cU], writes=[g.bU[2]])
    P.dve(lambda e: e.tensor_copy(s.acs, bank(g, 1)[:, 0:HB]), reads=[g.bU[1]], writes=du)
    P.act(lambda e: e.activation(out=s.eacs, in_=s.acs, func=AF.Exp), reads=du, writes=du)
    P.dve(lambda e: e.tensor_copy(s.edl, bank(g, 2)[:, 0:HB]), reads=[g.bU[2]], writes=du)
    P.dve(lambda e: e.tensor_tensor(out=s.dtd, in0=s.edl, in1=s.acs, op=ALU.subtract), reads=du, writes=du)
    P.act(lambda e: e.activation(out=s.dtd, in_=s.dtd, func=AF.Exp), reads=du, writes=du)
    P.dve(lambda e: e.tensor_tensor(out=s.dtd, in0=s.dtd, in1=s.dt, op=ALU.mult), reads=du, writes=du)
    P.act(lambda e: e.activation(out=s.edl, in_=s.edl, func=AF.Exp), reads=du, writes=du)
    for n in range(NT):
        for j in range(4):
            b = n * 4 + j
            P.pe(lambda e, b=b, j=j: e.transpose(bank(g, 3)[0:16, j * 128:(j + 1) * 128], s.acs[:, b * 16:(b + 1) * 16], c["ident"][:]),
                 reads=du + [g.cU], writes=[g.bU[3]])
        P.act(lambda e, n=n: e.copy(s.acsT[0:16, n * 512:(n + 1) * 512], bank(g, 3)[0:16, :]), reads=[g.bU[3]], writes=[s.acsTU])

    pcnt = [0]
    for grp in range(4):
        P.dma("pool", lambda e, grp=grp: e.dma_start(out=s.w.rearrange("p k c -> p (k c)"), in_=g.d["ev_w_ssd"][ei, grp]),
              "s_w", writes=[s.wU])
        chunks = [(256, 2 * grp, "x0"), (384, 2 * grp + 1, "x1"), (512, 8 + grp, "B"), (640, 12 + grp, "C")]
        cp1, cp2 = [], []
        for wc0, cch, kind in chunks:
            for n in range(NT):
                sl = slice(n * 512, (n + 1) * 512)
                bk = 3 + (pcnt[0] % 2)
                pcnt[0] += 1
                P.capture()
                for k in range(KC):
                    P.pe(lambda e, bk=bk, k=k, wc0=wc0, sl=sl: e.matmul(bank(g, bk), lhsT=s.w[:, k, wc0:wc0 + 128], rhs=g.uT[:, k, sl],
                                                                        start=(k == 0), stop=(k == KC - 1)),
                         reads=[g.uU[n], s.wU], writes=[g.bU[bk]])
                cp1.append(P.end_capture())
                P.capture()
                if n == 0:
                    P.dve(lambda e: e.memset(s.xpre[:, 0:3], 0.0), writes=[s.xpreU])
                else:
                    P.dve(lambda e: e.tensor_copy(s.xpre[:, 0:3], s.xpre[:, 512:515]), reads=[s.xpreU], writes=[s.xpreU])
                P.act(lambda e, bk=bk: e.copy(s.xpre[:, 3:515], bank(g, bk)), reads=[g.bU[bk]], writes=[s.xpreU])
                P.dve(lambda e, cch=cch: e.tensor_scalar(out=s.cacc, in0=s.xpre[:, 3:515], scalar1=s.cw[:, cch * 4 + 3:cch * 4 + 4],
                                                          scalar2=s.cb[:, cch:cch + 1], op0=ALU.mult, op1=ALU.add),
                      reads=[s.xpreU] + sm, writes=[s.caccU])
                for tap in (2, 1, 0):
                    P.dve(lambda e, cch=cch, tap=tap: e.scalar_tensor_tensor(
                        out=s.cacc, in0=s.xpre[:, tap:tap + 512], scalar=s.cw[:, cch * 4 + tap:cch * 4 + tap + 1], in1=s.cacc,
                        op0=ALU.mult, op1=ALU.add), reads=[s.xpreU, s.caccU] + sm, writes=[s.caccU])
                P.act(lambda e: e.activation(out=s.cacc, in_=s.cacc, func=AF.Silu), reads=[s.caccU], writes=[s.caccU])
                if kind in ("B", "C"):
                    dst = s.BT if kind == "B" else s.CT
                    P.dve(lambda e, dst=dst, sl=sl: e.tensor_copy(dst[:, sl], s.cacc), reads=[s.caccU], writes=[s.BCU[n]])
                if kind != "C":
                    for j in range(4):
                        P.pe(lambda e, j=j: e.transpose(bank(g, 5)[:, j * 128:(j + 1) * 128], s.cacc[:, j * 128:(j + 1) * 128], ident[:]),
                             reads=[s.caccU, g.cU], writes=[g.bU[5]])
                    src = bank(g, 5).rearrange("p (b c) -> p b c", b=4)
                    if kind == "B":
                        P.act(lambda e, n=n, src=src: e.copy(s.Btok[:, n * 4:(n + 1) * 4, :], src), reads=[g.bU[5]], writes=[s.BtokU[n]])
                    else:
                        co = 0 if kind == "x0" else 128
                        P.act(lambda e, n=n, src=src, co=co: e.copy(s.xtok[:, n * 4:(n + 1) * 4, co:co + 128], src),
                              reads=[g.bU[5]], writes=[s.xtokU[n]])
                cp2.append(P.end_capture())
        P.replay_merged(cp1[0], [])
        for i in range(len(cp2)):
            P.replay_merged(cp1[i + 1] if i + 1 < len(cp1) else [], [])
            P.replay_merged(cp2[i], [])
        hs4 = slice(4 * grp, 4 * grp + 4)
        fronts, backs = [], []
        for b in range(NB):
            P.capture()
            n = b // 4
            blk = slice(b * 128, (b + 1) * 128)
            hcol = lambda t, b=b: v3(t)[:, b, hs4]
            bch = lambda t, w, b=b: hcol(t, b).unsqueeze(2).to_broadcast([128, 4, w])
            x4 = s.xtok[:, b, :].rearrange("p (h q) -> p h q", h=4)
            pb_ = b % 2
            s.acsTb, s.acsTbU = s.acsTb2[pb_], s.acsTbU2[pb_]
            s.CBm, s.CBmU = s.CBm2[pb_], s.CBmU2[pb_]
            s.Dm, s.DmU = s.Dm2[pb_], s.DmU2[pb_]
            s.E, s.EU = s.E2[pb_], s.EU2[pb_]
            s.t1, s.t1U = s.t12[pb_], s.t1U2[pb_]
            s.t2, s.t2U = s.t22[pb_], s.t2U2[pb_]
            s.ytmp, s.ytmpU = s.ytmp2[pb_], s.ytmpU2[pb_]
            s.zs, s.zsU = s.zs2[pb_], s.zsU2[pb_]
            s.sc, s.scU = s.sc2[pb_], s.scU2[pb_]
            s.Xdt, s.XdtU = s.Xdt2[pb_], s.XdtU2[pb_]
            s.XB, s.XBU = s.XB2[pb_], s.XBU2[pb_]
            for k in range(KC):
                P.pe(lambda e, k=k, blk=blk: e.matmul(bank(g, 6)[:, 0:256], lhsT=g.uT[:, k, blk], rhs=s.w[:, k, 0:256],
                                                      start=(k == 0), stop=(k == KC - 1)),
                     reads=[g.uU[n], s.wU], writes=[g.bU[6]])
            P.act(lambda e: e.activation(out=s.zs, in_=bank(g, 6)[:, 0:256], func=AF.Silu), reads=[g.bU[6]], writes=[s.zsU])
            P.pe(lambda e, blk=blk: e.matmul(bank(g, 7)[:, 0:128], lhsT=s.BT[:, blk], rhs=s.CT[:, blk], start=True, stop=True),
                 reads=[s.BCU[n]], writes=[g.bU[7]])
            P.dve(lambda e: e.tensor_tensor(out=s.CBm, in0=bank(g, 7)[:, 0:128], in1=c["triI"][:], op=ALU.mult),
                  reads=[g.bU[7], g.cU], writes=[s.CBmU])
            P.dve(lambda e, blk=blk: e.tensor_tensor(out=s.Rbd[0:16, :].rearrange("p (h l) -> p h l", h=4),
                                                     in0=s.acsT[0:16, blk].unsqueeze(1).to_broadcast([16, 4, 128]),
                                                     in1=c["mg16"][:, hs4].unsqueeze(2).to_broadcast([16, 4, 128]), op=ALU.mult),
                  reads=[s.acsTU, g.cU], writes=[s.RbdU])
            P.pe(lambda e: e.matmul(bank(g, 1), lhsT=c["ones"][0:16, :], rhs=s.Rbd[0:16, :], start=True, stop=True),
                 reads=[s.RbdU, g.cU], writes=[g.bU[1]])
            P.dve(lambda e, b=b: e.tensor_tensor(out=s.Dm.rearrange("p (h l) -> p h l", h=4),
                                                 in0=bank(g, 1).rearrange("p (h l) -> p h l", h=4),
                                                 in1=bch(s.acs, 128, b), op=ALU.subtract),
                  reads=[g.bU[1]] + du, writes=[s.DmU])
            P.dve(lambda e: e.tensor_scalar(out=s.Dm, in0=s.Dm, scalar1=0.0, scalar2=None, op0=ALU.min), reads=[s.DmU], writes=[s.DmU])
            P.act(lambda e: e.activation(out=s.E, in_=s.Dm, func=AF.Exp), reads=[s.DmU], writes=[s.EU])
            P.dve(lambda e: e.tensor_tensor(out=s.sc, in0=s.E.rearrange("p (h l) -> p h l", h=4),
                                            in1=s.CBm.unsqueeze(1).to_broadcast([128, 4, 128]), op=ALU.mult),
                  reads=[s.EU, s.CBmU], writes=[s.scU])
            P.dve(lambda e, b=b: e.tensor_tensor(out=s.Xdt.rearrange("p (h q) -> p h q", h=4), in0=x4, in1=bch(s.dt, 64, b), op=ALU.mult),
                  reads=[s.xtokU[n]] + du, writes=[s.XdtU])
            P.dve(lambda e, b=b: e.tensor_tensor(out=s.XB.rearrange("p (h q) -> p h q", h=4), in0=x4, in1=bch(s.dtd, 64, b), op=ALU.mult),
                  reads=[s.xtokU[n]] + du, writes=[s.XBU])
            fronts.append(P.end_capture())
            P.capture()
            for h4 in range(4):
                P.pe(lambda e, h4=h4: e.matmul(bank(g, 2)[:, h4 * 64:(h4 + 1) * 64], lhsT=s.sc[:, h4, :], rhs=s.Xdt[:, h4 * 64:(h4 + 1) * 64],
                                               start=True, stop=True), reads=[s.scU, s.XdtU], writes=[g.bU[2]])
            if b > 0:
                P.pe(lambda e, blk=blk: e.matmul(bank(g, 3)[:, 0:256], lhsT=s.CT[:, blk], rhs=s.Sbf, start=True, stop=True),
                     reads=[s.BCU[n], s.SbfU], writes=[g.bU[3]])
            P.pe(lambda e, b=b: e.matmul(bank(g, 4)[:, 0:256], lhsT=s.Btok[:, b, :], rhs=s.XB, start=True, stop=True),
                 reads=[s.BtokU[n], s.XBU], writes=[g.bU[4]])
            if b > 0:
                P.dve(lambda e, b=b: e.tensor_tensor(out=s.t1.rearrange("p (h q) -> p h q", h=4),
                                                     in0=bank(g, 3)[:, 0:256].rearrange("p (h q) -> p h q", h=4),
                                                     in1=bch(s.eacs, 64, b), op=ALU.mult), reads=[g.bU[3]] + du, writes=[s.t1U])
                P.dve(lambda e: e.tensor_tensor(out=s.t2, in0=bank(g, 2)[:, 0:256], in1=s.t1, op=ALU.add), reads=[g.bU[2], s.t1U], writes=[s.t2U])
            else:
                P.dve(lambda e: e.tensor_copy(s.t2, bank(g, 2)[:, 0:256]), reads=[g.bU[2]], writes=[s.t2U])
            P.dve(lambda e: e.tensor_tensor(out=s.t1.rearrange("p (h q) -> p h q", h=4), in0=x4,
                                            in1=s.Dsk[:, hs4].unsqueeze(2).to_broadcast([128, 4, 64]), op=ALU.mult),
                  reads=[s.xtokU[n]] + sm, writes=[s.t1U])
            P.dve(lambda e: e.tensor_tensor(out=s.t2, in0=s.t2, in1=s.t1, op=ALU.add), reads=[s.t1U, s.t2U], writes=[s.t2U])
            P.dve(lambda e: e.tensor_tensor(out=s.ytmp, in0=s.t2, in1=s.zs, op=ALU.mult), reads=[s.t2U, s.zsU], writes=[s.ytmpU])
            for j in range(2):
                P.pe(lambda e, j=j: e.transpose(bank(g, 5)[:, j * 128:(j + 1) * 128], s.ytmp[:, j * 128:(j + 1) * 128], ident[:]),
                     reads=[s.ytmpU, g.cU], writes=[g.bU[5]])
            P.act(lambda e, blk=blk: e.copy(s.yTa[:, 2 * grp:2 * grp + 2, blk], bank(g, 5)[:, 0:256].rearrange("p (j t) -> p j t", j=2)),
                  reads=[g.bU[5]], writes=[s.yTaU[2 * grp][n], s.yTaU[2 * grp + 1][n]])
            if b == 0:
                P.dve(lambda e: e.tensor_copy(s.S, bank(g, 4)[:, 0:256]), reads=[g.bU[4]], writes=[s.SU])
            else:
                P.dve(lambda e, b=b: e.tensor_tensor(out=s.S.rearrange("p (h q) -> p h q", h=4), in0=s.S.rearrange("p (h q) -> p h q", h=4),
                                                     in1=bch(s.edl, 64, b), op=ALU.mult), reads=[s.SU] + du, writes=[s.SU])
                P.dve(lambda e: e.tensor_tensor(out=s.S, in0=s.S, in1=bank(g, 4)[:, 0:256], op=ALU.add), reads=[s.SU, g.bU[4]], writes=[s.SU])
            if b + 1 < NB:
                P.act(lambda e: e.copy(s.Sbf, s.S), reads=[s.SU], writes=[s.SbfU])
            backs.append(P.end_capture())
        P.replay_merged(fronts[0], [])
        for b in range(NB):
            P.replay_merged(fronts[b + 1] if b + 1 < NB else [], backs[b])

    P.barrier()
    for n in range(NT):
        sl = slice(n * 512, (n + 1) * 512)
        rms_rstd_tile(g, lambda k, sl=sl: s.yTa[:, k, sl], lambda k, n=n: [s.yTaU[k][n]], 8, 1024, slot=n % 2)
        P.dve(lambda e, sl=sl: e.tensor_copy(s.rstd_all[:, sl], g.rstd_t[:]), reads=[g.rstdU], writes=[s.rstdallU])
    cnt = 0

    def load_wo(m):
        wi = m % 2
        P.dma("pool", lambda e: e.dma_start(out=s.wo2[wi].rearrange("p c j -> p (c j)"), in_=g.d["ev_w_out_t"][ei, m, :, 0:1024]),
              f"s_wo{wi}", writes=[s.woU2[wi]])
        P.dve(lambda e: e.tensor_tensor(out=s.wo2[wi], in0=s.wo2[wi], in1=s.snw[:, 0:8].unsqueeze(2).to_broadcast([128, 8, 128]), op=ALU.mult),
              reads=[s.woU2[wi]] + sm, writes=[s.woU2[wi]])

    load_wo(0)
    for m in range(KC):
        if m + 1 < KC:
            load_wo(m + 1)
        wi = m % 2
        for n in range(NT):
            sl = slice(n * 512, (n + 1) * 512)
            bk = cnt % 2
            ti = cnt % 2
            cnt += 1
            for k in range(8):
                P.pe(lambda e, bk=bk, k=k, sl=sl: e.matmul(bank(g, bk), lhsT=s.wo2[wi][:, k, :], rhs=s.yTa[:, k, sl], start=(k == 0), stop=(k == 7)),
                     reads=[s.woU2[wi], s.yTaU[k][n]], writes=[g.bU[bk]])
            P.dve(lambda e, bk=bk, sl=sl, ti=ti: e.tensor_tensor(out=s.tmp2[ti], in0=bank(g, bk), in1=s.rstd_all[:, sl], op=ALU.mult),
                  reads=[g.bU[bk], s.rstdallU], writes=[s.tmpU2[ti]])
            P.pool(lambda e, m=m, sl=sl, ti=ti: e.tensor_tensor(out=g.hT[:, m, sl], in0=g.hT[:, m, sl], in1=s.tmp2[ti], op=ALU.add),
                   reads=[s.tmpU2[ti], g.hU[m][n]], writes=[g.hU[m][n]])


def even_attn(g, li):
    P, L, NB, NT = g.P, g.L, g.NB, g.NT
    ei = li // 2
    lam_init = 0.8 - 0.6 * math.exp(-0.3 * li)
    c = g.cst
    A = Carver(g)
    a = NS()
    a.corr = A.f(2048).rearrange("p (h d q) -> p h d q", h=8, d=2); a.corrU = P.unit()
    a.b31 = A.f(8); a.nb31 = A.f(8); a.bU_ = P.unit()
    a.lq = [A.f(64) for _ in range(4)]; a.lamU = P.unit()
    a.lam = A.f(8)
    a.slnw = A.f(128); a.slnwU = P.unit()
    a.w = [A.b(KC * 512).rearrange("p (k c) -> p k c", k=KC) for _ in range(2)]; a.wU = P.units(2)
    a.qT = [A.b(L) for _ in range(2)]; a.qTU = [P.unit() for _ in range(NT)]
    a.kT = A.b(L); a.kTU = [P.unit() for _ in range(NT)]
    a.v = A.b(NB * 132).rearrange("p (b c) -> p b c", c=132); a.vU = P.unit()
    a.gs = A.b(NB * 128).rearrange("p (b c) -> p b c", c=128); a.gsU = P.unit()
    a.PT = [[A.b(512) for _ in range(2)] for _ in range(2)]; a.PTU = [P.units(2), P.units(2)]
    a.yT = A.b(4 * L).rearrange("p (j t) -> p j t", j=4); a.yTU = P.units(4)
    a.wo = [A.b(512).rearrange("p (j c) -> p j c", j=4) for _ in range(2)]; a.woU = P.units(2)
    ident = c["ident"]

    P.dma("sp", lambda e: e.dma_start(out=a.corr.rearrange("p h d q -> p (h d q)"), in_=g.d["rel_biasD"]), "a_corr", writes=[a.corrU])
    P.dma("sp", lambda e: e.dma_start(out=a.b31, in_=g.d["rel_b31"].partition_broadcast(128)), "a_b31", writes=[a.bU_])
    P.dve(lambda e: e.tensor_scalar(out=a.nb31, in0=a.b31, scalar1=-1.0, scalar2=None, op0=ALU.mult), reads=[a.bU_], writes=[a.bU_])
    for h in range(8):
        P.act(lambda e, h=h: e.activation(out=a.corr[:, h], in_=a.corr[:, h], func=AF.Exp, bias=a.nb31[:, h:h + 1]),
              reads=[a.corrU, a.bU_], writes=[a.corrU])
    for i, nm in enumerate(("lambda_q1", "lambda_k1", "lambda_q2", "lambda_k2")):
        P.dma("sp", lambda e, i=i, nm=nm: e.dma_start(out=a.lq[i], in_=g.d[nm][ei].partition_broadcast(128)), "a_lam", writes=[a.lamU])
    lu = [a.lamU]
    P.dve(lambda e: e.tensor_tensor(out=a.lq[0], in0=a.lq[0], in1=a.lq[1], op=ALU.mult), reads=lu, writes=lu)
    P.dve(lambda e: e.tensor_tensor(out=a.lq[2], in0=a.lq[2], in1=a.lq[3], op=ALU.mult), reads=lu, writes=lu)
    P.dve(lambda e: e.tensor_reduce(out=a.lam[:, 0:1], in_=a.lq[0], axis=AX.X, op=ALU.add), reads=lu, writes=lu)
    P.dve(lambda e: e.tensor_reduce(out=a.lam[:, 1:2], in_=a.lq[2], axis=AX.X, op=ALU.add), reads=lu, writes=lu)
    P.act(lambda e: e.activation(out=a.lam[:, 2:4], in_=a.lam[:, 0:2], func=AF.Exp), reads=lu, writes=lu)
    P.dve(lambda e: e.tensor_tensor(out=a.lam[:, 4:5], in0=a.lam[:, 3:4], in1=a.lam[:, 2:3], op=ALU.subtract), reads=lu, writes=lu)
    P.dve(lambda e: e.tensor_scalar(out=a.lam[:, 5:6], in0=a.lam[:, 4:5], scalar1=-lam_init, scalar2=None, op0=ALU.add), reads=lu, writes=lu)
    P.dma("sp", lambda e: e.dma_start(out=a.slnw, in_=g.d["subln_w"][ei].partition_broadcast(128)), "a_slnw", writes=[a.slnwU])
    P.dve(lambda e: e.tensor_scalar(out=a.slnw, in0=a.slnw, scalar1=1.0 - lam_init, scalar2=None, op0=ALU.mult),
          reads=[a.slnwU], writes=[a.slnwU])
    P.dve(lambda e: e.memset(a.v, 1.0), writes=[a.vU])

    def load_w(h):
        i = h % 2
        P.dma("pool", lambda e: e.dma_start(out=a.w[i].rearrange("p k c -> p (k c)"), in_=g.d["ev_w_att"][ei, h]),
              f"a_w{i}", writes=[a.wU[i]])

    pc = [0]

    def project(h):
        wi = h % 2
        w = a.w[wi]
        for n in range(NT):
            sl = slice(n * 512, (n + 1) * 512)
            for which in range(2):
                bk = pc[0] % 2
                pc[0] += 1
                for k in range(KC):
                    P.pe(lambda e, bk=bk, k=k, sl=sl, which=which: e.matmul(bank(g, bk), lhsT=w[:, k, which * 128:(which + 1) * 128],
                                                                            rhs=g.uT[:, k, sl], start=(k == 0), stop=(k == KC - 1)),
                         reads=[g.uU[n], a.wU[wi]], writes=[g.bU[bk]])
                if which == 0:
                    for cc in range(2):
                        P.dve(lambda e, bk=bk, sl=sl, cc=cc: e.tensor_scalar(out=a.qT[cc][:, sl], in0=bank(g, bk), scalar1=c["maskq"][:, cc:cc + 1],
                                                                             scalar2=None, op0=ALU.mult),
                              reads=[g.bU[bk], g.cU], writes=[a.qTU[n]])
                else:
                    P.act(lambda e, bk=bk, sl=sl: e.copy(a.kT[:, sl], bank(g, bk)), reads=[g.bU[bk]], writes=[a.kTU[n]])
        for b in range(NB):
            bk = 2 + (b % 2)
            for k in range(KC):
                P.pe(lambda e, bk=bk, k=k, b=b: e.matmul(bank(g, bk)[:, 0:256], lhsT=g.uT[:, k, b * 128:(b + 1) * 128], rhs=w[:, k, 256:512],
                                                         start=(k == 0), stop=(k == KC - 1)),
                     reads=[g.uU[b // 4], a.wU[wi]], writes=[g.bU[bk]])
            P.act(lambda e, bk=bk, b=b: e.copy(a.v[:, b, 0:128], bank(g, bk)[:, 0:128]), reads=[g.bU[bk]], writes=[a.vU])
            P.act(lambda e, bk=bk, b=b: e.activation(out=a.gs[:, b, :], in_=bank(g, bk)[:, 128:256], func=AF.Silu),
                  reads=[g.bU[bk]], writes=[a.gsU])

    gc = [0]
    a.r2 = [A.f(8) for _ in range(2)]; a.rU2 = P.units(2)
    a.t2 = [A.f(128) for _ in range(2)]; a.tU2 = P.units(2)
    a.o2 = [A.f(128) for _ in range(2)]; a.oU2 = P.units(2)
    a.sqo2 = [A.f(128) for _ in range(2)]; a.sqoU2 = P.units(2)
    a.y2 = [A.f(128) for _ in range(2)]; a.yU2 = P.units(2)

    def attend(h):
        hj = h % 4
        groups = []
        for qb in range(NB):
            for gi in range(qb // 4 + 1):
                kbs = [kb for kb in range(gi * 4, gi * 4 + 4) if kb <= qb]
                groups.append((qb, gi, kbs, gc[0] % 2))
                gc[0] += 1

        def accb(qb, cc):
            return (2 + cc) if qb % 2 == 0 else cc

        def S_(grp):
            qb, gi, kbs, buf = grp
            qs = slice(qb * 128, (qb + 1) * 128)
            for cc in range(2):
                bk = 4 + 2 * cc + buf
                for j, kb in enumerate(kbs):
                    P.pe(lambda e, bk=bk, j=j, kb=kb, cc=cc: e.matmul(bank(g, bk)[:, j * 128:(j + 1) * 128],
                                                                      lhsT=a.kT[:, kb * 128:(kb + 1) * 128], rhs=a.qT[cc][:, qs],
                                                                      start=True, stop=True),
                         reads=[a.kTU[kb // 4], a.qTU[qb // 4]], writes=[g.bU[bk]])

        def E_(grp):
            qb, gi, kbs, buf = grp
            nv = len(kbs)
            for cc in range(2):
                bk = 4 + 2 * cc + buf
                pt = a.PT[cc][buf]
                ptu = a.PTU[cc][buf]
                P.act(lambda e, bk=bk, pt=pt, nv=nv: e.activation(out=pt[:, 0:nv * 128], in_=bank(g, bk)[:, 0:nv * 128], func=AF.Exp,
                                                                  scale=0.125, bias=a.b31[:, h:h + 1]),
                      reads=[g.bU[bk], a.bU_], writes=[ptu])
                for j, kb in enumerate(kbs):
                    Dd = qb - kb
                    if Dd <= 1:
                        P.dve(lambda e, pt=pt, j=j, Dd=Dd: e.tensor_tensor(out=pt[:, j * 128:(j + 1) * 128], in0=pt[:, j * 128:(j + 1) * 128],
                                                                           in1=a.corr[:, h, Dd, :], op=ALU.mult),
                              reads=[ptu, a.corrU], writes=[ptu])

        def PV_(grp):
            qb, gi, kbs, buf = grp
            for cc in range(2):
                pt = a.PT[cc][buf]
                ptu = a.PTU[cc][buf]
                ab = accb(qb, cc)
                for j, kb in enumerate(kbs):
                    P.pe(lambda e, ab=ab, pt=pt, j=j, kb=kb: e.matmul(bank(g, ab)[:, 0:129], lhsT=pt[:, j * 128:(j + 1) * 128],
                                                                      rhs=a.v[:, kb, 0:129], start=(kb == 0), stop=(kb == qb)),
                         reads=[ptu, a.vU], writes=[g.bU[ab]])

        def FIN_(qb):
            qs = slice(qb * 128, (qb + 1) * 128)
            pq = qb % 2
            b0, b1 = accb(qb, 0), accb(qb, 1)
            r, t_, o_, sqo, y_ = a.r2[pq], a.t2[pq], a.o2[pq], a.sqo2[pq], a.y2[pq]
            ru = [a.rU2[pq]]
            tU, oU, sqoU, yU = a.tU2[pq], a.oU2[pq], a.sqoU2[pq], a.yU2[pq]
            P.dve(lambda e: e.reciprocal(r[:, 0:1], bank(g, b0)[:, 128:129]), reads=[g.bU[b0]], writes=ru)
            P.dve(lambda e: e.reciprocal(r[:, 1:2], bank(g, b1)[:, 128:129]), reads=[g.bU[b1]], writes=ru)
            P.dve(lambda e: e.tensor_tensor(out=r[:, 2:3], in0=r[:, 1:2], in1=a.lam[:, 5:6], op=ALU.mult), reads=ru + lu, writes=ru)
            P.dve(lambda e: e.tensor_scalar(out=t_, in0=bank(g, b0)[:, 0:128], scalar1=r[:, 0:1], scalar2=None, op0=ALU.mult),
                  reads=[g.bU[b0]] + ru, writes=[tU])
            P.dve(lambda e: e.scalar_tensor_tensor(out=o_, in0=bank(g, b1)[:, 0:128], scalar=r[:, 2:3], in1=t_, op0=ALU.mult, op1=ALU.add),
                  reads=[g.bU[b1], tU] + ru, writes=[oU])
            P.dve(lambda e: e.tensor_tensor(out=sqo, in0=o_, in1=o_, op=ALU.mult), reads=[oU], writes=[sqoU])
            P.dve(lambda e: e.tensor_reduce(out=r[:, 3:4], in_=sqo, axis=AX.X, op=ALU.add), reads=[sqoU], writes=ru)
            P.act(lambda e: e.activation(out=r[:, 4:5], in_=r[:, 3:4], func=AF.Ln, scale=1.0 / 128, bias=EPS), reads=ru, writes=ru)
            P.act(lambda e: e.activation(out=r[:, 5:6], in_=r[:, 4:5], func=AF.Exp, scale=-0.5), reads=ru, writes=ru)
            P.dve(lambda e: e.scalar_tensor_tensor(out=y_, in0=o_, scalar=r[:, 5:6], in1=a.slnw, op0=ALU.mult, op1=ALU.mult),
                  reads=[oU, a.slnwU] + ru, writes=[yU])
            P.dve(lambda e: e.tensor_tensor(out=y_, in0=y_, in1=a.gs[:, qb, :], op=ALU.mult), reads=[yU, a.gsU], writes=[yU])
            P.pe(lambda e: e.transpose(bank(g, b0)[:, 256:384], y_, ident[:]), reads=[yU, g.cU], writes=[g.bU[b0]])
            P.act(lambda e: e.copy(a.yT[:, hj, qs], bank(g, b0)[:, 256:384]), reads=[g.bU[b0]], writes=[a.yTU[hj]])

        M = len(groups)
        S_(groups[0])
        pending_fin = None
        for i in range(M):
            if i + 1 < M:
                S_(groups[i + 1])
            E_(groups[i])
            PV_(groups[i])
            if pending_fin is not None:
                FIN_(pending_fin)
                pending_fin = None
            qb, gi, kbs, buf = groups[i]
            if kbs[-1] == qb:
                pending_fin = qb
        if pending_fin is not None:
            FIN_(pending_fin)

    oc = [0]

    def outproj(hg):
        for m in range(KC):
            wi = oc[0] % 2
            oc[0] += 1
            c0 = (8 + hg * 4) * 128
            P.dma("pool", lambda e, m=m, wi=wi, c0=c0: e.dma_start(out=a.wo[wi].rearrange("p j c -> p (j c)"),
                                                                  in_=g.d["ev_w_out_t"][ei, m, :, c0:c0 + 512]),
                  f"a_wo{wi}", writes=[a.woU[wi]])
            for n in range(NT):
                sl = slice(n * 512, (n + 1) * 512)
                bk = n % 2
                for j in range(4):
                    P.pe(lambda e, bk=bk, j=j, sl=sl, wi=wi: e.matmul(bank(g, bk), lhsT=a.wo[wi][:, j, :], rhs=a.yT[:, j, sl],
                                                                      start=(j == 0), stop=(j == 3)),
                         reads=[a.woU[wi], a.yTU[j]], writes=[g.bU[bk]])
                P.dve(lambda e, bk=bk, m=m, sl=sl: e.tensor_tensor(out=g.hT[:, m, sl], in0=g.hT[:, m, sl], in1=bank(g, bk), op=ALU.add),
                      reads=[g.bU[bk], g.hU[m][n]], writes=[g.hU[m][n]])

    load_w(0)
    for h in range(8):
        if h + 1 < 8:
            load_w(h + 1)
        project(h)
        attend(h)
        if h % 4 == 3:
            outproj(h // 4)


_CACHE = {}


def kernel(**inputs):
    x = np.ascontiguousarray(np.asarray(inputs["x"], dtype=np.float32))
    Bsz, L, _ = x.shape
    n_cores = 8
    nseq = Bsz // n_cores
    key = (L, nseq)
    if key not in _CACHE:
        _CACHE[key] = build(L, nseq, (0, 1, 2, 3))
    nc, _ = _CACHE[key]
    common = host_layout(inputs)
    in_maps = []
    for cidx in range(n_cores):
        m = dict(common)
        m["x"] = x[cidx * nseq:(cidx + 1) * nseq]
        in_maps.append(m)
    res = run_bass_kernel_spmd(nc, in_maps, core_ids=list(range(n_cores)))
    out = np.concatenate([np.asarray(r["out"]) for r in res.results], axis=0)
    return out.astype(np.float32)
```
